# Optimizing a Trainium2 kernel written in Bass

```python
import jax, jax.numpy as jnp
from jax import lax
import numpy as np

D_MODEL = 2048
BATCH = 4
SEQ = 2048
DEPTH = 1
DEC_BATCH = 128
DEC_SEQ = 4
PAST_LEN = 16384
PAGE_SIZE = 128

HEAD_DIM = 64
D_RWKV = D_MODEL // 2
N_HEADS = D_RWKV // HEAD_DIM
RANK_W = 64
RANK_A = 64
RANK_G = 160
D_SHIFT = 3 * D_RWKV + RANK_W + RANK_A + RANK_G
D_POOL = D_MODEL // 2
POOL_WINDOWS = (2, 4, 8, 16)
N_POOL_GROUPS = len(POOL_WINDOWS)
POOL_GROUP = D_POOL // N_POOL_GROUPS
POOL_HIST = max(POOL_WINDOWS) - 1
D_IN = D_SHIFT + D_POOL + 2 * D_MODEL
D_FF = 5632
CONV_W = 3
NORM_EPS = 1e-6
GN_EPS = 64e-5

kernel_name = 'rwkv7_pool_gated_hybrid_step'


def _rmsnorm(x, g):
    xf = x.astype(jnp.float32)
    y = xf * lax.rsqrt(jnp.mean(xf * xf, axis=-1, keepdims=True) + NORM_EPS)
    return (y * g.astype(jnp.float32)).astype(x.dtype)


def _wkv_scan(s0, r, decay, k, v, a, b):
    def step(s, inp):
        r_t, w_t, k_t, v_t, a_t, b_t = inp
        sa = jnp.einsum('bhvk,bhk->bhv', s, a_t)
        s = s * w_t[:, :, None, :] + sa[..., None] * b_t[:, :, None, :] + v_t[..., None] * k_t[:, :, None, :]
        return s, jnp.einsum('bhvk,bhk->bhv', s, r_t)
    xs = tuple(jnp.moveaxis(t, 1, 0) for t in (r, decay, k, v, a, b))
    s, ys = lax.scan(step, s0, xs)
    return jnp.moveaxis(ys, 0, 1), s


def _rwkv7_branch(zs, wkv0, w0, w2, a0, a2, g2, k_k, k_a, r_k, lnx_w, lnx_b):
    B, T, _ = zs.shape
    f = zs.astype(jnp.float32)
    cuts = [D_RWKV, 2 * D_RWKV, 3 * D_RWKV, 3 * D_RWKV + RANK_W, 3 * D_RWKV + RANK_W + RANK_A]
    r, k, v, zw, za, zg = jnp.split(f, cuts, axis=-1)
    w_log = -jax.nn.softplus(-(w0 + jnp.tanh(zw) @ w2)) - 0.5
    decay = jnp.exp(-jnp.exp(w_log))
    a = jax.nn.sigmoid(a0 + za @ a2)
    g = jax.nn.sigmoid(zg) @ g2
    hs = lambda t: t.reshape(B, T, N_HEADS, HEAD_DIM)
    kk = hs(k * k_k)
    kk = kk / jnp.maximum(jnp.sqrt(jnp.sum(kk * kk, axis=-1, keepdims=True)), 1e-12)
    k = k * (1.0 + (a - 1.0) * k_a)
    rh, kh, vh, ah = hs(r), hs(k), hs(v), hs(a)
    y, wkv = _wkv_scan(wkv0.astype(jnp.float32), rh, hs(decay), kh, vh, -kk, kk * ah)
    mu = jnp.mean(y, axis=-1, keepdims=True)
    var = jnp.mean(jnp.square(y - mu), axis=-1, keepdims=True)
    y = ((y - mu) * lax.rsqrt(var + GN_EPS)).reshape(B, T, D_RWKV) * lnx_w + lnx_b
    bonus = jnp.sum(rh * kh * r_k, axis=-1, keepdims=True) * vh
    y = (y + bonus.reshape(B, T, D_RWKV)) * g
    return y.astype(zs.dtype), wkv.astype(wkv0.dtype)


def _pool_branch(zp, pool0, pos0, pool_w, pool_scale):
    B, T, _ = zp.shape
    buf = jnp.concatenate([pool0.astype(zp.dtype), zp], axis=1)
    c = jnp.cumsum(buf.astype(jnp.float32), axis=1)
    c = jnp.concatenate([jnp.zeros((B, 1, D_POOL), jnp.float32), c], axis=1)
    end = c[:, POOL_HIST + 1:]
    pos = pos0 + jnp.arange(T)
    means = []
    for gi, win in enumerate(POOL_WINDOWS):
        lo, hi = gi * POOL_GROUP, (gi + 1) * POOL_GROUP
        start = c[:, POOL_HIST + 1 - win: POOL_HIST + 1 - win + T, lo:hi]
        cnt = jnp.minimum(win, pos + 1).astype(jnp.float32)[None, :, None]
        means.append((end[..., lo:hi] - start) / cnt)
    d = jnp.concatenate(means, axis=-1) - zp.astype(jnp.float32)
    d = d.reshape(B, T, N_POOL_GROUPS, POOL_GROUP)
    y = jnp.einsum('btgc,gcd->btgd', d, pool_w).reshape(B, T, D_POOL) * pool_scale
    return y.astype(zp.dtype), buf[:, -POOL_HIST:]


def _conv_ffn(h, conv0, w_ffn_in, conv_w, conv_b, w_ffn_out):
    T = h.shape[1]
    gate, up = jnp.split(h @ w_ffn_in, 2, axis=-1)
    buf = jnp.concatenate([conv0.astype(gate.dtype), gate], axis=1)
    cv = conv_b + sum(conv_w[j] * buf[:, j:j + T] for j in range(CONV_W))
    out = (jax.nn.gelu(cv, approximate=True) * up) @ w_ffn_out
    return out, buf[:, -(CONV_W - 1):]


def _layer(x, st_shift, st_wkv, st_pool, st_conv, pos0,
           norm_pre_mix, w_in, mu_shift, w0, w2, a0, a2, g2, k_k, k_a, r_k, lnx_w, lnx_b,
           w_branch_a, pool_w, pool_scale, w_branch_b, w_out, norm_post_mix,
           norm_pre_ffn, w_ffn_in, conv_w, conv_b, w_ffn_out, norm_post_ffn):
    h = _rmsnorm(x, norm_pre_mix)
    z = h @ w_in
    zs, zp, zga, zgb = jnp.split(z, [D_SHIFT, D_SHIFT + D_POOL, D_SHIFT + D_POOL + D_MODEL], axis=-1)
    prev = jnp.concatenate([st_shift.astype(zs.dtype), zs[:, :-1]], axis=1)
    zs_mix = zs + (prev - zs) * mu_shift
    ya, new_wkv = _rwkv7_branch(zs_mix, st_wkv, w0, w2, a0, a2, g2, k_k, k_a, r_k, lnx_w, lnx_b)
    yb, new_pool = _pool_branch(zp, st_pool, pos0, pool_w, pool_scale)
    m = jax.nn.sigmoid(zga) * (ya @ w_branch_a) + jax.nn.sigmoid(zgb) * (yb @ w_branch_b)
    x = x + _rmsnorm(m @ w_out, norm_post_mix)
    f, new_conv = _conv_ffn(_rmsnorm(x, norm_pre_ffn), st_conv, w_ffn_in, conv_w, conv_b, w_ffn_out)
    x = x + _rmsnorm(f.astype(x.dtype), norm_post_ffn)
    return x, zs[:, -1:], new_wkv, new_pool, new_conv


def setup_inputs(seed: int = 0) -> dict:
    key = jax.random.key(seed)
    ks = iter(jax.random.split(key, 40))
    nrm = lambda shape, scale: jax.random.normal(next(ks), shape, jnp.float32) * scale
    L = DEPTH
    return {
        'x_prompt': nrm((BATCH, SEQ, D_MODEL), 1.0),
        'x_sample': nrm((DEC_BATCH, DEC_SEQ, D_MODEL), 1.0),
        'state_shift': nrm((L, DEC_BATCH, 1, D_SHIFT), 1.0),
        'state_wkv': nrm((L, DEC_BATCH, N_HEADS, HEAD_DIM, HEAD_DIM), 0.3),
        'state_pool': nrm((L, DEC_BATCH, POOL_HIST, D_POOL), 1.0),
        'state_conv': nrm((L, DEC_BATCH, CONV_W - 1, D_FF), 1.0),
        'norm_pre_mix': 1.0 + nrm((L, D_MODEL), 0.02),
        'w_in': nrm((L, D_MODEL, D_IN), D_MODEL ** -0.5),
        'mu_shift': jax.random.uniform(next(ks), (L, D_SHIFT), jnp.float32),
        'w0': nrm((L, D_RWKV), 0.5) - 0.5,
        'w2': nrm((L, RANK_W, D_RWKV), 0.1),
        'a0': nrm((L, D_RWKV), 0.1),
        'a2': nrm((L, RANK_A, D_RWKV), 0.1),
        'g2': nrm((L, RANK_G, D_RWKV), RANK_G ** -0.5),
        'k_k': 0.85 + nrm((L, D_RWKV), 0.02),
        'k_a': 1.0 + nrm((L, D_RWKV), 0.02),
        'r_k': nrm((L, N_HEADS, HEAD_DIM), 0.1),
        'lnx_w': 1.0 + nrm((L, D_RWKV), 0.02),
        'lnx_b': nrm((L, D_RWKV), 0.02),
        'w_branch_a': nrm((L, D_RWKV, D_MODEL), D_RWKV ** -0.5),
        'pool_w': nrm((L, N_POOL_GROUPS, POOL_GROUP, POOL_GROUP), POOL_GROUP ** -0.5),
        'pool_scale': 1.0 + nrm((L, D_POOL), 0.02),
        'w_branch_b': nrm((L, D_POOL, D_MODEL), D_POOL ** -0.5),
        'w_out': nrm((L, D_MODEL, D_MODEL), D_MODEL ** -0.5),
        'norm_post_mix': 1.0 + nrm((L, D_MODEL), 0.02),
        'norm_pre_ffn': 1.0 + nrm((L, D_MODEL), 0.02),
        'w_ffn_in': nrm((L, D_MODEL, 2 * D_FF), D_MODEL ** -0.5),
        'conv_w': nrm((L, CONV_W, D_FF), CONV_W ** -0.5),
        'conv_b': nrm((L, D_FF), 0.02),
        'w_ffn_out': nrm((L, D_FF, D_MODEL), D_FF ** -0.5),
        'norm_post_ffn': 1.0 + nrm((L, D_MODEL), 0.02),
    }


def reference(x_prompt, x_sample, state_shift, state_wkv, state_pool, state_conv,
              norm_pre_mix, w_in, mu_shift, w0, w2, a0, a2, g2, k_k, k_a, r_k, lnx_w, lnx_b,
              w_branch_a, pool_w, pool_scale, w_branch_b, w_out, norm_post_mix,
              norm_pre_ffn, w_ffn_in, conv_w, conv_b, w_ffn_out, norm_post_ffn):
    weights = (norm_pre_mix, w_in, mu_shift, w0, w2, a0, a2, g2, k_k, k_a, r_k, lnx_w, lnx_b,
               w_branch_a, pool_w, pool_scale, w_branch_b, w_out, norm_post_mix,
               norm_pre_ffn, w_ffn_in, conv_w, conv_b, w_ffn_out, norm_post_ffn)
    bp = x_prompt.shape[0]
    dt = x_prompt.dtype
    yp, ys = x_prompt, x_sample
    p_shift, p_wkv, p_pool, p_conv = [], [], [], []
    s_shift, s_wkv, s_pool, s_conv = [], [], [], []
    for l in range(DEPTH):
        p = tuple(w[l] for w in weights)
        yp, a1, a2_, a3, a4 = _layer(
            yp,
            jnp.zeros((bp, 1, D_SHIFT), dt),
            jnp.zeros((bp, N_HEADS, HEAD_DIM, HEAD_DIM), state_wkv.dtype),
            jnp.zeros((bp, POOL_HIST, D_POOL), dt),
            jnp.zeros((bp, CONV_W - 1, D_FF), dt),
            0, *p)
        p_shift.append(a1); p_wkv.append(a2_); p_pool.append(a3); p_conv.append(a4)
        ys, b1, b2, b3, b4 = _layer(ys, state_shift[l], state_wkv[l], state_pool[l], state_conv[l],
                                    PAST_LEN, *p)
        s_shift.append(b1); s_wkv.append(b2); s_pool.append(b3); s_conv.append(b4)
    return (yp, ys,
            jnp.stack(p_shift), jnp.stack(p_wkv), jnp.stack(p_pool), jnp.stack(p_conv),
            jnp.stack(s_shift), jnp.stack(s_wkv), jnp.stack(s_pool), jnp.stack(s_conv))
```

```python
import os
import numpy as np
import concourse.bass as bass
import concourse.mybir as mybir
from contextlib import ExitStack
from concourse.bass_utils import run_bass_kernel_spmd

F32 = mybir.dt.float32
BF16 = mybir.dt.bfloat16
AF = mybir.ActivationFunctionType
ALU = mybir.AluOpType
AX = mybir.AxisListType

PE, ACT, DVE, POOL, SP = "pe", "act", "dve", "pool", "sp"
NDMASEM = 12
SAME_ENGINE_NOWAIT = os.environ.get('SENW', '0') == '1'


class Sched:
    def __init__(self, nc, es):
        self.nc = nc
        self.q = {e: [] for e in (PE, ACT, DVE, POOL, SP)}
        self.cnt = {e: 0 for e in (PE, ACT, DVE, POOL)}
        self.sem = {e: es.enter_context(nc.semaphore("s_" + e)) for e in (PE, ACT, DVE, POOL)}
        self.dsem = {}
        self.dcnt = {}
        for e in (SP, "spo", "bg", POOL):
            self.dsem[e] = [es.enter_context(nc.semaphore("d_%s%d" % (e, i))) for i in range(NDMASEM)]
            self.dcnt[e] = 0
        self.pending = {e: [] for e in (PE, ACT, DVE, POOL, SP)}
        self.tok_ring = {}
        self.waited = {}
        self.lastw = {}
        self.readers = {}
        self.all_tokens = {}

    def _deps(self, eng, reads, writes):
        toks = []
        for k in reads:
            w = self.lastw.get(k)
            if w is not None:
                toks.append(w)
        for k in writes:
            w = self.lastw.get(k)
            if w is not None:
                toks.append(w)
            toks.extend(self.readers.get(k, ()))
        need = {}
        for (s, v, src) in toks:
            if src == eng and (eng == PE or (SAME_ENGINE_NOWAIT and eng in (ACT, DVE))):
                continue
            if self.waited.get((eng, s.name), 0) >= v:
                continue
            if need.get(s.name, (None, 0))[1] < v:
                need[s.name] = (s, v)
        for (s, v) in self.pending[eng]:
            if self.waited.get((eng, s.name), 0) >= v:
                continue
            if need.get(s.name, (None, 0))[1] < v:
                need[s.name] = (s, v)
        self.pending[eng] = []
        out = []
        for name, (s, v) in need.items():
            self.waited[(eng, name)] = v
            out.append((s, v))
        return out

    def barrier(self):
        if self.frozen:
            return
        allw = []
        for e in (PE, ACT, DVE, POOL):
            if self.cnt[e] > 0:
                allw.append((self.sem[e], self.cnt[e], e))
        for name, (s, v) in self.all_tokens.items():
            if self.tok_ring.get(name) != "bg":
                allw.append((s, v, "dma"))
        for eng in (PE, ACT, DVE, POOL, SP):
            for (s, v, src) in allw:
                if src == eng:
                    continue
                self.pending[eng].append((s, v))

    def _commit(self, tok, reads, writes):
        for k in writes:
            self.lastw[k] = tok
            self.readers[k] = []
        for k in reads:
            lst = self.readers.setdefault(k, [])
            lst[:] = [t for t in lst if t[0].name != tok[0].name]
            lst.append(tok)

    frozen = False

    def op(self, eng, fn, reads=(), writes=(), track=True):
        if self.frozen:
            return None
        waits = self._deps(eng, reads, writes)
        tok = None
        if not track:
            assert eng == PE
            self.pend_r = getattr(self, "pend_r", set()) | set(reads)
        if track:
            self.cnt[eng] += 1
            tok = (self.sem[eng], self.cnt[eng], eng)
            if eng == PE and getattr(self, "pend_r", None):
                reads = list(set(reads) | self.pend_r)
                self.pend_r = set()
            self._commit(tok, reads, writes)
        self.q[eng].append((waits, fn, tok))
        return tok

    def dma(self, qeng, out, in_, reads=(), writes=(), ring=None, **kw):
        if self.frozen:
            return None
        waits = self._deps(qeng, reads, writes)
        rk = ring or qeng
        n = self.dcnt[rk]
        self.dcnt[rk] += 1
        s = self.dsem[rk][n % NDMASEM]
        v = 16 * (n // NDMASEM + 1)
        tok = (s, v, "dma")
        self._commit(tok, reads, writes)
        self.q[qeng].append((waits, lambda e: e.dma_start(out=out, in_=in_, **kw), tok))
        self.all_tokens[s.name] = (s, v)
        self.tok_ring[s.name] = rk
        return tok

    def emit(self):
        nc = self.nc
        fin = dict(self.all_tokens)
        for e in (PE, ACT, DVE, POOL):
            if self.cnt[e] > 0:
                fin[self.sem[e].name] = (self.sem[e], self.cnt[e])
        q = self.q
        with nc.Block() as block:
            def run(eng_name):
                def body(e):
                    for (waits, fn, tok) in q[eng_name]:
                        for (s, v) in waits:
                            e.wait_ge(s, v)
                        ins = fn(e)
                        if tok is not None:
                            ins.then_inc(tok[0], 16 if tok[2] == "dma" else 1)
                    if eng_name == SP:
                        for name, (s, v) in fin.items():
                            e.wait_ge(s, v)
                return body
            block.tensor(run(PE))
            block.scalar(run(ACT))
            block.vector(run(DVE))
            block.gpsimd(run(POOL))
            block.sync(run(SP))


D = 2048
DS = 3360
NPR = 2048
NSM = 64
OWN0 = 992
NOWN = 1120
BLK = 512
NH = 512
CH = 64
C_ID, C_BONES, C_ISEL, C_MUS, C_MUI, C_MLS, C_M01 = 0, 128, 256, 320, 448, 576, 704
NCONST = 704 + 512
V_MULR = 0
V_MU = 4
V_W0, V_A0, V_KK, V_KA, V_RK, V_LW, V_LB = 28, 36, 44, 52, 60, 68, 76
V_G1 = 84
V_PS = 100
V_G3 = 108
V_FLAG = 124
V_INVC = 125
V_CW = 189
NV = 189 + 176
DFF = 5632
NJ = 44
DECAY_C = 0.6065306597126334


def host_consts():
    c = np.zeros((128, NCONST), np.float32)
    p = np.arange(128)[:, None]
    j = np.arange(128)[None, :]
    c[:, C_ID:C_ID + 128] = (p == j)
    c[:, C_BONES:C_BONES + 128] = (p // 64 == j // 64)
    c[:, C_ISEL:C_ISEL + 64] = (p % 64 == np.arange(64)[None, :])
    c[:, C_MUS:C_MUS + 128] = (p % 64 < j % 64)
    c[:, C_MUI:C_MUI + 128] = (p % 64 <= j % 64)
    c[:, C_MLS:C_MLS + 128] = (p % 64 > j % 64)
    c[:, C_M01:C_M01 + 512] = (np.arange(512)[None, :] % 64 != 0)
    return c


def permA():
    idx = list(range(3072, 3360))
    for q in range(8):
        idx += list(range(q * 128, q * 128 + 128))
        idx += list(range(1024 + q * 128, 1024 + q * 128 + 128))
        idx += list(range(2048 + q * 128, 2048 + q * 128 + 128))
    return np.array(idx)


def host_pvec(inp):
    v = np.zeros((128, NV), np.float32)
    mu = inp["mu_shift"][0]
    v[0:64, 0] = mu[3072:3136]
    v[0:64, 1] = mu[3136:3200]
    v[0:128, 2] = mu[3200:3328]
    v[0:32, 3] = mu[3328:3360]
    for q in range(8):
        for t in range(3):
            v[:, V_MU + 3 * q + t] = mu[t * 1024 + q * 128: t * 1024 + q * 128 + 128]
    for (col, name) in ((V_W0, "w0"), (V_A0, "a0"), (V_KK, "k_k"), (V_KA, "k_a"), (V_RK, "r_k"), (V_LW, "lnx_w"), (V_LB, "lnx_b")):
        a = inp[name][0].reshape(-1)
        for q in range(8):
            v[:, col + q] = a[q * 128:(q + 1) * 128]
    g = inp["norm_pre_mix"][0]
    g3 = inp["norm_pre_ffn"][0]
    for k in range(16):
        v[:, V_G1 + k] = g[k * 128:(k + 1) * 128]
        v[:, V_G3 + k] = g3[k * 128:(k + 1) * 128]
    psc = inp["pool_scale"][0]
    for k in range(8):
        v[:, V_PS + k] = psc[k * 128:(k + 1) * 128]
    cw, cbias = inp["conv_w"][0], inp["conv_b"][0]
    for j in range(NJ):
        for t in range(3):
            v[:, V_CW + 4 * j + t] = cw[t, j * 128:(j + 1) * 128]
        v[:, V_CW + 4 * j + 3] = cbias[j * 128:(j + 1) * 128]
    return v


def core_pvec(base, p):
    v = base.copy()
    v[:, V_FLAG] = float(p)
    for gi, win in enumerate((2, 4, 8, 16)):
        for j in range(16):
            pos = p * 1024 + j
            v[:, V_INVC + gi * 16 + j] = 1.0 / min(win, pos + 1)
    return v


class _TAlias:
    def __init__(self, phys, alias, prefix="t_"):
        self.phys = phys
        self.alias = alias
        self.prefix = prefix

    def _n(self, n):
        return self.alias.get(n, n)

    def __getitem__(self, n):
        return self.phys[self._n(n)]

    def key(self, n):
        return self.prefix + self._n(n)


class StopBuild(Exception):
    pass


class Builder:
    stop_at = None

    def ckpt(self, name):
        if self.stop_at == name and not self.S.frozen:
            print("frozen at", name)
            self.S.frozen = True
            self.dbg = set()

    def __init__(self, dbg=None, stages="A"):
        self.dbg = dbg or set()
        self.stages = stages
        self.nc = bass.Bass("TRN2", target_bir_lowering=False)
        self.ins = {}
        self.outs = {}
        self.psn = 0

    def din(self, name, shape, dt=F32):
        t = self.nc.dram_tensor(name, list(shape), dt, kind="ExternalInput").ap()
        self.ins[name] = t
        return t

    def dout(self, name, shape, dt=F32):
        t = self.nc.dram_tensor(name, list(shape), dt, kind="ExternalOutput").ap()
        self.outs[name] = t
        return t

    def I(self, eng, meth, *args, reads=(), writes=(), track=True, **kw):
        return self.S.op(eng, lambda e: getattr(e, meth)(*args, **kw), reads=reads, writes=writes, track=track)

    def stage_out(self, name, tile_ap, shape, key):
        scr = self.nc.dram_tensor("scr_" + name, list(shape), F32).ap()
        self.S.dma(SP, scr, tile_ap, reads=[key], writes=["scr_" + name], ring="spo")
        return scr

    def bg(self, dst, src, name):
        self.S.dma(SP, dst, src, reads=["scr_" + name], ring="bg", allow_slow_non_contiguous=True)

    def sb(self, es, name, shape, dt=F32):
        return es.enter_context(self.nc.sbuf_tensor(name, list(shape), dt))

    def psum(self):
        i = self.psn % 8
        self.psn += 1
        return self.PS[i], "ps%d" % i

    def build(self):
        nc = self.nc
        xseq = self.xseq = self.din("xseq", [NPR, D])
        xsamp = self.xsamp = self.din("xsamp", [NSM, D])
        wA = self.din("wA", [D, DS])
        consts = self.din("consts", [128, NCONST])
        pvec = self.din("pvec", [128, NV])
        w2 = self.din("w2", [64, 1024])
        a2 = self.din("a2", [64, 1024])
        g2 = self.din("g2", [160, 1024])
        self.wP = self.din("wP", [D, 1024])
        self.wG = self.din("wG", [D, 16, 256])
        self.wa = self.din("wa", [1024, D])
        self.wbr = self.din("wbr", [1024, D])
        self.poolw = self.din("poolw", [4, 256, 256])
        self.wout = self.din("wout", [D, D])
        self.gpost = self.din("gpost", [128, 2, D])
        self.wF = self.din("wF", [D, NJ, 256])
        self.wfo = self.din("wfo", [DFF, D])
        self.st_pool = self.din("st_pool", [16, 15, 1024])
        self.st_conv = self.din("st_conv", [16, 2, DFF])
        self.o_pshift = self.dout("o_pshift", [DS])
        self.o_pwkv = self.dout("o_pwkv", [16, 64, 64])
        self.o_ppool = self.dout("o_ppool", [15, 1024])
        self.o_pconv = self.dout("o_pconv", [2, DFF])
        self.o_spool = self.dout("o_spool", [16, 15, 1024])
        self.o_sconv = self.dout("o_sconv", [16, 2, DFF])
        self.o_y = self.dout("o_y", [1024 + NSM, D])
        self.st_shift = self.din("st_shift", [16, DS])
        self.st_wkv = self.din("st_wkv", [256, 4096])
        self.o_sshift = self.dout("o_sshift", [16, DS])
        self.o_swkv = self.dout("o_swkv", [256, 4096])
        self.x1s = nc.dram_tensor("x1s", [NOWN, D], F32).ap()
        self.scrS = nc.dram_tensor("scrS", [NSM, 6, 1024], F32).ap()
        self.scrY = nc.dram_tensor("scrY", [NSM, 1024], F32).ap()
        self.scrH = nc.dram_tensor("scrH", [128, 16, NOWN], BF16).ap()
        if "ya" in self.dbg:
            self.o_ya = self.dout("d_ya", [128, 8, NOWN])
        with ExitStack() as es:
            S = self.S = Sched(nc, es)
            self.PS = [es.enter_context(nc.psum_tensor("ps%d" % i, [128, 512], F32)) for i in range(8)]
            cf = self.cf = self.sb(es, "cf", [128, NCONST])
            cb = self.cb = self.sb(es, "cb", [128, 320], BF16)
            pv = self.pv = self.sb(es, "pv", [128, NV])
            S.dma(SP, cf[:], consts, writes=["cf"])
            S.dma(SP, pv[:], pvec, writes=["pv"])
            S.dma(POOL, cb[:], consts[:, 0:320], writes=["cb"])
            self.wslot = 0
            bufX = self.bufX = self.sb(es, "bufX", [128, 16, NOWN], BF16)
            self.yaT = bufX[:, 0:8, :]
            self.ybT = bufX[:, 8:16, :]
            with ExitStack() as esw:
                self.wb = [self.sb(esw, "wb%d" % i, [128, 16, 384], BF16) for i in range(2)]
                self.wA_ap = wA
                with ExitStack() as esA:
                    w2b = self.w2b = self.sb(esA, "w2b", [64, 1024], BF16)
                    a2b = self.a2b = self.sb(esA, "a2b", [64, 1024], BF16)
                    g2a = self.g2a = self.sb(esA, "g2a", [128, 1024], BF16)
                    g2b = self.g2b = self.sb(esA, "g2b", [128, 1024], BF16)
                    S.dma(POOL, w2b[:], w2, writes=["w2b"])
                    S.dma(POOL, a2b[:], a2, writes=["a2b"])
                    S.dma(POOL, g2a[:], g2[0:128, :], writes=["g2a"])
                    self.I(POOL, "memset", g2b[:], 0.0, writes=["g2b"])
                    S.dma(POOL, g2b[0:32, :], g2[128:160, :], writes=["g2b"])
                    if "S" in self.stages:
                        self.stageS()
                    self.stageA(esA, xseq, wA)
                if "ya" in self.dbg:
                    with ExitStack() as esd:
                        yaf = self.sb(esd, "yaf", [128, 8, NOWN], F32)
                        self.I(DVE, "tensor_copy", yaf[:], self.yaT, reads=["yaT"], writes=["yaf"])
                        S.dma(SP, self.o_ya, yaf[:], reads=["yaf"], ring="spo")
                        S.barrier()
                if "B" in self.stages:
                    with ExitStack() as esm:
                        self.mT = self.sb(esm, "mT", [128, 16, NOWN], BF16)
                        with ExitStack() as esb:
                            self.hTo = self.sb(esb, "hTo", [128, 16, NOWN], BF16)
                            self.stageB12(esb)
                        self.stageB3(esm)
            if "C" in self.stages:
                self.stageC(es)
            S.emit()
        return nc

    def tok_groups(self):
        return [(0, 512), (512, 512), (1024, NOWN - 1024)]

    def stageB12(self, es_outer):
        S = self.S
        pv, cf, cb = self.pv, self.cf, self.cb
        hT = self.hTo
        with ExitStack() as es:
            sb = lambda n, s, d=F32: self.sb(es, n, s, d)
            self.xt = [sb("xtB", [128, D])]
            self.xb = [sb("xbB", [128, D], BF16)]
            self.xst = [sb("xstB", [128, 4])]
            for (c0, c1, k) in ((0, 32, "scrH_1"), (32, 544, "scrH_2"), (544, 1056, "scrH_3"), (1056, NOWN, "scrH_s")):
                S.dma(SP, hT[:, :, c0:c1], self.scrH[:, :, c0:c1], reads=[k], writes=["hTo"])
            self.ckpt("B0")
            ybT = self.ybT
            with ExitStack() as esp:
                sbp = lambda n, s, d=F32: self.sb(esp, n, s, d)
                pwb = sbp("pwb", [128, 4, 2, 256], BF16)
                S.dma(POOL, pwb[:], self.poolw.rearrange("g (k p) n -> p g k n", p=128), writes=["pwb"])
                phT = sbp("phT", [128, 8, 16, 15])
                for half in range(2):
                    sp_t = sbp("sp_t%d" % half, [120, 1024])
                    S.dma(SP, sp_t[:], self.st_pool[8 * half: 8 * half + 8].rearrange("b j c -> (b j) c"), writes=["sp_t%d" % half])
                    for c4 in range(2):
                        ps, pk = self.psum()
                        for ch in range(4):
                            self.I(PE, "transpose", ps[:, ch * 120:(ch + 1) * 120], sp_t[:, (c4 * 4 + ch) * 128:(c4 * 4 + ch + 1) * 128], cf[0:120, 0:120],
                                   reads=["sp_t%d" % half, "cf"], writes=[pk], track=(ch == 3))
                        self.I(DVE, "tensor_copy", phT[:, c4 * 4:c4 * 4 + 4, 8 * half:8 * half + 8, :],
                               ps[:, 0:480].rearrange("p (a b j) -> p a b j", a=4, b=8), reads=[pk], writes=["phT"])
                self.ckpt("B1a")
                S.dma(SP, self.o_spool[:, 0:11, :], self.st_pool[:, 4:15, :], ring="spo")
                zp = sbp("zp", [128, NOWN])
                pa_ = sbp("ppA", [128, NOWN])
                pb_ = sbp("ppB", [128, NOWN])
                dT = sbp("dT", [128, 8, NOWN], BF16)
                self.I(POOL, "memset", dT[:], 0.0, writes=["dT"])
                ppo = sbp("ppo", [128, 8, 15])
                spo = sbp("spo", [128, 8, 16, 4])
                bs = sbp("bs", [128, 16, 19])
                bs2 = sbp("bs2", [128, 16, 19])
                slabs = [(self.wP[:, ch * 128:(ch + 1) * 128], 128) for ch in range(8)]
                stream = self.slab_stream(slabs)
                NP_ = 1056
                for ch in range(8):
                    wt, wk = next(stream)
                    gi = ch // 2
                    win = 2 << gi
                    for (t0, tn) in self.tok_groups():
                        ps, pk = self.psum()
                        for k in range(16):
                            self.I(PE, "matmul", ps[:, 0:tn], wt[:, k, 0:128], hT[:, k, t0:t0 + tn], start=(k == 0), stop=(k == 15),
                                   reads=[wk, "hTo"], writes=[pk], track=(k == 15))
                        self.I(ACT, "copy", zp[:, t0:t0 + tn], ps[:, 0:tn], reads=[pk], writes=["zp"])
                    src, ksrc = zp, "zp"
                    bufs = [(pa_, "ppA"), (pb_, "ppB")]
                    for j in range(gi + 1):
                        sh = 1 << j
                        dst, kdst = bufs[j % 2]
                        self.I(DVE if j % 2 == 0 else POOL, "tensor_tensor", dst[:, sh:NP_], src[:, sh:NP_], src[:, 0:NP_ - sh], ALU.add,
                               reads=[ksrc], writes=[kdst])
                        src, ksrc = dst, kdst
                    lo = win - 1
                    self.I(DVE, "scalar_tensor_tensor", dT[:, ch, lo:NP_], src[:, lo:NP_], 1.0 / win, zp[:, lo:NP_], ALU.mult, ALU.subtract,
                           reads=[ksrc, "zp"], writes=["dT"])
                    other, kother = bufs[(gi + 1) % 2]
                    self.I(POOL, "tensor_tensor", other[:, 32:48], src[:, 32:48], pv[:, V_INVC + gi * 16: V_INVC + gi * 16 + 16], ALU.mult,
                           reads=[ksrc, "pv"], writes=[kother])
                    self.I(POOL, "tensor_tensor", dT[:, ch, 32:48], other[:, 32:48], zp[:, 32:48], ALU.subtract, reads=[kother, "zp"], writes=["dT"])
                    self.I(POOL, "tensor_copy", ppo[:, ch, :], zp[:, NP_ - 15:NP_], reads=["zp"], writes=["ppo"])
                    zps = zp[:, NP_:NOWN].rearrange("p (b t) -> p b t", t=4)
                    self.I(POOL, "tensor_copy", bs[:, :, 0:15], phT[:, ch, :, :], reads=["phT"], writes=["bs"])
                    self.I(POOL, "tensor_copy", bs[:, :, 15:19], zps, reads=["zp"], writes=["bs"])
                    self.I(POOL, "tensor_copy", spo[:, ch, :, :], zps, reads=["zp"], writes=["spo"])
                    ssrc, kss = bs, "bs"
                    sbufs = [(bs2, "bs2"), (bs, "bs")]
                    for j in range(gi + 1):
                        sh = 1 << j
                        dst, kdst = sbufs[j % 2]
                        self.I(DVE, "tensor_tensor", dst[:, :, sh:19], ssrc[:, :, sh:19], ssrc[:, :, 0:19 - sh], ALU.add, reads=[kss], writes=[kdst])
                        ssrc, kss = dst, kdst
                    self.I(DVE, "scalar_tensor_tensor", dT[:, ch, NP_:NOWN].rearrange("p (b t) -> p b t", t=4), ssrc[:, :, 15:19], 1.0 / win, zps,
                           ALU.mult, ALU.subtract, reads=[kss, "zp"], writes=["dT"])
                self.ckpt("B1b")
                scr = self.stage_out("ppo", ppo[:], [128, 8, 15], "ppo")
                for ch in range(8):
                    self.bg(self.o_ppool[:, ch * 128:(ch + 1) * 128].rearrange("j p -> p j"), scr[:, ch, :], "ppo")
                sprow = sbp("sprow", [NSM, 1024])
                for c4 in range(2):
                    ps, pk = self.psum()
                    for ch in range(4):
                        self.I(PE, "transpose", ps[0:NSM, ch * 128:(ch + 1) * 128], spo[:, c4 * 4 + ch, :, :].rearrange("p b t -> p (b t)"), cf[:, C_ID:C_ID + 128],
                               reads=["spo", "cf"], writes=[pk], track=(ch == 3))
                    self.I(DVE, "tensor_copy", sprow[:, c4 * 512:(c4 + 1) * 512], ps[0:NSM, :], reads=[pk], writes=["sprow"])
                for b in range(16):
                    S.dma(SP, self.o_spool[b, 11:15, :], sprow[4 * b:4 * b + 4, :], reads=["sprow"], ring="spo")
                self.ckpt("B1c")
                for oc in range(8):
                    gi, o2 = oc // 2, oc % 2
                    for (t0, tn) in self.tok_groups():
                        ps, pk = self.psum()
                        for kk in range(2):
                            self.I(PE, "matmul", ps[:, 0:tn], pwb[:, gi, kk, o2 * 128:(o2 + 1) * 128], dT[:, 2 * gi + kk, t0:t0 + tn],
                                   start=(kk == 0), stop=(kk == 1), reads=["pwb", "dT"], writes=[pk], track=(kk == 1))
                        self.I(ACT, "activation", ybT[:, oc, t0:t0 + tn], ps[:, 0:tn], AF.Copy, scale=pv[:, V_PS + oc: V_PS + oc + 1],
                               reads=[pk, "pv"], writes=["ybT"])
                S.barrier()
            self.ckpt("B1d")
            mT = self.mT
            tmp = [sb("b2t%d" % i, [128, 512]) for i in range(4)]
            njs = 16

            def issue(j):
                slot = self.wslot % 2
                self.wslot += 1
                key = "wb%d" % slot
                w = self.wb[slot]
                wflat = w[:].rearrange("p a b -> p (a b)")
                S.dma(POOL, wflat[:, 0:4096].rearrange("p (k n) -> p k n", n=256), self.wG[:, j, :].rearrange("(k p) n -> p k n", p=128), writes=[key])
                S.dma(POOL, wflat[:, 4096:5120].rearrange("p (k n) -> p k n", n=128), self.wa[:, j * 128:(j + 1) * 128].rearrange("(k p) n -> p k n", p=128), writes=[key])
                S.dma(POOL, wflat[:, 5120:6144].rearrange("p (k n) -> p k n", n=128), self.wbr[:, j * 128:(j + 1) * 128].rearrange("(k p) n -> p k n", p=128), writes=[key])
                return (wflat, key)
            cur = issue(0)
            for j in range(njs):
                nxt = issue(j + 1) if j + 1 < njs else None
                wflat, wk = cur
                wg = wflat[:, 0:4096].rearrange("p (k n) -> p k n", n=256)
                wa_ = wflat[:, 4096:5120].rearrange("p (k n) -> p k n", n=128)
                wb_ = wflat[:, 5120:6144].rearrange("p (k n) -> p k n", n=128)
                for (t0, tn) in self.tok_groups():
                    psa, pka = self.psum()
                    psb, pkb = self.psum()
                    ppa, pkpa = self.psum()
                    ppb, pkpb = self.psum()
                    for k in range(16):
                        self.I(PE, "matmul", psa[:, 0:tn], wg[:, k, 0:128], hT[:, k, t0:t0 + tn], start=(k == 0), stop=(k == 15),
                               reads=[wk, "hTo"], writes=[pka], track=(k == 15))
                    for k in range(16):
                        self.I(PE, "matmul", psb[:, 0:tn], wg[:, k, 128:256], hT[:, k, t0:t0 + tn], start=(k == 0), stop=(k == 15),
                               reads=[wk, "hTo"], writes=[pkb], track=(k == 15))
                    for k in range(8):
                        self.I(PE, "matmul", ppa[:, 0:tn], wa_[:, k, :], self.yaT[:, k, t0:t0 + tn], start=(k == 0), stop=(k == 7),
                               reads=[wk, "yaT"], writes=[pkpa], track=(k == 7))
                    for k in range(8):
                        self.I(PE, "matmul", ppb[:, 0:tn], wb_[:, k, :], ybT[:, k, t0:t0 + tn], start=(k == 0), stop=(k == 7),
                               reads=[wk, "ybT"], writes=[pkpb], track=(k == 7))
                    self.I(ACT, "activation", tmp[0][:, 0:tn], psa[:, 0:tn], AF.Sigmoid, reads=[pka], writes=["b2t0"])
                    self.I(ACT, "activation", tmp[1][:, 0:tn], psb[:, 0:tn], AF.Sigmoid, reads=[pkb], writes=["b2t1"])
                    self.I(DVE, "tensor_tensor", tmp[2][:, 0:tn], tmp[0][:, 0:tn], ppa[:, 0:tn], ALU.mult, reads=["b2t0", pkpa], writes=["b2t2"])
                    self.I(DVE, "tensor_tensor", tmp[3][:, 0:tn], tmp[1][:, 0:tn], ppb[:, 0:tn], ALU.mult, reads=["b2t1", pkpb], writes=["b2t3"])
                    self.I(POOL, "tensor_tensor", mT[:, j, t0:t0 + tn], tmp[2][:, 0:tn], tmp[3][:, 0:tn], ALU.add, reads=["b2t2", "b2t3"], writes=["mT"])
                cur = nxt
            S.barrier()

    def tok_tiles(self):
        self.ckpt("B2")
        return [(0, 32)] + [(32 + 128 * i, 128) for i in range(8)] + [(1056, NSM)]

    def stageB3(self, es_outer):
        S = self.S
        pv, cf, cb = self.pv, self.cf, self.cb
        mT, h2T = self.mT, self.bufX
        with ExitStack() as es:
            sb = lambda n, s, d=F32: self.sb(es, n, s, d)
            woutb = sb("woutb", [128, 16, D], BF16)
            for n in range(4):
                S.dma(POOL, woutb[:, :, n * 512:(n + 1) * 512], self.wout[:, n * 512:(n + 1) * 512].rearrange("(k p) n -> p k n", p=128), writes=["woutb%d" % n])
            gp = sb("gp", [128, D])
            S.dma(SP, gp[:], self.gpost[:, 0, :], writes=["gp"])
            mo = sb("mo", [128, D])
            xt = sb("xt3", [128, D])
            x1 = sb("x1t", [128, D])
            xb = sb("xb3", [128, D], BF16)
            st = sb("st3", [128, 16])
            for (c0, nt) in self.tok_tiles():
                xrows = self.xseq[OWN0 + c0: OWN0 + c0 + nt, :] if c0 < 1056 else self.xsamp
                S.dma(SP, xt[0:nt, :], xrows, writes=["xt3"])
                self.I(POOL, "memset", st[:, 0:4], 0.0, writes=["st3"])
                self.ckpt("B3pre")
                for n in range(4):
                    ps, pk = self.psum()
                    for k in range(16):
                        self.I(PE, "matmul", ps[0:nt, :], mT[:, k, c0:c0 + nt], woutb[:, k, n * 512:(n + 1) * 512], start=(k == 0), stop=(k == 15),
                               reads=["mT", "woutb%d" % n], writes=[pk], track=(k == 15))
                    self.I(DVE, "tensor_copy", mo[0:nt, n * 512:(n + 1) * 512], ps[0:nt, :], reads=[pk], writes=["mo"])
                    self.I(ACT, "activation", xb[0:nt, n * 512:(n + 1) * 512], mo[0:nt, n * 512:(n + 1) * 512], AF.Square, accum_out=st[0:nt, n:n + 1], reads=["mo"], writes=["xb3", "st3"])
                self.ckpt("B3a")
                self.I(DVE, "tensor_reduce", st[0:nt, 4:5], st[0:nt, 0:4], AX.X, ALU.add, reads=["st3"], writes=["st3"])
                self.I(DVE, "tensor_scalar", st[0:nt, 5:6], st[0:nt, 4:5], 1.0 / D, 1e-6, ALU.mult, ALU.add, reads=["st3"], writes=["st3"])
                self.I(ACT, "activation", st[0:nt, 6:7], st[0:nt, 5:6], AF.Sqrt, reads=["st3"], writes=["st3"])
                self.I(DVE, "reciprocal", st[0:nt, 7:8], st[0:nt, 6:7], reads=["st3"], writes=["st3"])
                self.I(DVE, "scalar_tensor_tensor", mo[0:nt, :], mo[0:nt, :], st[0:nt, 7:8], gp[0:nt, :], ALU.mult, ALU.mult, reads=["mo", "st3", "gp"], writes=["mo"])
                self.I(POOL, "tensor_tensor", x1[0:nt, :], mo[0:nt, :], xt[0:nt, :], ALU.add, reads=["mo", "xt3"], writes=["x1t"])
                self.ckpt("B3b")
                S.dma(SP, self.x1s[c0:c0 + nt, :], x1[0:nt, :], reads=["x1t"], writes=["x1s"])
                self.ckpt("B3c")
                self.norm_T(x1, "x1t", nt, h2T[:, :, c0:c0 + nt], "h2T", xb, "xb3", st, "st3", 8, V_G3)
                self.ckpt("B3d")
            S.barrier()

    def norm_T(self, xt, kx, ntok, hT, hkey, xb, kb, st, ks, sc0, gbase):
        self.I(POOL, "memset", st[:, sc0:sc0 + 1], 0.0, writes=[ks])
        self.I(ACT, "activation", xb[0:ntok, :], xt[0:ntok, :], AF.Square, accum_out=st[0:ntok, sc0:sc0 + 1], reads=[kx], writes=[kb, ks])
        self.I(DVE, "tensor_scalar", st[0:ntok, sc0 + 1:sc0 + 2], st[0:ntok, sc0:sc0 + 1], 1.0 / D, 1e-6, ALU.mult, ALU.add, reads=[ks], writes=[ks])
        self.I(ACT, "activation", st[0:ntok, sc0 + 2:sc0 + 3], st[0:ntok, sc0 + 1:sc0 + 2], AF.Sqrt, reads=[ks], writes=[ks])
        self.I(DVE, "reciprocal", st[0:ntok, sc0 + 3:sc0 + 4], st[0:ntok, sc0 + 2:sc0 + 3], reads=[ks], writes=[ks])
        self.I(ACT, "activation", xb[0:ntok, :], xt[0:ntok, :], AF.Copy, scale=st[0:ntok, sc0 + 3:sc0 + 4], reads=[kx, ks, kb], writes=[kb])
        for half in range(2):
            ps, pk = self.psum()
            pT = ps[:].bitcast(BF16).rearrange("p (a b) -> p a b", b=128)
            for k8 in range(8):
                kc = half * 8 + k8
                self.I(PE, "transpose", pT[:, k8, 0:ntok], xb[0:ntok, kc * 128:(kc + 1) * 128], self.cb[0:ntok, 0:ntok],
                       reads=[kb, "cb"], writes=[pk], track=(k8 == 7))
            gcol = self.pv[:, gbase + half * 8: gbase + half * 8 + 8].unsqueeze(2).to_broadcast([128, 8, ntok])
            self.I(DVE, "tensor_tensor", hT[:, half * 8:half * 8 + 8, :], pT[:, :, 0:ntok], gcol, ALU.mult, reads=[pk, "pv"], writes=[hkey])

    def stageC(self, es_outer):
        S = self.S
        pv, cf, cb = self.pv, self.cf, self.cb
        h2T = self.bufX
        NA = 1024 + NSM
        with ExitStack() as es:
            sb = lambda n, s, d=F32: self.sb(es, n, s, d)
            actT = sb("actT", [128, NJ, NA], BF16)
            with ExitStack() as es1:
                sb1 = lambda n, s, d=F32: self.sb(es1, n, s, d)
                wf = [sb1("wf%d" % i, [128, 16, 256], BF16) for i in range(2)]
                gt = [sb1("gt%d" % i, [128, NOWN]) for i in range(2)]
                up = [sb1("up%d" % i, [128, NOWN]) for i in range(2)]
                cv = sb1("cv", [128, NA])
                ge = sb1("ge", [128, NA])
                gs6 = sb1("gs6", [128, 16, 6])
                chT = sb1("chT", [128, NJ, 32])
                pco = sb1("pco", [128, NJ, 2])
                sco = sb1("sco", [128, NJ, 16, 2])
                ct = sb1("ct", [32, 1408])
                stc = self.st_conv.rearrange("b r c -> (b r) c")
                for pc in range(4):
                    S.dma(SP, ct[:], stc[:, pc * 1408:(pc + 1) * 1408], writes=["ct"])
                    ps, pk = self.psum()
                    for jj in range(11):
                        self.I(PE, "transpose", ps[:, jj * 32:(jj + 1) * 32], ct[:, jj * 128:(jj + 1) * 128], cf[0:32, 0:32],
                               reads=["ct", "cf"], writes=[pk], track=(jj == 10))
                    self.I(DVE, "tensor_copy", chT[:, pc * 11:(pc + 1) * 11, :], ps[:, 0:352].rearrange("p (a b) -> p a b", b=32), reads=[pk], writes=["chT"])

                def issue(j):
                    slot = j % 2
                    S.dma(POOL, wf[slot][:], self.wF[:, j, :].rearrange("(k p) n -> p k n", p=128), writes=["wf%d" % slot])
                cwc = lambda j, t: pv[:, V_CW + 4 * j + t: V_CW + 4 * j + t + 1]
                issue(0)
                for j in range(NJ):
                    if j + 1 < NJ:
                        issue(j + 1)
                    w, wk = wf[j % 2], "wf%d" % (j % 2)
                    g_, kg = gt[j % 2], "gt%d" % (j % 2)
                    u_, ku = up[j % 2], "up%d" % (j % 2)
                    for (t0, tn) in self.tok_groups():
                        psg, pkg = self.psum()
                        psu, pku = self.psum()
                        for k in range(16):
                            self.I(PE, "matmul", psg[:, 0:tn], w[:, k, 0:128], h2T[:, k, t0:t0 + tn], start=(k == 0), stop=(k == 15),
                                   reads=[wk, "h2T"], writes=[pkg], track=(k == 15))
                        for k in range(16):
                            self.I(PE, "matmul", psu[:, 0:tn], w[:, k, 128:256], h2T[:, k, t0:t0 + tn], start=(k == 0), stop=(k == 15),
                                   reads=[wk, "h2T"], writes=[pku], track=(k == 15))
                        self.I(ACT, "copy", g_[:, t0:t0 + tn], psg[:, 0:tn], reads=[pkg], writes=[kg])
                        self.I(DVE, "tensor_copy", u_[:, t0:t0 + tn], psu[:, 0:tn], reads=[pku], writes=[ku])
                    self.I(POOL, "tensor_scalar", g_[:, 30:32], g_[:, 30:32], pv[:, V_FLAG:V_FLAG + 1], None, ALU.mult, reads=[kg, "pv"], writes=[kg])
                    self.I(ACT, "activation", cv[:, 0:1024], g_[:, 32:1056], AF.Identity, bias=cwc(j, 3), scale=cwc(j, 2), reads=[kg, "pv"], writes=["cv"])
                    self.I(DVE, "scalar_tensor_tensor", cv[:, 0:1024], g_[:, 31:1055], cwc(j, 1), cv[:, 0:1024], ALU.mult, ALU.add, reads=[kg, "cv", "pv"], writes=["cv"])
                    self.I(DVE, "scalar_tensor_tensor", cv[:, 0:1024], g_[:, 30:1054], cwc(j, 0), cv[:, 0:1024], ALU.mult, ALU.add, reads=[kg, "cv", "pv"], writes=["cv"])
                    gss = g_[:, 1056:NOWN].rearrange("p (b t) -> p b t", t=4)
                    self.I(POOL, "tensor_copy", gs6[:, :, 0:2], chT[:, j, :].rearrange("p (b r) -> p b r", r=2), reads=["chT"], writes=["gs6"])
                    self.I(POOL, "tensor_copy", gs6[:, :, 2:6], gss, reads=[kg], writes=["gs6"])
                    cvs = cv[:, 1024:NA].rearrange("p (b t) -> p b t", t=4)
                    self.I(ACT, "activation", cvs, gs6[:, :, 2:6], AF.Identity, bias=cwc(j, 3), scale=cwc(j, 2), reads=["gs6", "pv"], writes=["cv"])
                    self.I(DVE, "scalar_tensor_tensor", cvs, gs6[:, :, 1:5], cwc(j, 1), cvs, ALU.mult, ALU.add, reads=["gs6", "cv", "pv"], writes=["cv"])
                    self.I(DVE, "scalar_tensor_tensor", cvs, gs6[:, :, 0:4], cwc(j, 0), cvs, ALU.mult, ALU.add, reads=["gs6", "cv", "pv"], writes=["cv"])
                    self.I(POOL, "tensor_copy", pco[:, j, :], g_[:, 1054:1056], reads=[kg], writes=["pco"])
                    self.I(POOL, "tensor_copy", sco[:, j, :, :], gss[:, :, 2:4], reads=[kg], writes=["sco"])
                    self.I(ACT, "activation", ge[:, :], cv[:, :], AF.Gelu_apprx_tanh, reads=["cv"], writes=["ge"])
                    self.I(DVE, "tensor_tensor", actT[:, j, 0:1024], ge[:, 0:1024], u_[:, 32:1056], ALU.mult, reads=["ge", ku], writes=["actT"])
                    self.I(POOL, "tensor_tensor", actT[:, j, 1024:NA], ge[:, 1024:NA], u_[:, 1056:NOWN], ALU.mult, reads=["ge", ku], writes=["actT"])
                scr = self.stage_out("pco", pco[:], [128, NJ, 2], "pco")
                for r in range(2):
                    self.bg(self.o_pconv[r].rearrange("(j p) -> p j", p=128), scr[:, :, r], "pco")
                scrow = sb1("scrow", [32, 1408])
                for pc in range(4):
                    for j4 in range(0, 11, 4):
                        nj = min(4, 11 - j4)
                        ps, pk = self.psum()
                        for jj in range(nj):
                            j = pc * 11 + j4 + jj
                            self.I(PE, "transpose", ps[0:32, jj * 128:(jj + 1) * 128], sco[:, j, :, :].rearrange("p b r -> p (b r)"), cf[:, C_ID:C_ID + 128],
                                   reads=["sco", "cf"], writes=[pk], track=(jj == nj - 1))
                        self.I(DVE, "tensor_copy", scrow[:, j4 * 128:(j4 + nj) * 128], ps[0:32, 0:nj * 128], reads=[pk], writes=["scrow"])
                    S.dma(SP, self.o_sconv.rearrange("b r c -> (b r) c")[:, pc * 1408:(pc + 1) * 1408], scrow[:], reads=["scrow"], ring="spo")
                S.barrier()
            with ExitStack() as es2:
                sb2 = lambda n, s, d=F32: self.sb(es2, n, s, d)
                dummy = sb2("dmy2", [128, 2])
                self.I(POOL, "memset", dummy[:], 0.0, writes=["h2T", "dmy2"])
                bxf = self.bufX[:].rearrange("p a b -> p (a b)")
                wo = [bxf[:, i * 5632:(i + 1) * 5632].rearrange("p (k n) -> p k n", n=512) for i in range(2)]
                fo = sb2("fo", [128, 5, D])
                gp2 = sb2("gp2", [128, D])
                S.dma(SP, gp2[:], self.gpost[:, 1, :], writes=["gp2"])
                x1t = sb2("x1r", [128, D])
                st = sb2("stc", [128, 5, 8])
                xb = sb2("xbc", [128, 512], BF16)
                tiles = [(128 * i, 128) for i in range(8)] + [(1024, NSM)]
                sets = [tiles[0:5], tiles[5:9]]
                wn = 0
                for tset in sets:
                    self.I(POOL, "memset", st[:], 0.0, writes=["stc"])
                    for n in range(4):
                        banks = [self.psum() for _ in tset]
                        for kp in range(4):
                            slot = wn % 2
                            wn += 1
                            S.dma(POOL, wo[slot], self.wfo[kp * 1408:(kp + 1) * 1408, n * 512:(n + 1) * 512].rearrange("(k p) n -> p k n", p=128),
                                  writes=["wo%d" % slot])
                            for ti, (c0, nt) in enumerate(tset):
                                ps, pk = banks[ti]
                                for jj in range(11):
                                    j = kp * 11 + jj
                                    self.I(PE, "matmul", ps[0:nt, :], actT[:, j, c0:c0 + nt], wo[slot][:, jj, :], start=(j == 0), stop=(j == NJ - 1),
                                           reads=["actT", "wo%d" % slot], writes=[pk], track=(jj == 10))
                        for ti, (c0, nt) in enumerate(tset):
                            ps, pk = banks[ti]
                            self.I(DVE, "tensor_copy", fo[0:nt, ti, n * 512:(n + 1) * 512], ps[0:nt, :], reads=[pk], writes=["fo%d" % ti])
                            self.I(ACT, "activation", xb[0:nt, :], fo[0:nt, ti, n * 512:(n + 1) * 512], AF.Square, accum_out=st[0:nt, ti, n:n + 1], reads=["fo%d" % ti], writes=["xbc", "stc"])
                    for ti, (c0, nt) in enumerate(tset):
                        self.I(DVE, "tensor_reduce", st[0:nt, ti, 4:5], st[0:nt, ti, 0:4], AX.X, ALU.add, reads=["stc"], writes=["stc"])
                        self.I(DVE, "tensor_scalar", st[0:nt, ti, 5:6], st[0:nt, ti, 4:5], 1.0 / D, 1e-6, ALU.mult, ALU.add, reads=["stc"], writes=["stc"])
                        self.I(ACT, "activation", st[0:nt, ti, 6:7], st[0:nt, ti, 5:6], AF.Sqrt, reads=["stc"], writes=["stc"])
                        self.I(DVE, "reciprocal", st[0:nt, ti, 7:8], st[0:nt, ti, 6:7], reads=["stc"], writes=["stc"])
                        xc0 = 32 + c0
                        S.dma(SP, x1t[0:nt, :], self.x1s[xc0:xc0 + nt, :], reads=["x1s"], writes=["x1r"])
                        self.I(DVE, "scalar_tensor_tensor", fo[0:nt, ti, :], fo[0:nt, ti, :], st[0:nt, ti, 7:8], gp2[0:nt, :], ALU.mult, ALU.mult,
                               reads=["fo%d" % ti, "stc", "gp2"], writes=["fo%d" % ti])
                        self.I(POOL, "tensor_tensor", fo[0:nt, ti, :], fo[0:nt, ti, :], x1t[0:nt, :], ALU.add, reads=["fo%d" % ti, "x1r"], writes=["fo%d" % ti])
                        S.dma(SP, self.o_y[c0:c0 + nt, :], fo[0:nt, ti, :], reads=["fo%d" % ti], ring="spo")

    def make_hT(self, xrows, ntok, hT, hkey, slot):
        S = self.S
        xt, xb, st = self.xt[slot], self.xb[slot], self.xst[slot]
        kx, kb, ks = "xt%d" % slot, "xb%d" % slot, "xst%d" % slot
        S.dma(SP, xt[0:ntok, :], xrows, writes=[kx])
        self.I(POOL, "memset", st[:, 0:1], 0.0, writes=[ks])
        self.I(ACT, "activation", xb[0:ntok, :], xt[0:ntok, :], AF.Square, accum_out=st[0:ntok, 0:1], reads=[kx], writes=[kb, ks])
        self.I(DVE, "tensor_scalar", st[0:ntok, 1:2], st[0:ntok, 0:1], 1.0 / D, 1e-6, ALU.mult, ALU.add, reads=[ks], writes=[ks])
        self.I(ACT, "activation", st[0:ntok, 2:3], st[0:ntok, 1:2], AF.Sqrt, reads=[ks], writes=[ks])
        self.I(DVE, "reciprocal", st[0:ntok, 3:4], st[0:ntok, 2:3], reads=[ks], writes=[ks])
        self.I(ACT, "activation", xb[0:ntok, :], xt[0:ntok, :], AF.Copy, scale=st[0:ntok, 3:4], reads=[kx, ks, kb], writes=[kb])
        for half in range(2):
            ps, pk = self.psum()
            pT = ps[:].bitcast(BF16).rearrange("p (a b) -> p a b", b=128)
            for k8 in range(8):
                kc = half * 8 + k8
                self.I(PE, "transpose", pT[:, k8, 0:ntok], xb[0:ntok, kc * 128:(kc + 1) * 128],
                                                                self.cb[0:ntok, 0:ntok], reads=[kb, "cb"], writes=[pk], track=(k8 == 7))
            gcol = self.pv[:, V_G1 + half * 8: V_G1 + half * 8 + 8].unsqueeze(2).to_broadcast([128, 8, ntok])
            self.I(DVE, "tensor_tensor", hT[:, half * 8:half * 8 + 8, :], pT[:, :, 0:ntok], gcol, ALU.mult, reads=[pk, "pv"], writes=[hkey])

    def slab_stream(self, slabs):
        S = self.S
        n = len(slabs)

        def issue(i):
            ap, ncol = slabs[i]
            slot = self.wslot % 2
            self.wslot += 1
            key = "wb%d" % slot
            S.dma(POOL, self.wb[slot][:, :, 0:ncol], ap.rearrange("(k p) n -> p k n", p=128), writes=[key])
            return (self.wb[slot], key)
        cur = issue(0)
        for i in range(n):
            nxt = issue(i + 1) if i + 1 < n else None
            yield cur
            cur = nxt

    def proj(self, ps, pk, wt, wkey, c0, M, hT, hkey, N):
        S = self.S
        for k in range(16):
            self.I(PE, "matmul", ps[0:M, 0:N], wt[:, k, c0:c0 + M], hT[:, k, 0:N], start=(k == 0), stop=(k == 15), reads=[wkey, hkey], writes=[pk], track=(k == 15))

    def stageS(self):
        S = self.S
        pv, cf, cb = self.pv, self.cf, self.cb
        wA = self.wA_ap
        N = NSM
        with ExitStack() as es0:
            sb0 = lambda n, s, d=F32: self.sb(es0, n, s, d)
            sbon = sb0("sbon", [128, 8, N], BF16)
            sgst = sb0("sgst", [128, 8, N], BF16)
            self._stageS_proj(es0, sbon, sgst)
            self._stageS_scan(es0, sbon, sgst)

    def _stageS_proj(self, es0, sbon, sgst):
        S = self.S
        pv, cf, cb = self.pv, self.cf, self.cb
        wA = self.wA_ap
        N = NSM
        with ExitStack() as es:
            sb = lambda n, s, d=F32: self.sb(es, n, s, d)
            self.xt = [sb("xtS", [128, D])]
            self.xb = [sb("xbS", [128, D], BF16)]
            self.xst = [sb("xstS", [128, 4])]
            hT = sb("hTs", [128, 16, N], BF16)
            self.make_hT(self.xsamp, N, hT[:, :, :], "hTs", 0)
            S.dma(SP, self.scrH[:, :, 1056:NOWN], hT[:, :, :], reads=["hTs"], writes=["scrH_s"])
            shT = sb("shT", [128, 28, 16])
            sst = sb("sst", [16, DS])
            S.dma(SP, sst[:], self.st_shift, writes=["sst"])
            chunks = [(0, 64), (64, 64), (128, 128), (256, 32)] + [(288 + 128 * i, 128) for i in range(24)]
            ps, pk = self.psum()
            for ci, (o, M) in enumerate(chunks):
                self.I(PE, "transpose", ps[0:M, ci * 16:(ci + 1) * 16], sst[:, o:o + M], cf[0:16, 0:16], reads=["sst", "cf"], writes=[pk], track=(ci == 27))
            self.I(DVE, "tensor_copy", shT[:, :, :], ps[:, 0:448].rearrange("p (a b) -> p a b", b=16), reads=[pk], writes=["shT"])
            zls = sb("zls", [128, 28, 16])
            tw = sb("stw", [64, N], BF16)
            zam = sb("szam", [64, N], BF16)
            sga = sb("ssga", [128, N], BF16)
            sgb = sb("ssgb", [128, N], BF16)
            self.I(POOL, "memset", sgb[:], 0.0, writes=["ssgb"])
            tokS = sb("tokS", [N, 6, 1024])
            phys = {n: sb("s_" + n, [128, N]) for n in ("zc", "d", "zr", "zk", "zv", "es", "wd", "as", "kk", "kkn", "km", "dd", "av")}
            phys["hi"] = sb("s_hi", [128, N], BF16)
            phys["lo"] = sb("s_lo", [128, N], BF16)
            T = _TAlias(phys, {"kk2": "d", "rn": "zc", "t1": "dd", "ta": "es", "rk": "wd2"}, prefix="s_")
            phys["wd2"] = sb("s_wd2", [128, N])

            def mix_s(ps, pk, M, ci, out, kout, mu):
                zc, d = T["zc"], T["d"]
                v3 = lambda t: t[0:M, 0:N].rearrange("p (b t) -> p b t", t=4)
                self.I(ACT, "copy", zc[0:M, 0:N], ps[0:M, 0:N], reads=[pk], writes=["s_zc"])
                self.I(DVE, "tensor_tensor", v3(d)[:, :, 1:4], v3(zc)[:, :, 0:3], v3(zc)[:, :, 1:4], ALU.subtract, reads=["s_zc"], writes=["s_d"])
                self.I(DVE, "tensor_tensor", v3(d)[:, :, 0], shT[0:M, ci, :], v3(zc)[:, :, 0], ALU.subtract, reads=["s_zc", "shT"], writes=["s_d"])
                self.I(DVE, "scalar_tensor_tensor", out[0:M, 0:N], d[0:M, 0:N], mu, zc[0:M, 0:N], ALU.mult, ALU.add, reads=["s_d", "s_zc", "pv"], writes=[kout])
                self.I(POOL, "tensor_copy", zls[0:M, ci, :], v3(zc)[:, :, 3], reads=["s_zc"], writes=["zls"])

            slabs = [(wA[:, 0:288], 288)] + [(wA[:, 288 + q * 384: 288 + (q + 1) * 384], 384) for q in range(8)]
            stream = self.slab_stream(slabs)
            wt, wk = next(stream)
            for (ci, (cc0, M, dst, dk, func)) in enumerate(((0, 64, tw, "stw", AF.Tanh), (64, 64, zam, "szam", AF.Copy),
                                                             (128, 128, sga, "ssga", AF.Sigmoid), (256, 32, sgb, "ssgb", AF.Sigmoid))):
                ps, pk = self.psum()
                self.proj(ps, pk, wt, wk, cc0, M, hT, "hTs", N)
                mix_s(ps, pk, M, ci, T["zr"], "s_zr", pv[0:M, ci:ci + 1])
                self.I(ACT, "activation", dst[0:M, 0:N], T["zr"][0:M, 0:N], func, reads=["s_zr"], writes=[dk])
            bonesb = cb[:, C_BONES:C_BONES + 128]
            for q in range(8):
                wt, wk = next(stream)
                col = lambda base: pv[:, base + q: base + q + 1]
                qs = slice(q * 128, (q + 1) * 128)
                for t, nm in enumerate(("zr", "zk", "zv")):
                    ps, pk = self.psum()
                    self.proj(ps, pk, wt, wk, t * 128, 128, hT, "hTs", N)
                    mix_s(ps, pk, 128, 4 + 3 * q + t, T[nm], "s_" + nm, pv[:, V_MU + 3 * q + t: V_MU + 3 * q + t + 1])
                zr, zk, zv = T["zr"], T["zk"], T["zv"]
                ps, pk = self.psum()
                self.I(PE, "matmul", ps[:, 0:N], self.w2b[:, qs], tw[:, 0:N], start=True, stop=True, reads=["w2b", "stw"], writes=[pk])
                self.I(ACT, "activation", T["es"][:, 0:N], ps[:, 0:N], AF.Sigmoid, bias=col(V_W0), reads=[pk, "pv"], writes=["s_es"])
                self.I(ACT, "activation", T["wd"][:, 0:N], T["es"][:, 0:N], AF.Exp, scale=-DECAY_C, reads=["s_es"], writes=["s_wd"])
                ps, pk = self.psum()
                self.I(PE, "matmul", ps[:, 0:N], self.a2b[:, qs], zam[:, 0:N], start=True, stop=True, reads=["a2b", "szam"], writes=[pk])
                self.I(ACT, "activation", T["as"][:, 0:N], ps[:, 0:N], AF.Sigmoid, bias=col(V_A0), reads=[pk, "pv"], writes=["s_as"])
                self.I(POOL, "tensor_scalar", T["kk"][:, 0:N], zk[:, 0:N], col(V_KK), None, ALU.mult, reads=["s_zk", "pv"], writes=["s_kk"])
                self.I(POOL, "tensor_tensor", T["kk2"][:, 0:N], T["kk"][:, 0:N], T["kk"][:, 0:N], ALU.mult, reads=["s_kk"], writes=["s_d"])
                ps, pk = self.psum()
                self.bsum(ps, pk, T, "kk2", N)
                self.I(ACT, "activation", T["rn"][:, 0:N], ps[:, 0:N], AF.Sqrt, reads=[pk], writes=["s_zc"])
                self.I(DVE, "tensor_scalar_max", T["rn"][:, 0:N], T["rn"][:, 0:N], 1e-12, reads=["s_zc"], writes=["s_zc"])
                self.I(DVE, "reciprocal", T["rn"][:, 0:N], T["rn"][:, 0:N], reads=["s_zc"], writes=["s_zc"])
                self.I(DVE, "tensor_tensor", T["kkn"][:, 0:N], T["kk"][:, 0:N], T["rn"][:, 0:N], ALU.mult, reads=["s_kk", "s_zc"], writes=["s_kkn"])
                self.I(POOL, "tensor_scalar", T["t1"][:, 0:N], T["as"][:, 0:N], -1.0, col(V_KA), ALU.add, ALU.mult, reads=["s_as", "pv"], writes=["s_dd"])
                self.I(POOL, "tensor_tensor", T["t1"][:, 0:N], T["t1"][:, 0:N], zk[:, 0:N], ALU.mult, reads=["s_dd", "s_zk"], writes=["s_dd"])
                self.I(POOL, "tensor_tensor", T["km"][:, 0:N], T["t1"][:, 0:N], zk[:, 0:N], ALU.add, reads=["s_dd", "s_zk"], writes=["s_km"])
                self.I(POOL, "tensor_tensor", T["ta"][:, 0:N], T["kkn"][:, 0:N], T["as"][:, 0:N], ALU.mult, reads=["s_kkn", "s_as"], writes=["s_es"])
                self.I(POOL, "tensor_scalar", T["av"][:, 0:N], T["kkn"][:, 0:N], -1.0, None, ALU.mult, reads=["s_kkn"], writes=["s_av"])
                self.I(DVE, "scalar_tensor_tensor", T["rk"][:, 0:N], zr[:, 0:N], col(V_RK), T["km"][:, 0:N], ALU.mult, ALU.mult, reads=["s_zr", "s_km", "pv"], writes=["s_wd2"])
                ps, pk = self.psum()
                self.bsum(ps, pk, T, "rk", N)
                self.I(DVE, "tensor_tensor", sbon[:, q, :], ps[:, 0:N], zv[:, 0:N], ALU.mult, reads=[pk, "s_zv"], writes=["sbon"])
                self.I(POOL, "tensor_scalar", sbon[:, q, :], sbon[:, q, :], col(V_LB), None, ALU.add, reads=["sbon", "pv"], writes=["sbon"])
                ps, pk = self.psum()
                self.I(PE, "matmul", ps[:, 0:N], self.g2a[:, qs], sga[:, 0:N], start=True, stop=False, reads=["g2a", "ssga"], writes=[pk], track=False)
                self.I(PE, "matmul", ps[:, 0:N], self.g2b[:, qs], sgb[:, 0:N], start=False, stop=True, reads=["g2b", "ssgb"], writes=[pk])
                self.I(ACT, "copy", sgst[:, q, :], ps[:, 0:N], reads=[pk], writes=["sgst"])
                srcs = [("zr", "s_zr"), ("km", "s_km"), ("zv", "s_zv"), ("wd", "s_wd"), ("av", "s_av"), ("ta", "s_es")]
                psA, pkA = self.psum()
                psB, pkB = self.psum()
                for qi, (nm, kk_) in enumerate(srcs):
                    pp, ppk = (psA, pkA) if qi < 4 else (psB, pkB)
                    o = (qi % 4) * 128
                    self.I(PE, "transpose", pp[0:N, o:o + 128], T[nm][:, 0:N], cf[:, C_ID:C_ID + 128], reads=[kk_, "cf"], writes=[ppk], track=(qi in (3, 5)))
                self.I(DVE, "tensor_copy", tokS[:, 0:4, qs], psA[0:N, :].rearrange("p (a b) -> p a b", b=128), reads=[pkA], writes=["tokS"])
                self.I(ACT, "copy", tokS[:, 4:6, qs], psB[0:N, 0:256].rearrange("p (a b) -> p a b", b=128), reads=[pkB], writes=["tokS"])
            zrow = sb("zrow", [16, DS])
            for c0_ in range(0, 28, 4):
                ps, pk = self.psum()
                grp = list(enumerate(chunks))[c0_:c0_ + 4]
                for n_, (ci, (o, M)) in enumerate(grp):
                    self.I(PE, "transpose", ps[0:16, n_ * 128:n_ * 128 + M], zls[0:M, ci, :], cf[0:M, 0:M], reads=["zls", "cf"], writes=[pk], track=(n_ == len(grp) - 1))
                for n_, (ci, (o, M)) in enumerate(grp):
                    self.I(DVE, "tensor_copy", zrow[:, o:o + M], ps[0:16, n_ * 128:n_ * 128 + M], reads=[pk], writes=["zrow"])
            S.dma(SP, self.o_sshift, zrow[:], reads=["zrow"], ring="spo")
            S.dma(SP, self.scrS, tokS[:], reads=["tokS"], writes=["scrS"])
            S.barrier()

    def _stageS_scan(self, es0, sbon, sgst):
        S = self.S
        pv, cf, cb = self.pv, self.cf, self.cb
        N = NSM
        with ExitStack() as es:
            sb = lambda n, s, d=F32: self.sb(es, n, s, d)
            ytok = sb("ytok", [N, 1024])
            for bt in range(2):
                E = DVE
                kS, kq, kt, ky = "Sst%d" % bt, "qin%d" % bt, "stmp%d" % bt, "ysb%d" % bt
                Sst = sb(kS, [128, 64, 64])
                tmp = sb(kt, [128, 64, 64])
                qin = sb(kq, [128, 4, 6, 64])
                ysb = sb(ky, [128, 4, 64])
                sa = sb("sa%d" % bt, [128, 64])
                yq = sb("syq%d" % bt, [128, 4, 64])
                sst_ = sb("sstat%d" % bt, [128, 8, 4])
                S.dma(SP, Sst[:].rearrange("p v k -> p (v k)"), self.st_wkv[bt * 128:(bt + 1) * 128, :], writes=[kS])
                for bl in range(8):
                    b = bt * 8 + bl
                    S.dma(SP, qin[16 * bl:16 * bl + 16, :, :, :], self.scrS[4 * b:4 * b + 4, :, :].rearrange("t q (h k) -> h t q k", k=64),
                          reads=["scrS"], writes=[kq + "_%d" % bl])
                bc_k = lambda t, qi: qin[:, t, qi, :].unsqueeze(1).to_broadcast([128, 64, 64])
                kqs = [kq + "_%d" % bl for bl in range(8)]
                for t in range(4):
                    self.I(E, "tensor_tensor", tmp[:], Sst[:], bc_k(t, 4), ALU.mult, reads=[kS] + kqs, writes=[kt])
                    self.I(DVE, "tensor_reduce", sa[:], tmp[:], AX.X, ALU.add, reads=[kt], writes=["sa%d" % bt])
                    self.I(E, "tensor_tensor", Sst[:], Sst[:], bc_k(t, 3), ALU.mult, reads=[kS] + kqs, writes=[kS])
                    self.I(E, "tensor_tensor", tmp[:], sa[:].unsqueeze(2).to_broadcast([128, 64, 64]), bc_k(t, 5), ALU.mult, reads=["sa%d" % bt] + kqs, writes=[kt])
                    self.I(E, "tensor_tensor", Sst[:], Sst[:], tmp[:], ALU.add, reads=[kS, kt], writes=[kS])
                    self.I(E, "tensor_tensor", tmp[:], qin[:, t, 2, :].unsqueeze(2).to_broadcast([128, 64, 64]), bc_k(t, 1), ALU.mult, reads=kqs, writes=[kt])
                    self.I(E, "tensor_tensor", Sst[:], Sst[:], tmp[:], ALU.add, reads=[kS, kt], writes=[kS])
                    self.I(E, "tensor_tensor", tmp[:], Sst[:], bc_k(t, 0), ALU.mult, reads=[kS] + kqs, writes=[kt])
                    self.I(DVE, "tensor_reduce", ysb[:, t, :], tmp[:], AX.X, ALU.add, reads=[kt], writes=[ky])
                S.dma(SP, self.o_swkv[bt * 128:(bt + 1) * 128, :], Sst[:].rearrange("p v k -> p (v k)"), reads=[kS], ring="spo")
                st_ = sst_
                ks_ = "sstat%d" % bt
                self.I(DVE, "tensor_reduce", st_[:, 0, :], ysb[:], AX.X, ALU.add, reads=[ky], writes=[ks_])
                self.I(E, "tensor_tensor", yq[:], ysb[:], ysb[:], ALU.mult, reads=[ky], writes=["syq%d" % bt])
                self.I(DVE, "tensor_reduce", st_[:, 1, :], yq[:], AX.X, ALU.add, reads=["syq%d" % bt], writes=[ks_])
                self.I(E, "tensor_scalar", st_[:, 2, :], st_[:, 0, :], 1.0 / 64, None, ALU.mult, reads=[ks_], writes=[ks_])
                self.I(E, "tensor_tensor", st_[:, 3, :], st_[:, 2, :], st_[:, 2, :], ALU.mult, reads=[ks_], writes=[ks_])
                self.I(E, "tensor_scalar", st_[:, 4, :], st_[:, 1, :], 1.0 / 64, 64e-5, ALU.mult, ALU.add, reads=[ks_], writes=[ks_])
                self.I(E, "tensor_tensor", st_[:, 4, :], st_[:, 4, :], st_[:, 3, :], ALU.subtract, reads=[ks_], writes=[ks_])
                self.I(ACT, "activation", st_[:, 5, :], st_[:, 4, :], AF.Sqrt, reads=[ks_], writes=[ks_])
                self.I(DVE, "reciprocal", st_[:, 6, :], st_[:, 5, :], reads=[ks_], writes=[ks_])
                self.I(E, "tensor_tensor", yq[:], ysb[:], st_[:, 2, :].unsqueeze(2).to_broadcast([128, 4, 64]), ALU.subtract, reads=[ky, ks_], writes=["syq%d" % bt])
                self.I(E, "tensor_tensor", yq[:], yq[:], st_[:, 6, :].unsqueeze(2).to_broadcast([128, 4, 64]), ALU.mult, reads=["syq%d" % bt, ks_], writes=["syq%d" % bt])
                for bl in range(8):
                    b = bt * 8 + bl
                    S.dma(SP, self.scrY[4 * b:4 * b + 4, :].rearrange("t (h v) -> h t v", v=64), yq[16 * bl:16 * bl + 16, :, :], reads=["syq%d" % bt], writes=["scrY%d" % b])
            S.dma(SP, ytok[:], self.scrY, reads=["scrY%d" % b for b in range(16)], writes=["ytok"])
            if "sdbg" in self.dbg:
                S.dma(SP, self.dout("d_ytok", [N, 1024]), ytok[:], reads=["ytok"], ring="spo")
                dbf = sb("dbf", [128, 2, 8, N])
                self.I(DVE, "tensor_copy", dbf[:, 0], sbon[:], reads=["sbon"], writes=["dbf"])
                self.I(DVE, "tensor_copy", dbf[:, 1], sgst[:], reads=["sgst"], writes=["dbf"])
                S.dma(SP, self.dout("d_sbg", [128, 2, 8, N]), dbf[:], reads=["dbf"], ring="spo")
            ytb = sb("ytb", [N, 1024], BF16)
            self.I(ACT, "copy", ytb[:], ytok[:], reads=["ytok"], writes=["ytb"])
            ps, pk = self.psum()
            pT = ps[:].bitcast(BF16).rearrange("p (a b) -> p a b", b=128)[:, :, 0:N]
            for q in range(8):
                self.I(PE, "transpose", pT[:, q, :], ytb[:, q * 128:(q + 1) * 128], cb[0:N, 0:N], reads=["ytb", "cb"], writes=[pk], track=(q == 7))
            yt = sb("syt", [128, 8, N])
            lw = pv[:, V_LW: V_LW + 8].unsqueeze(2).to_broadcast([128, 8, N])
            self.I(DVE, "tensor_tensor", yt[:], pT, lw, ALU.mult, reads=[pk, "pv"], writes=["syt"])
            self.I(POOL, "tensor_tensor", yt[:], yt[:], sbon[:], ALU.add, reads=["syt", "sbon"], writes=["syt"])
            self.I(DVE, "tensor_tensor", self.yaT[:, :, 1056:NOWN], yt[:], sgst[:], ALU.mult, reads=["syt", "sgst"], writes=["yaT"])
            S.barrier()

    def stageA(self, es0, xseq, wA):
        S = self.S
        pv, cf, cb = self.pv, self.cf, self.cb
        with ExitStack() as es:
            sb = lambda n, s, d=F32: self.sb(es, n, s, d)
            self.xt = [sb("xt%d" % i, [128, D]) for i in range(1)]
            self.xb = [sb("xb%d" % i, [128, D], BF16) for i in range(1)]
            self.xst = [sb("xst%d" % i, [128, 4]) for i in range(1)]
            hT = sb("hTb", [128, 16, BLK], BF16)
            zl = sb("zl", [128, 28])
            self.I(POOL, "memset", zl[:], 0.0, writes=["zl"])
            tw = sb("tw", [64, BLK], BF16)
            zam = sb("zam", [64, BLK], BF16)
            sga = sb("sga", [128, BLK], BF16)
            sgb = sb("sgb", [128, BLK], BF16)
            self.I(POOL, "memset", sgb[:], 0.0, writes=["sgb"])
            NCK = BLK // CH
            ARBD = sb("ARBD", [128, 4, NCK, 2, 128], BF16)
            BBD = sb("BBD", [128, 4, NCK, 128], BF16)
            KBD = sb("KBD", [128, 4, NCK, 128], BF16)
            VBD = sb("VBD", [128, 4, NCK, 128], BF16)
            for (t, k) in ((ARBD, "ARBD"), (BBD, "BBD"), (KBD, "KBD"), (VBD, "VBD")):
                self.I(POOL, "memset", t[:], 0.0, writes=[k + "%d" % q for q in range(4)])
            PCs = sb("PCs", [128, 4, NCK])
            bon = sb("bon", [128, 4, BLK], BF16)
            gst = sb("gst", [128, 4, BLK], BF16)
            Hf = [sb("Hf%d" % g, [128, 4, 64]) for g in range(2)]
            Hb = [sb("Hb%d" % g, [128, 4, 64], BF16) for g in range(2)]
            for g in range(2):
                self.I(POOL, "memset", Hf[g][:], 0.0, writes=["Hf%d" % g])
                self.I(POOL, "memset", Hb[g][:], 0.0, writes=["Hb%d" % g])
            names = ("zc", "d", "zr", "zk", "zv", "es", "cum", "dd", "pinv", "prr", "pa", "as", "kk", "kkn", "km")
            phys = {n: sb("t_" + n, [128, NH]) for n in names[:8]}
            spare = self.bufX[:, 8:16, :].rearrange("p a b -> p (a b)")
            sparef = spare.bitcast(F32)
            for i, n in enumerate(names[8:]):
                phys[n] = sparef[:, i * NH:(i + 1) * NH]
            phys["hi"] = spare[:, 7 * 2 * NH: 7 * 2 * NH + NH]
            phys["lo"] = spare[:, 7 * 2 * NH + NH: 7 * 2 * NH + 2 * NH]
            T = _TAlias(phys, {"kk2": "d", "rn": "zc", "t1": "dd", "ta": "es", "rk": "cum"})
            sc = {}
            for s in range(2):
                sc["NA1", s] = sb("NA1_%d" % s, [128, 4, 256], BF16)
                sc["NA2", s] = sb("NA2_%d" % s, [128, 4, 256], BF16)
                for i in range(2):
                    sc["Q", s, i] = sb("Q_%d%d" % (s, i), [128, 4, 128], BF16)
                    if s == 0:
                        sc["NL", s, i] = sb("NL_%d%d" % (s, i), [128, 4, 128], BF16)
                        sc["TU", s, i] = sb("TU_%d%d" % (s, i), [128, 4, 128], BF16)
                sc["B2", s] = sb("B2_%d" % s, [128, 4, 128], BF16)
                sc["K2", s] = sb("K2_%d" % s, [128, 4, 128], BF16)
                sc["V2", s] = sb("V2_%d" % s, [128, 4, 64], BF16)
                if s == 0:
                    sc["X2", s] = sb("X2_%d" % s, [128, 4, 64], BF16)
                    sc["U2", s] = sb("U2_%d" % s, [128, 4, 64], BF16)
                    sc["YBD", s] = sb("YBD_%d" % s, [128, 4, 128], BF16)
                    self.I(POOL, "memset", sc["YBD", s][:], 0.0, writes=["YBD_%d" % s])
                    sc["ys", s] = sb("ys_%d" % s, [128, 4, 64])
                    sc["yq", s] = sb("yq_%d" % s, [128, 4, 64])
                    sc["st", s] = sb("yst_%d" % s, [128, 8, 4])
                    sc["yt", s] = sb("yt_%d" % s, [128, 4, 64])
                else:
                    for nm in ("X2", "U2", "YBD", "ys", "yq", "st", "yt"):
                        sc[nm, s] = sc[nm, 0]
            tmpH = sb("tmpH", [128, 4, 64])

            nblk = NPR // BLK
            lr_slab = (wA[:, 0:288], 288)
            pair_slabs = [(wA[:, 288 + q * 384: 288 + (q + 1) * 384], 384) for q in range(8)]
            slabs = []
            for blk in range(nblk):
                slabs.append(lr_slab)
                slabs += pair_slabs
            stream = self.slab_stream(slabs)

            for blk in range(nblk):
                c0 = blk * BLK
                N = BLK
                for t4 in range(4):
                    self.make_hT(xseq[c0 + t4 * 128: c0 + (t4 + 1) * 128, :], 128, hT[:, :, t4 * 128:(t4 + 1) * 128], "hTb", 0)
                self.ckpt("hT")
                if blk == 1:
                    S.dma(SP, self.scrH[:, :, 0:32], hT[:, :, BLK - 32:BLK], reads=["hTb"], writes=["scrH_1"])
                elif blk >= 2:
                    S.dma(SP, self.scrH[:, :, 32 + (blk - 2) * BLK: 32 + (blk - 1) * BLK], hT[:, :, :], reads=["hTb"], writes=["scrH_%d" % blk])
                wt, wk = next(stream)
                for (ci, (cc0, M, dst, dk, func)) in enumerate(((0, 64, tw, "tw", AF.Tanh), (64, 64, zam, "zam", AF.Copy),
                                                                 (128, 128, sga, "sga", AF.Sigmoid), (256, 32, sgb, "sgb", AF.Sigmoid))):
                    ps, pk = self.psum()
                    self.proj(ps, pk, wt, wk, cc0, M, hT, "hTb", N)
                    for co in range(0, N, NH):
                        self.mix(ps[:, co:co + NH], pk, M, NH, ci, T, zl)
                        self.I(ACT, "activation", dst[0:M, co:co + NH], T["zr"][0:M, 0:NH], func, reads=["t_zr"], writes=[dk])
                self.ckpt("lowrank")
                for g in range(2):
                    for qg in range(4):
                        q = g * 4 + qg
                        wt, wk = next(stream)
                        self.prep_pair(wt, wk, hT, N, q, qg, T, zl, tw, zam, sga, sgb, ARBD, BBD, KBD, VBD, PCs, bon, gst, blk)
                        self.ckpt("prep0")
                    self.ckpt("prep")
                    self.scan_group(g, blk, NCK, sc, ARBD, BBD, KBD, VBD, PCs, bon, gst, Hf[g], Hb[g], tmpH)
            scr = self.stage_out("zl", zl[:], [128, 28], "zl")
            self.bg(self.o_pshift[0:64].rearrange("(p o) -> p o", o=1), scr[0:64, 0:1], "zl")
            self.bg(self.o_pshift[64:128].rearrange("(p o) -> p o", o=1), scr[0:64, 1:2], "zl")
            self.bg(self.o_pshift[128:256].rearrange("(p o) -> p o", o=1), scr[:, 2:3], "zl")
            self.bg(self.o_pshift[256:288].rearrange("(p o) -> p o", o=1), scr[0:32, 3:4], "zl")
            self.bg(self.o_pshift[288:DS].rearrange("(j p) -> p j", p=128), scr[:, 4:28], "zl")
            for g in range(2):
                hh = sb("hh%d" % g, [128, 4, 64], BF16)
                hl = sb("hl%d" % g, [128, 4, 64], BF16)
                self.I(POOL, "tensor_copy", hh[:], Hf[g][:], reads=["Hf%d" % g], writes=["hh%d" % g])
                self.I(POOL, "tensor_tensor", hl[:], Hf[g][:], hh[:], ALU.subtract, reads=["Hf%d" % g, "hh%d" % g], writes=["hl%d" % g])
                ps, pk = self.psum()
                for qg in range(4):
                    self.I(PE, "matmul", ps[0:64, qg * 128:(qg + 1) * 128], hh[:, qg, :], cb[:, 0:128], start=True, stop=False, reads=["hh%d" % g, "cb"], writes=[pk], track=False)
                    self.I(PE, "matmul", ps[0:64, qg * 128:(qg + 1) * 128], hl[:, qg, :], cb[:, 0:128], start=False, stop=True, reads=["hl%d" % g, "cb"], writes=[pk], track=(qg == 3))
                so = sb("so%d" % g, [64, 4, 2, 64])
                self.I(DVE, "tensor_copy", so[:], ps[0:64, :].rearrange("p (a b c) -> p a b c", a=4, b=2), reads=[pk], writes=["so%d" % g])
                S.dma(SP, self.o_pwkv[g * 8:(g + 1) * 8].rearrange("(q s) v k -> v q s k", s=2), so[:], reads=["so%d" % g], ring="spo")
            S.barrier()

    def mix(self, ps, pk, M, N, ci, T, zl):
        S = self.S
        zc, d, out = T["zc"], T["d"], T["zr"]
        self.mix_to(ps, pk, M, N, ci, zl, zc, "t_zc", d, "t_d", out, "t_zr", self.pv[0:M, ci:ci + 1] if ci < 4 else None)

    def mix_to(self, ps, pk, M, N, ci, zl, zc, kzc, d, kd, out, kout, mu):
        S = self.S
        self.I(ACT, "copy", zc[0:M, 0:N], ps[0:M, 0:N], reads=[pk], writes=[kzc])
        self.I(DVE, "tensor_tensor", d[0:M, 1:N], zc[0:M, 0:N - 1], zc[0:M, 1:N], ALU.subtract, reads=[kzc], writes=[kd])
        self.I(DVE, "tensor_tensor", d[0:M, 0:1], zl[0:M, ci:ci + 1], zc[0:M, 0:1], ALU.subtract, reads=[kzc, "zl"], writes=[kd])
        self.I(DVE, "scalar_tensor_tensor", out[0:M, 0:N], d[0:M, 0:N], mu, zc[0:M, 0:N], ALU.mult, ALU.add, reads=[kd, kzc, "pv"], writes=[kout])
        self.I(POOL, "tensor_copy", zl[0:M, ci:ci + 1], zc[0:M, N - 1:N], reads=[kzc], writes=["zl"])

    def bsum(self, ps, pk, T, name, N):
        S = self.S
        src, ksrc = T[name], T.key(name)
        bonesb = self.cb[:, C_BONES:C_BONES + 128]
        khi, klo = T.key("hi"), T.key("lo")
        self.I(ACT, "copy", T["hi"][:, 0:N], src[:, 0:N], reads=[ksrc], writes=[khi])
        self.I(POOL, "tensor_tensor", T["lo"][:, 0:N], src[:, 0:N], T["hi"][:, 0:N], ALU.subtract, reads=[ksrc, khi], writes=[klo])
        self.I(PE, "matmul", ps[:, 0:N], bonesb, T["hi"][:, 0:N], start=True, stop=False, reads=["cb", khi], writes=[pk], track=False)
        self.I(PE, "matmul", ps[:, 0:N], bonesb, T["lo"][:, 0:N], start=False, stop=True, reads=["cb", klo], writes=[pk])

    def prep_pair(self, wt, wk, hT, N, q, qg, T, zl, tw, zam, sga, sgb, ARBD, BBD, KBD, VBD, PCs, bon, gst, blk):
        S = self.S
        pv, cf = self.pv, self.cf
        col = lambda base: pv[:, base + q: base + q + 1]
        qs = slice(q * 128, (q + 1) * 128)
        pj = []
        for t in range(3):
            ps, pk = self.psum()
            self.proj(ps, pk, wt, wk, t * 128, 128, hT, "hTb", N)
            pj.append((ps, pk))
        W = min(N, NH)
        for co in range(0, N, W):
            cw = slice(co, co + W)
            for t, nm in enumerate(("zr", "zk", "zv")):
                ps, pk = pj[t]
                ci = 4 + 3 * q + t
                self.mix_to(ps[:, cw], pk, 128, W, ci, zl, T["zc"], "t_zc", T["d"], "t_d", T[nm], "t_" + nm, pv[:, V_MU + 3 * q + t: V_MU + 3 * q + t + 1])
            zr, zk, zv = T["zr"], T["zk"], T["zv"]
            ps, pk = self.psum()
            self.I(PE, "matmul", ps[:, 0:W], self.w2b[:, qs], tw[:, cw], start=True, stop=True, reads=["w2b", "tw"], writes=[pk])
            self.I(ACT, "activation", T["es"][:, 0:W], ps[:, 0:W], AF.Sigmoid, bias=col(V_W0), reads=[pk, "pv"], writes=["t_es"])
            self.I(DVE, "tensor_tensor_scan", T["cum"][:, 0:W], cf[:, C_M01:C_M01 + W], T["es"][:, 0:W], 0.0, ALU.mult, ALU.add, reads=["t_es", "cf"], writes=["t_cum"])
            self.I(POOL, "tensor_tensor", T["dd"][:, 0:W], T["cum"][:, 0:W], T["es"][:, 0:W], ALU.subtract, reads=["t_cum", "t_es"], writes=["t_dd"])
            self.I(ACT, "activation", T["pinv"][:, 0:W], T["cum"][:, 0:W], AF.Exp, scale=DECAY_C, reads=["t_cum"], writes=["t_pinv"])
            self.I(ACT, "activation", T["prr"][:, 0:W], T["cum"][:, 0:W], AF.Exp, scale=-DECAY_C, reads=["t_cum"], writes=["t_prr"])
            self.I(ACT, "activation", T["pa"][:, 0:W], T["dd"][:, 0:W], AF.Exp, scale=-DECAY_C, reads=["t_dd"], writes=["t_pa"])
            nck = W // CH
            ck0 = co // CH
            self.I(POOL, "tensor_copy", PCs[:, qg, ck0:ck0 + nck], T["prr"][:, 0:W].rearrange("p (c t) -> p c t", t=CH)[:, :, CH - 1], reads=["t_prr"], writes=["PCs%d" % qg])
            ps, pk = self.psum()
            self.I(PE, "matmul", ps[:, 0:W], self.a2b[:, qs], zam[:, cw], start=True, stop=True, reads=["a2b", "zam"], writes=[pk])
            self.I(ACT, "activation", T["as"][:, 0:W], ps[:, 0:W], AF.Sigmoid, bias=col(V_A0), reads=[pk, "pv"], writes=["t_as"])
            self.I(ACT, "activation", T["kk"][:, 0:W], zk[:, 0:W], AF.Copy, scale=col(V_KK), reads=["t_zk", "pv"], writes=["t_kk"])
            self.I(ACT, "activation", T["kk2"][:, 0:W], T["kk"][:, 0:W], AF.Square, reads=["t_kk"], writes=["t_d"])
            ps, pk = self.psum()
            self.bsum(ps, pk, T, "kk2", W)
            self.I(ACT, "activation", T["rn"][:, 0:W], ps[:, 0:W], AF.Sqrt, reads=[pk], writes=["t_zc"])
            self.I(DVE, "tensor_scalar_max", T["rn"][:, 0:W], T["rn"][:, 0:W], 1e-12, reads=["t_zc"], writes=["t_zc"])
            self.I(DVE, "reciprocal", T["rn"][:, 0:W], T["rn"][:, 0:W], reads=["t_zc"], writes=["t_zc"])
            self.I(DVE, "tensor_tensor", T["kkn"][:, 0:W], T["kk"][:, 0:W], T["rn"][:, 0:W], ALU.mult, reads=["t_kk", "t_zc"], writes=["t_kkn"])
            self.I(POOL, "tensor_scalar", T["t1"][:, 0:W], T["as"][:, 0:W], -1.0, col(V_KA), ALU.add, ALU.mult, reads=["t_as", "pv"], writes=["t_dd"])
            self.I(POOL, "tensor_tensor", T["t1"][:, 0:W], T["t1"][:, 0:W], zk[:, 0:W], ALU.mult, reads=["t_dd", "t_zk"], writes=["t_dd"])
            self.I(POOL, "tensor_tensor", T["km"][:, 0:W], T["t1"][:, 0:W], zk[:, 0:W], ALU.add, reads=["t_dd", "t_zk"], writes=["t_km"])
            self.I(POOL, "tensor_tensor", T["ta"][:, 0:W], T["kkn"][:, 0:W], T["as"][:, 0:W], ALU.mult, reads=["t_kkn", "t_as"], writes=["t_es"])
            self.I(DVE, "scalar_tensor_tensor", T["rk"][:, 0:W], zr[:, 0:W], col(V_RK), T["km"][:, 0:W], ALU.mult, ALU.mult, reads=["t_zr", "t_km", "pv"], writes=["t_cum"])
            ps, pk = self.psum()
            self.bsum(ps, pk, T, "rk", W)
            self.I(DVE, "tensor_tensor", bon[:, qg, cw], ps[:, 0:W], zv[:, 0:W], ALU.mult, reads=[pk, "t_zv"], writes=["bon%d" % qg])
            self.I(ACT, "activation", bon[:, qg, cw], bon[:, qg, cw], AF.Identity, bias=col(V_LB), reads=["bon%d" % qg, "pv"], writes=["bon%d" % qg])
            ps, pk = self.psum()
            self.I(PE, "matmul", ps[:, 0:W], self.g2a[:, qs], sga[:, cw], start=True, stop=False, reads=["g2a", "sga"], writes=[pk], track=False)
            self.I(PE, "matmul", ps[:, 0:W], self.g2b[:, qs], sgb[:, cw], start=False, stop=True, reads=["g2b", "sgb"], writes=[pk])
            self.I(ACT, "copy", gst[:, qg, cw], ps[:, 0:W], reads=[pk], writes=["gst%d" % qg])
            for h in range(2):
                pr = slice(64 * h, 64 * h + 64)
                cs = slice(64 * h, 64 * h + 64)
                v3 = lambda t: t[pr, 0:W].rearrange("p (c t) -> p c t", t=CH)
                e1 = DVE if h == 0 else POOL
                cks = slice(ck0, ck0 + nck)
                self.I(DVE, "scalar_tensor_tensor", ARBD[pr, qg, cks, 0, cs], v3(T["kkn"]), -1.0, v3(T["pa"]), ALU.mult, ALU.mult, reads=["t_kkn", "t_pa"], writes=["ARBD%d" % qg])
                self.I(e1, "tensor_tensor", ARBD[pr, qg, cks, 1, cs], v3(zr), v3(T["prr"]), ALU.mult, reads=["t_zr", "t_prr"], writes=["ARBD%d" % qg])
                self.I(e1, "tensor_tensor", KBD[pr, qg, cks, cs], v3(T["km"]), v3(T["pinv"]), ALU.mult, reads=["t_km", "t_pinv"], writes=["KBD%d" % qg])
                self.I(e1, "tensor_tensor", BBD[pr, qg, cks, cs], v3(T["ta"]), v3(T["pinv"]), ALU.mult, reads=["t_es", "t_pinv"], writes=["BBD%d" % qg])
                self.I(ACT, "copy", VBD[pr, qg, cks, cs], v3(zv), reads=["t_zv"], writes=["VBD%d" % qg])

    def scan_pre(self, g, c, gc, sc, ARBD, BBD, KBD, VBD):
        cf, cb = self.cf, self.cb
        s = gc % 2
        kin = ["ARBD%d" % q for q in range(4)] + ["BBD%d" % q for q in range(4)] + ["KBD%d" % q for q in range(4)] + ["VBD%d" % q for q in range(4)]
        NA1, NA2 = sc["NA1", s], sc["NA2", s]
        kNA1, kNA2 = "NA1_%d" % s, "NA2_%d" % s
        identb = cb[:, 0:128]
        iselb = cb[:, C_ISEL:C_ISEL + 64]
        mask2 = cf[:, C_MUS:C_MUS + 256].unsqueeze(1).to_broadcast([128, 2, 256])
        for (dst, kd, L) in ((NA1, kNA1, BBD), (NA2, kNA2, KBD)):
            for hf in range(2):
                ps, pk = self.psum()
                for j in range(2):
                    q = 2 * hf + j
                    self.I(PE, "matmul", ps[:, j * 256:(j + 1) * 256], L[:, q, c, :], ARBD[:, q, c, :, :].rearrange("p a b -> p (a b)"),
                           start=True, stop=True, reads=kin, writes=[pk], track=(j == 1))
                self.I(DVE, "tensor_tensor", dst[:, 2 * hf:2 * hf + 2, :], ps[:].rearrange("p (a b) -> p a b", b=256), mask2, ALU.mult, reads=[pk, "cf"], writes=[kd])
                yield
        NL0, kNL0 = sc["NL", 0, 0], "NL_00"
        ps, pk = self.psum()
        for q in range(4):
            self.I(PE, "matmul", ps[:, q * 128:(q + 1) * 128], ARBD[:, q, c, 0, :], BBD[:, q, c, :], start=True, stop=True, reads=kin, writes=[pk], track=(q == 3))
        mL = cf[:, C_MLS:C_MLS + 128].unsqueeze(1).to_broadcast([128, 4, 128])
        self.I(DVE, "tensor_tensor", NL0[:], ps[:].rearrange("p (a b) -> p a b", b=128), mL, ALU.mult, reads=[pk, "cf"], writes=[kNL0])
        yield
        B2, K2, V2 = sc["B2", s], sc["K2", s], sc["V2", s]
        for (dst, kd, L) in ((B2, "B2_%d" % s, BBD), (K2, "K2_%d" % s, KBD)):
            ps, pk = self.psum()
            for q in range(4):
                self.I(PE, "matmul", ps[:, q * 128:(q + 1) * 128], L[:, q, c, :], identb, start=True, stop=True, reads=kin + ["cb"], writes=[pk], track=(q == 3))
            self.I(ACT, "copy", dst[:], ps[:].rearrange("p (a b) -> p a b", b=128), reads=[pk], writes=[kd])
            yield
        ps, pk = self.psum()
        for q in range(4):
            self.I(PE, "matmul", ps[:, q * 64:(q + 1) * 64], VBD[:, q, c, :], iselb, start=True, stop=True, reads=kin + ["cb"], writes=[pk], track=(q == 3))
        self.I(ACT, "copy", V2[:], ps[:, 0:256].rearrange("p (a b) -> p a b", b=64), reads=[pk], writes=["V2_%d" % s])
        yield
        TUc, kTU = NA1[:, :, 0:128], kNA1
        NLc, kNL = NL0, kNL0
        Qc, kQ = sc["Q", s, 0], "Q_%d0" % s
        idb4 = identb.unsqueeze(1).to_broadcast([128, 4, 128])
        self.I(POOL, "tensor_tensor", Qc[:], NA1[:, :, 0:128], idb4, ALU.add, reads=[kNA1, "cb"], writes=[kQ])
        for lvl in range(1, 6):
            i = lvl % 2
            NLn, kNLn = sc["NL", 0, i], "NL_0%d" % i
            TUn, kTUn = sc["TU", 0, i], "TU_0%d" % i
            Qn, kQn = sc["Q", s, i], "Q_%d%d" % (s, i)
            psN, pkN = self.psum()
            for q in range(4):
                self.I(PE, "matmul", psN[:, q * 128:(q + 1) * 128], TUc[:, q, :], NLc[:, q, :], start=True, stop=True, reads=[kTU, kNL], writes=[pkN], track=(q == 3))
            if lvl < 5:
                psT, pkT = self.psum()
                for q in range(4):
                    self.I(PE, "matmul", psT[:, q * 128:(q + 1) * 128], NLc[:, q, :], TUc[:, q, :], start=True, stop=True, reads=[kTU, kNL], writes=[pkT], track=(q == 3))
            self.I(ACT, "copy", NLn[:], psN[:].rearrange("p (a b) -> p a b", b=128), reads=[pkN], writes=[kNLn])
            if lvl < 5:
                self.I(DVE, "tensor_copy", TUn[:], psT[:].rearrange("p (a b) -> p a b", b=128), reads=[pkT], writes=[kTUn])
            yield
            psQ, pkQ = self.psum()
            for q in range(4):
                self.I(PE, "matmul", psQ[:, q * 128:(q + 1) * 128], NLn[:, q, :], Qc[:, q, :], start=True, stop=True, reads=[kNLn, kQ], writes=[pkQ], track=(q == 3))
            self.I(DVE, "tensor_tensor", Qn[:], psQ[:].rearrange("p (a b) -> p a b", b=128), Qc[:], ALU.add, reads=[pkQ, kQ], writes=[kQn])
            yield
            TUc, kTU, NLc, kNL, Qc, kQ = TUn, kTUn, NLn, kNLn, Qn, kQn

    def scan_chain(self, g, c, gc, sc, ARBD, PCs, bon, gst, Hf, Hb, tmpH):
        s = gc % 2
        kin = ["ARBD%d" % q for q in range(4)]
        NA1, NA2 = sc["NA1", s], sc["NA2", s]
        kNA1, kNA2 = "NA1_%d" % s, "NA2_%d" % s
        B2, K2, V2 = sc["B2", s], sc["K2", s], sc["V2", s]
        kB2, kK2, kV2 = "B2_%d" % s, "K2_%d" % s, "V2_%d" % s
        Qc, kQ = sc["Q", s, 1], "Q_%d1" % s
        kH = "Hf%d" % g
        kHb = "Hb%d" % g
        X2, U2 = sc["X2", 0], sc["U2", 0]
        ps, pk = self.psum()
        for q in range(4):
            self.I(PE, "matmul", ps[:, q * 64:(q + 1) * 64], ARBD[:, q, c, 0, :], Hb[:, q, :], start=True, stop=False, reads=kin + [kHb], writes=[pk], track=False)
            self.I(PE, "matmul", ps[:, q * 64:(q + 1) * 64], NA2[:, q, 0:128], V2[:, q, :], start=False, stop=True, reads=[kNA2, kV2], writes=[pk], track=(q == 3))
        self.I(ACT, "copy", X2[:], ps[:, 0:256].rearrange("p (a b) -> p a b", b=64), reads=[pk], writes=["X2_0"])
        yield
        ps, pk = self.psum()
        for q in range(4):
            self.I(PE, "matmul", ps[:, q * 64:(q + 1) * 64], Qc[:, q, :], X2[:, q, :], start=True, stop=True, reads=[kQ, "X2_0"], writes=[pk], track=(q == 3))
        self.I(ACT, "copy", U2[:], ps[:, 0:256].rearrange("p (a b) -> p a b", b=64), reads=[pk], writes=["U2_0"])
        yield
        need_y = gc >= OWN0 // CH
        if need_y:
            psY, pkY = self.psum()
            for q in range(4):
                self.I(PE, "matmul", psY[:, q * 64:(q + 1) * 64], ARBD[:, q, c, 1, :], Hb[:, q, :], start=True, stop=False, reads=kin + [kHb], writes=[pkY], track=False)
                self.I(PE, "matmul", psY[:, q * 64:(q + 1) * 64], NA1[:, q, 128:256], U2[:, q, :], start=False, stop=False, reads=[kNA1, "U2_0"], writes=[pkY], track=False)
                self.I(PE, "matmul", psY[:, q * 64:(q + 1) * 64], NA2[:, q, 128:256], V2[:, q, :], start=False, stop=True, reads=[kNA2, kV2], writes=[pkY], track=(q == 3))
        ps, pk = self.psum()
        for q in range(4):
            self.I(PE, "matmul", ps[:, q * 64:(q + 1) * 64], B2[:, q, :], U2[:, q, :], start=True, stop=False, reads=[kB2, "U2_0"], writes=[pk], track=False)
            self.I(PE, "matmul", ps[:, q * 64:(q + 1) * 64], K2[:, q, :], V2[:, q, :], start=False, stop=True, reads=[kK2, kV2], writes=[pk], track=(q == 3))
        self.I(DVE, "tensor_tensor", tmpH[:], Hf[:], ps[:, 0:256].rearrange("p (a b) -> p a b", b=64), ALU.add, reads=[pk, kH], writes=["tmpH"])
        pcb = PCs[:, :, c:c + 1].to_broadcast([128, 4, 64])
        self.I(DVE, "tensor_tensor", Hf[:], tmpH[:], pcb, ALU.mult, reads=["tmpH"] + ["PCs%d" % q for q in range(4)], writes=[kH])
        self.I(ACT, "copy", Hb[:], Hf[:], reads=[kH], writes=[kHb])
        yield
        if need_y:
            yield from self.y_post(g, c, gc, s, sc, psY, pkY, bon, gst)

    def scan_group(self, g, blk, NCK, sc, ARBD, BBD, KBD, VBD, PCs, bon, gst, Hf, Hb, tmpH):
        def drain(gen):
            for _ in gen:
                pass

        def interleave(ga, gb):
            a_live = b_live = True
            while a_live or b_live:
                if a_live:
                    try:
                        next(ga)
                    except StopIteration:
                        a_live = False
                if b_live:
                    try:
                        next(gb)
                    except StopIteration:
                        b_live = False
        gc0 = blk * NCK
        drain(self.scan_pre(g, 0, gc0, sc, ARBD, BBD, KBD, VBD))
        for c in range(NCK):
            ch = self.scan_chain(g, c, gc0 + c, sc, ARBD, PCs, bon, gst, Hf, Hb, tmpH)
            if c + 1 < NCK:
                interleave(ch, self.scan_pre(g, c + 1, gc0 + c + 1, sc, ARBD, BBD, KBD, VBD))
            else:
                drain(ch)

    def y_post(self, g, c, gc, s, sc, psY, pkY, bon, gst):
        S = self.S
        pv, cb = self.pv, self.cb
        ys, yq, st, yt, YBD = sc["ys", s], sc["yq", s], sc["st", s], sc["yt", s], sc["YBD", s]
        kys, kyq, kst, kyt, kY = "ys_0", "yq_0", "yst_0", "yt_0", "YBD_0"
        self.I(ACT, "copy", ys[:], psY[:, 0:256].rearrange("p (a b) -> p a b", b=64), reads=[pkY], writes=[kys])
        self.I(DVE, "tensor_reduce", st[:, 0, :], ys[:], AX.X, ALU.add, reads=[kys], writes=[kst])
        self.I(POOL, "tensor_tensor", yq[:], ys[:], ys[:], ALU.mult, reads=[kys], writes=[kyq])
        self.I(DVE, "tensor_reduce", st[:, 1, :], yq[:], AX.X, ALU.add, reads=[kyq], writes=[kst])
        yield
        self.I(DVE, "tensor_scalar", st[:, 2, :], st[:, 0, :], 1.0 / 64, None, ALU.mult, reads=[kst], writes=[kst])
        self.I(DVE, "tensor_tensor", st[:, 3, :], st[:, 2, :], st[:, 2, :], ALU.mult, reads=[kst], writes=[kst])
        self.I(DVE, "scalar_tensor_tensor", st[:, 4, :], st[:, 1, :], 1.0 / 64, st[:, 3, :], ALU.mult, ALU.subtract, reads=[kst], writes=[kst])
        self.I(DVE, "tensor_scalar", st[:, 4, :], st[:, 4, :], 64e-5, None, ALU.add, reads=[kst], writes=[kst])
        self.I(ACT, "activation", st[:, 5, :], st[:, 4, :], AF.Sqrt, reads=[kst], writes=[kst])
        self.I(DVE, "reciprocal", st[:, 6, :], st[:, 5, :], reads=[kst], writes=[kst])
        yield
        self.I(DVE, "tensor_tensor", yq[:], ys[:], st[:, 2, :].unsqueeze(2).to_broadcast([128, 4, 64]), ALU.subtract, reads=[kys, kst], writes=[kyq])
        for h in range(2):
            pr = slice(64 * h, 64 * h + 64)
            self.I(DVE if h == 0 else POOL, "tensor_tensor", YBD[pr, :, pr], yq[pr, :, :], st[pr, 6, :].unsqueeze(2).to_broadcast([64, 4, 64]), ALU.mult, reads=[kyq, kst], writes=[kY])
        yield
        ps, pk = self.psum()
        for q in range(4):
            self.I(PE, "matmul", ps[:, q * 64:(q + 1) * 64], YBD[:, q, :], cb[:, C_ISEL:C_ISEL + 64], start=True, stop=True, reads=[kY, "cb"], writes=[pk], track=(q == 3))
        t0 = 0
        col0 = gc * CH - OWN0
        if col0 < 0:
            t0 = -col0
            col0 = 0
        nt = CH - t0
        cl = c * CH + t0
        lw = pv[:, V_LW + 4 * g: V_LW + 4 * g + 4].unsqueeze(2).to_broadcast([128, 4, nt])
        psv = ps[:, 0:256].rearrange("p (a b) -> p a b", b=64)[:, :, t0:CH]
        self.I(DVE, "tensor_tensor", yt[:, :, 0:nt], psv, lw, ALU.mult, reads=[pk, "pv"], writes=[kyt])
        self.I(POOL, "tensor_tensor", yt[:, :, 0:nt], yt[:, :, 0:nt], bon[:, :, cl:cl + nt], ALU.add, reads=[kyt] + ["bon%d" % q for q in range(4)], writes=[kyt])
        self.I(DVE, "tensor_tensor", self.yaT[:, 4 * g:4 * g + 4, col0:col0 + nt], yt[:, :, 0:nt], gst[:, :, cl:cl + nt], ALU.mult, reads=[kyt] + ["gst%d" % q for q in range(4)], writes=["yaT"])


def core_inputs(inp, c, pv_base):
    b, p = c // 2, c % 2
    x = inp["x_prompt"][b]
    if p == 1:
        xseq = np.ascontiguousarray(x)
    else:
        xseq = np.concatenate([np.zeros((1024, D), np.float32), x[:1024]], axis=0)
    pa = permA()
    return {
        "xseq": xseq,
        "xsamp": np.ascontiguousarray(inp["x_sample"][16 * c:16 * c + 16].reshape(NSM, D)),
        "pvec": core_pvec(pv_base, p),
        "st_pool": np.ascontiguousarray(inp["state_pool"][0, 16 * c:16 * c + 16]),
        "st_conv": np.ascontiguousarray(inp["state_conv"][0, 16 * c:16 * c + 16]),
        "st_shift": np.ascontiguousarray(inp["state_shift"][0, 16 * c:16 * c + 16, 0][:, pa]),
        "st_wkv": np.ascontiguousarray(inp["state_wkv"][0, 16 * c:16 * c + 16].reshape(256, 4096)),
    }


def shared_inputs(inp):
    w_in = inp["w_in"][0]
    wG = np.empty((D, 16, 256), np.float32)
    wG[:, :, 0:128] = w_in[:, 4384:6432].reshape(D, 16, 128)
    wG[:, :, 128:256] = w_in[:, 6432:8480].reshape(D, 16, 128)
    wfi = inp["w_ffn_in"][0]
    wF = np.empty((D, NJ, 256), np.float32)
    wF[:, :, 0:128] = wfi[:, 0:DFF].reshape(D, NJ, 128)
    wF[:, :, 128:256] = wfi[:, DFF:2 * DFF].reshape(D, NJ, 128)
    gpost = np.empty((128, 2, D), np.float32)
    gpost[:, 0, :] = inp["norm_post_mix"][0][None, :]
    gpost[:, 1, :] = inp["norm_post_ffn"][0][None, :]
    return {
        "wA": np.ascontiguousarray(w_in[:, permA()]),
        "consts": host_consts(),
        "w2": np.ascontiguousarray(inp["w2"][0]),
        "a2": np.ascontiguousarray(inp["a2"][0]),
        "g2": np.ascontiguousarray(inp["g2"][0]),
        "wP": np.ascontiguousarray(w_in[:, 3360:4384]),
        "wG": wG,
        "wa": np.ascontiguousarray(inp["w_branch_a"][0]),
        "wbr": np.ascontiguousarray(inp["w_branch_b"][0]),
        "poolw": np.ascontiguousarray(inp["pool_w"][0]),
        "wout": np.ascontiguousarray(inp["w_out"][0]),
        "gpost": gpost,
        "wF": wF,
        "wfo": np.ascontiguousarray(inp["w_ffn_out"][0]),
    }


_NC_CACHE = {}


def get_nc(dbg=(), stages="SABC"):
    key = (tuple(sorted(dbg)), stages)
    if key not in _NC_CACHE:
        B = Builder(dbg=set(dbg), stages=stages)
        nc = B.build()
        _NC_CACHE[key] = (nc, B)
    return _NC_CACHE[key]


def run_cores(inp, cores, dbg=(), stages="SABC", trace=False):
    nc, B = get_nc(dbg, stages)
    sh = shared_inputs(inp)
    pv_base = host_pvec(inp)
    names = set(B.ins.keys())
    maps = []
    for c in cores:
        m = dict(sh)
        m.update(core_inputs(inp, c, pv_base))
        maps.append({k: v for k, v in m.items() if k in names})
    res = run_bass_kernel_spmd(nc, maps, core_ids=list(range(len(cores))), trace=trace)
    return res


def kernel(**inp):
    inp = {k: np.asarray(v) for k, v in inp.items()}
    res = run_cores(inp, list(range(8)))
    R = res.results
    pa = permA()
    y_prompt = np.empty((4, 2048, D), np.float32)
    y_sample = np.empty((128, 4, D), np.float32)
    p_shift = np.empty((1, 4, 1, DS), np.float32)
    p_wkv = np.empty((1, 4, 16, 64, 64), np.float32)
    p_pool = np.empty((1, 4, 15, 1024), np.float32)
    p_conv = np.empty((1, 4, 2, DFF), np.float32)
    s_shift = np.zeros((1, 128, 1, DS), np.float32)
    s_wkv = np.zeros((1, 128, 16, 64, 64), np.float32)
    s_pool = np.empty((1, 128, 15, 1024), np.float32)
    s_conv = np.empty((1, 128, 2, DFF), np.float32)
    for c in range(8):
        b, p = c // 2, c % 2
        r = R[c]
        y_prompt[b, p * 1024:(p + 1) * 1024] = r["o_y"][0:1024]
        y_sample[16 * c:16 * c + 16] = r["o_y"][1024:1024 + NSM].reshape(16, 4, D)
        if p == 1:
            p_shift[0, b, 0, pa] = r["o_pshift"]
            p_wkv[0, b] = r["o_pwkv"]
            p_pool[0, b] = r["o_ppool"]
            p_conv[0, b] = r["o_pconv"]
        s_pool[0, 16 * c:16 * c + 16] = r["o_spool"]
        s_conv[0, 16 * c:16 * c + 16] = r["o_sconv"]
        if "o_sshift" in r:
            s_shift[0, 16 * c:16 * c + 16, 0][:, pa] = r["o_sshift"]
            s_wkv[0, 16 * c:16 * c + 16] = r["o_swkv"].reshape(16, 16, 64, 64)
    return (y_prompt, y_sample, p_shift, p_wkv, p_pool, p_conv, s_shift, s_wkv, s_pool, s_conv)
```

```python
import os
import numpy as np
import concourse.bass as bass
import concourse.mybir as mybir
from contextlib import ExitStack
from concourse.bass_utils import run_bass_kernel_spmd

F32 = mybir.dt.float32
BF16 = mybir.dt.bfloat16
AF = mybir.ActivationFunctionType
ALU = mybir.AluOpType
AX = mybir.AxisListType

PE, ACT, DVE, POOL, SP = "pe", "act", "dve", "pool", "sp"
NDMASEM = 12
EMBED_WAIT = os.environ.get('EMBW', '1') == '1'
SAME_ENGINE_NOWAIT = os.environ.get('SENW', '0') == '1'


class Sched:
    def __init__(self, nc, es):
        self.nc = nc
        self.q = {e: [] for e in (PE, ACT, DVE, POOL, SP)}
        self.cnt = {e: 0 for e in (PE, ACT, DVE, POOL)}
        self.sem = {e: es.enter_context(nc.semaphore("s_" + e)) for e in (PE, ACT, DVE, POOL)}
        self.dsem = {}
        self.dcnt = {}
        for e in (SP, "spo", "bg", POOL):
            self.dsem[e] = [es.enter_context(nc.semaphore("d_%s%d" % (e, i))) for i in range(NDMASEM)]
            self.dcnt[e] = 0
        self.pending = {e: [] for e in (PE, ACT, DVE, POOL, SP)}
        self.tok_ring = {}
        self.waited = {}
        self.lastw = {}
        self.readers = {}
        self.all_tokens = {}

    def _deps(self, eng, reads, writes):
        toks = []
        for k in reads:
            w = self.lastw.get(k)
            if w is not None:
                toks.append(w)
        for k in writes:
            w = self.lastw.get(k)
            if w is not None:
                toks.append(w)
            toks.extend(self.readers.get(k, ()))
        need = {}
        for (s, v, src) in toks:
            if src == eng and (eng == PE or (SAME_ENGINE_NOWAIT and eng in (ACT, DVE))):
                continue
            if self.waited.get((eng, s.name), 0) >= v:
                continue
            if need.get(s.name, (None, 0))[1] < v:
                need[s.name] = (s, v)
        for (s, v) in self.pending[eng]:
            if self.waited.get((eng, s.name), 0) >= v:
                continue
            if need.get(s.name, (None, 0))[1] < v:
                need[s.name] = (s, v)
        self.pending[eng] = []
        out = []
        for name, (s, v) in need.items():
            self.waited[(eng, name)] = v
            out.append((s, v))
        return out

    def barrier(self):
        if self.frozen:
            return
        allw = []
        for e in (PE, ACT, DVE, POOL):
            if self.cnt[e] > 0:
                allw.append((self.sem[e], self.cnt[e], e))
        for name, (s, v) in self.all_tokens.items():
            if self.tok_ring.get(name) != "bg":
                allw.append((s, v, "dma"))
        for eng in (PE, ACT, DVE, POOL, SP):
            for (s, v, src) in allw:
                if src == eng:
                    continue
                self.pending[eng].append((s, v))

    def _commit(self, tok, reads, writes):
        for k in writes:
            self.lastw[k] = tok
            self.readers[k] = []
        for k in reads:
            lst = self.readers.setdefault(k, [])
            lst[:] = [t for t in lst if t[0].name != tok[0].name]
            lst.append(tok)

    frozen = False

    def op(self, eng, fn, reads=(), writes=(), track=True):
        if self.frozen:
            return None
        waits = self._deps(eng, reads, writes)
        tok = None
        if not track:
            assert eng == PE
            self.pend_r = getattr(self, "pend_r", set()) | set(reads)
        if track:
            self.cnt[eng] += 1
            tok = (self.sem[eng], self.cnt[eng], eng)
            if eng == PE and getattr(self, "pend_r", None):
                reads = list(set(reads) | self.pend_r)
                self.pend_r = set()
            self._commit(tok, reads, writes)
        self.q[eng].append((waits, fn, tok))
        return tok

    def dma(self, qeng, out, in_, reads=(), writes=(), ring=None, **kw):
        if self.frozen:
            return None
        waits = self._deps(qeng, reads, writes)
        rk = ring or qeng
        n = self.dcnt[rk]
        self.dcnt[rk] += 1
        s = self.dsem[rk][n % NDMASEM]
        v = 16 * (n // NDMASEM + 1)
        tok = (s, v, "dma")
        self._commit(tok, reads, writes)
        self.q[qeng].append((waits, lambda e: e.dma_start(out=out, in_=in_, **kw), tok))
        self.all_tokens[s.name] = (s, v)
        self.tok_ring[s.name] = rk
        return tok

    def emit(self):
        nc = self.nc
        fin = dict(self.all_tokens)
        for e in (PE, ACT, DVE, POOL):
            if self.cnt[e] > 0:
                fin[self.sem[e].name] = (self.sem[e], self.cnt[e])
        q = self.q
        with nc.Block() as block:
            def run(eng_name):
                def body(e):
                    for (waits, fn, tok) in q[eng_name]:
                        emb = None
                        if EMBED_WAIT and waits:
                            emb = waits[-1]
                            waits = waits[:-1]
                        for (s, v) in waits:
                            e.wait_ge(s, v)
                        ins = fn(e)
                        if emb is not None:
                            ins._wait_ge(emb[0], emb[1])
                        if tok is not None:
                            ins.then_inc(tok[0], 16 if tok[2] == "dma" else 1)
                    if eng_name == SP:
                        for name, (s, v) in fin.items():
                            e.wait_ge(s, v)
                return body
            block.tensor(run(PE))
            block.scalar(run(ACT))
            block.vector(run(DVE))
            block.gpsimd(run(POOL))
            block.sync(run(SP))


D = 2048
DS = 3360
NPR = 2048
NSM = 64
OWN0 = 992
NOWN = 1120
BLK = 512
NH = 512
CH = 64
C_ID, C_BONES, C_ISEL, C_MUS, C_MUI, C_MLS, C_M01 = 0, 128, 256, 320, 448, 576, 704
NCONST = 704 + 512
V_MULR = 0
V_MU = 4
V_W0, V_A0, V_KK, V_KA, V_RK, V_LW, V_LB = 28, 36, 44, 52, 60, 68, 76
V_G1 = 84
V_PS = 100
V_G3 = 108
V_FLAG = 124
V_INVC = 125
V_CW = 189
NV = 189 + 176
DFF = 5632
NJ = 44
DECAY_C = 0.6065306597126334


def host_consts():
    c = np.zeros((128, NCONST), np.float32)
    p = np.arange(128)[:, None]
    j = np.arange(128)[None, :]
    c[:, C_ID:C_ID + 128] = (p == j)
    c[:, C_BONES:C_BONES + 128] = (p // 64 == j // 64)
    c[:, C_ISEL:C_ISEL + 64] = (p % 64 == np.arange(64)[None, :])
    c[:, C_MUS:C_MUS + 128] = (p % 64 < j % 64)
    c[:, C_MUI:C_MUI + 128] = (p % 64 <= j % 64)
    c[:, C_MLS:C_MLS + 128] = (p % 64 > j % 64)
    c[:, C_M01:C_M01 + 512] = (np.arange(512)[None, :] % 64 != 0)
    return c


def permA():
    idx = list(range(3072, 3360))
    for q in range(8):
        idx += list(range(q * 128, q * 128 + 128))
        idx += list(range(1024 + q * 128, 1024 + q * 128 + 128))
        idx += list(range(2048 + q * 128, 2048 + q * 128 + 128))
    return np.array(idx)


def host_pvec(inp):
    v = np.zeros((128, NV), np.float32)
    mu = inp["mu_shift"][0]
    v[0:64, 0] = mu[3072:3136]
    v[0:64, 1] = mu[3136:3200]
    v[0:128, 2] = mu[3200:3328]
    v[0:32, 3] = mu[3328:3360]
    for q in range(8):
        for t in range(3):
            v[:, V_MU + 3 * q + t] = mu[t * 1024 + q * 128: t * 1024 + q * 128 + 128]
    for (col, name) in ((V_W0, "w0"), (V_A0, "a0"), (V_KK, "k_k"), (V_KA, "k_a"), (V_RK, "r_k"), (V_LW, "lnx_w"), (V_LB, "lnx_b")):
        a = inp[name][0].reshape(-1)
        for q in range(8):
            v[:, col + q] = a[q * 128:(q + 1) * 128]
    g = inp["norm_pre_mix"][0]
    g3 = inp["norm_pre_ffn"][0]
    for k in range(16):
        v[:, V_G1 + k] = g[k * 128:(k + 1) * 128]
        v[:, V_G3 + k] = g3[k * 128:(k + 1) * 128]
    psc = inp["pool_scale"][0]
    for k in range(8):
        v[:, V_PS + k] = psc[k * 128:(k + 1) * 128]
    cw, cbias = inp["conv_w"][0], inp["conv_b"][0]
    for j in range(NJ):
        for t in range(3):
            v[:, V_CW + 4 * j + t] = cw[t, j * 128:(j + 1) * 128]
        v[:, V_CW + 4 * j + 3] = cbias[j * 128:(j + 1) * 128]
    return v


def core_pvec(base, p):
    v = base.copy()
    v[:, V_FLAG] = float(p)
    for gi, win in enumerate((2, 4, 8, 16)):
        for j in range(16):
            pos = p * 1024 + j
            v[:, V_INVC + gi * 16 + j] = 1.0 / min(win, pos + 1)
    return v


class _TAlias:
    def __init__(self, phys, alias, prefix="t_"):
        self.phys = phys
        self.alias = alias
        self.prefix = prefix

    def _n(self, n):
        return self.alias.get(n, n)

    def __getitem__(self, n):
        return self.phys[self._n(n)]

    def key(self, n):
        return self.prefix + self._n(n)


class StopBuild(Exception):
    pass


class Builder:
    stop_at = None

    def ckpt(self, name):
        if self.stop_at == name and not self.S.frozen:
            print("frozen at", name)
            self.S.frozen = True
            self.dbg = set()

    def __init__(self, dbg=None, stages="A"):
        self.dbg = dbg or set()
        self.stages = stages
        self.nc = bass.Bass("TRN2", target_bir_lowering=False)
        self.ins = {}
        self.outs = {}
        self.psn = 0

    def din(self, name, shape, dt=F32):
        t = self.nc.dram_tensor(name, list(shape), dt, kind="ExternalInput").ap()
        self.ins[name] = t
        return t

    def dout(self, name, shape, dt=F32):
        t = self.nc.dram_tensor(name, list(shape), dt, kind="ExternalOutput").ap()
        self.outs[name] = t
        return t

    def I(self, eng, meth, *args, reads=(), writes=(), track=True, **kw):
        return self.S.op(eng, lambda e: getattr(e, meth)(*args, **kw), reads=reads, writes=writes, track=track)

    def stage_out(self, name, tile_ap, shape, key):
        scr = self.nc.dram_tensor("scr_" + name, list(shape), F32).ap()
        self.S.dma(SP, scr, tile_ap, reads=[key], writes=["scr_" + name], ring="spo")
        return scr

    def bg(self, dst, src, name):
        self.S.dma(SP, dst, src, reads=["scr_" + name], ring="bg", allow_slow_non_contiguous=True)

    def sb(self, es, name, shape, dt=F32):
        return es.enter_context(self.nc.sbuf_tensor(name, list(shape), dt))

    def psum(self):
        i = self.psn % 8
        self.psn += 1
        return self.PS[i], "ps%d" % i

    def build(self):
        nc = self.nc
        xseq = self.xseq = self.din("xseq", [NPR, D])
        xsamp = self.xsamp = self.din("xsamp", [NSM, D])
        wA = self.din("wA", [D, DS])
        consts = self.din("consts", [128, NCONST])
        pvec = self.din("pvec", [128, NV])
        w2 = self.din("w2", [64, 1024])
        a2 = self.din("a2", [64, 1024])
        g2 = self.din("g2", [160, 1024])
        self.wP = self.din("wP", [D, 1024])
        self.wG = self.din("wG", [D, 16, 256])
        self.wa = self.din("wa", [1024, D])
        self.wbr = self.din("wbr", [1024, D])
        self.poolw = self.din("poolw", [4, 256, 256])
        self.wout = self.din("wout", [D, D])
        self.gpost = self.din("gpost", [128, 2, D])
        self.wF = self.din("wF", [D, NJ, 256])
        self.wfo = self.din("wfo", [DFF, D])
        self.st_pool = self.din("st_pool", [16, 15, 1024])
        self.st_conv = self.din("st_conv", [16, 2, DFF])
        self.o_pshift = self.dout("o_pshift", [DS])
        self.o_pwkv = self.dout("o_pwkv", [16, 64, 64])
        self.o_ppool = self.dout("o_ppool", [15, 1024])
        self.o_pconv = self.dout("o_pconv", [2, DFF])
        self.o_spool = self.dout("o_spool", [16, 15, 1024])
        self.o_sconv = self.dout("o_sconv", [16, 2, DFF])
        self.o_y = self.dout("o_y", [1024 + NSM, D])
        self.st_shift = self.din("st_shift", [16, DS])
        self.st_wkv = self.din("st_wkv", [256, 4096])
        self.o_sshift = self.dout("o_sshift", [16, DS])
        self.o_swkv = self.dout("o_swkv", [256, 4096])
        self.x1s = nc.dram_tensor("x1s", [NOWN, D], F32).ap()
        self.scrS = nc.dram_tensor("scrS", [NSM, 6, 1024], F32).ap()
        self.scrY = nc.dram_tensor("scrY", [NSM, 1024], F32).ap()
        self.scrH = nc.dram_tensor("scrH", [128, 16, NOWN], BF16).ap()
        if "ya" in self.dbg:
            self.o_ya = self.dout("d_ya", [128, 8, NOWN])
        with ExitStack() as es:
            S = self.S = Sched(nc, es)
            self.PS = [es.enter_context(nc.psum_tensor("ps%d" % i, [128, 512], F32)) for i in range(8)]
            cf = self.cf = self.sb(es, "cf", [128, NCONST])
            cb = self.cb = self.sb(es, "cb", [128, 320], BF16)
            pv = self.pv = self.sb(es, "pv", [128, NV])
            S.dma(SP, cf[:], consts, writes=["cf"])
            S.dma(SP, pv[:], pvec, writes=["pv"])
            S.dma(POOL, cb[:], consts[:, 0:320], writes=["cb"])
            self.wslot = 0
            bufX = self.bufX = self.sb(es, "bufX", [128, 16, NOWN], BF16)
            self.yaT = bufX[:, 0:8, :]
            self.ybT = bufX[:, 8:16, :]
            with ExitStack() as esw:
                self.wb = [self.sb(esw, "wb%d" % i, [128, 16, 384], BF16) for i in range(2)]
                self.wA_ap = wA
                with ExitStack() as esA:
                    w2b = self.w2b = self.sb(esA, "w2b", [64, 1024], BF16)
                    a2b = self.a2b = self.sb(esA, "a2b", [64, 1024], BF16)
                    g2a = self.g2a = self.sb(esA, "g2a", [128, 1024], BF16)
                    g2b = self.g2b = self.sb(esA, "g2b", [128, 1024], BF16)
                    S.dma(POOL, w2b[:], w2, writes=["w2b"])
                    S.dma(POOL, a2b[:], a2, writes=["a2b"])
                    S.dma(POOL, g2a[:], g2[0:128, :], writes=["g2a"])
                    self.I(POOL, "memset", g2b[:], 0.0, writes=["g2b"])
                    S.dma(POOL, g2b[0:32, :], g2[128:160, :], writes=["g2b"])
                    if "S" in self.stages:
                        self.stageS()
                    self.stageA(esA, xseq, wA)
                if "ya" in self.dbg:
                    with ExitStack() as esd:
                        yaf = self.sb(esd, "yaf", [128, 8, NOWN], F32)
                        self.I(DVE, "tensor_copy", yaf[:], self.yaT, reads=["yaT"], writes=["yaf"])
                        S.dma(SP, self.o_ya, yaf[:], reads=["yaf"], ring="spo")
                        S.barrier()
                if "B" in self.stages:
                    with ExitStack() as esm:
                        self.mT = self.sb(esm, "mT", [128, 16, NOWN], BF16)
                        with ExitStack() as esb:
                            self.hTo = self.sb(esb, "hTo", [128, 16, NOWN], BF16)
                            self.stageB12(esb)
                        self.stageB3(esm)
            if "C" in self.stages:
                self.stageC(es)
            S.emit()
        return nc

    def tok_groups(self):
        return [(0, 512), (512, 512), (1024, NOWN - 1024)]

    def stageB12(self, es_outer):
        S = self.S
        pv, cf, cb = self.pv, self.cf, self.cb
        hT = self.hTo
        with ExitStack() as es:
            sb = lambda n, s, d=F32: self.sb(es, n, s, d)
            self.xt = [sb("xtB", [128, D])]
            self.xb = [sb("xbB", [128, D], BF16)]
            self.xst = [sb("xstB", [128, 4])]
            for (c0, c1, k) in ((0, 32, "scrH_1"), (32, 544, "scrH_2"), (544, 1056, "scrH_3"), (1056, NOWN, "scrH_s")):
                S.dma(SP, hT[:, :, c0:c1], self.scrH[:, :, c0:c1], reads=[k], writes=["hTo"])
            self.ckpt("B0")
            ybT = self.ybT
            with ExitStack() as esp:
                sbp = lambda n, s, d=F32: self.sb(esp, n, s, d)
                pwb = sbp("pwb", [128, 4, 2, 256], BF16)
                S.dma(POOL, pwb[:], self.poolw.rearrange("g (k p) n -> p g k n", p=128), writes=["pwb"])
                phT = sbp("phT", [128, 8, 16, 15])
                for half in range(2):
                    sp_t = sbp("sp_t%d" % half, [120, 1024])
                    S.dma(SP, sp_t[:], self.st_pool[8 * half: 8 * half + 8].rearrange("b j c -> (b j) c"), writes=["sp_t%d" % half])
                    for c4 in range(2):
                        ps, pk = self.psum()
                        for ch in range(4):
                            self.I(PE, "transpose", ps[:, ch * 120:(ch + 1) * 120], sp_t[:, (c4 * 4 + ch) * 128:(c4 * 4 + ch + 1) * 128], cf[0:120, 0:120],
                                   reads=["sp_t%d" % half, "cf"], writes=[pk], track=(ch == 3))
                        self.I(DVE, "tensor_copy", phT[:, c4 * 4:c4 * 4 + 4, 8 * half:8 * half + 8, :],
                               ps[:, 0:480].rearrange("p (a b j) -> p a b j", a=4, b=8), reads=[pk], writes=["phT"])
                self.ckpt("B1a")
                S.dma(SP, self.o_spool[:, 0:11, :], self.st_pool[:, 4:15, :], ring="spo")
                zp = sbp("zp", [128, NOWN])
                pa_ = sbp("ppA", [128, NOWN])
                pb_ = sbp("ppB", [128, NOWN])
                dT = sbp("dT", [128, 8, NOWN], BF16)
                self.I(POOL, "memset", dT[:], 0.0, writes=["dT"])
                ppo = sbp("ppo", [128, 8, 15])
                spo = sbp("spo", [128, 8, 16, 4])
                bs = sbp("bs", [128, 16, 19])
                bs2 = sbp("bs2", [128, 16, 19])
                slabs = [(self.wP[:, ch * 128:(ch + 1) * 128], 128) for ch in range(8)]
                stream = self.slab_stream(slabs)
                NP_ = 1056
                for ch in range(8):
                    wt, wk = next(stream)
                    gi = ch // 2
                    win = 2 << gi
                    for (t0, tn) in self.tok_groups():
                        ps, pk = self.psum()
                        for k in range(16):
                            self.I(PE, "matmul", ps[:, 0:tn], wt[:, k, 0:128], hT[:, k, t0:t0 + tn], start=(k == 0), stop=(k == 15),
                                   reads=[wk, "hTo"], writes=[pk], track=(k == 15))
                        self.I(ACT, "copy", zp[:, t0:t0 + tn], ps[:, 0:tn], reads=[pk], writes=["zp"])
                    src, ksrc = zp, "zp"
                    bufs = [(pa_, "ppA"), (pb_, "ppB")]
                    for j in range(gi + 1):
                        sh = 1 << j
                        dst, kdst = bufs[j % 2]
                        self.I(DVE if j % 2 == 0 else POOL, "tensor_tensor", dst[:, sh:NP_], src[:, sh:NP_], src[:, 0:NP_ - sh], ALU.add,
                               reads=[ksrc], writes=[kdst])
                        src, ksrc = dst, kdst
                    lo = win - 1
                    self.I(DVE, "scalar_tensor_tensor", dT[:, ch, lo:NP_], src[:, lo:NP_], 1.0 / win, zp[:, lo:NP_], ALU.mult, ALU.subtract,
                           reads=[ksrc, "zp"], writes=["dT"])
                    other, kother = bufs[(gi + 1) % 2]
                    self.I(POOL, "tensor_tensor", other[:, 32:48], src[:, 32:48], pv[:, V_INVC + gi * 16: V_INVC + gi * 16 + 16], ALU.mult,
                           reads=[ksrc, "pv"], writes=[kother])
                    self.I(POOL, "tensor_tensor", dT[:, ch, 32:48], other[:, 32:48], zp[:, 32:48], ALU.subtract, reads=[kother, "zp"], writes=["dT"])
                    self.I(POOL, "tensor_copy", ppo[:, ch, :], zp[:, NP_ - 15:NP_], reads=["zp"], writes=["ppo"])
                    zps = zp[:, NP_:NOWN].rearrange("p (b t) -> p b t", t=4)
                    self.I(POOL, "tensor_copy", bs[:, :, 0:15], phT[:, ch, :, :], reads=["phT"], writes=["bs"])
                    self.I(POOL, "tensor_copy", bs[:, :, 15:19], zps, reads=["zp"], writes=["bs"])
                    self.I(POOL, "tensor_copy", spo[:, ch, :, :], zps, reads=["zp"], writes=["spo"])
                    ssrc, kss = bs, "bs"
                    sbufs = [(bs2, "bs2"), (bs, "bs")]
                    for j in range(gi + 1):
                        sh = 1 << j
                        dst, kdst = sbufs[j % 2]
                        self.I(DVE, "tensor_tensor", dst[:, :, sh:19], ssrc[:, :, sh:19], ssrc[:, :, 0:19 - sh], ALU.add, reads=[kss], writes=[kdst])
                        ssrc, kss = dst, kdst
                    self.I(DVE, "scalar_tensor_tensor", dT[:, ch, NP_:NOWN].rearrange("p (b t) -> p b t", t=4), ssrc[:, :, 15:19], 1.0 / win, zps,
                           ALU.mult, ALU.subtract, reads=[kss, "zp"], writes=["dT"])
                self.ckpt("B1b")
                scr = self.stage_out("ppo", ppo[:], [128, 8, 15], "ppo")
                for ch in range(8):
                    self.bg(self.o_ppool[:, ch * 128:(ch + 1) * 128].rearrange("j p -> p j"), scr[:, ch, :], "ppo")
                sprow = sbp("sprow", [NSM, 1024])
                for c4 in range(2):
                    ps, pk = self.psum()
                    for ch in range(4):
                        self.I(PE, "transpose", ps[0:NSM, ch * 128:(ch + 1) * 128], spo[:, c4 * 4 + ch, :, :].rearrange("p b t -> p (b t)"), cf[:, C_ID:C_ID + 128],
                               reads=["spo", "cf"], writes=[pk], track=(ch == 3))
                    self.I(DVE, "tensor_copy", sprow[:, c4 * 512:(c4 + 1) * 512], ps[0:NSM, :], reads=[pk], writes=["sprow"])
                for b in range(16):
                    S.dma(SP, self.o_spool[b, 11:15, :], sprow[4 * b:4 * b + 4, :], reads=["sprow"], ring="spo")
                self.ckpt("B1c")
                for oc in range(8):
                    gi, o2 = oc // 2, oc % 2
                    for (t0, tn) in self.tok_groups():
                        ps, pk = self.psum()
                        for kk in range(2):
                            self.I(PE, "matmul", ps[:, 0:tn], pwb[:, gi, kk, o2 * 128:(o2 + 1) * 128], dT[:, 2 * gi + kk, t0:t0 + tn],
                                   start=(kk == 0), stop=(kk == 1), reads=["pwb", "dT"], writes=[pk], track=(kk == 1))
                        self.I(ACT, "activation", ybT[:, oc, t0:t0 + tn], ps[:, 0:tn], AF.Copy, scale=pv[:, V_PS + oc: V_PS + oc + 1],
                               reads=[pk, "pv"], writes=["ybT"])
                S.barrier()
            self.ckpt("B1d")
            mT = self.mT
            tmp = [sb("b2t%d" % i, [128, 512]) for i in range(4)]
            njs = 16

            def issue(j):
                slot = self.wslot % 2
                self.wslot += 1
                key = "wb%d" % slot
                w = self.wb[slot]
                wflat = w[:].rearrange("p a b -> p (a b)")
                S.dma(POOL, wflat[:, 0:4096].rearrange("p (k n) -> p k n", n=256), self.wG[:, j, :].rearrange("(k p) n -> p k n", p=128), writes=[key])
                S.dma(POOL, wflat[:, 4096:5120].rearrange("p (k n) -> p k n", n=128), self.wa[:, j * 128:(j + 1) * 128].rearrange("(k p) n -> p k n", p=128), writes=[key])
                S.dma(POOL, wflat[:, 5120:6144].rearrange("p (k n) -> p k n", n=128), self.wbr[:, j * 128:(j + 1) * 128].rearrange("(k p) n -> p k n", p=128), writes=[key])
                return (wflat, key)
            cur = issue(0)
            for j in range(njs):
                nxt = issue(j + 1) if j + 1 < njs else None
                wflat, wk = cur
                wg = wflat[:, 0:4096].rearrange("p (k n) -> p k n", n=256)
                wa_ = wflat[:, 4096:5120].rearrange("p (k n) -> p k n", n=128)
                wb_ = wflat[:, 5120:6144].rearrange("p (k n) -> p k n", n=128)
                for (t0, tn) in self.tok_groups():
                    psa, pka = self.psum()
                    psb, pkb = self.psum()
                    ppa, pkpa = self.psum()
                    ppb, pkpb = self.psum()
                    for k in range(16):
                        self.I(PE, "matmul", psa[:, 0:tn], wg[:, k, 0:128], hT[:, k, t0:t0 + tn], start=(k == 0), stop=(k == 15),
                               reads=[wk, "hTo"], writes=[pka], track=(k == 15))
                    for k in range(16):
                        self.I(PE, "matmul", psb[:, 0:tn], wg[:, k, 128:256], hT[:, k, t0:t0 + tn], start=(k == 0), stop=(k == 15),
                               reads=[wk, "hTo"], writes=[pkb], track=(k == 15))
                    for k in range(8):
                        self.I(PE, "matmul", ppa[:, 0:tn], wa_[:, k, :], self.yaT[:, k, t0:t0 + tn], start=(k == 0), stop=(k == 7),
                               reads=[wk, "yaT"], writes=[pkpa], track=(k == 7))
                    for k in range(8):
                        self.I(PE, "matmul", ppb[:, 0:tn], wb_[:, k, :], ybT[:, k, t0:t0 + tn], start=(k == 0), stop=(k == 7),
                               reads=[wk, "ybT"], writes=[pkpb], track=(k == 7))
                    self.I(ACT, "activation", tmp[0][:, 0:tn], psa[:, 0:tn], AF.Sigmoid, reads=[pka], writes=["b2t0"])
                    self.I(ACT, "activation", tmp[1][:, 0:tn], psb[:, 0:tn], AF.Sigmoid, reads=[pkb], writes=["b2t1"])
                    self.I(DVE, "tensor_tensor", tmp[2][:, 0:tn], tmp[0][:, 0:tn], ppa[:, 0:tn], ALU.mult, reads=["b2t0", pkpa], writes=["b2t2"])
                    self.I(DVE, "tensor_tensor", tmp[3][:, 0:tn], tmp[1][:, 0:tn], ppb[:, 0:tn], ALU.mult, reads=["b2t1", pkpb], writes=["b2t3"])
                    self.I(POOL, "tensor_tensor", mT[:, j, t0:t0 + tn], tmp[2][:, 0:tn], tmp[3][:, 0:tn], ALU.add, reads=["b2t2", "b2t3"], writes=["mT"])
                cur = nxt
            S.barrier()

    def tok_tiles(self):
        self.ckpt("B2")
        return [(0, 32)] + [(32 + 128 * i, 128) for i in range(8)] + [(1056, NSM)]

    def stageB3(self, es_outer):
        S = self.S
        pv, cf, cb = self.pv, self.cf, self.cb
        mT, h2T = self.mT, self.bufX
        with ExitStack() as es:
            sb = lambda n, s, d=F32: self.sb(es, n, s, d)
            woutb = sb("woutb", [128, 16, D], BF16)
            for n in range(4):
                S.dma(POOL, woutb[:, :, n * 512:(n + 1) * 512], self.wout[:, n * 512:(n + 1) * 512].rearrange("(k p) n -> p k n", p=128), writes=["woutb%d" % n])
            gp = sb("gp", [128, D])
            S.dma(SP, gp[:], self.gpost[:, 0, :], writes=["gp"])
            mo = sb("mo", [128, D])
            xt = sb("xt3", [128, D])
            x1 = sb("x1t", [128, D])
            xb = sb("xb3", [128, D], BF16)
            st = sb("st3", [128, 16])
            for (c0, nt) in self.tok_tiles():
                xrows = self.xseq[OWN0 + c0: OWN0 + c0 + nt, :] if c0 < 1056 else self.xsamp
                S.dma(SP, xt[0:nt, :], xrows, writes=["xt3"])
                self.I(POOL, "memset", st[:, 0:4], 0.0, writes=["st3"])
                self.ckpt("B3pre")
                for n in range(4):
                    ps, pk = self.psum()
                    for k in range(16):
                        self.I(PE, "matmul", ps[0:nt, :], mT[:, k, c0:c0 + nt], woutb[:, k, n * 512:(n + 1) * 512], start=(k == 0), stop=(k == 15),
                               reads=["mT", "woutb%d" % n], writes=[pk], track=(k == 15))
                    self.I(DVE, "tensor_copy", mo[0:nt, n * 512:(n + 1) * 512], ps[0:nt, :], reads=[pk], writes=["mo"])
                    self.I(ACT, "activation", xb[0:nt, n * 512:(n + 1) * 512], mo[0:nt, n * 512:(n + 1) * 512], AF.Square, accum_out=st[0:nt, n:n + 1], reads=["mo"], writes=["xb3", "st3"])
                self.ckpt("B3a")
                self.I(DVE, "tensor_reduce", st[0:nt, 4:5], st[0:nt, 0:4], AX.X, ALU.add, reads=["st3"], writes=["st3"])
                self.I(DVE, "tensor_scalar", st[0:nt, 5:6], st[0:nt, 4:5], 1.0 / D, 1e-6, ALU.mult, ALU.add, reads=["st3"], writes=["st3"])
                self.I(ACT, "activation", st[0:nt, 6:7], st[0:nt, 5:6], AF.Sqrt, reads=["st3"], writes=["st3"])
                self.I(DVE, "reciprocal", st[0:nt, 7:8], st[0:nt, 6:7], reads=["st3"], writes=["st3"])
                self.I(DVE, "scalar_tensor_tensor", mo[0:nt, :], mo[0:nt, :], st[0:nt, 7:8], gp[0:nt, :], ALU.mult, ALU.mult, reads=["mo", "st3", "gp"], writes=["mo"])
                self.I(POOL, "tensor_tensor", x1[0:nt, :], mo[0:nt, :], xt[0:nt, :], ALU.add, reads=["mo", "xt3"], writes=["x1t"])
                self.ckpt("B3b")
                S.dma(SP, self.x1s[c0:c0 + nt, :], x1[0:nt, :], reads=["x1t"], writes=["x1s"])
                self.ckpt("B3c")
                self.norm_T(x1, "x1t", nt, h2T[:, :, c0:c0 + nt], "h2T", xb, "xb3", st, "st3", 8, V_G3)
                self.ckpt("B3d")
            S.barrier()

    def norm_T(self, xt, kx, ntok, hT, hkey, xb, kb, st, ks, sc0, gbase):
        self.I(POOL, "memset", st[:, sc0:sc0 + 1], 0.0, writes=[ks])
        self.I(ACT, "activation", xb[0:ntok, :], xt[0:ntok, :], AF.Square, accum_out=st[0:ntok, sc0:sc0 + 1], reads=[kx], writes=[kb, ks])
        self.I(DVE, "tensor_scalar", st[0:ntok, sc0 + 1:sc0 + 2], st[0:ntok, sc0:sc0 + 1], 1.0 / D, 1e-6, ALU.mult, ALU.add, reads=[ks], writes=[ks])
        self.I(ACT, "activation", st[0:ntok, sc0 + 2:sc0 + 3], st[0:ntok, sc0 + 1:sc0 + 2], AF.Sqrt, reads=[ks], writes=[ks])
        self.I(DVE, "reciprocal", st[0:ntok, sc0 + 3:sc0 + 4], st[0:ntok, sc0 + 2:sc0 + 3], reads=[ks], writes=[ks])
        self.I(ACT, "activation", xb[0:ntok, :], xt[0:ntok, :], AF.Copy, scale=st[0:ntok, sc0 + 3:sc0 + 4], reads=[kx, ks, kb], writes=[kb])
        for half in range(2):
            ps, pk = self.psum()
            pT = ps[:].bitcast(BF16).rearrange("p (a b) -> p a b", b=128)
            for k8 in range(8):
                kc = half * 8 + k8
                self.I(PE, "transpose", pT[:, k8, 0:ntok], xb[0:ntok, kc * 128:(kc + 1) * 128], self.cb[0:ntok, 0:ntok],
                       reads=[kb, "cb"], writes=[pk], track=(k8 == 7))
            gcol = self.pv[:, gbase + half * 8: gbase + half * 8 + 8].unsqueeze(2).to_broadcast([128, 8, ntok])
            self.I(DVE, "tensor_tensor", hT[:, half * 8:half * 8 + 8, :], pT[:, :, 0:ntok], gcol, ALU.mult, reads=[pk, "pv"], writes=[hkey])

    def stageC(self, es_outer):
        S = self.S
        pv, cf, cb = self.pv, self.cf, self.cb
        h2T = self.bufX
        NA = 1024 + NSM
        with ExitStack() as es:
            sb = lambda n, s, d=F32: self.sb(es, n, s, d)
            actT = sb("actT", [128, NJ, NA], BF16)
            with ExitStack() as es1:
                sb1 = lambda n, s, d=F32: self.sb(es1, n, s, d)
                wf = [sb1("wf%d" % i, [128, 16, 256], BF16) for i in range(2)]
                gt = [sb1("gt%d" % i, [128, NOWN]) for i in range(2)]
                up = [sb1("up%d" % i, [128, NOWN]) for i in range(2)]
                cv = sb1("cv", [128, NA])
                ge = sb1("ge", [128, NA])
                gs6 = sb1("gs6", [128, 16, 6])
                chT = sb1("chT", [128, NJ, 32])
                pco = sb1("pco", [128, NJ, 2])
                sco = sb1("sco", [128, NJ, 16, 2])
                ct = sb1("ct", [32, 1408])
                stc = self.st_conv.rearrange("b r c -> (b r) c")
                for pc in range(4):
                    S.dma(SP, ct[:], stc[:, pc * 1408:(pc + 1) * 1408], writes=["ct"])
                    ps, pk = self.psum()
                    for jj in range(11):
                        self.I(PE, "transpose", ps[:, jj * 32:(jj + 1) * 32], ct[:, jj * 128:(jj + 1) * 128], cf[0:32, 0:32],
                               reads=["ct", "cf"], writes=[pk], track=(jj == 10))
                    self.I(DVE, "tensor_copy", chT[:, pc * 11:(pc + 1) * 11, :], ps[:, 0:352].rearrange("p (a b) -> p a b", b=32), reads=[pk], writes=["chT"])

                def issue(j):
                    slot = j % 2
                    S.dma(POOL, wf[slot][:], self.wF[:, j, :].rearrange("(k p) n -> p k n", p=128), writes=["wf%d" % slot])
                cwc = lambda j, t: pv[:, V_CW + 4 * j + t: V_CW + 4 * j + t + 1]
                issue(0)
                for j in range(NJ):
                    if j + 1 < NJ:
                        issue(j + 1)
                    w, wk = wf[j % 2], "wf%d" % (j % 2)
                    g_, kg = gt[j % 2], "gt%d" % (j % 2)
                    u_, ku = up[j % 2], "up%d" % (j % 2)
                    for (t0, tn) in self.tok_groups():
                        psg, pkg = self.psum()
                        psu, pku = self.psum()
                        for k in range(16):
                            self.I(PE, "matmul", psg[:, 0:tn], w[:, k, 0:128], h2T[:, k, t0:t0 + tn], start=(k == 0), stop=(k == 15),
                                   reads=[wk, "h2T"], writes=[pkg], track=(k == 15))
                        for k in range(16):
                            self.I(PE, "matmul", psu[:, 0:tn], w[:, k, 128:256], h2T[:, k, t0:t0 + tn], start=(k == 0), stop=(k == 15),
                                   reads=[wk, "h2T"], writes=[pku], track=(k == 15))
                        self.I(ACT, "copy", g_[:, t0:t0 + tn], psg[:, 0:tn], reads=[pkg], writes=[kg])
                        self.I(DVE, "tensor_copy", u_[:, t0:t0 + tn], psu[:, 0:tn], reads=[pku], writes=[ku])
                    self.I(POOL, "tensor_scalar", g_[:, 30:32], g_[:, 30:32], pv[:, V_FLAG:V_FLAG + 1], None, ALU.mult, reads=[kg, "pv"], writes=[kg])
                    self.I(ACT, "activation", cv[:, 0:1024], g_[:, 32:1056], AF.Identity, bias=cwc(j, 3), scale=cwc(j, 2), reads=[kg, "pv"], writes=["cv"])
                    self.I(DVE, "scalar_tensor_tensor", cv[:, 0:1024], g_[:, 31:1055], cwc(j, 1), cv[:, 0:1024], ALU.mult, ALU.add, reads=[kg, "cv", "pv"], writes=["cv"])
                    self.I(DVE, "scalar_tensor_tensor", cv[:, 0:1024], g_[:, 30:1054], cwc(j, 0), cv[:, 0:1024], ALU.mult, ALU.add, reads=[kg, "cv", "pv"], writes=["cv"])
                    gss = g_[:, 1056:NOWN].rearrange("p (b t) -> p b t", t=4)
                    self.I(POOL, "tensor_copy", gs6[:, :, 0:2], chT[:, j, :].rearrange("p (b r) -> p b r", r=2), reads=["chT"], writes=["gs6"])
                    self.I(POOL, "tensor_copy", gs6[:, :, 2:6], gss, reads=[kg], writes=["gs6"])
                    cvs = cv[:, 1024:NA].rearrange("p (b t) -> p b t", t=4)
                    self.I(ACT, "activation", cvs, gs6[:, :, 2:6], AF.Identity, bias=cwc(j, 3), scale=cwc(j, 2), reads=["gs6", "pv"], writes=["cv"])
                    self.I(DVE, "scalar_tensor_tensor", cvs, gs6[:, :, 1:5], cwc(j, 1), cvs, ALU.mult, ALU.add, reads=["gs6", "cv", "pv"], writes=["cv"])
                    self.I(DVE, "scalar_tensor_tensor", cvs, gs6[:, :, 0:4], cwc(j, 0), cvs, ALU.mult, ALU.add, reads=["gs6", "cv", "pv"], writes=["cv"])
                    self.I(POOL, "tensor_copy", pco[:, j, :], g_[:, 1054:1056], reads=[kg], writes=["pco"])
                    self.I(POOL, "tensor_copy", sco[:, j, :, :], gss[:, :, 2:4], reads=[kg], writes=["sco"])
                    self.I(ACT, "activation", ge[:, :], cv[:, :], AF.Gelu_apprx_tanh, reads=["cv"], writes=["ge"])
                    self.I(DVE, "tensor_tensor", actT[:, j, 0:1024], ge[:, 0:1024], u_[:, 32:1056], ALU.mult, reads=["ge", ku], writes=["actT"])
                    self.I(POOL, "tensor_tensor", actT[:, j, 1024:NA], ge[:, 1024:NA], u_[:, 1056:NOWN], ALU.mult, reads=["ge", ku], writes=["actT"])
                scr = self.stage_out("pco", pco[:], [128, NJ, 2], "pco")
                for r in range(2):
                    self.bg(self.o_pconv[r].rearrange("(j p) -> p j", p=128), scr[:, :, r], "pco")
                scrow = sb1("scrow", [32, 1408])
                for pc in range(4):
                    for j4 in range(0, 11, 4):
                        nj = min(4, 11 - j4)
                        ps, pk = self.psum()
                        for jj in range(nj):
                            j = pc * 11 + j4 + jj
                            self.I(PE, "transpose", ps[0:32, jj * 128:(jj + 1) * 128], sco[:, j, :, :].rearrange("p b r -> p (b r)"), cf[:, C_ID:C_ID + 128],
                                   reads=["sco", "cf"], writes=[pk], track=(jj == nj - 1))
                        self.I(DVE, "tensor_copy", scrow[:, j4 * 128:(j4 + nj) * 128], ps[0:32, 0:nj * 128], reads=[pk], writes=["scrow"])
                    S.dma(SP, self.o_sconv.rearrange("b r c -> (b r) c")[:, pc * 1408:(pc + 1) * 1408], scrow[:], reads=["scrow"], ring="spo")
                S.barrier()
            with ExitStack() as es2:
                sb2 = lambda n, s, d=F32: self.sb(es2, n, s, d)
                dummy = sb2("dmy2", [128, 2])
                self.I(POOL, "memset", dummy[:], 0.0, writes=["h2T", "dmy2"])
                bxf = self.bufX[:].rearrange("p a b -> p (a b)")
                wo = [bxf[:, i * 5632:(i + 1) * 5632].rearrange("p (k n) -> p k n", n=512) for i in range(2)]
                fo = sb2("fo", [128, 5, D])
                gp2 = sb2("gp2", [128, D])
                S.dma(SP, gp2[:], self.gpost[:, 1, :], writes=["gp2"])
                x1t = sb2("x1r", [128, D])
                st = sb2("stc", [128, 5, 8])
                xb = sb2("xbc", [128, 512], BF16)
                tiles = [(128 * i, 128) for i in range(8)] + [(1024, NSM)]
                sets = [tiles[0:5], tiles[5:9]]
                wn = 0
                for tset in sets:
                    self.I(POOL, "memset", st[:], 0.0, writes=["stc"])
                    for n in range(4):
                        banks = [self.psum() for _ in tset]
                        for kp in range(4):
                            slot = wn % 2
                            wn += 1
                            S.dma(POOL, wo[slot], self.wfo[kp * 1408:(kp + 1) * 1408, n * 512:(n + 1) * 512].rearrange("(k p) n -> p k n", p=128),
                                  writes=["wo%d" % slot])
                            for ti, (c0, nt) in enumerate(tset):
                                ps, pk = banks[ti]
                                for jj in range(11):
                                    j = kp * 11 + jj
                                    self.I(PE, "matmul", ps[0:nt, :], actT[:, j, c0:c0 + nt], wo[slot][:, jj, :], start=(j == 0), stop=(j == NJ - 1),
                                           reads=["actT", "wo%d" % slot], writes=[pk], track=(jj == 10))
                        for ti, (c0, nt) in enumerate(tset):
                            ps, pk = banks[ti]
                            self.I(DVE, "tensor_copy", fo[0:nt, ti, n * 512:(n + 1) * 512], ps[0:nt, :], reads=[pk], writes=["fo%d" % ti])
                            self.I(ACT, "activation", xb[0:nt, :], fo[0:nt, ti, n * 512:(n + 1) * 512], AF.Square, accum_out=st[0:nt, ti, n:n + 1], reads=["fo%d" % ti], writes=["xbc", "stc"])
                    for ti, (c0, nt) in enumerate(tset):
                        self.I(DVE, "tensor_reduce", st[0:nt, ti, 4:5], st[0:nt, ti, 0:4], AX.X, ALU.add, reads=["stc"], writes=["stc"])
                        self.I(DVE, "tensor_scalar", st[0:nt, ti, 5:6], st[0:nt, ti, 4:5], 1.0 / D, 1e-6, ALU.mult, ALU.add, reads=["stc"], writes=["stc"])
                        self.I(ACT, "activation", st[0:nt, ti, 6:7], st[0:nt, ti, 5:6], AF.Sqrt, reads=["stc"], writes=["stc"])
                        self.I(DVE, "reciprocal", st[0:nt, ti, 7:8], st[0:nt, ti, 6:7], reads=["stc"], writes=["stc"])
                        xc0 = 32 + c0
                        S.dma(SP, x1t[0:nt, :], self.x1s[xc0:xc0 + nt, :], reads=["x1s"], writes=["x1r"])
                        self.I(DVE, "scalar_tensor_tensor", fo[0:nt, ti, :], fo[0:nt, ti, :], st[0:nt, ti, 7:8], gp2[0:nt, :], ALU.mult, ALU.mult,
                               reads=["fo%d" % ti, "stc", "gp2"], writes=["fo%d" % ti])
                        self.I(POOL, "tensor_tensor", fo[0:nt, ti, :], fo[0:nt, ti, :], x1t[0:nt, :], ALU.add, reads=["fo%d" % ti, "x1r"], writes=["fo%d" % ti])
                        S.dma(SP, self.o_y[c0:c0 + nt, :], fo[0:nt, ti, :], reads=["fo%d" % ti], ring="spo")

    def make_hT(self, xrows, ntok, hT, hkey, slot):
        S = self.S
        xt, xb, st = self.xt[slot], self.xb[slot], self.xst[slot]
        kx, kb, ks = "xt%d" % slot, "xb%d" % slot, "xst%d" % slot
        S.dma(SP, xt[0:ntok, :], xrows, writes=[kx])
        self.I(POOL, "memset", st[:, 0:1], 0.0, writes=[ks])
        self.I(ACT, "activation", xb[0:ntok, :], xt[0:ntok, :], AF.Square, accum_out=st[0:ntok, 0:1], reads=[kx], writes=[kb, ks])
        self.I(DVE, "tensor_scalar", st[0:ntok, 1:2], st[0:ntok, 0:1], 1.0 / D, 1e-6, ALU.mult, ALU.add, reads=[ks], writes=[ks])
        self.I(ACT, "activation", st[0:ntok, 2:3], st[0:ntok, 1:2], AF.Sqrt, reads=[ks], writes=[ks])
        self.I(DVE, "reciprocal", st[0:ntok, 3:4], st[0:ntok, 2:3], reads=[ks], writes=[ks])
        self.I(ACT, "activation", xb[0:ntok, :], xt[0:ntok, :], AF.Copy, scale=st[0:ntok, 3:4], reads=[kx, ks, kb], writes=[kb])
        for half in range(2):
            ps, pk = self.psum()
            pT = ps[:].bitcast(BF16).rearrange("p (a b) -> p a b", b=128)
            for k8 in range(8):
                kc = half * 8 + k8
                self.I(PE, "transpose", pT[:, k8, 0:ntok], xb[0:ntok, kc * 128:(kc + 1) * 128],
                                                                self.cb[0:ntok, 0:ntok], reads=[kb, "cb"], writes=[pk], track=(k8 == 7))
            gcol = self.pv[:, V_G1 + half * 8: V_G1 + half * 8 + 8].unsqueeze(2).to_broadcast([128, 8, ntok])
            self.I(DVE, "tensor_tensor", hT[:, half * 8:half * 8 + 8, :], pT[:, :, 0:ntok], gcol, ALU.mult, reads=[pk, "pv"], writes=[hkey])

    def slab_stream(self, slabs):
        S = self.S
        n = len(slabs)

        def issue(i):
            ap, ncol = slabs[i]
            slot = self.wslot % 2
            self.wslot += 1
            key = "wb%d" % slot
            S.dma(POOL, self.wb[slot][:, :, 0:ncol], ap.rearrange("(k p) n -> p k n", p=128), writes=[key])
            return (self.wb[slot], key)
        cur = issue(0)
        for i in range(n):
            nxt = issue(i + 1) if i + 1 < n else None
            yield cur
            cur = nxt

    def proj(self, ps, pk, wt, wkey, c0, M, hT, hkey, N):
        S = self.S
        for k in range(16):
            self.I(PE, "matmul", ps[0:M, 0:N], wt[:, k, c0:c0 + M], hT[:, k, 0:N], start=(k == 0), stop=(k == 15), reads=[wkey, hkey], writes=[pk], track=(k == 15))

    def stageS(self):
        S = self.S
        pv, cf, cb = self.pv, self.cf, self.cb
        wA = self.wA_ap
        N = NSM
        with ExitStack() as es0:
            sb0 = lambda n, s, d=F32: self.sb(es0, n, s, d)
            sbon = sb0("sbon", [128, 8, N], BF16)
            sgst = sb0("sgst", [128, 8, N], BF16)
            self._stageS_proj(es0, sbon, sgst)
            self._stageS_scan(es0, sbon, sgst)

    def _stageS_proj(self, es0, sbon, sgst):
        S = self.S
        pv, cf, cb = self.pv, self.cf, self.cb
        wA = self.wA_ap
        N = NSM
        with ExitStack() as es:
            sb = lambda n, s, d=F32: self.sb(es, n, s, d)
            self.xt = [sb("xtS", [128, D])]
            self.xb = [sb("xbS", [128, D], BF16)]
            self.xst = [sb("xstS", [128, 4])]
            hT = sb("hTs", [128, 16, N], BF16)
            self.make_hT(self.xsamp, N, hT[:, :, :], "hTs", 0)
            S.dma(SP, self.scrH[:, :, 1056:NOWN], hT[:, :, :], reads=["hTs"], writes=["scrH_s"])
            shT = sb("shT", [128, 28, 16])
            sst = sb("sst", [16, DS])
            S.dma(SP, sst[:], self.st_shift, writes=["sst"])
            chunks = [(0, 64), (64, 64), (128, 128), (256, 32)] + [(288 + 128 * i, 128) for i in range(24)]
            ps, pk = self.psum()
            for ci, (o, M) in enumerate(chunks):
                self.I(PE, "transpose", ps[0:M, ci * 16:(ci + 1) * 16], sst[:, o:o + M], cf[0:16, 0:16], reads=["sst", "cf"], writes=[pk], track=(ci == 27))
            self.I(DVE, "tensor_copy", shT[:, :, :], ps[:, 0:448].rearrange("p (a b) -> p a b", b=16), reads=[pk], writes=["shT"])
            zls = sb("zls", [128, 28, 16])
            tw = sb("stw", [64, N], BF16)
            zam = sb("szam", [64, N], BF16)
            sga = sb("ssga", [128, N], BF16)
            sgb = sb("ssgb", [128, N], BF16)
            self.I(POOL, "memset", sgb[:], 0.0, writes=["ssgb"])
            tokS = sb("tokS", [N, 6, 1024])
            phys = {n: sb("s_" + n, [128, N]) for n in ("zc", "d", "zr", "zk", "zv", "es", "wd", "as", "kk", "kkn", "km", "dd", "av")}
            phys["hi"] = sb("s_hi", [128, N], BF16)
            phys["lo"] = sb("s_lo", [128, N], BF16)
            T = _TAlias(phys, {"kk2": "d", "rn": "zc", "t1": "dd", "ta": "es", "rk": "wd2"}, prefix="s_")
            phys["wd2"] = sb("s_wd2", [128, N])

            def mix_s(ps, pk, M, ci, out, kout, mu):
                zc, d = T["zc"], T["d"]
                v3 = lambda t: t[0:M, 0:N].rearrange("p (b t) -> p b t", t=4)
                self.I(ACT, "copy", zc[0:M, 0:N], ps[0:M, 0:N], reads=[pk], writes=["s_zc"])
                self.I(DVE, "tensor_tensor", v3(d)[:, :, 1:4], v3(zc)[:, :, 0:3], v3(zc)[:, :, 1:4], ALU.subtract, reads=["s_zc"], writes=["s_d"])
                self.I(DVE, "tensor_tensor", v3(d)[:, :, 0], shT[0:M, ci, :], v3(zc)[:, :, 0], ALU.subtract, reads=["s_zc", "shT"], writes=["s_d"])
                self.I(DVE, "scalar_tensor_tensor", out[0:M, 0:N], d[0:M, 0:N], mu, zc[0:M, 0:N], ALU.mult, ALU.add, reads=["s_d", "s_zc", "pv"], writes=[kout])
                self.I(POOL, "tensor_copy", zls[0:M, ci, :], v3(zc)[:, :, 3], reads=["s_zc"], writes=["zls"])

            slabs = [(wA[:, 0:288], 288)] + [(wA[:, 288 + q * 384: 288 + (q + 1) * 384], 384) for q in range(8)]
            stream = self.slab_stream(slabs)
            wt, wk = next(stream)
            for (ci, (cc0, M, dst, dk, func)) in enumerate(((0, 64, tw, "stw", AF.Tanh), (64, 64, zam, "szam", AF.Copy),
                                                             (128, 128, sga, "ssga", AF.Sigmoid), (256, 32, sgb, "ssgb", AF.Sigmoid))):
                ps, pk = self.psum()
                self.proj(ps, pk, wt, wk, cc0, M, hT, "hTs", N)
                mix_s(ps, pk, M, ci, T["zr"], "s_zr", pv[0:M, ci:ci + 1])
                self.I(ACT, "activation", dst[0:M, 0:N], T["zr"][0:M, 0:N], func, reads=["s_zr"], writes=[dk])
            bonesb = cb[:, C_BONES:C_BONES + 128]
            for q in range(8):
                wt, wk = next(stream)
                col = lambda base: pv[:, base + q: base + q + 1]
                qs = slice(q * 128, (q + 1) * 128)
                for t, nm in enumerate(("zr", "zk", "zv")):
                    ps, pk = self.psum()
                    self.proj(ps, pk, wt, wk, t * 128, 128, hT, "hTs", N)
                    mix_s(ps, pk, 128, 4 + 3 * q + t, T[nm], "s_" + nm, pv[:, V_MU + 3 * q + t: V_MU + 3 * q + t + 1])
                zr, zk, zv = T["zr"], T["zk"], T["zv"]
                ps, pk = self.psum()
                self.I(PE, "matmul", ps[:, 0:N], self.w2b[:, qs], tw[:, 0:N], start=True, stop=True, reads=["w2b", "stw"], writes=[pk])
                self.I(ACT, "activation", T["es"][:, 0:N], ps[:, 0:N], AF.Sigmoid, bias=col(V_W0), reads=[pk, "pv"], writes=["s_es"])
                self.I(ACT, "activation", T["wd"][:, 0:N], T["es"][:, 0:N], AF.Exp, scale=-DECAY_C, reads=["s_es"], writes=["s_wd"])
                ps, pk = self.psum()
                self.I(PE, "matmul", ps[:, 0:N], self.a2b[:, qs], zam[:, 0:N], start=True, stop=True, reads=["a2b", "szam"], writes=[pk])
                self.I(ACT, "activation", T["as"][:, 0:N], ps[:, 0:N], AF.Sigmoid, bias=col(V_A0), reads=[pk, "pv"], writes=["s_as"])
                self.I(POOL, "tensor_scalar", T["kk"][:, 0:N], zk[:, 0:N], col(V_KK), None, ALU.mult, reads=["s_zk", "pv"], writes=["s_kk"])
                self.I(POOL, "tensor_tensor", T["kk2"][:, 0:N], T["kk"][:, 0:N], T["kk"][:, 0:N], ALU.mult, reads=["s_kk"], writes=["s_d"])
                ps, pk = self.psum()
                self.bsum(ps, pk, T, "kk2", N)
                self.I(ACT, "activation", T["rn"][:, 0:N], ps[:, 0:N], AF.Sqrt, reads=[pk], writes=["s_zc"])
                self.I(DVE, "tensor_scalar_max", T["rn"][:, 0:N], T["rn"][:, 0:N], 1e-12, reads=["s_zc"], writes=["s_zc"])
                self.I(DVE, "reciprocal", T["rn"][:, 0:N], T["rn"][:, 0:N], reads=["s_zc"], writes=["s_zc"])
                self.I(DVE, "tensor_tensor", T["kkn"][:, 0:N], T["kk"][:, 0:N], T["rn"][:, 0:N], ALU.mult, reads=["s_kk", "s_zc"], writes=["s_kkn"])
                self.I(POOL, "tensor_scalar", T["t1"][:, 0:N], T["as"][:, 0:N], -1.0, col(V_KA), ALU.add, ALU.mult, reads=["s_as", "pv"], writes=["s_dd"])
                self.I(POOL, "tensor_tensor", T["t1"][:, 0:N], T["t1"][:, 0:N], zk[:, 0:N], ALU.mult, reads=["s_dd", "s_zk"], writes=["s_dd"])
                self.I(POOL, "tensor_tensor", T["km"][:, 0:N], T["t1"][:, 0:N], zk[:, 0:N], ALU.add, reads=["s_dd", "s_zk"], writes=["s_km"])
                self.I(POOL, "tensor_tensor", T["ta"][:, 0:N], T["kkn"][:, 0:N], T["as"][:, 0:N], ALU.mult, reads=["s_kkn", "s_as"], writes=["s_es"])
                self.I(POOL, "tensor_scalar", T["av"][:, 0:N], T["kkn"][:, 0:N], -1.0, None, ALU.mult, reads=["s_kkn"], writes=["s_av"])
                self.I(DVE, "scalar_tensor_tensor", T["rk"][:, 0:N], zr[:, 0:N], col(V_RK), T["km"][:, 0:N], ALU.mult, ALU.mult, reads=["s_zr", "s_km", "pv"], writes=["s_wd2"])
                ps, pk = self.psum()
                self.bsum(ps, pk, T, "rk", N)
                self.I(DVE, "tensor_tensor", sbon[:, q, :], ps[:, 0:N], zv[:, 0:N], ALU.mult, reads=[pk, "s_zv"], writes=["sbon"])
                self.I(POOL, "tensor_scalar", sbon[:, q, :], sbon[:, q, :], col(V_LB), None, ALU.add, reads=["sbon", "pv"], writes=["sbon"])
                ps, pk = self.psum()
                self.I(PE, "matmul", ps[:, 0:N], self.g2a[:, qs], sga[:, 0:N], start=True, stop=False, reads=["g2a", "ssga"], writes=[pk], track=False)
                self.I(PE, "matmul", ps[:, 0:N], self.g2b[:, qs], sgb[:, 0:N], start=False, stop=True, reads=["g2b", "ssgb"], writes=[pk])
                self.I(ACT, "copy", sgst[:, q, :], ps[:, 0:N], reads=[pk], writes=["sgst"])
                srcs = [("zr", "s_zr"), ("km", "s_km"), ("zv", "s_zv"), ("wd", "s_wd"), ("av", "s_av"), ("ta", "s_es")]
                psA, pkA = self.psum()
                psB, pkB = self.psum()
                for qi, (nm, kk_) in enumerate(srcs):
                    pp, ppk = (psA, pkA) if qi < 4 else (psB, pkB)
                    o = (qi % 4) * 128
                    self.I(PE, "transpose", pp[0:N, o:o + 128], T[nm][:, 0:N], cf[:, C_ID:C_ID + 128], reads=[kk_, "cf"], writes=[ppk], track=(qi in (3, 5)))
                self.I(DVE, "tensor_copy", tokS[:, 0:4, qs], psA[0:N, :].rearrange("p (a b) -> p a b", b=128), reads=[pkA], writes=["tokS"])
                self.I(ACT, "copy", tokS[:, 4:6, qs], psB[0:N, 0:256].rearrange("p (a b) -> p a b", b=128), reads=[pkB], writes=["tokS"])
            zrow = sb("zrow", [16, DS])
            for c0_ in range(0, 28, 4):
                ps, pk = self.psum()
                grp = list(enumerate(chunks))[c0_:c0_ + 4]
                for n_, (ci, (o, M)) in enumerate(grp):
                    self.I(PE, "transpose", ps[0:16, n_ * 128:n_ * 128 + M], zls[0:M, ci, :], cf[0:M, 0:M], reads=["zls", "cf"], writes=[pk], track=(n_ == len(grp) - 1))
                for n_, (ci, (o, M)) in enumerate(grp):
                    self.I(DVE, "tensor_copy", zrow[:, o:o + M], ps[0:16, n_ * 128:n_ * 128 + M], reads=[pk], writes=["zrow"])
            S.dma(SP, self.o_sshift, zrow[:], reads=["zrow"], ring="spo")
            S.dma(SP, self.scrS, tokS[:], reads=["tokS"], writes=["scrS"])
            S.barrier()

    def _stageS_scan(self, es0, sbon, sgst):
        S = self.S
        pv, cf, cb = self.pv, self.cf, self.cb
        N = NSM
        with ExitStack() as es:
            sb = lambda n, s, d=F32: self.sb(es, n, s, d)
            ytok = sb("ytok", [N, 1024])
            for bt in range(2):
                E = DVE
                kS, kq, kt, ky = "Sst%d" % bt, "qin%d" % bt, "stmp%d" % bt, "ysb%d" % bt
                Sst = sb(kS, [128, 64, 64])
                tmp = sb(kt, [128, 64, 64])
                qin = sb(kq, [128, 4, 6, 64])
                ysb = sb(ky, [128, 4, 64])
                sa = sb("sa%d" % bt, [128, 64])
                yq = sb("syq%d" % bt, [128, 4, 64])
                sst_ = sb("sstat%d" % bt, [128, 8, 4])
                S.dma(SP, Sst[:].rearrange("p v k -> p (v k)"), self.st_wkv[bt * 128:(bt + 1) * 128, :], writes=[kS])
                for bl in range(8):
                    b = bt * 8 + bl
                    S.dma(SP, qin[16 * bl:16 * bl + 16, :, :, :], self.scrS[4 * b:4 * b + 4, :, :].rearrange("t q (h k) -> h t q k", k=64),
                          reads=["scrS"], writes=[kq + "_%d" % bl])
                bc_k = lambda t, qi: qin[:, t, qi, :].unsqueeze(1).to_broadcast([128, 64, 64])
                kqs = [kq + "_%d" % bl for bl in range(8)]
                for t in range(4):
                    self.I(E, "tensor_tensor", tmp[:], Sst[:], bc_k(t, 4), ALU.mult, reads=[kS] + kqs, writes=[kt])
                    self.I(DVE, "tensor_reduce", sa[:], tmp[:], AX.X, ALU.add, reads=[kt], writes=["sa%d" % bt])
                    self.I(E, "tensor_tensor", Sst[:], Sst[:], bc_k(t, 3), ALU.mult, reads=[kS] + kqs, writes=[kS])
                    self.I(E, "tensor_tensor", tmp[:], sa[:].unsqueeze(2).to_broadcast([128, 64, 64]), bc_k(t, 5), ALU.mult, reads=["sa%d" % bt] + kqs, writes=[kt])
                    self.I(E, "tensor_tensor", Sst[:], Sst[:], tmp[:], ALU.add, reads=[kS, kt], writes=[kS])
                    self.I(E, "tensor_tensor", tmp[:], qin[:, t, 2, :].unsqueeze(2).to_broadcast([128, 64, 64]), bc_k(t, 1), ALU.mult, reads=kqs, writes=[kt])
                    self.I(E, "tensor_tensor", Sst[:], Sst[:], tmp[:], ALU.add, reads=[kS, kt], writes=[kS])
                    self.I(E, "tensor_tensor", tmp[:], Sst[:], bc_k(t, 0), ALU.mult, reads=[kS] + kqs, writes=[kt])
                    self.I(DVE, "tensor_reduce", ysb[:, t, :], tmp[:], AX.X, ALU.add, reads=[kt], writes=[ky])
                S.dma(SP, self.o_swkv[bt * 128:(bt + 1) * 128, :], Sst[:].rearrange("p v k -> p (v k)"), reads=[kS], ring="spo")
                st_ = sst_
                ks_ = "sstat%d" % bt
                self.I(DVE, "tensor_reduce", st_[:, 0, :], ysb[:], AX.X, ALU.add, reads=[ky], writes=[ks_])
                self.I(E, "tensor_tensor", yq[:], ysb[:], ysb[:], ALU.mult, reads=[ky], writes=["syq%d" % bt])
                self.I(DVE, "tensor_reduce", st_[:, 1, :], yq[:], AX.X, ALU.add, reads=["syq%d" % bt], writes=[ks_])
                self.I(E, "tensor_scalar", st_[:, 2, :], st_[:, 0, :], 1.0 / 64, None, ALU.mult, reads=[ks_], writes=[ks_])
                self.I(E, "tensor_tensor", st_[:, 3, :], st_[:, 2, :], st_[:, 2, :], ALU.mult, reads=[ks_], writes=[ks_])
                self.I(E, "tensor_scalar", st_[:, 4, :], st_[:, 1, :], 1.0 / 64, 64e-5, ALU.mult, ALU.add, reads=[ks_], writes=[ks_])
                self.I(E, "tensor_tensor", st_[:, 4, :], st_[:, 4, :], st_[:, 3, :], ALU.subtract, reads=[ks_], writes=[ks_])
                self.I(ACT, "activation", st_[:, 5, :], st_[:, 4, :], AF.Sqrt, reads=[ks_], writes=[ks_])
                self.I(DVE, "reciprocal", st_[:, 6, :], st_[:, 5, :], reads=[ks_], writes=[ks_])
                self.I(E, "tensor_tensor", yq[:], ysb[:], st_[:, 2, :].unsqueeze(2).to_broadcast([128, 4, 64]), ALU.subtract, reads=[ky, ks_], writes=["syq%d" % bt])
                self.I(E, "tensor_tensor", yq[:], yq[:], st_[:, 6, :].unsqueeze(2).to_broadcast([128, 4, 64]), ALU.mult, reads=["syq%d" % bt, ks_], writes=["syq%d" % bt])
                for bl in range(8):
                    b = bt * 8 + bl
                    S.dma(SP, self.scrY[4 * b:4 * b + 4, :].rearrange("t (h v) -> h t v", v=64), yq[16 * bl:16 * bl + 16, :, :], reads=["syq%d" % bt], writes=["scrY%d" % b])
            S.dma(SP, ytok[:], self.scrY, reads=["scrY%d" % b for b in range(16)], writes=["ytok"])
            if "sdbg" in self.dbg:
                S.dma(SP, self.dout("d_ytok", [N, 1024]), ytok[:], reads=["ytok"], ring="spo")
                dbf = sb("dbf", [128, 2, 8, N])
                self.I(DVE, "tensor_copy", dbf[:, 0], sbon[:], reads=["sbon"], writes=["dbf"])
                self.I(DVE, "tensor_copy", dbf[:, 1], sgst[:], reads=["sgst"], writes=["dbf"])
                S.dma(SP, self.dout("d_sbg", [128, 2, 8, N]), dbf[:], reads=["dbf"], ring="spo")
            ytb = sb("ytb", [N, 1024], BF16)
            self.I(ACT, "copy", ytb[:], ytok[:], reads=["ytok"], writes=["ytb"])
            ps, pk = self.psum()
            pT = ps[:].bitcast(BF16).rearrange("p (a b) -> p a b", b=128)[:, :, 0:N]
            for q in range(8):
                self.I(PE, "transpose", pT[:, q, :], ytb[:, q * 128:(q + 1) * 128], cb[0:N, 0:N], reads=["ytb", "cb"], writes=[pk], track=(q == 7))
            yt = sb("syt", [128, 8, N])
            lw = pv[:, V_LW: V_LW + 8].unsqueeze(2).to_broadcast([128, 8, N])
            self.I(DVE, "tensor_tensor", yt[:], pT, lw, ALU.mult, reads=[pk, "pv"], writes=["syt"])
            self.I(POOL, "tensor_tensor", yt[:], yt[:], sbon[:], ALU.add, reads=["syt", "sbon"], writes=["syt"])
            self.I(DVE, "tensor_tensor", self.yaT[:, :, 1056:NOWN], yt[:], sgst[:], ALU.mult, reads=["syt", "sgst"], writes=["yaT"])
            S.barrier()

    def stageA(self, es0, xseq, wA):
        S = self.S
        pv, cf, cb = self.pv, self.cf, self.cb
        with ExitStack() as es:
            sb = lambda n, s, d=F32: self.sb(es, n, s, d)
            self.xt = [sb("xt%d" % i, [128, D]) for i in range(1)]
            self.xb = [sb("xb%d" % i, [128, D], BF16) for i in range(1)]
            self.xst = [sb("xst%d" % i, [128, 4]) for i in range(1)]
            hT = sb("hTb", [128, 16, BLK], BF16)
            zl = sb("zl", [128, 28])
            self.I(POOL, "memset", zl[:], 0.0, writes=["zl"])
            tw = sb("tw", [64, BLK], BF16)
            zam = sb("zam", [64, BLK], BF16)
            sga = sb("sga", [128, BLK], BF16)
            sgb = sb("sgb", [128, BLK], BF16)
            self.I(POOL, "memset", sgb[:], 0.0, writes=["sgb"])
            NCK = BLK // CH
            ARBD = sb("ARBD", [128, 4, NCK, 2, 128], BF16)
            BBD = sb("BBD", [128, 4, NCK, 128], BF16)
            KBD = sb("KBD", [128, 4, NCK, 128], BF16)
            VBD = sb("VBD", [128, 4, NCK, 128], BF16)
            for (t, k) in ((ARBD, "ARBD"), (BBD, "BBD"), (KBD, "KBD"), (VBD, "VBD")):
                self.I(POOL, "memset", t[:], 0.0, writes=[k + "%d" % q for q in range(4)])
            PCs = sb("PCs", [128, 4, NCK])
            bon = sb("bon", [128, 4, BLK], BF16)
            gst = sb("gst", [128, 4, BLK], BF16)
            Hf = [sb("Hf%d" % g, [128, 4, 64]) for g in range(2)]
            Hb = [sb("Hb%d" % g, [128, 4, 64], BF16) for g in range(2)]
            for g in range(2):
                self.I(POOL, "memset", Hf[g][:], 0.0, writes=["Hf%d" % g])
                self.I(POOL, "memset", Hb[g][:], 0.0, writes=["Hb%d" % g])
            names = ("zc", "d", "zr", "zk", "zv", "es", "cum", "dd", "pinv", "prr", "pa", "as", "kk", "kkn", "km")
            phys = {n: sb("t_" + n, [128, NH]) for n in names[:8]}
            spare = self.bufX[:, 8:16, :].rearrange("p a b -> p (a b)")
            sparef = spare.bitcast(F32)
            for i, n in enumerate(names[8:]):
                phys[n] = sparef[:, i * NH:(i + 1) * NH]
            phys["hi"] = spare[:, 7 * 2 * NH: 7 * 2 * NH + NH]
            phys["lo"] = spare[:, 7 * 2 * NH + NH: 7 * 2 * NH + 2 * NH]
            T = _TAlias(phys, {"kk2": "d", "rn": "zc", "t1": "dd", "ta": "es", "rk": "cum"})
            sc = {}
            for s in range(2):
                sc["NA1", s] = sb("NA1_%d" % s, [128, 4, 256], BF16)
                sc["NA2", s] = sb("NA2_%d" % s, [128, 4, 256], BF16)
                for i in range(2):
                    sc["Q", s, i] = sb("Q_%d%d" % (s, i), [128, 4, 128], BF16)
                    if s == 0:
                        sc["NL", s, i] = sb("NL_%d%d" % (s, i), [128, 4, 128], BF16)
                        sc["TU", s, i] = sb("TU_%d%d" % (s, i), [128, 4, 128], BF16)
                sc["B2", s] = sb("B2_%d" % s, [128, 4, 128], BF16)
                sc["K2", s] = sb("K2_%d" % s, [128, 4, 128], BF16)
                sc["V2", s] = sb("V2_%d" % s, [128, 4, 64], BF16)
                if s == 0:
                    sc["X2", s] = sb("X2_%d" % s, [128, 4, 64], BF16)
                    sc["U2", s] = sb("U2_%d" % s, [128, 4, 64], BF16)
                    sc["YBD", s] = sb("YBD_%d" % s, [128, 4, 128], BF16)
                    self.I(POOL, "memset", sc["YBD", s][:], 0.0, writes=["YBD_%d" % s])
                    sc["ys", s] = sb("ys_%d" % s, [128, 4, 64])
                    sc["yq", s] = sb("yq_%d" % s, [128, 4, 64])
                    sc["st", s] = sb("yst_%d" % s, [128, 8, 4])
                    sc["yt", s] = sb("yt_%d" % s, [128, 4, 64])
                else:
                    for nm in ("X2", "U2", "YBD", "ys", "yq", "st", "yt"):
                        sc[nm, s] = sc[nm, 0]
            tmpH = sb("tmpH", [128, 4, 64])

            nblk = NPR // BLK
            lr_slab = (wA[:, 0:288], 288)
            pair_slabs = [(wA[:, 288 + q * 384: 288 + (q + 1) * 384], 384) for q in range(8)]
            slabs = []
            for blk in range(nblk):
                slabs.append(lr_slab)
                slabs += pair_slabs
            stream = self.slab_stream(slabs)

            for blk in range(nblk):
                c0 = blk * BLK
                N = BLK
                for t4 in range(4):
                    self.make_hT(xseq[c0 + t4 * 128: c0 + (t4 + 1) * 128, :], 128, hT[:, :, t4 * 128:(t4 + 1) * 128], "hTb", 0)
                self.ckpt("hT")
                if blk == 1:
                    S.dma(SP, self.scrH[:, :, 0:32], hT[:, :, BLK - 32:BLK], reads=["hTb"], writes=["scrH_1"])
                elif blk >= 2:
                    S.dma(SP, self.scrH[:, :, 32 + (blk - 2) * BLK: 32 + (blk - 1) * BLK], hT[:, :, :], reads=["hTb"], writes=["scrH_%d" % blk])
                wt, wk = next(stream)
                for (ci, (cc0, M, dst, dk, func)) in enumerate(((0, 64, tw, "tw", AF.Tanh), (64, 64, zam, "zam", AF.Copy),
                                                                 (128, 128, sga, "sga", AF.Sigmoid), (256, 32, sgb, "sgb", AF.Sigmoid))):
                    ps, pk = self.psum()
                    self.proj(ps, pk, wt, wk, cc0, M, hT, "hTb", N)
                    for co in range(0, N, NH):
                        self.mix(ps[:, co:co + NH], pk, M, NH, ci, T, zl)
                        self.I(ACT, "activation", dst[0:M, co:co + NH], T["zr"][0:M, 0:NH], func, reads=["t_zr"], writes=[dk])
                self.ckpt("lowrank")
                for g in range(2):
                    for qg in range(4):
                        q = g * 4 + qg
                        wt, wk = next(stream)
                        self.prep_pair(wt, wk, hT, N, q, qg, T, zl, tw, zam, sga, sgb, ARBD, BBD, KBD, VBD, PCs, bon, gst, blk)
                        self.ckpt("prep0")
                    self.ckpt("prep")
                    self.scan_group(g, blk, NCK, sc, ARBD, BBD, KBD, VBD, PCs, bon, gst, Hf[g], Hb[g], tmpH)
            scr = self.stage_out("zl", zl[:], [128, 28], "zl")
            self.bg(self.o_pshift[0:64].rearrange("(p o) -> p o", o=1), scr[0:64, 0:1], "zl")
            self.bg(self.o_pshift[64:128].rearrange("(p o) -> p o", o=1), scr[0:64, 1:2], "zl")
            self.bg(self.o_pshift[128:256].rearrange("(p o) -> p o", o=1), scr[:, 2:3], "zl")
            self.bg(self.o_pshift[256:288].rearrange("(p o) -> p o", o=1), scr[0:32, 3:4], "zl")
            self.bg(self.o_pshift[288:DS].rearrange("(j p) -> p j", p=128), scr[:, 4:28], "zl")
            for g in range(2):
                hh = sb("hh%d" % g, [128, 4, 64], BF16)
                hl = sb("hl%d" % g, [128, 4, 64], BF16)
                self.I(POOL, "tensor_copy", hh[:], Hf[g][:], reads=["Hf%d" % g], writes=["hh%d" % g])
                self.I(POOL, "tensor_tensor", hl[:], Hf[g][:], hh[:], ALU.subtract, reads=["Hf%d" % g, "hh%d" % g], writes=["hl%d" % g])
                ps, pk = self.psum()
                for qg in range(4):
                    self.I(PE, "matmul", ps[0:64, qg * 128:(qg + 1) * 128], hh[:, qg, :], cb[:, 0:128], start=True, stop=False, reads=["hh%d" % g, "cb"], writes=[pk], track=False)
                    self.I(PE, "matmul", ps[0:64, qg * 128:(qg + 1) * 128], hl[:, qg, :], cb[:, 0:128], start=False, stop=True, reads=["hl%d" % g, "cb"], writes=[pk], track=(qg == 3))
                so = sb("so%d" % g, [64, 4, 2, 64])
                self.I(DVE, "tensor_copy", so[:], ps[0:64, :].rearrange("p (a b c) -> p a b c", a=4, b=2), reads=[pk], writes=["so%d" % g])
                S.dma(SP, self.o_pwkv[g * 8:(g + 1) * 8].rearrange("(q s) v k -> v q s k", s=2), so[:], reads=["so%d" % g], ring="spo")
            S.barrier()

    def mix(self, ps, pk, M, N, ci, T, zl):
        S = self.S
        zc, d, out = T["zc"], T["d"], T["zr"]
        self.mix_to(ps, pk, M, N, ci, zl, zc, "t_zc", d, "t_d", out, "t_zr", self.pv[0:M, ci:ci + 1] if ci < 4 else None)

    def mix_to(self, ps, pk, M, N, ci, zl, zc, kzc, d, kd, out, kout, mu):
        S = self.S
        self.I(ACT, "copy", zc[0:M, 0:N], ps[0:M, 0:N], reads=[pk], writes=[kzc])
        self.I(DVE, "tensor_tensor", d[0:M, 1:N], zc[0:M, 0:N - 1], zc[0:M, 1:N], ALU.subtract, reads=[kzc], writes=[kd])
        self.I(DVE, "tensor_tensor", d[0:M, 0:1], zl[0:M, ci:ci + 1], zc[0:M, 0:1], ALU.subtract, reads=[kzc, "zl"], writes=[kd])
        self.I(DVE, "scalar_tensor_tensor", out[0:M, 0:N], d[0:M, 0:N], mu, zc[0:M, 0:N], ALU.mult, ALU.add, reads=[kd, kzc, "pv"], writes=[kout])
        self.I(POOL, "tensor_copy", zl[0:M, ci:ci + 1], zc[0:M, N - 1:N], reads=[kzc], writes=["zl"])

    def bsum(self, ps, pk, T, name, N):
        S = self.S
        src, ksrc = T[name], T.key(name)
        bonesb = self.cb[:, C_BONES:C_BONES + 128]
        khi, klo = T.key("hi"), T.key("lo")
        self.I(ACT, "copy", T["hi"][:, 0:N], src[:, 0:N], reads=[ksrc], writes=[khi])
        self.I(POOL, "tensor_tensor", T["lo"][:, 0:N], src[:, 0:N], T["hi"][:, 0:N], ALU.subtract, reads=[ksrc, khi], writes=[klo])
        self.I(PE, "matmul", ps[:, 0:N], bonesb, T["hi"][:, 0:N], start=True, stop=False, reads=["cb", khi], writes=[pk], track=False)
        self.I(PE, "matmul", ps[:, 0:N], bonesb, T["lo"][:, 0:N], start=False, stop=True, reads=["cb", klo], writes=[pk])

    def prep_pair(self, wt, wk, hT, N, q, qg, T, zl, tw, zam, sga, sgb, ARBD, BBD, KBD, VBD, PCs, bon, gst, blk):
        S = self.S
        pv, cf = self.pv, self.cf
        col = lambda base: pv[:, base + q: base + q + 1]
        qs = slice(q * 128, (q + 1) * 128)
        pj = []
        for t in range(3):
            ps, pk = self.psum()
            self.proj(ps, pk, wt, wk, t * 128, 128, hT, "hTb", N)
            pj.append((ps, pk))
        W = min(N, NH)
        for co in range(0, N, W):
            cw = slice(co, co + W)
            for t, nm in enumerate(("zr", "zk", "zv")):
                ps, pk = pj[t]
                ci = 4 + 3 * q + t
                self.mix_to(ps[:, cw], pk, 128, W, ci, zl, T["zc"], "t_zc", T["d"], "t_d", T[nm], "t_" + nm, pv[:, V_MU + 3 * q + t: V_MU + 3 * q + t + 1])
            zr, zk, zv = T["zr"], T["zk"], T["zv"]
            ps, pk = self.psum()
            self.I(PE, "matmul", ps[:, 0:W], self.w2b[:, qs], tw[:, cw], start=True, stop=True, reads=["w2b", "tw"], writes=[pk])
            self.I(ACT, "activation", T["es"][:, 0:W], ps[:, 0:W], AF.Sigmoid, bias=col(V_W0), reads=[pk, "pv"], writes=["t_es"])
            self.I(DVE, "tensor_tensor_scan", T["cum"][:, 0:W], cf[:, C_M01:C_M01 + W], T["es"][:, 0:W], 0.0, ALU.mult, ALU.add, reads=["t_es", "cf"], writes=["t_cum"])
            self.I(POOL, "tensor_tensor", T["dd"][:, 0:W], T["cum"][:, 0:W], T["es"][:, 0:W], ALU.subtract, reads=["t_cum", "t_es"], writes=["t_dd"])
            self.I(ACT, "activation", T["pinv"][:, 0:W], T["cum"][:, 0:W], AF.Exp, scale=DECAY_C, reads=["t_cum"], writes=["t_pinv"])
            self.I(ACT, "activation", T["prr"][:, 0:W], T["cum"][:, 0:W], AF.Exp, scale=-DECAY_C, reads=["t_cum"], writes=["t_prr"])
            self.I(ACT, "activation", T["pa"][:, 0:W], T["dd"][:, 0:W], AF.Exp, scale=-DECAY_C, reads=["t_dd"], writes=["t_pa"])
            nck = W // CH
            ck0 = co // CH
            self.I(POOL, "tensor_copy", PCs[:, qg, ck0:ck0 + nck], T["prr"][:, 0:W].rearrange("p (c t) -> p c t", t=CH)[:, :, CH - 1], reads=["t_prr"], writes=["PCs%d" % qg])
            ps, pk = self.psum()
            self.I(PE, "matmul", ps[:, 0:W], self.a2b[:, qs], zam[:, cw], start=True, stop=True, reads=["a2b", "zam"], writes=[pk])
            self.I(ACT, "activation", T["as"][:, 0:W], ps[:, 0:W], AF.Sigmoid, bias=col(V_A0), reads=[pk, "pv"], writes=["t_as"])
            self.I(ACT, "activation", T["kk"][:, 0:W], zk[:, 0:W], AF.Copy, scale=col(V_KK), reads=["t_zk", "pv"], writes=["t_kk"])
            self.I(ACT, "activation", T["kk2"][:, 0:W], T["kk"][:, 0:W], AF.Square, reads=["t_kk"], writes=["t_d"])
            ps, pk = self.psum()
            self.bsum(ps, pk, T, "kk2", W)
            self.I(ACT, "activation", T["rn"][:, 0:W], ps[:, 0:W], AF.Sqrt, reads=[pk], writes=["t_zc"])
            self.I(DVE, "tensor_scalar_max", T["rn"][:, 0:W], T["rn"][:, 0:W], 1e-12, reads=["t_zc"], writes=["t_zc"])
            self.I(DVE, "reciprocal", T["rn"][:, 0:W], T["rn"][:, 0:W], reads=["t_zc"], writes=["t_zc"])
            self.I(DVE, "tensor_tensor", T["kkn"][:, 0:W], T["kk"][:, 0:W], T["rn"][:, 0:W], ALU.mult, reads=["t_kk", "t_zc"], writes=["t_kkn"])
            self.I(POOL, "tensor_scalar", T["t1"][:, 0:W], T["as"][:, 0:W], -1.0, col(V_KA), ALU.add, ALU.mult, reads=["t_as", "pv"], writes=["t_dd"])
            self.I(POOL, "tensor_tensor", T["t1"][:, 0:W], T["t1"][:, 0:W], zk[:, 0:W], ALU.mult, reads=["t_dd", "t_zk"], writes=["t_dd"])
            self.I(POOL, "tensor_tensor", T["km"][:, 0:W], T["t1"][:, 0:W], zk[:, 0:W], ALU.add, reads=["t_dd", "t_zk"], writes=["t_km"])
            self.I(POOL, "tensor_tensor", T["ta"][:, 0:W], T["kkn"][:, 0:W], T["as"][:, 0:W], ALU.mult, reads=["t_kkn", "t_as"], writes=["t_es"])
            self.I(DVE, "scalar_tensor_tensor", T["rk"][:, 0:W], zr[:, 0:W], col(V_RK), T["km"][:, 0:W], ALU.mult, ALU.mult, reads=["t_zr", "t_km", "pv"], writes=["t_cum"])
            ps, pk = self.psum()
            self.bsum(ps, pk, T, "rk", W)
            self.I(DVE, "tensor_tensor", bon[:, qg, cw], ps[:, 0:W], zv[:, 0:W], ALU.mult, reads=[pk, "t_zv"], writes=["bon%d" % qg])
            self.I(ACT, "activation", bon[:, qg, cw], bon[:, qg, cw], AF.Identity, bias=col(V_LB), reads=["bon%d" % qg, "pv"], writes=["bon%d" % qg])
            ps, pk = self.psum()
            self.I(PE, "matmul", ps[:, 0:W], self.g2a[:, qs], sga[:, cw], start=True, stop=False, reads=["g2a", "sga"], writes=[pk], track=False)
            self.I(PE, "matmul", ps[:, 0:W], self.g2b[:, qs], sgb[:, cw], start=False, stop=True, reads=["g2b", "sgb"], writes=[pk])
            self.I(ACT, "copy", gst[:, qg, cw], ps[:, 0:W], reads=[pk], writes=["gst%d" % qg])
            for h in range(2):
                pr = slice(64 * h, 64 * h + 64)
                cs = slice(64 * h, 64 * h + 64)
                v3 = lambda t: t[pr, 0:W].rearrange("p (c t) -> p c t", t=CH)
                e1 = DVE if h == 0 else POOL
                cks = slice(ck0, ck0 + nck)
                self.I(DVE, "scalar_tensor_tensor", ARBD[pr, qg, cks, 0, cs], v3(T["kkn"]), -1.0, v3(T["pa"]), ALU.mult, ALU.mult, reads=["t_kkn", "t_pa"], writes=["ARBD%d" % qg])
                self.I(e1, "tensor_tensor", ARBD[pr, qg, cks, 1, cs], v3(zr), v3(T["prr"]), ALU.mult, reads=["t_zr", "t_prr"], writes=["ARBD%d" % qg])
                self.I(e1, "tensor_tensor", KBD[pr, qg, cks, cs], v3(T["km"]), v3(T["pinv"]), ALU.mult, reads=["t_km", "t_pinv"], writes=["KBD%d" % qg])
                self.I(e1, "tensor_tensor", BBD[pr, qg, cks, cs], v3(T["ta"]), v3(T["pinv"]), ALU.mult, reads=["t_es", "t_pinv"], writes=["BBD%d" % qg])
                self.I(ACT, "copy", VBD[pr, qg, cks, cs], v3(zv), reads=["t_zv"], writes=["VBD%d" % qg])

    def scan_pre(self, g, c, gc, sc, ARBD, BBD, KBD, VBD):
        cf, cb = self.cf, self.cb
        s = gc % 2
        kin = ["ARBD%d" % q for q in range(4)] + ["BBD%d" % q for q in range(4)] + ["KBD%d" % q for q in range(4)] + ["VBD%d" % q for q in range(4)]
        NA1, NA2 = sc["NA1", s], sc["NA2", s]
        kNA1, kNA2 = "NA1_%d" % s, "NA2_%d" % s
        identb = cb[:, 0:128]
        iselb = cb[:, C_ISEL:C_ISEL + 64]
        mask2 = cf[:, C_MUS:C_MUS + 256].unsqueeze(1).to_broadcast([128, 2, 256])
        for (dst, kd, L) in ((NA1, kNA1, BBD), (NA2, kNA2, KBD)):
            for hf in range(2):
                ps, pk = self.psum()
                for j in range(2):
                    q = 2 * hf + j
                    self.I(PE, "matmul", ps[:, j * 256:(j + 1) * 256], L[:, q, c, :], ARBD[:, q, c, :, :].rearrange("p a b -> p (a b)"),
                           start=True, stop=True, reads=kin, writes=[pk], track=(j == 1))
                self.I(DVE, "tensor_tensor", dst[:, 2 * hf:2 * hf + 2, :], ps[:].rearrange("p (a b) -> p a b", b=256), mask2, ALU.mult, reads=[pk, "cf"], writes=[kd])
                yield
        NL0, kNL0 = sc["NL", 0, 0], "NL_00"
        ps, pk = self.psum()
        for q in range(4):
            self.I(PE, "matmul", ps[:, q * 128:(q + 1) * 128], ARBD[:, q, c, 0, :], BBD[:, q, c, :], start=True, stop=True, reads=kin, writes=[pk], track=(q == 3))
        mL = cf[:, C_MLS:C_MLS + 128].unsqueeze(1).to_broadcast([128, 4, 128])
        self.I(DVE, "tensor_tensor", NL0[:], ps[:].rearrange("p (a b) -> p a b", b=128), mL, ALU.mult, reads=[pk, "cf"], writes=[kNL0])
        yield
        B2, K2, V2 = sc["B2", s], sc["K2", s], sc["V2", s]
        for (dst, kd, L) in ((B2, "B2_%d" % s, BBD), (K2, "K2_%d" % s, KBD)):
            ps, pk = self.psum()
            for q in range(4):
                self.I(PE, "matmul", ps[:, q * 128:(q + 1) * 128], L[:, q, c, :], identb, start=True, stop=True, reads=kin + ["cb"], writes=[pk], track=(q == 3))
            self.I(ACT, "copy", dst[:], ps[:].rearrange("p (a b) -> p a b", b=128), reads=[pk], writes=[kd])
            yield
        ps, pk = self.psum()
        for q in range(4):
            self.I(PE, "matmul", ps[:, q * 64:(q + 1) * 64], VBD[:, q, c, :], iselb, start=True, stop=True, reads=kin + ["cb"], writes=[pk], track=(q == 3))
        self.I(ACT, "copy", V2[:], ps[:, 0:256].rearrange("p (a b) -> p a b", b=64), reads=[pk], writes=["V2_%d" % s])
        yield
        TUc, kTU = NA1[:, :, 0:128], kNA1
        NLc, kNL = NL0, kNL0
        Qc, kQ = sc["Q", s, 0], "Q_%d0" % s
        idb4 = identb.unsqueeze(1).to_broadcast([128, 4, 128])
        self.I(POOL, "tensor_tensor", Qc[:], NA1[:, :, 0:128], idb4, ALU.add, reads=[kNA1, "cb"], writes=[kQ])
        for lvl in range(1, 6):
            i = lvl % 2
            NLn, kNLn = sc["NL", 0, i], "NL_0%d" % i
            TUn, kTUn = sc["TU", 0, i], "TU_0%d" % i
            Qn, kQn = sc["Q", s, i], "Q_%d%d" % (s, i)
            psN, pkN = self.psum()
            for q in range(4):
                self.I(PE, "matmul", psN[:, q * 128:(q + 1) * 128], TUc[:, q, :], NLc[:, q, :], start=True, stop=True, reads=[kTU, kNL], writes=[pkN], track=(q == 3))
            if lvl < 5:
                psT, pkT = self.psum()
                for q in range(4):
                    self.I(PE, "matmul", psT[:, q * 128:(q + 1) * 128], NLc[:, q, :], TUc[:, q, :], start=True, stop=True, reads=[kTU, kNL], writes=[pkT], track=(q == 3))
            self.I(ACT, "copy", NLn[:], psN[:].rearrange("p (a b) -> p a b", b=128), reads=[pkN], writes=[kNLn])
            if lvl < 5:
                self.I(DVE, "tensor_copy", TUn[:], psT[:].rearrange("p (a b) -> p a b", b=128), reads=[pkT], writes=[kTUn])
            yield
            psQ, pkQ = self.psum()
            for q in range(4):
                self.I(PE, "matmul", psQ[:, q * 128:(q + 1) * 128], NLn[:, q, :], Qc[:, q, :], start=True, stop=True, reads=[kNLn, kQ], writes=[pkQ], track=(q == 3))
            self.I(DVE, "tensor_tensor", Qn[:], psQ[:].rearrange("p (a b) -> p a b", b=128), Qc[:], ALU.add, reads=[pkQ, kQ], writes=[kQn])
            yield
            TUc, kTU, NLc, kNL, Qc, kQ = TUn, kTUn, NLn, kNLn, Qn, kQn

    def scan_chain(self, g, c, gc, sc, ARBD, PCs, bon, gst, Hf, Hb, tmpH):
        s = gc % 2
        kin = ["ARBD%d" % q for q in range(4)]
        NA1, NA2 = sc["NA1", s], sc["NA2", s]
        kNA1, kNA2 = "NA1_%d" % s, "NA2_%d" % s
        B2, K2, V2 = sc["B2", s], sc["K2", s], sc["V2", s]
        kB2, kK2, kV2 = "B2_%d" % s, "K2_%d" % s, "V2_%d" % s
        Qc, kQ = sc["Q", s, 1], "Q_%d1" % s
        kH = "Hf%d" % g
        kHb = "Hb%d" % g
        X2, U2 = sc["X2", 0], sc["U2", 0]
        ps, pk = self.psum()
        for q in range(4):
            self.I(PE, "matmul", ps[:, q * 64:(q + 1) * 64], ARBD[:, q, c, 0, :], Hb[:, q, :], start=True, stop=False, reads=kin + [kHb], writes=[pk], track=False)
            self.I(PE, "matmul", ps[:, q * 64:(q + 1) * 64], NA2[:, q, 0:128], V2[:, q, :], start=False, stop=True, reads=[kNA2, kV2], writes=[pk], track=(q == 3))
        self.I(ACT, "copy", X2[:], ps[:, 0:256].rearrange("p (a b) -> p a b", b=64), reads=[pk], writes=["X2_0"])
        yield
        ps, pk = self.psum()
        for q in range(4):
            self.I(PE, "matmul", ps[:, q * 64:(q + 1) * 64], Qc[:, q, :], X2[:, q, :], start=True, stop=True, reads=[kQ, "X2_0"], writes=[pk], track=(q == 3))
        self.I(ACT, "copy", U2[:], ps[:, 0:256].rearrange("p (a b) -> p a b", b=64), reads=[pk], writes=["U2_0"])
        yield
        need_y = gc >= OWN0 // CH
        if need_y:
            psY, pkY = self.psum()
            for q in range(4):
                self.I(PE, "matmul", psY[:, q * 64:(q + 1) * 64], ARBD[:, q, c, 1, :], Hb[:, q, :], start=True, stop=False, reads=kin + [kHb], writes=[pkY], track=False)
                self.I(PE, "matmul", psY[:, q * 64:(q + 1) * 64], NA1[:, q, 128:256], U2[:, q, :], start=False, stop=False, reads=[kNA1, "U2_0"], writes=[pkY], track=False)
                self.I(PE, "matmul", psY[:, q * 64:(q + 1) * 64], NA2[:, q, 128:256], V2[:, q, :], start=False, stop=True, reads=[kNA2, kV2], writes=[pkY], track=(q == 3))
        ps, pk = self.psum()
        for q in range(4):
            self.I(PE, "matmul", ps[:, q * 64:(q + 1) * 64], B2[:, q, :], U2[:, q, :], start=True, stop=False, reads=[kB2, "U2_0"], writes=[pk], track=False)
            self.I(PE, "matmul", ps[:, q * 64:(q + 1) * 64], K2[:, q, :], V2[:, q, :], start=False, stop=True, reads=[kK2, kV2], writes=[pk], track=(q == 3))
        self.I(DVE, "tensor_tensor", tmpH[:], Hf[:], ps[:, 0:256].rearrange("p (a b) -> p a b", b=64), ALU.add, reads=[pk, kH], writes=["tmpH"])
        pcb = PCs[:, :, c:c + 1].to_broadcast([128, 4, 64])
        self.I(DVE, "tensor_tensor", Hf[:], tmpH[:], pcb, ALU.mult, reads=["tmpH"] + ["PCs%d" % q for q in range(4)], writes=[kH])
        self.I(ACT, "copy", Hb[:], Hf[:], reads=[kH], writes=[kHb])
        yield
        if need_y:
            yield from self.y_post(g, c, gc, s, sc, psY, pkY, bon, gst)

    def scan_group(self, g, blk, NCK, sc, ARBD, BBD, KBD, VBD, PCs, bon, gst, Hf, Hb, tmpH):
        def drain(gen):
            for _ in gen:
                pass

        def interleave(ga, gb):
            a_live = b_live = True
            while a_live or b_live:
                if a_live:
                    try:
                        next(ga)
                    except StopIteration:
                        a_live = False
                if b_live:
                    try:
                        next(gb)
                    except StopIteration:
                        b_live = False
        gc0 = blk * NCK
        drain(self.scan_pre(g, 0, gc0, sc, ARBD, BBD, KBD, VBD))
        for c in range(NCK):
            ch = self.scan_chain(g, c, gc0 + c, sc, ARBD, PCs, bon, gst, Hf, Hb, tmpH)
            if c + 1 < NCK:
                interleave(ch, self.scan_pre(g, c + 1, gc0 + c + 1, sc, ARBD, BBD, KBD, VBD))
            else:
                drain(ch)

    def y_post(self, g, c, gc, s, sc, psY, pkY, bon, gst):
        S = self.S
        pv, cb = self.pv, self.cb
        ys, yq, st, yt, YBD = sc["ys", s], sc["yq", s], sc["st", s], sc["yt", s], sc["YBD", s]
        kys, kyq, kst, kyt, kY = "ys_0", "yq_0", "yst_0", "yt_0", "YBD_0"
        self.I(ACT, "copy", ys[:], psY[:, 0:256].rearrange("p (a b) -> p a b", b=64), reads=[pkY], writes=[kys])
        self.I(DVE, "tensor_reduce", st[:, 0, :], ys[:], AX.X, ALU.add, reads=[kys], writes=[kst])
        self.I(POOL, "tensor_tensor", yq[:], ys[:], ys[:], ALU.mult, reads=[kys], writes=[kyq])
        self.I(DVE, "tensor_reduce", st[:, 1, :], yq[:], AX.X, ALU.add, reads=[kyq], writes=[kst])
        yield
        self.I(DVE, "tensor_scalar", st[:, 2, :], st[:, 0, :], 1.0 / 64, None, ALU.mult, reads=[kst], writes=[kst])
        self.I(DVE, "tensor_tensor", st[:, 3, :], st[:, 2, :], st[:, 2, :], ALU.mult, reads=[kst], writes=[kst])
        self.I(DVE, "scalar_tensor_tensor", st[:, 4, :], st[:, 1, :], 1.0 / 64, st[:, 3, :], ALU.mult, ALU.subtract, reads=[kst], writes=[kst])
        self.I(DVE, "tensor_scalar", st[:, 4, :], st[:, 4, :], 64e-5, None, ALU.add, reads=[kst], writes=[kst])
        self.I(ACT, "activation", st[:, 5, :], st[:, 4, :], AF.Sqrt, reads=[kst], writes=[kst])
        self.I(DVE, "reciprocal", st[:, 6, :], st[:, 5, :], reads=[kst], writes=[kst])
        yield
        self.I(DVE, "tensor_tensor", yq[:], ys[:], st[:, 2, :].unsqueeze(2).to_broadcast([128, 4, 64]), ALU.subtract, reads=[kys, kst], writes=[kyq])
        for h in range(2):
            pr = slice(64 * h, 64 * h + 64)
            self.I(DVE if h == 0 else POOL, "tensor_tensor", YBD[pr, :, pr], yq[pr, :, :], st[pr, 6, :].unsqueeze(2).to_broadcast([64, 4, 64]), ALU.mult, reads=[kyq, kst], writes=[kY])
        yield
        ps, pk = self.psum()
        for q in range(4):
            self.I(PE, "matmul", ps[:, q * 64:(q + 1) * 64], YBD[:, q, :], cb[:, C_ISEL:C_ISEL + 64], start=True, stop=True, reads=[kY, "cb"], writes=[pk], track=(q == 3))
        t0 = 0
        col0 = gc * CH - OWN0
        if col0 < 0:
            t0 = -col0
            col0 = 0
        nt = CH - t0
        cl = c * CH + t0
        lw = pv[:, V_LW + 4 * g: V_LW + 4 * g + 4].unsqueeze(2).to_broadcast([128, 4, nt])
        psv = ps[:, 0:256].rearrange("p (a b) -> p a b", b=64)[:, :, t0:CH]
        self.I(DVE, "tensor_tensor", yt[:, :, 0:nt], psv, lw, ALU.mult, reads=[pk, "pv"], writes=[kyt])
        self.I(POOL, "tensor_tensor", yt[:, :, 0:nt], yt[:, :, 0:nt], bon[:, :, cl:cl + nt], ALU.add, reads=[kyt] + ["bon%d" % q for q in range(4)], writes=[kyt])
        self.I(DVE, "tensor_tensor", self.yaT[:, 4 * g:4 * g + 4, col0:col0 + nt], yt[:, :, 0:nt], gst[:, :, cl:cl + nt], ALU.mult, reads=[kyt] + ["gst%d" % q for q in range(4)], writes=["yaT"])


def core_inputs(inp, c, pv_base):
    b, p = c // 2, c % 2
    x = inp["x_prompt"][b]
    if p == 1:
        xseq = np.ascontiguousarray(x)
    else:
        xseq = np.concatenate([np.zeros((1024, D), np.float32), x[:1024]], axis=0)
    pa = permA()
    return {
        "xseq": xseq,
        "xsamp": np.ascontiguousarray(inp["x_sample"][16 * c:16 * c + 16].reshape(NSM, D)),
        "pvec": core_pvec(pv_base, p),
        "st_pool": np.ascontiguousarray(inp["state_pool"][0, 16 * c:16 * c + 16]),
        "st_conv": np.ascontiguousarray(inp["state_conv"][0, 16 * c:16 * c + 16]),
        "st_shift": np.ascontiguousarray(inp["state_shift"][0, 16 * c:16 * c + 16, 0][:, pa]),
        "st_wkv": np.ascontiguousarray(inp["state_wkv"][0, 16 * c:16 * c + 16].reshape(256, 4096)),
    }


def shared_inputs(inp):
    w_in = inp["w_in"][0]
    wG = np.empty((D, 16, 256), np.float32)
    wG[:, :, 0:128] = w_in[:, 4384:6432].reshape(D, 16, 128)
    wG[:, :, 128:256] = w_in[:, 6432:8480].reshape(D, 16, 128)
    wfi = inp["w_ffn_in"][0]
    wF = np.empty((D, NJ, 256), np.float32)
    wF[:, :, 0:128] = wfi[:, 0:DFF].reshape(D, NJ, 128)
    wF[:, :, 128:256] = wfi[:, DFF:2 * DFF].reshape(D, NJ, 128)
    gpost = np.empty((128, 2, D), np.float32)
    gpost[:, 0, :] = inp["norm_post_mix"][0][None, :]
    gpost[:, 1, :] = inp["norm_post_ffn"][0][None, :]
    return {
        "wA": np.ascontiguousarray(w_in[:, permA()]),
        "consts": host_consts(),
        "w2": np.ascontiguousarray(inp["w2"][0]),
        "a2": np.ascontiguousarray(inp["a2"][0]),
        "g2": np.ascontiguousarray(inp["g2"][0]),
        "wP": np.ascontiguousarray(w_in[:, 3360:4384]),
        "wG": wG,
        "wa": np.ascontiguousarray(inp["w_branch_a"][0]),
        "wbr": np.ascontiguousarray(inp["w_branch_b"][0]),
        "poolw": np.ascontiguousarray(inp["pool_w"][0]),
        "wout": np.ascontiguousarray(inp["w_out"][0]),
        "gpost": gpost,
        "wF": wF,
        "wfo": np.ascontiguousarray(inp["w_ffn_out"][0]),
    }


_NC_CACHE = {}


def get_nc(dbg=(), stages="SABC"):
    key = (tuple(sorted(dbg)), stages)
    if key not in _NC_CACHE:
        B = Builder(dbg=set(dbg), stages=stages)
        nc = B.build()
        _NC_CACHE[key] = (nc, B)
    return _NC_CACHE[key]


def run_cores(inp, cores, dbg=(), stages="SABC", trace=False):
    nc, B = get_nc(dbg, stages)
    sh = shared_inputs(inp)
    pv_base = host_pvec(inp)
    names = set(B.ins.keys())
    maps = []
    for c in cores:
        m = dict(sh)
        m.update(core_inputs(inp, c, pv_base))
        maps.append({k: v for k, v in m.items() if k in names})
    res = run_bass_kernel_spmd(nc, maps, core_ids=list(range(len(cores))), trace=trace)
    return res


def kernel(**inp):
    inp = {k: np.asarray(v) for k, v in inp.items()}
    res = run_cores(inp, list(range(8)))
    R = res.results
    pa = permA()
    y_prompt = np.empty((4, 2048, D), np.float32)
    y_sample = np.empty((128, 4, D), np.float32)
    p_shift = np.empty((1, 4, 1, DS), np.float32)
    p_wkv = np.empty((1, 4, 16, 64, 64), np.float32)
    p_pool = np.empty((1, 4, 15, 1024), np.float32)
    p_conv = np.empty((1, 4, 2, DFF), np.float32)
    s_shift = np.zeros((1, 128, 1, DS), np.float32)
    s_wkv = np.zeros((1, 128, 16, 64, 64), np.float32)
    s_pool = np.empty((1, 128, 15, 1024), np.float32)
    s_conv = np.empty((1, 128, 2, DFF), np.float32)
    for c in range(8):
        b, p = c // 2, c % 2
        r = R[c]
        y_prompt[b, p * 1024:(p + 1) * 1024] = r["o_y"][0:1024]
        y_sample[16 * c:16 * c + 16] = r["o_y"][1024:1024 + NSM].reshape(16, 4, D)
        if p == 1:
            p_shift[0, b, 0, pa] = r["o_pshift"]
            p_wkv[0, b] = r["o_pwkv"]
            p_pool[0, b] = r["o_ppool"]
            p_conv[0, b] = r["o_pconv"]
        s_pool[0, 16 * c:16 * c + 16] = r["o_spool"]
        s_conv[0, 16 * c:16 * c + 16] = r["o_sconv"]
        if "o_sshift" in r:
            s_shift[0, 16 * c:16 * c + 16, 0][:, pa] = r["o_sshift"]
            s_wkv[0, 16 * c:16 * c + 16] = r["o_swkv"].reshape(16, 16, 64, 64)
    return (y_prompt, y_sample, p_shift, p_wkv, p_pool, p_conv, s_shift, s_wkv, s_pool, s_conv)
```

```python
import os
import numpy as np
import concourse.bass as bass
import concourse.mybir as mybir
from contextlib import ExitStack
from concourse.bass_utils import run_bass_kernel_spmd

F32 = mybir.dt.float32
BF16 = mybir.dt.bfloat16
AF = mybir.ActivationFunctionType
ALU = mybir.AluOpType
AX = mybir.AxisListType

PE, ACT, DVE, POOL, SP = "pe", "act", "dve", "pool", "sp"
NDMASEM = 12
EMBED_WAIT = os.environ.get('EMBW', '1') == '1'
SAME_ENGINE_NOWAIT = os.environ.get('SENW', '0') == '1'


class Sched:
    def __init__(self, nc, es):
        self.nc = nc
        self.q = {e: [] for e in (PE, ACT, DVE, POOL, SP)}
        self.cnt = {e: 0 for e in (PE, ACT, DVE, POOL)}
        self.sem = {e: es.enter_context(nc.semaphore("s_" + e)) for e in (PE, ACT, DVE, POOL)}
        self.dsem = {}
        self.dcnt = {}
        for e in (SP, "spo", "bg", POOL):
            self.dsem[e] = [es.enter_context(nc.semaphore("d_%s%d" % (e, i))) for i in range(NDMASEM)]
            self.dcnt[e] = 0
        self.pending = {e: [] for e in (PE, ACT, DVE, POOL, SP)}
        self.tok_ring = {}
        self.waited = {}
        self.lastw = {}
        self.readers = {}
        self.all_tokens = {}

    def _deps(self, eng, reads, writes):
        toks = []
        for k in reads:
            w = self.lastw.get(k)
            if w is not None:
                toks.append(w)
        for k in writes:
            w = self.lastw.get(k)
            if w is not None:
                toks.append(w)
            toks.extend(self.readers.get(k, ()))
        need = {}
        for (s, v, src) in toks:
            if src == eng and (eng == PE or (SAME_ENGINE_NOWAIT and eng in (ACT, DVE))):
                continue
            if self.waited.get((eng, s.name), 0) >= v:
                continue
            if need.get(s.name, (None, 0))[1] < v:
                need[s.name] = (s, v)
        for (s, v) in self.pending[eng]:
            if self.waited.get((eng, s.name), 0) >= v:
                continue
            if need.get(s.name, (None, 0))[1] < v:
                need[s.name] = (s, v)
        self.pending[eng] = []
        out = []
        for name, (s, v) in need.items():
            self.waited[(eng, name)] = v
            out.append((s, v))
        return out

    def barrier(self):
        if self.frozen:
            return
        allw = []
        for e in (PE, ACT, DVE, POOL):
            if self.cnt[e] > 0:
                allw.append((self.sem[e], self.cnt[e], e))
        for name, (s, v) in self.all_tokens.items():
            if self.tok_ring.get(name) != "bg":
                allw.append((s, v, "dma"))
        for eng in (PE, ACT, DVE, POOL, SP):
            for (s, v, src) in allw:
                if src == eng:
                    continue
                self.pending[eng].append((s, v))

    def _commit(self, tok, reads, writes):
        for k in writes:
            self.lastw[k] = tok
            self.readers[k] = []
        for k in reads:
            lst = self.readers.setdefault(k, [])
            lst[:] = [t for t in lst if t[0].name != tok[0].name]
            lst.append(tok)

    frozen = False

    def op(self, eng, fn, reads=(), writes=(), track=True):
        if self.frozen:
            return None
        waits = self._deps(eng, reads, writes)
        tok = None
        if not track:
            assert eng == PE
            self.pend_r = getattr(self, "pend_r", set()) | set(reads)
        if track:
            self.cnt[eng] += 1
            tok = (self.sem[eng], self.cnt[eng], eng)
            if eng == PE and getattr(self, "pend_r", None):
                reads = list(set(reads) | self.pend_r)
                self.pend_r = set()
            self._commit(tok, reads, writes)
        self.q[eng].append((waits, fn, tok))
        return tok

    def dma(self, qeng, out, in_, reads=(), writes=(), ring=None, **kw):
        if self.frozen:
            return None
        waits = self._deps(qeng, reads, writes)
        rk = ring or qeng
        n = self.dcnt[rk]
        self.dcnt[rk] += 1
        s = self.dsem[rk][n % NDMASEM]
        v = 16 * (n // NDMASEM + 1)
        tok = (s, v, "dma")
        self._commit(tok, reads, writes)
        self.q[qeng].append((waits, lambda e: e.dma_start(out=out, in_=in_, **kw), tok))
        self.all_tokens[s.name] = (s, v)
        self.tok_ring[s.name] = rk
        return tok

    def emit(self):
        nc = self.nc
        fin = dict(self.all_tokens)
        for e in (PE, ACT, DVE, POOL):
            if self.cnt[e] > 0:
                fin[self.sem[e].name] = (self.sem[e], self.cnt[e])
        q = self.q
        with nc.Block() as block:
            def run(eng_name):
                def body(e):
                    for (waits, fn, tok) in q[eng_name]:
                        emb = None
                        if EMBED_WAIT and waits:
                            emb = waits[-1]
                            waits = waits[:-1]
                        for (s, v) in waits:
                            e.wait_ge(s, v)
                        ins = fn(e)
                        if emb is not None:
                            ins._wait_ge(emb[0], emb[1])
                        if tok is not None:
                            ins.then_inc(tok[0], 16 if tok[2] == "dma" else 1)
                    if eng_name == SP:
                        for name, (s, v) in fin.items():
                            e.wait_ge(s, v)
                return body
            block.tensor(run(PE))
            block.scalar(run(ACT))
            block.vector(run(DVE))
            block.gpsimd(run(POOL))
            block.sync(run(SP))


D = 2048
DS = 3360
NPR = 2048
NSM = 64
OWN0 = 992
NOWN = 1120
BLK = 512
NH = 512
CH = 64
C_ID, C_BONES, C_ISEL, C_MUS, C_MUI, C_MLS, C_M01 = 0, 128, 256, 320, 448, 576, 704
NCONST = 704 + 512
V_MULR = 0
V_MU = 4
V_W0, V_A0, V_KK, V_KA, V_RK, V_LW, V_LB = 28, 36, 44, 52, 60, 68, 76
V_G1 = 84
V_PS = 100
V_G3 = 108
V_FLAG = 124
V_INVC = 125
V_CW = 189
NV = 189 + 176
DFF = 5632
NJ = 44
DECAY_C = 0.6065306597126334


def host_consts():
    c = np.zeros((128, NCONST), np.float32)
    p = np.arange(128)[:, None]
    j = np.arange(128)[None, :]
    c[:, C_ID:C_ID + 128] = (p == j)
    c[:, C_BONES:C_BONES + 128] = (p // 64 == j // 64)
    c[:, C_ISEL:C_ISEL + 64] = (p % 64 == np.arange(64)[None, :])
    c[:, C_MUS:C_MUS + 128] = (p % 64 < j % 64)
    c[:, C_MUI:C_MUI + 128] = (p % 64 <= j % 64)
    c[:, C_MLS:C_MLS + 128] = (p % 64 > j % 64)
    c[:, C_M01:C_M01 + 512] = (np.arange(512)[None, :] % 64 != 0)
    return c


def permA():
    idx = list(range(3072, 3360))
    for q in range(8):
        idx += list(range(q * 128, q * 128 + 128))
        idx += list(range(1024 + q * 128, 1024 + q * 128 + 128))
        idx += list(range(2048 + q * 128, 2048 + q * 128 + 128))
    return np.array(idx)


def host_pvec(inp):
    v = np.zeros((128, NV), np.float32)
    mu = inp["mu_shift"][0]
    v[0:64, 0] = mu[3072:3136]
    v[0:64, 1] = mu[3136:3200]
    v[0:128, 2] = mu[3200:3328]
    v[0:32, 3] = mu[3328:3360]
    for q in range(8):
        for t in range(3):
            v[:, V_MU + 3 * q + t] = mu[t * 1024 + q * 128: t * 1024 + q * 128 + 128]
    for (col, name) in ((V_W0, "w0"), (V_A0, "a0"), (V_KK, "k_k"), (V_KA, "k_a"), (V_RK, "r_k"), (V_LW, "lnx_w"), (V_LB, "lnx_b")):
        a = inp[name][0].reshape(-1)
        for q in range(8):
            v[:, col + q] = a[q * 128:(q + 1) * 128]
    g = inp["norm_pre_mix"][0]
    g3 = inp["norm_pre_ffn"][0]
    for k in range(16):
        v[:, V_G1 + k] = g[k * 128:(k + 1) * 128]
        v[:, V_G3 + k] = g3[k * 128:(k + 1) * 128]
    psc = inp["pool_scale"][0]
    for k in range(8):
        v[:, V_PS + k] = psc[k * 128:(k + 1) * 128]
    cw, cbias = inp["conv_w"][0], inp["conv_b"][0]
    for j in range(NJ):
        for t in range(3):
            v[:, V_CW + 4 * j + t] = cw[t, j * 128:(j + 1) * 128]
        v[:, V_CW + 4 * j + 3] = cbias[j * 128:(j + 1) * 128]
    return v


def core_pvec(base, p):
    v = base.copy()
    v[:, V_FLAG] = float(p)
    for gi, win in enumerate((2, 4, 8, 16)):
        for j in range(16):
            pos = p * 1024 + j
            v[:, V_INVC + gi * 16 + j] = 1.0 / min(win, pos + 1)
    return v


class _TAlias:
    def __init__(self, phys, alias, prefix="t_"):
        self.phys = phys
        self.alias = alias
        self.prefix = prefix

    def _n(self, n):
        return self.alias.get(n, n)

    def __getitem__(self, n):
        return self.phys[self._n(n)]

    def key(self, n):
        return self.prefix + self._n(n)


class StopBuild(Exception):
    pass


class Builder:
    stop_at = None

    def ckpt(self, name):
        if self.stop_at == name and not self.S.frozen:
            print("frozen at", name)
            self.S.frozen = True
            self.dbg = set()

    def __init__(self, dbg=None, stages="A"):
        self.dbg = dbg or set()
        self.stages = stages
        self.nc = bass.Bass("TRN2", target_bir_lowering=False)
        self.ins = {}
        self.outs = {}
        self.psn = 0

    def din(self, name, shape, dt=F32):
        t = self.nc.dram_tensor(name, list(shape), dt, kind="ExternalInput").ap()
        self.ins[name] = t
        return t

    def dout(self, name, shape, dt=F32):
        t = self.nc.dram_tensor(name, list(shape), dt, kind="ExternalOutput").ap()
        self.outs[name] = t
        return t

    def I(self, eng, meth, *args, reads=(), writes=(), track=True, **kw):
        return self.S.op(eng, lambda e: getattr(e, meth)(*args, **kw), reads=reads, writes=writes, track=track)

    def stage_out(self, name, tile_ap, shape, key):
        scr = self.nc.dram_tensor("scr_" + name, list(shape), F32).ap()
        self.S.dma(SP, scr, tile_ap, reads=[key], writes=["scr_" + name], ring="spo")
        return scr

    def bg(self, dst, src, name):
        self.S.dma(SP, dst, src, reads=["scr_" + name], ring="bg", allow_slow_non_contiguous=True)

    def sb(self, es, name, shape, dt=F32):
        return es.enter_context(self.nc.sbuf_tensor(name, list(shape), dt))

    def psum(self):
        i = self.psn % 8
        self.psn += 1
        return self.PS[i], "ps%d" % i

    def build(self):
        nc = self.nc
        xseq = self.xseq = self.din("xseq", [NPR, D])
        xsamp = self.xsamp = self.din("xsamp", [NSM, D])
        wA = self.din("wA", [D, DS])
        consts = self.din("consts", [128, NCONST])
        pvec = self.din("pvec", [128, NV])
        w2 = self.din("w2", [64, 1024])
        a2 = self.din("a2", [64, 1024])
        g2 = self.din("g2", [160, 1024])
        self.wP = self.din("wP", [D, 1024])
        self.wG = self.din("wG", [D, 16, 256])
        self.wa = self.din("wa", [1024, D])
        self.wbr = self.din("wbr", [1024, D])
        self.poolw = self.din("poolw", [4, 256, 256])
        self.wout = self.din("wout", [D, D])
        self.gpost = self.din("gpost", [128, 2, D])
        self.wF = self.din("wF", [D, NJ, 256])
        self.wfo = self.din("wfo", [DFF, D])
        self.st_pool = self.din("st_pool", [16, 15, 1024])
        self.st_conv = self.din("st_conv", [16, 2, DFF])
        self.o_pshift = self.dout("o_pshift", [DS])
        self.o_pwkv = self.dout("o_pwkv", [16, 64, 64])
        self.o_ppool = self.dout("o_ppool", [15, 1024])
        self.o_pconv = self.dout("o_pconv", [2, DFF])
        self.o_spool = self.dout("o_spool", [16, 15, 1024])
        self.o_sconv = self.dout("o_sconv", [16, 2, DFF])
        self.o_y = self.dout("o_y", [1024 + NSM, D])
        self.st_shift = self.din("st_shift", [16, DS])
        self.st_wkv = self.din("st_wkv", [256, 4096])
        self.o_sshift = self.dout("o_sshift", [16, DS])
        self.o_swkv = self.dout("o_swkv", [256, 4096])
        self.x1s = nc.dram_tensor("x1s", [NOWN, D], F32).ap()
        self.scrS = nc.dram_tensor("scrS", [NSM, 6, 1024], F32).ap()
        self.scrY = nc.dram_tensor("scrY", [NSM, 1024], F32).ap()
        self.scrH = nc.dram_tensor("scrH", [128, 16, NOWN], BF16).ap()
        if "ya" in self.dbg:
            self.o_ya = self.dout("d_ya", [128, 8, NOWN])
        with ExitStack() as es:
            S = self.S = Sched(nc, es)
            self.PS = [es.enter_context(nc.psum_tensor("ps%d" % i, [128, 512], F32)) for i in range(8)]
            cf = self.cf = self.sb(es, "cf", [128, NCONST])
            cb = self.cb = self.sb(es, "cb", [128, 320], BF16)
            pv = self.pv = self.sb(es, "pv", [128, NV])
            S.dma(SP, cf[:], consts, writes=["cf"])
            S.dma(SP, pv[:], pvec, writes=["pv"])
            S.dma(POOL, cb[:], consts[:, 0:320], writes=["cb"])
            self.wslot = 0
            bufX = self.bufX = self.sb(es, "bufX", [128, 16, NOWN], BF16)
            self.yaT = bufX[:, 0:8, :]
            self.ybT = bufX[:, 8:16, :]
            with ExitStack() as esw:
                self.wb = [self.sb(esw, "wb%d" % i, [128, 16, 384], BF16) for i in range(2)]
                self.wA_ap = wA
                with ExitStack() as esA:
                    w2b = self.w2b = self.sb(esA, "w2b", [64, 1024], BF16)
                    a2b = self.a2b = self.sb(esA, "a2b", [64, 1024], BF16)
                    g2a = self.g2a = self.sb(esA, "g2a", [128, 1024], BF16)
                    g2b = self.g2b = self.sb(esA, "g2b", [128, 1024], BF16)
                    S.dma(POOL, w2b[:], w2, writes=["w2b"])
                    S.dma(POOL, a2b[:], a2, writes=["a2b"])
                    S.dma(POOL, g2a[:], g2[0:128, :], writes=["g2a"])
                    self.I(POOL, "memset", g2b[:], 0.0, writes=["g2b"])
                    S.dma(POOL, g2b[0:32, :], g2[128:160, :], writes=["g2b"])
                    if "S" in self.stages:
                        self.stageS()
                    self.stageA(esA, xseq, wA)
                if "ya" in self.dbg:
                    with ExitStack() as esd:
                        yaf = self.sb(esd, "yaf", [128, 8, NOWN], F32)
                        self.I(DVE, "tensor_copy", yaf[:], self.yaT, reads=["yaT"], writes=["yaf"])
                        S.dma(SP, self.o_ya, yaf[:], reads=["yaf"], ring="spo")
                        S.barrier()
                if "B" in self.stages:
                    with ExitStack() as esm:
                        self.mT = self.sb(esm, "mT", [128, 16, NOWN], BF16)
                        with ExitStack() as esb:
                            self.hTo = self.sb(esb, "hTo", [128, 16, NOWN], BF16)
                            self.stageB12(esb)
                        self.stageB3(esm)
            if "C" in self.stages:
                self.stageC(es)
            S.emit()
        return nc

    def tok_groups(self):
        return [(0, 512), (512, 512), (1024, NOWN - 1024)]

    def stageB12(self, es_outer):
        S = self.S
        pv, cf, cb = self.pv, self.cf, self.cb
        hT = self.hTo
        with ExitStack() as es:
            sb = lambda n, s, d=F32: self.sb(es, n, s, d)
            self.xt = [sb("xtB", [128, D])]
            self.xb = [sb("xbB", [128, D], BF16)]
            self.xst = [sb("xstB", [128, 4])]
            for (c0, c1, k) in ((0, 32, "scrH_1"), (32, 544, "scrH_2"), (544, 1056, "scrH_3"), (1056, NOWN, "scrH_s")):
                S.dma(SP, hT[:, :, c0:c1], self.scrH[:, :, c0:c1], reads=[k], writes=["hTo"])
            self.ckpt("B0")
            ybT = self.ybT
            with ExitStack() as esp:
                sbp = lambda n, s, d=F32: self.sb(esp, n, s, d)
                pwb = sbp("pwb", [128, 4, 2, 256], BF16)
                S.dma(POOL, pwb[:], self.poolw.rearrange("g (k p) n -> p g k n", p=128), writes=["pwb"])
                phT = sbp("phT", [128, 8, 16, 15])
                for half in range(2):
                    sp_t = sbp("sp_t%d" % half, [120, 1024])
                    S.dma(SP, sp_t[:], self.st_pool[8 * half: 8 * half + 8].rearrange("b j c -> (b j) c"), writes=["sp_t%d" % half])
                    for c4 in range(2):
                        ps, pk = self.psum()
                        for ch in range(4):
                            self.I(PE, "transpose", ps[:, ch * 120:(ch + 1) * 120], sp_t[:, (c4 * 4 + ch) * 128:(c4 * 4 + ch + 1) * 128], cf[0:120, 0:120],
                                   reads=["sp_t%d" % half, "cf"], writes=[pk], track=(ch == 3))
                        self.I(DVE, "tensor_copy", phT[:, c4 * 4:c4 * 4 + 4, 8 * half:8 * half + 8, :],
                               ps[:, 0:480].rearrange("p (a b j) -> p a b j", a=4, b=8), reads=[pk], writes=["phT"])
                self.ckpt("B1a")
                S.dma(SP, self.o_spool[:, 0:11, :], self.st_pool[:, 4:15, :], ring="spo")
                zp = sbp("zp", [128, NOWN])
                pa_ = sbp("ppA", [128, NOWN])
                pb_ = sbp("ppB", [128, NOWN])
                dT = sbp("dT", [128, 8, NOWN], BF16)
                self.I(POOL, "memset", dT[:], 0.0, writes=["dT"])
                ppo = sbp("ppo", [128, 8, 15])
                spo = sbp("spo", [128, 8, 16, 4])
                bs = sbp("bs", [128, 16, 19])
                bs2 = sbp("bs2", [128, 16, 19])
                slabs = [(self.wP[:, ch * 128:(ch + 1) * 128], 128) for ch in range(8)]
                stream = self.slab_stream(slabs)
                NP_ = 1056
                for ch in range(8):
                    wt, wk = next(stream)
                    gi = ch // 2
                    win = 2 << gi
                    for (t0, tn) in self.tok_groups():
                        ps, pk = self.psum()
                        for k in range(16):
                            self.I(PE, "matmul", ps[:, 0:tn], wt[:, k, 0:128], hT[:, k, t0:t0 + tn], start=(k == 0), stop=(k == 15),
                                   reads=[wk, "hTo"], writes=[pk], track=(k == 15))
                        self.I(ACT, "copy", zp[:, t0:t0 + tn], ps[:, 0:tn], reads=[pk], writes=["zp"])
                    src, ksrc = zp, "zp"
                    bufs = [(pa_, "ppA"), (pb_, "ppB")]
                    for j in range(gi + 1):
                        sh = 1 << j
                        dst, kdst = bufs[j % 2]
                        self.I(DVE if j % 2 == 0 else POOL, "tensor_tensor", dst[:, sh:NP_], src[:, sh:NP_], src[:, 0:NP_ - sh], ALU.add,
                               reads=[ksrc], writes=[kdst])
                        src, ksrc = dst, kdst
                    lo = win - 1
                    self.I(DVE, "scalar_tensor_tensor", dT[:, ch, lo:NP_], src[:, lo:NP_], 1.0 / win, zp[:, lo:NP_], ALU.mult, ALU.subtract,
                           reads=[ksrc, "zp"], writes=["dT"])
                    other, kother = bufs[(gi + 1) % 2]
                    self.I(POOL, "tensor_tensor", other[:, 32:48], src[:, 32:48], pv[:, V_INVC + gi * 16: V_INVC + gi * 16 + 16], ALU.mult,
                           reads=[ksrc, "pv"], writes=[kother])
                    self.I(POOL, "tensor_tensor", dT[:, ch, 32:48], other[:, 32:48], zp[:, 32:48], ALU.subtract, reads=[kother, "zp"], writes=["dT"])
                    self.I(POOL, "tensor_copy", ppo[:, ch, :], zp[:, NP_ - 15:NP_], reads=["zp"], writes=["ppo"])
                    zps = zp[:, NP_:NOWN].rearrange("p (b t) -> p b t", t=4)
                    self.I(POOL, "tensor_copy", bs[:, :, 0:15], phT[:, ch, :, :], reads=["phT"], writes=["bs"])
                    self.I(POOL, "tensor_copy", bs[:, :, 15:19], zps, reads=["zp"], writes=["bs"])
                    self.I(POOL, "tensor_copy", spo[:, ch, :, :], zps, reads=["zp"], writes=["spo"])
                    ssrc, kss = bs, "bs"
                    sbufs = [(bs2, "bs2"), (bs, "bs")]
                    for j in range(gi + 1):
                        sh = 1 << j
                        dst, kdst = sbufs[j % 2]
                        self.I(DVE, "tensor_tensor", dst[:, :, sh:19], ssrc[:, :, sh:19], ssrc[:, :, 0:19 - sh], ALU.add, reads=[kss], writes=[kdst])
                        ssrc, kss = dst, kdst
                    self.I(DVE, "scalar_tensor_tensor", dT[:, ch, NP_:NOWN].rearrange("p (b t) -> p b t", t=4), ssrc[:, :, 15:19], 1.0 / win, zps,
                           ALU.mult, ALU.subtract, reads=[kss, "zp"], writes=["dT"])
                self.ckpt("B1b")
                scr = self.stage_out("ppo", ppo[:], [128, 8, 15], "ppo")
                for ch in range(8):
                    self.bg(self.o_ppool[:, ch * 128:(ch + 1) * 128].rearrange("j p -> p j"), scr[:, ch, :], "ppo")
                sprow = sbp("sprow", [NSM, 1024])
                for c4 in range(2):
                    ps, pk = self.psum()
                    for ch in range(4):
                        self.I(PE, "transpose", ps[0:NSM, ch * 128:(ch + 1) * 128], spo[:, c4 * 4 + ch, :, :].rearrange("p b t -> p (b t)"), cf[:, C_ID:C_ID + 128],
                               reads=["spo", "cf"], writes=[pk], track=(ch == 3))
                    self.I(DVE, "tensor_copy", sprow[:, c4 * 512:(c4 + 1) * 512], ps[0:NSM, :], reads=[pk], writes=["sprow"])
                for b in range(16):
                    S.dma(SP, self.o_spool[b, 11:15, :], sprow[4 * b:4 * b + 4, :], reads=["sprow"], ring="spo")
                self.ckpt("B1c")
                for oc in range(8):
                    gi, o2 = oc // 2, oc % 2
                    for (t0, tn) in self.tok_groups():
                        ps, pk = self.psum()
                        for kk in range(2):
                            self.I(PE, "matmul", ps[:, 0:tn], pwb[:, gi, kk, o2 * 128:(o2 + 1) * 128], dT[:, 2 * gi + kk, t0:t0 + tn],
                                   start=(kk == 0), stop=(kk == 1), reads=["pwb", "dT"], writes=[pk], track=(kk == 1))
                        self.I(ACT, "activation", ybT[:, oc, t0:t0 + tn], ps[:, 0:tn], AF.Copy, scale=pv[:, V_PS + oc: V_PS + oc + 1],
                               reads=[pk, "pv"], writes=["ybT"])
                S.barrier()
            self.ckpt("B1d")
            mT = self.mT
            tmp = [sb("b2t%d" % i, [128, 512]) for i in range(4)]
            njs = 16

            def issue(j):
                slot = self.wslot % 2
                self.wslot += 1
                key = "wb%d" % slot
                w = self.wb[slot]
                wflat = w[:].rearrange("p a b -> p (a b)")
                S.dma(POOL, wflat[:, 0:4096].rearrange("p (k n) -> p k n", n=256), self.wG[:, j, :].rearrange("(k p) n -> p k n", p=128), writes=[key])
                S.dma(POOL, wflat[:, 4096:5120].rearrange("p (k n) -> p k n", n=128), self.wa[:, j * 128:(j + 1) * 128].rearrange("(k p) n -> p k n", p=128), writes=[key])
                S.dma(POOL, wflat[:, 5120:6144].rearrange("p (k n) -> p k n", n=128), self.wbr[:, j * 128:(j + 1) * 128].rearrange("(k p) n -> p k n", p=128), writes=[key])
                return (wflat, key)
            cur = issue(0)
            for j in range(njs):
                nxt = issue(j + 1) if j + 1 < njs else None
                wflat, wk = cur
                wg = wflat[:, 0:4096].rearrange("p (k n) -> p k n", n=256)
                wa_ = wflat[:, 4096:5120].rearrange("p (k n) -> p k n", n=128)
                wb_ = wflat[:, 5120:6144].rearrange("p (k n) -> p k n", n=128)
                for (t0, tn) in self.tok_groups():
                    psa, pka = self.psum()
                    psb, pkb = self.psum()
                    ppa, pkpa = self.psum()
                    ppb, pkpb = self.psum()
                    for k in range(16):
                        self.I(PE, "matmul", psa[:, 0:tn], wg[:, k, 0:128], hT[:, k, t0:t0 + tn], start=(k == 0), stop=(k == 15),
                               reads=[wk, "hTo"], writes=[pka], track=(k == 15))
                    for k in range(16):
                        self.I(PE, "matmul", psb[:, 0:tn], wg[:, k, 128:256], hT[:, k, t0:t0 + tn], start=(k == 0), stop=(k == 15),
                               reads=[wk, "hTo"], writes=[pkb], track=(k == 15))
                    for k in range(8):
                        self.I(PE, "matmul", ppa[:, 0:tn], wa_[:, k, :], self.yaT[:, k, t0:t0 + tn], start=(k == 0), stop=(k == 7),
                               reads=[wk, "yaT"], writes=[pkpa], track=(k == 7))
                    for k in range(8):
                        self.I(PE, "matmul", ppb[:, 0:tn], wb_[:, k, :], ybT[:, k, t0:t0 + tn], start=(k == 0), stop=(k == 7),
                               reads=[wk, "ybT"], writes=[pkpb], track=(k == 7))
                    self.I(ACT, "activation", tmp[0][:, 0:tn], psa[:, 0:tn], AF.Sigmoid, reads=[pka], writes=["b2t0"])
                    self.I(ACT, "activation", tmp[1][:, 0:tn], psb[:, 0:tn], AF.Sigmoid, reads=[pkb], writes=["b2t1"])
                    self.I(DVE, "tensor_tensor", tmp[2][:, 0:tn], tmp[0][:, 0:tn], ppa[:, 0:tn], ALU.mult, reads=["b2t0", pkpa], writes=["b2t2"])
                    self.I(DVE, "tensor_tensor", tmp[3][:, 0:tn], tmp[1][:, 0:tn], ppb[:, 0:tn], ALU.mult, reads=["b2t1", pkpb], writes=["b2t3"])
                    self.I(POOL, "tensor_tensor", mT[:, j, t0:t0 + tn], tmp[2][:, 0:tn], tmp[3][:, 0:tn], ALU.add, reads=["b2t2", "b2t3"], writes=["mT"])
                cur = nxt
            S.barrier()

    def tok_tiles(self):
        self.ckpt("B2")
        return [(0, 32)] + [(32 + 128 * i, 128) for i in range(8)] + [(1056, NSM)]

    def stageB3(self, es_outer):
        S = self.S
        pv, cf, cb = self.pv, self.cf, self.cb
        mT, h2T = self.mT, self.bufX
        with ExitStack() as es:
            sb = lambda n, s, d=F32: self.sb(es, n, s, d)
            woutb = sb("woutb", [128, 16, D], BF16)
            for n in range(4):
                S.dma(POOL, woutb[:, :, n * 512:(n + 1) * 512], self.wout[:, n * 512:(n + 1) * 512].rearrange("(k p) n -> p k n", p=128), writes=["woutb%d" % n])
            gp = sb("gp", [128, D])
            S.dma(SP, gp[:], self.gpost[:, 0, :], writes=["gp"])
            mo = sb("mo", [128, D])
            xt = sb("xt3", [128, D])
            x1 = sb("x1t", [128, D])
            xb = sb("xb3", [128, D], BF16)
            st = sb("st3", [128, 16])
            for (c0, nt) in self.tok_tiles():
                xrows = self.xseq[OWN0 + c0: OWN0 + c0 + nt, :] if c0 < 1056 else self.xsamp
                S.dma(SP, xt[0:nt, :], xrows, writes=["xt3"])
                self.I(POOL, "memset", st[:, 0:4], 0.0, writes=["st3"])
                self.ckpt("B3pre")
                for n in range(4):
                    ps, pk = self.psum()
                    for k in range(16):
                        self.I(PE, "matmul", ps[0:nt, :], mT[:, k, c0:c0 + nt], woutb[:, k, n * 512:(n + 1) * 512], start=(k == 0), stop=(k == 15),
                               reads=["mT", "woutb%d" % n], writes=[pk], track=(k == 15))
                    self.I(DVE, "tensor_copy", mo[0:nt, n * 512:(n + 1) * 512], ps[0:nt, :], reads=[pk], writes=["mo"])
                    self.I(ACT, "activation", xb[0:nt, n * 512:(n + 1) * 512], mo[0:nt, n * 512:(n + 1) * 512], AF.Square, accum_out=st[0:nt, n:n + 1], reads=["mo"], writes=["xb3", "st3"])
                self.ckpt("B3a")
                self.I(DVE, "tensor_reduce", st[0:nt, 4:5], st[0:nt, 0:4], AX.X, ALU.add, reads=["st3"], writes=["st3"])
                self.I(DVE, "tensor_scalar", st[0:nt, 5:6], st[0:nt, 4:5], 1.0 / D, 1e-6, ALU.mult, ALU.add, reads=["st3"], writes=["st3"])
                self.I(ACT, "activation", st[0:nt, 6:7], st[0:nt, 5:6], AF.Sqrt, reads=["st3"], writes=["st3"])
                self.I(DVE, "reciprocal", st[0:nt, 7:8], st[0:nt, 6:7], reads=["st3"], writes=["st3"])
                self.I(DVE, "scalar_tensor_tensor", mo[0:nt, :], mo[0:nt, :], st[0:nt, 7:8], gp[0:nt, :], ALU.mult, ALU.mult, reads=["mo", "st3", "gp"], writes=["mo"])
                self.I(POOL, "tensor_tensor", x1[0:nt, :], mo[0:nt, :], xt[0:nt, :], ALU.add, reads=["mo", "xt3"], writes=["x1t"])
                self.ckpt("B3b")
                S.dma(SP, self.x1s[c0:c0 + nt, :], x1[0:nt, :], reads=["x1t"], writes=["x1s"])
                self.ckpt("B3c")
                self.norm_T(x1, "x1t", nt, h2T[:, :, c0:c0 + nt], "h2T", xb, "xb3", st, "st3", 8, V_G3)
                self.ckpt("B3d")
            S.barrier()

    def norm_T(self, xt, kx, ntok, hT, hkey, xb, kb, st, ks, sc0, gbase):
        self.I(POOL, "memset", st[:, sc0:sc0 + 1], 0.0, writes=[ks])
        self.I(ACT, "activation", xb[0:ntok, :], xt[0:ntok, :], AF.Square, accum_out=st[0:ntok, sc0:sc0 + 1], reads=[kx], writes=[kb, ks])
        self.I(DVE, "tensor_scalar", st[0:ntok, sc0 + 1:sc0 + 2], st[0:ntok, sc0:sc0 + 1], 1.0 / D, 1e-6, ALU.mult, ALU.add, reads=[ks], writes=[ks])
        self.I(ACT, "activation", st[0:ntok, sc0 + 2:sc0 + 3], st[0:ntok, sc0 + 1:sc0 + 2], AF.Sqrt, reads=[ks], writes=[ks])
        self.I(DVE, "reciprocal", st[0:ntok, sc0 + 3:sc0 + 4], st[0:ntok, sc0 + 2:sc0 + 3], reads=[ks], writes=[ks])
        self.I(ACT, "activation", xb[0:ntok, :], xt[0:ntok, :], AF.Copy, scale=st[0:ntok, sc0 + 3:sc0 + 4], reads=[kx, ks, kb], writes=[kb])
        for half in range(2):
            ps, pk = self.psum()
            pT = ps[:].bitcast(BF16).rearrange("p (a b) -> p a b", b=128)
            for k8 in range(8):
                kc = half * 8 + k8
                self.I(PE, "transpose", pT[:, k8, 0:ntok], xb[0:ntok, kc * 128:(kc + 1) * 128], self.cb[0:ntok, 0:ntok],
                       reads=[kb, "cb"], writes=[pk], track=(k8 == 7))
            gcol = self.pv[:, gbase + half * 8: gbase + half * 8 + 8].unsqueeze(2).to_broadcast([128, 8, ntok])
            self.I(DVE, "tensor_tensor", hT[:, half * 8:half * 8 + 8, :], pT[:, :, 0:ntok], gcol, ALU.mult, reads=[pk, "pv"], writes=[hkey])

    def stageC(self, es_outer):
        S = self.S
        pv, cf, cb = self.pv, self.cf, self.cb
        h2T = self.bufX
        NA = 1024 + NSM
        with ExitStack() as es:
            sb = lambda n, s, d=F32: self.sb(es, n, s, d)
            actT = sb("actT", [128, NJ, NA], BF16)
            with ExitStack() as es1:
                sb1 = lambda n, s, d=F32: self.sb(es1, n, s, d)
                wf = [sb1("wf%d" % i, [128, 16, 256], BF16) for i in range(2)]
                gt = [sb1("gt%d" % i, [128, NOWN]) for i in range(2)]
                up = [sb1("up%d" % i, [128, NOWN]) for i in range(2)]
                cv = sb1("cv", [128, NA])
                ge = sb1("ge", [128, NA])
                gs6 = sb1("gs6", [128, 16, 6])
                chT = sb1("chT", [128, NJ, 32])
                pco = sb1("pco", [128, NJ, 2])
                sco = sb1("sco", [128, NJ, 16, 2])
                ct = sb1("ct", [32, 1408])
                stc = self.st_conv.rearrange("b r c -> (b r) c")
                for pc in range(4):
                    S.dma(SP, ct[:], stc[:, pc * 1408:(pc + 1) * 1408], writes=["ct"])
                    ps, pk = self.psum()
                    for jj in range(11):
                        self.I(PE, "transpose", ps[:, jj * 32:(jj + 1) * 32], ct[:, jj * 128:(jj + 1) * 128], cf[0:32, 0:32],
                               reads=["ct", "cf"], writes=[pk], track=(jj == 10))
                    self.I(DVE, "tensor_copy", chT[:, pc * 11:(pc + 1) * 11, :], ps[:, 0:352].rearrange("p (a b) -> p a b", b=32), reads=[pk], writes=["chT"])

                def issue(j):
                    slot = j % 2
                    S.dma(POOL, wf[slot][:], self.wF[:, j, :].rearrange("(k p) n -> p k n", p=128), writes=["wf%d" % slot])
                cwc = lambda j, t: pv[:, V_CW + 4 * j + t: V_CW + 4 * j + t + 1]
                issue(0)
                for j in range(NJ):
                    if j + 1 < NJ:
                        issue(j + 1)
                    w, wk = wf[j % 2], "wf%d" % (j % 2)
                    g_, kg = gt[j % 2], "gt%d" % (j % 2)
                    u_, ku = up[j % 2], "up%d" % (j % 2)
                    for (t0, tn) in self.tok_groups():
                        psg, pkg = self.psum()
                        psu, pku = self.psum()
                        for k in range(16):
                            self.I(PE, "matmul", psg[:, 0:tn], w[:, k, 0:128], h2T[:, k, t0:t0 + tn], start=(k == 0), stop=(k == 15),
                                   reads=[wk, "h2T"], writes=[pkg], track=(k == 15))
                        for k in range(16):
                            self.I(PE, "matmul", psu[:, 0:tn], w[:, k, 128:256], h2T[:, k, t0:t0 + tn], start=(k == 0), stop=(k == 15),
                                   reads=[wk, "h2T"], writes=[pku], track=(k == 15))
                        self.I(ACT, "copy", g_[:, t0:t0 + tn], psg[:, 0:tn], reads=[pkg], writes=[kg])
                        self.I(DVE, "tensor_copy", u_[:, t0:t0 + tn], psu[:, 0:tn], reads=[pku], writes=[ku])
                    self.I(POOL, "tensor_scalar", g_[:, 30:32], g_[:, 30:32], pv[:, V_FLAG:V_FLAG + 1], None, ALU.mult, reads=[kg, "pv"], writes=[kg])
                    self.I(ACT, "activation", cv[:, 0:1024], g_[:, 32:1056], AF.Identity, bias=cwc(j, 3), scale=cwc(j, 2), reads=[kg, "pv"], writes=["cv"])
                    self.I(DVE, "scalar_tensor_tensor", cv[:, 0:1024], g_[:, 31:1055], cwc(j, 1), cv[:, 0:1024], ALU.mult, ALU.add, reads=[kg, "cv", "pv"], writes=["cv"])
                    self.I(DVE, "scalar_tensor_tensor", cv[:, 0:1024], g_[:, 30:1054], cwc(j, 0), cv[:, 0:1024], ALU.mult, ALU.add, reads=[kg, "cv", "pv"], writes=["cv"])
                    gss = g_[:, 1056:NOWN].rearrange("p (b t) -> p b t", t=4)
                    self.I(POOL, "tensor_copy", gs6[:, :, 0:2], chT[:, j, :].rearrange("p (b r) -> p b r", r=2), reads=["chT"], writes=["gs6"])
                    self.I(POOL, "tensor_copy", gs6[:, :, 2:6], gss, reads=[kg], writes=["gs6"])
                    cvs = cv[:, 1024:NA].rearrange("p (b t) -> p b t", t=4)
                    self.I(ACT, "activation", cvs, gs6[:, :, 2:6], AF.Identity, bias=cwc(j, 3), scale=cwc(j, 2), reads=["gs6", "pv"], writes=["cv"])
                    self.I(DVE, "scalar_tensor_tensor", cvs, gs6[:, :, 1:5], cwc(j, 1), cvs, ALU.mult, ALU.add, reads=["gs6", "cv", "pv"], writes=["cv"])
                    self.I(DVE, "scalar_tensor_tensor", cvs, gs6[:, :, 0:4], cwc(j, 0), cvs, ALU.mult, ALU.add, reads=["gs6", "cv", "pv"], writes=["cv"])
                    self.I(POOL, "tensor_copy", pco[:, j, :], g_[:, 1054:1056], reads=[kg], writes=["pco"])
                    self.I(POOL, "tensor_copy", sco[:, j, :, :], gss[:, :, 2:4], reads=[kg], writes=["sco"])
                    self.I(ACT, "activation", ge[:, :], cv[:, :], AF.Gelu_apprx_tanh, reads=["cv"], writes=["ge"])
                    self.I(DVE, "tensor_tensor", actT[:, j, 0:1024], ge[:, 0:1024], u_[:, 32:1056], ALU.mult, reads=["ge", ku], writes=["actT"])
                    self.I(POOL, "tensor_tensor", actT[:, j, 1024:NA], ge[:, 1024:NA], u_[:, 1056:NOWN], ALU.mult, reads=["ge", ku], writes=["actT"])
                scr = self.stage_out("pco", pco[:], [128, NJ, 2], "pco")
                for r in range(2):
                    self.bg(self.o_pconv[r].rearrange("(j p) -> p j", p=128), scr[:, :, r], "pco")
                scrow = sb1("scrow", [32, 1408])
                for pc in range(4):
                    for j4 in range(0, 11, 4):
                        nj = min(4, 11 - j4)
                        ps, pk = self.psum()
                        for jj in range(nj):
                            j = pc * 11 + j4 + jj
                            self.I(PE, "transpose", ps[0:32, jj * 128:(jj + 1) * 128], sco[:, j, :, :].rearrange("p b r -> p (b r)"), cf[:, C_ID:C_ID + 128],
                                   reads=["sco", "cf"], writes=[pk], track=(jj == nj - 1))
                        self.I(DVE, "tensor_copy", scrow[:, j4 * 128:(j4 + nj) * 128], ps[0:32, 0:nj * 128], reads=[pk], writes=["scrow"])
                    S.dma(SP, self.o_sconv.rearrange("b r c -> (b r) c")[:, pc * 1408:(pc + 1) * 1408], scrow[:], reads=["scrow"], ring="spo")
                S.barrier()
            with ExitStack() as es2:
                sb2 = lambda n, s, d=F32: self.sb(es2, n, s, d)
                dummy = sb2("dmy2", [128, 2])
                self.I(POOL, "memset", dummy[:], 0.0, writes=["h2T", "dmy2"])
                bxf = self.bufX[:].rearrange("p a b -> p (a b)")
                wo = [bxf[:, i * 5632:(i + 1) * 5632].rearrange("p (k n) -> p k n", n=512) for i in range(2)]
                fo = sb2("fo", [128, 5, D])
                gp2 = sb2("gp2", [128, D])
                S.dma(SP, gp2[:], self.gpost[:, 1, :], writes=["gp2"])
                x1t = sb2("x1r", [128, D])
                st = sb2("stc", [128, 5, 8])
                xb = sb2("xbc", [128, 512], BF16)
                tiles = [(128 * i, 128) for i in range(8)] + [(1024, NSM)]
                sets = [tiles[0:5], tiles[5:9]]
                wn = 0
                for tset in sets:
                    self.I(POOL, "memset", st[:], 0.0, writes=["stc"])
                    for n in range(4):
                        banks = [self.psum() for _ in tset]
                        for kp in range(4):
                            slot = wn % 2
                            wn += 1
                            S.dma(POOL, wo[slot], self.wfo[kp * 1408:(kp + 1) * 1408, n * 512:(n + 1) * 512].rearrange("(k p) n -> p k n", p=128),
                                  writes=["wo%d" % slot])
                            for ti, (c0, nt) in enumerate(tset):
                                ps, pk = banks[ti]
                                for jj in range(11):
                                    j = kp * 11 + jj
                                    self.I(PE, "matmul", ps[0:nt, :], actT[:, j, c0:c0 + nt], wo[slot][:, jj, :], start=(j == 0), stop=(j == NJ - 1),
                                           reads=["actT", "wo%d" % slot], writes=[pk], track=(jj == 10))
                        for ti, (c0, nt) in enumerate(tset):
                            ps, pk = banks[ti]
                            self.I(DVE, "tensor_copy", fo[0:nt, ti, n * 512:(n + 1) * 512], ps[0:nt, :], reads=[pk], writes=["fo%d" % ti])
                            self.I(ACT, "activation", xb[0:nt, :], fo[0:nt, ti, n * 512:(n + 1) * 512], AF.Square, accum_out=st[0:nt, ti, n:n + 1], reads=["fo%d" % ti], writes=["xbc", "stc"])
                    for ti, (c0, nt) in enumerate(tset):
                        self.I(DVE, "tensor_reduce", st[0:nt, ti, 4:5], st[0:nt, ti, 0:4], AX.X, ALU.add, reads=["stc"], writes=["stc"])
                        self.I(DVE, "tensor_scalar", st[0:nt, ti, 5:6], st[0:nt, ti, 4:5], 1.0 / D, 1e-6, ALU.mult, ALU.add, reads=["stc"], writes=["stc"])
                        self.I(ACT, "activation", st[0:nt, ti, 6:7], st[0:nt, ti, 5:6], AF.Sqrt, reads=["stc"], writes=["stc"])
                        self.I(DVE, "reciprocal", st[0:nt, ti, 7:8], st[0:nt, ti, 6:7], reads=["stc"], writes=["stc"])
                        xc0 = 32 + c0
                        S.dma(SP, x1t[0:nt, :], self.x1s[xc0:xc0 + nt, :], reads=["x1s"], writes=["x1r"])
                        self.I(DVE, "scalar_tensor_tensor", fo[0:nt, ti, :], fo[0:nt, ti, :], st[0:nt, ti, 7:8], gp2[0:nt, :], ALU.mult, ALU.mult,
                               reads=["fo%d" % ti, "stc", "gp2"], writes=["fo%d" % ti])
                        self.I(POOL, "tensor_tensor", fo[0:nt, ti, :], fo[0:nt, ti, :], x1t[0:nt, :], ALU.add, reads=["fo%d" % ti, "x1r"], writes=["fo%d" % ti])
                        S.dma(SP, self.o_y[c0:c0 + nt, :], fo[0:nt, ti, :], reads=["fo%d" % ti], ring="spo")

    def make_hT(self, xrows, ntok, hT, hkey, slot):
        S = self.S
        xt, xb, st = self.xt[slot], self.xb[slot], self.xst[slot]
        kx, kb, ks = "xt%d" % slot, "xb%d" % slot, "xst%d" % slot
        S.dma(SP, xt[0:ntok, :], xrows, writes=[kx])
        self.I(POOL, "memset", st[:, 0:1], 0.0, writes=[ks])
        self.I(ACT, "activation", xb[0:ntok, :], xt[0:ntok, :], AF.Square, accum_out=st[0:ntok, 0:1], reads=[kx], writes=[kb, ks])
        self.I(DVE, "tensor_scalar", st[0:ntok, 1:2], st[0:ntok, 0:1], 1.0 / D, 1e-6, ALU.mult, ALU.add, reads=[ks], writes=[ks])
        self.I(ACT, "activation", st[0:ntok, 2:3], st[0:ntok, 1:2], AF.Sqrt, reads=[ks], writes=[ks])
        self.I(DVE, "reciprocal", st[0:ntok, 3:4], st[0:ntok, 2:3], reads=[ks], writes=[ks])
        self.I(ACT, "activation", xb[0:ntok, :], xt[0:ntok, :], AF.Copy, scale=st[0:ntok, 3:4], reads=[kx, ks, kb], writes=[kb])
        for half in range(2):
            ps, pk = self.psum()
            pT = ps[:].bitcast(BF16).rearrange("p (a b) -> p a b", b=128)
            for k8 in range(8):
                kc = half * 8 + k8
                self.I(PE, "transpose", pT[:, k8, 0:ntok], xb[0:ntok, kc * 128:(kc + 1) * 128],
                                                                self.cb[0:ntok, 0:ntok], reads=[kb, "cb"], writes=[pk], track=(k8 == 7))
            gcol = self.pv[:, V_G1 + half * 8: V_G1 + half * 8 + 8].unsqueeze(2).to_broadcast([128, 8, ntok])
            self.I(DVE, "tensor_tensor", hT[:, half * 8:half * 8 + 8, :], pT[:, :, 0:ntok], gcol, ALU.mult, reads=[pk, "pv"], writes=[hkey])

    def slab_stream(self, slabs):
        S = self.S
        n = len(slabs)

        def issue(i):
            ap, ncol = slabs[i]
            slot = self.wslot % 2
            self.wslot += 1
            key = "wb%d" % slot
            S.dma(POOL, self.wb[slot][:, :, 0:ncol], ap.rearrange("(k p) n -> p k n", p=128), writes=[key])
            return (self.wb[slot], key)
        cur = issue(0)
        for i in range(n):
            nxt = issue(i + 1) if i + 1 < n else None
            yield cur
            cur = nxt

    def proj(self, ps, pk, wt, wkey, c0, M, hT, hkey, N):
        S = self.S
        for k in range(16):
            self.I(PE, "matmul", ps[0:M, 0:N], wt[:, k, c0:c0 + M], hT[:, k, 0:N], start=(k == 0), stop=(k == 15), reads=[wkey, hkey], writes=[pk], track=(k == 15))

    def stageS(self):
        S = self.S
        pv, cf, cb = self.pv, self.cf, self.cb
        wA = self.wA_ap
        N = NSM
        with ExitStack() as es0:
            sb0 = lambda n, s, d=F32: self.sb(es0, n, s, d)
            sbon = sb0("sbon", [128, 8, N], BF16)
            sgst = sb0("sgst", [128, 8, N], BF16)
            self._stageS_proj(es0, sbon, sgst)
            self._stageS_scan(es0, sbon, sgst)

    def _stageS_proj(self, es0, sbon, sgst):
        S = self.S
        pv, cf, cb = self.pv, self.cf, self.cb
        wA = self.wA_ap
        N = NSM
        with ExitStack() as es:
            sb = lambda n, s, d=F32: self.sb(es, n, s, d)
            self.xt = [sb("xtS", [128, D])]
            self.xb = [sb("xbS", [128, D], BF16)]
            self.xst = [sb("xstS", [128, 4])]
            hT = sb("hTs", [128, 16, N], BF16)
            self.make_hT(self.xsamp, N, hT[:, :, :], "hTs", 0)
            S.dma(SP, self.scrH[:, :, 1056:NOWN], hT[:, :, :], reads=["hTs"], writes=["scrH_s"])
            shT = sb("shT", [128, 28, 16])
            sst = sb("sst", [16, DS])
            S.dma(SP, sst[:], self.st_shift, writes=["sst"])
            chunks = [(0, 64), (64, 64), (128, 128), (256, 32)] + [(288 + 128 * i, 128) for i in range(24)]
            ps, pk = self.psum()
            for ci, (o, M) in enumerate(chunks):
                self.I(PE, "transpose", ps[0:M, ci * 16:(ci + 1) * 16], sst[:, o:o + M], cf[0:16, 0:16], reads=["sst", "cf"], writes=[pk], track=(ci == 27))
            self.I(DVE, "tensor_copy", shT[:, :, :], ps[:, 0:448].rearrange("p (a b) -> p a b", b=16), reads=[pk], writes=["shT"])
            zls = sb("zls", [128, 28, 16])
            tw = sb("stw", [64, N], BF16)
            zam = sb("szam", [64, N], BF16)
            sga = sb("ssga", [128, N], BF16)
            sgb = sb("ssgb", [128, N], BF16)
            self.I(POOL, "memset", sgb[:], 0.0, writes=["ssgb"])
            tokS = sb("tokS", [N, 6, 1024])
            phys = {n: sb("s_" + n, [128, N]) for n in ("zc", "d", "zr", "zk", "zv", "es", "wd", "as", "kk", "kkn", "km", "dd", "av")}
            phys["hi"] = sb("s_hi", [128, N], BF16)
            phys["lo"] = sb("s_lo", [128, N], BF16)
            T = _TAlias(phys, {"kk2": "d", "rn": "zc", "t1": "dd", "ta": "es", "rk": "wd2"}, prefix="s_")
            phys["wd2"] = sb("s_wd2", [128, N])

            def mix_s(ps, pk, M, ci, out, kout, mu):
                zc, d = T["zc"], T["d"]
                v3 = lambda t: t[0:M, 0:N].rearrange("p (b t) -> p b t", t=4)
                self.I(ACT, "copy", zc[0:M, 0:N], ps[0:M, 0:N], reads=[pk], writes=["s_zc"])
                self.I(DVE, "tensor_tensor", v3(d)[:, :, 1:4], v3(zc)[:, :, 0:3], v3(zc)[:, :, 1:4], ALU.subtract, reads=["s_zc"], writes=["s_d"])
                self.I(DVE, "tensor_tensor", v3(d)[:, :, 0], shT[0:M, ci, :], v3(zc)[:, :, 0], ALU.subtract, reads=["s_zc", "shT"], writes=["s_d"])
                self.I(DVE, "scalar_tensor_tensor", out[0:M, 0:N], d[0:M, 0:N], mu, zc[0:M, 0:N], ALU.mult, ALU.add, reads=["s_d", "s_zc", "pv"], writes=[kout])
                self.I(POOL, "tensor_copy", zls[0:M, ci, :], v3(zc)[:, :, 3], reads=["s_zc"], writes=["zls"])

            slabs = [(wA[:, 0:288], 288)] + [(wA[:, 288 + q * 384: 288 + (q + 1) * 384], 384) for q in range(8)]
            stream = self.slab_stream(slabs)
            wt, wk = next(stream)
            for (ci, (cc0, M, dst, dk, func)) in enumerate(((0, 64, tw, "stw", AF.Tanh), (64, 64, zam, "szam", AF.Copy),
                                                             (128, 128, sga, "ssga", AF.Sigmoid), (256, 32, sgb, "ssgb", AF.Sigmoid))):
                ps, pk = self.psum()
                self.proj(ps, pk, wt, wk, cc0, M, hT, "hTs", N)
                mix_s(ps, pk, M, ci, T["zr"], "s_zr", pv[0:M, ci:ci + 1])
                self.I(ACT, "activation", dst[0:M, 0:N], T["zr"][0:M, 0:N], func, reads=["s_zr"], writes=[dk])
            bonesb = cb[:, C_BONES:C_BONES + 128]
            for q in range(8):
                wt, wk = next(stream)
                col = lambda base: pv[:, base + q: base + q + 1]
                qs = slice(q * 128, (q + 1) * 128)
                for t, nm in enumerate(("zr", "zk", "zv")):
                    ps, pk = self.psum()
                    self.proj(ps, pk, wt, wk, t * 128, 128, hT, "hTs", N)
                    mix_s(ps, pk, 128, 4 + 3 * q + t, T[nm], "s_" + nm, pv[:, V_MU + 3 * q + t: V_MU + 3 * q + t + 1])
                zr, zk, zv = T["zr"], T["zk"], T["zv"]
                ps, pk = self.psum()
                self.I(PE, "matmul", ps[:, 0:N], self.w2b[:, qs], tw[:, 0:N], start=True, stop=True, reads=["w2b", "stw"], writes=[pk])
                self.I(ACT, "activation", T["es"][:, 0:N], ps[:, 0:N], AF.Sigmoid, bias=col(V_W0), reads=[pk, "pv"], writes=["s_es"])
                self.I(ACT, "activation", T["wd"][:, 0:N], T["es"][:, 0:N], AF.Exp, scale=-DECAY_C, reads=["s_es"], writes=["s_wd"])
                ps, pk = self.psum()
                self.I(PE, "matmul", ps[:, 0:N], self.a2b[:, qs], zam[:, 0:N], start=True, stop=True, reads=["a2b", "szam"], writes=[pk])
                self.I(ACT, "activation", T["as"][:, 0:N], ps[:, 0:N], AF.Sigmoid, bias=col(V_A0), reads=[pk, "pv"], writes=["s_as"])
                self.I(POOL, "tensor_scalar", T["kk"][:, 0:N], zk[:, 0:N], col(V_KK), None, ALU.mult, reads=["s_zk", "pv"], writes=["s_kk"])
                self.I(POOL, "tensor_tensor", T["kk2"][:, 0:N], T["kk"][:, 0:N], T["kk"][:, 0:N], ALU.mult, reads=["s_kk"], writes=["s_d"])
                ps, pk = self.psum()
                self.bsum(ps, pk, T, "kk2", N)
                self.I(ACT, "activation", T["rn"][:, 0:N], ps[:, 0:N], AF.Sqrt, reads=[pk], writes=["s_zc"])
                self.I(DVE, "tensor_scalar_max", T["rn"][:, 0:N], T["rn"][:, 0:N], 1e-12, reads=["s_zc"], writes=["s_zc"])
                self.I(DVE, "reciprocal", T["rn"][:, 0:N], T["rn"][:, 0:N], reads=["s_zc"], writes=["s_zc"])
                self.I(DVE, "tensor_tensor", T["kkn"][:, 0:N], T["kk"][:, 0:N], T["rn"][:, 0:N], ALU.mult, reads=["s_kk", "s_zc"], writes=["s_kkn"])
                self.I(POOL, "tensor_scalar", T["t1"][:, 0:N], T["as"][:, 0:N], -1.0, col(V_KA), ALU.add, ALU.mult, reads=["s_as", "pv"], writes=["s_dd"])
                self.I(POOL, "tensor_tensor", T["t1"][:, 0:N], T["t1"][:, 0:N], zk[:, 0:N], ALU.mult, reads=["s_dd", "s_zk"], writes=["s_dd"])
                self.I(POOL, "tensor_tensor", T["km"][:, 0:N], T["t1"][:, 0:N], zk[:, 0:N], ALU.add, reads=["s_dd", "s_zk"], writes=["s_km"])
                self.I(POOL, "tensor_tensor", T["ta"][:, 0:N], T["kkn"][:, 0:N], T["as"][:, 0:N], ALU.mult, reads=["s_kkn", "s_as"], writes=["s_es"])
                self.I(POOL, "tensor_scalar", T["av"][:, 0:N], T["kkn"][:, 0:N], -1.0, None, ALU.mult, reads=["s_kkn"], writes=["s_av"])
                self.I(DVE, "scalar_tensor_tensor", T["rk"][:, 0:N], zr[:, 0:N], col(V_RK), T["km"][:, 0:N], ALU.mult, ALU.mult, reads=["s_zr", "s_km", "pv"], writes=["s_wd2"])
                ps, pk = self.psum()
                self.bsum(ps, pk, T, "rk", N)
                self.I(DVE, "tensor_tensor", sbon[:, q, :], ps[:, 0:N], zv[:, 0:N], ALU.mult, reads=[pk, "s_zv"], writes=["sbon"])
                self.I(POOL, "tensor_scalar", sbon[:, q, :], sbon[:, q, :], col(V_LB), None, ALU.add, reads=["sbon", "pv"], writes=["sbon"])
                ps, pk = self.psum()
                self.I(PE, "matmul", ps[:, 0:N], self.g2a[:, qs], sga[:, 0:N], start=True, stop=False, reads=["g2a", "ssga"], writes=[pk], track=False)
                self.I(PE, "matmul", ps[:, 0:N], self.g2b[:, qs], sgb[:, 0:N], start=False, stop=True, reads=["g2b", "ssgb"], writes=[pk])
                self.I(ACT, "copy", sgst[:, q, :], ps[:, 0:N], reads=[pk], writes=["sgst"])
                srcs = [("zr", "s_zr"), ("km", "s_km"), ("zv", "s_zv"), ("wd", "s_wd"), ("av", "s_av"), ("ta", "s_es")]
                psA, pkA = self.psum()
                psB, pkB = self.psum()
                for qi, (nm, kk_) in enumerate(srcs):
                    pp, ppk = (psA, pkA) if qi < 4 else (psB, pkB)
                    o = (qi % 4) * 128
                    self.I(PE, "transpose", pp[0:N, o:o + 128], T[nm][:, 0:N], cf[:, C_ID:C_ID + 128], reads=[kk_, "cf"], writes=[ppk], track=(qi in (3, 5)))
                self.I(DVE, "tensor_copy", tokS[:, 0:4, qs], psA[0:N, :].rearrange("p (a b) -> p a b", b=128), reads=[pkA], writes=["tokS"])
                self.I(ACT, "copy", tokS[:, 4:6, qs], psB[0:N, 0:256].rearrange("p (a b) -> p a b", b=128), reads=[pkB], writes=["tokS"])
            zrow = sb("zrow", [16, DS])
            for c0_ in range(0, 28, 4):
                ps, pk = self.psum()
                grp = list(enumerate(chunks))[c0_:c0_ + 4]
                for n_, (ci, (o, M)) in enumerate(grp):
                    self.I(PE, "transpose", ps[0:16, n_ * 128:n_ * 128 + M], zls[0:M, ci, :], cf[0:M, 0:M], reads=["zls", "cf"], writes=[pk], track=(n_ == len(grp) - 1))
                for n_, (ci, (o, M)) in enumerate(grp):
                    self.I(DVE, "tensor_copy", zrow[:, o:o + M], ps[0:16, n_ * 128:n_ * 128 + M], reads=[pk], writes=["zrow"])
            S.dma(SP, self.o_sshift, zrow[:], reads=["zrow"], ring="spo")
            S.dma(SP, self.scrS, tokS[:], reads=["tokS"], writes=["scrS"])
            S.barrier()

    def _stageS_scan(self, es0, sbon, sgst):
        S = self.S
        pv, cf, cb = self.pv, self.cf, self.cb
        N = NSM
        with ExitStack() as es:
            sb = lambda n, s, d=F32: self.sb(es, n, s, d)
            ytok = sb("ytok", [N, 1024])
            for bt in range(2):
                E = DVE
                kS, kq, kt, ky = "Sst%d" % bt, "qin%d" % bt, "stmp%d" % bt, "ysb%d" % bt
                Sst = sb(kS, [128, 64, 64])
                tmp = sb(kt, [128, 64, 64])
                qin = sb(kq, [128, 4, 6, 64])
                ysb = sb(ky, [128, 4, 64])
                sa = sb("sa%d" % bt, [128, 64])
                yq = sb("syq%d" % bt, [128, 4, 64])
                sst_ = sb("sstat%d" % bt, [128, 8, 4])
                S.dma(SP, Sst[:].rearrange("p v k -> p (v k)"), self.st_wkv[bt * 128:(bt + 1) * 128, :], writes=[kS])
                for bl in range(8):
                    b = bt * 8 + bl
                    S.dma(SP, qin[16 * bl:16 * bl + 16, :, :, :], self.scrS[4 * b:4 * b + 4, :, :].rearrange("t q (h k) -> h t q k", k=64),
                          reads=["scrS"], writes=[kq + "_%d" % bl])
                bc_k = lambda t, qi: qin[:, t, qi, :].unsqueeze(1).to_broadcast([128, 64, 64])
                kqs = [kq + "_%d" % bl for bl in range(8)]
                for t in range(4):
                    self.I(E, "tensor_tensor", tmp[:], Sst[:], bc_k(t, 4), ALU.mult, reads=[kS] + kqs, writes=[kt])
                    self.I(DVE, "tensor_reduce", sa[:], tmp[:], AX.X, ALU.add, reads=[kt], writes=["sa%d" % bt])
                    self.I(E, "tensor_tensor", Sst[:], Sst[:], bc_k(t, 3), ALU.mult, reads=[kS] + kqs, writes=[kS])
                    self.I(E, "tensor_tensor", tmp[:], sa[:].unsqueeze(2).to_broadcast([128, 64, 64]), bc_k(t, 5), ALU.mult, reads=["sa%d" % bt] + kqs, writes=[kt])
                    self.I(E, "tensor_tensor", Sst[:], Sst[:], tmp[:], ALU.add, reads=[kS, kt], writes=[kS])
                    self.I(E, "tensor_tensor", tmp[:], qin[:, t, 2, :].unsqueeze(2).to_broadcast([128, 64, 64]), bc_k(t, 1), ALU.mult, reads=kqs, writes=[kt])
                    self.I(E, "tensor_tensor", Sst[:], Sst[:], tmp[:], ALU.add, reads=[kS, kt], writes=[kS])
                    self.I(E, "tensor_tensor", tmp[:], Sst[:], bc_k(t, 0), ALU.mult, reads=[kS] + kqs, writes=[kt])
                    self.I(DVE, "tensor_reduce", ysb[:, t, :], tmp[:], AX.X, ALU.add, reads=[kt], writes=[ky])
                S.dma(SP, self.o_swkv[bt * 128:(bt + 1) * 128, :], Sst[:].rearrange("p v k -> p (v k)"), reads=[kS], ring="spo")
                st_ = sst_
                ks_ = "sstat%d" % bt
                self.I(DVE, "tensor_reduce", st_[:, 0, :], ysb[:], AX.X, ALU.add, reads=[ky], writes=[ks_])
                self.I(E, "tensor_tensor", yq[:], ysb[:], ysb[:], ALU.mult, reads=[ky], writes=["syq%d" % bt])
                self.I(DVE, "tensor_reduce", st_[:, 1, :], yq[:], AX.X, ALU.add, reads=["syq%d" % bt], writes=[ks_])
                self.I(E, "tensor_scalar", st_[:, 2, :], st_[:, 0, :], 1.0 / 64, None, ALU.mult, reads=[ks_], writes=[ks_])
                self.I(E, "tensor_tensor", st_[:, 3, :], st_[:, 2, :], st_[:, 2, :], ALU.mult, reads=[ks_], writes=[ks_])
                self.I(E, "tensor_scalar", st_[:, 4, :], st_[:, 1, :], 1.0 / 64, 64e-5, ALU.mult, ALU.add, reads=[ks_], writes=[ks_])
                self.I(E, "tensor_tensor", st_[:, 4, :], st_[:, 4, :], st_[:, 3, :], ALU.subtract, reads=[ks_], writes=[ks_])
                self.I(ACT, "activation", st_[:, 5, :], st_[:, 4, :], AF.Sqrt, reads=[ks_], writes=[ks_])
                self.I(DVE, "reciprocal", st_[:, 6, :], st_[:, 5, :], reads=[ks_], writes=[ks_])
                self.I(E, "tensor_tensor", yq[:], ysb[:], st_[:, 2, :].unsqueeze(2).to_broadcast([128, 4, 64]), ALU.subtract, reads=[ky, ks_], writes=["syq%d" % bt])
                self.I(E, "tensor_tensor", yq[:], yq[:], st_[:, 6, :].unsqueeze(2).to_broadcast([128, 4, 64]), ALU.mult, reads=["syq%d" % bt, ks_], writes=["syq%d" % bt])
                for bl in range(8):
                    b = bt * 8 + bl
                    S.dma(SP, self.scrY[4 * b:4 * b + 4, :].rearrange("t (h v) -> h t v", v=64), yq[16 * bl:16 * bl + 16, :, :], reads=["syq%d" % bt], writes=["scrY%d" % b])
            S.dma(SP, ytok[:], self.scrY, reads=["scrY%d" % b for b in range(16)], writes=["ytok"])
            if "sdbg" in self.dbg:
                S.dma(SP, self.dout("d_ytok", [N, 1024]), ytok[:], reads=["ytok"], ring="spo")
                dbf = sb("dbf", [128, 2, 8, N])
                self.I(DVE, "tensor_copy", dbf[:, 0], sbon[:], reads=["sbon"], writes=["dbf"])
                self.I(DVE, "tensor_copy", dbf[:, 1], sgst[:], reads=["sgst"], writes=["dbf"])
                S.dma(SP, self.dout("d_sbg", [128, 2, 8, N]), dbf[:], reads=["dbf"], ring="spo")
            ytb = sb("ytb", [N, 1024], BF16)
            self.I(ACT, "copy", ytb[:], ytok[:], reads=["ytok"], writes=["ytb"])
            ps, pk = self.psum()
            pT = ps[:].bitcast(BF16).rearrange("p (a b) -> p a b", b=128)[:, :, 0:N]
            for q in range(8):
                self.I(PE, "transpose", pT[:, q, :], ytb[:, q * 128:(q + 1) * 128], cb[0:N, 0:N], reads=["ytb", "cb"], writes=[pk], track=(q == 7))
            yt = sb("syt", [128, 8, N])
            lw = pv[:, V_LW: V_LW + 8].unsqueeze(2).to_broadcast([128, 8, N])
            self.I(DVE, "tensor_tensor", yt[:], pT, lw, ALU.mult, reads=[pk, "pv"], writes=["syt"])
            self.I(POOL, "tensor_tensor", yt[:], yt[:], sbon[:], ALU.add, reads=["syt", "sbon"], writes=["syt"])
            self.I(DVE, "tensor_tensor", self.yaT[:, :, 1056:NOWN], yt[:], sgst[:], ALU.mult, reads=["syt", "sgst"], writes=["yaT"])
            S.barrier()

    def stageA(self, es0, xseq, wA):
        S = self.S
        pv, cf, cb = self.pv, self.cf, self.cb
        with ExitStack() as es:
            sb = lambda n, s, d=F32: self.sb(es, n, s, d)
            self.xt = [sb("xt%d" % i, [128, D]) for i in range(1)]
            self.xb = [sb("xb%d" % i, [128, D], BF16) for i in range(1)]
            self.xst = [sb("xst%d" % i, [128, 4]) for i in range(1)]
            hT = sb("hTb", [128, 16, BLK], BF16)
            zl = sb("zl", [128, 28])
            self.I(POOL, "memset", zl[:], 0.0, writes=["zl"])
            tw = sb("tw", [64, BLK], BF16)
            zam = sb("zam", [64, BLK], BF16)
            sga = sb("sga", [128, BLK], BF16)
            sgb = sb("sgb", [128, BLK], BF16)
            self.I(POOL, "memset", sgb[:], 0.0, writes=["sgb"])
            NCK = BLK // CH
            ARBD = sb("ARBD", [128, 4, NCK, 2, 128], BF16)
            BBD = sb("BBD", [128, 4, NCK, 128], BF16)
            KBD = sb("KBD", [128, 4, NCK, 128], BF16)
            VBD = sb("VBD", [128, 4, NCK, 128], BF16)
            for (t, k) in ((ARBD, "ARBD"), (BBD, "BBD"), (KBD, "KBD"), (VBD, "VBD")):
                self.I(POOL, "memset", t[:], 0.0, writes=[k + "%d" % q for q in range(4)])
            PCs = sb("PCs", [128, 4, NCK])
            bon = sb("bon", [128, 4, BLK], BF16)
            gst = sb("gst", [128, 4, BLK], BF16)
            Hf = [sb("Hf%d" % g, [128, 4, 64]) for g in range(2)]
            Hb = [sb("Hb%d" % g, [128, 4, 64], BF16) for g in range(2)]
            for g in range(2):
                self.I(POOL, "memset", Hf[g][:], 0.0, writes=["Hf%d" % g])
                self.I(POOL, "memset", Hb[g][:], 0.0, writes=["Hb%d" % g])
            names = ("zc", "d", "zr", "zk", "zv", "es", "cum", "dd", "pinv", "prr", "pa", "as", "kk", "kkn", "km")
            phys = {n: sb("t_" + n, [128, NH]) for n in names[:8]}
            spare = self.bufX[:, 8:16, :].rearrange("p a b -> p (a b)")
            sparef = spare.bitcast(F32)
            for i, n in enumerate(names[8:]):
                phys[n] = sparef[:, i * NH:(i + 1) * NH]
            phys["hi"] = spare[:, 7 * 2 * NH: 7 * 2 * NH + NH]
            phys["lo"] = spare[:, 7 * 2 * NH + NH: 7 * 2 * NH + 2 * NH]
            T = _TAlias(phys, {"kk2": "d", "rn": "zc", "t1": "dd", "ta": "es", "rk": "cum"})
            sc = {}
            for s in range(2):
                sc["NA1", s] = sb("NA1_%d" % s, [128, 4, 256], BF16)
                sc["NA2", s] = sb("NA2_%d" % s, [128, 4, 256], BF16)
                for i in range(2):
                    sc["Q", s, i] = sb("Q_%d%d" % (s, i), [128, 4, 128], BF16)
                    if s == 0:
                        sc["NL", s, i] = sb("NL_%d%d" % (s, i), [128, 4, 128], BF16)
                        sc["TU", s, i] = sb("TU_%d%d" % (s, i), [128, 4, 128], BF16)
                sc["B2", s] = sb("B2_%d" % s, [128, 4, 128], BF16)
                sc["K2", s] = sb("K2_%d" % s, [128, 4, 128], BF16)
                sc["V2", s] = sb("V2_%d" % s, [128, 4, 64], BF16)
                if s == 0:
                    sc["X2", s] = sb("X2_%d" % s, [128, 4, 64], BF16)
                    sc["U2", s] = sb("U2_%d" % s, [128, 4, 64], BF16)
                    sc["YBD", s] = sb("YBD_%d" % s, [128, 4, 128], BF16)
                    self.I(POOL, "memset", sc["YBD", s][:], 0.0, writes=["YBD_%d" % s])
                    sc["ys", s] = sb("ys_%d" % s, [128, 4, 64])
                    sc["yq", s] = sb("yq_%d" % s, [128, 4, 64])
                    sc["st", s] = sb("yst_%d" % s, [128, 8, 4])
                    sc["yt", s] = sb("yt_%d" % s, [128, 4, 64])
                else:
                    for nm in ("X2", "U2", "YBD", "ys", "yq", "st", "yt"):
                        sc[nm, s] = sc[nm, 0]
            tmpH = sb("tmpH", [128, 4, 64])

            nblk = NPR // BLK
            lr_slab = (wA[:, 0:288], 288)
            pair_slabs = [(wA[:, 288 + q * 384: 288 + (q + 1) * 384], 384) for q in range(8)]
            slabs = []
            for blk in range(nblk):
                slabs.append(lr_slab)
                slabs += pair_slabs
            stream = self.slab_stream(slabs)

            for blk in range(nblk):
                c0 = blk * BLK
                N = BLK
                for t4 in range(4):
                    self.make_hT(xseq[c0 + t4 * 128: c0 + (t4 + 1) * 128, :], 128, hT[:, :, t4 * 128:(t4 + 1) * 128], "hTb", 0)
                self.ckpt("hT")
                if blk == 1:
                    S.dma(SP, self.scrH[:, :, 0:32], hT[:, :, BLK - 32:BLK], reads=["hTb"], writes=["scrH_1"])
                elif blk >= 2:
                    S.dma(SP, self.scrH[:, :, 32 + (blk - 2) * BLK: 32 + (blk - 1) * BLK], hT[:, :, :], reads=["hTb"], writes=["scrH_%d" % blk])
                wt, wk = next(stream)
                for (ci, (cc0, M, dst, dk, func)) in enumerate(((0, 64, tw, "tw", AF.Tanh), (64, 64, zam, "zam", AF.Copy),
                                                                 (128, 128, sga, "sga", AF.Sigmoid), (256, 32, sgb, "sgb", AF.Sigmoid))):
                    ps, pk = self.psum()
                    self.proj(ps, pk, wt, wk, cc0, M, hT, "hTb", N)
                    for co in range(0, N, NH):
                        self.mix(ps[:, co:co + NH], pk, M, NH, ci, T, zl)
                        self.I(ACT, "activation", dst[0:M, co:co + NH], T["zr"][0:M, 0:NH], func, reads=["t_zr"], writes=[dk])
                self.ckpt("lowrank")
                for g in range(2):
                    for qg in range(4):
                        q = g * 4 + qg
                        wt, wk = next(stream)
                        self.prep_pair(wt, wk, hT, N, q, qg, T, zl, tw, zam, sga, sgb, ARBD, BBD, KBD, VBD, PCs, bon, gst, blk)
                        self.ckpt("prep0")
                    self.ckpt("prep")
                    self.scan_group(g, blk, NCK, sc, ARBD, BBD, KBD, VBD, PCs, bon, gst, Hf[g], Hb[g], tmpH)
            scr = self.stage_out("zl", zl[:], [128, 28], "zl")
            self.bg(self.o_pshift[0:64].rearrange("(p o) -> p o", o=1), scr[0:64, 0:1], "zl")
            self.bg(self.o_pshift[64:128].rearrange("(p o) -> p o", o=1), scr[0:64, 1:2], "zl")
            self.bg(self.o_pshift[128:256].rearrange("(p o) -> p o", o=1), scr[:, 2:3], "zl")
            self.bg(self.o_pshift[256:288].rearrange("(p o) -> p o", o=1), scr[0:32, 3:4], "zl")
            self.bg(self.o_pshift[288:DS].rearrange("(j p) -> p j", p=128), scr[:, 4:28], "zl")
            for g in range(2):
                hh = sb("hh%d" % g, [128, 4, 64], BF16)
                hl = sb("hl%d" % g, [128, 4, 64], BF16)
                self.I(POOL, "tensor_copy", hh[:], Hf[g][:], reads=["Hf%d" % g], writes=["hh%d" % g])
                self.I(POOL, "tensor_tensor", hl[:], Hf[g][:], hh[:], ALU.subtract, reads=["Hf%d" % g, "hh%d" % g], writes=["hl%d" % g])
                ps, pk = self.psum()
                for qg in range(4):
                    self.I(PE, "matmul", ps[0:64, qg * 128:(qg + 1) * 128], hh[:, qg, :], cb[:, 0:128], start=True, stop=False, reads=["hh%d" % g, "cb"], writes=[pk], track=False)
                    self.I(PE, "matmul", ps[0:64, qg * 128:(qg + 1) * 128], hl[:, qg, :], cb[:, 0:128], start=False, stop=True, reads=["hl%d" % g, "cb"], writes=[pk], track=(qg == 3))
                so = sb("so%d" % g, [64, 4, 2, 64])
                self.I(DVE, "tensor_copy", so[:], ps[0:64, :].rearrange("p (a b c) -> p a b c", a=4, b=2), reads=[pk], writes=["so%d" % g])
                S.dma(SP, self.o_pwkv[g * 8:(g + 1) * 8].rearrange("(q s) v k -> v q s k", s=2), so[:], reads=["so%d" % g], ring="spo")
            S.barrier()

    def mix(self, ps, pk, M, N, ci, T, zl):
        S = self.S
        zc, d, out = T["zc"], T["d"], T["zr"]
        self.mix_to(ps, pk, M, N, ci, zl, zc, "t_zc", d, "t_d", out, "t_zr", self.pv[0:M, ci:ci + 1] if ci < 4 else None)

    def mix_to(self, ps, pk, M, N, ci, zl, zc, kzc, d, kd, out, kout, mu):
        S = self.S
        self.I(ACT, "copy", zc[0:M, 0:N], ps[0:M, 0:N], reads=[pk], writes=[kzc])
        self.I(DVE, "tensor_tensor", d[0:M, 1:N], zc[0:M, 0:N - 1], zc[0:M, 1:N], ALU.subtract, reads=[kzc], writes=[kd])
        self.I(DVE, "tensor_tensor", d[0:M, 0:1], zl[0:M, ci:ci + 1], zc[0:M, 0:1], ALU.subtract, reads=[kzc, "zl"], writes=[kd])
        self.I(DVE, "scalar_tensor_tensor", out[0:M, 0:N], d[0:M, 0:N], mu, zc[0:M, 0:N], ALU.mult, ALU.add, reads=[kd, kzc, "pv"], writes=[kout])
        self.I(POOL, "tensor_copy", zl[0:M, ci:ci + 1], zc[0:M, N - 1:N], reads=[kzc], writes=["zl"])

    def bsum(self, ps, pk, T, name, N):
        S = self.S
        src, ksrc = T[name], T.key(name)
        bonesb = self.cb[:, C_BONES:C_BONES + 128]
        khi, klo = T.key("hi"), T.key("lo")
        self.I(ACT, "copy", T["hi"][:, 0:N], src[:, 0:N], reads=[ksrc], writes=[khi])
        self.I(POOL, "tensor_tensor", T["lo"][:, 0:N], src[:, 0:N], T["hi"][:, 0:N], ALU.subtract, reads=[ksrc, khi], writes=[klo])
        self.I(PE, "matmul", ps[:, 0:N], bonesb, T["hi"][:, 0:N], start=True, stop=False, reads=["cb", khi], writes=[pk], track=False)
        self.I(PE, "matmul", ps[:, 0:N], bonesb, T["lo"][:, 0:N], start=False, stop=True, reads=["cb", klo], writes=[pk])

    def prep_pair(self, wt, wk, hT, N, q, qg, T, zl, tw, zam, sga, sgb, ARBD, BBD, KBD, VBD, PCs, bon, gst, blk):
        S = self.S
        pv, cf = self.pv, self.cf
        col = lambda base: pv[:, base + q: base + q + 1]
        qs = slice(q * 128, (q + 1) * 128)
        W = N
        co = 0
        cw = slice(0, W)
        psw, pkw = self.psum()
        self.I(PE, "matmul", psw[:, 0:W], self.w2b[:, qs], tw[:, cw], start=True, stop=True, reads=["w2b", "tw"], writes=[pkw])
        psa, pka = self.psum()
        self.I(PE, "matmul", psa[:, 0:W], self.a2b[:, qs], zam[:, cw], start=True, stop=True, reads=["a2b", "zam"], writes=[pka])
        pj = []
        for t in range(3):
            ps, pk = self.psum()
            self.proj(ps, pk, wt, wk, t * 128, 128, hT, "hTb", N)
            pj.append((ps, pk))
        self.I(ACT, "activation", T["es"][:, 0:W], psw[:, 0:W], AF.Sigmoid, bias=col(V_W0), reads=[pkw, "pv"], writes=["t_es"])
        self.I(ACT, "activation", T["as"][:, 0:W], psa[:, 0:W], AF.Sigmoid, bias=col(V_A0), reads=[pka, "pv"], writes=["t_as"])
        self.I(DVE, "tensor_tensor_scan", T["cum"][:, 0:W], cf[:, C_M01:C_M01 + W], T["es"][:, 0:W], 0.0, ALU.mult, ALU.add, reads=["t_es", "cf"], writes=["t_cum"])
        self.I(POOL, "tensor_tensor", T["dd"][:, 0:W], T["cum"][:, 0:W], T["es"][:, 0:W], ALU.subtract, reads=["t_cum", "t_es"], writes=["t_dd"])
        nck = W // CH
        ck0 = 0

        def mixq(t, nm):
            ps, pk = pj[t]
            ci = 4 + 3 * q + t
            self.mix_to(ps[:, cw], pk, 128, W, ci, zl, T["zc"], "t_zc", T["d"], "t_d", T[nm], "t_" + nm, pv[:, V_MU + 3 * q + t: V_MU + 3 * q + t + 1])
        mixq(0, "zr")
        self.I(ACT, "activation", T["pinv"][:, 0:W], T["cum"][:, 0:W], AF.Exp, scale=DECAY_C, reads=["t_cum"], writes=["t_pinv"])
        self.I(ACT, "activation", T["prr"][:, 0:W], T["cum"][:, 0:W], AF.Exp, scale=-DECAY_C, reads=["t_cum"], writes=["t_prr"])
        self.I(ACT, "activation", T["pa"][:, 0:W], T["dd"][:, 0:W], AF.Exp, scale=-DECAY_C, reads=["t_dd"], writes=["t_pa"])
        self.I(POOL, "tensor_copy", PCs[:, qg, ck0:ck0 + nck], T["prr"][:, 0:W].rearrange("p (c t) -> p c t", t=CH)[:, :, CH - 1], reads=["t_prr"], writes=["PCs%d" % qg])
        mixq(1, "zk")
        mixq(2, "zv")
        if True:
            zr, zk, zv = T["zr"], T["zk"], T["zv"]
            self.I(ACT, "activation", T["kk"][:, 0:W], zk[:, 0:W], AF.Copy, scale=col(V_KK), reads=["t_zk", "pv"], writes=["t_kk"])
            self.I(ACT, "activation", T["kk2"][:, 0:W], T["kk"][:, 0:W], AF.Square, reads=["t_kk"], writes=["t_d"])
            ps, pk = self.psum()
            self.bsum(ps, pk, T, "kk2", W)
            self.I(ACT, "activation", T["rn"][:, 0:W], ps[:, 0:W], AF.Sqrt, reads=[pk], writes=["t_zc"])
            self.I(DVE, "tensor_scalar_max", T["rn"][:, 0:W], T["rn"][:, 0:W], 1e-12, reads=["t_zc"], writes=["t_zc"])
            self.I(DVE, "reciprocal", T["rn"][:, 0:W], T["rn"][:, 0:W], reads=["t_zc"], writes=["t_zc"])
            self.I(DVE, "tensor_tensor", T["kkn"][:, 0:W], T["kk"][:, 0:W], T["rn"][:, 0:W], ALU.mult, reads=["t_kk", "t_zc"], writes=["t_kkn"])
            self.I(POOL, "tensor_scalar", T["t1"][:, 0:W], T["as"][:, 0:W], -1.0, col(V_KA), ALU.add, ALU.mult, reads=["t_as", "pv"], writes=["t_dd"])
            self.I(POOL, "tensor_tensor", T["t1"][:, 0:W], T["t1"][:, 0:W], zk[:, 0:W], ALU.mult, reads=["t_dd", "t_zk"], writes=["t_dd"])
            self.I(POOL, "tensor_tensor", T["km"][:, 0:W], T["t1"][:, 0:W], zk[:, 0:W], ALU.add, reads=["t_dd", "t_zk"], writes=["t_km"])
            self.I(POOL, "tensor_tensor", T["ta"][:, 0:W], T["kkn"][:, 0:W], T["as"][:, 0:W], ALU.mult, reads=["t_kkn", "t_as"], writes=["t_es"])
            self.I(DVE, "scalar_tensor_tensor", T["rk"][:, 0:W], zr[:, 0:W], col(V_RK), T["km"][:, 0:W], ALU.mult, ALU.mult, reads=["t_zr", "t_km", "pv"], writes=["t_cum"])
            ps, pk = self.psum()
            self.bsum(ps, pk, T, "rk", W)
            self.I(DVE, "tensor_tensor", bon[:, qg, cw], ps[:, 0:W], zv[:, 0:W], ALU.mult, reads=[pk, "t_zv"], writes=["bon%d" % qg])
            self.I(ACT, "activation", bon[:, qg, cw], bon[:, qg, cw], AF.Identity, bias=col(V_LB), reads=["bon%d" % qg, "pv"], writes=["bon%d" % qg])
            ps, pk = self.psum()
            self.I(PE, "matmul", ps[:, 0:W], self.g2a[:, qs], sga[:, cw], start=True, stop=False, reads=["g2a", "sga"], writes=[pk], track=False)
            self.I(PE, "matmul", ps[:, 0:W], self.g2b[:, qs], sgb[:, cw], start=False, stop=True, reads=["g2b", "sgb"], writes=[pk])
            self.I(ACT, "copy", gst[:, qg, cw], ps[:, 0:W], reads=[pk], writes=["gst%d" % qg])
            for h in range(2):
                pr = slice(64 * h, 64 * h + 64)
                cs = slice(64 * h, 64 * h + 64)
                v3 = lambda t: t[pr, 0:W].rearrange("p (c t) -> p c t", t=CH)
                e1 = DVE if h == 0 else POOL
                cks = slice(ck0, ck0 + nck)
                self.I(DVE, "scalar_tensor_tensor", ARBD[pr, qg, cks, 0, cs], v3(T["kkn"]), -1.0, v3(T["pa"]), ALU.mult, ALU.mult, reads=["t_kkn", "t_pa"], writes=["ARBD%d" % qg])
                self.I(e1, "tensor_tensor", ARBD[pr, qg, cks, 1, cs], v3(zr), v3(T["prr"]), ALU.mult, reads=["t_zr", "t_prr"], writes=["ARBD%d" % qg])
                self.I(e1, "tensor_tensor", KBD[pr, qg, cks, cs], v3(T["km"]), v3(T["pinv"]), ALU.mult, reads=["t_km", "t_pinv"], writes=["KBD%d" % qg])
                self.I(e1, "tensor_tensor", BBD[pr, qg, cks, cs], v3(T["ta"]), v3(T["pinv"]), ALU.mult, reads=["t_es", "t_pinv"], writes=["BBD%d" % qg])
                self.I(ACT, "copy", VBD[pr, qg, cks, cs], v3(zv), reads=["t_zv"], writes=["VBD%d" % qg])

    def scan_pre(self, g, c, gc, sc, ARBD, BBD, KBD, VBD):
        cf, cb = self.cf, self.cb
        s = gc % 2
        kin = ["ARBD%d" % q for q in range(4)] + ["BBD%d" % q for q in range(4)] + ["KBD%d" % q for q in range(4)] + ["VBD%d" % q for q in range(4)]
        NA1, NA2 = sc["NA1", s], sc["NA2", s]
        kNA1, kNA2 = "NA1_%d" % s, "NA2_%d" % s
        identb = cb[:, 0:128]
        iselb = cb[:, C_ISEL:C_ISEL + 64]
        mask2 = cf[:, C_MUS:C_MUS + 256].unsqueeze(1).to_broadcast([128, 2, 256])
        for (dst, kd, L) in ((NA1, kNA1, BBD), (NA2, kNA2, KBD)):
            for hf in range(2):
                ps, pk = self.psum()
                for j in range(2):
                    q = 2 * hf + j
                    self.I(PE, "matmul", ps[:, j * 256:(j + 1) * 256], L[:, q, c, :], ARBD[:, q, c, :, :].rearrange("p a b -> p (a b)"),
                           start=True, stop=True, reads=kin, writes=[pk], track=(j == 1))
                self.I(DVE, "tensor_tensor", dst[:, 2 * hf:2 * hf + 2, :], ps[:].rearrange("p (a b) -> p a b", b=256), mask2, ALU.mult, reads=[pk, "cf"], writes=[kd])
                yield
        NL0, kNL0 = sc["NL", 0, 0], "NL_00"
        ps, pk = self.psum()
        for q in range(4):
            self.I(PE, "matmul", ps[:, q * 128:(q + 1) * 128], ARBD[:, q, c, 0, :], BBD[:, q, c, :], start=True, stop=True, reads=kin, writes=[pk], track=(q == 3))
        mL = cf[:, C_MLS:C_MLS + 128].unsqueeze(1).to_broadcast([128, 4, 128])
        self.I(DVE, "tensor_tensor", NL0[:], ps[:].rearrange("p (a b) -> p a b", b=128), mL, ALU.mult, reads=[pk, "cf"], writes=[kNL0])
        yield
        B2, K2, V2 = sc["B2", s], sc["K2", s], sc["V2", s]
        for (dst, kd, L) in ((B2, "B2_%d" % s, BBD), (K2, "K2_%d" % s, KBD)):
            ps, pk = self.psum()
            for q in range(4):
                self.I(PE, "matmul", ps[:, q * 128:(q + 1) * 128], L[:, q, c, :], identb, start=True, stop=True, reads=kin + ["cb"], writes=[pk], track=(q == 3))
            self.I(ACT, "copy", dst[:], ps[:].rearrange("p (a b) -> p a b", b=128), reads=[pk], writes=[kd])
            yield
        ps, pk = self.psum()
        for q in range(4):
            self.I(PE, "matmul", ps[:, q * 64:(q + 1) * 64], VBD[:, q, c, :], iselb, start=True, stop=True, reads=kin + ["cb"], writes=[pk], track=(q == 3))
        self.I(ACT, "copy", V2[:], ps[:, 0:256].rearrange("p (a b) -> p a b", b=64), reads=[pk], writes=["V2_%d" % s])
        yield
        TUc, kTU = NA1[:, :, 0:128], kNA1
        NLc, kNL = NL0, kNL0
        Qc, kQ = sc["Q", s, 0], "Q_%d0" % s
        idb4 = identb.unsqueeze(1).to_broadcast([128, 4, 128])
        self.I(POOL, "tensor_tensor", Qc[:], NA1[:, :, 0:128], idb4, ALU.add, reads=[kNA1, "cb"], writes=[kQ])
        for lvl in range(1, 6):
            i = lvl % 2
            NLn, kNLn = sc["NL", 0, i], "NL_0%d" % i
            TUn, kTUn = sc["TU", 0, i], "TU_0%d" % i
            Qn, kQn = sc["Q", s, i], "Q_%d%d" % (s, i)
            psN, pkN = self.psum()
            for q in range(4):
                self.I(PE, "matmul", psN[:, q * 128:(q + 1) * 128], TUc[:, q, :], NLc[:, q, :], start=True, stop=True, reads=[kTU, kNL], writes=[pkN], track=(q == 3))
            if lvl < 5:
                psT, pkT = self.psum()
                for q in range(4):
                    self.I(PE, "matmul", psT[:, q * 128:(q + 1) * 128], NLc[:, q, :], TUc[:, q, :], start=True, stop=True, reads=[kTU, kNL], writes=[pkT], track=(q == 3))
            self.I(ACT, "copy", NLn[:], psN[:].rearrange("p (a b) -> p a b", b=128), reads=[pkN], writes=[kNLn])
            if lvl < 5:
                self.I(DVE, "tensor_copy", TUn[:], psT[:].rearrange("p (a b) -> p a b", b=128), reads=[pkT], writes=[kTUn])
            yield
            psQ, pkQ = self.psum()
            for q in range(4):
                self.I(PE, "matmul", psQ[:, q * 128:(q + 1) * 128], NLn[:, q, :], Qc[:, q, :], start=True, stop=True, reads=[kNLn, kQ], writes=[pkQ], track=(q == 3))
            self.I(DVE, "tensor_tensor", Qn[:], psQ[:].rearrange("p (a b) -> p a b", b=128), Qc[:], ALU.add, reads=[pkQ, kQ], writes=[kQn])
            yield
            TUc, kTU, NLc, kNL, Qc, kQ = TUn, kTUn, NLn, kNLn, Qn, kQn

    def scan_chain(self, g, c, gc, sc, ARBD, PCs, bon, gst, Hf, Hb, tmpH):
        s = gc % 2
        kin = ["ARBD%d" % q for q in range(4)]
        NA1, NA2 = sc["NA1", s], sc["NA2", s]
        kNA1, kNA2 = "NA1_%d" % s, "NA2_%d" % s
        B2, K2, V2 = sc["B2", s], sc["K2", s], sc["V2", s]
        kB2, kK2, kV2 = "B2_%d" % s, "K2_%d" % s, "V2_%d" % s
        Qc, kQ = sc["Q", s, 1], "Q_%d1" % s
        kH = "Hf%d" % g
        kHb = "Hb%d" % g
        X2, U2 = sc["X2", 0], sc["U2", 0]
        ps, pk = self.psum()
        for q in range(4):
            self.I(PE, "matmul", ps[:, q * 64:(q + 1) * 64], ARBD[:, q, c, 0, :], Hb[:, q, :], start=True, stop=False, reads=kin + [kHb], writes=[pk], track=False)
            self.I(PE, "matmul", ps[:, q * 64:(q + 1) * 64], NA2[:, q, 0:128], V2[:, q, :], start=False, stop=True, reads=[kNA2, kV2], writes=[pk], track=(q == 3))
        self.I(ACT, "copy", X2[:], ps[:, 0:256].rearrange("p (a b) -> p a b", b=64), reads=[pk], writes=["X2_0"])
        yield
        ps, pk = self.psum()
        for q in range(4):
            self.I(PE, "matmul", ps[:, q * 64:(q + 1) * 64], Qc[:, q, :], X2[:, q, :], start=True, stop=True, reads=[kQ, "X2_0"], writes=[pk], track=(q == 3))
        self.I(ACT, "copy", U2[:], ps[:, 0:256].rearrange("p (a b) -> p a b", b=64), reads=[pk], writes=["U2_0"])
        yield
        need_y = gc >= OWN0 // CH
        if need_y:
            psY, pkY = self.psum()
            for q in range(4):
                self.I(PE, "matmul", psY[:, q * 64:(q + 1) * 64], ARBD[:, q, c, 1, :], Hb[:, q, :], start=True, stop=False, reads=kin + [kHb], writes=[pkY], track=False)
                self.I(PE, "matmul", psY[:, q * 64:(q + 1) * 64], NA1[:, q, 128:256], U2[:, q, :], start=False, stop=False, reads=[kNA1, "U2_0"], writes=[pkY], track=False)
                self.I(PE, "matmul", psY[:, q * 64:(q + 1) * 64], NA2[:, q, 128:256], V2[:, q, :], start=False, stop=True, reads=[kNA2, kV2], writes=[pkY], track=(q == 3))
        ps, pk = self.psum()
        for q in range(4):
            self.I(PE, "matmul", ps[:, q * 64:(q + 1) * 64], B2[:, q, :], U2[:, q, :], start=True, stop=False, reads=[kB2, "U2_0"], writes=[pk], track=False)
            self.I(PE, "matmul", ps[:, q * 64:(q + 1) * 64], K2[:, q, :], V2[:, q, :], start=False, stop=True, reads=[kK2, kV2], writes=[pk], track=(q == 3))
        self.I(DVE, "tensor_tensor", tmpH[:], Hf[:], ps[:, 0:256].rearrange("p (a b) -> p a b", b=64), ALU.add, reads=[pk, kH], writes=["tmpH"])
        pcb = PCs[:, :, c:c + 1].to_broadcast([128, 4, 64])
        self.I(DVE, "tensor_tensor", Hf[:], tmpH[:], pcb, ALU.mult, reads=["tmpH"] + ["PCs%d" % q for q in range(4)], writes=[kH])
        self.I(ACT, "copy", Hb[:], Hf[:], reads=[kH], writes=[kHb])
        yield
        if need_y:
            yield from self.y_post(g, c, gc, s, sc, psY, pkY, bon, gst)

    def scan_group(self, g, blk, NCK, sc, ARBD, BBD, KBD, VBD, PCs, bon, gst, Hf, Hb, tmpH):
        def drain(gen):
            for _ in gen:
                pass

        def interleave(ga, gb):
            a_live = b_live = True
            while a_live or b_live:
                if a_live:
                    try:
                        next(ga)
                    except StopIteration:
                        a_live = False
                if b_live:
                    try:
                        next(gb)
                    except StopIteration:
                        b_live = False
        gc0 = blk * NCK
        drain(self.scan_pre(g, 0, gc0, sc, ARBD, BBD, KBD, VBD))
        for c in range(NCK):
            ch = self.scan_chain(g, c, gc0 + c, sc, ARBD, PCs, bon, gst, Hf, Hb, tmpH)
            if c + 1 < NCK:
                interleave(ch, self.scan_pre(g, c + 1, gc0 + c + 1, sc, ARBD, BBD, KBD, VBD))
            else:
                drain(ch)

    def y_post(self, g, c, gc, s, sc, psY, pkY, bon, gst):
        S = self.S
        pv, cb = self.pv, self.cb
        ys, yq, st, yt, YBD = sc["ys", s], sc["yq", s], sc["st", s], sc["yt", s], sc["YBD", s]
        kys, kyq, kst, kyt, kY = "ys_0", "yq_0", "yst_0", "yt_0", "YBD_0"
        self.I(ACT, "copy", ys[:], psY[:, 0:256].rearrange("p (a b) -> p a b", b=64), reads=[pkY], writes=[kys])
        self.I(DVE, "tensor_reduce", st[:, 0, :], ys[:], AX.X, ALU.add, reads=[kys], writes=[kst])
        self.I(POOL, "tensor_tensor", yq[:], ys[:], ys[:], ALU.mult, reads=[kys], writes=[kyq])
        self.I(DVE, "tensor_reduce", st[:, 1, :], yq[:], AX.X, ALU.add, reads=[kyq], writes=[kst])
        yield
        self.I(DVE, "tensor_scalar", st[:, 2, :], st[:, 0, :], 1.0 / 64, None, ALU.mult, reads=[kst], writes=[kst])
        self.I(DVE, "tensor_tensor", st[:, 3, :], st[:, 2, :], st[:, 2, :], ALU.mult, reads=[kst], writes=[kst])
        self.I(DVE, "scalar_tensor_tensor", st[:, 4, :], st[:, 1, :], 1.0 / 64, st[:, 3, :], ALU.mult, ALU.subtract, reads=[kst], writes=[kst])
        self.I(DVE, "tensor_scalar", st[:, 4, :], st[:, 4, :], 64e-5, None, ALU.add, reads=[kst], writes=[kst])
        self.I(ACT, "activation", st[:, 5, :], st[:, 4, :], AF.Sqrt, reads=[kst], writes=[kst])
        self.I(DVE, "reciprocal", st[:, 6, :], st[:, 5, :], reads=[kst], writes=[kst])
        yield
        self.I(DVE, "tensor_tensor", yq[:], ys[:], st[:, 2, :].unsqueeze(2).to_broadcast([128, 4, 64]), ALU.subtract, reads=[kys, kst], writes=[kyq])
        for h in range(2):
            pr = slice(64 * h, 64 * h + 64)
            self.I(DVE if h == 0 else POOL, "tensor_tensor", YBD[pr, :, pr], yq[pr, :, :], st[pr, 6, :].unsqueeze(2).to_broadcast([64, 4, 64]), ALU.mult, reads=[kyq, kst], writes=[kY])
        yield
        ps, pk = self.psum()
        for q in range(4):
            self.I(PE, "matmul", ps[:, q * 64:(q + 1) * 64], YBD[:, q, :], cb[:, C_ISEL:C_ISEL + 64], start=True, stop=True, reads=[kY, "cb"], writes=[pk], track=(q == 3))
        t0 = 0
        col0 = gc * CH - OWN0
        if col0 < 0:
            t0 = -col0
            col0 = 0
        nt = CH - t0
        cl = c * CH + t0
        lw = pv[:, V_LW + 4 * g: V_LW + 4 * g + 4].unsqueeze(2).to_broadcast([128, 4, nt])
        psv = ps[:, 0:256].rearrange("p (a b) -> p a b", b=64)[:, :, t0:CH]
        self.I(DVE, "tensor_tensor", yt[:, :, 0:nt], psv, lw, ALU.mult, reads=[pk, "pv"], writes=[kyt])
        self.I(POOL, "tensor_tensor", yt[:, :, 0:nt], yt[:, :, 0:nt], bon[:, :, cl:cl + nt], ALU.add, reads=[kyt] + ["bon%d" % q for q in range(4)], writes=[kyt])
        self.I(DVE, "tensor_tensor", self.yaT[:, 4 * g:4 * g + 4, col0:col0 + nt], yt[:, :, 0:nt], gst[:, :, cl:cl + nt], ALU.mult, reads=[kyt] + ["gst%d" % q for q in range(4)], writes=["yaT"])


def core_inputs(inp, c, pv_base):
    b, p = c // 2, c % 2
    x = inp["x_prompt"][b]
    if p == 1:
        xseq = np.ascontiguousarray(x)
    else:
        xseq = np.concatenate([np.zeros((1024, D), np.float32), x[:1024]], axis=0)
    pa = permA()
    return {
        "xseq": xseq,
        "xsamp": np.ascontiguousarray(inp["x_sample"][16 * c:16 * c + 16].reshape(NSM, D)),
        "pvec": core_pvec(pv_base, p),
        "st_pool": np.ascontiguousarray(inp["state_pool"][0, 16 * c:16 * c + 16]),
        "st_conv": np.ascontiguousarray(inp["state_conv"][0, 16 * c:16 * c + 16]),
        "st_shift": np.ascontiguousarray(inp["state_shift"][0, 16 * c:16 * c + 16, 0][:, pa]),
        "st_wkv": np.ascontiguousarray(inp["state_wkv"][0, 16 * c:16 * c + 16].reshape(256, 4096)),
    }


def shared_inputs(inp):
    w_in = inp["w_in"][0]
    wG = np.empty((D, 16, 256), np.float32)
    wG[:, :, 0:128] = w_in[:, 4384:6432].reshape(D, 16, 128)
    wG[:, :, 128:256] = w_in[:, 6432:8480].reshape(D, 16, 128)
    wfi = inp["w_ffn_in"][0]
    wF = np.empty((D, NJ, 256), np.float32)
    wF[:, :, 0:128] = wfi[:, 0:DFF].reshape(D, NJ, 128)
    wF[:, :, 128:256] = wfi[:, DFF:2 * DFF].reshape(D, NJ, 128)
    gpost = np.empty((128, 2, D), np.float32)
    gpost[:, 0, :] = inp["norm_post_mix"][0][None, :]
    gpost[:, 1, :] = inp["norm_post_ffn"][0][None, :]
    return {
        "wA": np.ascontiguousarray(w_in[:, permA()]),
        "consts": host_consts(),
        "w2": np.ascontiguousarray(inp["w2"][0]),
        "a2": np.ascontiguousarray(inp["a2"][0]),
        "g2": np.ascontiguousarray(inp["g2"][0]),
        "wP": np.ascontiguousarray(w_in[:, 3360:4384]),
        "wG": wG,
        "wa": np.ascontiguousarray(inp["w_branch_a"][0]),
        "wbr": np.ascontiguousarray(inp["w_branch_b"][0]),
        "poolw": np.ascontiguousarray(inp["pool_w"][0]),
        "wout": np.ascontiguousarray(inp["w_out"][0]),
        "gpost": gpost,
        "wF": wF,
        "wfo": np.ascontiguousarray(inp["w_ffn_out"][0]),
    }


_NC_CACHE = {}


def get_nc(dbg=(), stages="SABC"):
    key = (tuple(sorted(dbg)), stages)
    if key not in _NC_CACHE:
        B = Builder(dbg=set(dbg), stages=stages)
        nc = B.build()
        _NC_CACHE[key] = (nc, B)
    return _NC_CACHE[key]


def run_cores(inp, cores, dbg=(), stages="SABC", trace=False):
    nc, B = get_nc(dbg, stages)
    sh = shared_inputs(inp)
    pv_base = host_pvec(inp)
    names = set(B.ins.keys())
    maps = []
    for c in cores:
        m = dict(sh)
        m.update(core_inputs(inp, c, pv_base))
        maps.append({k: v for k, v in m.items() if k in names})
    res = run_bass_kernel_spmd(nc, maps, core_ids=list(range(len(cores))), trace=trace)
    return res


def kernel(**inp):
    inp = {k: np.asarray(v) for k, v in inp.items()}
    res = run_cores(inp, list(range(8)))
    R = res.results
    pa = permA()
    y_prompt = np.empty((4, 2048, D), np.float32)
    y_sample = np.empty((128, 4, D), np.float32)
    p_shift = np.empty((1, 4, 1, DS), np.float32)
    p_wkv = np.empty((1, 4, 16, 64, 64), np.float32)
    p_pool = np.empty((1, 4, 15, 1024), np.float32)
    p_conv = np.empty((1, 4, 2, DFF), np.float32)
    s_shift = np.zeros((1, 128, 1, DS), np.float32)
    s_wkv = np.zeros((1, 128, 16, 64, 64), np.float32)
    s_pool = np.empty((1, 128, 15, 1024), np.float32)
    s_conv = np.empty((1, 128, 2, DFF), np.float32)
    for c in range(8):
        b, p = c // 2, c % 2
        r = R[c]
        y_prompt[b, p * 1024:(p + 1) * 1024] = r["o_y"][0:1024]
        y_sample[16 * c:16 * c + 16] = r["o_y"][1024:1024 + NSM].reshape(16, 4, D)
        if p == 1:
            p_shift[0, b, 0, pa] = r["o_pshift"]
            p_wkv[0, b] = r["o_pwkv"]
            p_pool[0, b] = r["o_ppool"]
            p_conv[0, b] = r["o_pconv"]
        s_pool[0, 16 * c:16 * c + 16] = r["o_spool"]
        s_conv[0, 16 * c:16 * c + 16] = r["o_sconv"]
        if "o_sshift" in r:
            s_shift[0, 16 * c:16 * c + 16, 0][:, pa] = r["o_sshift"]
            s_wkv[0, 16 * c:16 * c + 16] = r["o_swkv"].reshape(16, 16, 64, 64)
    return (y_prompt, y_sample, p_shift, p_wkv, p_pool, p_conv, s_shift, s_wkv, s_pool, s_conv)
```

```python
import os
import numpy as np
import concourse.bass as bass
import concourse.mybir as mybir
from contextlib import ExitStack
from concourse.bass_utils import run_bass_kernel_spmd

F32 = mybir.dt.float32
BF16 = mybir.dt.bfloat16
AF = mybir.ActivationFunctionType
ALU = mybir.AluOpType
AX = mybir.AxisListType

PE, ACT, DVE, POOL, SP = "pe", "act", "dve", "pool", "sp"
NDMASEM = 12
VCLOCK = os.environ.get('VCLOCK', '1') == '1'
EMBED_WAIT = os.environ.get('EMBW', '1') == '1'
SAME_ENGINE_NOWAIT = os.environ.get('SENW', '0') == '1'


class Sched:
    def __init__(self, nc, es):
        self.nc = nc
        self.q = {e: [] for e in (PE, ACT, DVE, POOL, SP)}
        self.cnt = {e: 0 for e in (PE, ACT, DVE, POOL)}
        self.sem = {e: es.enter_context(nc.semaphore("s_" + e)) for e in (PE, ACT, DVE, POOL)}
        self.dsem = {}
        self.dcnt = {}
        for e in (SP, "spo", "bg", POOL):
            self.dsem[e] = [es.enter_context(nc.semaphore("d_%s%d" % (e, i))) for i in range(NDMASEM)]
            self.dcnt[e] = 0
        self.pending = {e: [] for e in (PE, ACT, DVE, POOL, SP)}
        self.tok_ring = {}
        self.clock = {}
        self.tclk = {}
        self.lastw = {}
        self.readers = {}
        self.all_tokens = {}

    def _deps(self, eng, reads, writes):
        toks = []
        for k in reads:
            w = self.lastw.get(k)
            if w is not None:
                toks.append(w)
        for k in writes:
            w = self.lastw.get(k)
            if w is not None:
                toks.append(w)
            toks.extend(self.readers.get(k, ()))
        clk = self.clock.setdefault(eng, {})
        need = {}
        for tok in toks:
            s, v, src = tok[0], tok[1], tok[2]
            if src == eng and eng == PE:
                continue
            if clk.get(s.name, 0) >= v:
                continue
            if need.get(s.name, (None, 0))[1] < v:
                need[s.name] = (s, v, self.tclk.get((s.name, v), {}))
        for (s, v) in self.pending[eng]:
            if clk.get(s.name, 0) >= v:
                continue
            if need.get(s.name, (None, 0))[1] < v:
                need[s.name] = (s, v, self.tclk.get((s.name, v), {}))
        self.pending[eng] = []
        if VCLOCK and len(need) > 1:
            drop = set()
            for name, (s, v, c) in need.items():
                for name2, (s2, v2, c2) in need.items():
                    if name2 != name and name2 not in drop and c2.get(name, 0) >= v:
                        drop.add(name)
                        break
            for name in drop:
                del need[name]
        out = []
        for name, (s, v, c) in need.items():
            if clk.get(name, 0) < v:
                clk[name] = v
            for n2, v2 in c.items():
                if clk.get(n2, 0) < v2:
                    clk[n2] = v2
            out.append((s, v))
        return out

    def _stamp(self, eng, tok):
        c = dict(self.clock.get(eng, {}))
        c[tok[0].name] = tok[1]
        self.tclk[(tok[0].name, tok[1])] = c

    def barrier(self):
        if self.frozen:
            return
        allw = []
        for e in (PE, ACT, DVE, POOL):
            if self.cnt[e] > 0:
                allw.append((self.sem[e], self.cnt[e], e))
        for name, (s, v) in self.all_tokens.items():
            if self.tok_ring.get(name) != "bg":
                allw.append((s, v, "dma"))
        for eng in (PE, ACT, DVE, POOL, SP):
            for (s, v, src) in allw:
                if src == eng:
                    continue
                self.pending[eng].append((s, v))

    def _commit(self, tok, reads, writes):
        for k in writes:
            self.lastw[k] = tok
            self.readers[k] = []
        for k in reads:
            lst = self.readers.setdefault(k, [])
            lst[:] = [t for t in lst if t[0].name != tok[0].name]
            lst.append(tok)

    frozen = False

    def op(self, eng, fn, reads=(), writes=(), track=True):
        if self.frozen:
            return None
        waits = self._deps(eng, reads, writes)
        tok = None
        if not track:
            assert eng == PE
            self.pend_r = getattr(self, "pend_r", set()) | set(reads)
        if track:
            self.cnt[eng] += 1
            tok = (self.sem[eng], self.cnt[eng], eng)
            self._stamp(eng, tok)
            if eng == PE and getattr(self, "pend_r", None):
                reads = list(set(reads) | self.pend_r)
                self.pend_r = set()
            self._commit(tok, reads, writes)
        self.q[eng].append((waits, fn, tok))
        return tok

    def dma(self, qeng, out, in_, reads=(), writes=(), ring=None, **kw):
        if self.frozen:
            return None
        waits = self._deps(qeng, reads, writes)
        rk = ring or qeng
        n = self.dcnt[rk]
        self.dcnt[rk] += 1
        s = self.dsem[rk][n % NDMASEM]
        v = 16 * (n // NDMASEM + 1)
        tok = (s, v, "dma")
        self._stamp(qeng, tok)
        self._commit(tok, reads, writes)
        self.q[qeng].append((waits, lambda e: e.dma_start(out=out, in_=in_, **kw), tok))
        self.all_tokens[s.name] = (s, v)
        self.tok_ring[s.name] = rk
        return tok

    def emit(self):
        nc = self.nc
        fin = dict(self.all_tokens)
        for e in (PE, ACT, DVE, POOL):
            if self.cnt[e] > 0:
                fin[self.sem[e].name] = (self.sem[e], self.cnt[e])
        q = self.q
        with nc.Block() as block:
            def run(eng_name):
                def body(e):
                    for (waits, fn, tok) in q[eng_name]:
                        emb = None
                        if EMBED_WAIT and waits:
                            emb = waits[-1]
                            waits = waits[:-1]
                        for (s, v) in waits:
                            e.wait_ge(s, v)
                        ins = fn(e)
                        if emb is not None:
                            ins._wait_ge(emb[0], emb[1])
                        if tok is not None:
                            ins.then_inc(tok[0], 16 if tok[2] == "dma" else 1)
                    if eng_name == SP:
                        for name, (s, v) in fin.items():
                            e.wait_ge(s, v)
                return body
            block.tensor(run(PE))
            block.scalar(run(ACT))
            block.vector(run(DVE))
            block.gpsimd(run(POOL))
            block.sync(run(SP))


D = 2048
DS = 3360
NPR = 2048
NSM = 64
OWN0 = 992
NOWN = 1120
BLK = 512
NH = 512
CH = 64
C_ID, C_BONES, C_ISEL, C_MUS, C_MUI, C_MLS, C_M01 = 0, 128, 256, 320, 448, 576, 704
NCONST = 704 + 512
V_MULR = 0
V_MU = 4
V_W0, V_A0, V_KK, V_KA, V_RK, V_LW, V_LB = 28, 36, 44, 52, 60, 68, 76
V_G1 = 84
V_PS = 100
V_G3 = 108
V_FLAG = 124
V_INVC = 125
V_CW = 189
NV = 189 + 176
DFF = 5632
NJ = 44
DECAY_C = 0.6065306597126334


def host_consts():
    c = np.zeros((128, NCONST), np.float32)
    p = np.arange(128)[:, None]
    j = np.arange(128)[None, :]
    c[:, C_ID:C_ID + 128] = (p == j)
    c[:, C_BONES:C_BONES + 128] = (p // 64 == j // 64)
    c[:, C_ISEL:C_ISEL + 64] = (p % 64 == np.arange(64)[None, :])
    c[:, C_MUS:C_MUS + 128] = (p % 64 < j % 64)
    c[:, C_MUI:C_MUI + 128] = (p % 64 <= j % 64)
    c[:, C_MLS:C_MLS + 128] = (p % 64 > j % 64)
    c[:, C_M01:C_M01 + 512] = (np.arange(512)[None, :] % 64 != 0)
    return c


def permA():
    idx = list(range(3072, 3360))
    for q in range(8):
        idx += list(range(q * 128, q * 128 + 128))
        idx += list(range(1024 + q * 128, 1024 + q * 128 + 128))
        idx += list(range(2048 + q * 128, 2048 + q * 128 + 128))
    return np.array(idx)


def host_pvec(inp):
    v = np.zeros((128, NV), np.float32)
    mu = inp["mu_shift"][0]
    v[0:64, 0] = mu[3072:3136]
    v[0:64, 1] = mu[3136:3200]
    v[0:128, 2] = mu[3200:3328]
    v[0:32, 3] = mu[3328:3360]
    for q in range(8):
        for t in range(3):
            v[:, V_MU + 3 * q + t] = mu[t * 1024 + q * 128: t * 1024 + q * 128 + 128]
    for (col, name) in ((V_W0, "w0"), (V_A0, "a0"), (V_KK, "k_k"), (V_KA, "k_a"), (V_RK, "r_k"), (V_LW, "lnx_w"), (V_LB, "lnx_b")):
        a = inp[name][0].reshape(-1)
        for q in range(8):
            v[:, col + q] = a[q * 128:(q + 1) * 128]
    g = inp["norm_pre_mix"][0]
    g3 = inp["norm_pre_ffn"][0]
    for k in range(16):
        v[:, V_G1 + k] = g[k * 128:(k + 1) * 128]
        v[:, V_G3 + k] = g3[k * 128:(k + 1) * 128]
    psc = inp["pool_scale"][0]
    for k in range(8):
        v[:, V_PS + k] = psc[k * 128:(k + 1) * 128]
    cw, cbias = inp["conv_w"][0], inp["conv_b"][0]
    for j in range(NJ):
        for t in range(3):
            v[:, V_CW + 4 * j + t] = cw[t, j * 128:(j + 1) * 128]
        v[:, V_CW + 4 * j + 3] = cbias[j * 128:(j + 1) * 128]
    return v


def core_pvec(base, p):
    v = base.copy()
    v[:, V_FLAG] = float(p)
    for gi, win in enumerate((2, 4, 8, 16)):
        for j in range(16):
            pos = p * 1024 + j
            v[:, V_INVC + gi * 16 + j] = 1.0 / min(win, pos + 1)
    return v


class _TAlias:
    def __init__(self, phys, alias, prefix="t_"):
        self.phys = phys
        self.alias = alias
        self.prefix = prefix

    def _n(self, n):
        return self.alias.get(n, n)

    def __getitem__(self, n):
        return self.phys[self._n(n)]

    def key(self, n):
        return self.prefix + self._n(n)


class StopBuild(Exception):
    pass


class Builder:
    stop_at = None

    def ckpt(self, name):
        if self.stop_at == name and not self.S.frozen:
            print("frozen at", name)
            self.S.frozen = True
            self.dbg = set()

    def __init__(self, dbg=None, stages="A"):
        self.dbg = dbg or set()
        self.stages = stages
        self.nc = bass.Bass("TRN2", target_bir_lowering=False)
        self.ins = {}
        self.outs = {}
        self.psn = 0

    def din(self, name, shape, dt=F32):
        t = self.nc.dram_tensor(name, list(shape), dt, kind="ExternalInput").ap()
        self.ins[name] = t
        return t

    def dout(self, name, shape, dt=F32):
        t = self.nc.dram_tensor(name, list(shape), dt, kind="ExternalOutput").ap()
        self.outs[name] = t
        return t

    def I(self, eng, meth, *args, reads=(), writes=(), track=True, **kw):
        return self.S.op(eng, lambda e: getattr(e, meth)(*args, **kw), reads=reads, writes=writes, track=track)

    def stage_out(self, name, tile_ap, shape, key):
        scr = self.nc.dram_tensor("scr_" + name, list(shape), F32).ap()
        self.S.dma(SP, scr, tile_ap, reads=[key], writes=["scr_" + name], ring="spo")
        return scr

    def bg(self, dst, src, name):
        self.S.dma(SP, dst, src, reads=["scr_" + name], ring="bg", allow_slow_non_contiguous=True)

    def sb(self, es, name, shape, dt=F32):
        return es.enter_context(self.nc.sbuf_tensor(name, list(shape), dt))

    def psum(self):
        i = self.psn % 8
        self.psn += 1
        return self.PS[i], "ps%d" % i

    def build(self):
        nc = self.nc
        xseq = self.xseq = self.din("xseq", [NPR, D])
        xsamp = self.xsamp = self.din("xsamp", [NSM, D])
        wA = self.din("wA", [D, DS])
        consts = self.din("consts", [128, NCONST])
        pvec = self.din("pvec", [128, NV])
        w2 = self.din("w2", [64, 1024])
        a2 = self.din("a2", [64, 1024])
        g2 = self.din("g2", [160, 1024])
        self.wP = self.din("wP", [D, 1024])
        self.wG = self.din("wG", [D, 16, 256])
        self.wa = self.din("wa", [1024, D])
        self.wbr = self.din("wbr", [1024, D])
        self.poolw = self.din("poolw", [4, 256, 256])
        self.wout = self.din("wout", [D, D])
        self.gpost = self.din("gpost", [128, 2, D])
        self.wF = self.din("wF", [D, NJ, 256])
        self.wfo = self.din("wfo", [DFF, D])
        self.st_pool = self.din("st_pool", [16, 15, 1024])
        self.st_conv = self.din("st_conv", [16, 2, DFF])
        self.o_pshift = self.dout("o_pshift", [DS])
        self.o_pwkv = self.dout("o_pwkv", [16, 64, 64])
        self.o_ppool = self.dout("o_ppool", [15, 1024])
        self.o_pconv = self.dout("o_pconv", [2, DFF])
        self.o_spool = self.dout("o_spool", [16, 15, 1024])
        self.o_sconv = self.dout("o_sconv", [16, 2, DFF])
        self.o_y = self.dout("o_y", [1024 + NSM, D])
        self.st_shift = self.din("st_shift", [16, DS])
        self.st_wkv = self.din("st_wkv", [256, 4096])
        self.o_sshift = self.dout("o_sshift", [16, DS])
        self.o_swkv = self.dout("o_swkv", [256, 4096])
        self.x1s = nc.dram_tensor("x1s", [NOWN, D], F32).ap()
        self.scrS = nc.dram_tensor("scrS", [NSM, 6, 1024], F32).ap()
        self.scrY = nc.dram_tensor("scrY", [NSM, 1024], F32).ap()
        self.scrH = nc.dram_tensor("scrH", [128, 16, NOWN], BF16).ap()
        if "ya" in self.dbg:
            self.o_ya = self.dout("d_ya", [128, 8, NOWN])
        with ExitStack() as es:
            S = self.S = Sched(nc, es)
            self.PS = [es.enter_context(nc.psum_tensor("ps%d" % i, [128, 512], F32)) for i in range(8)]
            cf = self.cf = self.sb(es, "cf", [128, NCONST])
            cb = self.cb = self.sb(es, "cb", [128, 320], BF16)
            pv = self.pv = self.sb(es, "pv", [128, NV])
            S.dma(SP, cf[:], consts, writes=["cf"])
            S.dma(SP, pv[:], pvec, writes=["pv"])
            S.dma(POOL, cb[:], consts[:, 0:320], writes=["cb"])
            self.wslot = 0
            bufX = self.bufX = self.sb(es, "bufX", [128, 16, NOWN], BF16)
            self.yaT = bufX[:, 0:8, :]
            self.ybT = bufX[:, 8:16, :]
            with ExitStack() as esw:
                self.wb = [self.sb(esw, "wb%d" % i, [128, 16, 384], BF16) for i in range(2)]
                self.wA_ap = wA
                with ExitStack() as esA:
                    w2b = self.w2b = self.sb(esA, "w2b", [64, 1024], BF16)
                    a2b = self.a2b = self.sb(esA, "a2b", [64, 1024], BF16)
                    g2a = self.g2a = self.sb(esA, "g2a", [128, 1024], BF16)
                    g2b = self.g2b = self.sb(esA, "g2b", [128, 1024], BF16)
                    S.dma(POOL, w2b[:], w2, writes=["w2b"])
                    S.dma(POOL, a2b[:], a2, writes=["a2b"])
                    S.dma(POOL, g2a[:], g2[0:128, :], writes=["g2a"])
                    self.I(POOL, "memset", g2b[:], 0.0, writes=["g2b"])
                    S.dma(POOL, g2b[0:32, :], g2[128:160, :], writes=["g2b"])
                    if "S" in self.stages:
                        self.stageS()
                    self.stageA(esA, xseq, wA)
                if "ya" in self.dbg:
                    with ExitStack() as esd:
                        yaf = self.sb(esd, "yaf", [128, 8, NOWN], F32)
                        self.I(DVE, "tensor_copy", yaf[:], self.yaT, reads=["yaT"], writes=["yaf"])
                        S.dma(SP, self.o_ya, yaf[:], reads=["yaf"], ring="spo")
                        S.barrier()
                if "B" in self.stages:
                    with ExitStack() as esm:
                        self.mT = self.sb(esm, "mT", [128, 16, NOWN], BF16)
                        with ExitStack() as esb:
                            self.hTo = self.sb(esb, "hTo", [128, 16, NOWN], BF16)
                            self.stageB12(esb)
                        self.stageB3(esm)
            if "C" in self.stages:
                self.stageC(es)
            S.emit()
        return nc

    def tok_groups(self):
        return [(0, 512), (512, 512), (1024, NOWN - 1024)]

    def stageB12(self, es_outer):
        S = self.S
        pv, cf, cb = self.pv, self.cf, self.cb
        hT = self.hTo
        with ExitStack() as es:
            sb = lambda n, s, d=F32: self.sb(es, n, s, d)
            self.xt = [sb("xtB", [128, D])]
            self.xb = [sb("xbB", [128, D], BF16)]
            self.xst = [sb("xstB", [128, 4])]
            for (c0, c1, k) in ((0, 32, "scrH_1"), (32, 544, "scrH_2"), (544, 1056, "scrH_3"), (1056, NOWN, "scrH_s")):
                S.dma(SP, hT[:, :, c0:c1], self.scrH[:, :, c0:c1], reads=[k], writes=["hTo"])
            self.ckpt("B0")
            ybT = self.ybT
            with ExitStack() as esp:
                sbp = lambda n, s, d=F32: self.sb(esp, n, s, d)
                pwb = sbp("pwb", [128, 4, 2, 256], BF16)
                S.dma(POOL, pwb[:], self.poolw.rearrange("g (k p) n -> p g k n", p=128), writes=["pwb"])
                phT = sbp("phT", [128, 8, 16, 15])
                for half in range(2):
                    sp_t = sbp("sp_t%d" % half, [120, 1024])
                    S.dma(SP, sp_t[:], self.st_pool[8 * half: 8 * half + 8].rearrange("b j c -> (b j) c"), writes=["sp_t%d" % half])
                    for c4 in range(2):
                        ps, pk = self.psum()
                        for ch in range(4):
                            self.I(PE, "transpose", ps[:, ch * 120:(ch + 1) * 120], sp_t[:, (c4 * 4 + ch) * 128:(c4 * 4 + ch + 1) * 128], cf[0:120, 0:120],
                                   reads=["sp_t%d" % half, "cf"], writes=[pk], track=(ch == 3))
                        self.I(DVE, "tensor_copy", phT[:, c4 * 4:c4 * 4 + 4, 8 * half:8 * half + 8, :],
                               ps[:, 0:480].rearrange("p (a b j) -> p a b j", a=4, b=8), reads=[pk], writes=["phT"])
                self.ckpt("B1a")
                S.dma(SP, self.o_spool[:, 0:11, :], self.st_pool[:, 4:15, :], ring="spo")
                zp = sbp("zp", [128, NOWN])
                pa_ = sbp("ppA", [128, NOWN])
                pb_ = sbp("ppB", [128, NOWN])
                dT = sbp("dT", [128, 8, NOWN], BF16)
                self.I(POOL, "memset", dT[:], 0.0, writes=["dT"])
                ppo = sbp("ppo", [128, 8, 15])
                spo = sbp("spo", [128, 8, 16, 4])
                bs = sbp("bs", [128, 16, 19])
                bs2 = sbp("bs2", [128, 16, 19])
                slabs = [(self.wP[:, ch * 128:(ch + 1) * 128], 128) for ch in range(8)]
                stream = self.slab_stream(slabs)
                NP_ = 1056
                for ch in range(8):
                    wt, wk = next(stream)
                    gi = ch // 2
                    win = 2 << gi
                    for (t0, tn) in self.tok_groups():
                        ps, pk = self.psum()
                        for k in range(16):
                            self.I(PE, "matmul", ps[:, 0:tn], wt[:, k, 0:128], hT[:, k, t0:t0 + tn], start=(k == 0), stop=(k == 15),
                                   reads=[wk, "hTo"], writes=[pk], track=(k == 15))
                        self.I(ACT, "copy", zp[:, t0:t0 + tn], ps[:, 0:tn], reads=[pk], writes=["zp"])
                    src, ksrc = zp, "zp"
                    bufs = [(pa_, "ppA"), (pb_, "ppB")]
                    for j in range(gi + 1):
                        sh = 1 << j
                        dst, kdst = bufs[j % 2]
                        self.I(DVE if j % 2 == 0 else POOL, "tensor_tensor", dst[:, sh:NP_], src[:, sh:NP_], src[:, 0:NP_ - sh], ALU.add,
                               reads=[ksrc], writes=[kdst])
                        src, ksrc = dst, kdst
                    lo = win - 1
                    self.I(DVE, "scalar_tensor_tensor", dT[:, ch, lo:NP_], src[:, lo:NP_], 1.0 / win, zp[:, lo:NP_], ALU.mult, ALU.subtract,
                           reads=[ksrc, "zp"], writes=["dT"])
                    other, kother = bufs[(gi + 1) % 2]
                    self.I(POOL, "tensor_tensor", other[:, 32:48], src[:, 32:48], pv[:, V_INVC + gi * 16: V_INVC + gi * 16 + 16], ALU.mult,
                           reads=[ksrc, "pv"], writes=[kother])
                    self.I(POOL, "tensor_tensor", dT[:, ch, 32:48], other[:, 32:48], zp[:, 32:48], ALU.subtract, reads=[kother, "zp"], writes=["dT"])
                    self.I(POOL, "tensor_copy", ppo[:, ch, :], zp[:, NP_ - 15:NP_], reads=["zp"], writes=["ppo"])
                    zps = zp[:, NP_:NOWN].rearrange("p (b t) -> p b t", t=4)
                    self.I(POOL, "tensor_copy", bs[:, :, 0:15], phT[:, ch, :, :], reads=["phT"], writes=["bs"])
                    self.I(POOL, "tensor_copy", bs[:, :, 15:19], zps, reads=["zp"], writes=["bs"])
                    self.I(POOL, "tensor_copy", spo[:, ch, :, :], zps, reads=["zp"], writes=["spo"])
                    ssrc, kss = bs, "bs"
                    sbufs = [(bs2, "bs2"), (bs, "bs")]
                    for j in range(gi + 1):
                        sh = 1 << j
                        dst, kdst = sbufs[j % 2]
                        self.I(DVE, "tensor_tensor", dst[:, :, sh:19], ssrc[:, :, sh:19], ssrc[:, :, 0:19 - sh], ALU.add, reads=[kss], writes=[kdst])
                        ssrc, kss = dst, kdst
                    self.I(DVE, "scalar_tensor_tensor", dT[:, ch, NP_:NOWN].rearrange("p (b t) -> p b t", t=4), ssrc[:, :, 15:19], 1.0 / win, zps,
                           ALU.mult, ALU.subtract, reads=[kss, "zp"], writes=["dT"])
                self.ckpt("B1b")
                scr = self.stage_out("ppo", ppo[:], [128, 8, 15], "ppo")
                for ch in range(8):
                    self.bg(self.o_ppool[:, ch * 128:(ch + 1) * 128].rearrange("j p -> p j"), scr[:, ch, :], "ppo")
                sprow = sbp("sprow", [NSM, 1024])
                for c4 in range(2):
                    ps, pk = self.psum()
                    for ch in range(4):
                        self.I(PE, "transpose", ps[0:NSM, ch * 128:(ch + 1) * 128], spo[:, c4 * 4 + ch, :, :].rearrange("p b t -> p (b t)"), cf[:, C_ID:C_ID + 128],
                               reads=["spo", "cf"], writes=[pk], track=(ch == 3))
                    self.I(DVE, "tensor_copy", sprow[:, c4 * 512:(c4 + 1) * 512], ps[0:NSM, :], reads=[pk], writes=["sprow"])
                for b in range(16):
                    S.dma(SP, self.o_spool[b, 11:15, :], sprow[4 * b:4 * b + 4, :], reads=["sprow"], ring="spo")
                self.ckpt("B1c")
                for oc in range(8):
                    gi, o2 = oc // 2, oc % 2
                    for (t0, tn) in self.tok_groups():
                        ps, pk = self.psum()
                        for kk in range(2):
                            self.I(PE, "matmul", ps[:, 0:tn], pwb[:, gi, kk, o2 * 128:(o2 + 1) * 128], dT[:, 2 * gi + kk, t0:t0 + tn],
                                   start=(kk == 0), stop=(kk == 1), reads=["pwb", "dT"], writes=[pk], track=(kk == 1))
                        self.I(ACT, "activation", ybT[:, oc, t0:t0 + tn], ps[:, 0:tn], AF.Copy, scale=pv[:, V_PS + oc: V_PS + oc + 1],
                               reads=[pk, "pv"], writes=["ybT"])
                S.barrier()
            self.ckpt("B1d")
            mT = self.mT
            tmp = [sb("b2t%d" % i, [128, 512]) for i in range(4)]
            njs = 16

            def issue(j):
                slot = self.wslot % 2
                self.wslot += 1
                key = "wb%d" % slot
                w = self.wb[slot]
                wflat = w[:].rearrange("p a b -> p (a b)")
                S.dma(POOL, wflat[:, 0:4096].rearrange("p (k n) -> p k n", n=256), self.wG[:, j, :].rearrange("(k p) n -> p k n", p=128), writes=[key])
                S.dma(POOL, wflat[:, 4096:5120].rearrange("p (k n) -> p k n", n=128), self.wa[:, j * 128:(j + 1) * 128].rearrange("(k p) n -> p k n", p=128), writes=[key])
                S.dma(POOL, wflat[:, 5120:6144].rearrange("p (k n) -> p k n", n=128), self.wbr[:, j * 128:(j + 1) * 128].rearrange("(k p) n -> p k n", p=128), writes=[key])
                return (wflat, key)
            cur = issue(0)
            for j in range(njs):
                nxt = issue(j + 1) if j + 1 < njs else None
                wflat, wk = cur
                wg = wflat[:, 0:4096].rearrange("p (k n) -> p k n", n=256)
                wa_ = wflat[:, 4096:5120].rearrange("p (k n) -> p k n", n=128)
                wb_ = wflat[:, 5120:6144].rearrange("p (k n) -> p k n", n=128)
                for (t0, tn) in self.tok_groups():
                    psa, pka = self.psum()
                    psb, pkb = self.psum()
                    ppa, pkpa = self.psum()
                    ppb, pkpb = self.psum()
                    for k in range(16):
                        self.I(PE, "matmul", psa[:, 0:tn], wg[:, k, 0:128], hT[:, k, t0:t0 + tn], start=(k == 0), stop=(k == 15),
                               reads=[wk, "hTo"], writes=[pka], track=(k == 15))
                    for k in range(16):
                        self.I(PE, "matmul", psb[:, 0:tn], wg[:, k, 128:256], hT[:, k, t0:t0 + tn], start=(k == 0), stop=(k == 15),
                               reads=[wk, "hTo"], writes=[pkb], track=(k == 15))
                    for k in range(8):
                        self.I(PE, "matmul", ppa[:, 0:tn], wa_[:, k, :], self.yaT[:, k, t0:t0 + tn], start=(k == 0), stop=(k == 7),
                               reads=[wk, "yaT"], writes=[pkpa], track=(k == 7))
                    for k in range(8):
                        self.I(PE, "matmul", ppb[:, 0:tn], wb_[:, k, :], ybT[:, k, t0:t0 + tn], start=(k == 0), stop=(k == 7),
                               reads=[wk, "ybT"], writes=[pkpb], track=(k == 7))
                    self.I(ACT, "activation", tmp[0][:, 0:tn], psa[:, 0:tn], AF.Sigmoid, reads=[pka], writes=["b2t0"])
                    self.I(ACT, "activation", tmp[1][:, 0:tn], psb[:, 0:tn], AF.Sigmoid, reads=[pkb], writes=["b2t1"])
                    self.I(DVE, "tensor_tensor", tmp[2][:, 0:tn], tmp[0][:, 0:tn], ppa[:, 0:tn], ALU.mult, reads=["b2t0", pkpa], writes=["b2t2"])
                    self.I(DVE, "tensor_tensor", tmp[3][:, 0:tn], tmp[1][:, 0:tn], ppb[:, 0:tn], ALU.mult, reads=["b2t1", pkpb], writes=["b2t3"])
                    self.I(POOL, "tensor_tensor", mT[:, j, t0:t0 + tn], tmp[2][:, 0:tn], tmp[3][:, 0:tn], ALU.add, reads=["b2t2", "b2t3"], writes=["mT"])
                cur = nxt
            S.barrier()

    def tok_tiles(self):
        self.ckpt("B2")
        return [(0, 32)] + [(32 + 128 * i, 128) for i in range(8)] + [(1056, NSM)]

    def stageB3(self, es_outer):
        S = self.S
        pv, cf, cb = self.pv, self.cf, self.cb
        mT, h2T = self.mT, self.bufX
        with ExitStack() as es:
            sb = lambda n, s, d=F32: self.sb(es, n, s, d)
            woutb = sb("woutb", [128, 16, D], BF16)
            for n in range(4):
                S.dma(POOL, woutb[:, :, n * 512:(n + 1) * 512], self.wout[:, n * 512:(n + 1) * 512].rearrange("(k p) n -> p k n", p=128), writes=["woutb%d" % n])
            gp = sb("gp", [128, D])
            S.dma(SP, gp[:], self.gpost[:, 0, :], writes=["gp"])
            mo = sb("mo", [128, D])
            xt = sb("xt3", [128, D])
            x1 = sb("x1t", [128, D])
            xb = sb("xb3", [128, D], BF16)
            st = sb("st3", [128, 16])
            for (c0, nt) in self.tok_tiles():
                xrows = self.xseq[OWN0 + c0: OWN0 + c0 + nt, :] if c0 < 1056 else self.xsamp
                S.dma(SP, xt[0:nt, :], xrows, writes=["xt3"])
                self.I(POOL, "memset", st[:, 0:4], 0.0, writes=["st3"])
                self.ckpt("B3pre")
                for n in range(4):
                    ps, pk = self.psum()
                    for k in range(16):
                        self.I(PE, "matmul", ps[0:nt, :], mT[:, k, c0:c0 + nt], woutb[:, k, n * 512:(n + 1) * 512], start=(k == 0), stop=(k == 15),
                               reads=["mT", "woutb%d" % n], writes=[pk], track=(k == 15))
                    self.I(DVE, "tensor_copy", mo[0:nt, n * 512:(n + 1) * 512], ps[0:nt, :], reads=[pk], writes=["mo"])
                    self.I(ACT, "activation", xb[0:nt, n * 512:(n + 1) * 512], mo[0:nt, n * 512:(n + 1) * 512], AF.Square, accum_out=st[0:nt, n:n + 1], reads=["mo"], writes=["xb3", "st3"])
                self.ckpt("B3a")
                self.I(DVE, "tensor_reduce", st[0:nt, 4:5], st[0:nt, 0:4], AX.X, ALU.add, reads=["st3"], writes=["st3"])
                self.I(DVE, "tensor_scalar", st[0:nt, 5:6], st[0:nt, 4:5], 1.0 / D, 1e-6, ALU.mult, ALU.add, reads=["st3"], writes=["st3"])
                self.I(ACT, "activation", st[0:nt, 6:7], st[0:nt, 5:6], AF.Sqrt, reads=["st3"], writes=["st3"])
                self.I(DVE, "reciprocal", st[0:nt, 7:8], st[0:nt, 6:7], reads=["st3"], writes=["st3"])
                self.I(DVE, "scalar_tensor_tensor", mo[0:nt, :], mo[0:nt, :], st[0:nt, 7:8], gp[0:nt, :], ALU.mult, ALU.mult, reads=["mo", "st3", "gp"], writes=["mo"])
                self.I(POOL, "tensor_tensor", x1[0:nt, :], mo[0:nt, :], xt[0:nt, :], ALU.add, reads=["mo", "xt3"], writes=["x1t"])
                self.ckpt("B3b")
                S.dma(SP, self.x1s[c0:c0 + nt, :], x1[0:nt, :], reads=["x1t"], writes=["x1s"])
                self.ckpt("B3c")
                self.norm_T(x1, "x1t", nt, h2T[:, :, c0:c0 + nt], "h2T", xb, "xb3", st, "st3", 8, V_G3)
                self.ckpt("B3d")
            S.barrier()

    def norm_T(self, xt, kx, ntok, hT, hkey, xb, kb, st, ks, sc0, gbase):
        self.I(POOL, "memset", st[:, sc0:sc0 + 1], 0.0, writes=[ks])
        self.I(ACT, "activation", xb[0:ntok, :], xt[0:ntok, :], AF.Square, accum_out=st[0:ntok, sc0:sc0 + 1], reads=[kx], writes=[kb, ks])
        self.I(DVE, "tensor_scalar", st[0:ntok, sc0 + 1:sc0 + 2], st[0:ntok, sc0:sc0 + 1], 1.0 / D, 1e-6, ALU.mult, ALU.add, reads=[ks], writes=[ks])
        self.I(ACT, "activation", st[0:ntok, sc0 + 2:sc0 + 3], st[0:ntok, sc0 + 1:sc0 + 2], AF.Sqrt, reads=[ks], writes=[ks])
        self.I(DVE, "reciprocal", st[0:ntok, sc0 + 3:sc0 + 4], st[0:ntok, sc0 + 2:sc0 + 3], reads=[ks], writes=[ks])
        self.I(ACT, "activation", xb[0:ntok, :], xt[0:ntok, :], AF.Copy, scale=st[0:ntok, sc0 + 3:sc0 + 4], reads=[kx, ks, kb], writes=[kb])
        for half in range(2):
            ps, pk = self.psum()
            pT = ps[:].bitcast(BF16).rearrange("p (a b) -> p a b", b=128)
            for k8 in range(8):
                kc = half * 8 + k8
                self.I(PE, "transpose", pT[:, k8, 0:ntok], xb[0:ntok, kc * 128:(kc + 1) * 128], self.cb[0:ntok, 0:ntok],
                       reads=[kb, "cb"], writes=[pk], track=(k8 == 7))
            gcol = self.pv[:, gbase + half * 8: gbase + half * 8 + 8].unsqueeze(2).to_broadcast([128, 8, ntok])
            self.I(DVE, "tensor_tensor", hT[:, half * 8:half * 8 + 8, :], pT[:, :, 0:ntok], gcol, ALU.mult, reads=[pk, "pv"], writes=[hkey])

    def stageC(self, es_outer):
        S = self.S
        pv, cf, cb = self.pv, self.cf, self.cb
        h2T = self.bufX
        NA = 1024 + NSM
        with ExitStack() as es:
            sb = lambda n, s, d=F32: self.sb(es, n, s, d)
            actT = sb("actT", [128, NJ, NA], BF16)
            with ExitStack() as es1:
                sb1 = lambda n, s, d=F32: self.sb(es1, n, s, d)
                wf = [sb1("wf%d" % i, [128, 16, 256], BF16) for i in range(2)]
                gt = [sb1("gt%d" % i, [128, NOWN]) for i in range(2)]
                up = [sb1("up%d" % i, [128, NOWN]) for i in range(2)]
                cv = sb1("cv", [128, NA])
                ge = sb1("ge", [128, NA])
                gs6 = sb1("gs6", [128, 16, 6])
                chT = sb1("chT", [128, NJ, 32])
                pco = sb1("pco", [128, NJ, 2])
                sco = sb1("sco", [128, NJ, 16, 2])
                ct = sb1("ct", [32, 1408])
                stc = self.st_conv.rearrange("b r c -> (b r) c")
                for pc in range(4):
                    S.dma(SP, ct[:], stc[:, pc * 1408:(pc + 1) * 1408], writes=["ct"])
                    ps, pk = self.psum()
                    for jj in range(11):
                        self.I(PE, "transpose", ps[:, jj * 32:(jj + 1) * 32], ct[:, jj * 128:(jj + 1) * 128], cf[0:32, 0:32],
                               reads=["ct", "cf"], writes=[pk], track=(jj == 10))
                    self.I(DVE, "tensor_copy", chT[:, pc * 11:(pc + 1) * 11, :], ps[:, 0:352].rearrange("p (a b) -> p a b", b=32), reads=[pk], writes=["chT"])

                def issue(j):
                    slot = j % 2
                    S.dma(POOL, wf[slot][:], self.wF[:, j, :].rearrange("(k p) n -> p k n", p=128), writes=["wf%d" % slot])
                cwc = lambda j, t: pv[:, V_CW + 4 * j + t: V_CW + 4 * j + t + 1]
                issue(0)
                for j in range(NJ):
                    if j + 1 < NJ:
                        issue(j + 1)
                    w, wk = wf[j % 2], "wf%d" % (j % 2)
                    g_, kg = gt[j % 2], "gt%d" % (j % 2)
                    u_, ku = up[j % 2], "up%d" % (j % 2)
                    for (t0, tn) in self.tok_groups():
                        psg, pkg = self.psum()
                        psu, pku = self.psum()
                        for k in range(16):
                            self.I(PE, "matmul", psg[:, 0:tn], w[:, k, 0:128], h2T[:, k, t0:t0 + tn], start=(k == 0), stop=(k == 15),
                                   reads=[wk, "h2T"], writes=[pkg], track=(k == 15))
                        for k in range(16):
                            self.I(PE, "matmul", psu[:, 0:tn], w[:, k, 128:256], h2T[:, k, t0:t0 + tn], start=(k == 0), stop=(k == 15),
                                   reads=[wk, "h2T"], writes=[pku], track=(k == 15))
                        self.I(ACT, "copy", g_[:, t0:t0 + tn], psg[:, 0:tn], reads=[pkg], writes=[kg])
                        self.I(DVE, "tensor_copy", u_[:, t0:t0 + tn], psu[:, 0:tn], reads=[pku], writes=[ku])
                    self.I(POOL, "tensor_scalar", g_[:, 30:32], g_[:, 30:32], pv[:, V_FLAG:V_FLAG + 1], None, ALU.mult, reads=[kg, "pv"], writes=[kg])
                    self.I(ACT, "activation", cv[:, 0:1024], g_[:, 32:1056], AF.Identity, bias=cwc(j, 3), scale=cwc(j, 2), reads=[kg, "pv"], writes=["cv"])
                    self.I(DVE, "scalar_tensor_tensor", cv[:, 0:1024], g_[:, 31:1055], cwc(j, 1), cv[:, 0:1024], ALU.mult, ALU.add, reads=[kg, "cv", "pv"], writes=["cv"])
                    self.I(DVE, "scalar_tensor_tensor", cv[:, 0:1024], g_[:, 30:1054], cwc(j, 0), cv[:, 0:1024], ALU.mult, ALU.add, reads=[kg, "cv", "pv"], writes=["cv"])
                    gss = g_[:, 1056:NOWN].rearrange("p (b t) -> p b t", t=4)
                    self.I(POOL, "tensor_copy", gs6[:, :, 0:2], chT[:, j, :].rearrange("p (b r) -> p b r", r=2), reads=["chT"], writes=["gs6"])
                    self.I(POOL, "tensor_copy", gs6[:, :, 2:6], gss, reads=[kg], writes=["gs6"])
                    cvs = cv[:, 1024:NA].rearrange("p (b t) -> p b t", t=4)
                    self.I(ACT, "activation", cvs, gs6[:, :, 2:6], AF.Identity, bias=cwc(j, 3), scale=cwc(j, 2), reads=["gs6", "pv"], writes=["cv"])
                    self.I(DVE, "scalar_tensor_tensor", cvs, gs6[:, :, 1:5], cwc(j, 1), cvs, ALU.mult, ALU.add, reads=["gs6", "cv", "pv"], writes=["cv"])
                    self.I(DVE, "scalar_tensor_tensor", cvs, gs6[:, :, 0:4], cwc(j, 0), cvs, ALU.mult, ALU.add, reads=["gs6", "cv", "pv"], writes=["cv"])
                    self.I(POOL, "tensor_copy", pco[:, j, :], g_[:, 1054:1056], reads=[kg], writes=["pco"])
                    self.I(POOL, "tensor_copy", sco[:, j, :, :], gss[:, :, 2:4], reads=[kg], writes=["sco"])
                    self.I(ACT, "activation", ge[:, :], cv[:, :], AF.Gelu_apprx_tanh, reads=["cv"], writes=["ge"])
                    self.I(DVE, "tensor_tensor", actT[:, j, 0:1024], ge[:, 0:1024], u_[:, 32:1056], ALU.mult, reads=["ge", ku], writes=["actT"])
                    self.I(POOL, "tensor_tensor", actT[:, j, 1024:NA], ge[:, 1024:NA], u_[:, 1056:NOWN], ALU.mult, reads=["ge", ku], writes=["actT"])
                scr = self.stage_out("pco", pco[:], [128, NJ, 2], "pco")
                for r in range(2):
                    self.bg(self.o_pconv[r].rearrange("(j p) -> p j", p=128), scr[:, :, r], "pco")
                scrow = sb1("scrow", [32, 1408])
                for pc in range(4):
                    for j4 in range(0, 11, 4):
                        nj = min(4, 11 - j4)
                        ps, pk = self.psum()
                        for jj in range(nj):
                            j = pc * 11 + j4 + jj
                            self.I(PE, "transpose", ps[0:32, jj * 128:(jj + 1) * 128], sco[:, j, :, :].rearrange("p b r -> p (b r)"), cf[:, C_ID:C_ID + 128],
                                   reads=["sco", "cf"], writes=[pk], track=(jj == nj - 1))
                        self.I(DVE, "tensor_copy", scrow[:, j4 * 128:(j4 + nj) * 128], ps[0:32, 0:nj * 128], reads=[pk], writes=["scrow"])
                    S.dma(SP, self.o_sconv.rearrange("b r c -> (b r) c")[:, pc * 1408:(pc + 1) * 1408], scrow[:], reads=["scrow"], ring="spo")
                S.barrier()
            with ExitStack() as es2:
                sb2 = lambda n, s, d=F32: self.sb(es2, n, s, d)
                dummy = sb2("dmy2", [128, 2])
                self.I(POOL, "memset", dummy[:], 0.0, writes=["h2T", "dmy2"])
                bxf = self.bufX[:].rearrange("p a b -> p (a b)")
                wo = [bxf[:, i * 5632:(i + 1) * 5632].rearrange("p (k n) -> p k n", n=512) for i in range(2)]
                fo = sb2("fo", [128, 5, D])
                gp2 = sb2("gp2", [128, D])
                S.dma(SP, gp2[:], self.gpost[:, 1, :], writes=["gp2"])
                x1t = sb2("x1r", [128, D])
                st = sb2("stc", [128, 5, 8])
                xb = sb2("xbc", [128, 512], BF16)
                tiles = [(128 * i, 128) for i in range(8)] + [(1024, NSM)]
                sets = [tiles[0:5], tiles[5:9]]
                wn = 0
                for tset in sets:
                    self.I(POOL, "memset", st[:], 0.0, writes=["stc"])
                    for n in range(4):
                        banks = [self.psum() for _ in tset]
                        for kp in range(4):
                            slot = wn % 2
                            wn += 1
                            S.dma(POOL, wo[slot], self.wfo[kp * 1408:(kp + 1) * 1408, n * 512:(n + 1) * 512].rearrange("(k p) n -> p k n", p=128),
                                  writes=["wo%d" % slot])
                            for ti, (c0, nt) in enumerate(tset):
                                ps, pk = banks[ti]
                                for jj in range(11):
                                    j = kp * 11 + jj
                                    self.I(PE, "matmul", ps[0:nt, :], actT[:, j, c0:c0 + nt], wo[slot][:, jj, :], start=(j == 0), stop=(j == NJ - 1),
                                           reads=["actT", "wo%d" % slot], writes=[pk], track=(jj == 10))
                        for ti, (c0, nt) in enumerate(tset):
                            ps, pk = banks[ti]
                            self.I(DVE, "tensor_copy", fo[0:nt, ti, n * 512:(n + 1) * 512], ps[0:nt, :], reads=[pk], writes=["fo%d" % ti])
                            self.I(ACT, "activation", xb[0:nt, :], fo[0:nt, ti, n * 512:(n + 1) * 512], AF.Square, accum_out=st[0:nt, ti, n:n + 1], reads=["fo%d" % ti], writes=["xbc", "stc"])
                    for ti, (c0, nt) in enumerate(tset):
                        self.I(DVE, "tensor_reduce", st[0:nt, ti, 4:5], st[0:nt, ti, 0:4], AX.X, ALU.add, reads=["stc"], writes=["stc"])
                        self.I(DVE, "tensor_scalar", st[0:nt, ti, 5:6], st[0:nt, ti, 4:5], 1.0 / D, 1e-6, ALU.mult, ALU.add, reads=["stc"], writes=["stc"])
                        self.I(ACT, "activation", st[0:nt, ti, 6:7], st[0:nt, ti, 5:6], AF.Sqrt, reads=["stc"], writes=["stc"])
                        self.I(DVE, "reciprocal", st[0:nt, ti, 7:8], st[0:nt, ti, 6:7], reads=["stc"], writes=["stc"])
                        xc0 = 32 + c0
                        S.dma(SP, x1t[0:nt, :], self.x1s[xc0:xc0 + nt, :], reads=["x1s"], writes=["x1r"])
                        self.I(DVE, "scalar_tensor_tensor", fo[0:nt, ti, :], fo[0:nt, ti, :], st[0:nt, ti, 7:8], gp2[0:nt, :], ALU.mult, ALU.mult,
                               reads=["fo%d" % ti, "stc", "gp2"], writes=["fo%d" % ti])
                        self.I(POOL, "tensor_tensor", fo[0:nt, ti, :], fo[0:nt, ti, :], x1t[0:nt, :], ALU.add, reads=["fo%d" % ti, "x1r"], writes=["fo%d" % ti])
                        S.dma(SP, self.o_y[c0:c0 + nt, :], fo[0:nt, ti, :], reads=["fo%d" % ti], ring="spo")

    def make_hT(self, xrows, ntok, hT, hkey, slot):
        S = self.S
        xt, xb, st = self.xt[slot], self.xb[slot], self.xst[slot]
        kx, kb, ks = "xt%d" % slot, "xb%d" % slot, "xst%d" % slot
        S.dma(SP, xt[0:ntok, :], xrows, writes=[kx])
        self.I(POOL, "memset", st[:, 0:1], 0.0, writes=[ks])
        self.I(ACT, "activation", xb[0:ntok, :], xt[0:ntok, :], AF.Square, accum_out=st[0:ntok, 0:1], reads=[kx], writes=[kb, ks])
        self.I(DVE, "tensor_scalar", st[0:ntok, 1:2], st[0:ntok, 0:1], 1.0 / D, 1e-6, ALU.mult, ALU.add, reads=[ks], writes=[ks])
        self.I(ACT, "activation", st[0:ntok, 2:3], st[0:ntok, 1:2], AF.Sqrt, reads=[ks], writes=[ks])
        self.I(DVE, "reciprocal", st[0:ntok, 3:4], st[0:ntok, 2:3], reads=[ks], writes=[ks])
        self.I(ACT, "activation", xb[0:ntok, :], xt[0:ntok, :], AF.Copy, scale=st[0:ntok, 3:4], reads=[kx, ks, kb], writes=[kb])
        for half in range(2):
            ps, pk = self.psum()
            pT = ps[:].bitcast(BF16).rearrange("p (a b) -> p a b", b=128)
            for k8 in range(8):
                kc = half * 8 + k8
                self.I(PE, "transpose", pT[:, k8, 0:ntok], xb[0:ntok, kc * 128:(kc + 1) * 128],
                                                                self.cb[0:ntok, 0:ntok], reads=[kb, "cb"], writes=[pk], track=(k8 == 7))
            gcol = self.pv[:, V_G1 + half * 8: V_G1 + half * 8 + 8].unsqueeze(2).to_broadcast([128, 8, ntok])
            self.I(DVE, "tensor_tensor", hT[:, half * 8:half * 8 + 8, :], pT[:, :, 0:ntok], gcol, ALU.mult, reads=[pk, "pv"], writes=[hkey])

    def slab_stream(self, slabs):
        S = self.S
        n = len(slabs)

        def issue(i):
            ap, ncol = slabs[i]
            slot = self.wslot % 2
            self.wslot += 1
            key = "wb%d" % slot
            S.dma(POOL, self.wb[slot][:, :, 0:ncol], ap.rearrange("(k p) n -> p k n", p=128), writes=[key])
            return (self.wb[slot], key)
        cur = issue(0)
        for i in range(n):
            nxt = issue(i + 1) if i + 1 < n else None
            yield cur
            cur = nxt

    def proj(self, ps, pk, wt, wkey, c0, M, hT, hkey, N):
        S = self.S
        for k in range(16):
            self.I(PE, "matmul", ps[0:M, 0:N], wt[:, k, c0:c0 + M], hT[:, k, 0:N], start=(k == 0), stop=(k == 15), reads=[wkey, hkey], writes=[pk], track=(k == 15))

    def stageS(self):
        S = self.S
        pv, cf, cb = self.pv, self.cf, self.cb
        wA = self.wA_ap
        N = NSM
        with ExitStack() as es0:
            sb0 = lambda n, s, d=F32: self.sb(es0, n, s, d)
            sbon = sb0("sbon", [128, 8, N], BF16)
            sgst = sb0("sgst", [128, 8, N], BF16)
            self._stageS_proj(es0, sbon, sgst)
            self._stageS_scan(es0, sbon, sgst)

    def _stageS_proj(self, es0, sbon, sgst):
        S = self.S
        pv, cf, cb = self.pv, self.cf, self.cb
        wA = self.wA_ap
        N = NSM
        with ExitStack() as es:
            sb = lambda n, s, d=F32: self.sb(es, n, s, d)
            self.xt = [sb("xtS", [128, D])]
            self.xb = [sb("xbS", [128, D], BF16)]
            self.xst = [sb("xstS", [128, 4])]
            hT = sb("hTs", [128, 16, N], BF16)
            self.make_hT(self.xsamp, N, hT[:, :, :], "hTs", 0)
            S.dma(SP, self.scrH[:, :, 1056:NOWN], hT[:, :, :], reads=["hTs"], writes=["scrH_s"])
            shT = sb("shT", [128, 28, 16])
            sst = sb("sst", [16, DS])
            S.dma(SP, sst[:], self.st_shift, writes=["sst"])
            chunks = [(0, 64), (64, 64), (128, 128), (256, 32)] + [(288 + 128 * i, 128) for i in range(24)]
            ps, pk = self.psum()
            for ci, (o, M) in enumerate(chunks):
                self.I(PE, "transpose", ps[0:M, ci * 16:(ci + 1) * 16], sst[:, o:o + M], cf[0:16, 0:16], reads=["sst", "cf"], writes=[pk], track=(ci == 27))
            self.I(DVE, "tensor_copy", shT[:, :, :], ps[:, 0:448].rearrange("p (a b) -> p a b", b=16), reads=[pk], writes=["shT"])
            zls = sb("zls", [128, 28, 16])
            tw = sb("stw", [64, N], BF16)
            zam = sb("szam", [64, N], BF16)
            sga = sb("ssga", [128, N], BF16)
            sgb = sb("ssgb", [128, N], BF16)
            self.I(POOL, "memset", sgb[:], 0.0, writes=["ssgb"])
            tokS = sb("tokS", [N, 6, 1024])
            phys = {n: sb("s_" + n, [128, N]) for n in ("zc", "d", "zr", "zk", "zv", "es", "wd", "as", "kk", "kkn", "km", "dd", "av")}
            phys["hi"] = sb("s_hi", [128, N], BF16)
            phys["lo"] = sb("s_lo", [128, N], BF16)
            T = _TAlias(phys, {"kk2": "d", "rn": "zc", "t1": "dd", "ta": "es", "rk": "wd2"}, prefix="s_")
            phys["wd2"] = sb("s_wd2", [128, N])

            def mix_s(ps, pk, M, ci, out, kout, mu):
                zc, d = T["zc"], T["d"]
                v3 = lambda t: t[0:M, 0:N].rearrange("p (b t) -> p b t", t=4)
                self.I(ACT, "copy", zc[0:M, 0:N], ps[0:M, 0:N], reads=[pk], writes=["s_zc"])
                self.I(DVE, "tensor_tensor", v3(d)[:, :, 1:4], v3(zc)[:, :, 0:3], v3(zc)[:, :, 1:4], ALU.subtract, reads=["s_zc"], writes=["s_d"])
                self.I(DVE, "tensor_tensor", v3(d)[:, :, 0], shT[0:M, ci, :], v3(zc)[:, :, 0], ALU.subtract, reads=["s_zc", "shT"], writes=["s_d"])
                self.I(DVE, "scalar_tensor_tensor", out[0:M, 0:N], d[0:M, 0:N], mu, zc[0:M, 0:N], ALU.mult, ALU.add, reads=["s_d", "s_zc", "pv"], writes=[kout])
                self.I(POOL, "tensor_copy", zls[0:M, ci, :], v3(zc)[:, :, 3], reads=["s_zc"], writes=["zls"])

            slabs = [(wA[:, 0:288], 288)] + [(wA[:, 288 + q * 384: 288 + (q + 1) * 384], 384) for q in range(8)]
            stream = self.slab_stream(slabs)
            wt, wk = next(stream)
            for (ci, (cc0, M, dst, dk, func)) in enumerate(((0, 64, tw, "stw", AF.Tanh), (64, 64, zam, "szam", AF.Copy),
                                                             (128, 128, sga, "ssga", AF.Sigmoid), (256, 32, sgb, "ssgb", AF.Sigmoid))):
                ps, pk = self.psum()
                self.proj(ps, pk, wt, wk, cc0, M, hT, "hTs", N)
                mix_s(ps, pk, M, ci, T["zr"], "s_zr", pv[0:M, ci:ci + 1])
                self.I(ACT, "activation", dst[0:M, 0:N], T["zr"][0:M, 0:N], func, reads=["s_zr"], writes=[dk])
            bonesb = cb[:, C_BONES:C_BONES + 128]
            for q in range(8):
                wt, wk = next(stream)
                col = lambda base: pv[:, base + q: base + q + 1]
                qs = slice(q * 128, (q + 1) * 128)
                for t, nm in enumerate(("zr", "zk", "zv")):
                    ps, pk = self.psum()
                    self.proj(ps, pk, wt, wk, t * 128, 128, hT, "hTs", N)
                    mix_s(ps, pk, 128, 4 + 3 * q + t, T[nm], "s_" + nm, pv[:, V_MU + 3 * q + t: V_MU + 3 * q + t + 1])
                zr, zk, zv = T["zr"], T["zk"], T["zv"]
                ps, pk = self.psum()
                self.I(PE, "matmul", ps[:, 0:N], self.w2b[:, qs], tw[:, 0:N], start=True, stop=True, reads=["w2b", "stw"], writes=[pk])
                self.I(ACT, "activation", T["es"][:, 0:N], ps[:, 0:N], AF.Sigmoid, bias=col(V_W0), reads=[pk, "pv"], writes=["s_es"])
                self.I(ACT, "activation", T["wd"][:, 0:N], T["es"][:, 0:N], AF.Exp, scale=-DECAY_C, reads=["s_es"], writes=["s_wd"])
                ps, pk = self.psum()
                self.I(PE, "matmul", ps[:, 0:N], self.a2b[:, qs], zam[:, 0:N], start=True, stop=True, reads=["a2b", "szam"], writes=[pk])
                self.I(ACT, "activation", T["as"][:, 0:N], ps[:, 0:N], AF.Sigmoid, bias=col(V_A0), reads=[pk, "pv"], writes=["s_as"])
                self.I(POOL, "tensor_scalar", T["kk"][:, 0:N], zk[:, 0:N], col(V_KK), None, ALU.mult, reads=["s_zk", "pv"], writes=["s_kk"])
                self.I(POOL, "tensor_tensor", T["kk2"][:, 0:N], T["kk"][:, 0:N], T["kk"][:, 0:N], ALU.mult, reads=["s_kk"], writes=["s_d"])
                ps, pk = self.psum()
                self.bsum(ps, pk, T, "kk2", N)
                self.I(ACT, "activation", T["rn"][:, 0:N], ps[:, 0:N], AF.Sqrt, reads=[pk], writes=["s_zc"])
                self.I(DVE, "tensor_scalar_max", T["rn"][:, 0:N], T["rn"][:, 0:N], 1e-12, reads=["s_zc"], writes=["s_zc"])
                self.I(DVE, "reciprocal", T["rn"][:, 0:N], T["rn"][:, 0:N], reads=["s_zc"], writes=["s_zc"])
                self.I(DVE, "tensor_tensor", T["kkn"][:, 0:N], T["kk"][:, 0:N], T["rn"][:, 0:N], ALU.mult, reads=["s_kk", "s_zc"], writes=["s_kkn"])
                self.I(POOL, "tensor_scalar", T["t1"][:, 0:N], T["as"][:, 0:N], -1.0, col(V_KA), ALU.add, ALU.mult, reads=["s_as", "pv"], writes=["s_dd"])
                self.I(POOL, "tensor_tensor", T["t1"][:, 0:N], T["t1"][:, 0:N], zk[:, 0:N], ALU.mult, reads=["s_dd", "s_zk"], writes=["s_dd"])
                self.I(POOL, "tensor_tensor", T["km"][:, 0:N], T["t1"][:, 0:N], zk[:, 0:N], ALU.add, reads=["s_dd", "s_zk"], writes=["s_km"])
                self.I(POOL, "tensor_tensor", T["ta"][:, 0:N], T["kkn"][:, 0:N], T["as"][:, 0:N], ALU.mult, reads=["s_kkn", "s_as"], writes=["s_es"])
                self.I(POOL, "tensor_scalar", T["av"][:, 0:N], T["kkn"][:, 0:N], -1.0, None, ALU.mult, reads=["s_kkn"], writes=["s_av"])
                self.I(DVE, "scalar_tensor_tensor", T["rk"][:, 0:N], zr[:, 0:N], col(V_RK), T["km"][:, 0:N], ALU.mult, ALU.mult, reads=["s_zr", "s_km", "pv"], writes=["s_wd2"])
                ps, pk = self.psum()
                self.bsum(ps, pk, T, "rk", N)
                self.I(DVE, "tensor_tensor", sbon[:, q, :], ps[:, 0:N], zv[:, 0:N], ALU.mult, reads=[pk, "s_zv"], writes=["sbon"])
                self.I(POOL, "tensor_scalar", sbon[:, q, :], sbon[:, q, :], col(V_LB), None, ALU.add, reads=["sbon", "pv"], writes=["sbon"])
                ps, pk = self.psum()
                self.I(PE, "matmul", ps[:, 0:N], self.g2a[:, qs], sga[:, 0:N], start=True, stop=False, reads=["g2a", "ssga"], writes=[pk], track=False)
                self.I(PE, "matmul", ps[:, 0:N], self.g2b[:, qs], sgb[:, 0:N], start=False, stop=True, reads=["g2b", "ssgb"], writes=[pk])
                self.I(ACT, "copy", sgst[:, q, :], ps[:, 0:N], reads=[pk], writes=["sgst"])
                srcs = [("zr", "s_zr"), ("km", "s_km"), ("zv", "s_zv"), ("wd", "s_wd"), ("av", "s_av"), ("ta", "s_es")]
                psA, pkA = self.psum()
                psB, pkB = self.psum()
                for qi, (nm, kk_) in enumerate(srcs):
                    pp, ppk = (psA, pkA) if qi < 4 else (psB, pkB)
                    o = (qi % 4) * 128
                    self.I(PE, "transpose", pp[0:N, o:o + 128], T[nm][:, 0:N], cf[:, C_ID:C_ID + 128], reads=[kk_, "cf"], writes=[ppk], track=(qi in (3, 5)))
                self.I(DVE, "tensor_copy", tokS[:, 0:4, qs], psA[0:N, :].rearrange("p (a b) -> p a b", b=128), reads=[pkA], writes=["tokS"])
                self.I(ACT, "copy", tokS[:, 4:6, qs], psB[0:N, 0:256].rearrange("p (a b) -> p a b", b=128), reads=[pkB], writes=["tokS"])
            zrow = sb("zrow", [16, DS])
            for c0_ in range(0, 28, 4):
                ps, pk = self.psum()
                grp = list(enumerate(chunks))[c0_:c0_ + 4]
                for n_, (ci, (o, M)) in enumerate(grp):
                    self.I(PE, "transpose", ps[0:16, n_ * 128:n_ * 128 + M], zls[0:M, ci, :], cf[0:M, 0:M], reads=["zls", "cf"], writes=[pk], track=(n_ == len(grp) - 1))
                for n_, (ci, (o, M)) in enumerate(grp):
                    self.I(DVE, "tensor_copy", zrow[:, o:o + M], ps[0:16, n_ * 128:n_ * 128 + M], reads=[pk], writes=["zrow"])
            S.dma(SP, self.o_sshift, zrow[:], reads=["zrow"], ring="spo")
            S.dma(SP, self.scrS, tokS[:], reads=["tokS"], writes=["scrS"])
            S.barrier()

    def _stageS_scan(self, es0, sbon, sgst):
        S = self.S
        pv, cf, cb = self.pv, self.cf, self.cb
        N = NSM
        with ExitStack() as es:
            sb = lambda n, s, d=F32: self.sb(es, n, s, d)
            ytok = sb("ytok", [N, 1024])
            for bt in range(2):
                E = DVE
                kS, kq, kt, ky = "Sst%d" % bt, "qin%d" % bt, "stmp%d" % bt, "ysb%d" % bt
                Sst = sb(kS, [128, 64, 64])
                tmp = sb(kt, [128, 64, 64])
                qin = sb(kq, [128, 4, 6, 64])
                ysb = sb(ky, [128, 4, 64])
                sa = sb("sa%d" % bt, [128, 64])
                yq = sb("syq%d" % bt, [128, 4, 64])
                sst_ = sb("sstat%d" % bt, [128, 8, 4])
                S.dma(SP, Sst[:].rearrange("p v k -> p (v k)"), self.st_wkv[bt * 128:(bt + 1) * 128, :], writes=[kS])
                for bl in range(8):
                    b = bt * 8 + bl
                    S.dma(SP, qin[16 * bl:16 * bl + 16, :, :, :], self.scrS[4 * b:4 * b + 4, :, :].rearrange("t q (h k) -> h t q k", k=64),
                          reads=["scrS"], writes=[kq + "_%d" % bl])
                bc_k = lambda t, qi: qin[:, t, qi, :].unsqueeze(1).to_broadcast([128, 64, 64])
                kqs = [kq + "_%d" % bl for bl in range(8)]
                for t in range(4):
                    self.I(E, "tensor_tensor", tmp[:], Sst[:], bc_k(t, 4), ALU.mult, reads=[kS] + kqs, writes=[kt])
                    self.I(DVE, "tensor_reduce", sa[:], tmp[:], AX.X, ALU.add, reads=[kt], writes=["sa%d" % bt])
                    self.I(E, "tensor_tensor", Sst[:], Sst[:], bc_k(t, 3), ALU.mult, reads=[kS] + kqs, writes=[kS])
                    self.I(E, "tensor_tensor", tmp[:], sa[:].unsqueeze(2).to_broadcast([128, 64, 64]), bc_k(t, 5), ALU.mult, reads=["sa%d" % bt] + kqs, writes=[kt])
                    self.I(E, "tensor_tensor", Sst[:], Sst[:], tmp[:], ALU.add, reads=[kS, kt], writes=[kS])
                    self.I(E, "tensor_tensor", tmp[:], qin[:, t, 2, :].unsqueeze(2).to_broadcast([128, 64, 64]), bc_k(t, 1), ALU.mult, reads=kqs, writes=[kt])
                    self.I(E, "tensor_tensor", Sst[:], Sst[:], tmp[:], ALU.add, reads=[kS, kt], writes=[kS])
                    self.I(E, "tensor_tensor", tmp[:], Sst[:], bc_k(t, 0), ALU.mult, reads=[kS] + kqs, writes=[kt])
                    self.I(DVE, "tensor_reduce", ysb[:, t, :], tmp[:], AX.X, ALU.add, reads=[kt], writes=[ky])
                S.dma(SP, self.o_swkv[bt * 128:(bt + 1) * 128, :], Sst[:].rearrange("p v k -> p (v k)"), reads=[kS], ring="spo")
                st_ = sst_
                ks_ = "sstat%d" % bt
                self.I(DVE, "tensor_reduce", st_[:, 0, :], ysb[:], AX.X, ALU.add, reads=[ky], writes=[ks_])
                self.I(E, "tensor_tensor", yq[:], ysb[:], ysb[:], ALU.mult, reads=[ky], writes=["syq%d" % bt])
                self.I(DVE, "tensor_reduce", st_[:, 1, :], yq[:], AX.X, ALU.add, reads=["syq%d" % bt], writes=[ks_])
                self.I(E, "tensor_scalar", st_[:, 2, :], st_[:, 0, :], 1.0 / 64, None, ALU.mult, reads=[ks_], writes=[ks_])
                self.I(E, "tensor_tensor", st_[:, 3, :], st_[:, 2, :], st_[:, 2, :], ALU.mult, reads=[ks_], writes=[ks_])
                self.I(E, "tensor_scalar", st_[:, 4, :], st_[:, 1, :], 1.0 / 64, 64e-5, ALU.mult, ALU.add, reads=[ks_], writes=[ks_])
                self.I(E, "tensor_tensor", st_[:, 4, :], st_[:, 4, :], st_[:, 3, :], ALU.subtract, reads=[ks_], writes=[ks_])
                self.I(ACT, "activation", st_[:, 5, :], st_[:, 4, :], AF.Sqrt, reads=[ks_], writes=[ks_])
                self.I(DVE, "reciprocal", st_[:, 6, :], st_[:, 5, :], reads=[ks_], writes=[ks_])
                self.I(E, "tensor_tensor", yq[:], ysb[:], st_[:, 2, :].unsqueeze(2).to_broadcast([128, 4, 64]), ALU.subtract, reads=[ky, ks_], writes=["syq%d" % bt])
                self.I(E, "tensor_tensor", yq[:], yq[:], st_[:, 6, :].unsqueeze(2).to_broadcast([128, 4, 64]), ALU.mult, reads=["syq%d" % bt, ks_], writes=["syq%d" % bt])
                for bl in range(8):
                    b = bt * 8 + bl
                    S.dma(SP, self.scrY[4 * b:4 * b + 4, :].rearrange("t (h v) -> h t v", v=64), yq[16 * bl:16 * bl + 16, :, :], reads=["syq%d" % bt], writes=["scrY%d" % b])
            S.dma(SP, ytok[:], self.scrY, reads=["scrY%d" % b for b in range(16)], writes=["ytok"])
            if "sdbg" in self.dbg:
                S.dma(SP, self.dout("d_ytok", [N, 1024]), ytok[:], reads=["ytok"], ring="spo")
                dbf = sb("dbf", [128, 2, 8, N])
                self.I(DVE, "tensor_copy", dbf[:, 0], sbon[:], reads=["sbon"], writes=["dbf"])
                self.I(DVE, "tensor_copy", dbf[:, 1], sgst[:], reads=["sgst"], writes=["dbf"])
                S.dma(SP, self.dout("d_sbg", [128, 2, 8, N]), dbf[:], reads=["dbf"], ring="spo")
            ytb = sb("ytb", [N, 1024], BF16)
            self.I(ACT, "copy", ytb[:], ytok[:], reads=["ytok"], writes=["ytb"])
            ps, pk = self.psum()
            pT = ps[:].bitcast(BF16).rearrange("p (a b) -> p a b", b=128)[:, :, 0:N]
            for q in range(8):
                self.I(PE, "transpose", pT[:, q, :], ytb[:, q * 128:(q + 1) * 128], cb[0:N, 0:N], reads=["ytb", "cb"], writes=[pk], track=(q == 7))
            yt = sb("syt", [128, 8, N])
            lw = pv[:, V_LW: V_LW + 8].unsqueeze(2).to_broadcast([128, 8, N])
            self.I(DVE, "tensor_tensor", yt[:], pT, lw, ALU.mult, reads=[pk, "pv"], writes=["syt"])
            self.I(POOL, "tensor_tensor", yt[:], yt[:], sbon[:], ALU.add, reads=["syt", "sbon"], writes=["syt"])
            self.I(DVE, "tensor_tensor", self.yaT[:, :, 1056:NOWN], yt[:], sgst[:], ALU.mult, reads=["syt", "sgst"], writes=["yaT"])
            S.barrier()

    def stageA(self, es0, xseq, wA):
        S = self.S
        pv, cf, cb = self.pv, self.cf, self.cb
        with ExitStack() as es:
            sb = lambda n, s, d=F32: self.sb(es, n, s, d)
            self.xt = [sb("xt%d" % i, [128, D]) for i in range(1)]
            self.xb = [sb("xb%d" % i, [128, D], BF16) for i in range(1)]
            self.xst = [sb("xst%d" % i, [128, 4]) for i in range(1)]
            hT = sb("hTb", [128, 16, BLK], BF16)
            zl = sb("zl", [128, 28])
            self.I(POOL, "memset", zl[:], 0.0, writes=["zl"])
            tw = sb("tw", [64, BLK], BF16)
            zam = sb("zam", [64, BLK], BF16)
            sga = sb("sga", [128, BLK], BF16)
            sgb = sb("sgb", [128, BLK], BF16)
            self.I(POOL, "memset", sgb[:], 0.0, writes=["sgb"])
            NCK = BLK // CH
            ARBD = sb("ARBD", [128, 4, NCK, 2, 128], BF16)
            BBD = sb("BBD", [128, 4, NCK, 128], BF16)
            KBD = sb("KBD", [128, 4, NCK, 128], BF16)
            VBD = sb("VBD", [128, 4, NCK, 128], BF16)
            for (t, k) in ((ARBD, "ARBD"), (BBD, "BBD"), (KBD, "KBD"), (VBD, "VBD")):
                self.I(POOL, "memset", t[:], 0.0, writes=[k + "%d" % q for q in range(4)])
            PCs = sb("PCs", [128, 4, NCK])
            bon = sb("bon", [128, 4, BLK], BF16)
            gst = sb("gst", [128, 4, BLK], BF16)
            Hf = [sb("Hf%d" % g, [128, 4, 64]) for g in range(2)]
            Hb = [sb("Hb%d" % g, [128, 4, 64], BF16) for g in range(2)]
            for g in range(2):
                self.I(POOL, "memset", Hf[g][:], 0.0, writes=["Hf%d" % g])
                self.I(POOL, "memset", Hb[g][:], 0.0, writes=["Hb%d" % g])
            names = ("zc", "d", "zr", "zk", "zv", "es", "cum", "dd", "pinv", "prr", "pa", "as", "kk", "kkn", "km")
            phys = {n: sb("t_" + n, [128, NH]) for n in names[:8]}
            spare = self.bufX[:, 8:16, :].rearrange("p a b -> p (a b)")
            sparef = spare.bitcast(F32)
            for i, n in enumerate(names[8:]):
                phys[n] = sparef[:, i * NH:(i + 1) * NH]
            phys["hi"] = spare[:, 7 * 2 * NH: 7 * 2 * NH + NH]
            phys["lo"] = spare[:, 7 * 2 * NH + NH: 7 * 2 * NH + 2 * NH]
            T = _TAlias(phys, {"kk2": "d", "rn": "zc", "t1": "dd", "ta": "es", "rk": "cum"})
            sc = {}
            for s in range(2):
                sc["NA1", s] = sb("NA1_%d" % s, [128, 4, 256], BF16)
                sc["NA2", s] = sb("NA2_%d" % s, [128, 4, 256], BF16)
                for i in range(2):
                    sc["Q", s, i] = sb("Q_%d%d" % (s, i), [128, 4, 128], BF16)
                    if s == 0:
                        sc["NL", s, i] = sb("NL_%d%d" % (s, i), [128, 4, 128], BF16)
                        sc["TU", s, i] = sb("TU_%d%d" % (s, i), [128, 4, 128], BF16)
                sc["B2", s] = sb("B2_%d" % s, [128, 4, 128], BF16)
                sc["K2", s] = sb("K2_%d" % s, [128, 4, 128], BF16)
                sc["V2", s] = sb("V2_%d" % s, [128, 4, 64], BF16)
                if s == 0:
                    sc["X2", s] = sb("X2_%d" % s, [128, 4, 64], BF16)
                    sc["U2", s] = sb("U2_%d" % s, [128, 4, 64], BF16)
                    sc["YBD", s] = sb("YBD_%d" % s, [128, 4, 128], BF16)
                    self.I(POOL, "memset", sc["YBD", s][:], 0.0, writes=["YBD_%d" % s])
                    sc["ys", s] = sb("ys_%d" % s, [128, 4, 64])
                    sc["yq", s] = sb("yq_%d" % s, [128, 4, 64])
                    sc["st", s] = sb("yst_%d" % s, [128, 8, 4])
                    sc["yt", s] = sb("yt_%d" % s, [128, 4, 64])
                else:
                    for nm in ("X2", "U2", "YBD", "ys", "yq", "st", "yt"):
                        sc[nm, s] = sc[nm, 0]
            tmpH = sb("tmpH", [128, 4, 64])

            nblk = NPR // BLK
            lr_slab = (wA[:, 0:288], 288)
            pair_slabs = [(wA[:, 288 + q * 384: 288 + (q + 1) * 384], 384) for q in range(8)]
            slabs = []
            for blk in range(nblk):
                slabs.append(lr_slab)
                slabs += pair_slabs
            stream = self.slab_stream(slabs)

            for blk in range(nblk):
                c0 = blk * BLK
                N = BLK
                for t4 in range(4):
                    self.make_hT(xseq[c0 + t4 * 128: c0 + (t4 + 1) * 128, :], 128, hT[:, :, t4 * 128:(t4 + 1) * 128], "hTb", 0)
                self.ckpt("hT")
                if blk == 1:
                    S.dma(SP, self.scrH[:, :, 0:32], hT[:, :, BLK - 32:BLK], reads=["hTb"], writes=["scrH_1"])
                elif blk >= 2:
                    S.dma(SP, self.scrH[:, :, 32 + (blk - 2) * BLK: 32 + (blk - 1) * BLK], hT[:, :, :], reads=["hTb"], writes=["scrH_%d" % blk])
                wt, wk = next(stream)
                for (ci, (cc0, M, dst, dk, func)) in enumerate(((0, 64, tw, "tw", AF.Tanh), (64, 64, zam, "zam", AF.Copy),
                                                                 (128, 128, sga, "sga", AF.Sigmoid), (256, 32, sgb, "sgb", AF.Sigmoid))):
                    ps, pk = self.psum()
                    self.proj(ps, pk, wt, wk, cc0, M, hT, "hTb", N)
                    for co in range(0, N, NH):
                        self.mix(ps[:, co:co + NH], pk, M, NH, ci, T, zl)
                        self.I(ACT, "activation", dst[0:M, co:co + NH], T["zr"][0:M, 0:NH], func, reads=["t_zr"], writes=[dk])
                self.ckpt("lowrank")
                for g in range(2):
                    for qg in range(4):
                        q = g * 4 + qg
                        wt, wk = next(stream)
                        self.prep_pair(wt, wk, hT, N, q, qg, T, zl, tw, zam, sga, sgb, ARBD, BBD, KBD, VBD, PCs, bon, gst, blk)
                        self.ckpt("prep0")
                    self.ckpt("prep")
                    self.scan_group(g, blk, NCK, sc, ARBD, BBD, KBD, VBD, PCs, bon, gst, Hf[g], Hb[g], tmpH)
            scr = self.stage_out("zl", zl[:], [128, 28], "zl")
            self.bg(self.o_pshift[0:64].rearrange("(p o) -> p o", o=1), scr[0:64, 0:1], "zl")
            self.bg(self.o_pshift[64:128].rearrange("(p o) -> p o", o=1), scr[0:64, 1:2], "zl")
            self.bg(self.o_pshift[128:256].rearrange("(p o) -> p o", o=1), scr[:, 2:3], "zl")
            self.bg(self.o_pshift[256:288].rearrange("(p o) -> p o", o=1), scr[0:32, 3:4], "zl")
            self.bg(self.o_pshift[288:DS].rearrange("(j p) -> p j", p=128), scr[:, 4:28], "zl")
            for g in range(2):
                hh = sb("hh%d" % g, [128, 4, 64], BF16)
                hl = sb("hl%d" % g, [128, 4, 64], BF16)
                self.I(POOL, "tensor_copy", hh[:], Hf[g][:], reads=["Hf%d" % g], writes=["hh%d" % g])
                self.I(POOL, "tensor_tensor", hl[:], Hf[g][:], hh[:], ALU.subtract, reads=["Hf%d" % g, "hh%d" % g], writes=["hl%d" % g])
                ps, pk = self.psum()
                for qg in range(4):
                    self.I(PE, "matmul", ps[0:64, qg * 128:(qg + 1) * 128], hh[:, qg, :], cb[:, 0:128], start=True, stop=False, reads=["hh%d" % g, "cb"], writes=[pk], track=False)
                    self.I(PE, "matmul", ps[0:64, qg * 128:(qg + 1) * 128], hl[:, qg, :], cb[:, 0:128], start=False, stop=True, reads=["hl%d" % g, "cb"], writes=[pk], track=(qg == 3))
                so = sb("so%d" % g, [64, 4, 2, 64])
                self.I(DVE, "tensor_copy", so[:], ps[0:64, :].rearrange("p (a b c) -> p a b c", a=4, b=2), reads=[pk], writes=["so%d" % g])
                S.dma(SP, self.o_pwkv[g * 8:(g + 1) * 8].rearrange("(q s) v k -> v q s k", s=2), so[:], reads=["so%d" % g], ring="spo")
            S.barrier()

    def mix(self, ps, pk, M, N, ci, T, zl):
        S = self.S
        zc, d, out = T["zc"], T["d"], T["zr"]
        self.mix_to(ps, pk, M, N, ci, zl, zc, "t_zc", d, "t_d", out, "t_zr", self.pv[0:M, ci:ci + 1] if ci < 4 else None)

    def mix_to(self, ps, pk, M, N, ci, zl, zc, kzc, d, kd, out, kout, mu):
        S = self.S
        self.I(ACT, "copy", zc[0:M, 0:N], ps[0:M, 0:N], reads=[pk], writes=[kzc])
        self.I(DVE, "tensor_tensor", d[0:M, 1:N], zc[0:M, 0:N - 1], zc[0:M, 1:N], ALU.subtract, reads=[kzc], writes=[kd])
        self.I(DVE, "tensor_tensor", d[0:M, 0:1], zl[0:M, ci:ci + 1], zc[0:M, 0:1], ALU.subtract, reads=[kzc, "zl"], writes=[kd])
        self.I(DVE, "scalar_tensor_tensor", out[0:M, 0:N], d[0:M, 0:N], mu, zc[0:M, 0:N], ALU.mult, ALU.add, reads=[kd, kzc, "pv"], writes=[kout])
        self.I(POOL, "tensor_copy", zl[0:M, ci:ci + 1], zc[0:M, N - 1:N], reads=[kzc], writes=["zl"])

    def bsum(self, ps, pk, T, name, N):
        S = self.S
        src, ksrc = T[name], T.key(name)
        bonesb = self.cb[:, C_BONES:C_BONES + 128]
        khi, klo = T.key("hi"), T.key("lo")
        self.I(ACT, "copy", T["hi"][:, 0:N], src[:, 0:N], reads=[ksrc], writes=[khi])
        self.I(POOL, "tensor_tensor", T["lo"][:, 0:N], src[:, 0:N], T["hi"][:, 0:N], ALU.subtract, reads=[ksrc, khi], writes=[klo])
        self.I(PE, "matmul", ps[:, 0:N], bonesb, T["hi"][:, 0:N], start=True, stop=False, reads=["cb", khi], writes=[pk], track=False)
        self.I(PE, "matmul", ps[:, 0:N], bonesb, T["lo"][:, 0:N], start=False, stop=True, reads=["cb", klo], writes=[pk])

    def prep_pair(self, wt, wk, hT, N, q, qg, T, zl, tw, zam, sga, sgb, ARBD, BBD, KBD, VBD, PCs, bon, gst, blk):
        S = self.S
        pv, cf = self.pv, self.cf
        col = lambda base: pv[:, base + q: base + q + 1]
        qs = slice(q * 128, (q + 1) * 128)
        W = N
        co = 0
        cw = slice(0, W)
        psw, pkw = self.psum()
        self.I(PE, "matmul", psw[:, 0:W], self.w2b[:, qs], tw[:, cw], start=True, stop=True, reads=["w2b", "tw"], writes=[pkw])
        psa, pka = self.psum()
        self.I(PE, "matmul", psa[:, 0:W], self.a2b[:, qs], zam[:, cw], start=True, stop=True, reads=["a2b", "zam"], writes=[pka])
        pj = []
        for t in range(3):
            ps, pk = self.psum()
            self.proj(ps, pk, wt, wk, t * 128, 128, hT, "hTb", N)
            pj.append((ps, pk))
        self.I(ACT, "activation", T["es"][:, 0:W], psw[:, 0:W], AF.Sigmoid, bias=col(V_W0), reads=[pkw, "pv"], writes=["t_es"])
        self.I(ACT, "activation", T["as"][:, 0:W], psa[:, 0:W], AF.Sigmoid, bias=col(V_A0), reads=[pka, "pv"], writes=["t_as"])
        self.I(DVE, "tensor_tensor_scan", T["cum"][:, 0:W], cf[:, C_M01:C_M01 + W], T["es"][:, 0:W], 0.0, ALU.mult, ALU.add, reads=["t_es", "cf"], writes=["t_cum"])
        self.I(POOL, "tensor_tensor", T["dd"][:, 0:W], T["cum"][:, 0:W], T["es"][:, 0:W], ALU.subtract, reads=["t_cum", "t_es"], writes=["t_dd"])
        nck = W // CH
        ck0 = 0

        def mixq(t, nm):
            ps, pk = pj[t]
            ci = 4 + 3 * q + t
            self.mix_to(ps[:, cw], pk, 128, W, ci, zl, T["zc"], "t_zc", T["d"], "t_d", T[nm], "t_" + nm, pv[:, V_MU + 3 * q + t: V_MU + 3 * q + t + 1])
        mixq(0, "zr")
        self.I(ACT, "activation", T["pinv"][:, 0:W], T["cum"][:, 0:W], AF.Exp, scale=DECAY_C, reads=["t_cum"], writes=["t_pinv"])
        self.I(ACT, "activation", T["prr"][:, 0:W], T["cum"][:, 0:W], AF.Exp, scale=-DECAY_C, reads=["t_cum"], writes=["t_prr"])
        self.I(ACT, "activation", T["pa"][:, 0:W], T["dd"][:, 0:W], AF.Exp, scale=-DECAY_C, reads=["t_dd"], writes=["t_pa"])
        self.I(POOL, "tensor_copy", PCs[:, qg, ck0:ck0 + nck], T["prr"][:, 0:W].rearrange("p (c t) -> p c t", t=CH)[:, :, CH - 1], reads=["t_prr"], writes=["PCs%d" % qg])
        mixq(1, "zk")
        mixq(2, "zv")
        if True:
            zr, zk, zv = T["zr"], T["zk"], T["zv"]
            self.I(ACT, "activation", T["kk"][:, 0:W], zk[:, 0:W], AF.Copy, scale=col(V_KK), reads=["t_zk", "pv"], writes=["t_kk"])
            self.I(ACT, "activation", T["kk2"][:, 0:W], T["kk"][:, 0:W], AF.Square, reads=["t_kk"], writes=["t_d"])
            ps, pk = self.psum()
            self.bsum(ps, pk, T, "kk2", W)
            self.I(ACT, "activation", T["rn"][:, 0:W], ps[:, 0:W], AF.Sqrt, reads=[pk], writes=["t_zc"])
            self.I(DVE, "tensor_scalar_max", T["rn"][:, 0:W], T["rn"][:, 0:W], 1e-12, reads=["t_zc"], writes=["t_zc"])
            self.I(DVE, "reciprocal", T["rn"][:, 0:W], T["rn"][:, 0:W], reads=["t_zc"], writes=["t_zc"])
            self.I(DVE, "tensor_tensor", T["kkn"][:, 0:W], T["kk"][:, 0:W], T["rn"][:, 0:W], ALU.mult, reads=["t_kk", "t_zc"], writes=["t_kkn"])
            self.I(POOL, "tensor_scalar", T["t1"][:, 0:W], T["as"][:, 0:W], -1.0, col(V_KA), ALU.add, ALU.mult, reads=["t_as", "pv"], writes=["t_dd"])
            self.I(POOL, "tensor_tensor", T["t1"][:, 0:W], T["t1"][:, 0:W], zk[:, 0:W], ALU.mult, reads=["t_dd", "t_zk"], writes=["t_dd"])
            self.I(POOL, "tensor_tensor", T["km"][:, 0:W], T["t1"][:, 0:W], zk[:, 0:W], ALU.add, reads=["t_dd", "t_zk"], writes=["t_km"])
            self.I(POOL, "tensor_tensor", T["ta"][:, 0:W], T["kkn"][:, 0:W], T["as"][:, 0:W], ALU.mult, reads=["t_kkn", "t_as"], writes=["t_es"])
            self.I(DVE, "scalar_tensor_tensor", T["rk"][:, 0:W], zr[:, 0:W], col(V_RK), T["km"][:, 0:W], ALU.mult, ALU.mult, reads=["t_zr", "t_km", "pv"], writes=["t_cum"])
            ps, pk = self.psum()
            self.bsum(ps, pk, T, "rk", W)
            self.I(DVE, "tensor_tensor", bon[:, qg, cw], ps[:, 0:W], zv[:, 0:W], ALU.mult, reads=[pk, "t_zv"], writes=["bon%d" % qg])
            self.I(ACT, "activation", bon[:, qg, cw], bon[:, qg, cw], AF.Identity, bias=col(V_LB), reads=["bon%d" % qg, "pv"], writes=["bon%d" % qg])
            ps, pk = self.psum()
            self.I(PE, "matmul", ps[:, 0:W], self.g2a[:, qs], sga[:, cw], start=True, stop=False, reads=["g2a", "sga"], writes=[pk], track=False)
            self.I(PE, "matmul", ps[:, 0:W], self.g2b[:, qs], sgb[:, cw], start=False, stop=True, reads=["g2b", "sgb"], writes=[pk])
            self.I(ACT, "copy", gst[:, qg, cw], ps[:, 0:W], reads=[pk], writes=["gst%d" % qg])
            for h in range(2):
                pr = slice(64 * h, 64 * h + 64)
                cs = slice(64 * h, 64 * h + 64)
                v3 = lambda t: t[pr, 0:W].rearrange("p (c t) -> p c t", t=CH)
                e1 = DVE if h == 0 else POOL
                cks = slice(ck0, ck0 + nck)
                self.I(DVE, "scalar_tensor_tensor", ARBD[pr, qg, cks, 0, cs], v3(T["kkn"]), -1.0, v3(T["pa"]), ALU.mult, ALU.mult, reads=["t_kkn", "t_pa"], writes=["ARBD%d" % qg])
                self.I(e1, "tensor_tensor", ARBD[pr, qg, cks, 1, cs], v3(zr), v3(T["prr"]), ALU.mult, reads=["t_zr", "t_prr"], writes=["ARBD%d" % qg])
                self.I(e1, "tensor_tensor", KBD[pr, qg, cks, cs], v3(T["km"]), v3(T["pinv"]), ALU.mult, reads=["t_km", "t_pinv"], writes=["KBD%d" % qg])
                self.I(e1, "tensor_tensor", BBD[pr, qg, cks, cs], v3(T["ta"]), v3(T["pinv"]), ALU.mult, reads=["t_es", "t_pinv"], writes=["BBD%d" % qg])
                self.I(ACT, "copy", VBD[pr, qg, cks, cs], v3(zv), reads=["t_zv"], writes=["VBD%d" % qg])

    def scan_pre(self, g, c, gc, sc, ARBD, BBD, KBD, VBD):
        cf, cb = self.cf, self.cb
        s = gc % 2
        kin = ["ARBD%d" % q for q in range(4)] + ["BBD%d" % q for q in range(4)] + ["KBD%d" % q for q in range(4)] + ["VBD%d" % q for q in range(4)]
        NA1, NA2 = sc["NA1", s], sc["NA2", s]
        kNA1, kNA2 = "NA1_%d" % s, "NA2_%d" % s
        identb = cb[:, 0:128]
        iselb = cb[:, C_ISEL:C_ISEL + 64]
        mask2 = cf[:, C_MUS:C_MUS + 256].unsqueeze(1).to_broadcast([128, 2, 256])
        for (dst, kd, L) in ((NA1, kNA1, BBD), (NA2, kNA2, KBD)):
            for hf in range(2):
                ps, pk = self.psum()
                for j in range(2):
                    q = 2 * hf + j
                    self.I(PE, "matmul", ps[:, j * 256:(j + 1) * 256], L[:, q, c, :], ARBD[:, q, c, :, :].rearrange("p a b -> p (a b)"),
                           start=True, stop=True, reads=kin, writes=[pk], track=(j == 1))
                self.I(DVE, "tensor_tensor", dst[:, 2 * hf:2 * hf + 2, :], ps[:].rearrange("p (a b) -> p a b", b=256), mask2, ALU.mult, reads=[pk, "cf"], writes=[kd])
                yield
        NL0, kNL0 = sc["NL", 0, 0], "NL_00"
        ps, pk = self.psum()
        for q in range(4):
            self.I(PE, "matmul", ps[:, q * 128:(q + 1) * 128], ARBD[:, q, c, 0, :], BBD[:, q, c, :], start=True, stop=True, reads=kin, writes=[pk], track=(q == 3))
        mL = cf[:, C_MLS:C_MLS + 128].unsqueeze(1).to_broadcast([128, 4, 128])
        self.I(DVE, "tensor_tensor", NL0[:], ps[:].rearrange("p (a b) -> p a b", b=128), mL, ALU.mult, reads=[pk, "cf"], writes=[kNL0])
        yield
        B2, K2, V2 = sc["B2", s], sc["K2", s], sc["V2", s]
        for (dst, kd, L) in ((B2, "B2_%d" % s, BBD), (K2, "K2_%d" % s, KBD)):
            ps, pk = self.psum()
            for q in range(4):
                self.I(PE, "matmul", ps[:, q * 128:(q + 1) * 128], L[:, q, c, :], identb, start=True, stop=True, reads=kin + ["cb"], writes=[pk], track=(q == 3))
            self.I(ACT, "copy", dst[:], ps[:].rearrange("p (a b) -> p a b", b=128), reads=[pk], writes=[kd])
            yield
        ps, pk = self.psum()
        for q in range(4):
            self.I(PE, "matmul", ps[:, q * 64:(q + 1) * 64], VBD[:, q, c, :], iselb, start=True, stop=True, reads=kin + ["cb"], writes=[pk], track=(q == 3))
        self.I(ACT, "copy", V2[:], ps[:, 0:256].rearrange("p (a b) -> p a b", b=64), reads=[pk], writes=["V2_%d" % s])
        yield
        TUc, kTU = NA1[:, :, 0:128], kNA1
        NLc, kNL = NL0, kNL0
        Qc, kQ = sc["Q", s, 0], "Q_%d0" % s
        idb4 = identb.unsqueeze(1).to_broadcast([128, 4, 128])
        self.I(POOL, "tensor_tensor", Qc[:], NA1[:, :, 0:128], idb4, ALU.add, reads=[kNA1, "cb"], writes=[kQ])
        for lvl in range(1, 6):
            i = lvl % 2
            NLn, kNLn = sc["NL", 0, i], "NL_0%d" % i
            TUn, kTUn = sc["TU", 0, i], "TU_0%d" % i
            Qn, kQn = sc["Q", s, i], "Q_%d%d" % (s, i)
            psN, pkN = self.psum()
            for q in range(4):
                self.I(PE, "matmul", psN[:, q * 128:(q + 1) * 128], TUc[:, q, :], NLc[:, q, :], start=True, stop=True, reads=[kTU, kNL], writes=[pkN], track=(q == 3))
            if lvl < 5:
                psT, pkT = self.psum()
                for q in range(4):
                    self.I(PE, "matmul", psT[:, q * 128:(q + 1) * 128], NLc[:, q, :], TUc[:, q, :], start=True, stop=True, reads=[kTU, kNL], writes=[pkT], track=(q == 3))
            self.I(ACT, "copy", NLn[:], psN[:].rearrange("p (a b) -> p a b", b=128), reads=[pkN], writes=[kNLn])
            if lvl < 5:
                self.I(DVE, "tensor_copy", TUn[:], psT[:].rearrange("p (a b) -> p a b", b=128), reads=[pkT], writes=[kTUn])
            yield
            psQ, pkQ = self.psum()
            for q in range(4):
                self.I(PE, "matmul", psQ[:, q * 128:(q + 1) * 128], NLn[:, q, :], Qc[:, q, :], start=True, stop=True, reads=[kNLn, kQ], writes=[pkQ], track=(q == 3))
            self.I(DVE, "tensor_tensor", Qn[:], psQ[:].rearrange("p (a b) -> p a b", b=128), Qc[:], ALU.add, reads=[pkQ, kQ], writes=[kQn])
            yield
            TUc, kTU, NLc, kNL, Qc, kQ = TUn, kTUn, NLn, kNLn, Qn, kQn

    def scan_chain(self, g, c, gc, sc, ARBD, PCs, bon, gst, Hf, Hb, tmpH):
        s = gc % 2
        kin = ["ARBD%d" % q for q in range(4)]
        NA1, NA2 = sc["NA1", s], sc["NA2", s]
        kNA1, kNA2 = "NA1_%d" % s, "NA2_%d" % s
        B2, K2, V2 = sc["B2", s], sc["K2", s], sc["V2", s]
        kB2, kK2, kV2 = "B2_%d" % s, "K2_%d" % s, "V2_%d" % s
        Qc, kQ = sc["Q", s, 1], "Q_%d1" % s
        kH = "Hf%d" % g
        kHb = "Hb%d" % g
        X2, U2 = sc["X2", 0], sc["U2", 0]
        ps, pk = self.psum()
        for q in range(4):
            self.I(PE, "matmul", ps[:, q * 64:(q + 1) * 64], ARBD[:, q, c, 0, :], Hb[:, q, :], start=True, stop=False, reads=kin + [kHb], writes=[pk], track=False)
            self.I(PE, "matmul", ps[:, q * 64:(q + 1) * 64], NA2[:, q, 0:128], V2[:, q, :], start=False, stop=True, reads=[kNA2, kV2], writes=[pk], track=(q == 3))
        self.I(ACT, "copy", X2[:], ps[:, 0:256].rearrange("p (a b) -> p a b", b=64), reads=[pk], writes=["X2_0"])
        yield
        ps, pk = self.psum()
        for q in range(4):
            self.I(PE, "matmul", ps[:, q * 64:(q + 1) * 64], Qc[:, q, :], X2[:, q, :], start=True, stop=True, reads=[kQ, "X2_0"], writes=[pk], track=(q == 3))
        self.I(ACT, "copy", U2[:], ps[:, 0:256].rearrange("p (a b) -> p a b", b=64), reads=[pk], writes=["U2_0"])
        yield
        need_y = gc >= OWN0 // CH
        if need_y:
            psY, pkY = self.psum()
            for q in range(4):
                self.I(PE, "matmul", psY[:, q * 64:(q + 1) * 64], ARBD[:, q, c, 1, :], Hb[:, q, :], start=True, stop=False, reads=kin + [kHb], writes=[pkY], track=False)
                self.I(PE, "matmul", psY[:, q * 64:(q + 1) * 64], NA1[:, q, 128:256], U2[:, q, :], start=False, stop=False, reads=[kNA1, "U2_0"], writes=[pkY], track=False)
                self.I(PE, "matmul", psY[:, q * 64:(q + 1) * 64], NA2[:, q, 128:256], V2[:, q, :], start=False, stop=True, reads=[kNA2, kV2], writes=[pkY], track=(q == 3))
        ps, pk = self.psum()
        for q in range(4):
            self.I(PE, "matmul", ps[:, q * 64:(q + 1) * 64], B2[:, q, :], U2[:, q, :], start=True, stop=False, reads=[kB2, "U2_0"], writes=[pk], track=False)
            self.I(PE, "matmul", ps[:, q * 64:(q + 1) * 64], K2[:, q, :], V2[:, q, :], start=False, stop=True, reads=[kK2, kV2], writes=[pk], track=(q == 3))
        self.I(DVE, "tensor_tensor", tmpH[:], Hf[:], ps[:, 0:256].rearrange("p (a b) -> p a b", b=64), ALU.add, reads=[pk, kH], writes=["tmpH"])
        pcb = PCs[:, :, c:c + 1].to_broadcast([128, 4, 64])
        self.I(DVE, "tensor_tensor", Hf[:], tmpH[:], pcb, ALU.mult, reads=["tmpH"] + ["PCs%d" % q for q in range(4)], writes=[kH])
        self.I(ACT, "copy", Hb[:], Hf[:], reads=[kH], writes=[kHb])
        yield
        if need_y:
            yield from self.y_post(g, c, gc, s, sc, psY, pkY, bon, gst)

    def scan_group(self, g, blk, NCK, sc, ARBD, BBD, KBD, VBD, PCs, bon, gst, Hf, Hb, tmpH):
        def drain(gen):
            for _ in gen:
                pass

        def interleave(ga, gb):
            a_live = b_live = True
            while a_live or b_live:
                if a_live:
                    try:
                        next(ga)
                    except StopIteration:
                        a_live = False
                if b_live:
                    try:
                        next(gb)
                    except StopIteration:
                        b_live = False
        gc0 = blk * NCK
        drain(self.scan_pre(g, 0, gc0, sc, ARBD, BBD, KBD, VBD))
        for c in range(NCK):
            ch = self.scan_chain(g, c, gc0 + c, sc, ARBD, PCs, bon, gst, Hf, Hb, tmpH)
            if c + 1 < NCK:
                interleave(ch, self.scan_pre(g, c + 1, gc0 + c + 1, sc, ARBD, BBD, KBD, VBD))
            else:
                drain(ch)

    def y_post(self, g, c, gc, s, sc, psY, pkY, bon, gst):
        S = self.S
        pv, cb = self.pv, self.cb
        ys, yq, st, yt, YBD = sc["ys", s], sc["yq", s], sc["st", s], sc["yt", s], sc["YBD", s]
        kys, kyq, kst, kyt, kY = "ys_0", "yq_0", "yst_0", "yt_0", "YBD_0"
        self.I(ACT, "copy", ys[:], psY[:, 0:256].rearrange("p (a b) -> p a b", b=64), reads=[pkY], writes=[kys])
        self.I(DVE, "tensor_reduce", st[:, 0, :], ys[:], AX.X, ALU.add, reads=[kys], writes=[kst])
        self.I(POOL, "tensor_tensor", yq[:], ys[:], ys[:], ALU.mult, reads=[kys], writes=[kyq])
        self.I(DVE, "tensor_reduce", st[:, 1, :], yq[:], AX.X, ALU.add, reads=[kyq], writes=[kst])
        yield
        self.I(DVE, "tensor_scalar", st[:, 2, :], st[:, 0, :], 1.0 / 64, None, ALU.mult, reads=[kst], writes=[kst])
        self.I(DVE, "tensor_tensor", st[:, 3, :], st[:, 2, :], st[:, 2, :], ALU.mult, reads=[kst], writes=[kst])
        self.I(DVE, "scalar_tensor_tensor", st[:, 4, :], st[:, 1, :], 1.0 / 64, st[:, 3, :], ALU.mult, ALU.subtract, reads=[kst], writes=[kst])
        self.I(DVE, "tensor_scalar", st[:, 4, :], st[:, 4, :], 64e-5, None, ALU.add, reads=[kst], writes=[kst])
        self.I(ACT, "activation", st[:, 5, :], st[:, 4, :], AF.Sqrt, reads=[kst], writes=[kst])
        self.I(DVE, "reciprocal", st[:, 6, :], st[:, 5, :], reads=[kst], writes=[kst])
        yield
        self.I(DVE, "tensor_tensor", yq[:], ys[:], st[:, 2, :].unsqueeze(2).to_broadcast([128, 4, 64]), ALU.subtract, reads=[kys, kst], writes=[kyq])
        for h in range(2):
            pr = slice(64 * h, 64 * h + 64)
            self.I(DVE if h == 0 else POOL, "tensor_tensor", YBD[pr, :, pr], yq[pr, :, :], st[pr, 6, :].unsqueeze(2).to_broadcast([64, 4, 64]), ALU.mult, reads=[kyq, kst], writes=[kY])
        yield
        ps, pk = self.psum()
        for q in range(4):
            self.I(PE, "matmul", ps[:, q * 64:(q + 1) * 64], YBD[:, q, :], cb[:, C_ISEL:C_ISEL + 64], start=True, stop=True, reads=[kY, "cb"], writes=[pk], track=(q == 3))
        t0 = 0
        col0 = gc * CH - OWN0
        if col0 < 0:
            t0 = -col0
            col0 = 0
        nt = CH - t0
        cl = c * CH + t0
        lw = pv[:, V_LW + 4 * g: V_LW + 4 * g + 4].unsqueeze(2).to_broadcast([128, 4, nt])
        psv = ps[:, 0:256].rearrange("p (a b) -> p a b", b=64)[:, :, t0:CH]
        self.I(DVE, "tensor_tensor", yt[:, :, 0:nt], psv, lw, ALU.mult, reads=[pk, "pv"], writes=[kyt])
        self.I(POOL, "tensor_tensor", yt[:, :, 0:nt], yt[:, :, 0:nt], bon[:, :, cl:cl + nt], ALU.add, reads=[kyt] + ["bon%d" % q for q in range(4)], writes=[kyt])
        self.I(DVE, "tensor_tensor", self.yaT[:, 4 * g:4 * g + 4, col0:col0 + nt], yt[:, :, 0:nt], gst[:, :, cl:cl + nt], ALU.mult, reads=[kyt] + ["gst%d" % q for q in range(4)], writes=["yaT"])


def core_inputs(inp, c, pv_base):
    b, p = c // 2, c % 2
    x = inp["x_prompt"][b]
    if p == 1:
        xseq = np.ascontiguousarray(x)
    else:
        xseq = np.concatenate([np.zeros((1024, D), np.float32), x[:1024]], axis=0)
    pa = permA()
    return {
        "xseq": xseq,
        "xsamp": np.ascontiguousarray(inp["x_sample"][16 * c:16 * c + 16].reshape(NSM, D)),
        "pvec": core_pvec(pv_base, p),
        "st_pool": np.ascontiguousarray(inp["state_pool"][0, 16 * c:16 * c + 16]),
        "st_conv": np.ascontiguousarray(inp["state_conv"][0, 16 * c:16 * c + 16]),
        "st_shift": np.ascontiguousarray(inp["state_shift"][0, 16 * c:16 * c + 16, 0][:, pa]),
        "st_wkv": np.ascontiguousarray(inp["state_wkv"][0, 16 * c:16 * c + 16].reshape(256, 4096)),
    }


def shared_inputs(inp):
    w_in = inp["w_in"][0]
    wG = np.empty((D, 16, 256), np.float32)
    wG[:, :, 0:128] = w_in[:, 4384:6432].reshape(D, 16, 128)
    wG[:, :, 128:256] = w_in[:, 6432:8480].reshape(D, 16, 128)
    wfi = inp["w_ffn_in"][0]
    wF = np.empty((D, NJ, 256), np.float32)
    wF[:, :, 0:128] = wfi[:, 0:DFF].reshape(D, NJ, 128)
    wF[:, :, 128:256] = wfi[:, DFF:2 * DFF].reshape(D, NJ, 128)
    gpost = np.empty((128, 2, D), np.float32)
    gpost[:, 0, :] = inp["norm_post_mix"][0][None, :]
    gpost[:, 1, :] = inp["norm_post_ffn"][0][None, :]
    return {
        "wA": np.ascontiguousarray(w_in[:, permA()]),
        "consts": host_consts(),
        "w2": np.ascontiguousarray(inp["w2"][0]),
        "a2": np.ascontiguousarray(inp["a2"][0]),
        "g2": np.ascontiguousarray(inp["g2"][0]),
        "wP": np.ascontiguousarray(w_in[:, 3360:4384]),
        "wG": wG,
        "wa": np.ascontiguousarray(inp["w_branch_a"][0]),
        "wbr": np.ascontiguousarray(inp["w_branch_b"][0]),
        "poolw": np.ascontiguousarray(inp["pool_w"][0]),
        "wout": np.ascontiguousarray(inp["w_out"][0]),
        "gpost": gpost,
        "wF": wF,
        "wfo": np.ascontiguousarray(inp["w_ffn_out"][0]),
    }


_NC_CACHE = {}


def get_nc(dbg=(), stages="SABC"):
    key = (tuple(sorted(dbg)), stages)
    if key not in _NC_CACHE:
        B = Builder(dbg=set(dbg), stages=stages)
        nc = B.build()
        _NC_CACHE[key] = (nc, B)
    return _NC_CACHE[key]


def run_cores(inp, cores, dbg=(), stages="SABC", trace=False):
    nc, B = get_nc(dbg, stages)
    sh = shared_inputs(inp)
    pv_base = host_pvec(inp)
    names = set(B.ins.keys())
    maps = []
    for c in cores:
        m = dict(sh)
        m.update(core_inputs(inp, c, pv_base))
        maps.append({k: v for k, v in m.items() if k in names})
    res = run_bass_kernel_spmd(nc, maps, core_ids=list(range(len(cores))), trace=trace)
    return res


def kernel(**inp):
    inp = {k: np.asarray(v) for k, v in inp.items()}
    res = run_cores(inp, list(range(8)))
    R = res.results
    pa = permA()
    y_prompt = np.empty((4, 2048, D), np.float32)
    y_sample = np.empty((128, 4, D), np.float32)
    p_shift = np.empty((1, 4, 1, DS), np.float32)
    p_wkv = np.empty((1, 4, 16, 64, 64), np.float32)
    p_pool = np.empty((1, 4, 15, 1024), np.float32)
    p_conv = np.empty((1, 4, 2, DFF), np.float32)
    s_shift = np.zeros((1, 128, 1, DS), np.float32)
    s_wkv = np.zeros((1, 128, 16, 64, 64), np.float32)
    s_pool = np.empty((1, 128, 15, 1024), np.float32)
    s_conv = np.empty((1, 128, 2, DFF), np.float32)
    for c in range(8):
        b, p = c // 2, c % 2
        r = R[c]
        y_prompt[b, p * 1024:(p + 1) * 1024] = r["o_y"][0:1024]
        y_sample[16 * c:16 * c + 16] = r["o_y"][1024:1024 + NSM].reshape(16, 4, D)
        if p == 1:
            p_shift[0, b, 0, pa] = r["o_pshift"]
            p_wkv[0, b] = r["o_pwkv"]
            p_pool[0, b] = r["o_ppool"]
            p_conv[0, b] = r["o_pconv"]
        s_pool[0, 16 * c:16 * c + 16] = r["o_spool"]
        s_conv[0, 16 * c:16 * c + 16] = r["o_sconv"]
        if "o_sshift" in r:
            s_shift[0, 16 * c:16 * c + 16, 0][:, pa] = r["o_sshift"]
            s_wkv[0, 16 * c:16 * c + 16] = r["o_swkv"].reshape(16, 16, 64, 64)
    return (y_prompt, y_sample, p_shift, p_wkv, p_pool, p_conv, s_shift, s_wkv, s_pool, s_conv)
```

```python
import os
import numpy as np
import concourse.bass as bass
import concourse.mybir as mybir
from contextlib import ExitStack
from concourse.bass_utils import run_bass_kernel_spmd

F32 = mybir.dt.float32
BF16 = mybir.dt.bfloat16
AF = mybir.ActivationFunctionType
ALU = mybir.AluOpType
AX = mybir.AxisListType

PE, ACT, DVE, POOL, SP = "pe", "act", "dve", "pool", "sp"
NDMASEM = 12
VCLOCK = os.environ.get('VCLOCK', '1') == '1'
EMBED_WAIT = os.environ.get('EMBW', '1') == '1'
SAME_ENGINE_NOWAIT = os.environ.get('SENW', '0') == '1'


class Sched:
    def __init__(self, nc, es):
        self.nc = nc
        self.q = {e: [] for e in (PE, ACT, DVE, POOL, SP)}
        self.cnt = {e: 0 for e in (PE, ACT, DVE, POOL)}
        self.sem = {e: es.enter_context(nc.semaphore("s_" + e)) for e in (PE, ACT, DVE, POOL)}
        self.dsem = {}
        self.dcnt = {}
        for e in (SP, "spo", "bg", POOL):
            self.dsem[e] = [es.enter_context(nc.semaphore("d_%s%d" % (e, i))) for i in range(NDMASEM)]
            self.dcnt[e] = 0
        self.pending = {e: [] for e in (PE, ACT, DVE, POOL, SP)}
        self.tok_ring = {}
        self.clock = {}
        self.tclk = {}
        self.lastw = {}
        self.readers = {}
        self.all_tokens = {}

    def _deps(self, eng, reads, writes):
        toks = []
        for k in reads:
            w = self.lastw.get(k)
            if w is not None:
                toks.append(w)
        for k in writes:
            w = self.lastw.get(k)
            if w is not None:
                toks.append(w)
            toks.extend(self.readers.get(k, ()))
        clk = self.clock.setdefault(eng, {})
        need = {}
        for tok in toks:
            s, v, src = tok[0], tok[1], tok[2]
            if src == eng and eng == PE:
                continue
            if clk.get(s.name, 0) >= v:
                continue
            if need.get(s.name, (None, 0))[1] < v:
                need[s.name] = (s, v, self.tclk.get((s.name, v), {}))
        for (s, v) in self.pending[eng]:
            if clk.get(s.name, 0) >= v:
                continue
            if need.get(s.name, (None, 0))[1] < v:
                need[s.name] = (s, v, self.tclk.get((s.name, v), {}))
        self.pending[eng] = []
        if VCLOCK and len(need) > 1:
            drop = set()
            for name, (s, v, c) in need.items():
                for name2, (s2, v2, c2) in need.items():
                    if name2 != name and name2 not in drop and c2.get(name, 0) >= v:
                        drop.add(name)
                        break
            for name in drop:
                del need[name]
        out = []
        for name, (s, v, c) in need.items():
            if clk.get(name, 0) < v:
                clk[name] = v
            for n2, v2 in c.items():
                if clk.get(n2, 0) < v2:
                    clk[n2] = v2
            out.append((s, v))
        return out

    def _stamp(self, eng, tok):
        c = dict(self.clock.get(eng, {}))
        c[tok[0].name] = tok[1]
        self.tclk[(tok[0].name, tok[1])] = c

    def barrier(self):
        if self.frozen:
            return
        allw = []
        for e in (PE, ACT, DVE, POOL):
            if self.cnt[e] > 0:
                allw.append((self.sem[e], self.cnt[e], e))
        for name, (s, v) in self.all_tokens.items():
            if self.tok_ring.get(name) != "bg":
                allw.append((s, v, "dma"))
        for eng in (PE, ACT, DVE, POOL, SP):
            for (s, v, src) in allw:
                if src == eng:
                    continue
                self.pending[eng].append((s, v))

    def _commit(self, tok, reads, writes):
        for k in writes:
            self.lastw[k] = tok
            self.readers[k] = []
        for k in reads:
            lst = self.readers.setdefault(k, [])
            lst[:] = [t for t in lst if t[0].name != tok[0].name]
            lst.append(tok)

    frozen = False

    def op(self, eng, fn, reads=(), writes=(), track=True):
        if self.frozen:
            return None
        waits = self._deps(eng, reads, writes)
        tok = None
        if not track:
            assert eng == PE
            self.pend_r = getattr(self, "pend_r", set()) | set(reads)
        if track:
            self.cnt[eng] += 1
            tok = (self.sem[eng], self.cnt[eng], eng)
            self._stamp(eng, tok)
            if eng == PE and getattr(self, "pend_r", None):
                reads = list(set(reads) | self.pend_r)
                self.pend_r = set()
            self._commit(tok, reads, writes)
        self.q[eng].append((waits, fn, tok))
        return tok

    def dma(self, qeng, out, in_, reads=(), writes=(), ring=None, **kw):
        if self.frozen:
            return None
        waits = self._deps(qeng, reads, writes)
        rk = ring or qeng
        n = self.dcnt[rk]
        self.dcnt[rk] += 1
        s = self.dsem[rk][n % NDMASEM]
        v = 16 * (n // NDMASEM + 1)
        tok = (s, v, "dma")
        self._stamp(qeng, tok)
        self._commit(tok, reads, writes)
        self.q[qeng].append((waits, lambda e: e.dma_start(out=out, in_=in_, **kw), tok))
        self.all_tokens[s.name] = (s, v)
        self.tok_ring[s.name] = rk
        return tok

    def emit(self):
        nc = self.nc
        fin = dict(self.all_tokens)
        for e in (PE, ACT, DVE, POOL):
            if self.cnt[e] > 0:
                fin[self.sem[e].name] = (self.sem[e], self.cnt[e])
        q = self.q
        with nc.Block() as block:
            def run(eng_name):
                def body(e):
                    for (waits, fn, tok) in q[eng_name]:
                        emb = None
                        if EMBED_WAIT and waits:
                            emb = waits[-1]
                            waits = waits[:-1]
                        for (s, v) in waits:
                            e.wait_ge(s, v)
                        ins = fn(e)
                        if emb is not None:
                            ins._wait_ge(emb[0], emb[1])
                        if tok is not None:
                            ins.then_inc(tok[0], 16 if tok[2] == "dma" else 1)
                    if eng_name == SP:
                        for name, (s, v) in fin.items():
                            e.wait_ge(s, v)
                return body
            block.tensor(run(PE))
            block.scalar(run(ACT))
            block.vector(run(DVE))
            block.gpsimd(run(POOL))
            block.sync(run(SP))


D = 2048
DS = 3360
NPR = 2048
NSM = 64
OWN0 = 992
NOWN = 1120
BLK = 512
NH = 512
CH = 64
C_ID, C_BONES, C_ISEL, C_MUS, C_MUI, C_MLS, C_M01 = 0, 128, 256, 320, 448, 576, 704
NCONST = 704 + 512
V_MULR = 0
V_MU = 4
V_W0, V_A0, V_KK, V_KA, V_RK, V_LW, V_LB = 28, 36, 44, 52, 60, 68, 76
V_G1 = 84
V_PS = 100
V_G3 = 108
V_FLAG = 124
V_INVC = 125
V_CW = 189
NV = 189 + 176
DFF = 5632
NJ = 44
DECAY_C = 0.6065306597126334


def host_consts():
    c = np.zeros((128, NCONST), np.float32)
    p = np.arange(128)[:, None]
    j = np.arange(128)[None, :]
    c[:, C_ID:C_ID + 128] = (p == j)
    c[:, C_BONES:C_BONES + 128] = (p // 64 == j // 64)
    c[:, C_ISEL:C_ISEL + 64] = (p % 64 == np.arange(64)[None, :])
    c[:, C_MUS:C_MUS + 128] = (p % 64 < j % 64)
    c[:, C_MUI:C_MUI + 128] = (p % 64 <= j % 64)
    c[:, C_MLS:C_MLS + 128] = (p % 64 > j % 64)
    c[:, C_M01:C_M01 + 512] = (np.arange(512)[None, :] % 64 != 0)
    return c


def permA():
    idx = list(range(3072, 3360))
    for q in range(8):
        idx += list(range(q * 128, q * 128 + 128))
        idx += list(range(1024 + q * 128, 1024 + q * 128 + 128))
        idx += list(range(2048 + q * 128, 2048 + q * 128 + 128))
    return np.array(idx)


def host_pvec(inp):
    v = np.zeros((128, NV), np.float32)
    mu = inp["mu_shift"][0]
    v[0:64, 0] = mu[3072:3136]
    v[0:64, 1] = mu[3136:3200]
    v[0:128, 2] = mu[3200:3328]
    v[0:32, 3] = mu[3328:3360]
    for q in range(8):
        for t in range(3):
            v[:, V_MU + 3 * q + t] = mu[t * 1024 + q * 128: t * 1024 + q * 128 + 128]
    for (col, name) in ((V_W0, "w0"), (V_A0, "a0"), (V_KK, "k_k"), (V_KA, "k_a"), (V_RK, "r_k"), (V_LW, "lnx_w"), (V_LB, "lnx_b")):
        a = inp[name][0].reshape(-1)
        for q in range(8):
            v[:, col + q] = a[q * 128:(q + 1) * 128]
    g = inp["norm_pre_mix"][0]
    g3 = inp["norm_pre_ffn"][0]
    for k in range(16):
        v[:, V_G1 + k] = g[k * 128:(k + 1) * 128]
        v[:, V_G3 + k] = g3[k * 128:(k + 1) * 128]
    psc = inp["pool_scale"][0]
    for k in range(8):
        v[:, V_PS + k] = psc[k * 128:(k + 1) * 128]
    cw, cbias = inp["conv_w"][0], inp["conv_b"][0]
    for j in range(NJ):
        for t in range(3):
            v[:, V_CW + 4 * j + t] = cw[t, j * 128:(j + 1) * 128]
        v[:, V_CW + 4 * j + 3] = cbias[j * 128:(j + 1) * 128]
    return v


def core_pvec(base, p):
    v = base.copy()
    v[:, V_FLAG] = float(p)
    for gi, win in enumerate((2, 4, 8, 16)):
        for j in range(16):
            pos = p * 1024 + j
            v[:, V_INVC + gi * 16 + j] = 1.0 / min(win, pos + 1)
    return v


class _TAlias:
    def __init__(self, phys, alias, prefix="t_"):
        self.phys = phys
        self.alias = alias
        self.prefix = prefix

    def _n(self, n):
        return self.alias.get(n, n)

    def __getitem__(self, n):
        return self.phys[self._n(n)]

    def key(self, n):
        return self.prefix + self._n(n)


class StopBuild(Exception):
    pass


class Builder:
    stop_at = None

    def ckpt(self, name):
        if self.stop_at == name and not self.S.frozen:
            print("frozen at", name)
            self.S.frozen = True
            self.dbg = set()

    def __init__(self, dbg=None, stages="A"):
        self.dbg = dbg or set()
        self.stages = stages
        self.nc = bass.Bass("TRN2", target_bir_lowering=False)
        self.ins = {}
        self.outs = {}
        self.psn = 0

    def din(self, name, shape, dt=F32):
        t = self.nc.dram_tensor(name, list(shape), dt, kind="ExternalInput").ap()
        self.ins[name] = t
        return t

    def dout(self, name, shape, dt=F32):
        t = self.nc.dram_tensor(name, list(shape), dt, kind="ExternalOutput").ap()
        self.outs[name] = t
        return t

    def I(self, eng, meth, *args, reads=(), writes=(), track=True, **kw):
        return self.S.op(eng, lambda e: getattr(e, meth)(*args, **kw), reads=reads, writes=writes, track=track)

    def stage_out(self, name, tile_ap, shape, key):
        scr = self.nc.dram_tensor("scr_" + name, list(shape), F32).ap()
        self.S.dma(SP, scr, tile_ap, reads=[key], writes=["scr_" + name], ring="spo")
        return scr

    def bg(self, dst, src, name):
        self.S.dma(SP, dst, src, reads=["scr_" + name], ring="bg", allow_slow_non_contiguous=True)

    def sb(self, es, name, shape, dt=F32):
        return es.enter_context(self.nc.sbuf_tensor(name, list(shape), dt))

    def psum(self):
        i = self.psn % 8
        self.psn += 1
        return self.PS[i], "ps%d" % i

    def build(self):
        nc = self.nc
        xseq = self.xseq = self.din("xseq", [NPR, D])
        xsamp = self.xsamp = self.din("xsamp", [NSM, D])
        wA = self.din("wA", [D, DS])
        consts = self.din("consts", [128, NCONST])
        pvec = self.din("pvec", [128, NV])
        w2 = self.din("w2", [64, 1024])
        a2 = self.din("a2", [64, 1024])
        g2 = self.din("g2", [160, 1024])
        self.wP = self.din("wP", [D, 1024])
        self.wG = self.din("wG", [D, 16, 256])
        self.wa = self.din("wa", [1024, D])
        self.wbr = self.din("wbr", [1024, D])
        self.poolw = self.din("poolw", [4, 256, 256])
        self.wout = self.din("wout", [D, D])
        self.gpost = self.din("gpost", [128, 2, D])
        self.wF = self.din("wF", [D, NJ, 256])
        self.wfo = self.din("wfo", [DFF, D])
        self.st_pool = self.din("st_pool", [16, 15, 1024])
        self.st_conv = self.din("st_conv", [16, 2, DFF])
        self.o_pshift = self.dout("o_pshift", [DS])
        self.o_pwkv = self.dout("o_pwkv", [16, 64, 64])
        self.o_ppool = self.dout("o_ppool", [15, 1024])
        self.o_pconv = self.dout("o_pconv", [2, DFF])
        self.o_spool = self.dout("o_spool", [16, 15, 1024])
        self.o_sconv = self.dout("o_sconv", [16, 2, DFF])
        self.o_y = self.dout("o_y", [1024 + NSM, D])
        self.st_shift = self.din("st_shift", [16, DS])
        self.st_wkv = self.din("st_wkv", [256, 4096])
        self.o_sshift = self.dout("o_sshift", [16, DS])
        self.o_swkv = self.dout("o_swkv", [256, 4096])
        self.x1s = nc.dram_tensor("x1s", [NOWN, D], F32).ap()
        self.scrS = nc.dram_tensor("scrS", [NSM, 6, 1024], F32).ap()
        self.scrY = nc.dram_tensor("scrY", [NSM, 1024], F32).ap()
        self.scrH = nc.dram_tensor("scrH", [128, 16, NOWN], BF16).ap()
        if "ya" in self.dbg:
            self.o_ya = self.dout("d_ya", [128, 8, NOWN])
        with ExitStack() as es:
            S = self.S = Sched(nc, es)
            self.PS = [es.enter_context(nc.psum_tensor("ps%d" % i, [128, 512], F32)) for i in range(8)]
            cf = self.cf = self.sb(es, "cf", [128, NCONST])
            cb = self.cb = self.sb(es, "cb", [128, 320], BF16)
            pv = self.pv = self.sb(es, "pv", [128, NV])
            S.dma(SP, cf[:], consts, writes=["cf"])
            S.dma(SP, pv[:], pvec, writes=["pv"])
            S.dma(POOL, cb[:], consts[:, 0:320], writes=["cb"])
            self.wslot = 0
            bufX = self.bufX = self.sb(es, "bufX", [128, 16, NOWN], BF16)
            self.yaT = bufX[:, 0:8, :]
            self.ybT = bufX[:, 8:16, :]
            with ExitStack() as esw:
                self.wb = [self.sb(esw, "wb%d" % i, [128, 16, 384], BF16) for i in range(2)]
                self.wA_ap = wA
                with ExitStack() as esA:
                    w2b = self.w2b = self.sb(esA, "w2b", [64, 1024], BF16)
                    a2b = self.a2b = self.sb(esA, "a2b", [64, 1024], BF16)
                    g2a = self.g2a = self.sb(esA, "g2a", [128, 1024], BF16)
                    g2b = self.g2b = self.sb(esA, "g2b", [128, 1024], BF16)
                    S.dma(POOL, w2b[:], w2, writes=["w2b"])
                    S.dma(POOL, a2b[:], a2, writes=["a2b"])
                    S.dma(POOL, g2a[:], g2[0:128, :], writes=["g2a"])
                    self.I(POOL, "memset", g2b[:], 0.0, writes=["g2b"])
                    S.dma(POOL, g2b[0:32, :], g2[128:160, :], writes=["g2b"])
                    if "S" in self.stages:
                        self.stageS()
                    self.stageA(esA, xseq, wA)
                if "ya" in self.dbg:
                    with ExitStack() as esd:
                        yaf = self.sb(esd, "yaf", [128, 8, NOWN], F32)
                        self.I(DVE, "tensor_copy", yaf[:], self.yaT, reads=["yaT"], writes=["yaf"])
                        S.dma(SP, self.o_ya, yaf[:], reads=["yaf"], ring="spo")
                        S.barrier()
                if "B" in self.stages:
                    with ExitStack() as esm:
                        self.mT = self.sb(esm, "mT", [128, 16, NOWN], BF16)
                        with ExitStack() as esb:
                            self.hTo = self.sb(esb, "hTo", [128, 16, NOWN], BF16)
                            self.stageB12(esb)
                        self.stageB3(esm)
            if "C" in self.stages:
                self.stageC(es)
            S.emit()
        return nc

    def tok_groups(self):
        return [(0, 512), (512, 512), (1024, NOWN - 1024)]

    def stageB12(self, es_outer):
        S = self.S
        pv, cf, cb = self.pv, self.cf, self.cb
        hT = self.hTo
        with ExitStack() as es:
            sb = lambda n, s, d=F32: self.sb(es, n, s, d)
            self.xt = [sb("xtB", [128, D])]
            self.xb = [sb("xbB", [128, D], BF16)]
            self.xst = [sb("xstB", [128, 4])]
            for (c0, c1, k) in ((0, 32, "scrH_1"), (32, 544, "scrH_2"), (544, 1056, "scrH_3"), (1056, NOWN, "scrH_s")):
                S.dma(SP, hT[:, :, c0:c1], self.scrH[:, :, c0:c1], reads=[k], writes=["hTo"])
            self.ckpt("B0")
            ybT = self.ybT
            with ExitStack() as esp:
                sbp = lambda n, s, d=F32: self.sb(esp, n, s, d)
                pwb = sbp("pwb", [128, 4, 2, 256], BF16)
                S.dma(POOL, pwb[:], self.poolw.rearrange("g (k p) n -> p g k n", p=128), writes=["pwb"])
                phT = sbp("phT", [128, 8, 16, 15])
                for half in range(2):
                    sp_t = sbp("sp_t%d" % half, [120, 1024])
                    S.dma(SP, sp_t[:], self.st_pool[8 * half: 8 * half + 8].rearrange("b j c -> (b j) c"), writes=["sp_t%d" % half])
                    for c4 in range(2):
                        ps, pk = self.psum()
                        for ch in range(4):
                            self.I(PE, "transpose", ps[:, ch * 120:(ch + 1) * 120], sp_t[:, (c4 * 4 + ch) * 128:(c4 * 4 + ch + 1) * 128], cf[0:120, 0:120],
                                   reads=["sp_t%d" % half, "cf"], writes=[pk], track=(ch == 3))
                        self.I(DVE, "tensor_copy", phT[:, c4 * 4:c4 * 4 + 4, 8 * half:8 * half + 8, :],
                               ps[:, 0:480].rearrange("p (a b j) -> p a b j", a=4, b=8), reads=[pk], writes=["phT"])
                self.ckpt("B1a")
                S.dma(SP, self.o_spool[:, 0:11, :], self.st_pool[:, 4:15, :], ring="spo")
                zp = sbp("zp", [128, NOWN])
                pa_ = sbp("ppA", [128, NOWN])
                pb_ = sbp("ppB", [128, NOWN])
                dT = sbp("dT", [128, 8, NOWN], BF16)
                self.I(POOL, "memset", dT[:], 0.0, writes=["dT"])
                ppo = sbp("ppo", [128, 8, 15])
                spo = sbp("spo", [128, 8, 16, 4])
                bs = sbp("bs", [128, 16, 19])
                bs2 = sbp("bs2", [128, 16, 19])
                slabs = [(self.wP[:, ch * 128:(ch + 1) * 128], 128) for ch in range(8)]
                stream = self.slab_stream(slabs)
                NP_ = 1056
                for ch in range(8):
                    wt, wk = next(stream)
                    gi = ch // 2
                    win = 2 << gi
                    for (t0, tn) in self.tok_groups():
                        ps, pk = self.psum()
                        for k in range(16):
                            self.I(PE, "matmul", ps[:, 0:tn], wt[:, k, 0:128], hT[:, k, t0:t0 + tn], start=(k == 0), stop=(k == 15),
                                   reads=[wk, "hTo"], writes=[pk], track=(k == 15))
                        self.I(ACT, "copy", zp[:, t0:t0 + tn], ps[:, 0:tn], reads=[pk], writes=["zp"])
                    src, ksrc = zp, "zp"
                    bufs = [(pa_, "ppA"), (pb_, "ppB")]
                    for j in range(gi + 1):
                        sh = 1 << j
                        dst, kdst = bufs[j % 2]
                        self.I(DVE if j % 2 == 0 else POOL, "tensor_tensor", dst[:, sh:NP_], src[:, sh:NP_], src[:, 0:NP_ - sh], ALU.add,
                               reads=[ksrc], writes=[kdst])
                        src, ksrc = dst, kdst
                    lo = win - 1
                    self.I(DVE, "scalar_tensor_tensor", dT[:, ch, lo:NP_], src[:, lo:NP_], 1.0 / win, zp[:, lo:NP_], ALU.mult, ALU.subtract,
                           reads=[ksrc, "zp"], writes=["dT"])
                    other, kother = bufs[(gi + 1) % 2]
                    self.I(POOL, "tensor_tensor", other[:, 32:48], src[:, 32:48], pv[:, V_INVC + gi * 16: V_INVC + gi * 16 + 16], ALU.mult,
                           reads=[ksrc, "pv"], writes=[kother])
                    self.I(POOL, "tensor_tensor", dT[:, ch, 32:48], other[:, 32:48], zp[:, 32:48], ALU.subtract, reads=[kother, "zp"], writes=["dT"])
                    self.I(POOL, "tensor_copy", ppo[:, ch, :], zp[:, NP_ - 15:NP_], reads=["zp"], writes=["ppo"])
                    zps = zp[:, NP_:NOWN].rearrange("p (b t) -> p b t", t=4)
                    self.I(POOL, "tensor_copy", bs[:, :, 0:15], phT[:, ch, :, :], reads=["phT"], writes=["bs"])
                    self.I(POOL, "tensor_copy", bs[:, :, 15:19], zps, reads=["zp"], writes=["bs"])
                    self.I(POOL, "tensor_copy", spo[:, ch, :, :], zps, reads=["zp"], writes=["spo"])
                    ssrc, kss = bs, "bs"
                    sbufs = [(bs2, "bs2"), (bs, "bs")]
                    for j in range(gi + 1):
                        sh = 1 << j
                        dst, kdst = sbufs[j % 2]
                        self.I(DVE, "tensor_tensor", dst[:, :, sh:19], ssrc[:, :, sh:19], ssrc[:, :, 0:19 - sh], ALU.add, reads=[kss], writes=[kdst])
                        ssrc, kss = dst, kdst
                    self.I(DVE, "scalar_tensor_tensor", dT[:, ch, NP_:NOWN].rearrange("p (b t) -> p b t", t=4), ssrc[:, :, 15:19], 1.0 / win, zps,
                           ALU.mult, ALU.subtract, reads=[kss, "zp"], writes=["dT"])
                self.ckpt("B1b")
                scr = self.stage_out("ppo", ppo[:], [128, 8, 15], "ppo")
                for ch in range(8):
                    self.bg(self.o_ppool[:, ch * 128:(ch + 1) * 128].rearrange("j p -> p j"), scr[:, ch, :], "ppo")
                sprow = sbp("sprow", [NSM, 1024])
                for c4 in range(2):
                    ps, pk = self.psum()
                    for ch in range(4):
                        self.I(PE, "transpose", ps[0:NSM, ch * 128:(ch + 1) * 128], spo[:, c4 * 4 + ch, :, :].rearrange("p b t -> p (b t)"), cf[:, C_ID:C_ID + 128],
                               reads=["spo", "cf"], writes=[pk], track=(ch == 3))
                    self.I(DVE, "tensor_copy", sprow[:, c4 * 512:(c4 + 1) * 512], ps[0:NSM, :], reads=[pk], writes=["sprow"])
                for b in range(16):
                    S.dma(SP, self.o_spool[b, 11:15, :], sprow[4 * b:4 * b + 4, :], reads=["sprow"], ring="spo")
                self.ckpt("B1c")
                for oc in range(8):
                    gi, o2 = oc // 2, oc % 2
                    for (t0, tn) in self.tok_groups():
                        ps, pk = self.psum()
                        for kk in range(2):
                            self.I(PE, "matmul", ps[:, 0:tn], pwb[:, gi, kk, o2 * 128:(o2 + 1) * 128], dT[:, 2 * gi + kk, t0:t0 + tn],
                                   start=(kk == 0), stop=(kk == 1), reads=["pwb", "dT"], writes=[pk], track=(kk == 1))
                        self.I(ACT, "activation", ybT[:, oc, t0:t0 + tn], ps[:, 0:tn], AF.Copy, scale=pv[:, V_PS + oc: V_PS + oc + 1],
                               reads=[pk, "pv"], writes=["ybT"])
                S.barrier()
            self.ckpt("B1d")
            mT = self.mT
            tmp = [sb("b2t%d" % i, [128, 512]) for i in range(4)]
            njs = 16

            def issue(j):
                slot = self.wslot % 2
                self.wslot += 1
                key = "wb%d" % slot
                w = self.wb[slot]
                wflat = w[:].rearrange("p a b -> p (a b)")
                S.dma(POOL, wflat[:, 0:4096].rearrange("p (k n) -> p k n", n=256), self.wG[:, j, :].rearrange("(k p) n -> p k n", p=128), writes=[key])
                S.dma(POOL, wflat[:, 4096:5120].rearrange("p (k n) -> p k n", n=128), self.wa[:, j * 128:(j + 1) * 128].rearrange("(k p) n -> p k n", p=128), writes=[key])
                S.dma(POOL, wflat[:, 5120:6144].rearrange("p (k n) -> p k n", n=128), self.wbr[:, j * 128:(j + 1) * 128].rearrange("(k p) n -> p k n", p=128), writes=[key])
                return (wflat, key)
            cur = issue(0)
            for j in range(njs):
                nxt = issue(j + 1) if j + 1 < njs else None
                wflat, wk = cur
                wg = wflat[:, 0:4096].rearrange("p (k n) -> p k n", n=256)
                wa_ = wflat[:, 4096:5120].rearrange("p (k n) -> p k n", n=128)
                wb_ = wflat[:, 5120:6144].rearrange("p (k n) -> p k n", n=128)
                for (t0, tn) in self.tok_groups():
                    psa, pka = self.psum()
                    psb, pkb = self.psum()
                    ppa, pkpa = self.psum()
                    ppb, pkpb = self.psum()
                    for k in range(16):
                        self.I(PE, "matmul", psa[:, 0:tn], wg[:, k, 0:128], hT[:, k, t0:t0 + tn], start=(k == 0), stop=(k == 15),
                               reads=[wk, "hTo"], writes=[pka], track=(k == 15))
                    for k in range(16):
                        self.I(PE, "matmul", psb[:, 0:tn], wg[:, k, 128:256], hT[:, k, t0:t0 + tn], start=(k == 0), stop=(k == 15),
                               reads=[wk, "hTo"], writes=[pkb], track=(k == 15))
                    for k in range(8):
                        self.I(PE, "matmul", ppa[:, 0:tn], wa_[:, k, :], self.yaT[:, k, t0:t0 + tn], start=(k == 0), stop=(k == 7),
                               reads=[wk, "yaT"], writes=[pkpa], track=(k == 7))
                    for k in range(8):
                        self.I(PE, "matmul", ppb[:, 0:tn], wb_[:, k, :], ybT[:, k, t0:t0 + tn], start=(k == 0), stop=(k == 7),
                               reads=[wk, "ybT"], writes=[pkpb], track=(k == 7))
                    self.I(ACT, "activation", tmp[0][:, 0:tn], psa[:, 0:tn], AF.Sigmoid, reads=[pka], writes=["b2t0"])
                    self.I(ACT, "activation", tmp[1][:, 0:tn], psb[:, 0:tn], AF.Sigmoid, reads=[pkb], writes=["b2t1"])
                    self.I(DVE, "tensor_tensor", tmp[2][:, 0:tn], tmp[0][:, 0:tn], ppa[:, 0:tn], ALU.mult, reads=["b2t0", pkpa], writes=["b2t2"])
                    self.I(DVE, "tensor_tensor", tmp[3][:, 0:tn], tmp[1][:, 0:tn], ppb[:, 0:tn], ALU.mult, reads=["b2t1", pkpb], writes=["b2t3"])
                    self.I(POOL, "tensor_tensor", mT[:, j, t0:t0 + tn], tmp[2][:, 0:tn], tmp[3][:, 0:tn], ALU.add, reads=["b2t2", "b2t3"], writes=["mT"])
                cur = nxt
            S.barrier()

    def tok_tiles(self):
        self.ckpt("B2")
        return [(0, 32)] + [(32 + 128 * i, 128) for i in range(8)] + [(1056, NSM)]

    def stageB3(self, es_outer):
        S = self.S
        pv, cf, cb = self.pv, self.cf, self.cb
        mT, h2T = self.mT, self.bufX
        with ExitStack() as es:
            sb = lambda n, s, d=F32: self.sb(es, n, s, d)
            woutb = sb("woutb", [128, 16, D], BF16)
            for n in range(4):
                S.dma(POOL, woutb[:, :, n * 512:(n + 1) * 512], self.wout[:, n * 512:(n + 1) * 512].rearrange("(k p) n -> p k n", p=128), writes=["woutb%d" % n])
            gp = sb("gp", [128, D])
            S.dma(SP, gp[:], self.gpost[:, 0, :], writes=["gp"])
            mo = sb("mo", [128, D])
            xt = sb("xt3", [128, D])
            x1 = sb("x1t", [128, D])
            xb = sb("xb3", [128, D], BF16)
            st = sb("st3", [128, 16])
            for (c0, nt) in self.tok_tiles():
                xrows = self.xseq[OWN0 + c0: OWN0 + c0 + nt, :] if c0 < 1056 else self.xsamp
                S.dma(SP, xt[0:nt, :], xrows, writes=["xt3"])
                self.I(POOL, "memset", st[:, 0:4], 0.0, writes=["st3"])
                self.ckpt("B3pre")
                for n in range(4):
                    ps, pk = self.psum()
                    for k in range(16):
                        self.I(PE, "matmul", ps[0:nt, :], mT[:, k, c0:c0 + nt], woutb[:, k, n * 512:(n + 1) * 512], start=(k == 0), stop=(k == 15),
                               reads=["mT", "woutb%d" % n], writes=[pk], track=(k == 15))
                    self.I(DVE, "tensor_copy", mo[0:nt, n * 512:(n + 1) * 512], ps[0:nt, :], reads=[pk], writes=["mo"])
                    self.I(ACT, "activation", xb[0:nt, n * 512:(n + 1) * 512], mo[0:nt, n * 512:(n + 1) * 512], AF.Square, accum_out=st[0:nt, n:n + 1], reads=["mo"], writes=["xb3", "st3"])
                self.ckpt("B3a")
                self.I(DVE, "tensor_reduce", st[0:nt, 4:5], st[0:nt, 0:4], AX.X, ALU.add, reads=["st3"], writes=["st3"])
                self.I(DVE, "tensor_scalar", st[0:nt, 5:6], st[0:nt, 4:5], 1.0 / D, 1e-6, ALU.mult, ALU.add, reads=["st3"], writes=["st3"])
                self.I(ACT, "activation", st[0:nt, 6:7], st[0:nt, 5:6], AF.Sqrt, reads=["st3"], writes=["st3"])
                self.I(DVE, "reciprocal", st[0:nt, 7:8], st[0:nt, 6:7], reads=["st3"], writes=["st3"])
                self.I(DVE, "scalar_tensor_tensor", mo[0:nt, :], mo[0:nt, :], st[0:nt, 7:8], gp[0:nt, :], ALU.mult, ALU.mult, reads=["mo", "st3", "gp"], writes=["mo"])
                self.I(POOL, "tensor_tensor", x1[0:nt, :], mo[0:nt, :], xt[0:nt, :], ALU.add, reads=["mo", "xt3"], writes=["x1t"])
                self.ckpt("B3b")
                S.dma(SP, self.x1s[c0:c0 + nt, :], x1[0:nt, :], reads=["x1t"], writes=["x1s"])
                self.ckpt("B3c")
                self.norm_T(x1, "x1t", nt, h2T[:, :, c0:c0 + nt], "h2T", xb, "xb3", st, "st3", 8, V_G3)
                self.ckpt("B3d")
            S.barrier()

    def norm_T(self, xt, kx, ntok, hT, hkey, xb, kb, st, ks, sc0, gbase):
        self.I(POOL, "memset", st[:, sc0:sc0 + 1], 0.0, writes=[ks])
        self.I(ACT, "activation", xb[0:ntok, :], xt[0:ntok, :], AF.Square, accum_out=st[0:ntok, sc0:sc0 + 1], reads=[kx], writes=[kb, ks])
        self.I(DVE, "tensor_scalar", st[0:ntok, sc0 + 1:sc0 + 2], st[0:ntok, sc0:sc0 + 1], 1.0 / D, 1e-6, ALU.mult, ALU.add, reads=[ks], writes=[ks])
        self.I(ACT, "activation", st[0:ntok, sc0 + 2:sc0 + 3], st[0:ntok, sc0 + 1:sc0 + 2], AF.Sqrt, reads=[ks], writes=[ks])
        self.I(DVE, "reciprocal", st[0:ntok, sc0 + 3:sc0 + 4], st[0:ntok, sc0 + 2:sc0 + 3], reads=[ks], writes=[ks])
        self.I(ACT, "activation", xb[0:ntok, :], xt[0:ntok, :], AF.Copy, scale=st[0:ntok, sc0 + 3:sc0 + 4], reads=[kx, ks, kb], writes=[kb])
        for half in range(2):
            ps, pk = self.psum()
            pT = ps[:].bitcast(BF16).rearrange("p (a b) -> p a b", b=128)
            for k8 in range(8):
                kc = half * 8 + k8
                self.I(PE, "transpose", pT[:, k8, 0:ntok], xb[0:ntok, kc * 128:(kc + 1) * 128], self.cb[0:ntok, 0:ntok],
                       reads=[kb, "cb"], writes=[pk], track=(k8 == 7))
            gcol = self.pv[:, gbase + half * 8: gbase + half * 8 + 8].unsqueeze(2).to_broadcast([128, 8, ntok])
            self.I(DVE, "tensor_tensor", hT[:, half * 8:half * 8 + 8, :], pT[:, :, 0:ntok], gcol, ALU.mult, reads=[pk, "pv"], writes=[hkey])

    def stageC(self, es_outer):
        S = self.S
        pv, cf, cb = self.pv, self.cf, self.cb
        h2T = self.bufX
        NA = 1024 + NSM
        with ExitStack() as es:
            sb = lambda n, s, d=F32: self.sb(es, n, s, d)
            actT = sb("actT", [128, NJ, NA], BF16)
            with ExitStack() as es1:
                sb1 = lambda n, s, d=F32: self.sb(es1, n, s, d)
                wf = [sb1("wf%d" % i, [128, 16, 256], BF16) for i in range(2)]
                gt = [sb1("gt%d" % i, [128, NOWN]) for i in range(2)]
                up = [sb1("up%d" % i, [128, NOWN]) for i in range(2)]
                cv = sb1("cv", [128, NA])
                ge = sb1("ge", [128, NA])
                gs6 = sb1("gs6", [128, 16, 6])
                chT = sb1("chT", [128, NJ, 32])
                pco = sb1("pco", [128, NJ, 2])
                sco = sb1("sco", [128, NJ, 16, 2])
                ct = sb1("ct", [32, 1408])
                stc = self.st_conv.rearrange("b r c -> (b r) c")
                for pc in range(4):
                    S.dma(SP, ct[:], stc[:, pc * 1408:(pc + 1) * 1408], writes=["ct"])
                    ps, pk = self.psum()
                    for jj in range(11):
                        self.I(PE, "transpose", ps[:, jj * 32:(jj + 1) * 32], ct[:, jj * 128:(jj + 1) * 128], cf[0:32, 0:32],
                               reads=["ct", "cf"], writes=[pk], track=(jj == 10))
                    self.I(DVE, "tensor_copy", chT[:, pc * 11:(pc + 1) * 11, :], ps[:, 0:352].rearrange("p (a b) -> p a b", b=32), reads=[pk], writes=["chT"])

                def issue(j):
                    slot = j % 2
                    S.dma(POOL, wf[slot][:], self.wF[:, j, :].rearrange("(k p) n -> p k n", p=128), writes=["wf%d" % slot])
                cwc = lambda j, t: pv[:, V_CW + 4 * j + t: V_CW + 4 * j + t + 1]
                issue(0)
                for j in range(NJ):
                    if j + 1 < NJ:
                        issue(j + 1)
                    w, wk = wf[j % 2], "wf%d" % (j % 2)
                    g_, kg = gt[j % 2], "gt%d" % (j % 2)
                    u_, ku = up[j % 2], "up%d" % (j % 2)
                    for (t0, tn) in self.tok_groups():
                        psg, pkg = self.psum()
                        psu, pku = self.psum()
                        for k in range(16):
                            self.I(PE, "matmul", psg[:, 0:tn], w[:, k, 0:128], h2T[:, k, t0:t0 + tn], start=(k == 0), stop=(k == 15),
                                   reads=[wk, "h2T"], writes=[pkg], track=(k == 15))
                        for k in range(16):
                            self.I(PE, "matmul", psu[:, 0:tn], w[:, k, 128:256], h2T[:, k, t0:t0 + tn], start=(k == 0), stop=(k == 15),
                                   reads=[wk, "h2T"], writes=[pku], track=(k == 15))
                        self.I(ACT, "copy", g_[:, t0:t0 + tn], psg[:, 0:tn], reads=[pkg], writes=[kg])
                        self.I(DVE, "tensor_copy", u_[:, t0:t0 + tn], psu[:, 0:tn], reads=[pku], writes=[ku])
                    self.I(POOL, "tensor_scalar", g_[:, 30:32], g_[:, 30:32], pv[:, V_FLAG:V_FLAG + 1], None, ALU.mult, reads=[kg, "pv"], writes=[kg])
                    self.I(ACT, "activation", cv[:, 0:1024], g_[:, 32:1056], AF.Identity, bias=cwc(j, 3), scale=cwc(j, 2), reads=[kg, "pv"], writes=["cv"])
                    self.I(DVE, "scalar_tensor_tensor", cv[:, 0:1024], g_[:, 31:1055], cwc(j, 1), cv[:, 0:1024], ALU.mult, ALU.add, reads=[kg, "cv", "pv"], writes=["cv"])
                    self.I(DVE, "scalar_tensor_tensor", cv[:, 0:1024], g_[:, 30:1054], cwc(j, 0), cv[:, 0:1024], ALU.mult, ALU.add, reads=[kg, "cv", "pv"], writes=["cv"])
                    gss = g_[:, 1056:NOWN].rearrange("p (b t) -> p b t", t=4)
                    self.I(POOL, "tensor_copy", gs6[:, :, 0:2], chT[:, j, :].rearrange("p (b r) -> p b r", r=2), reads=["chT"], writes=["gs6"])
                    self.I(POOL, "tensor_copy", gs6[:, :, 2:6], gss, reads=[kg], writes=["gs6"])
                    cvs = cv[:, 1024:NA].rearrange("p (b t) -> p b t", t=4)
                    self.I(ACT, "activation", cvs, gs6[:, :, 2:6], AF.Identity, bias=cwc(j, 3), scale=cwc(j, 2), reads=["gs6", "pv"], writes=["cv"])
                    self.I(DVE, "scalar_tensor_tensor", cvs, gs6[:, :, 1:5], cwc(j, 1), cvs, ALU.mult, ALU.add, reads=["gs6", "cv", "pv"], writes=["cv"])
                    self.I(DVE, "scalar_tensor_tensor", cvs, gs6[:, :, 0:4], cwc(j, 0), cvs, ALU.mult, ALU.add, reads=["gs6", "cv", "pv"], writes=["cv"])
                    self.I(POOL, "tensor_copy", pco[:, j, :], g_[:, 1054:1056], reads=[kg], writes=["pco"])
                    self.I(POOL, "tensor_copy", sco[:, j, :, :], gss[:, :, 2:4], reads=[kg], writes=["sco"])
                    self.I(ACT, "activation", ge[:, :], cv[:, :], AF.Gelu_apprx_tanh, reads=["cv"], writes=["ge"])
                    self.I(DVE, "tensor_tensor", actT[:, j, 0:1024], ge[:, 0:1024], u_[:, 32:1056], ALU.mult, reads=["ge", ku], writes=["actT"])
                    self.I(POOL, "tensor_tensor", actT[:, j, 1024:NA], ge[:, 1024:NA], u_[:, 1056:NOWN], ALU.mult, reads=["ge", ku], writes=["actT"])
                scr = self.stage_out("pco", pco[:], [128, NJ, 2], "pco")
                for r in range(2):
                    self.bg(self.o_pconv[r].rearrange("(j p) -> p j", p=128), scr[:, :, r], "pco")
                scrow = sb1("scrow", [32, 1408])
                for pc in range(4):
                    for j4 in range(0, 11, 4):
                        nj = min(4, 11 - j4)
                        ps, pk = self.psum()
                        for jj in range(nj):
                            j = pc * 11 + j4 + jj
                            self.I(PE, "transpose", ps[0:32, jj * 128:(jj + 1) * 128], sco[:, j, :, :].rearrange("p b r -> p (b r)"), cf[:, C_ID:C_ID + 128],
                                   reads=["sco", "cf"], writes=[pk], track=(jj == nj - 1))
                        self.I(DVE, "tensor_copy", scrow[:, j4 * 128:(j4 + nj) * 128], ps[0:32, 0:nj * 128], reads=[pk], writes=["scrow"])
                    S.dma(SP, self.o_sconv.rearrange("b r c -> (b r) c")[:, pc * 1408:(pc + 1) * 1408], scrow[:], reads=["scrow"], ring="spo")
                S.barrier()
            with ExitStack() as es2:
                sb2 = lambda n, s, d=F32: self.sb(es2, n, s, d)
                dummy = sb2("dmy2", [128, 2])
                self.I(POOL, "memset", dummy[:], 0.0, writes=["h2T", "dmy2"])
                bxf = self.bufX[:].rearrange("p a b -> p (a b)")
                wo = [bxf[:, i * 5632:(i + 1) * 5632].rearrange("p (k n) -> p k n", n=512) for i in range(2)]
                fo = sb2("fo", [128, 5, D])
                gp2 = sb2("gp2", [128, D])
                S.dma(SP, gp2[:], self.gpost[:, 1, :], writes=["gp2"])
                x1t = sb2("x1r", [128, D])
                st = sb2("stc", [128, 5, 8])
                xb = sb2("xbc", [128, 512], BF16)
                tiles = [(128 * i, 128) for i in range(8)] + [(1024, NSM)]
                sets = [tiles[0:5], tiles[5:9]]
                wn = 0
                for tset in sets:
                    self.I(POOL, "memset", st[:], 0.0, writes=["stc"])
                    for n in range(4):
                        banks = [self.psum() for _ in tset]
                        for kp in range(4):
                            slot = wn % 2
                            wn += 1
                            S.dma(POOL, wo[slot], self.wfo[kp * 1408:(kp + 1) * 1408, n * 512:(n + 1) * 512].rearrange("(k p) n -> p k n", p=128),
                                  writes=["wo%d" % slot])
                            for ti, (c0, nt) in enumerate(tset):
                                ps, pk = banks[ti]
                                for jj in range(11):
                                    j = kp * 11 + jj
                                    self.I(PE, "matmul", ps[0:nt, :], actT[:, j, c0:c0 + nt], wo[slot][:, jj, :], start=(j == 0), stop=(j == NJ - 1),
                                           reads=["actT", "wo%d" % slot], writes=[pk], track=(jj == 10))
                        for ti, (c0, nt) in enumerate(tset):
                            ps, pk = banks[ti]
                            self.I(DVE, "tensor_copy", fo[0:nt, ti, n * 512:(n + 1) * 512], ps[0:nt, :], reads=[pk], writes=["fo%d" % ti])
                            self.I(ACT, "activation", xb[0:nt, :], fo[0:nt, ti, n * 512:(n + 1) * 512], AF.Square, accum_out=st[0:nt, ti, n:n + 1], reads=["fo%d" % ti], writes=["xbc", "stc"])
                    for ti, (c0, nt) in enumerate(tset):
                        self.I(DVE, "tensor_reduce", st[0:nt, ti, 4:5], st[0:nt, ti, 0:4], AX.X, ALU.add, reads=["stc"], writes=["stc"])
                        self.I(DVE, "tensor_scalar", st[0:nt, ti, 5:6], st[0:nt, ti, 4:5], 1.0 / D, 1e-6, ALU.mult, ALU.add, reads=["stc"], writes=["stc"])
                        self.I(ACT, "activation", st[0:nt, ti, 6:7], st[0:nt, ti, 5:6], AF.Sqrt, reads=["stc"], writes=["stc"])
                        self.I(DVE, "reciprocal", st[0:nt, ti, 7:8], st[0:nt, ti, 6:7], reads=["stc"], writes=["stc"])
                        xc0 = 32 + c0
                        S.dma(SP, x1t[0:nt, :], self.x1s[xc0:xc0 + nt, :], reads=["x1s"], writes=["x1r"])
                        self.I(DVE, "scalar_tensor_tensor", fo[0:nt, ti, :], fo[0:nt, ti, :], st[0:nt, ti, 7:8], gp2[0:nt, :], ALU.mult, ALU.mult,
                               reads=["fo%d" % ti, "stc", "gp2"], writes=["fo%d" % ti])
                        self.I(POOL, "tensor_tensor", fo[0:nt, ti, :], fo[0:nt, ti, :], x1t[0:nt, :], ALU.add, reads=["fo%d" % ti, "x1r"], writes=["fo%d" % ti])
                        S.dma(SP, self.o_y[c0:c0 + nt, :], fo[0:nt, ti, :], reads=["fo%d" % ti], ring="spo")

    def make_hT(self, xrows, ntok, hT, hkey, slot):
        S = self.S
        xt, xb, st = self.xt[slot], self.xb[slot], self.xst[slot]
        kx, kb, ks = "xt%d" % slot, "xb%d" % slot, "xst%d" % slot
        S.dma(SP, xt[0:ntok, :], xrows, writes=[kx])
        self.I(POOL, "memset", st[:, 0:1], 0.0, writes=[ks])
        self.I(ACT, "activation", xb[0:ntok, :], xt[0:ntok, :], AF.Square, accum_out=st[0:ntok, 0:1], reads=[kx], writes=[kb, ks])
        self.I(DVE, "tensor_scalar", st[0:ntok, 1:2], st[0:ntok, 0:1], 1.0 / D, 1e-6, ALU.mult, ALU.add, reads=[ks], writes=[ks])
        self.I(ACT, "activation", st[0:ntok, 2:3], st[0:ntok, 1:2], AF.Sqrt, reads=[ks], writes=[ks])
        self.I(DVE, "reciprocal", st[0:ntok, 3:4], st[0:ntok, 2:3], reads=[ks], writes=[ks])
        self.I(ACT, "activation", xb[0:ntok, :], xt[0:ntok, :], AF.Copy, scale=st[0:ntok, 3:4], reads=[kx, ks, kb], writes=[kb])
        for half in range(2):
            ps, pk = self.psum()
            pT = ps[:].bitcast(BF16).rearrange("p (a b) -> p a b", b=128)
            for k8 in range(8):
                kc = half * 8 + k8
                self.I(PE, "transpose", pT[:, k8, 0:ntok], xb[0:ntok, kc * 128:(kc + 1) * 128],
                                                                self.cb[0:ntok, 0:ntok], reads=[kb, "cb"], writes=[pk], track=(k8 == 7))
            gcol = self.pv[:, V_G1 + half * 8: V_G1 + half * 8 + 8].unsqueeze(2).to_broadcast([128, 8, ntok])
            self.I(DVE, "tensor_tensor", hT[:, half * 8:half * 8 + 8, :], pT[:, :, 0:ntok], gcol, ALU.mult, reads=[pk, "pv"], writes=[hkey])

    def slab_stream(self, slabs):
        S = self.S
        n = len(slabs)

        def issue(i):
            ap, ncol = slabs[i]
            slot = self.wslot % 2
            self.wslot += 1
            key = "wb%d" % slot
            S.dma(POOL, self.wb[slot][:, :, 0:ncol], ap.rearrange("(k p) n -> p k n", p=128), writes=[key])
            return (self.wb[slot], key)
        cur = issue(0)
        for i in range(n):
            nxt = issue(i + 1) if i + 1 < n else None
            yield cur
            cur = nxt

    def proj(self, ps, pk, wt, wkey, c0, M, hT, hkey, N):
        S = self.S
        for k in range(16):
            self.I(PE, "matmul", ps[0:M, 0:N], wt[:, k, c0:c0 + M], hT[:, k, 0:N], start=(k == 0), stop=(k == 15), reads=[wkey, hkey], writes=[pk], track=(k == 15))

    def stageS(self):
        S = self.S
        pv, cf, cb = self.pv, self.cf, self.cb
        wA = self.wA_ap
        N = NSM
        with ExitStack() as es0:
            sb0 = lambda n, s, d=F32: self.sb(es0, n, s, d)
            sbon = sb0("sbon", [128, 8, N], BF16)
            sgst = sb0("sgst", [128, 8, N], BF16)
            self._stageS_proj(es0, sbon, sgst)
            self._stageS_scan(es0, sbon, sgst)

    def _stageS_proj(self, es0, sbon, sgst):
        S = self.S
        pv, cf, cb = self.pv, self.cf, self.cb
        wA = self.wA_ap
        N = NSM
        with ExitStack() as es:
            sb = lambda n, s, d=F32: self.sb(es, n, s, d)
            self.xt = [sb("xtS", [128, D])]
            self.xb = [sb("xbS", [128, D], BF16)]
            self.xst = [sb("xstS", [128, 4])]
            hT = sb("hTs", [128, 16, N], BF16)
            self.make_hT(self.xsamp, N, hT[:, :, :], "hTs", 0)
            S.dma(SP, self.scrH[:, :, 1056:NOWN], hT[:, :, :], reads=["hTs"], writes=["scrH_s"])
            shT = sb("shT", [128, 28, 16])
            sst = sb("sst", [16, DS])
            S.dma(SP, sst[:], self.st_shift, writes=["sst"])
            chunks = [(0, 64), (64, 64), (128, 128), (256, 32)] + [(288 + 128 * i, 128) for i in range(24)]
            ps, pk = self.psum()
            for ci, (o, M) in enumerate(chunks):
                self.I(PE, "transpose", ps[0:M, ci * 16:(ci + 1) * 16], sst[:, o:o + M], cf[0:16, 0:16], reads=["sst", "cf"], writes=[pk], track=(ci == 27))
            self.I(DVE, "tensor_copy", shT[:, :, :], ps[:, 0:448].rearrange("p (a b) -> p a b", b=16), reads=[pk], writes=["shT"])
            zls = sb("zls", [128, 28, 16])
            tw = sb("stw", [64, N], BF16)
            zam = sb("szam", [64, N], BF16)
            sga = sb("ssga", [128, N], BF16)
            sgb = sb("ssgb", [128, N], BF16)
            self.I(POOL, "memset", sgb[:], 0.0, writes=["ssgb"])
            tokS = sb("tokS", [N, 6, 1024])
            phys = {n: sb("s_" + n, [128, N]) for n in ("zc", "d", "zr", "zk", "zv", "es", "wd", "as", "kk", "kkn", "km", "dd", "av")}
            phys["hi"] = sb("s_hi", [128, N], BF16)
            phys["lo"] = sb("s_lo", [128, N], BF16)
            T = _TAlias(phys, {"kk2": "d", "rn": "zc", "t1": "dd", "ta": "es", "rk": "wd2"}, prefix="s_")
            phys["wd2"] = sb("s_wd2", [128, N])

            def mix_s(ps, pk, M, ci, out, kout, mu):
                zc, d = T["zc"], T["d"]
                v3 = lambda t: t[0:M, 0:N].rearrange("p (b t) -> p b t", t=4)
                self.I(ACT, "copy", zc[0:M, 0:N], ps[0:M, 0:N], reads=[pk], writes=["s_zc"])
                self.I(DVE, "tensor_tensor", v3(d)[:, :, 1:4], v3(zc)[:, :, 0:3], v3(zc)[:, :, 1:4], ALU.subtract, reads=["s_zc"], writes=["s_d"])
                self.I(DVE, "tensor_tensor", v3(d)[:, :, 0], shT[0:M, ci, :], v3(zc)[:, :, 0], ALU.subtract, reads=["s_zc", "shT"], writes=["s_d"])
                self.I(DVE, "scalar_tensor_tensor", out[0:M, 0:N], d[0:M, 0:N], mu, zc[0:M, 0:N], ALU.mult, ALU.add, reads=["s_d", "s_zc", "pv"], writes=[kout])
                self.I(POOL, "tensor_copy", zls[0:M, ci, :], v3(zc)[:, :, 3], reads=["s_zc"], writes=["zls"])

            slabs = [(wA[:, 0:288], 288)] + [(wA[:, 288 + q * 384: 288 + (q + 1) * 384], 384) for q in range(8)]
            stream = self.slab_stream(slabs)
            wt, wk = next(stream)
            for (ci, (cc0, M, dst, dk, func)) in enumerate(((0, 64, tw, "stw", AF.Tanh), (64, 64, zam, "szam", AF.Copy),
                                                             (128, 128, sga, "ssga", AF.Sigmoid), (256, 32, sgb, "ssgb", AF.Sigmoid))):
                ps, pk = self.psum()
                self.proj(ps, pk, wt, wk, cc0, M, hT, "hTs", N)
                mix_s(ps, pk, M, ci, T["zr"], "s_zr", pv[0:M, ci:ci + 1])
                self.I(ACT, "activation", dst[0:M, 0:N], T["zr"][0:M, 0:N], func, reads=["s_zr"], writes=[dk])
            bonesb = cb[:, C_BONES:C_BONES + 128]
            for q in range(8):
                wt, wk = next(stream)
                col = lambda base: pv[:, base + q: base + q + 1]
                qs = slice(q * 128, (q + 1) * 128)
                for t, nm in enumerate(("zr", "zk", "zv")):
                    ps, pk = self.psum()
                    self.proj(ps, pk, wt, wk, t * 128, 128, hT, "hTs", N)
                    mix_s(ps, pk, 128, 4 + 3 * q + t, T[nm], "s_" + nm, pv[:, V_MU + 3 * q + t: V_MU + 3 * q + t + 1])
                zr, zk, zv = T["zr"], T["zk"], T["zv"]
                ps, pk = self.psum()
                self.I(PE, "matmul", ps[:, 0:N], self.w2b[:, qs], tw[:, 0:N], start=True, stop=True, reads=["w2b", "stw"], writes=[pk])
                self.I(ACT, "activation", T["es"][:, 0:N], ps[:, 0:N], AF.Sigmoid, bias=col(V_W0), reads=[pk, "pv"], writes=["s_es"])
                self.I(ACT, "activation", T["wd"][:, 0:N], T["es"][:, 0:N], AF.Exp, scale=-DECAY_C, reads=["s_es"], writes=["s_wd"])
                ps, pk = self.psum()
                self.I(PE, "matmul", ps[:, 0:N], self.a2b[:, qs], zam[:, 0:N], start=True, stop=True, reads=["a2b", "szam"], writes=[pk])
                self.I(ACT, "activation", T["as"][:, 0:N], ps[:, 0:N], AF.Sigmoid, bias=col(V_A0), reads=[pk, "pv"], writes=["s_as"])
                self.I(POOL, "tensor_scalar", T["kk"][:, 0:N], zk[:, 0:N], col(V_KK), None, ALU.mult, reads=["s_zk", "pv"], writes=["s_kk"])
                self.I(POOL, "tensor_tensor", T["kk2"][:, 0:N], T["kk"][:, 0:N], T["kk"][:, 0:N], ALU.mult, reads=["s_kk"], writes=["s_d"])
                ps, pk = self.psum()
                self.bsum(ps, pk, T, "kk2", N)
                self.I(ACT, "activation", T["rn"][:, 0:N], ps[:, 0:N], AF.Sqrt, reads=[pk], writes=["s_zc"])
                self.I(DVE, "tensor_scalar_max", T["rn"][:, 0:N], T["rn"][:, 0:N], 1e-12, reads=["s_zc"], writes=["s_zc"])
                self.I(DVE, "reciprocal", T["rn"][:, 0:N], T["rn"][:, 0:N], reads=["s_zc"], writes=["s_zc"])
                self.I(DVE, "tensor_tensor", T["kkn"][:, 0:N], T["kk"][:, 0:N], T["rn"][:, 0:N], ALU.mult, reads=["s_kk", "s_zc"], writes=["s_kkn"])
                self.I(POOL, "tensor_scalar", T["t1"][:, 0:N], T["as"][:, 0:N], -1.0, col(V_KA), ALU.add, ALU.mult, reads=["s_as", "pv"], writes=["s_dd"])
                self.I(POOL, "tensor_tensor", T["t1"][:, 0:N], T["t1"][:, 0:N], zk[:, 0:N], ALU.mult, reads=["s_dd", "s_zk"], writes=["s_dd"])
                self.I(POOL, "tensor_tensor", T["km"][:, 0:N], T["t1"][:, 0:N], zk[:, 0:N], ALU.add, reads=["s_dd", "s_zk"], writes=["s_km"])
                self.I(POOL, "tensor_tensor", T["ta"][:, 0:N], T["kkn"][:, 0:N], T["as"][:, 0:N], ALU.mult, reads=["s_kkn", "s_as"], writes=["s_es"])
                self.I(POOL, "tensor_scalar", T["av"][:, 0:N], T["kkn"][:, 0:N], -1.0, None, ALU.mult, reads=["s_kkn"], writes=["s_av"])
                self.I(DVE, "scalar_tensor_tensor", T["rk"][:, 0:N], zr[:, 0:N], col(V_RK), T["km"][:, 0:N], ALU.mult, ALU.mult, reads=["s_zr", "s_km", "pv"], writes=["s_wd2"])
                ps, pk = self.psum()
                self.bsum(ps, pk, T, "rk", N)
                self.I(DVE, "tensor_tensor", sbon[:, q, :], ps[:, 0:N], zv[:, 0:N], ALU.mult, reads=[pk, "s_zv"], writes=["sbon"])
                self.I(POOL, "tensor_scalar", sbon[:, q, :], sbon[:, q, :], col(V_LB), None, ALU.add, reads=["sbon", "pv"], writes=["sbon"])
                ps, pk = self.psum()
                self.I(PE, "matmul", ps[:, 0:N], self.g2a[:, qs], sga[:, 0:N], start=True, stop=False, reads=["g2a", "ssga"], writes=[pk], track=False)
                self.I(PE, "matmul", ps[:, 0:N], self.g2b[:, qs], sgb[:, 0:N], start=False, stop=True, reads=["g2b", "ssgb"], writes=[pk])
                self.I(ACT, "copy", sgst[:, q, :], ps[:, 0:N], reads=[pk], writes=["sgst"])
                srcs = [("zr", "s_zr"), ("km", "s_km"), ("zv", "s_zv"), ("wd", "s_wd"), ("av", "s_av"), ("ta", "s_es")]
                psA, pkA = self.psum()
                psB, pkB = self.psum()
                for qi, (nm, kk_) in enumerate(srcs):
                    pp, ppk = (psA, pkA) if qi < 4 else (psB, pkB)
                    o = (qi % 4) * 128
                    self.I(PE, "transpose", pp[0:N, o:o + 128], T[nm][:, 0:N], cf[:, C_ID:C_ID + 128], reads=[kk_, "cf"], writes=[ppk], track=(qi in (3, 5)))
                self.I(DVE, "tensor_copy", tokS[:, 0:4, qs], psA[0:N, :].rearrange("p (a b) -> p a b", b=128), reads=[pkA], writes=["tokS"])
                self.I(ACT, "copy", tokS[:, 4:6, qs], psB[0:N, 0:256].rearrange("p (a b) -> p a b", b=128), reads=[pkB], writes=["tokS"])
            zrow = sb("zrow", [16, DS])
            for c0_ in range(0, 28, 4):
                ps, pk = self.psum()
                grp = list(enumerate(chunks))[c0_:c0_ + 4]
                for n_, (ci, (o, M)) in enumerate(grp):
                    self.I(PE, "transpose", ps[0:16, n_ * 128:n_ * 128 + M], zls[0:M, ci, :], cf[0:M, 0:M], reads=["zls", "cf"], writes=[pk], track=(n_ == len(grp) - 1))
                for n_, (ci, (o, M)) in enumerate(grp):
                    self.I(DVE, "tensor_copy", zrow[:, o:o + M], ps[0:16, n_ * 128:n_ * 128 + M], reads=[pk], writes=["zrow"])
            S.dma(SP, self.o_sshift, zrow[:], reads=["zrow"], ring="spo")
            S.dma(SP, self.scrS, tokS[:], reads=["tokS"], writes=["scrS"])
            S.barrier()

    def _stageS_scan(self, es0, sbon, sgst):
        S = self.S
        pv, cf, cb = self.pv, self.cf, self.cb
        N = NSM
        with ExitStack() as es:
            sb = lambda n, s, d=F32: self.sb(es, n, s, d)
            ytok = sb("ytok", [N, 1024])
            for bt in range(2):
                E = DVE
                kS, kq, kt, ky = "Sst%d" % bt, "qin%d" % bt, "stmp%d" % bt, "ysb%d" % bt
                Sst = sb(kS, [128, 64, 64])
                tmp = sb(kt, [128, 64, 64])
                qin = sb(kq, [128, 4, 6, 64])
                ysb = sb(ky, [128, 4, 64])
                sa = sb("sa%d" % bt, [128, 64])
                yq = sb("syq%d" % bt, [128, 4, 64])
                sst_ = sb("sstat%d" % bt, [128, 8, 4])
                S.dma(SP, Sst[:].rearrange("p v k -> p (v k)"), self.st_wkv[bt * 128:(bt + 1) * 128, :], writes=[kS])
                for bl in range(8):
                    b = bt * 8 + bl
                    S.dma(SP, qin[16 * bl:16 * bl + 16, :, :, :], self.scrS[4 * b:4 * b + 4, :, :].rearrange("t q (h k) -> h t q k", k=64),
                          reads=["scrS"], writes=[kq + "_%d" % bl])
                bc_k = lambda t, qi: qin[:, t, qi, :].unsqueeze(1).to_broadcast([128, 64, 64])
                kqs = [kq + "_%d" % bl for bl in range(8)]
                for t in range(4):
                    self.I(E, "tensor_tensor", tmp[:], Sst[:], bc_k(t, 4), ALU.mult, reads=[kS] + kqs, writes=[kt])
                    self.I(DVE, "tensor_reduce", sa[:], tmp[:], AX.X, ALU.add, reads=[kt], writes=["sa%d" % bt])
                    self.I(E, "tensor_tensor", Sst[:], Sst[:], bc_k(t, 3), ALU.mult, reads=[kS] + kqs, writes=[kS])
                    self.I(E, "tensor_tensor", tmp[:], sa[:].unsqueeze(2).to_broadcast([128, 64, 64]), bc_k(t, 5), ALU.mult, reads=["sa%d" % bt] + kqs, writes=[kt])
                    self.I(E, "tensor_tensor", Sst[:], Sst[:], tmp[:], ALU.add, reads=[kS, kt], writes=[kS])
                    self.I(E, "tensor_tensor", tmp[:], qin[:, t, 2, :].unsqueeze(2).to_broadcast([128, 64, 64]), bc_k(t, 1), ALU.mult, reads=kqs, writes=[kt])
                    self.I(E, "tensor_tensor", Sst[:], Sst[:], tmp[:], ALU.add, reads=[kS, kt], writes=[kS])
                    self.I(E, "tensor_tensor", tmp[:], Sst[:], bc_k(t, 0), ALU.mult, reads=[kS] + kqs, writes=[kt])
                    self.I(DVE, "tensor_reduce", ysb[:, t, :], tmp[:], AX.X, ALU.add, reads=[kt], writes=[ky])
                S.dma(SP, self.o_swkv[bt * 128:(bt + 1) * 128, :], Sst[:].rearrange("p v k -> p (v k)"), reads=[kS], ring="spo")
                st_ = sst_
                ks_ = "sstat%d" % bt
                self.I(DVE, "tensor_reduce", st_[:, 0, :], ysb[:], AX.X, ALU.add, reads=[ky], writes=[ks_])
                self.I(E, "tensor_tensor", yq[:], ysb[:], ysb[:], ALU.mult, reads=[ky], writes=["syq%d" % bt])
                self.I(DVE, "tensor_reduce", st_[:, 1, :], yq[:], AX.X, ALU.add, reads=["syq%d" % bt], writes=[ks_])
                self.I(E, "tensor_scalar", st_[:, 2, :], st_[:, 0, :], 1.0 / 64, None, ALU.mult, reads=[ks_], writes=[ks_])
                self.I(E, "tensor_tensor", st_[:, 3, :], st_[:, 2, :], st_[:, 2, :], ALU.mult, reads=[ks_], writes=[ks_])
                self.I(E, "tensor_scalar", st_[:, 4, :], st_[:, 1, :], 1.0 / 64, 64e-5, ALU.mult, ALU.add, reads=[ks_], writes=[ks_])
                self.I(E, "tensor_tensor", st_[:, 4, :], st_[:, 4, :], st_[:, 3, :], ALU.subtract, reads=[ks_], writes=[ks_])
                self.I(ACT, "activation", st_[:, 5, :], st_[:, 4, :], AF.Sqrt, reads=[ks_], writes=[ks_])
                self.I(DVE, "reciprocal", st_[:, 6, :], st_[:, 5, :], reads=[ks_], writes=[ks_])
                self.I(E, "tensor_tensor", yq[:], ysb[:], st_[:, 2, :].unsqueeze(2).to_broadcast([128, 4, 64]), ALU.subtract, reads=[ky, ks_], writes=["syq%d" % bt])
                self.I(E, "tensor_tensor", yq[:], yq[:], st_[:, 6, :].unsqueeze(2).to_broadcast([128, 4, 64]), ALU.mult, reads=["syq%d" % bt, ks_], writes=["syq%d" % bt])
                for bl in range(8):
                    b = bt * 8 + bl
                    S.dma(SP, self.scrY[4 * b:4 * b + 4, :].rearrange("t (h v) -> h t v", v=64), yq[16 * bl:16 * bl + 16, :, :], reads=["syq%d" % bt], writes=["scrY%d" % b])
            S.dma(SP, ytok[:], self.scrY, reads=["scrY%d" % b for b in range(16)], writes=["ytok"])
            if "sdbg" in self.dbg:
                S.dma(SP, self.dout("d_ytok", [N, 1024]), ytok[:], reads=["ytok"], ring="spo")
                dbf = sb("dbf", [128, 2, 8, N])
                self.I(DVE, "tensor_copy", dbf[:, 0], sbon[:], reads=["sbon"], writes=["dbf"])
                self.I(DVE, "tensor_copy", dbf[:, 1], sgst[:], reads=["sgst"], writes=["dbf"])
                S.dma(SP, self.dout("d_sbg", [128, 2, 8, N]), dbf[:], reads=["dbf"], ring="spo")
            ytb = sb("ytb", [N, 1024], BF16)
            self.I(ACT, "copy", ytb[:], ytok[:], reads=["ytok"], writes=["ytb"])
            ps, pk = self.psum()
            pT = ps[:].bitcast(BF16).rearrange("p (a b) -> p a b", b=128)[:, :, 0:N]
            for q in range(8):
                self.I(PE, "transpose", pT[:, q, :], ytb[:, q * 128:(q + 1) * 128], cb[0:N, 0:N], reads=["ytb", "cb"], writes=[pk], track=(q == 7))
            yt = sb("syt", [128, 8, N])
            lw = pv[:, V_LW: V_LW + 8].unsqueeze(2).to_broadcast([128, 8, N])
            self.I(DVE, "tensor_tensor", yt[:], pT, lw, ALU.mult, reads=[pk, "pv"], writes=["syt"])
            self.I(POOL, "tensor_tensor", yt[:], yt[:], sbon[:], ALU.add, reads=["syt", "sbon"], writes=["syt"])
            self.I(DVE, "tensor_tensor", self.yaT[:, :, 1056:NOWN], yt[:], sgst[:], ALU.mult, reads=["syt", "sgst"], writes=["yaT"])
            S.barrier()

    def stageA(self, es0, xseq, wA):
        S = self.S
        pv, cf, cb = self.pv, self.cf, self.cb
        with ExitStack() as es:
            sb = lambda n, s, d=F32: self.sb(es, n, s, d)
            self.xt = [sb("xt%d" % i, [128, D]) for i in range(1)]
            self.xb = [sb("xb%d" % i, [128, D], BF16) for i in range(1)]
            self.xst = [sb("xst%d" % i, [128, 4]) for i in range(1)]
            hT = sb("hTb", [128, 16, BLK], BF16)
            zl = sb("zl", [128, 28])
            self.I(POOL, "memset", zl[:], 0.0, writes=["zl"])
            tw = sb("tw", [64, BLK], BF16)
            zam = sb("zam", [64, BLK], BF16)
            sga = sb("sga", [128, BLK], BF16)
            sgb = sb("sgb", [128, BLK], BF16)
            self.I(POOL, "memset", sgb[:], 0.0, writes=["sgb"])
            NCK = BLK // CH
            ARBD = sb("ARBD", [128, 4, NCK, 2, 128], BF16)
            BBD = sb("BBD", [128, 4, NCK, 128], BF16)
            KBD = sb("KBD", [128, 4, NCK, 128], BF16)
            VBD = sb("VBD", [128, 4, NCK, 128], BF16)
            for (t, k) in ((ARBD, "ARBD"), (BBD, "BBD"), (KBD, "KBD"), (VBD, "VBD")):
                self.I(POOL, "memset", t[:], 0.0, writes=[k + "%d" % q for q in range(4)])
            PCs = sb("PCs", [128, 4, NCK])
            bon = sb("bon", [128, 4, BLK], BF16)
            gst = sb("gst", [128, 4, BLK], BF16)
            Hf = [sb("Hf%d" % g, [128, 4, 64]) for g in range(2)]
            Hb = [sb("Hb%d" % g, [128, 4, 64], BF16) for g in range(2)]
            for g in range(2):
                self.I(POOL, "memset", Hf[g][:], 0.0, writes=["Hf%d" % g])
                self.I(POOL, "memset", Hb[g][:], 0.0, writes=["Hb%d" % g])
            names = ("zc", "d", "zr", "zk", "zv", "es", "cum", "dd", "pinv", "prr", "pa", "as", "kk", "kkn", "km")
            phys = {n: sb("t_" + n, [128, NH]) for n in names[:8]}
            spare = self.bufX[:, 8:16, :].rearrange("p a b -> p (a b)")
            sparef = spare.bitcast(F32)
            for i, n in enumerate(names[8:]):
                phys[n] = sparef[:, i * NH:(i + 1) * NH]
            phys["hi"] = spare[:, 7 * 2 * NH: 7 * 2 * NH + NH]
            phys["lo"] = spare[:, 7 * 2 * NH + NH: 7 * 2 * NH + 2 * NH]
            T = _TAlias(phys, {"kk2": "d", "rn": "zc", "t1": "dd", "ta": "es", "rk": "cum"})
            sc = {}
            for s in range(2):
                sc["NA1", s] = sb("NA1_%d" % s, [128, 4, 256], BF16)
                sc["NA2", s] = sb("NA2_%d" % s, [128, 4, 256], BF16)
                for i in range(2):
                    sc["Q", s, i] = sb("Q_%d%d" % (s, i), [128, 4, 128], BF16)
                    if s == 0:
                        sc["NL", s, i] = sb("NL_%d%d" % (s, i), [128, 4, 128], BF16)
                        sc["TU", s, i] = sb("TU_%d%d" % (s, i), [128, 4, 128], BF16)
                sc["B2", s] = sb("B2_%d" % s, [128, 4, 128], BF16)
                sc["K2", s] = sb("K2_%d" % s, [128, 4, 128], BF16)
                sc["V2", s] = sb("V2_%d" % s, [128, 4, 64], BF16)
                if s == 0:
                    sc["X2", s] = sb("X2_%d" % s, [128, 4, 64], BF16)
                    sc["U2", s] = sb("U2_%d" % s, [128, 4, 64], BF16)
                    sc["YBD", s] = sb("YBD_%d" % s, [128, 4, 128], BF16)
                    self.I(POOL, "memset", sc["YBD", s][:], 0.0, writes=["YBD_%d" % s])
                    sc["ys", s] = sb("ys_%d" % s, [128, 4, 64])
                    sc["yq", s] = sb("yq_%d" % s, [128, 4, 64])
                    sc["st", s] = sb("yst_%d" % s, [128, 8, 4])
                    sc["yt", s] = sb("yt_%d" % s, [128, 4, 64])
                else:
                    for nm in ("X2", "U2", "YBD", "ys", "yq", "st", "yt"):
                        sc[nm, s] = sc[nm, 0]
            tmpH = sb("tmpH", [128, 4, 64])

            nblk = NPR // BLK
            lr_slab = (wA[:, 0:288], 288)
            pair_slabs = [(wA[:, 288 + q * 384: 288 + (q + 1) * 384], 384) for q in range(8)]
            slabs = []
            for blk in range(nblk):
                slabs.append(lr_slab)
                slabs += pair_slabs
            stream = self.slab_stream(slabs)

            for blk in range(nblk):
                c0 = blk * BLK
                N = BLK
                for t4 in range(4):
                    self.make_hT(xseq[c0 + t4 * 128: c0 + (t4 + 1) * 128, :], 128, hT[:, :, t4 * 128:(t4 + 1) * 128], "hTb", 0)
                self.ckpt("hT")
                if blk == 1:
                    S.dma(SP, self.scrH[:, :, 0:32], hT[:, :, BLK - 32:BLK], reads=["hTb"], writes=["scrH_1"])
                elif blk >= 2:
                    S.dma(SP, self.scrH[:, :, 32 + (blk - 2) * BLK: 32 + (blk - 1) * BLK], hT[:, :, :], reads=["hTb"], writes=["scrH_%d" % blk])
                wt, wk = next(stream)
                for (ci, (cc0, M, dst, dk, func)) in enumerate(((0, 64, tw, "tw", AF.Tanh), (64, 64, zam, "zam", AF.Copy),
                                                                 (128, 128, sga, "sga", AF.Sigmoid), (256, 32, sgb, "sgb", AF.Sigmoid))):
                    ps, pk = self.psum()
                    self.proj(ps, pk, wt, wk, cc0, M, hT, "hTb", N)
                    for co in range(0, N, NH):
                        self.mix(ps[:, co:co + NH], pk, M, NH, ci, T, zl)
                        self.I(ACT, "activation", dst[0:M, co:co + NH], T["zr"][0:M, 0:NH], func, reads=["t_zr"], writes=[dk])
                self.ckpt("lowrank")
                for g in range(2):
                    for qg in range(4):
                        q = g * 4 + qg
                        wt, wk = next(stream)
                        self.prep_pair(wt, wk, hT, N, q, qg, T, zl, tw, zam, sga, sgb, ARBD, BBD, KBD, VBD, PCs, bon, gst, blk)
                        self.ckpt("prep0")
                    self.ckpt("prep")
                    self.scan_group(g, blk, NCK, sc, ARBD, BBD, KBD, VBD, PCs, bon, gst, Hf[g], Hb[g], tmpH)
            scr = self.stage_out("zl", zl[:], [128, 28], "zl")
            self.bg(self.o_pshift[0:64].rearrange("(p o) -> p o", o=1), scr[0:64, 0:1], "zl")
            self.bg(self.o_pshift[64:128].rearrange("(p o) -> p o", o=1), scr[0:64, 1:2], "zl")
            self.bg(self.o_pshift[128:256].rearrange("(p o) -> p o", o=1), scr[:, 2:3], "zl")
            self.bg(self.o_pshift[256:288].rearrange("(p o) -> p o", o=1), scr[0:32, 3:4], "zl")
            self.bg(self.o_pshift[288:DS].rearrange("(j p) -> p j", p=128), scr[:, 4:28], "zl")
            for g in range(2):
                hh = sb("hh%d" % g, [128, 4, 64], BF16)
                hl = sb("hl%d" % g, [128, 4, 64], BF16)
                self.I(POOL, "tensor_copy", hh[:], Hf[g][:], reads=["Hf%d" % g], writes=["hh%d" % g])
                self.I(POOL, "tensor_tensor", hl[:], Hf[g][:], hh[:], ALU.subtract, reads=["Hf%d" % g, "hh%d" % g], writes=["hl%d" % g])
                ps, pk = self.psum()
                for qg in range(4):
                    self.I(PE, "matmul", ps[0:64, qg * 128:(qg + 1) * 128], hh[:, qg, :], cb[:, 0:128], start=True, stop=False, reads=["hh%d" % g, "cb"], writes=[pk], track=False)
                    self.I(PE, "matmul", ps[0:64, qg * 128:(qg + 1) * 128], hl[:, qg, :], cb[:, 0:128], start=False, stop=True, reads=["hl%d" % g, "cb"], writes=[pk], track=(qg == 3))
                so = sb("so%d" % g, [64, 4, 2, 64])
                self.I(DVE, "tensor_copy", so[:], ps[0:64, :].rearrange("p (a b c) -> p a b c", a=4, b=2), reads=[pk], writes=["so%d" % g])
                S.dma(SP, self.o_pwkv[g * 8:(g + 1) * 8].rearrange("(q s) v k -> v q s k", s=2), so[:], reads=["so%d" % g], ring="spo")
            S.barrier()

    def mix(self, ps, pk, M, N, ci, T, zl):
        S = self.S
        zc, d, out = T["zc"], T["d"], T["zr"]
        self.mix_to(ps, pk, M, N, ci, zl, zc, "t_zc", d, "t_d", out, "t_zr", self.pv[0:M, ci:ci + 1] if ci < 4 else None)

    def mix_to(self, ps, pk, M, N, ci, zl, zc, kzc, d, kd, out, kout, mu):
        S = self.S
        self.I(ACT, "copy", zc[0:M, 0:N], ps[0:M, 0:N], reads=[pk], writes=[kzc])
        self.I(DVE, "tensor_tensor", d[0:M, 1:N], zc[0:M, 0:N - 1], zc[0:M, 1:N], ALU.subtract, reads=[kzc], writes=[kd])
        self.I(DVE, "tensor_tensor", d[0:M, 0:1], zl[0:M, ci:ci + 1], zc[0:M, 0:1], ALU.subtract, reads=[kzc, "zl"], writes=[kd])
        self.I(DVE, "scalar_tensor_tensor", out[0:M, 0:N], d[0:M, 0:N], mu, zc[0:M, 0:N], ALU.mult, ALU.add, reads=[kd, kzc, "pv"], writes=[kout])
        self.I(POOL, "tensor_copy", zl[0:M, ci:ci + 1], zc[0:M, N - 1:N], reads=[kzc], writes=["zl"])

    def bsum(self, ps, pk, T, name, N):
        S = self.S
        src, ksrc = T[name], T.key(name)
        bonesb = self.cb[:, C_BONES:C_BONES + 128]
        khi, klo = T.key("hi"), T.key("lo")
        self.I(ACT, "copy", T["hi"][:, 0:N], src[:, 0:N], reads=[ksrc], writes=[khi])
        self.I(POOL, "tensor_tensor", T["lo"][:, 0:N], src[:, 0:N], T["hi"][:, 0:N], ALU.subtract, reads=[ksrc, khi], writes=[klo])
        self.I(PE, "matmul", ps[:, 0:N], bonesb, T["hi"][:, 0:N], start=True, stop=False, reads=["cb", khi], writes=[pk], track=False)
        self.I(PE, "matmul", ps[:, 0:N], bonesb, T["lo"][:, 0:N], start=False, stop=True, reads=["cb", klo], writes=[pk])

    def prep_pair(self, wt, wk, hT, N, q, qg, T, zl, tw, zam, sga, sgb, ARBD, BBD, KBD, VBD, PCs, bon, gst, blk):
        S = self.S
        pv, cf = self.pv, self.cf
        col = lambda base: pv[:, base + q: base + q + 1]
        qs = slice(q * 128, (q + 1) * 128)
        need_y = blk >= 1
        W = N
        co = 0
        cw = slice(0, W)
        psw, pkw = self.psum()
        self.I(PE, "matmul", psw[:, 0:W], self.w2b[:, qs], tw[:, cw], start=True, stop=True, reads=["w2b", "tw"], writes=[pkw])
        psa, pka = self.psum()
        self.I(PE, "matmul", psa[:, 0:W], self.a2b[:, qs], zam[:, cw], start=True, stop=True, reads=["a2b", "zam"], writes=[pka])
        pj = []
        for t in range(3):
            if t == 0 and not need_y:
                pj.append(None)
                continue
            ps, pk = self.psum()
            self.proj(ps, pk, wt, wk, t * 128, 128, hT, "hTb", N)
            pj.append((ps, pk))
        self.I(ACT, "activation", T["es"][:, 0:W], psw[:, 0:W], AF.Sigmoid, bias=col(V_W0), reads=[pkw, "pv"], writes=["t_es"])
        self.I(ACT, "activation", T["as"][:, 0:W], psa[:, 0:W], AF.Sigmoid, bias=col(V_A0), reads=[pka, "pv"], writes=["t_as"])
        self.I(DVE, "tensor_tensor_scan", T["cum"][:, 0:W], cf[:, C_M01:C_M01 + W], T["es"][:, 0:W], 0.0, ALU.mult, ALU.add, reads=["t_es", "cf"], writes=["t_cum"])
        self.I(POOL, "tensor_tensor", T["dd"][:, 0:W], T["cum"][:, 0:W], T["es"][:, 0:W], ALU.subtract, reads=["t_cum", "t_es"], writes=["t_dd"])
        nck = W // CH
        ck0 = 0

        def mixq(t, nm):
            ps, pk = pj[t]
            ci = 4 + 3 * q + t
            self.mix_to(ps[:, cw], pk, 128, W, ci, zl, T["zc"], "t_zc", T["d"], "t_d", T[nm], "t_" + nm, pv[:, V_MU + 3 * q + t: V_MU + 3 * q + t + 1])
        if need_y:
            mixq(0, "zr")
        self.I(ACT, "activation", T["pinv"][:, 0:W], T["cum"][:, 0:W], AF.Exp, scale=DECAY_C, reads=["t_cum"], writes=["t_pinv"])
        self.I(ACT, "activation", T["prr"][:, 0:W], T["cum"][:, 0:W], AF.Exp, scale=-DECAY_C, reads=["t_cum"], writes=["t_prr"])
        self.I(ACT, "activation", T["pa"][:, 0:W], T["dd"][:, 0:W], AF.Exp, scale=-DECAY_C, reads=["t_dd"], writes=["t_pa"])
        self.I(POOL, "tensor_copy", PCs[:, qg, ck0:ck0 + nck], T["prr"][:, 0:W].rearrange("p (c t) -> p c t", t=CH)[:, :, CH - 1], reads=["t_prr"], writes=["PCs%d" % qg])
        mixq(1, "zk")
        mixq(2, "zv")
        if True:
            zr, zk, zv = T["zr"], T["zk"], T["zv"]
            self.I(ACT, "activation", T["kk"][:, 0:W], zk[:, 0:W], AF.Copy, scale=col(V_KK), reads=["t_zk", "pv"], writes=["t_kk"])
            self.I(ACT, "activation", T["kk2"][:, 0:W], T["kk"][:, 0:W], AF.Square, reads=["t_kk"], writes=["t_d"])
            ps, pk = self.psum()
            self.bsum(ps, pk, T, "kk2", W)
            self.I(ACT, "activation", T["rn"][:, 0:W], ps[:, 0:W], AF.Sqrt, reads=[pk], writes=["t_zc"])
            self.I(DVE, "tensor_scalar_max", T["rn"][:, 0:W], T["rn"][:, 0:W], 1e-12, reads=["t_zc"], writes=["t_zc"])
            self.I(DVE, "reciprocal", T["rn"][:, 0:W], T["rn"][:, 0:W], reads=["t_zc"], writes=["t_zc"])
            self.I(DVE, "tensor_tensor", T["kkn"][:, 0:W], T["kk"][:, 0:W], T["rn"][:, 0:W], ALU.mult, reads=["t_kk", "t_zc"], writes=["t_kkn"])
            self.I(POOL, "tensor_scalar", T["t1"][:, 0:W], T["as"][:, 0:W], -1.0, col(V_KA), ALU.add, ALU.mult, reads=["t_as", "pv"], writes=["t_dd"])
            self.I(POOL, "tensor_tensor", T["t1"][:, 0:W], T["t1"][:, 0:W], zk[:, 0:W], ALU.mult, reads=["t_dd", "t_zk"], writes=["t_dd"])
            self.I(POOL, "tensor_tensor", T["km"][:, 0:W], T["t1"][:, 0:W], zk[:, 0:W], ALU.add, reads=["t_dd", "t_zk"], writes=["t_km"])
            self.I(POOL, "tensor_tensor", T["ta"][:, 0:W], T["kkn"][:, 0:W], T["as"][:, 0:W], ALU.mult, reads=["t_kkn", "t_as"], writes=["t_es"])
            if need_y:
                self.I(DVE, "scalar_tensor_tensor", T["rk"][:, 0:W], zr[:, 0:W], col(V_RK), T["km"][:, 0:W], ALU.mult, ALU.mult, reads=["t_zr", "t_km", "pv"], writes=["t_cum"])
                ps, pk = self.psum()
                self.bsum(ps, pk, T, "rk", W)
                self.I(DVE, "tensor_tensor", bon[:, qg, cw], ps[:, 0:W], zv[:, 0:W], ALU.mult, reads=[pk, "t_zv"], writes=["bon%d" % qg])
                self.I(ACT, "activation", bon[:, qg, cw], bon[:, qg, cw], AF.Identity, bias=col(V_LB), reads=["bon%d" % qg, "pv"], writes=["bon%d" % qg])
                ps, pk = self.psum()
                self.I(PE, "matmul", ps[:, 0:W], self.g2a[:, qs], sga[:, cw], start=True, stop=False, reads=["g2a", "sga"], writes=[pk], track=False)
                self.I(PE, "matmul", ps[:, 0:W], self.g2b[:, qs], sgb[:, cw], start=False, stop=True, reads=["g2b", "sgb"], writes=[pk])
                self.I(ACT, "copy", gst[:, qg, cw], ps[:, 0:W], reads=[pk], writes=["gst%d" % qg])
            for h in range(2):
                pr = slice(64 * h, 64 * h + 64)
                cs = slice(64 * h, 64 * h + 64)
                v3 = lambda t: t[pr, 0:W].rearrange("p (c t) -> p c t", t=CH)
                e1 = DVE if h == 0 else POOL
                cks = slice(ck0, ck0 + nck)
                self.I(DVE, "scalar_tensor_tensor", ARBD[pr, qg, cks, 0, cs], v3(T["kkn"]), -1.0, v3(T["pa"]), ALU.mult, ALU.mult, reads=["t_kkn", "t_pa"], writes=["ARBD%d" % qg])
                if need_y:
                    self.I(e1, "tensor_tensor", ARBD[pr, qg, cks, 1, cs], v3(zr), v3(T["prr"]), ALU.mult, reads=["t_zr", "t_prr"], writes=["ARBD%d" % qg])
                self.I(e1, "tensor_tensor", KBD[pr, qg, cks, cs], v3(T["km"]), v3(T["pinv"]), ALU.mult, reads=["t_km", "t_pinv"], writes=["KBD%d" % qg])
                self.I(e1, "tensor_tensor", BBD[pr, qg, cks, cs], v3(T["ta"]), v3(T["pinv"]), ALU.mult, reads=["t_es", "t_pinv"], writes=["BBD%d" % qg])
                self.I(ACT, "copy", VBD[pr, qg, cks, cs], v3(zv), reads=["t_zv"], writes=["VBD%d" % qg])

    def scan_pre(self, g, c, gc, sc, ARBD, BBD, KBD, VBD):
        cf, cb = self.cf, self.cb
        s = gc % 2
        kin = ["ARBD%d" % q for q in range(4)] + ["BBD%d" % q for q in range(4)] + ["KBD%d" % q for q in range(4)] + ["VBD%d" % q for q in range(4)]
        NA1, NA2 = sc["NA1", s], sc["NA2", s]
        kNA1, kNA2 = "NA1_%d" % s, "NA2_%d" % s
        identb = cb[:, 0:128]
        iselb = cb[:, C_ISEL:C_ISEL + 64]
        mask2 = cf[:, C_MUS:C_MUS + 256].unsqueeze(1).to_broadcast([128, 2, 256])
        for (dst, kd, L) in ((NA1, kNA1, BBD), (NA2, kNA2, KBD)):
            for hf in range(2):
                ps, pk = self.psum()
                for j in range(2):
                    q = 2 * hf + j
                    self.I(PE, "matmul", ps[:, j * 256:(j + 1) * 256], L[:, q, c, :], ARBD[:, q, c, :, :].rearrange("p a b -> p (a b)"),
                           start=True, stop=True, reads=kin, writes=[pk], track=(j == 1))
                self.I(DVE, "tensor_tensor", dst[:, 2 * hf:2 * hf + 2, :], ps[:].rearrange("p (a b) -> p a b", b=256), mask2, ALU.mult, reads=[pk, "cf"], writes=[kd])
                yield
        NL0, kNL0 = sc["NL", 0, 0], "NL_00"
        ps, pk = self.psum()
        for q in range(4):
            self.I(PE, "matmul", ps[:, q * 128:(q + 1) * 128], ARBD[:, q, c, 0, :], BBD[:, q, c, :], start=True, stop=True, reads=kin, writes=[pk], track=(q == 3))
        mL = cf[:, C_MLS:C_MLS + 128].unsqueeze(1).to_broadcast([128, 4, 128])
        self.I(DVE, "tensor_tensor", NL0[:], ps[:].rearrange("p (a b) -> p a b", b=128), mL, ALU.mult, reads=[pk, "cf"], writes=[kNL0])
        yield
        B2, K2, V2 = sc["B2", s], sc["K2", s], sc["V2", s]
        for (dst, kd, L) in ((B2, "B2_%d" % s, BBD), (K2, "K2_%d" % s, KBD)):
            ps, pk = self.psum()
            for q in range(4):
                self.I(PE, "matmul", ps[:, q * 128:(q + 1) * 128], L[:, q, c, :], identb, start=True, stop=True, reads=kin + ["cb"], writes=[pk], track=(q == 3))
            self.I(ACT, "copy", dst[:], ps[:].rearrange("p (a b) -> p a b", b=128), reads=[pk], writes=[kd])
            yield
        ps, pk = self.psum()
        for q in range(4):
            self.I(PE, "matmul", ps[:, q * 64:(q + 1) * 64], VBD[:, q, c, :], iselb, start=True, stop=True, reads=kin + ["cb"], writes=[pk], track=(q == 3))
        self.I(ACT, "copy", V2[:], ps[:, 0:256].rearrange("p (a b) -> p a b", b=64), reads=[pk], writes=["V2_%d" % s])
        yield
        TUc, kTU = NA1[:, :, 0:128], [kNA1]
        NLc, kNL = NL0, kNL0
        Qc, kQ = sc["Q", s, 0], "Q_%d0" % s
        idb4 = identb.unsqueeze(1).to_broadcast([128, 4, 128])
        self.I(POOL, "tensor_tensor", Qc[:], NA1[:, :, 0:128], idb4, ALU.add, reads=[kNA1, "cb"], writes=[kQ])
        for lvl in range(1, 6):
            i = lvl % 2
            NLn, kNLn = sc["NL", 0, i], "NL_0%d" % i
            TUn = sc["TU", 0, i]
            kTUn = ["TU_0%da" % i, "TU_0%db" % i]
            Qn, kQn = sc["Q", s, i], "Q_%d%d" % (s, i)
            psN, pkN = self.psum()
            for q in range(4):
                self.I(PE, "matmul", psN[:, q * 128:(q + 1) * 128], TUc[:, q, :], NLc[:, q, :], start=True, stop=True, reads=kTU + [kNL], writes=[pkN], track=(q == 3))
            if lvl < 5:
                psTa, pkTa = self.psum()
                psTb, pkTb = self.psum()
                for q in range(4):
                    pT_, pkT_ = (psTa, pkTa) if q < 2 else (psTb, pkTb)
                    self.I(PE, "matmul", pT_[:, (q % 2) * 128:(q % 2 + 1) * 128], NLc[:, q, :], TUc[:, q, :], start=True, stop=True, reads=kTU + [kNL], writes=[pkT_], track=(q % 2 == 1))
            self.I(ACT, "copy", NLn[:], psN[:].rearrange("p (a b) -> p a b", b=128), reads=[pkN], writes=[kNLn])
            if lvl < 5:
                self.I(ACT, "copy", TUn[:, 0:2, :], psTa[:, 0:256].rearrange("p (a b) -> p a b", b=128), reads=[pkTa], writes=[kTUn[0]])
                self.I(DVE, "tensor_copy", TUn[:, 2:4, :], psTb[:, 0:256].rearrange("p (a b) -> p a b", b=128), reads=[pkTb], writes=[kTUn[1]])
            yield
            psQ, pkQ = self.psum()
            for q in range(4):
                self.I(PE, "matmul", psQ[:, q * 128:(q + 1) * 128], NLn[:, q, :], Qc[:, q, :], start=True, stop=True, reads=[kNLn, kQ], writes=[pkQ], track=(q == 3))
            self.I(DVE, "tensor_tensor", Qn[:], psQ[:].rearrange("p (a b) -> p a b", b=128), Qc[:], ALU.add, reads=[pkQ, kQ], writes=[kQn])
            yield
            TUc, kTU, NLc, kNL, Qc, kQ = TUn, kTUn, NLn, kNLn, Qn, kQn

    def scan_chain(self, g, c, gc, sc, ARBD, PCs, bon, gst, Hf, Hb, tmpH):
        s = gc % 2
        kin = ["ARBD%d" % q for q in range(4)]
        NA1, NA2 = sc["NA1", s], sc["NA2", s]
        kNA1, kNA2 = "NA1_%d" % s, "NA2_%d" % s
        B2, K2, V2 = sc["B2", s], sc["K2", s], sc["V2", s]
        kB2, kK2, kV2 = "B2_%d" % s, "K2_%d" % s, "V2_%d" % s
        Qc, kQ = sc["Q", s, 1], "Q_%d1" % s
        kH = "Hf%d" % g
        kHb = "Hb%d" % g
        X2, U2 = sc["X2", 0], sc["U2", 0]
        ps, pk = self.psum()
        for q in range(4):
            self.I(PE, "matmul", ps[:, q * 64:(q + 1) * 64], ARBD[:, q, c, 0, :], Hb[:, q, :], start=True, stop=False, reads=kin + [kHb], writes=[pk], track=False)
            self.I(PE, "matmul", ps[:, q * 64:(q + 1) * 64], NA2[:, q, 0:128], V2[:, q, :], start=False, stop=True, reads=[kNA2, kV2], writes=[pk], track=(q == 3))
        self.I(ACT, "copy", X2[:], ps[:, 0:256].rearrange("p (a b) -> p a b", b=64), reads=[pk], writes=["X2_0"])
        yield
        ps, pk = self.psum()
        for q in range(4):
            self.I(PE, "matmul", ps[:, q * 64:(q + 1) * 64], Qc[:, q, :], X2[:, q, :], start=True, stop=True, reads=[kQ, "X2_0"], writes=[pk], track=(q == 3))
        self.I(ACT, "copy", U2[:], ps[:, 0:256].rearrange("p (a b) -> p a b", b=64), reads=[pk], writes=["U2_0"])
        yield
        need_y = gc >= OWN0 // CH
        if need_y:
            psY, pkY = self.psum()
            for q in range(4):
                self.I(PE, "matmul", psY[:, q * 64:(q + 1) * 64], ARBD[:, q, c, 1, :], Hb[:, q, :], start=True, stop=False, reads=kin + [kHb], writes=[pkY], track=False)
                self.I(PE, "matmul", psY[:, q * 64:(q + 1) * 64], NA1[:, q, 128:256], U2[:, q, :], start=False, stop=False, reads=[kNA1, "U2_0"], writes=[pkY], track=False)
                self.I(PE, "matmul", psY[:, q * 64:(q + 1) * 64], NA2[:, q, 128:256], V2[:, q, :], start=False, stop=True, reads=[kNA2, kV2], writes=[pkY], track=(q == 3))
        ps, pk = self.psum()
        for q in range(4):
            self.I(PE, "matmul", ps[:, q * 64:(q + 1) * 64], B2[:, q, :], U2[:, q, :], start=True, stop=False, reads=[kB2, "U2_0"], writes=[pk], track=False)
            self.I(PE, "matmul", ps[:, q * 64:(q + 1) * 64], K2[:, q, :], V2[:, q, :], start=False, stop=True, reads=[kK2, kV2], writes=[pk], track=(q == 3))
        self.I(DVE, "tensor_tensor", tmpH[:], Hf[:], ps[:, 0:256].rearrange("p (a b) -> p a b", b=64), ALU.add, reads=[pk, kH], writes=["tmpH"])
        pcb = PCs[:, :, c:c + 1].to_broadcast([128, 4, 64])
        self.I(DVE, "tensor_tensor", Hf[:], tmpH[:], pcb, ALU.mult, reads=["tmpH"] + ["PCs%d" % q for q in range(4)], writes=[kH])
        self.I(ACT, "copy", Hb[:], Hf[:], reads=[kH], writes=[kHb])
        yield
        if need_y:
            yield from self.y_post(g, c, gc, s, sc, psY, pkY, bon, gst)

    def scan_group(self, g, blk, NCK, sc, ARBD, BBD, KBD, VBD, PCs, bon, gst, Hf, Hb, tmpH):
        def drain(gen):
            for _ in gen:
                pass

        def interleave(ga, gb):
            a_live = b_live = True
            while a_live or b_live:
                if a_live:
                    try:
                        next(ga)
                    except StopIteration:
                        a_live = False
                if b_live:
                    try:
                        next(gb)
                    except StopIteration:
                        b_live = False
        gc0 = blk * NCK
        drain(self.scan_pre(g, 0, gc0, sc, ARBD, BBD, KBD, VBD))
        for c in range(NCK):
            ch = self.scan_chain(g, c, gc0 + c, sc, ARBD, PCs, bon, gst, Hf, Hb, tmpH)
            if c + 1 < NCK:
                interleave(ch, self.scan_pre(g, c + 1, gc0 + c + 1, sc, ARBD, BBD, KBD, VBD))
            else:
                drain(ch)

    def y_post(self, g, c, gc, s, sc, psY, pkY, bon, gst):
        S = self.S
        pv, cb = self.pv, self.cb
        ys, yq, st, yt, YBD = sc["ys", s], sc["yq", s], sc["st", s], sc["yt", s], sc["YBD", s]
        kys, kyq, kst, kyt, kY = "ys_0", "yq_0", "yst_0", "yt_0", "YBD_0"
        self.I(ACT, "copy", ys[:], psY[:, 0:256].rearrange("p (a b) -> p a b", b=64), reads=[pkY], writes=[kys])
        self.I(DVE, "tensor_reduce", st[:, 0, :], ys[:], AX.X, ALU.add, reads=[kys], writes=[kst])
        self.I(POOL, "tensor_tensor", yq[:], ys[:], ys[:], ALU.mult, reads=[kys], writes=[kyq])
        self.I(DVE, "tensor_reduce", st[:, 1, :], yq[:], AX.X, ALU.add, reads=[kyq], writes=[kst])
        yield
        self.I(DVE, "tensor_scalar", st[:, 2, :], st[:, 0, :], 1.0 / 64, None, ALU.mult, reads=[kst], writes=[kst])
        self.I(DVE, "tensor_tensor", st[:, 3, :], st[:, 2, :], st[:, 2, :], ALU.mult, reads=[kst], writes=[kst])
        self.I(DVE, "scalar_tensor_tensor", st[:, 4, :], st[:, 1, :], 1.0 / 64, st[:, 3, :], ALU.mult, ALU.subtract, reads=[kst], writes=[kst])
        self.I(DVE, "tensor_scalar", st[:, 4, :], st[:, 4, :], 64e-5, None, ALU.add, reads=[kst], writes=[kst])
        self.I(ACT, "activation", st[:, 5, :], st[:, 4, :], AF.Sqrt, reads=[kst], writes=[kst])
        self.I(DVE, "reciprocal", st[:, 6, :], st[:, 5, :], reads=[kst], writes=[kst])
        yield
        self.I(DVE, "tensor_tensor", yq[:], ys[:], st[:, 2, :].unsqueeze(2).to_broadcast([128, 4, 64]), ALU.subtract, reads=[kys, kst], writes=[kyq])
        for h in range(2):
            pr = slice(64 * h, 64 * h + 64)
            self.I(DVE if h == 0 else POOL, "tensor_tensor", YBD[pr, :, pr], yq[pr, :, :], st[pr, 6, :].unsqueeze(2).to_broadcast([64, 4, 64]), ALU.mult, reads=[kyq, kst], writes=[kY])
        yield
        ps, pk = self.psum()
        for q in range(4):
            self.I(PE, "matmul", ps[:, q * 64:(q + 1) * 64], YBD[:, q, :], cb[:, C_ISEL:C_ISEL + 64], start=True, stop=True, reads=[kY, "cb"], writes=[pk], track=(q == 3))
        t0 = 0
        col0 = gc * CH - OWN0
        if col0 < 0:
            t0 = -col0
            col0 = 0
        nt = CH - t0
        cl = c * CH + t0
        lw = pv[:, V_LW + 4 * g: V_LW + 4 * g + 4].unsqueeze(2).to_broadcast([128, 4, nt])
        psv = ps[:, 0:256].rearrange("p (a b) -> p a b", b=64)[:, :, t0:CH]
        self.I(DVE, "tensor_tensor", yt[:, :, 0:nt], psv, lw, ALU.mult, reads=[pk, "pv"], writes=[kyt])
        self.I(POOL, "tensor_tensor", yt[:, :, 0:nt], yt[:, :, 0:nt], bon[:, :, cl:cl + nt], ALU.add, reads=[kyt] + ["bon%d" % q for q in range(4)], writes=[kyt])
        self.I(DVE, "tensor_tensor", self.yaT[:, 4 * g:4 * g + 4, col0:col0 + nt], yt[:, :, 0:nt], gst[:, :, cl:cl + nt], ALU.mult, reads=[kyt] + ["gst%d" % q for q in range(4)], writes=["yaT"])


def core_inputs(inp, c, pv_base):
    b, p = c // 2, c % 2
    x = inp["x_prompt"][b]
    if p == 1:
        xseq = np.ascontiguousarray(x)
    else:
        xseq = np.concatenate([np.zeros((1024, D), np.float32), x[:1024]], axis=0)
    pa = permA()
    return {
        "xseq": xseq,
        "xsamp": np.ascontiguousarray(inp["x_sample"][16 * c:16 * c + 16].reshape(NSM, D)),
        "pvec": core_pvec(pv_base, p),
        "st_pool": np.ascontiguousarray(inp["state_pool"][0, 16 * c:16 * c + 16]),
        "st_conv": np.ascontiguousarray(inp["state_conv"][0, 16 * c:16 * c + 16]),
        "st_shift": np.ascontiguousarray(inp["state_shift"][0, 16 * c:16 * c + 16, 0][:, pa]),
        "st_wkv": np.ascontiguousarray(inp["state_wkv"][0, 16 * c:16 * c + 16].reshape(256, 4096)),
    }


def shared_inputs(inp):
    w_in = inp["w_in"][0]
    wG = np.empty((D, 16, 256), np.float32)
    wG[:, :, 0:128] = w_in[:, 4384:6432].reshape(D, 16, 128)
    wG[:, :, 128:256] = w_in[:, 6432:8480].reshape(D, 16, 128)
    wfi = inp["w_ffn_in"][0]
    wF = np.empty((D, NJ, 256), np.float32)
    wF[:, :, 0:128] = wfi[:, 0:DFF].reshape(D, NJ, 128)
    wF[:, :, 128:256] = wfi[:, DFF:2 * DFF].reshape(D, NJ, 128)
    gpost = np.empty((128, 2, D), np.float32)
    gpost[:, 0, :] = inp["norm_post_mix"][0][None, :]
    gpost[:, 1, :] = inp["norm_post_ffn"][0][None, :]
    return {
        "wA": np.ascontiguousarray(w_in[:, permA()]),
        "consts": host_consts(),
        "w2": np.ascontiguousarray(inp["w2"][0]),
        "a2": np.ascontiguousarray(inp["a2"][0]),
        "g2": np.ascontiguousarray(inp["g2"][0]),
        "wP": np.ascontiguousarray(w_in[:, 3360:4384]),
        "wG": wG,
        "wa": np.ascontiguousarray(inp["w_branch_a"][0]),
        "wbr": np.ascontiguousarray(inp["w_branch_b"][0]),
        "poolw": np.ascontiguousarray(inp["pool_w"][0]),
        "wout": np.ascontiguousarray(inp["w_out"][0]),
        "gpost": gpost,
        "wF": wF,
        "wfo": np.ascontiguousarray(inp["w_ffn_out"][0]),
    }


_NC_CACHE = {}


def get_nc(dbg=(), stages="SABC"):
    key = (tuple(sorted(dbg)), stages)
    if key not in _NC_CACHE:
        B = Builder(dbg=set(dbg), stages=stages)
        nc = B.build()
        _NC_CACHE[key] = (nc, B)
    return _NC_CACHE[key]


def run_cores(inp, cores, dbg=(), stages="SABC", trace=False):
    nc, B = get_nc(dbg, stages)
    sh = shared_inputs(inp)
    pv_base = host_pvec(inp)
    names = set(B.ins.keys())
    maps = []
    for c in cores:
        m = dict(sh)
        m.update(core_inputs(inp, c, pv_base))
        maps.append({k: v for k, v in m.items() if k in names})
    res = run_bass_kernel_spmd(nc, maps, core_ids=list(range(len(cores))), trace=trace)
    return res


def kernel(**inp):
    inp = {k: np.asarray(v) for k, v in inp.items()}
    res = run_cores(inp, list(range(8)))
    R = res.results
    pa = permA()
    y_prompt = np.empty((4, 2048, D), np.float32)
    y_sample = np.empty((128, 4, D), np.float32)
    p_shift = np.empty((1, 4, 1, DS), np.float32)
    p_wkv = np.empty((1, 4, 16, 64, 64), np.float32)
    p_pool = np.empty((1, 4, 15, 1024), np.float32)
    p_conv = np.empty((1, 4, 2, DFF), np.float32)
    s_shift = np.zeros((1, 128, 1, DS), np.float32)
    s_wkv = np.zeros((1, 128, 16, 64, 64), np.float32)
    s_pool = np.empty((1, 128, 15, 1024), np.float32)
    s_conv = np.empty((1, 128, 2, DFF), np.float32)
    for c in range(8):
        b, p = c // 2, c % 2
        r = R[c]
        y_prompt[b, p * 1024:(p + 1) * 1024] = r["o_y"][0:1024]
        y_sample[16 * c:16 * c + 16] = r["o_y"][1024:1024 + NSM].reshape(16, 4, D)
        if p == 1:
            p_shift[0, b, 0, pa] = r["o_pshift"]
            p_wkv[0, b] = r["o_pwkv"]
            p_pool[0, b] = r["o_ppool"]
            p_conv[0, b] = r["o_pconv"]
        s_pool[0, 16 * c:16 * c + 16] = r["o_spool"]
        s_conv[0, 16 * c:16 * c + 16] = r["o_sconv"]
        if "o_sshift" in r:
            s_shift[0, 16 * c:16 * c + 16, 0][:, pa] = r["o_sshift"]
            s_wkv[0, 16 * c:16 * c + 16] = r["o_swkv"].reshape(16, 16, 64, 64)
    return (y_prompt, y_sample, p_shift, p_wkv, p_pool, p_conv, s_shift, s_wkv, s_pool, s_conv)
```

```python
import os
import numpy as np
import concourse.bass as bass
import concourse.mybir as mybir
from contextlib import ExitStack
from concourse.bass_utils import run_bass_kernel_spmd

F32 = mybir.dt.float32
BF16 = mybir.dt.bfloat16
AF = mybir.ActivationFunctionType
ALU = mybir.AluOpType
AX = mybir.AxisListType

PE, ACT, DVE, POOL, SP = "pe", "act", "dve", "pool", "sp"
NDMASEM = 12
VCLOCK = os.environ.get('VCLOCK', '1') == '1'
EMBED_WAIT = os.environ.get('EMBW', '1') == '1'
SAME_ENGINE_NOWAIT = os.environ.get('SENW', '0') == '1'


class Sched:
    def __init__(self, nc, es):
        self.nc = nc
        self.q = {e: [] for e in (PE, ACT, DVE, POOL, SP)}
        self.cnt = {e: 0 for e in (PE, ACT, DVE, POOL)}
        self.sem = {e: es.enter_context(nc.semaphore("s_" + e)) for e in (PE, ACT, DVE, POOL)}
        self.dsem = {}
        self.dcnt = {}
        for e in (SP, "spo", "bg", POOL):
            self.dsem[e] = [es.enter_context(nc.semaphore("d_%s%d" % (e, i))) for i in range(NDMASEM)]
            self.dcnt[e] = 0
        self.pending = {e: [] for e in (PE, ACT, DVE, POOL, SP)}
        self.tok_ring = {}
        self.clock = {}
        self.tclk = {}
        self.lastw = {}
        self.readers = {}
        self.all_tokens = {}

    def _deps(self, eng, reads, writes):
        toks = []
        for k in reads:
            w = self.lastw.get(k)
            if w is not None:
                toks.append(w)
        for k in writes:
            w = self.lastw.get(k)
            if w is not None:
                toks.append(w)
            toks.extend(self.readers.get(k, ()))
        clk = self.clock.setdefault(eng, {})
        need = {}
        for tok in toks:
            s, v, src = tok[0], tok[1], tok[2]
            if src == eng and eng == PE:
                continue
            if clk.get(s.name, 0) >= v:
                continue
            if need.get(s.name, (None, 0))[1] < v:
                need[s.name] = (s, v, self.tclk.get((s.name, v), {}))
        for (s, v) in self.pending[eng]:
            if clk.get(s.name, 0) >= v:
                continue
            if need.get(s.name, (None, 0))[1] < v:
                need[s.name] = (s, v, self.tclk.get((s.name, v), {}))
        self.pending[eng] = []
        if VCLOCK and len(need) > 1:
            drop = set()
            for name, (s, v, c) in need.items():
                for name2, (s2, v2, c2) in need.items():
                    if name2 != name and name2 not in drop and c2.get(name, 0) >= v:
                        drop.add(name)
                        break
            for name in drop:
                del need[name]
        out = []
        for name, (s, v, c) in need.items():
            if clk.get(name, 0) < v:
                clk[name] = v
            for n2, v2 in c.items():
                if clk.get(n2, 0) < v2:
                    clk[n2] = v2
            out.append((s, v))
        return out

    def _stamp(self, eng, tok):
        c = dict(self.clock.get(eng, {}))
        c[tok[0].name] = tok[1]
        self.tclk[(tok[0].name, tok[1])] = c

    def barrier(self):
        if self.frozen:
            return
        allw = []
        for e in (PE, ACT, DVE, POOL):
            if self.cnt[e] > 0:
                allw.append((self.sem[e], self.cnt[e], e))
        for name, (s, v) in self.all_tokens.items():
            if self.tok_ring.get(name) != "bg":
                allw.append((s, v, "dma"))
        for eng in (PE, ACT, DVE, POOL, SP):
            for (s, v, src) in allw:
                if src == eng:
                    continue
                self.pending[eng].append((s, v))

    def _commit(self, tok, reads, writes):
        for k in writes:
            self.lastw[k] = tok
            self.readers[k] = []
        for k in reads:
            lst = self.readers.setdefault(k, [])
            lst[:] = [t for t in lst if t[0].name != tok[0].name]
            lst.append(tok)

    frozen = False

    def op(self, eng, fn, reads=(), writes=(), track=True):
        if self.frozen:
            return None
        waits = self._deps(eng, reads, writes)
        tok = None
        if not track:
            assert eng == PE
            self.pend_r = getattr(self, "pend_r", set()) | set(reads)
        if track:
            self.cnt[eng] += 1
            tok = (self.sem[eng], self.cnt[eng], eng)
            self._stamp(eng, tok)
            if eng == PE and getattr(self, "pend_r", None):
                reads = list(set(reads) | self.pend_r)
                self.pend_r = set()
            self._commit(tok, reads, writes)
        self.q[eng].append((waits, fn, tok))
        return tok

    def dma(self, qeng, out, in_, reads=(), writes=(), ring=None, **kw):
        if self.frozen:
            return None
        waits = self._deps(qeng, reads, writes)
        rk = ring or qeng
        n = self.dcnt[rk]
        self.dcnt[rk] += 1
        s = self.dsem[rk][n % NDMASEM]
        v = 16 * (n // NDMASEM + 1)
        tok = (s, v, "dma")
        self._stamp(qeng, tok)
        self._commit(tok, reads, writes)
        self.q[qeng].append((waits, lambda e: e.dma_start(out=out, in_=in_, **kw), tok))
        self.all_tokens[s.name] = (s, v)
        self.tok_ring[s.name] = rk
        return tok

    def emit(self):
        nc = self.nc
        fin = dict(self.all_tokens)
        for e in (PE, ACT, DVE, POOL):
            if self.cnt[e] > 0:
                fin[self.sem[e].name] = (self.sem[e], self.cnt[e])
        q = self.q
        with nc.Block() as block:
            def run(eng_name):
                def body(e):
                    for (waits, fn, tok) in q[eng_name]:
                        emb = None
                        if EMBED_WAIT and waits:
                            emb = waits[-1]
                            waits = waits[:-1]
                        for (s, v) in waits:
                            e.wait_ge(s, v)
                        ins = fn(e)
                        if emb is not None:
                            ins._wait_ge(emb[0], emb[1])
                        if tok is not None:
                            ins.then_inc(tok[0], 16 if tok[2] == "dma" else 1)
                    if eng_name == SP:
                        for name, (s, v) in fin.items():
                            e.wait_ge(s, v)
                return body
            block.tensor(run(PE))
            block.scalar(run(ACT))
            block.vector(run(DVE))
            block.gpsimd(run(POOL))
            block.sync(run(SP))


D = 2048
DS = 3360
NPR = 2048
NSM = 64
OWN0 = 992
NOWN = 1120
BLK = 512
NH = 512
CH = 64
C_ID, C_BONES, C_ISEL, C_MUS, C_MUI, C_MLS, C_M01 = 0, 128, 256, 320, 448, 576, 704
NCONST = 704 + 512
V_MULR = 0
V_MU = 4
V_W0, V_A0, V_KK, V_KA, V_RK, V_LW, V_LB = 28, 36, 44, 52, 60, 68, 76
V_G1 = 84
V_PS = 100
V_G3 = 108
V_FLAG = 124
V_INVC = 125
V_CW = 189
NV = 189 + 176
DFF = 5632
NJ = 44
DECAY_C = 0.6065306597126334


def host_consts():
    c = np.zeros((128, NCONST), np.float32)
    p = np.arange(128)[:, None]
    j = np.arange(128)[None, :]
    c[:, C_ID:C_ID + 128] = (p == j)
    c[:, C_BONES:C_BONES + 128] = (p // 64 == j // 64)
    c[:, C_ISEL:C_ISEL + 64] = (p % 64 == np.arange(64)[None, :])
    c[:, C_MUS:C_MUS + 128] = (p % 64 < j % 64)
    c[:, C_MUI:C_MUI + 128] = (p % 64 <= j % 64)
    c[:, C_MLS:C_MLS + 128] = (p % 64 > j % 64)
    c[:, C_M01:C_M01 + 512] = (np.arange(512)[None, :] % 64 != 0)
    return c


def permA():
    idx = list(range(3072, 3360))
    for q in range(8):
        idx += list(range(q * 128, q * 128 + 128))
        idx += list(range(1024 + q * 128, 1024 + q * 128 + 128))
        idx += list(range(2048 + q * 128, 2048 + q * 128 + 128))
    return np.array(idx)


def host_pvec(inp):
    v = np.zeros((128, NV), np.float32)
    mu = inp["mu_shift"][0]
    v[0:64, 0] = mu[3072:3136]
    v[0:64, 1] = mu[3136:3200]
    v[0:128, 2] = mu[3200:3328]
    v[0:32, 3] = mu[3328:3360]
    for q in range(8):
        for t in range(3):
            v[:, V_MU + 3 * q + t] = mu[t * 1024 + q * 128: t * 1024 + q * 128 + 128]
    for (col, name) in ((V_W0, "w0"), (V_A0, "a0"), (V_KK, "k_k"), (V_KA, "k_a"), (V_RK, "r_k"), (V_LW, "lnx_w"), (V_LB, "lnx_b")):
        a = inp[name][0].reshape(-1)
        for q in range(8):
            v[:, col + q] = a[q * 128:(q + 1) * 128]
    g = inp["norm_pre_mix"][0]
    g3 = inp["norm_pre_ffn"][0]
    for k in range(16):
        v[:, V_G1 + k] = g[k * 128:(k + 1) * 128]
        v[:, V_G3 + k] = g3[k * 128:(k + 1) * 128]
    psc = inp["pool_scale"][0]
    for k in range(8):
        v[:, V_PS + k] = psc[k * 128:(k + 1) * 128]
    cw, cbias = inp["conv_w"][0], inp["conv_b"][0]
    for j in range(NJ):
        for t in range(3):
            v[:, V_CW + 4 * j + t] = cw[t, j * 128:(j + 1) * 128]
        v[:, V_CW + 4 * j + 3] = cbias[j * 128:(j + 1) * 128]
    return v


def core_pvec(base, p):
    v = base.copy()
    v[:, V_FLAG] = float(p)
    for gi, win in enumerate((2, 4, 8, 16)):
        for j in range(16):
            pos = p * 1024 + j
            v[:, V_INVC + gi * 16 + j] = 1.0 / min(win, pos + 1)
    return v


class _TAlias:
    def __init__(self, phys, alias, prefix="t_"):
        self.phys = phys
        self.alias = alias
        self.prefix = prefix

    def _n(self, n):
        return self.alias.get(n, n)

    def __getitem__(self, n):
        return self.phys[self._n(n)]

    def key(self, n):
        return self.prefix + self._n(n)


class StopBuild(Exception):
    pass


class Builder:
    stop_at = None

    def ckpt(self, name):
        if self.stop_at == name and not self.S.frozen:
            print("frozen at", name)
            self.S.frozen = True
            self.dbg = set()

    def __init__(self, dbg=None, stages="A"):
        self.dbg = dbg or set()
        self.stages = stages
        self.nc = bass.Bass("TRN2", target_bir_lowering=False)
        self.ins = {}
        self.outs = {}
        self.psn = 0

    def din(self, name, shape, dt=F32):
        t = self.nc.dram_tensor(name, list(shape), dt, kind="ExternalInput").ap()
        self.ins[name] = t
        return t

    def dout(self, name, shape, dt=F32):
        t = self.nc.dram_tensor(name, list(shape), dt, kind="ExternalOutput").ap()
        self.outs[name] = t
        return t

    def I(self, eng, meth, *args, reads=(), writes=(), track=True, **kw):
        return self.S.op(eng, lambda e: getattr(e, meth)(*args, **kw), reads=reads, writes=writes, track=track)

    def stage_out(self, name, tile_ap, shape, key):
        scr = self.nc.dram_tensor("scr_" + name, list(shape), F32).ap()
        self.S.dma(SP, scr, tile_ap, reads=[key], writes=["scr_" + name], ring="spo")
        return scr

    def bg(self, dst, src, name):
        self.S.dma(SP, dst, src, reads=["scr_" + name], ring="bg", allow_slow_non_contiguous=True)

    def sb(self, es, name, shape, dt=F32):
        return es.enter_context(self.nc.sbuf_tensor(name, list(shape), dt))

    def psum(self):
        i = self.psn % 8
        self.psn += 1
        return self.PS[i], "ps%d" % i

    def build(self):
        nc = self.nc
        xseq = self.xseq = self.din("xseq", [NPR, D])
        xsamp = self.xsamp = self.din("xsamp", [NSM, D])
        wA = self.din("wA", [D, DS])
        consts = self.din("consts", [128, NCONST])
        pvec = self.din("pvec", [128, NV])
        w2 = self.din("w2", [64, 1024])
        a2 = self.din("a2", [64, 1024])
        g2 = self.din("g2", [160, 1024])
        self.wP = self.din("wP", [D, 1024])
        self.wG = self.din("wG", [D, 16, 256])
        self.wa = self.din("wa", [1024, D])
        self.wbr = self.din("wbr", [1024, D])
        self.poolw = self.din("poolw", [4, 256, 256])
        self.wout = self.din("wout", [D, D])
        self.gpost = self.din("gpost", [128, 2, D])
        self.wF = self.din("wF", [D, NJ, 256])
        self.wfo = self.din("wfo", [DFF, D])
        self.st_pool = self.din("st_pool", [16, 15, 1024])
        self.st_conv = self.din("st_conv", [16, 2, DFF])
        self.o_pshift = self.dout("o_pshift", [DS])
        self.o_pwkv = self.dout("o_pwkv", [16, 64, 64])
        self.o_ppool = self.dout("o_ppool", [15, 1024])
        self.o_pconv = self.dout("o_pconv", [2, DFF])
        self.o_spool = self.dout("o_spool", [16, 15, 1024])
        self.o_sconv = self.dout("o_sconv", [16, 2, DFF])
        self.o_y = self.dout("o_y", [1024 + NSM, D])
        self.st_shift = self.din("st_shift", [16, DS])
        self.st_wkv = self.din("st_wkv", [256, 4096])
        self.o_sshift = self.dout("o_sshift", [16, DS])
        self.o_swkv = self.dout("o_swkv", [256, 4096])
        self.x1s = nc.dram_tensor("x1s", [NOWN, D], F32).ap()
        self.scrS = nc.dram_tensor("scrS", [NSM, 6, 1024], F32).ap()
        self.scrY = nc.dram_tensor("scrY", [NSM, 1024], F32).ap()
        self.scrH = nc.dram_tensor("scrH", [128, 16, NOWN], BF16).ap()
        if "ya" in self.dbg:
            self.o_ya = self.dout("d_ya", [128, 8, NOWN])
        with ExitStack() as es:
            S = self.S = Sched(nc, es)
            self.PS = [es.enter_context(nc.psum_tensor("ps%d" % i, [128, 512], F32)) for i in range(8)]
            cf = self.cf = self.sb(es, "cf", [128, NCONST])
            cb = self.cb = self.sb(es, "cb", [128, 320], BF16)
            pv = self.pv = self.sb(es, "pv", [128, NV])
            S.dma(SP, cf[:], consts, writes=["cf"])
            S.dma(SP, pv[:], pvec, writes=["pv"])
            S.dma(POOL, cb[:], consts[:, 0:320], writes=["cb"])
            self.wslot = 0
            bufX = self.bufX = self.sb(es, "bufX", [128, 16, NOWN], BF16)
            self.yaT = bufX[:, 0:8, :]
            self.ybT = bufX[:, 8:16, :]
            with ExitStack() as esw:
                self.wb = [self.sb(esw, "wb%d" % i, [128, 16, 384], BF16) for i in range(2)]
                self.wA_ap = wA
                with ExitStack() as esA:
                    w2b = self.w2b = self.sb(esA, "w2b", [64, 1024], BF16)
                    a2b = self.a2b = self.sb(esA, "a2b", [64, 1024], BF16)
                    g2a = self.g2a = self.sb(esA, "g2a", [128, 1024], BF16)
                    g2b = self.g2b = self.sb(esA, "g2b", [128, 1024], BF16)
                    S.dma(POOL, w2b[:], w2, writes=["w2b"])
                    S.dma(POOL, a2b[:], a2, writes=["a2b"])
                    S.dma(POOL, g2a[:], g2[0:128, :], writes=["g2a"])
                    self.I(POOL, "memset", g2b[:], 0.0, writes=["g2b"])
                    S.dma(POOL, g2b[0:32, :], g2[128:160, :], writes=["g2b"])
                    if "S" in self.stages:
                        self.stageS()
                    self.stageA(esA, xseq, wA)
                if "ya" in self.dbg:
                    with ExitStack() as esd:
                        yaf = self.sb(esd, "yaf", [128, 8, NOWN], F32)
                        self.I(DVE, "tensor_copy", yaf[:], self.yaT, reads=["yaT"], writes=["yaf"])
                        S.dma(SP, self.o_ya, yaf[:], reads=["yaf"], ring="spo")
                        S.barrier()
                if "B" in self.stages:
                    with ExitStack() as esm:
                        self.mT = self.sb(esm, "mT", [128, 16, NOWN], BF16)
                        with ExitStack() as esb:
                            self.hTo = self.sb(esb, "hTo", [128, 16, NOWN], BF16)
                            self.stageB12(esb)
                        self.stageB3(esm)
            if "C" in self.stages:
                self.stageC(es)
            S.emit()
        return nc

    def tok_groups(self):
        return [(0, 512), (512, 512), (1024, NOWN - 1024)]

    def stageB12(self, es_outer):
        S = self.S
        pv, cf, cb = self.pv, self.cf, self.cb
        hT = self.hTo
        with ExitStack() as es:
            sb = lambda n, s, d=F32: self.sb(es, n, s, d)
            self.xt = [sb("xtB", [128, D])]
            self.xb = [sb("xbB", [128, D], BF16)]
            self.xst = [sb("xstB", [128, 4])]
            for (c0, c1, k) in ((0, 32, "scrH_1"), (32, 544, "scrH_2"), (544, 1056, "scrH_3"), (1056, NOWN, "scrH_s")):
                S.dma(SP, hT[:, :, c0:c1], self.scrH[:, :, c0:c1], reads=[k], writes=["hTo"])
            self.ckpt("B0")
            ybT = self.ybT
            with ExitStack() as esp:
                sbp = lambda n, s, d=F32: self.sb(esp, n, s, d)
                pwb = sbp("pwb", [128, 4, 2, 256], BF16)
                S.dma(POOL, pwb[:], self.poolw.rearrange("g (k p) n -> p g k n", p=128), writes=["pwb"])
                phT = sbp("phT", [128, 8, 16, 15])
                for half in range(2):
                    sp_t = sbp("sp_t%d" % half, [120, 1024])
                    S.dma(SP, sp_t[:], self.st_pool[8 * half: 8 * half + 8].rearrange("b j c -> (b j) c"), writes=["sp_t%d" % half])
                    for c4 in range(2):
                        ps, pk = self.psum()
                        for ch in range(4):
                            self.I(PE, "transpose", ps[:, ch * 120:(ch + 1) * 120], sp_t[:, (c4 * 4 + ch) * 128:(c4 * 4 + ch + 1) * 128], cf[0:120, 0:120],
                                   reads=["sp_t%d" % half, "cf"], writes=[pk], track=(ch == 3))
                        self.I(DVE, "tensor_copy", phT[:, c4 * 4:c4 * 4 + 4, 8 * half:8 * half + 8, :],
                               ps[:, 0:480].rearrange("p (a b j) -> p a b j", a=4, b=8), reads=[pk], writes=["phT"])
                self.ckpt("B1a")
                S.dma(SP, self.o_spool[:, 0:11, :], self.st_pool[:, 4:15, :], ring="spo")
                zp = sbp("zp", [128, NOWN])
                pa_ = sbp("ppA", [128, NOWN])
                pb_ = sbp("ppB", [128, NOWN])
                dT = sbp("dT", [128, 8, NOWN], BF16)
                self.I(POOL, "memset", dT[:], 0.0, writes=["dT"])
                ppo = sbp("ppo", [128, 8, 15])
                spo = sbp("spo", [128, 8, 16, 4])
                bs = sbp("bs", [128, 16, 19])
                bs2 = sbp("bs2", [128, 16, 19])
                slabs = [(self.wP[:, ch * 128:(ch + 1) * 128], 128) for ch in range(8)]
                stream = self.slab_stream(slabs)
                NP_ = 1056
                for ch in range(8):
                    wt, wk = next(stream)
                    gi = ch // 2
                    win = 2 << gi
                    for (t0, tn) in self.tok_groups():
                        ps, pk = self.psum()
                        for k in range(16):
                            self.I(PE, "matmul", ps[:, 0:tn], wt[:, k, 0:128], hT[:, k, t0:t0 + tn], start=(k == 0), stop=(k == 15),
                                   reads=[wk, "hTo"], writes=[pk], track=(k == 15))
                        self.I(ACT, "copy", zp[:, t0:t0 + tn], ps[:, 0:tn], reads=[pk], writes=["zp"])
                    src, ksrc = zp, "zp"
                    bufs = [(pa_, "ppA"), (pb_, "ppB")]
                    for j in range(gi + 1):
                        sh = 1 << j
                        dst, kdst = bufs[j % 2]
                        self.I(DVE if j % 2 == 0 else POOL, "tensor_tensor", dst[:, sh:NP_], src[:, sh:NP_], src[:, 0:NP_ - sh], ALU.add,
                               reads=[ksrc], writes=[kdst])
                        src, ksrc = dst, kdst
                    lo = win - 1
                    self.I(DVE, "scalar_tensor_tensor", dT[:, ch, lo:NP_], src[:, lo:NP_], 1.0 / win, zp[:, lo:NP_], ALU.mult, ALU.subtract,
                           reads=[ksrc, "zp"], writes=["dT"])
                    other, kother = bufs[(gi + 1) % 2]
                    self.I(POOL, "tensor_tensor", other[:, 32:48], src[:, 32:48], pv[:, V_INVC + gi * 16: V_INVC + gi * 16 + 16], ALU.mult,
                           reads=[ksrc, "pv"], writes=[kother])
                    self.I(POOL, "tensor_tensor", dT[:, ch, 32:48], other[:, 32:48], zp[:, 32:48], ALU.subtract, reads=[kother, "zp"], writes=["dT"])
                    self.I(POOL, "tensor_copy", ppo[:, ch, :], zp[:, NP_ - 15:NP_], reads=["zp"], writes=["ppo"])
                    zps = zp[:, NP_:NOWN].rearrange("p (b t) -> p b t", t=4)
                    self.I(POOL, "tensor_copy", bs[:, :, 0:15], phT[:, ch, :, :], reads=["phT"], writes=["bs"])
                    self.I(POOL, "tensor_copy", bs[:, :, 15:19], zps, reads=["zp"], writes=["bs"])
                    self.I(POOL, "tensor_copy", spo[:, ch, :, :], zps, reads=["zp"], writes=["spo"])
                    ssrc, kss = bs, "bs"
                    sbufs = [(bs2, "bs2"), (bs, "bs")]
                    for j in range(gi + 1):
                        sh = 1 << j
                        dst, kdst = sbufs[j % 2]
                        self.I(DVE, "tensor_tensor", dst[:, :, sh:19], ssrc[:, :, sh:19], ssrc[:, :, 0:19 - sh], ALU.add, reads=[kss], writes=[kdst])
                        ssrc, kss = dst, kdst
                    self.I(DVE, "scalar_tensor_tensor", dT[:, ch, NP_:NOWN].rearrange("p (b t) -> p b t", t=4), ssrc[:, :, 15:19], 1.0 / win, zps,
                           ALU.mult, ALU.subtract, reads=[kss, "zp"], writes=["dT"])
                self.ckpt("B1b")
                scr = self.stage_out("ppo", ppo[:], [128, 8, 15], "ppo")
                for ch in range(8):
                    self.bg(self.o_ppool[:, ch * 128:(ch + 1) * 128].rearrange("j p -> p j"), scr[:, ch, :], "ppo")
                sprow = sbp("sprow", [NSM, 1024])
                for c4 in range(2):
                    ps, pk = self.psum()
                    for ch in range(4):
                        self.I(PE, "transpose", ps[0:NSM, ch * 128:(ch + 1) * 128], spo[:, c4 * 4 + ch, :, :].rearrange("p b t -> p (b t)"), cf[:, C_ID:C_ID + 128],
                               reads=["spo", "cf"], writes=[pk], track=(ch == 3))
                    self.I(DVE, "tensor_copy", sprow[:, c4 * 512:(c4 + 1) * 512], ps[0:NSM, :], reads=[pk], writes=["sprow"])
                for b in range(16):
                    S.dma(SP, self.o_spool[b, 11:15, :], sprow[4 * b:4 * b + 4, :], reads=["sprow"], ring="spo")
                self.ckpt("B1c")
                for oc in range(8):
                    gi, o2 = oc // 2, oc % 2
                    for (t0, tn) in self.tok_groups():
                        ps, pk = self.psum()
                        for kk in range(2):
                            self.I(PE, "matmul", ps[:, 0:tn], pwb[:, gi, kk, o2 * 128:(o2 + 1) * 128], dT[:, 2 * gi + kk, t0:t0 + tn],
                                   start=(kk == 0), stop=(kk == 1), reads=["pwb", "dT"], writes=[pk], track=(kk == 1))
                        self.I(ACT, "activation", ybT[:, oc, t0:t0 + tn], ps[:, 0:tn], AF.Copy, scale=pv[:, V_PS + oc: V_PS + oc + 1],
                               reads=[pk, "pv"], writes=["ybT"])
                S.barrier()
            self.ckpt("B1d")
            mT = self.mT
            tmp = [sb("b2t%d" % i, [128, 512]) for i in range(4)]
            njs = 16

            def issue(j):
                slot = self.wslot % 2
                self.wslot += 1
                key = "wb%d" % slot
                w = self.wb[slot]
                wflat = w[:].rearrange("p a b -> p (a b)")
                S.dma(POOL, wflat[:, 0:4096].rearrange("p (k n) -> p k n", n=256), self.wG[:, j, :].rearrange("(k p) n -> p k n", p=128), writes=[key])
                S.dma(POOL, wflat[:, 4096:5120].rearrange("p (k n) -> p k n", n=128), self.wa[:, j * 128:(j + 1) * 128].rearrange("(k p) n -> p k n", p=128), writes=[key])
                S.dma(POOL, wflat[:, 5120:6144].rearrange("p (k n) -> p k n", n=128), self.wbr[:, j * 128:(j + 1) * 128].rearrange("(k p) n -> p k n", p=128), writes=[key])
                return (wflat, key)
            cur = issue(0)
            for j in range(njs):
                nxt = issue(j + 1) if j + 1 < njs else None
                wflat, wk = cur
                wg = wflat[:, 0:4096].rearrange("p (k n) -> p k n", n=256)
                wa_ = wflat[:, 4096:5120].rearrange("p (k n) -> p k n", n=128)
                wb_ = wflat[:, 5120:6144].rearrange("p (k n) -> p k n", n=128)
                for (t0, tn) in self.tok_groups():
                    psa, pka = self.psum()
                    psb, pkb = self.psum()
                    ppa, pkpa = self.psum()
                    ppb, pkpb = self.psum()
                    for k in range(16):
                        self.I(PE, "matmul", psa[:, 0:tn], wg[:, k, 0:128], hT[:, k, t0:t0 + tn], start=(k == 0), stop=(k == 15),
                               reads=[wk, "hTo"], writes=[pka], track=(k == 15))
                    for k in range(16):
                        self.I(PE, "matmul", psb[:, 0:tn], wg[:, k, 128:256], hT[:, k, t0:t0 + tn], start=(k == 0), stop=(k == 15),
                               reads=[wk, "hTo"], writes=[pkb], track=(k == 15))
                    for k in range(8):
                        self.I(PE, "matmul", ppa[:, 0:tn], wa_[:, k, :], self.yaT[:, k, t0:t0 + tn], start=(k == 0), stop=(k == 7),
                               reads=[wk, "yaT"], writes=[pkpa], track=(k == 7))
                    for k in range(8):
                        self.I(PE, "matmul", ppb[:, 0:tn], wb_[:, k, :], ybT[:, k, t0:t0 + tn], start=(k == 0), stop=(k == 7),
                               reads=[wk, "ybT"], writes=[pkpb], track=(k == 7))
                    self.I(ACT, "activation", tmp[0][:, 0:tn], psa[:, 0:tn], AF.Sigmoid, reads=[pka], writes=["b2t0"])
                    self.I(ACT, "activation", tmp[1][:, 0:tn], psb[:, 0:tn], AF.Sigmoid, reads=[pkb], writes=["b2t1"])
                    self.I(DVE, "tensor_tensor", tmp[2][:, 0:tn], tmp[0][:, 0:tn], ppa[:, 0:tn], ALU.mult, reads=["b2t0", pkpa], writes=["b2t2"])
                    self.I(DVE, "tensor_tensor", tmp[3][:, 0:tn], tmp[1][:, 0:tn], ppb[:, 0:tn], ALU.mult, reads=["b2t1", pkpb], writes=["b2t3"])
                    self.I(POOL, "tensor_tensor", mT[:, j, t0:t0 + tn], tmp[2][:, 0:tn], tmp[3][:, 0:tn], ALU.add, reads=["b2t2", "b2t3"], writes=["mT"])
                cur = nxt
            S.barrier()

    def tok_tiles(self):
        self.ckpt("B2")
        return [(0, 32)] + [(32 + 128 * i, 128) for i in range(8)] + [(1056, NSM)]

    def stageB3(self, es_outer):
        S = self.S
        pv, cf, cb = self.pv, self.cf, self.cb
        mT, h2T = self.mT, self.bufX
        with ExitStack() as es:
            sb = lambda n, s, d=F32: self.sb(es, n, s, d)
            woutb = sb("woutb", [128, 16, D], BF16)
            for n in range(4):
                S.dma(POOL, woutb[:, :, n * 512:(n + 1) * 512], self.wout[:, n * 512:(n + 1) * 512].rearrange("(k p) n -> p k n", p=128), writes=["woutb%d" % n])
            gp = sb("gp", [128, D])
            S.dma(SP, gp[:], self.gpost[:, 0, :], writes=["gp"])
            mo = sb("mo", [128, D])
            xt = sb("xt3", [128, D])
            x1 = sb("x1t", [128, D])
            xb = sb("xb3", [128, D], BF16)
            st = sb("st3", [128, 16])
            for (c0, nt) in self.tok_tiles():
                xrows = self.xseq[OWN0 + c0: OWN0 + c0 + nt, :] if c0 < 1056 else self.xsamp
                S.dma(SP, xt[0:nt, :], xrows, writes=["xt3"])
                self.I(POOL, "memset", st[:, 0:4], 0.0, writes=["st3"])
                self.ckpt("B3pre")
                for n in range(4):
                    ps, pk = self.psum()
                    for k in range(16):
                        self.I(PE, "matmul", ps[0:nt, :], mT[:, k, c0:c0 + nt], woutb[:, k, n * 512:(n + 1) * 512], start=(k == 0), stop=(k == 15),
                               reads=["mT", "woutb%d" % n], writes=[pk], track=(k == 15))
                    self.I(DVE, "tensor_copy", mo[0:nt, n * 512:(n + 1) * 512], ps[0:nt, :], reads=[pk], writes=["mo"])
                    self.I(ACT, "activation", xb[0:nt, n * 512:(n + 1) * 512], mo[0:nt, n * 512:(n + 1) * 512], AF.Square, accum_out=st[0:nt, n:n + 1], reads=["mo"], writes=["xb3", "st3"])
                self.ckpt("B3a")
                self.I(DVE, "tensor_reduce", st[0:nt, 4:5], st[0:nt, 0:4], AX.X, ALU.add, reads=["st3"], writes=["st3"])
                self.I(DVE, "tensor_scalar", st[0:nt, 5:6], st[0:nt, 4:5], 1.0 / D, 1e-6, ALU.mult, ALU.add, reads=["st3"], writes=["st3"])
                self.I(ACT, "activation", st[0:nt, 6:7], st[0:nt, 5:6], AF.Sqrt, reads=["st3"], writes=["st3"])
                self.I(DVE, "reciprocal", st[0:nt, 7:8], st[0:nt, 6:7], reads=["st3"], writes=["st3"])
                self.I(DVE, "scalar_tensor_tensor", mo[0:nt, :], mo[0:nt, :], st[0:nt, 7:8], gp[0:nt, :], ALU.mult, ALU.mult, reads=["mo", "st3", "gp"], writes=["mo"])
                self.I(POOL, "tensor_tensor", x1[0:nt, :], mo[0:nt, :], xt[0:nt, :], ALU.add, reads=["mo", "xt3"], writes=["x1t"])
                self.ckpt("B3b")
                S.dma(SP, self.x1s[c0:c0 + nt, :], x1[0:nt, :], reads=["x1t"], writes=["x1s"])
                self.ckpt("B3c")
                self.norm_T(x1, "x1t", nt, h2T[:, :, c0:c0 + nt], "h2T", xb, "xb3", st, "st3", 8, V_G3)
                self.ckpt("B3d")
            S.barrier()

    def norm_T(self, xt, kx, ntok, hT, hkey, xb, kb, st, ks, sc0, gbase):
        self.I(POOL, "memset", st[:, sc0:sc0 + 1], 0.0, writes=[ks])
        self.I(ACT, "activation", xb[0:ntok, :], xt[0:ntok, :], AF.Square, accum_out=st[0:ntok, sc0:sc0 + 1], reads=[kx], writes=[kb, ks])
        self.I(DVE, "tensor_scalar", st[0:ntok, sc0 + 1:sc0 + 2], st[0:ntok, sc0:sc0 + 1], 1.0 / D, 1e-6, ALU.mult, ALU.add, reads=[ks], writes=[ks])
        self.I(ACT, "activation", st[0:ntok, sc0 + 2:sc0 + 3], st[0:ntok, sc0 + 1:sc0 + 2], AF.Sqrt, reads=[ks], writes=[ks])
        self.I(DVE, "reciprocal", st[0:ntok, sc0 + 3:sc0 + 4], st[0:ntok, sc0 + 2:sc0 + 3], reads=[ks], writes=[ks])
        self.I(ACT, "activation", xb[0:ntok, :], xt[0:ntok, :], AF.Copy, scale=st[0:ntok, sc0 + 3:sc0 + 4], reads=[kx, ks, kb], writes=[kb])
        for half in range(2):
            ps, pk = self.psum()
            pT = ps[:].bitcast(BF16).rearrange("p (a b) -> p a b", b=128)
            for k8 in range(8):
                kc = half * 8 + k8
                self.I(PE, "transpose", pT[:, k8, 0:ntok], xb[0:ntok, kc * 128:(kc + 1) * 128], self.cb[0:ntok, 0:ntok],
                       reads=[kb, "cb"], writes=[pk], track=(k8 == 7))
            gcol = self.pv[:, gbase + half * 8: gbase + half * 8 + 8].unsqueeze(2).to_broadcast([128, 8, ntok])
            self.I(DVE, "tensor_tensor", hT[:, half * 8:half * 8 + 8, :], pT[:, :, 0:ntok], gcol, ALU.mult, reads=[pk, "pv"], writes=[hkey])

    def stageC(self, es_outer):
        S = self.S
        pv, cf, cb = self.pv, self.cf, self.cb
        h2T = self.bufX
        NA = 1024 + NSM
        with ExitStack() as es:
            sb = lambda n, s, d=F32: self.sb(es, n, s, d)
            actT = sb("actT", [128, NJ, NA], BF16)
            with ExitStack() as es1:
                sb1 = lambda n, s, d=F32: self.sb(es1, n, s, d)
                wf = [sb1("wf%d" % i, [128, 16, 256], BF16) for i in range(2)]
                gt = [sb1("gt%d" % i, [128, NOWN]) for i in range(2)]
                up = [sb1("up%d" % i, [128, NOWN]) for i in range(2)]
                cv = sb1("cv", [128, NA])
                ge = sb1("ge", [128, NA])
                gs6 = sb1("gs6", [128, 16, 6])
                chT = sb1("chT", [128, NJ, 32])
                pco = sb1("pco", [128, NJ, 2])
                sco = sb1("sco", [128, NJ, 16, 2])
                ct = sb1("ct", [32, 1408])
                stc = self.st_conv.rearrange("b r c -> (b r) c")
                for pc in range(4):
                    S.dma(SP, ct[:], stc[:, pc * 1408:(pc + 1) * 1408], writes=["ct"])
                    ps, pk = self.psum()
                    for jj in range(11):
                        self.I(PE, "transpose", ps[:, jj * 32:(jj + 1) * 32], ct[:, jj * 128:(jj + 1) * 128], cf[0:32, 0:32],
                               reads=["ct", "cf"], writes=[pk], track=(jj == 10))
                    self.I(DVE, "tensor_copy", chT[:, pc * 11:(pc + 1) * 11, :], ps[:, 0:352].rearrange("p (a b) -> p a b", b=32), reads=[pk], writes=["chT"])

                def issue(j):
                    slot = j % 2
                    S.dma(POOL, wf[slot][:], self.wF[:, j, :].rearrange("(k p) n -> p k n", p=128), writes=["wf%d" % slot])
                cwc = lambda j, t: pv[:, V_CW + 4 * j + t: V_CW + 4 * j + t + 1]
                issue(0)
                for j in range(NJ):
                    if j + 1 < NJ:
                        issue(j + 1)
                    w, wk = wf[j % 2], "wf%d" % (j % 2)
                    g_, kg = gt[j % 2], "gt%d" % (j % 2)
                    u_, ku = up[j % 2], "up%d" % (j % 2)
                    for (t0, tn) in self.tok_groups():
                        psg, pkg = self.psum()
                        psu, pku = self.psum()
                        for k in range(16):
                            self.I(PE, "matmul", psg[:, 0:tn], w[:, k, 0:128], h2T[:, k, t0:t0 + tn], start=(k == 0), stop=(k == 15),
                                   reads=[wk, "h2T"], writes=[pkg], track=(k == 15))
                        for k in range(16):
                            self.I(PE, "matmul", psu[:, 0:tn], w[:, k, 128:256], h2T[:, k, t0:t0 + tn], start=(k == 0), stop=(k == 15),
                                   reads=[wk, "h2T"], writes=[pku], track=(k == 15))
                        self.I(ACT, "copy", g_[:, t0:t0 + tn], psg[:, 0:tn], reads=[pkg], writes=[kg])
                        self.I(DVE, "tensor_copy", u_[:, t0:t0 + tn], psu[:, 0:tn], reads=[pku], writes=[ku])
                    self.I(POOL, "tensor_scalar", g_[:, 30:32], g_[:, 30:32], pv[:, V_FLAG:V_FLAG + 1], None, ALU.mult, reads=[kg, "pv"], writes=[kg])
                    self.I(ACT, "activation", cv[:, 0:1024], g_[:, 32:1056], AF.Identity, bias=cwc(j, 3), scale=cwc(j, 2), reads=[kg, "pv"], writes=["cv"])
                    self.I(DVE, "scalar_tensor_tensor", cv[:, 0:1024], g_[:, 31:1055], cwc(j, 1), cv[:, 0:1024], ALU.mult, ALU.add, reads=[kg, "cv", "pv"], writes=["cv"])
                    self.I(DVE, "scalar_tensor_tensor", cv[:, 0:1024], g_[:, 30:1054], cwc(j, 0), cv[:, 0:1024], ALU.mult, ALU.add, reads=[kg, "cv", "pv"], writes=["cv"])
                    gss = g_[:, 1056:NOWN].rearrange("p (b t) -> p b t", t=4)
                    self.I(POOL, "tensor_copy", gs6[:, :, 0:2], chT[:, j, :].rearrange("p (b r) -> p b r", r=2), reads=["chT"], writes=["gs6"])
                    self.I(POOL, "tensor_copy", gs6[:, :, 2:6], gss, reads=[kg], writes=["gs6"])
                    cvs = cv[:, 1024:NA].rearrange("p (b t) -> p b t", t=4)
                    self.I(ACT, "activation", cvs, gs6[:, :, 2:6], AF.Identity, bias=cwc(j, 3), scale=cwc(j, 2), reads=["gs6", "pv"], writes=["cv"])
                    self.I(DVE, "scalar_tensor_tensor", cvs, gs6[:, :, 1:5], cwc(j, 1), cvs, ALU.mult, ALU.add, reads=["gs6", "cv", "pv"], writes=["cv"])
                    self.I(DVE, "scalar_tensor_tensor", cvs, gs6[:, :, 0:4], cwc(j, 0), cvs, ALU.mult, ALU.add, reads=["gs6", "cv", "pv"], writes=["cv"])
                    self.I(POOL, "tensor_copy", pco[:, j, :], g_[:, 1054:1056], reads=[kg], writes=["pco"])
                    self.I(POOL, "tensor_copy", sco[:, j, :, :], gss[:, :, 2:4], reads=[kg], writes=["sco"])
                    self.I(ACT, "activation", ge[:, :], cv[:, :], AF.Gelu_apprx_tanh, reads=["cv"], writes=["ge"])
                    self.I(DVE, "tensor_tensor", actT[:, j, 0:1024], ge[:, 0:1024], u_[:, 32:1056], ALU.mult, reads=["ge", ku], writes=["actT"])
                    self.I(POOL, "tensor_tensor", actT[:, j, 1024:NA], ge[:, 1024:NA], u_[:, 1056:NOWN], ALU.mult, reads=["ge", ku], writes=["actT"])
                scr = self.stage_out("pco", pco[:], [128, NJ, 2], "pco")
                for r in range(2):
                    self.bg(self.o_pconv[r].rearrange("(j p) -> p j", p=128), scr[:, :, r], "pco")
                scrow = sb1("scrow", [32, 1408])
                for pc in range(4):
                    for j4 in range(0, 11, 4):
                        nj = min(4, 11 - j4)
                        ps, pk = self.psum()
                        for jj in range(nj):
                            j = pc * 11 + j4 + jj
                            self.I(PE, "transpose", ps[0:32, jj * 128:(jj + 1) * 128], sco[:, j, :, :].rearrange("p b r -> p (b r)"), cf[:, C_ID:C_ID + 128],
                                   reads=["sco", "cf"], writes=[pk], track=(jj == nj - 1))
                        self.I(DVE, "tensor_copy", scrow[:, j4 * 128:(j4 + nj) * 128], ps[0:32, 0:nj * 128], reads=[pk], writes=["scrow"])
                    S.dma(SP, self.o_sconv.rearrange("b r c -> (b r) c")[:, pc * 1408:(pc + 1) * 1408], scrow[:], reads=["scrow"], ring="spo")
                S.barrier()
            with ExitStack() as es2:
                sb2 = lambda n, s, d=F32: self.sb(es2, n, s, d)
                dummy = sb2("dmy2", [128, 2])
                self.I(POOL, "memset", dummy[:], 0.0, writes=["h2T", "dmy2"])
                bxf = self.bufX[:].rearrange("p a b -> p (a b)")
                wo = [bxf[:, i * 5632:(i + 1) * 5632].rearrange("p (k n) -> p k n", n=512) for i in range(2)]
                fo = sb2("fo", [128, 5, D])
                gp2 = sb2("gp2", [128, D])
                S.dma(SP, gp2[:], self.gpost[:, 1, :], writes=["gp2"])
                x1t = sb2("x1r", [128, D])
                st = sb2("stc", [128, 5, 8])
                xb = sb2("xbc", [128, 512], BF16)
                tiles = [(128 * i, 128) for i in range(8)] + [(1024, NSM)]
                sets = [tiles[0:5], tiles[5:9]]
                wn = 0
                for tset in sets:
                    self.I(POOL, "memset", st[:], 0.0, writes=["stc"])
                    for n in range(4):
                        banks = [self.psum() for _ in tset]
                        for kp in range(4):
                            slot = wn % 2
                            wn += 1
                            S.dma(POOL, wo[slot], self.wfo[kp * 1408:(kp + 1) * 1408, n * 512:(n + 1) * 512].rearrange("(k p) n -> p k n", p=128),
                                  writes=["wo%d" % slot])
                            for ti, (c0, nt) in enumerate(tset):
                                ps, pk = banks[ti]
                                for jj in range(11):
                                    j = kp * 11 + jj
                                    self.I(PE, "matmul", ps[0:nt, :], actT[:, j, c0:c0 + nt], wo[slot][:, jj, :], start=(j == 0), stop=(j == NJ - 1),
                                           reads=["actT", "wo%d" % slot], writes=[pk], track=(jj == 10))
                        for ti, (c0, nt) in enumerate(tset):
                            ps, pk = banks[ti]
                            self.I(DVE, "tensor_copy", fo[0:nt, ti, n * 512:(n + 1) * 512], ps[0:nt, :], reads=[pk], writes=["fo%d" % ti])
                            self.I(ACT, "activation", xb[0:nt, :], fo[0:nt, ti, n * 512:(n + 1) * 512], AF.Square, accum_out=st[0:nt, ti, n:n + 1], reads=["fo%d" % ti], writes=["xbc", "stc"])
                    for ti, (c0, nt) in enumerate(tset):
                        self.I(DVE, "tensor_reduce", st[0:nt, ti, 4:5], st[0:nt, ti, 0:4], AX.X, ALU.add, reads=["stc"], writes=["stc"])
                        self.I(DVE, "tensor_scalar", st[0:nt, ti, 5:6], st[0:nt, ti, 4:5], 1.0 / D, 1e-6, ALU.mult, ALU.add, reads=["stc"], writes=["stc"])
                        self.I(ACT, "activation", st[0:nt, ti, 6:7], st[0:nt, ti, 5:6], AF.Sqrt, reads=["stc"], writes=["stc"])
                        self.I(DVE, "reciprocal", st[0:nt, ti, 7:8], st[0:nt, ti, 6:7], reads=["stc"], writes=["stc"])
                        xc0 = 32 + c0
                        S.dma(SP, x1t[0:nt, :], self.x1s[xc0:xc0 + nt, :], reads=["x1s"], writes=["x1r"])
                        self.I(DVE, "scalar_tensor_tensor", fo[0:nt, ti, :], fo[0:nt, ti, :], st[0:nt, ti, 7:8], gp2[0:nt, :], ALU.mult, ALU.mult,
                               reads=["fo%d" % ti, "stc", "gp2"], writes=["fo%d" % ti])
                        self.I(POOL, "tensor_tensor", fo[0:nt, ti, :], fo[0:nt, ti, :], x1t[0:nt, :], ALU.add, reads=["fo%d" % ti, "x1r"], writes=["fo%d" % ti])
                        S.dma(SP, self.o_y[c0:c0 + nt, :], fo[0:nt, ti, :], reads=["fo%d" % ti], ring="spo")

    def make_hT(self, xrows, ntok, hT, hkey, slot):
        S = self.S
        xt, xb, st = self.xt[slot], self.xb[slot], self.xst[slot]
        kx, kb, ks = "xt%d" % slot, "xb%d" % slot, "xst%d" % slot
        S.dma(SP, xt[0:ntok, :], xrows, writes=[kx])
        self.I(POOL, "memset", st[:, 0:1], 0.0, writes=[ks])
        self.I(ACT, "activation", xb[0:ntok, :], xt[0:ntok, :], AF.Square, accum_out=st[0:ntok, 0:1], reads=[kx], writes=[kb, ks])
        self.I(DVE, "tensor_scalar", st[0:ntok, 1:2], st[0:ntok, 0:1], 1.0 / D, 1e-6, ALU.mult, ALU.add, reads=[ks], writes=[ks])
        self.I(ACT, "activation", st[0:ntok, 2:3], st[0:ntok, 1:2], AF.Sqrt, reads=[ks], writes=[ks])
        self.I(DVE, "reciprocal", st[0:ntok, 3:4], st[0:ntok, 2:3], reads=[ks], writes=[ks])
        self.I(ACT, "activation", xb[0:ntok, :], xt[0:ntok, :], AF.Copy, scale=st[0:ntok, 3:4], reads=[kx, ks, kb], writes=[kb])
        for half in range(2):
            ps, pk = self.psum()
            pT = ps[:].bitcast(BF16).rearrange("p (a b) -> p a b", b=128)
            for k8 in range(8):
                kc = half * 8 + k8
                self.I(PE, "transpose", pT[:, k8, 0:ntok], xb[0:ntok, kc * 128:(kc + 1) * 128],
                                                                self.cb[0:ntok, 0:ntok], reads=[kb, "cb"], writes=[pk], track=(k8 == 7))
            gcol = self.pv[:, V_G1 + half * 8: V_G1 + half * 8 + 8].unsqueeze(2).to_broadcast([128, 8, ntok])
            self.I(DVE, "tensor_tensor", hT[:, half * 8:half * 8 + 8, :], pT[:, :, 0:ntok], gcol, ALU.mult, reads=[pk, "pv"], writes=[hkey])

    def slab_stream(self, slabs):
        S = self.S
        n = len(slabs)

        def issue(i):
            ap, ncol = slabs[i]
            slot = self.wslot % 2
            self.wslot += 1
            key = "wb%d" % slot
            S.dma(POOL, self.wb[slot][:, :, 0:ncol], ap.rearrange("(k p) n -> p k n", p=128), writes=[key])
            return (self.wb[slot], key)
        cur = issue(0)
        for i in range(n):
            nxt = issue(i + 1) if i + 1 < n else None
            yield cur
            cur = nxt

    def proj(self, ps, pk, wt, wkey, c0, M, hT, hkey, N):
        S = self.S
        for k in range(16):
            self.I(PE, "matmul", ps[0:M, 0:N], wt[:, k, c0:c0 + M], hT[:, k, 0:N], start=(k == 0), stop=(k == 15), reads=[wkey, hkey], writes=[pk], track=(k == 15))

    def stageS(self):
        S = self.S
        pv, cf, cb = self.pv, self.cf, self.cb
        wA = self.wA_ap
        N = NSM
        with ExitStack() as es0:
            sb0 = lambda n, s, d=F32: self.sb(es0, n, s, d)
            sbon = sb0("sbon", [128, 8, N], BF16)
            sgst = sb0("sgst", [128, 8, N], BF16)
            self._stageS_proj(es0, sbon, sgst)
            self._stageS_scan(es0, sbon, sgst)

    def _stageS_proj(self, es0, sbon, sgst):
        S = self.S
        pv, cf, cb = self.pv, self.cf, self.cb
        wA = self.wA_ap
        N = NSM
        with ExitStack() as es:
            sb = lambda n, s, d=F32: self.sb(es, n, s, d)
            self.xt = [sb("xtS", [128, D])]
            self.xb = [sb("xbS", [128, D], BF16)]
            self.xst = [sb("xstS", [128, 4])]
            hT = sb("hTs", [128, 16, N], BF16)
            self.make_hT(self.xsamp, N, hT[:, :, :], "hTs", 0)
            S.dma(SP, self.scrH[:, :, 1056:NOWN], hT[:, :, :], reads=["hTs"], writes=["scrH_s"])
            shT = sb("shT", [128, 28, 16])
            sst = sb("sst", [16, DS])
            S.dma(SP, sst[:], self.st_shift, writes=["sst"])
            chunks = [(0, 64), (64, 64), (128, 128), (256, 32)] + [(288 + 128 * i, 128) for i in range(24)]
            ps, pk = self.psum()
            for ci, (o, M) in enumerate(chunks):
                self.I(PE, "transpose", ps[0:M, ci * 16:(ci + 1) * 16], sst[:, o:o + M], cf[0:16, 0:16], reads=["sst", "cf"], writes=[pk], track=(ci == 27))
            self.I(DVE, "tensor_copy", shT[:, :, :], ps[:, 0:448].rearrange("p (a b) -> p a b", b=16), reads=[pk], writes=["shT"])
            zls = sb("zls", [128, 28, 16])
            tw = sb("stw", [64, N], BF16)
            zam = sb("szam", [64, N], BF16)
            sga = sb("ssga", [128, N], BF16)
            sgb = sb("ssgb", [128, N], BF16)
            self.I(POOL, "memset", sgb[:], 0.0, writes=["ssgb"])
            tokS = sb("tokS", [N, 6, 1024])
            phys = {n: sb("s_" + n, [128, N]) for n in ("zc", "d", "zr", "zk", "zv", "es", "wd", "as", "kk", "kkn", "km", "dd", "av")}
            phys["hi"] = sb("s_hi", [128, N], BF16)
            phys["lo"] = sb("s_lo", [128, N], BF16)
            T = _TAlias(phys, {"kk2": "d", "rn": "zc", "t1": "dd", "ta": "es", "rk": "wd2"}, prefix="s_")
            phys["wd2"] = sb("s_wd2", [128, N])

            def mix_s(ps, pk, M, ci, out, kout, mu):
                zc, d = T["zc"], T["d"]
                v3 = lambda t: t[0:M, 0:N].rearrange("p (b t) -> p b t", t=4)
                self.I(ACT, "copy", zc[0:M, 0:N], ps[0:M, 0:N], reads=[pk], writes=["s_zc"])
                self.I(DVE, "tensor_tensor", v3(d)[:, :, 1:4], v3(zc)[:, :, 0:3], v3(zc)[:, :, 1:4], ALU.subtract, reads=["s_zc"], writes=["s_d"])
                self.I(DVE, "tensor_tensor", v3(d)[:, :, 0], shT[0:M, ci, :], v3(zc)[:, :, 0], ALU.subtract, reads=["s_zc", "shT"], writes=["s_d"])
                self.I(DVE, "scalar_tensor_tensor", out[0:M, 0:N], d[0:M, 0:N], mu, zc[0:M, 0:N], ALU.mult, ALU.add, reads=["s_d", "s_zc", "pv"], writes=[kout])
                self.I(POOL, "tensor_copy", zls[0:M, ci, :], v3(zc)[:, :, 3], reads=["s_zc"], writes=["zls"])

            slabs = [(wA[:, 0:288], 288)] + [(wA[:, 288 + q * 384: 288 + (q + 1) * 384], 384) for q in range(8)]
            stream = self.slab_stream(slabs)
            wt, wk = next(stream)
            for (ci, (cc0, M, dst, dk, func)) in enumerate(((0, 64, tw, "stw", AF.Tanh), (64, 64, zam, "szam", AF.Copy),
                                                             (128, 128, sga, "ssga", AF.Sigmoid), (256, 32, sgb, "ssgb", AF.Sigmoid))):
                ps, pk = self.psum()
                self.proj(ps, pk, wt, wk, cc0, M, hT, "hTs", N)
                mix_s(ps, pk, M, ci, T["zr"], "s_zr", pv[0:M, ci:ci + 1])
                self.I(ACT, "activation", dst[0:M, 0:N], T["zr"][0:M, 0:N], func, reads=["s_zr"], writes=[dk])
            bonesb = cb[:, C_BONES:C_BONES + 128]
            for q in range(8):
                wt, wk = next(stream)
                col = lambda base: pv[:, base + q: base + q + 1]
                qs = slice(q * 128, (q + 1) * 128)
                for t, nm in enumerate(("zr", "zk", "zv")):
                    ps, pk = self.psum()
                    self.proj(ps, pk, wt, wk, t * 128, 128, hT, "hTs", N)
                    mix_s(ps, pk, 128, 4 + 3 * q + t, T[nm], "s_" + nm, pv[:, V_MU + 3 * q + t: V_MU + 3 * q + t + 1])
                zr, zk, zv = T["zr"], T["zk"], T["zv"]
                ps, pk = self.psum()
                self.I(PE, "matmul", ps[:, 0:N], self.w2b[:, qs], tw[:, 0:N], start=True, stop=True, reads=["w2b", "stw"], writes=[pk])
                self.I(ACT, "activation", T["es"][:, 0:N], ps[:, 0:N], AF.Sigmoid, bias=col(V_W0), reads=[pk, "pv"], writes=["s_es"])
                self.I(ACT, "activation", T["wd"][:, 0:N], T["es"][:, 0:N], AF.Exp, scale=-DECAY_C, reads=["s_es"], writes=["s_wd"])
                ps, pk = self.psum()
                self.I(PE, "matmul", ps[:, 0:N], self.a2b[:, qs], zam[:, 0:N], start=True, stop=True, reads=["a2b", "szam"], writes=[pk])
                self.I(ACT, "activation", T["as"][:, 0:N], ps[:, 0:N], AF.Sigmoid, bias=col(V_A0), reads=[pk, "pv"], writes=["s_as"])
                self.I(POOL, "tensor_scalar", T["kk"][:, 0:N], zk[:, 0:N], col(V_KK), None, ALU.mult, reads=["s_zk", "pv"], writes=["s_kk"])
                self.I(POOL, "tensor_tensor", T["kk2"][:, 0:N], T["kk"][:, 0:N], T["kk"][:, 0:N], ALU.mult, reads=["s_kk"], writes=["s_d"])
                ps, pk = self.psum()
                self.bsum(ps, pk, T, "kk2", N)
                self.I(ACT, "activation", T["rn"][:, 0:N], ps[:, 0:N], AF.Sqrt, reads=[pk], writes=["s_zc"])
                self.I(DVE, "tensor_scalar_max", T["rn"][:, 0:N], T["rn"][:, 0:N], 1e-12, reads=["s_zc"], writes=["s_zc"])
                self.I(DVE, "reciprocal", T["rn"][:, 0:N], T["rn"][:, 0:N], reads=["s_zc"], writes=["s_zc"])
                self.I(DVE, "tensor_tensor", T["kkn"][:, 0:N], T["kk"][:, 0:N], T["rn"][:, 0:N], ALU.mult, reads=["s_kk", "s_zc"], writes=["s_kkn"])
                self.I(POOL, "tensor_scalar", T["t1"][:, 0:N], T["as"][:, 0:N], -1.0, col(V_KA), ALU.add, ALU.mult, reads=["s_as", "pv"], writes=["s_dd"])
                self.I(POOL, "tensor_tensor", T["t1"][:, 0:N], T["t1"][:, 0:N], zk[:, 0:N], ALU.mult, reads=["s_dd", "s_zk"], writes=["s_dd"])
                self.I(POOL, "tensor_tensor", T["km"][:, 0:N], T["t1"][:, 0:N], zk[:, 0:N], ALU.add, reads=["s_dd", "s_zk"], writes=["s_km"])
                self.I(POOL, "tensor_tensor", T["ta"][:, 0:N], T["kkn"][:, 0:N], T["as"][:, 0:N], ALU.mult, reads=["s_kkn", "s_as"], writes=["s_es"])
                self.I(POOL, "tensor_scalar", T["av"][:, 0:N], T["kkn"][:, 0:N], -1.0, None, ALU.mult, reads=["s_kkn"], writes=["s_av"])
                self.I(DVE, "scalar_tensor_tensor", T["rk"][:, 0:N], zr[:, 0:N], col(V_RK), T["km"][:, 0:N], ALU.mult, ALU.mult, reads=["s_zr", "s_km", "pv"], writes=["s_wd2"])
                ps, pk = self.psum()
                self.bsum(ps, pk, T, "rk", N)
                self.I(DVE, "tensor_tensor", sbon[:, q, :], ps[:, 0:N], zv[:, 0:N], ALU.mult, reads=[pk, "s_zv"], writes=["sbon"])
                self.I(POOL, "tensor_scalar", sbon[:, q, :], sbon[:, q, :], col(V_LB), None, ALU.add, reads=["sbon", "pv"], writes=["sbon"])
                ps, pk = self.psum()
                self.I(PE, "matmul", ps[:, 0:N], self.g2a[:, qs], sga[:, 0:N], start=True, stop=False, reads=["g2a", "ssga"], writes=[pk], track=False)
                self.I(PE, "matmul", ps[:, 0:N], self.g2b[:, qs], sgb[:, 0:N], start=False, stop=True, reads=["g2b", "ssgb"], writes=[pk])
                self.I(ACT, "copy", sgst[:, q, :], ps[:, 0:N], reads=[pk], writes=["sgst"])
                srcs = [("zr", "s_zr"), ("km", "s_km"), ("zv", "s_zv"), ("wd", "s_wd"), ("av", "s_av"), ("ta", "s_es")]
                psA, pkA = self.psum()
                psB, pkB = self.psum()
                for qi, (nm, kk_) in enumerate(srcs):
                    pp, ppk = (psA, pkA) if qi < 4 else (psB, pkB)
                    o = (qi % 4) * 128
                    self.I(PE, "transpose", pp[0:N, o:o + 128], T[nm][:, 0:N], cf[:, C_ID:C_ID + 128], reads=[kk_, "cf"], writes=[ppk], track=(qi in (3, 5)))
                self.I(DVE, "tensor_copy", tokS[:, 0:4, qs], psA[0:N, :].rearrange("p (a b) -> p a b", b=128), reads=[pkA], writes=["tokS"])
                self.I(ACT, "copy", tokS[:, 4:6, qs], psB[0:N, 0:256].rearrange("p (a b) -> p a b", b=128), reads=[pkB], writes=["tokS"])
            zrow = sb("zrow", [16, DS])
            for c0_ in range(0, 28, 4):
                ps, pk = self.psum()
                grp = list(enumerate(chunks))[c0_:c0_ + 4]
                for n_, (ci, (o, M)) in enumerate(grp):
                    self.I(PE, "transpose", ps[0:16, n_ * 128:n_ * 128 + M], zls[0:M, ci, :], cf[0:M, 0:M], reads=["zls", "cf"], writes=[pk], track=(n_ == len(grp) - 1))
                for n_, (ci, (o, M)) in enumerate(grp):
                    self.I(DVE, "tensor_copy", zrow[:, o:o + M], ps[0:16, n_ * 128:n_ * 128 + M], reads=[pk], writes=["zrow"])
            S.dma(SP, self.o_sshift, zrow[:], reads=["zrow"], ring="spo")
            S.dma(SP, self.scrS, tokS[:], reads=["tokS"], writes=["scrS"])
            S.barrier()

    def _stageS_scan(self, es0, sbon, sgst):
        S = self.S
        pv, cf, cb = self.pv, self.cf, self.cb
        N = NSM
        with ExitStack() as es:
            sb = lambda n, s, d=F32: self.sb(es, n, s, d)
            ytok = sb("ytok", [N, 1024])
            for bt in range(2):
                E = DVE
                kS, kq, kt, ky = "Sst%d" % bt, "qin%d" % bt, "stmp%d" % bt, "ysb%d" % bt
                Sst = sb(kS, [128, 64, 64])
                tmp = sb(kt, [128, 64, 64])
                qin = sb(kq, [128, 4, 6, 64])
                ysb = sb(ky, [128, 4, 64])
                sa = sb("sa%d" % bt, [128, 64])
                yq = sb("syq%d" % bt, [128, 4, 64])
                sst_ = sb("sstat%d" % bt, [128, 8, 4])
                S.dma(SP, Sst[:].rearrange("p v k -> p (v k)"), self.st_wkv[bt * 128:(bt + 1) * 128, :], writes=[kS])
                for bl in range(8):
                    b = bt * 8 + bl
                    S.dma(SP, qin[16 * bl:16 * bl + 16, :, :, :], self.scrS[4 * b:4 * b + 4, :, :].rearrange("t q (h k) -> h t q k", k=64),
                          reads=["scrS"], writes=[kq + "_%d" % bl])
                bc_k = lambda t, qi: qin[:, t, qi, :].unsqueeze(1).to_broadcast([128, 64, 64])
                kqs = [kq + "_%d" % bl for bl in range(8)]
                for t in range(4):
                    self.I(E, "tensor_tensor", tmp[:], Sst[:], bc_k(t, 4), ALU.mult, reads=[kS] + kqs, writes=[kt])
                    self.I(DVE, "tensor_reduce", sa[:], tmp[:], AX.X, ALU.add, reads=[kt], writes=["sa%d" % bt])
                    self.I(E, "tensor_tensor", Sst[:], Sst[:], bc_k(t, 3), ALU.mult, reads=[kS] + kqs, writes=[kS])
                    self.I(E, "tensor_tensor", tmp[:], sa[:].unsqueeze(2).to_broadcast([128, 64, 64]), bc_k(t, 5), ALU.mult, reads=["sa%d" % bt] + kqs, writes=[kt])
                    self.I(E, "tensor_tensor", Sst[:], Sst[:], tmp[:], ALU.add, reads=[kS, kt], writes=[kS])
                    self.I(E, "tensor_tensor", tmp[:], qin[:, t, 2, :].unsqueeze(2).to_broadcast([128, 64, 64]), bc_k(t, 1), ALU.mult, reads=kqs, writes=[kt])
                    self.I(E, "tensor_tensor", Sst[:], Sst[:], tmp[:], ALU.add, reads=[kS, kt], writes=[kS])
                    self.I(E, "tensor_tensor", tmp[:], Sst[:], bc_k(t, 0), ALU.mult, reads=[kS] + kqs, writes=[kt])
                    self.I(DVE, "tensor_reduce", ysb[:, t, :], tmp[:], AX.X, ALU.add, reads=[kt], writes=[ky])
                S.dma(SP, self.o_swkv[bt * 128:(bt + 1) * 128, :], Sst[:].rearrange("p v k -> p (v k)"), reads=[kS], ring="spo")
                st_ = sst_
                ks_ = "sstat%d" % bt
                self.I(DVE, "tensor_reduce", st_[:, 0, :], ysb[:], AX.X, ALU.add, reads=[ky], writes=[ks_])
                self.I(E, "tensor_tensor", yq[:], ysb[:], ysb[:], ALU.mult, reads=[ky], writes=["syq%d" % bt])
                self.I(DVE, "tensor_reduce", st_[:, 1, :], yq[:], AX.X, ALU.add, reads=["syq%d" % bt], writes=[ks_])
                self.I(E, "tensor_scalar", st_[:, 2, :], st_[:, 0, :], 1.0 / 64, None, ALU.mult, reads=[ks_], writes=[ks_])
                self.I(E, "tensor_tensor", st_[:, 3, :], st_[:, 2, :], st_[:, 2, :], ALU.mult, reads=[ks_], writes=[ks_])
                self.I(E, "tensor_scalar", st_[:, 4, :], st_[:, 1, :], 1.0 / 64, 64e-5, ALU.mult, ALU.add, reads=[ks_], writes=[ks_])
                self.I(E, "tensor_tensor", st_[:, 4, :], st_[:, 4, :], st_[:, 3, :], ALU.subtract, reads=[ks_], writes=[ks_])
                self.I(ACT, "activation", st_[:, 5, :], st_[:, 4, :], AF.Sqrt, reads=[ks_], writes=[ks_])
                self.I(DVE, "reciprocal", st_[:, 6, :], st_[:, 5, :], reads=[ks_], writes=[ks_])
                self.I(E, "tensor_tensor", yq[:], ysb[:], st_[:, 2, :].unsqueeze(2).to_broadcast([128, 4, 64]), ALU.subtract, reads=[ky, ks_], writes=["syq%d" % bt])
                self.I(E, "tensor_tensor", yq[:], yq[:], st_[:, 6, :].unsqueeze(2).to_broadcast([128, 4, 64]), ALU.mult, reads=["syq%d" % bt, ks_], writes=["syq%d" % bt])
                for bl in range(8):
                    b = bt * 8 + bl
                    S.dma(SP, self.scrY[4 * b:4 * b + 4, :].rearrange("t (h v) -> h t v", v=64), yq[16 * bl:16 * bl + 16, :, :], reads=["syq%d" % bt], writes=["scrY%d" % b])
            S.dma(SP, ytok[:], self.scrY, reads=["scrY%d" % b for b in range(16)], writes=["ytok"])
            if "sdbg" in self.dbg:
                S.dma(SP, self.dout("d_ytok", [N, 1024]), ytok[:], reads=["ytok"], ring="spo")
                dbf = sb("dbf", [128, 2, 8, N])
                self.I(DVE, "tensor_copy", dbf[:, 0], sbon[:], reads=["sbon"], writes=["dbf"])
                self.I(DVE, "tensor_copy", dbf[:, 1], sgst[:], reads=["sgst"], writes=["dbf"])
                S.dma(SP, self.dout("d_sbg", [128, 2, 8, N]), dbf[:], reads=["dbf"], ring="spo")
            ytb = sb("ytb", [N, 1024], BF16)
            self.I(ACT, "copy", ytb[:], ytok[:], reads=["ytok"], writes=["ytb"])
            ps, pk = self.psum()
            pT = ps[:].bitcast(BF16).rearrange("p (a b) -> p a b", b=128)[:, :, 0:N]
            for q in range(8):
                self.I(PE, "transpose", pT[:, q, :], ytb[:, q * 128:(q + 1) * 128], cb[0:N, 0:N], reads=["ytb", "cb"], writes=[pk], track=(q == 7))
            yt = sb("syt", [128, 8, N])
            lw = pv[:, V_LW: V_LW + 8].unsqueeze(2).to_broadcast([128, 8, N])
            self.I(DVE, "tensor_tensor", yt[:], pT, lw, ALU.mult, reads=[pk, "pv"], writes=["syt"])
            self.I(POOL, "tensor_tensor", yt[:], yt[:], sbon[:], ALU.add, reads=["syt", "sbon"], writes=["syt"])
            self.I(DVE, "tensor_tensor", self.yaT[:, :, 1056:NOWN], yt[:], sgst[:], ALU.mult, reads=["syt", "sgst"], writes=["yaT"])
            S.barrier()

    def stageA(self, es0, xseq, wA):
        S = self.S
        pv, cf, cb = self.pv, self.cf, self.cb
        with ExitStack() as es:
            sb = lambda n, s, d=F32: self.sb(es, n, s, d)
            self.xt = [sb("xt%d" % i, [128, D]) for i in range(1)]
            self.xb = [sb("xb%d" % i, [128, D], BF16) for i in range(1)]
            self.xst = [sb("xst%d" % i, [128, 4]) for i in range(1)]
            hT = sb("hTb", [128, 16, BLK], BF16)
            zl = sb("zl", [128, 28])
            self.I(POOL, "memset", zl[:], 0.0, writes=["zl"])
            tw = sb("tw", [64, BLK], BF16)
            zam = sb("zam", [64, BLK], BF16)
            sga = sb("sga", [128, BLK], BF16)
            sgb = sb("sgb", [128, BLK], BF16)
            self.I(POOL, "memset", sgb[:], 0.0, writes=["sgb"])
            NCK = BLK // CH
            ARBD = sb("ARBD", [128, 4, NCK, 2, 128], BF16)
            BBD = sb("BBD", [128, 4, NCK, 128], BF16)
            KBD = sb("KBD", [128, 4, NCK, 128], BF16)
            VBD = sb("VBD", [128, 4, NCK, 128], BF16)
            for (t, k) in ((ARBD, "ARBD"), (BBD, "BBD"), (KBD, "KBD"), (VBD, "VBD")):
                self.I(POOL, "memset", t[:], 0.0, writes=[k + "%d" % q for q in range(4)])
            PCs = sb("PCs", [128, 4, NCK])
            bon = sb("bon", [128, 4, BLK], BF16)
            gst = sb("gst", [128, 4, BLK], BF16)
            Hf = [sb("Hf%d" % g, [128, 4, 64]) for g in range(2)]
            Hb = [sb("Hb%d" % g, [128, 4, 64], BF16) for g in range(2)]
            for g in range(2):
                self.I(POOL, "memset", Hf[g][:], 0.0, writes=["Hf%d" % g])
                self.I(POOL, "memset", Hb[g][:], 0.0, writes=["Hb%d" % g])
            names = ("zc", "d", "zr", "zk", "zv", "es", "cum", "dd", "pinv", "prr", "pa", "as", "kk", "kkn", "km")
            phys = {n: sb("t_" + n, [128, NH]) for n in names[:8]}
            spare = self.bufX[:, 8:16, :].rearrange("p a b -> p (a b)")
            sparef = spare.bitcast(F32)
            for i, n in enumerate(names[8:]):
                phys[n] = sparef[:, i * NH:(i + 1) * NH]
            phys["hi"] = spare[:, 7 * 2 * NH: 7 * 2 * NH + NH]
            phys["lo"] = spare[:, 7 * 2 * NH + NH: 7 * 2 * NH + 2 * NH]
            T = _TAlias(phys, {"kk2": "d", "rn": "zc", "t1": "dd", "ta": "es", "rk": "cum"})
            sc = {}
            for s in range(2):
                sc["NA1", s] = sb("NA1_%d" % s, [128, 4, 256], BF16)
                sc["NA2", s] = sb("NA2_%d" % s, [128, 4, 256], BF16)
                for i in range(2):
                    sc["Q", s, i] = sb("Q_%d%d" % (s, i), [128, 4, 128], BF16)
                    if s == 0:
                        sc["NL", s, i] = sb("NL_%d%d" % (s, i), [128, 4, 128], BF16)
                        sc["TU", s, i] = sb("TU_%d%d" % (s, i), [128, 4, 128], BF16)
                sc["B2", s] = sb("B2_%d" % s, [128, 4, 128], BF16)
                sc["K2", s] = sb("K2_%d" % s, [128, 4, 128], BF16)
                sc["V2", s] = sb("V2_%d" % s, [128, 4, 64], BF16)
                if s == 0:
                    sc["X2", s] = sb("X2_%d" % s, [128, 4, 64], BF16)
                    sc["U2", s] = sb("U2_%d" % s, [128, 4, 64], BF16)
                    sc["YBD", s] = sb("YBD_%d" % s, [128, 4, 128], BF16)
                    self.I(POOL, "memset", sc["YBD", s][:], 0.0, writes=["YBD_%d" % s])
                    sc["ys", s] = sb("ys_%d" % s, [128, 4, 64])
                    sc["yq", s] = sb("yq_%d" % s, [128, 4, 64])
                    sc["st", s] = sb("yst_%d" % s, [128, 8, 4])
                    sc["yt", s] = sb("yt_%d" % s, [128, 4, 64])
                else:
                    for nm in ("X2", "U2", "YBD", "ys", "yq", "st", "yt"):
                        sc[nm, s] = sc[nm, 0]
            tmpH = sb("tmpH", [128, 4, 64])

            nblk = NPR // BLK
            lr_slab = (wA[:, 0:288], 288)
            pair_slabs = [(wA[:, 288 + q * 384: 288 + (q + 1) * 384], 384) for q in range(8)]
            slabs = []
            for blk in range(nblk):
                slabs.append(lr_slab)
                slabs += pair_slabs
            stream = self.slab_stream(slabs)

            for blk in range(nblk):
                c0 = blk * BLK
                N = BLK
                for t4 in range(4):
                    self.make_hT(xseq[c0 + t4 * 128: c0 + (t4 + 1) * 128, :], 128, hT[:, :, t4 * 128:(t4 + 1) * 128], "hTb", 0)
                self.ckpt("hT")
                if blk == 1:
                    S.dma(SP, self.scrH[:, :, 0:32], hT[:, :, BLK - 32:BLK], reads=["hTb"], writes=["scrH_1"])
                elif blk >= 2:
                    S.dma(SP, self.scrH[:, :, 32 + (blk - 2) * BLK: 32 + (blk - 1) * BLK], hT[:, :, :], reads=["hTb"], writes=["scrH_%d" % blk])
                wt, wk = next(stream)
                for (ci, (cc0, M, dst, dk, func)) in enumerate(((0, 64, tw, "tw", AF.Tanh), (64, 64, zam, "zam", AF.Copy),
                                                                 (128, 128, sga, "sga", AF.Sigmoid), (256, 32, sgb, "sgb", AF.Sigmoid))):
                    ps, pk = self.psum()
                    self.proj(ps, pk, wt, wk, cc0, M, hT, "hTb", N)
                    for co in range(0, N, NH):
                        self.mix(ps[:, co:co + NH], pk, M, NH, ci, T, zl)
                        self.I(ACT, "activation", dst[0:M, co:co + NH], T["zr"][0:M, 0:NH], func, reads=["t_zr"], writes=[dk])
                self.ckpt("lowrank")
                for g in range(2):
                    for qg in range(4):
                        q = g * 4 + qg
                        wt, wk = next(stream)
                        self.prep_pair(wt, wk, hT, N, q, qg, T, zl, tw, zam, sga, sgb, ARBD, BBD, KBD, VBD, PCs, bon, gst, blk)
                        self.ckpt("prep0")
                    self.ckpt("prep")
                    self.scan_group(g, blk, NCK, sc, ARBD, BBD, KBD, VBD, PCs, bon, gst, Hf[g], Hb[g], tmpH)
            scr = self.stage_out("zl", zl[:], [128, 28], "zl")
            self.bg(self.o_pshift[0:64].rearrange("(p o) -> p o", o=1), scr[0:64, 0:1], "zl")
            self.bg(self.o_pshift[64:128].rearrange("(p o) -> p o", o=1), scr[0:64, 1:2], "zl")
            self.bg(self.o_pshift[128:256].rearrange("(p o) -> p o", o=1), scr[:, 2:3], "zl")
            self.bg(self.o_pshift[256:288].rearrange("(p o) -> p o", o=1), scr[0:32, 3:4], "zl")
            self.bg(self.o_pshift[288:DS].rearrange("(j p) -> p j", p=128), scr[:, 4:28], "zl")
            for g in range(2):
                hh = sb("hh%d" % g, [128, 4, 64], BF16)
                hl = sb("hl%d" % g, [128, 4, 64], BF16)
                self.I(POOL, "tensor_copy", hh[:], Hf[g][:], reads=["Hf%d" % g], writes=["hh%d" % g])
                self.I(POOL, "tensor_tensor", hl[:], Hf[g][:], hh[:], ALU.subtract, reads=["Hf%d" % g, "hh%d" % g], writes=["hl%d" % g])
                ps, pk = self.psum()
                for qg in range(4):
                    self.I(PE, "matmul", ps[0:64, qg * 128:(qg + 1) * 128], hh[:, qg, :], cb[:, 0:128], start=True, stop=False, reads=["hh%d" % g, "cb"], writes=[pk], track=False)
                    self.I(PE, "matmul", ps[0:64, qg * 128:(qg + 1) * 128], hl[:, qg, :], cb[:, 0:128], start=False, stop=True, reads=["hl%d" % g, "cb"], writes=[pk], track=(qg == 3))
                so = sb("so%d" % g, [64, 4, 2, 64])
                self.I(DVE, "tensor_copy", so[:], ps[0:64, :].rearrange("p (a b c) -> p a b c", a=4, b=2), reads=[pk], writes=["so%d" % g])
                S.dma(SP, self.o_pwkv[g * 8:(g + 1) * 8].rearrange("(q s) v k -> v q s k", s=2), so[:], reads=["so%d" % g], ring="spo")
            S.barrier()

    def mix(self, ps, pk, M, N, ci, T, zl):
        S = self.S
        zc, d, out = T["zc"], T["d"], T["zr"]
        self.mix_to(ps, pk, M, N, ci, zl, zc, "t_zc", d, "t_d", out, "t_zr", self.pv[0:M, ci:ci + 1] if ci < 4 else None)

    def mix_to(self, ps, pk, M, N, ci, zl, zc, kzc, d, kd, out, kout, mu):
        S = self.S
        self.I(ACT, "copy", zc[0:M, 0:N], ps[0:M, 0:N], reads=[pk], writes=[kzc])
        self.I(DVE, "tensor_tensor", d[0:M, 1:N], zc[0:M, 0:N - 1], zc[0:M, 1:N], ALU.subtract, reads=[kzc], writes=[kd])
        self.I(DVE, "tensor_tensor", d[0:M, 0:1], zl[0:M, ci:ci + 1], zc[0:M, 0:1], ALU.subtract, reads=[kzc, "zl"], writes=[kd])
        self.I(DVE, "scalar_tensor_tensor", out[0:M, 0:N], d[0:M, 0:N], mu, zc[0:M, 0:N], ALU.mult, ALU.add, reads=[kd, kzc, "pv"], writes=[kout])
        self.I(POOL, "tensor_copy", zl[0:M, ci:ci + 1], zc[0:M, N - 1:N], reads=[kzc], writes=["zl"])

    def bsum(self, ps, pk, T, name, N):
        S = self.S
        src, ksrc = T[name], T.key(name)
        bonesb = self.cb[:, C_BONES:C_BONES + 128]
        khi, klo = T.key("hi"), T.key("lo")
        self.I(ACT, "copy", T["hi"][:, 0:N], src[:, 0:N], reads=[ksrc], writes=[khi])
        self.I(POOL, "tensor_tensor", T["lo"][:, 0:N], src[:, 0:N], T["hi"][:, 0:N], ALU.subtract, reads=[ksrc, khi], writes=[klo])
        self.I(PE, "matmul", ps[:, 0:N], bonesb, T["hi"][:, 0:N], start=True, stop=False, reads=["cb", khi], writes=[pk], track=False)
        self.I(PE, "matmul", ps[:, 0:N], bonesb, T["lo"][:, 0:N], start=False, stop=True, reads=["cb", klo], writes=[pk])

    def prep_pair(self, wt, wk, hT, N, q, qg, T, zl, tw, zam, sga, sgb, ARBD, BBD, KBD, VBD, PCs, bon, gst, blk):
        S = self.S
        pv, cf = self.pv, self.cf
        col = lambda base: pv[:, base + q: base + q + 1]
        qs = slice(q * 128, (q + 1) * 128)
        need_y = blk >= 1
        W = N
        co = 0
        cw = slice(0, W)
        psw, pkw = self.psum()
        self.I(PE, "matmul", psw[:, 0:W], self.w2b[:, qs], tw[:, cw], start=True, stop=True, reads=["w2b", "tw"], writes=[pkw])
        psa, pka = self.psum()
        self.I(PE, "matmul", psa[:, 0:W], self.a2b[:, qs], zam[:, cw], start=True, stop=True, reads=["a2b", "zam"], writes=[pka])
        pj = []
        for t in range(3):
            if t == 0 and not need_y:
                pj.append(None)
                continue
            ps, pk = self.psum()
            self.proj(ps, pk, wt, wk, t * 128, 128, hT, "hTb", N)
            pj.append((ps, pk))
        self.I(ACT, "activation", T["es"][:, 0:W], psw[:, 0:W], AF.Sigmoid, bias=col(V_W0), reads=[pkw, "pv"], writes=["t_es"])
        self.I(ACT, "activation", T["as"][:, 0:W], psa[:, 0:W], AF.Sigmoid, bias=col(V_A0), reads=[pka, "pv"], writes=["t_as"])
        self.I(DVE, "tensor_tensor_scan", T["cum"][:, 0:W], cf[:, C_M01:C_M01 + W], T["es"][:, 0:W], 0.0, ALU.mult, ALU.add, reads=["t_es", "cf"], writes=["t_cum"])
        self.I(POOL, "tensor_tensor", T["dd"][:, 0:W], T["cum"][:, 0:W], T["es"][:, 0:W], ALU.subtract, reads=["t_cum", "t_es"], writes=["t_dd"])
        nck = W // CH
        ck0 = 0

        def mixq(t, nm):
            ps, pk = pj[t]
            ci = 4 + 3 * q + t
            self.mix_to(ps[:, cw], pk, 128, W, ci, zl, T["zc"], "t_zc", T["d"], "t_d", T[nm], "t_" + nm, pv[:, V_MU + 3 * q + t: V_MU + 3 * q + t + 1])
        if need_y:
            mixq(0, "zr")
        self.I(ACT, "activation", T["pinv"][:, 0:W], T["cum"][:, 0:W], AF.Exp, scale=DECAY_C, reads=["t_cum"], writes=["t_pinv"])
        self.I(ACT, "activation", T["prr"][:, 0:W], T["cum"][:, 0:W], AF.Exp, scale=-DECAY_C, reads=["t_cum"], writes=["t_prr"])
        self.I(ACT, "activation", T["pa"][:, 0:W], T["dd"][:, 0:W], AF.Exp, scale=-DECAY_C, reads=["t_dd"], writes=["t_pa"])
        self.I(POOL, "tensor_copy", PCs[:, qg, ck0:ck0 + nck], T["prr"][:, 0:W].rearrange("p (c t) -> p c t", t=CH)[:, :, CH - 1], reads=["t_prr"], writes=["PCs%d" % qg])
        mixq(1, "zk")
        mixq(2, "zv")
        if True:
            zr, zk, zv = T["zr"], T["zk"], T["zv"]
            self.I(ACT, "activation", T["kk"][:, 0:W], zk[:, 0:W], AF.Copy, scale=col(V_KK), reads=["t_zk", "pv"], writes=["t_kk"])
            bonesb = self.cb[:, C_BONES:C_BONES + 128]
            self.I(ACT, "activation", T["hi"][:, 0:W], T["kk"][:, 0:W], AF.Square, reads=["t_kk"], writes=["t_hi"])
            ps, pk = self.psum()
            self.I(PE, "matmul", ps[:, 0:W], bonesb, T["hi"][:, 0:W], start=True, stop=True, reads=["cb", "t_hi"], writes=[pk])
            self.I(ACT, "activation", T["rn"][:, 0:W], ps[:, 0:W], AF.Sqrt, reads=[pk], writes=["t_zc"])
            self.I(DVE, "tensor_scalar_max", T["rn"][:, 0:W], T["rn"][:, 0:W], 1e-12, reads=["t_zc"], writes=["t_zc"])
            self.I(DVE, "reciprocal", T["rn"][:, 0:W], T["rn"][:, 0:W], reads=["t_zc"], writes=["t_zc"])
            self.I(DVE, "tensor_tensor", T["kkn"][:, 0:W], T["kk"][:, 0:W], T["rn"][:, 0:W], ALU.mult, reads=["t_kk", "t_zc"], writes=["t_kkn"])
            self.I(POOL, "tensor_scalar", T["t1"][:, 0:W], T["as"][:, 0:W], -1.0, col(V_KA), ALU.add, ALU.mult, reads=["t_as", "pv"], writes=["t_dd"])
            self.I(POOL, "tensor_tensor", T["t1"][:, 0:W], T["t1"][:, 0:W], zk[:, 0:W], ALU.mult, reads=["t_dd", "t_zk"], writes=["t_dd"])
            self.I(POOL, "tensor_tensor", T["km"][:, 0:W], T["t1"][:, 0:W], zk[:, 0:W], ALU.add, reads=["t_dd", "t_zk"], writes=["t_km"])
            self.I(POOL, "tensor_tensor", T["ta"][:, 0:W], T["kkn"][:, 0:W], T["as"][:, 0:W], ALU.mult, reads=["t_kkn", "t_as"], writes=["t_es"])
            if need_y:
                self.I(DVE, "scalar_tensor_tensor", T["lo"][:, 0:W], zr[:, 0:W], col(V_RK), T["km"][:, 0:W], ALU.mult, ALU.mult, reads=["t_zr", "t_km", "pv"], writes=["t_lo"])
                ps, pk = self.psum()
                self.I(PE, "matmul", ps[:, 0:W], bonesb, T["lo"][:, 0:W], start=True, stop=True, reads=["cb", "t_lo"], writes=[pk])
                self.I(DVE, "tensor_tensor", bon[:, qg, cw], ps[:, 0:W], zv[:, 0:W], ALU.mult, reads=[pk, "t_zv"], writes=["bon%d" % qg])
                self.I(ACT, "activation", bon[:, qg, cw], bon[:, qg, cw], AF.Identity, bias=col(V_LB), reads=["bon%d" % qg, "pv"], writes=["bon%d" % qg])
                ps, pk = self.psum()
                self.I(PE, "matmul", ps[:, 0:W], self.g2a[:, qs], sga[:, cw], start=True, stop=False, reads=["g2a", "sga"], writes=[pk], track=False)
                self.I(PE, "matmul", ps[:, 0:W], self.g2b[:, qs], sgb[:, cw], start=False, stop=True, reads=["g2b", "sgb"], writes=[pk])
                self.I(ACT, "copy", gst[:, qg, cw], ps[:, 0:W], reads=[pk], writes=["gst%d" % qg])
            for h in range(2):
                pr = slice(64 * h, 64 * h + 64)
                cs = slice(64 * h, 64 * h + 64)
                v3 = lambda t: t[pr, 0:W].rearrange("p (c t) -> p c t", t=CH)
                e1 = DVE if h == 0 else POOL
                cks = slice(ck0, ck0 + nck)
                self.I(DVE, "scalar_tensor_tensor", ARBD[pr, qg, cks, 0, cs], v3(T["kkn"]), -1.0, v3(T["pa"]), ALU.mult, ALU.mult, reads=["t_kkn", "t_pa"], writes=["ARBD%d" % qg])
                if need_y:
                    self.I(e1, "tensor_tensor", ARBD[pr, qg, cks, 1, cs], v3(zr), v3(T["prr"]), ALU.mult, reads=["t_zr", "t_prr"], writes=["ARBD%d" % qg])
                self.I(e1, "tensor_tensor", KBD[pr, qg, cks, cs], v3(T["km"]), v3(T["pinv"]), ALU.mult, reads=["t_km", "t_pinv"], writes=["KBD%d" % qg])
                self.I(e1, "tensor_tensor", BBD[pr, qg, cks, cs], v3(T["ta"]), v3(T["pinv"]), ALU.mult, reads=["t_es", "t_pinv"], writes=["BBD%d" % qg])
                self.I(ACT, "copy", VBD[pr, qg, cks, cs], v3(zv), reads=["t_zv"], writes=["VBD%d" % qg])

    def scan_pre(self, g, c, gc, sc, ARBD, BBD, KBD, VBD):
        cf, cb = self.cf, self.cb
        s = gc % 2
        kin = ["ARBD%d" % q for q in range(4)] + ["BBD%d" % q for q in range(4)] + ["KBD%d" % q for q in range(4)] + ["VBD%d" % q for q in range(4)]
        NA1, NA2 = sc["NA1", s], sc["NA2", s]
        kNA1, kNA2 = "NA1_%d" % s, "NA2_%d" % s
        identb = cb[:, 0:128]
        iselb = cb[:, C_ISEL:C_ISEL + 64]
        mask2 = cf[:, C_MUS:C_MUS + 256].unsqueeze(1).to_broadcast([128, 2, 256])
        for (dst, kd, L) in ((NA1, kNA1, BBD), (NA2, kNA2, KBD)):
            for hf in range(2):
                ps, pk = self.psum()
                for j in range(2):
                    q = 2 * hf + j
                    self.I(PE, "matmul", ps[:, j * 256:(j + 1) * 256], L[:, q, c, :], ARBD[:, q, c, :, :].rearrange("p a b -> p (a b)"),
                           start=True, stop=True, reads=kin, writes=[pk], track=(j == 1))
                self.I(DVE, "tensor_tensor", dst[:, 2 * hf:2 * hf + 2, :], ps[:].rearrange("p (a b) -> p a b", b=256), mask2, ALU.mult, reads=[pk, "cf"], writes=[kd])
                yield
        NL0, kNL0 = sc["NL", 0, 0], "NL_00"
        ps, pk = self.psum()
        for q in range(4):
            self.I(PE, "matmul", ps[:, q * 128:(q + 1) * 128], ARBD[:, q, c, 0, :], BBD[:, q, c, :], start=True, stop=True, reads=kin, writes=[pk], track=(q == 3))
        mL = cf[:, C_MLS:C_MLS + 128].unsqueeze(1).to_broadcast([128, 4, 128])
        self.I(DVE, "tensor_tensor", NL0[:], ps[:].rearrange("p (a b) -> p a b", b=128), mL, ALU.mult, reads=[pk, "cf"], writes=[kNL0])
        yield
        B2, K2, V2 = sc["B2", s], sc["K2", s], sc["V2", s]
        for (dst, kd, L) in ((B2, "B2_%d" % s, BBD), (K2, "K2_%d" % s, KBD)):
            ps, pk = self.psum()
            for q in range(4):
                self.I(PE, "matmul", ps[:, q * 128:(q + 1) * 128], L[:, q, c, :], identb, start=True, stop=True, reads=kin + ["cb"], writes=[pk], track=(q == 3))
            self.I(ACT, "copy", dst[:], ps[:].rearrange("p (a b) -> p a b", b=128), reads=[pk], writes=[kd])
            yield
        ps, pk = self.psum()
        for q in range(4):
            self.I(PE, "matmul", ps[:, q * 64:(q + 1) * 64], VBD[:, q, c, :], iselb, start=True, stop=True, reads=kin + ["cb"], writes=[pk], track=(q == 3))
        self.I(ACT, "copy", V2[:], ps[:, 0:256].rearrange("p (a b) -> p a b", b=64), reads=[pk], writes=["V2_%d" % s])
        yield
        TUc, kTU = NA1[:, :, 0:128], [kNA1]
        NLc, kNL = NL0, kNL0
        Qc, kQ = sc["Q", s, 0], "Q_%d0" % s
        idb4 = identb.unsqueeze(1).to_broadcast([128, 4, 128])
        self.I(POOL, "tensor_tensor", Qc[:], NA1[:, :, 0:128], idb4, ALU.add, reads=[kNA1, "cb"], writes=[kQ])
        for lvl in range(1, 6):
            i = lvl % 2
            NLn, kNLn = sc["NL", 0, i], "NL_0%d" % i
            TUn = sc["TU", 0, i]
            kTUn = ["TU_0%da" % i, "TU_0%db" % i]
            Qn, kQn = sc["Q", s, i], "Q_%d%d" % (s, i)
            psN, pkN = self.psum()
            for q in range(4):
                self.I(PE, "matmul", psN[:, q * 128:(q + 1) * 128], TUc[:, q, :], NLc[:, q, :], start=True, stop=True, reads=kTU + [kNL], writes=[pkN], track=(q == 3))
            if lvl < 5:
                psTa, pkTa = self.psum()
                psTb, pkTb = self.psum()
                for q in range(4):
                    pT_, pkT_ = (psTa, pkTa) if q < 2 else (psTb, pkTb)
                    self.I(PE, "matmul", pT_[:, (q % 2) * 128:(q % 2 + 1) * 128], NLc[:, q, :], TUc[:, q, :], start=True, stop=True, reads=kTU + [kNL], writes=[pkT_], track=(q % 2 == 1))
            self.I(ACT, "copy", NLn[:], psN[:].rearrange("p (a b) -> p a b", b=128), reads=[pkN], writes=[kNLn])
            if lvl < 5:
                self.I(ACT, "copy", TUn[:, 0:2, :], psTa[:, 0:256].rearrange("p (a b) -> p a b", b=128), reads=[pkTa], writes=[kTUn[0]])
                self.I(DVE, "tensor_copy", TUn[:, 2:4, :], psTb[:, 0:256].rearrange("p (a b) -> p a b", b=128), reads=[pkTb], writes=[kTUn[1]])
            yield
            psQ, pkQ = self.psum()
            for q in range(4):
                self.I(PE, "matmul", psQ[:, q * 128:(q + 1) * 128], NLn[:, q, :], Qc[:, q, :], start=True, stop=True, reads=[kNLn, kQ], writes=[pkQ], track=(q == 3))
            self.I(DVE, "tensor_tensor", Qn[:], psQ[:].rearrange("p (a b) -> p a b", b=128), Qc[:], ALU.add, reads=[pkQ, kQ], writes=[kQn])
            yield
            TUc, kTU, NLc, kNL, Qc, kQ = TUn, kTUn, NLn, kNLn, Qn, kQn

    def scan_chain(self, g, c, gc, sc, ARBD, PCs, bon, gst, Hf, Hb, tmpH):
        s = gc % 2
        kin = ["ARBD%d" % q for q in range(4)]
        NA1, NA2 = sc["NA1", s], sc["NA2", s]
        kNA1, kNA2 = "NA1_%d" % s, "NA2_%d" % s
        B2, K2, V2 = sc["B2", s], sc["K2", s], sc["V2", s]
        kB2, kK2, kV2 = "B2_%d" % s, "K2_%d" % s, "V2_%d" % s
        Qc, kQ = sc["Q", s, 1], "Q_%d1" % s
        kH = "Hf%d" % g
        kHb = "Hb%d" % g
        X2, U2 = sc["X2", 0], sc["U2", 0]
        ps, pk = self.psum()
        for q in range(4):
            self.I(PE, "matmul", ps[:, q * 64:(q + 1) * 64], ARBD[:, q, c, 0, :], Hb[:, q, :], start=True, stop=False, reads=kin + [kHb], writes=[pk], track=False)
            self.I(PE, "matmul", ps[:, q * 64:(q + 1) * 64], NA2[:, q, 0:128], V2[:, q, :], start=False, stop=True, reads=[kNA2, kV2], writes=[pk], track=(q == 3))
        self.I(ACT, "copy", X2[:], ps[:, 0:256].rearrange("p (a b) -> p a b", b=64), reads=[pk], writes=["X2_0"])
        yield
        ps, pk = self.psum()
        for q in range(4):
            self.I(PE, "matmul", ps[:, q * 64:(q + 1) * 64], Qc[:, q, :], X2[:, q, :], start=True, stop=True, reads=[kQ, "X2_0"], writes=[pk], track=(q == 3))
        self.I(ACT, "copy", U2[:], ps[:, 0:256].rearrange("p (a b) -> p a b", b=64), reads=[pk], writes=["U2_0"])
        yield
        need_y = gc >= OWN0 // CH
        if need_y:
            psY, pkY = self.psum()
            for q in range(4):
                self.I(PE, "matmul", psY[:, q * 64:(q + 1) * 64], ARBD[:, q, c, 1, :], Hb[:, q, :], start=True, stop=False, reads=kin + [kHb], writes=[pkY], track=False)
                self.I(PE, "matmul", psY[:, q * 64:(q + 1) * 64], NA1[:, q, 128:256], U2[:, q, :], start=False, stop=False, reads=[kNA1, "U2_0"], writes=[pkY], track=False)
                self.I(PE, "matmul", psY[:, q * 64:(q + 1) * 64], NA2[:, q, 128:256], V2[:, q, :], start=False, stop=True, reads=[kNA2, kV2], writes=[pkY], track=(q == 3))
        ps, pk = self.psum()
        for q in range(4):
            self.I(PE, "matmul", ps[:, q * 64:(q + 1) * 64], B2[:, q, :], U2[:, q, :], start=True, stop=False, reads=[kB2, "U2_0"], writes=[pk], track=False)
            self.I(PE, "matmul", ps[:, q * 64:(q + 1) * 64], K2[:, q, :], V2[:, q, :], start=False, stop=True, reads=[kK2, kV2], writes=[pk], track=(q == 3))
        self.I(DVE, "tensor_tensor", tmpH[:], Hf[:], ps[:, 0:256].rearrange("p (a b) -> p a b", b=64), ALU.add, reads=[pk, kH], writes=["tmpH"])
        pcb = PCs[:, :, c:c + 1].to_broadcast([128, 4, 64])
        self.I(DVE, "tensor_tensor", Hf[:], tmpH[:], pcb, ALU.mult, reads=["tmpH"] + ["PCs%d" % q for q in range(4)], writes=[kH])
        self.I(ACT, "copy", Hb[:], Hf[:], reads=[kH], writes=[kHb])
        yield
        if need_y:
            yield from self.y_post(g, c, gc, s, sc, psY, pkY, bon, gst)

    def scan_group(self, g, blk, NCK, sc, ARBD, BBD, KBD, VBD, PCs, bon, gst, Hf, Hb, tmpH):
        def drain(gen):
            for _ in gen:
                pass

        def interleave(ga, gb):
            a_live = b_live = True
            while a_live or b_live:
                if a_live:
                    try:
                        next(ga)
                    except StopIteration:
                        a_live = False
                if b_live:
                    try:
                        next(gb)
                    except StopIteration:
                        b_live = False
        gc0 = blk * NCK
        drain(self.scan_pre(g, 0, gc0, sc, ARBD, BBD, KBD, VBD))
        for c in range(NCK):
            ch = self.scan_chain(g, c, gc0 + c, sc, ARBD, PCs, bon, gst, Hf, Hb, tmpH)
            if c + 1 < NCK:
                interleave(ch, self.scan_pre(g, c + 1, gc0 + c + 1, sc, ARBD, BBD, KBD, VBD))
            else:
                drain(ch)

    def y_post(self, g, c, gc, s, sc, psY, pkY, bon, gst):
        S = self.S
        pv, cb = self.pv, self.cb
        ys, yq, st, yt, YBD = sc["ys", s], sc["yq", s], sc["st", s], sc["yt", s], sc["YBD", s]
        kys, kyq, kst, kyt, kY = "ys_0", "yq_0", "yst_0", "yt_0", "YBD_0"
        self.I(ACT, "copy", ys[:], psY[:, 0:256].rearrange("p (a b) -> p a b", b=64), reads=[pkY], writes=[kys])
        self.I(DVE, "tensor_reduce", st[:, 0, :], ys[:], AX.X, ALU.add, reads=[kys], writes=[kst])
        self.I(POOL, "tensor_tensor", yq[:], ys[:], ys[:], ALU.mult, reads=[kys], writes=[kyq])
        self.I(DVE, "tensor_reduce", st[:, 1, :], yq[:], AX.X, ALU.add, reads=[kyq], writes=[kst])
        yield
        self.I(DVE, "tensor_scalar", st[:, 2, :], st[:, 0, :], 1.0 / 64, None, ALU.mult, reads=[kst], writes=[kst])
        self.I(DVE, "tensor_tensor", st[:, 3, :], st[:, 2, :], st[:, 2, :], ALU.mult, reads=[kst], writes=[kst])
        self.I(DVE, "scalar_tensor_tensor", st[:, 4, :], st[:, 1, :], 1.0 / 64, st[:, 3, :], ALU.mult, ALU.subtract, reads=[kst], writes=[kst])
        self.I(DVE, "tensor_scalar", st[:, 4, :], st[:, 4, :], 64e-5, None, ALU.add, reads=[kst], writes=[kst])
        self.I(ACT, "activation", st[:, 5, :], st[:, 4, :], AF.Sqrt, reads=[kst], writes=[kst])
        self.I(DVE, "reciprocal", st[:, 6, :], st[:, 5, :], reads=[kst], writes=[kst])
        yield
        self.I(DVE, "tensor_tensor", yq[:], ys[:], st[:, 2, :].unsqueeze(2).to_broadcast([128, 4, 64]), ALU.subtract, reads=[kys, kst], writes=[kyq])
        for h in range(2):
            pr = slice(64 * h, 64 * h + 64)
            self.I(DVE if h == 0 else POOL, "tensor_tensor", YBD[pr, :, pr], yq[pr, :, :], st[pr, 6, :].unsqueeze(2).to_broadcast([64, 4, 64]), ALU.mult, reads=[kyq, kst], writes=[kY])
        yield
        ps, pk = self.psum()
        for q in range(4):
            self.I(PE, "matmul", ps[:, q * 64:(q + 1) * 64], YBD[:, q, :], cb[:, C_ISEL:C_ISEL + 64], start=True, stop=True, reads=[kY, "cb"], writes=[pk], track=(q == 3))
        t0 = 0
        col0 = gc * CH - OWN0
        if col0 < 0:
            t0 = -col0
            col0 = 0
        nt = CH - t0
        cl = c * CH + t0
        lw = pv[:, V_LW + 4 * g: V_LW + 4 * g + 4].unsqueeze(2).to_broadcast([128, 4, nt])
        psv = ps[:, 0:256].rearrange("p (a b) -> p a b", b=64)[:, :, t0:CH]
        self.I(DVE, "tensor_tensor", yt[:, :, 0:nt], psv, lw, ALU.mult, reads=[pk, "pv"], writes=[kyt])
        self.I(POOL, "tensor_tensor", yt[:, :, 0:nt], yt[:, :, 0:nt], bon[:, :, cl:cl + nt], ALU.add, reads=[kyt] + ["bon%d" % q for q in range(4)], writes=[kyt])
        self.I(DVE, "tensor_tensor", self.yaT[:, 4 * g:4 * g + 4, col0:col0 + nt], yt[:, :, 0:nt], gst[:, :, cl:cl + nt], ALU.mult, reads=[kyt] + ["gst%d" % q for q in range(4)], writes=["yaT"])


def core_inputs(inp, c, pv_base):
    b, p = c // 2, c % 2
    x = inp["x_prompt"][b]
    if p == 1:
        xseq = np.ascontiguousarray(x)
    else:
        xseq = np.concatenate([np.zeros((1024, D), np.float32), x[:1024]], axis=0)
    pa = permA()
    return {
        "xseq": xseq,
        "xsamp": np.ascontiguousarray(inp["x_sample"][16 * c:16 * c + 16].reshape(NSM, D)),
        "pvec": core_pvec(pv_base, p),
        "st_pool": np.ascontiguousarray(inp["state_pool"][0, 16 * c:16 * c + 16]),
        "st_conv": np.ascontiguousarray(inp["state_conv"][0, 16 * c:16 * c + 16]),
        "st_shift": np.ascontiguousarray(inp["state_shift"][0, 16 * c:16 * c + 16, 0][:, pa]),
        "st_wkv": np.ascontiguousarray(inp["state_wkv"][0, 16 * c:16 * c + 16].reshape(256, 4096)),
    }


def shared_inputs(inp):
    w_in = inp["w_in"][0]
    wG = np.empty((D, 16, 256), np.float32)
    wG[:, :, 0:128] = w_in[:, 4384:6432].reshape(D, 16, 128)
    wG[:, :, 128:256] = w_in[:, 6432:8480].reshape(D, 16, 128)
    wfi = inp["w_ffn_in"][0]
    wF = np.empty((D, NJ, 256), np.float32)
    wF[:, :, 0:128] = wfi[:, 0:DFF].reshape(D, NJ, 128)
    wF[:, :, 128:256] = wfi[:, DFF:2 * DFF].reshape(D, NJ, 128)
    gpost = np.empty((128, 2, D), np.float32)
    gpost[:, 0, :] = inp["norm_post_mix"][0][None, :]
    gpost[:, 1, :] = inp["norm_post_ffn"][0][None, :]
    return {
        "wA": np.ascontiguousarray(w_in[:, permA()]),
        "consts": host_consts(),
        "w2": np.ascontiguousarray(inp["w2"][0]),
        "a2": np.ascontiguousarray(inp["a2"][0]),
        "g2": np.ascontiguousarray(inp["g2"][0]),
        "wP": np.ascontiguousarray(w_in[:, 3360:4384]),
        "wG": wG,
        "wa": np.ascontiguousarray(inp["w_branch_a"][0]),
        "wbr": np.ascontiguousarray(inp["w_branch_b"][0]),
        "poolw": np.ascontiguousarray(inp["pool_w"][0]),
        "wout": np.ascontiguousarray(inp["w_out"][0]),
        "gpost": gpost,
        "wF": wF,
        "wfo": np.ascontiguousarray(inp["w_ffn_out"][0]),
    }


_NC_CACHE = {}


def get_nc(dbg=(), stages="SABC"):
    key = (tuple(sorted(dbg)), stages)
    if key not in _NC_CACHE:
        B = Builder(dbg=set(dbg), stages=stages)
        nc = B.build()
        _NC_CACHE[key] = (nc, B)
    return _NC_CACHE[key]


def run_cores(inp, cores, dbg=(), stages="SABC", trace=False):
    nc, B = get_nc(dbg, stages)
    sh = shared_inputs(inp)
    pv_base = host_pvec(inp)
    names = set(B.ins.keys())
    maps = []
    for c in cores:
        m = dict(sh)
        m.update(core_inputs(inp, c, pv_base))
        maps.append({k: v for k, v in m.items() if k in names})
    res = run_bass_kernel_spmd(nc, maps, core_ids=list(range(len(cores))), trace=trace)
    return res


def kernel(**inp):
    inp = {k: np.asarray(v) for k, v in inp.items()}
    res = run_cores(inp, list(range(8)))
    R = res.results
    pa = permA()
    y_prompt = np.empty((4, 2048, D), np.float32)
    y_sample = np.empty((128, 4, D), np.float32)
    p_shift = np.empty((1, 4, 1, DS), np.float32)
    p_wkv = np.empty((1, 4, 16, 64, 64), np.float32)
    p_pool = np.empty((1, 4, 15, 1024), np.float32)
    p_conv = np.empty((1, 4, 2, DFF), np.float32)
    s_shift = np.zeros((1, 128, 1, DS), np.float32)
    s_wkv = np.zeros((1, 128, 16, 64, 64), np.float32)
    s_pool = np.empty((1, 128, 15, 1024), np.float32)
    s_conv = np.empty((1, 128, 2, DFF), np.float32)
    for c in range(8):
        b, p = c // 2, c % 2
        r = R[c]
        y_prompt[b, p * 1024:(p + 1) * 1024] = r["o_y"][0:1024]
        y_sample[16 * c:16 * c + 16] = r["o_y"][1024:1024 + NSM].reshape(16, 4, D)
        if p == 1:
            p_shift[0, b, 0, pa] = r["o_pshift"]
            p_wkv[0, b] = r["o_pwkv"]
            p_pool[0, b] = r["o_ppool"]
            p_conv[0, b] = r["o_pconv"]
        s_pool[0, 16 * c:16 * c + 16] = r["o_spool"]
        s_conv[0, 16 * c:16 * c + 16] = r["o_sconv"]
        if "o_sshift" in r:
            s_shift[0, 16 * c:16 * c + 16, 0][:, pa] = r["o_sshift"]
            s_wkv[0, 16 * c:16 * c + 16] = r["o_swkv"].reshape(16, 16, 64, 64)
    return (y_prompt, y_sample, p_shift, p_wkv, p_pool, p_conv, s_shift, s_wkv, s_pool, s_conv)
```

```python
import os
import numpy as np
import concourse.bass as bass
import concourse.mybir as mybir
from contextlib import ExitStack
from concourse.bass_utils import run_bass_kernel_spmd

F32 = mybir.dt.float32
BF16 = mybir.dt.bfloat16
AF = mybir.ActivationFunctionType
ALU = mybir.AluOpType
AX = mybir.AxisListType

PE, ACT, DVE, POOL, SP = "pe", "act", "dve", "pool", "sp"
NDMASEM = 16
VCLOCK = os.environ.get('VCLOCK', '1') == '1'
EMBED_WAIT = os.environ.get('EMBW', '1') == '1'
SAME_ENGINE_NOWAIT = os.environ.get('SENW', '0') == '1'


class Sched:
    def __init__(self, nc, es):
        self.nc = nc
        self.q = {e: [] for e in (PE, ACT, DVE, POOL, SP)}
        self.cnt = {e: 0 for e in (PE, ACT, DVE, POOL)}
        self.sem = {e: es.enter_context(nc.semaphore("s_" + e)) for e in (PE, ACT, DVE, POOL)}
        self.dsem = {}
        self.dcnt = {}
        for e in (SP, "spo", "bg", POOL):
            self.dsem[e] = [es.enter_context(nc.semaphore("d_%s%d" % (e, i))) for i in range(NDMASEM)]
            self.dcnt[e] = 0
        self.pending = {e: [] for e in (PE, ACT, DVE, POOL, SP)}
        self.tok_ring = {}
        self.clock = {}
        self.tclk = {}
        self.lastw = {}
        self.readers = {}
        self.all_tokens = {}

    def _deps(self, eng, reads, writes):
        toks = []
        for k in reads:
            w = self.lastw.get(k)
            if w is not None:
                toks.append(w)
        for k in writes:
            w = self.lastw.get(k)
            if w is not None:
                toks.append(w)
            toks.extend(self.readers.get(k, ()))
        clk = self.clock.setdefault(eng, {})
        need = {}
        for tok in toks:
            s, v, src = tok[0], tok[1], tok[2]
            if src == eng and eng == PE:
                continue
            if clk.get(s.name, 0) >= v:
                continue
            if need.get(s.name, (None, 0))[1] < v:
                need[s.name] = (s, v, self.tclk.get((s.name, v), {}))
        for (s, v) in self.pending[eng]:
            if clk.get(s.name, 0) >= v:
                continue
            if need.get(s.name, (None, 0))[1] < v:
                need[s.name] = (s, v, self.tclk.get((s.name, v), {}))
        self.pending[eng] = []
        if VCLOCK and len(need) > 1:
            drop = set()
            for name, (s, v, c) in need.items():
                for name2, (s2, v2, c2) in need.items():
                    if name2 != name and name2 not in drop and c2.get(name, 0) >= v:
                        drop.add(name)
                        break
            for name in drop:
                del need[name]
        out = []
        for name, (s, v, c) in need.items():
            if clk.get(name, 0) < v:
                clk[name] = v
            for n2, v2 in c.items():
                if clk.get(n2, 0) < v2:
                    clk[n2] = v2
            out.append((s, v))
        return out

    def _stamp(self, eng, tok):
        c = dict(self.clock.get(eng, {}))
        c[tok[0].name] = tok[1]
        self.tclk[(tok[0].name, tok[1])] = c

    def barrier(self):
        if self.frozen:
            return
        allw = []
        for e in (PE, ACT, DVE, POOL):
            if self.cnt[e] > 0:
                allw.append((self.sem[e], self.cnt[e], e))
        for name, (s, v) in self.all_tokens.items():
            if self.tok_ring.get(name) != "bg":
                allw.append((s, v, "dma"))
        for eng in (PE, ACT, DVE, POOL, SP):
            for (s, v, src) in allw:
                if src == eng:
                    continue
                self.pending[eng].append((s, v))

    def _commit(self, tok, reads, writes):
        for k in writes:
            self.lastw[k] = tok
            self.readers[k] = []
        for k in reads:
            lst = self.readers.setdefault(k, [])
            lst[:] = [t for t in lst if t[0].name != tok[0].name]
            lst.append(tok)

    frozen = False

    def op(self, eng, fn, reads=(), writes=(), track=True):
        if self.frozen:
            return None
        waits = self._deps(eng, reads, writes)
        tok = None
        if not track:
            assert eng == PE
            self.pend_r = getattr(self, "pend_r", set()) | set(reads)
        if track:
            self.cnt[eng] += 1
            tok = (self.sem[eng], self.cnt[eng], eng)
            self._stamp(eng, tok)
            if eng == PE and getattr(self, "pend_r", None):
                reads = list(set(reads) | self.pend_r)
                self.pend_r = set()
            self._commit(tok, reads, writes)
        self.q[eng].append((waits, fn, tok))
        return tok

    def dma(self, qeng, out, in_, reads=(), writes=(), ring=None, **kw):
        if self.frozen:
            return None
        waits = self._deps(qeng, reads, writes)
        rk = ring or qeng
        n = self.dcnt[rk]
        self.dcnt[rk] += 1
        s = self.dsem[rk][n % NDMASEM]
        v = 16 * (n // NDMASEM + 1)
        tok = (s, v, "dma")
        self._stamp(qeng, tok)
        self._commit(tok, reads, writes)
        self.q[qeng].append((waits, lambda e: e.dma_start(out=out, in_=in_, **kw), tok))
        self.all_tokens[s.name] = (s, v)
        self.tok_ring[s.name] = rk
        return tok

    def emit(self):
        nc = self.nc
        fin = dict(self.all_tokens)
        for e in (PE, ACT, DVE, POOL):
            if self.cnt[e] > 0:
                fin[self.sem[e].name] = (self.sem[e], self.cnt[e])
        q = self.q
        with nc.Block() as block:
            def run(eng_name):
                def body(e):
                    for (waits, fn, tok) in q[eng_name]:
                        emb = None
                        if EMBED_WAIT and waits:
                            emb = waits[-1]
                            waits = waits[:-1]
                        for (s, v) in waits:
                            e.wait_ge(s, v)
                        ins = fn(e)
                        if emb is not None:
                            ins._wait_ge(emb[0], emb[1])
                        if tok is not None:
                            ins.then_inc(tok[0], 16 if tok[2] == "dma" else 1)
                    if eng_name == SP:
                        for name, (s, v) in fin.items():
                            e.wait_ge(s, v)
                return body
            block.tensor(run(PE))
            block.scalar(run(ACT))
            block.vector(run(DVE))
            block.gpsimd(run(POOL))
            block.sync(run(SP))


D = 2048
DS = 3360
NPR = 2048
NSM = 64
OWN0 = 992
NOWN = 1120
BLK = 512
NH = 512
CH = 64
C_ID, C_BONES, C_ISEL, C_MUS, C_MUI, C_MLS, C_M01 = 0, 128, 256, 320, 448, 576, 704
NCONST = 704 + 512
V_MULR = 0
V_MU = 4
V_W0, V_A0, V_KK, V_KA, V_RK, V_LW, V_LB = 28, 36, 44, 52, 60, 68, 76
V_G1 = 84
V_PS = 100
V_G3 = 108
V_FLAG = 124
V_INVC = 125
V_CW = 189
NV = 189 + 176
DFF = 5632
NJ = 44
DECAY_C = 0.6065306597126334


def host_consts():
    c = np.zeros((128, NCONST), np.float32)
    p = np.arange(128)[:, None]
    j = np.arange(128)[None, :]
    c[:, C_ID:C_ID + 128] = (p == j)
    c[:, C_BONES:C_BONES + 128] = (p // 64 == j // 64)
    c[:, C_ISEL:C_ISEL + 64] = (p % 64 == np.arange(64)[None, :])
    c[:, C_MUS:C_MUS + 128] = (p % 64 < j % 64)
    c[:, C_MUI:C_MUI + 128] = (p % 64 <= j % 64)
    c[:, C_MLS:C_MLS + 128] = (p % 64 > j % 64)
    c[:, C_M01:C_M01 + 512] = (np.arange(512)[None, :] % 64 != 0)
    return c


def permA():
    idx = list(range(3072, 3360))
    for q in range(8):
        idx += list(range(q * 128, q * 128 + 128))
        idx += list(range(1024 + q * 128, 1024 + q * 128 + 128))
        idx += list(range(2048 + q * 128, 2048 + q * 128 + 128))
    return np.array(idx)


def host_pvec(inp):
    v = np.zeros((128, NV), np.float32)
    mu = inp["mu_shift"][0]
    v[0:64, 0] = mu[3072:3136]
    v[0:64, 1] = mu[3136:3200]
    v[0:128, 2] = mu[3200:3328]
    v[0:32, 3] = mu[3328:3360]
    for q in range(8):
        for t in range(3):
            v[:, V_MU + 3 * q + t] = mu[t * 1024 + q * 128: t * 1024 + q * 128 + 128]
    for (col, name) in ((V_W0, "w0"), (V_A0, "a0"), (V_KK, "k_k"), (V_KA, "k_a"), (V_RK, "r_k"), (V_LW, "lnx_w"), (V_LB, "lnx_b")):
        a = inp[name][0].reshape(-1)
        for q in range(8):
            v[:, col + q] = a[q * 128:(q + 1) * 128]
    g = inp["norm_pre_mix"][0]
    g3 = inp["norm_pre_ffn"][0]
    for k in range(16):
        v[:, V_G1 + k] = g[k * 128:(k + 1) * 128]
        v[:, V_G3 + k] = g3[k * 128:(k + 1) * 128]
    psc = inp["pool_scale"][0]
    for k in range(8):
        v[:, V_PS + k] = psc[k * 128:(k + 1) * 128]
    cw, cbias = inp["conv_w"][0], inp["conv_b"][0]
    for j in range(NJ):
        for t in range(3):
            v[:, V_CW + 4 * j + t] = cw[t, j * 128:(j + 1) * 128]
        v[:, V_CW + 4 * j + 3] = cbias[j * 128:(j + 1) * 128]
    return v


def core_pvec(base, p):
    v = base.copy()
    v[:, V_FLAG] = float(p)
    for gi, win in enumerate((2, 4, 8, 16)):
        for j in range(16):
            pos = p * 1024 + j
            v[:, V_INVC + gi * 16 + j] = 1.0 / min(win, pos + 1)
    return v


class _TAlias:
    def __init__(self, phys, alias, prefix="t_"):
        self.phys = phys
        self.alias = alias
        self.prefix = prefix

    def _n(self, n):
        return self.alias.get(n, n)

    def __getitem__(self, n):
        return self.phys[self._n(n)]

    def key(self, n):
        return self.prefix + self._n(n)


class StopBuild(Exception):
    pass


class Builder:
    stop_at = None

    def ckpt(self, name):
        if self.stop_at == name and not self.S.frozen:
            print("frozen at", name)
            self.S.frozen = True
            self.dbg = set()

    def __init__(self, dbg=None, stages="A"):
        self.dbg = dbg or set()
        self.stages = stages
        self.nc = bass.Bass("TRN2", target_bir_lowering=False)
        self.ins = {}
        self.outs = {}
        self.psn = 0

    def din(self, name, shape, dt=F32):
        t = self.nc.dram_tensor(name, list(shape), dt, kind="ExternalInput").ap()
        self.ins[name] = t
        return t

    def dout(self, name, shape, dt=F32):
        t = self.nc.dram_tensor(name, list(shape), dt, kind="ExternalOutput").ap()
        self.outs[name] = t
        return t

    def I(self, eng, meth, *args, reads=(), writes=(), track=True, **kw):
        return self.S.op(eng, lambda e: getattr(e, meth)(*args, **kw), reads=reads, writes=writes, track=track)

    def stage_out(self, name, tile_ap, shape, key):
        scr = self.nc.dram_tensor("scr_" + name, list(shape), F32).ap()
        self.S.dma(SP, scr, tile_ap, reads=[key], writes=["scr_" + name], ring="spo")
        return scr

    def bg(self, dst, src, name):
        self.S.dma(SP, dst, src, reads=["scr_" + name], ring="bg", allow_slow_non_contiguous=True)

    def sb(self, es, name, shape, dt=F32):
        return es.enter_context(self.nc.sbuf_tensor(name, list(shape), dt))

    def psum(self):
        i = self.psn % 8
        self.psn += 1
        return self.PS[i], "ps%d" % i

    def build(self):
        nc = self.nc
        xseq = self.xseq = self.din("xseq", [NPR, D])
        xsamp = self.xsamp = self.din("xsamp", [NSM, D])
        wA = self.din("wA", [D, DS])
        consts = self.din("consts", [128, NCONST])
        pvec = self.din("pvec", [128, NV])
        w2 = self.din("w2", [64, 1024])
        a2 = self.din("a2", [64, 1024])
        g2 = self.din("g2", [160, 1024])
        self.wP = self.din("wP", [D, 1024])
        self.wG = self.din("wG", [D, 16, 256])
        self.wa = self.din("wa", [1024, D])
        self.wbr = self.din("wbr", [1024, D])
        self.poolw = self.din("poolw", [4, 256, 256])
        self.wout = self.din("wout", [D, D])
        self.gpost = self.din("gpost", [128, 2, D])
        self.wF = self.din("wF", [D, NJ, 256])
        self.wfo = self.din("wfo", [DFF, D])
        self.st_pool = self.din("st_pool", [16, 15, 1024])
        self.st_conv = self.din("st_conv", [16, 2, DFF])
        self.o_pshift = self.dout("o_pshift", [DS])
        self.o_pwkv = self.dout("o_pwkv", [16, 64, 64])
        self.o_ppool = self.dout("o_ppool", [15, 1024])
        self.o_pconv = self.dout("o_pconv", [2, DFF])
        self.o_spool = self.dout("o_spool", [16, 15, 1024])
        self.o_sconv = self.dout("o_sconv", [16, 2, DFF])
        self.o_y = self.dout("o_y", [1024 + NSM, D])
        self.st_shift = self.din("st_shift", [16, DS])
        self.st_wkv = self.din("st_wkv", [256, 4096])
        self.o_sshift = self.dout("o_sshift", [16, DS])
        self.o_swkv = self.dout("o_swkv", [256, 4096])
        self.x1s = nc.dram_tensor("x1s", [NOWN, D], F32).ap()
        self.scrS = nc.dram_tensor("scrS", [NSM, 6, 1024], F32).ap()
        self.scrY = nc.dram_tensor("scrY", [NSM, 1024], F32).ap()
        self.scrH = nc.dram_tensor("scrH", [128, 16, NOWN], BF16).ap()
        if "ya" in self.dbg:
            self.o_ya = self.dout("d_ya", [128, 8, NOWN])
        with ExitStack() as es:
            S = self.S = Sched(nc, es)
            self.PS = [es.enter_context(nc.psum_tensor("ps%d" % i, [128, 512], F32)) for i in range(8)]
            cf = self.cf = self.sb(es, "cf", [128, NCONST])
            cb = self.cb = self.sb(es, "cb", [128, 320], BF16)
            pv = self.pv = self.sb(es, "pv", [128, NV])
            S.dma(SP, cf[:], consts, writes=["cf"])
            S.dma(SP, pv[:], pvec, writes=["pv"])
            S.dma(POOL, cb[:], consts[:, 0:320], writes=["cb"])
            self.wslot = 0
            bufX = self.bufX = self.sb(es, "bufX", [128, 16, NOWN], BF16)
            self.yaT = bufX[:, 0:8, :]
            self.ybT = bufX[:, 8:16, :]
            with ExitStack() as esw:
                self.wb = [self.sb(esw, "wb%d" % i, [128, 16, 384], BF16) for i in range(2)]
                self.wA_ap = wA
                with ExitStack() as esA:
                    w2b = self.w2b = self.sb(esA, "w2b", [64, 1024], BF16)
                    a2b = self.a2b = self.sb(esA, "a2b", [64, 1024], BF16)
                    g2a = self.g2a = self.sb(esA, "g2a", [128, 1024], BF16)
                    g2b = self.g2b = self.sb(esA, "g2b", [128, 1024], BF16)
                    S.dma(POOL, w2b[:], w2, writes=["w2b"])
                    S.dma(POOL, a2b[:], a2, writes=["a2b"])
                    S.dma(POOL, g2a[:], g2[0:128, :], writes=["g2a"])
                    self.I(POOL, "memset", g2b[:], 0.0, writes=["g2b"])
                    S.dma(POOL, g2b[0:32, :], g2[128:160, :], writes=["g2b"])
                    if "S" in self.stages:
                        self.stageS()
                    self.stageA(esA, xseq, wA)
                if "ya" in self.dbg:
                    with ExitStack() as esd:
                        yaf = self.sb(esd, "yaf", [128, 8, NOWN], F32)
                        self.I(DVE, "tensor_copy", yaf[:], self.yaT, reads=["yaT"], writes=["yaf"])
                        S.dma(SP, self.o_ya, yaf[:], reads=["yaf"], ring="spo")
                        S.barrier()
                if "B" in self.stages:
                    with ExitStack() as esm:
                        self.mT = self.sb(esm, "mT", [128, 16, NOWN], BF16)
                        with ExitStack() as esb:
                            self.hTo = self.sb(esb, "hTo", [128, 16, NOWN], BF16)
                            self.stageB12(esb)
                        self.stageB3(esm)
            if "C" in self.stages:
                self.stageC(es)
            S.emit()
        return nc

    def tok_groups(self):
        return [(0, 512), (512, 512), (1024, NOWN - 1024)]

    def stageB12(self, es_outer):
        S = self.S
        pv, cf, cb = self.pv, self.cf, self.cb
        hT = self.hTo
        with ExitStack() as es:
            sb = lambda n, s, d=F32: self.sb(es, n, s, d)
            self.xt = [sb("xtB", [128, D])]
            self.xb = [sb("xbB", [128, D], BF16)]
            self.xst = [sb("xstB", [128, 4])]
            for (c0, c1, k) in ((0, 32, "scrH_1"), (32, 544, "scrH_2"), (544, 1056, "scrH_3"), (1056, NOWN, "scrH_s")):
                S.dma(SP, hT[:, :, c0:c1], self.scrH[:, :, c0:c1], reads=[k], writes=["hTo"])
            self.ckpt("B0")
            ybT = self.ybT
            with ExitStack() as esp:
                sbp = lambda n, s, d=F32: self.sb(esp, n, s, d)
                pwb = sbp("pwb", [128, 4, 2, 256], BF16)
                S.dma(POOL, pwb[:], self.poolw.rearrange("g (k p) n -> p g k n", p=128), writes=["pwb"])
                phT = sbp("phT", [128, 8, 16, 15])
                for half in range(2):
                    sp_t = sbp("sp_t%d" % half, [120, 1024])
                    S.dma(SP, sp_t[:], self.st_pool[8 * half: 8 * half + 8].rearrange("b j c -> (b j) c"), writes=["sp_t%d" % half])
                    for c4 in range(2):
                        ps, pk = self.psum()
                        for ch in range(4):
                            self.I(PE, "transpose", ps[:, ch * 120:(ch + 1) * 120], sp_t[:, (c4 * 4 + ch) * 128:(c4 * 4 + ch + 1) * 128], cf[0:120, 0:120],
                                   reads=["sp_t%d" % half, "cf"], writes=[pk], track=(ch == 3))
                        self.I(DVE, "tensor_copy", phT[:, c4 * 4:c4 * 4 + 4, 8 * half:8 * half + 8, :],
                               ps[:, 0:480].rearrange("p (a b j) -> p a b j", a=4, b=8), reads=[pk], writes=["phT"])
                self.ckpt("B1a")
                S.dma(SP, self.o_spool[:, 0:11, :], self.st_pool[:, 4:15, :], ring="spo")
                zp = sbp("zp", [128, NOWN])
                pa_ = sbp("ppA", [128, NOWN])
                pb_ = sbp("ppB", [128, NOWN])
                dT = sbp("dT", [128, 8, NOWN], BF16)
                self.I(POOL, "memset", dT[:], 0.0, writes=["dT"])
                ppo = sbp("ppo", [128, 8, 15])
                spo = sbp("spo", [128, 8, 16, 4])
                bs = sbp("bs", [128, 16, 19])
                bs2 = sbp("bs2", [128, 16, 19])
                slabs = [(self.wP[:, ch * 128:(ch + 1) * 128], 128) for ch in range(8)]
                stream = self.slab_stream(slabs)
                NP_ = 1056
                for ch in range(8):
                    wt, wk = next(stream)
                    gi = ch // 2
                    win = 2 << gi
                    for (t0, tn) in self.tok_groups():
                        ps, pk = self.psum()
                        for k in range(16):
                            self.I(PE, "matmul", ps[:, 0:tn], wt[:, k, 0:128], hT[:, k, t0:t0 + tn], start=(k == 0), stop=(k == 15),
                                   reads=[wk, "hTo"], writes=[pk], track=(k == 15))
                        self.I(ACT, "copy", zp[:, t0:t0 + tn], ps[:, 0:tn], reads=[pk], writes=["zp"])
                    src, ksrc = zp, "zp"
                    bufs = [(pa_, "ppA"), (pb_, "ppB")]
                    for j in range(gi + 1):
                        sh = 1 << j
                        dst, kdst = bufs[j % 2]
                        self.I(DVE if j % 2 == 0 else POOL, "tensor_tensor", dst[:, sh:NP_], src[:, sh:NP_], src[:, 0:NP_ - sh], ALU.add,
                               reads=[ksrc], writes=[kdst])
                        src, ksrc = dst, kdst
                    lo = win - 1
                    self.I(DVE, "scalar_tensor_tensor", dT[:, ch, lo:NP_], src[:, lo:NP_], 1.0 / win, zp[:, lo:NP_], ALU.mult, ALU.subtract,
                           reads=[ksrc, "zp"], writes=["dT"])
                    other, kother = bufs[(gi + 1) % 2]
                    self.I(POOL, "tensor_tensor", other[:, 32:48], src[:, 32:48], pv[:, V_INVC + gi * 16: V_INVC + gi * 16 + 16], ALU.mult,
                           reads=[ksrc, "pv"], writes=[kother])
                    self.I(POOL, "tensor_tensor", dT[:, ch, 32:48], other[:, 32:48], zp[:, 32:48], ALU.subtract, reads=[kother, "zp"], writes=["dT"])
                    self.I(POOL, "tensor_copy", ppo[:, ch, :], zp[:, NP_ - 15:NP_], reads=["zp"], writes=["ppo"])
                    zps = zp[:, NP_:NOWN].rearrange("p (b t) -> p b t", t=4)
                    self.I(POOL, "tensor_copy", bs[:, :, 0:15], phT[:, ch, :, :], reads=["phT"], writes=["bs"])
                    self.I(POOL, "tensor_copy", bs[:, :, 15:19], zps, reads=["zp"], writes=["bs"])
                    self.I(POOL, "tensor_copy", spo[:, ch, :, :], zps, reads=["zp"], writes=["spo"])
                    ssrc, kss = bs, "bs"
                    sbufs = [(bs2, "bs2"), (bs, "bs")]
                    for j in range(gi + 1):
                        sh = 1 << j
                        dst, kdst = sbufs[j % 2]
                        self.I(DVE, "tensor_tensor", dst[:, :, sh:19], ssrc[:, :, sh:19], ssrc[:, :, 0:19 - sh], ALU.add, reads=[kss], writes=[kdst])
                        ssrc, kss = dst, kdst
                    self.I(DVE, "scalar_tensor_tensor", dT[:, ch, NP_:NOWN].rearrange("p (b t) -> p b t", t=4), ssrc[:, :, 15:19], 1.0 / win, zps,
                           ALU.mult, ALU.subtract, reads=[kss, "zp"], writes=["dT"])
                self.ckpt("B1b")
                scr = self.stage_out("ppo", ppo[:], [128, 8, 15], "ppo")
                for ch in range(8):
                    self.bg(self.o_ppool[:, ch * 128:(ch + 1) * 128].rearrange("j p -> p j"), scr[:, ch, :], "ppo")
                sprow = sbp("sprow", [NSM, 1024])
                for c4 in range(2):
                    ps, pk = self.psum()
                    for ch in range(4):
                        self.I(PE, "transpose", ps[0:NSM, ch * 128:(ch + 1) * 128], spo[:, c4 * 4 + ch, :, :].rearrange("p b t -> p (b t)"), cf[:, C_ID:C_ID + 128],
                               reads=["spo", "cf"], writes=[pk], track=(ch == 3))
                    self.I(DVE, "tensor_copy", sprow[:, c4 * 512:(c4 + 1) * 512], ps[0:NSM, :], reads=[pk], writes=["sprow"])
                for b in range(16):
                    S.dma(SP, self.o_spool[b, 11:15, :], sprow[4 * b:4 * b + 4, :], reads=["sprow"], ring="spo")
                self.ckpt("B1c")
                for oc in range(8):
                    gi, o2 = oc // 2, oc % 2
                    for (t0, tn) in self.tok_groups():
                        ps, pk = self.psum()
                        for kk in range(2):
                            self.I(PE, "matmul", ps[:, 0:tn], pwb[:, gi, kk, o2 * 128:(o2 + 1) * 128], dT[:, 2 * gi + kk, t0:t0 + tn],
                                   start=(kk == 0), stop=(kk == 1), reads=["pwb", "dT"], writes=[pk], track=(kk == 1))
                        self.I(ACT, "activation", ybT[:, oc, t0:t0 + tn], ps[:, 0:tn], AF.Copy, scale=pv[:, V_PS + oc: V_PS + oc + 1],
                               reads=[pk, "pv"], writes=["ybT"])
                S.barrier()
            self.ckpt("B1d")
            mT = self.mT
            tmp = [sb("b2t%d" % i, [128, 512]) for i in range(4)]
            njs = 16

            def issue(j):
                slot = self.wslot % 2
                self.wslot += 1
                key = "wb%d" % slot
                w = self.wb[slot]
                wflat = w[:].rearrange("p a b -> p (a b)")
                S.dma(POOL, wflat[:, 0:4096].rearrange("p (k n) -> p k n", n=256), self.wG[:, j, :].rearrange("(k p) n -> p k n", p=128), writes=[key])
                S.dma(POOL, wflat[:, 4096:5120].rearrange("p (k n) -> p k n", n=128), self.wa[:, j * 128:(j + 1) * 128].rearrange("(k p) n -> p k n", p=128), writes=[key])
                S.dma(POOL, wflat[:, 5120:6144].rearrange("p (k n) -> p k n", n=128), self.wbr[:, j * 128:(j + 1) * 128].rearrange("(k p) n -> p k n", p=128), writes=[key])
                return (wflat, key)
            cur = issue(0)
            for j in range(njs):
                nxt = issue(j + 1) if j + 1 < njs else None
                wflat, wk = cur
                wg = wflat[:, 0:4096].rearrange("p (k n) -> p k n", n=256)
                wa_ = wflat[:, 4096:5120].rearrange("p (k n) -> p k n", n=128)
                wb_ = wflat[:, 5120:6144].rearrange("p (k n) -> p k n", n=128)
                for (t0, tn) in self.tok_groups():
                    psa, pka = self.psum()
                    psb, pkb = self.psum()
                    ppa, pkpa = self.psum()
                    ppb, pkpb = self.psum()
                    for k in range(16):
                        self.I(PE, "matmul", psa[:, 0:tn], wg[:, k, 0:128], hT[:, k, t0:t0 + tn], start=(k == 0), stop=(k == 15),
                               reads=[wk, "hTo"], writes=[pka], track=(k == 15))
                    for k in range(16):
                        self.I(PE, "matmul", psb[:, 0:tn], wg[:, k, 128:256], hT[:, k, t0:t0 + tn], start=(k == 0), stop=(k == 15),
                               reads=[wk, "hTo"], writes=[pkb], track=(k == 15))
                    for k in range(8):
                        self.I(PE, "matmul", ppa[:, 0:tn], wa_[:, k, :], self.yaT[:, k, t0:t0 + tn], start=(k == 0), stop=(k == 7),
                               reads=[wk, "yaT"], writes=[pkpa], track=(k == 7))
                    for k in range(8):
                        self.I(PE, "matmul", ppb[:, 0:tn], wb_[:, k, :], ybT[:, k, t0:t0 + tn], start=(k == 0), stop=(k == 7),
                               reads=[wk, "ybT"], writes=[pkpb], track=(k == 7))
                    self.I(ACT, "activation", tmp[0][:, 0:tn], psa[:, 0:tn], AF.Sigmoid, reads=[pka], writes=["b2t0"])
                    self.I(ACT, "activation", tmp[1][:, 0:tn], psb[:, 0:tn], AF.Sigmoid, reads=[pkb], writes=["b2t1"])
                    self.I(DVE, "tensor_tensor", tmp[2][:, 0:tn], tmp[0][:, 0:tn], ppa[:, 0:tn], ALU.mult, reads=["b2t0", pkpa], writes=["b2t2"])
                    self.I(DVE, "tensor_tensor", tmp[3][:, 0:tn], tmp[1][:, 0:tn], ppb[:, 0:tn], ALU.mult, reads=["b2t1", pkpb], writes=["b2t3"])
                    self.I(POOL, "tensor_tensor", mT[:, j, t0:t0 + tn], tmp[2][:, 0:tn], tmp[3][:, 0:tn], ALU.add, reads=["b2t2", "b2t3"], writes=["mT"])
                cur = nxt
            S.barrier()

    def tok_tiles(self):
        self.ckpt("B2")
        return [(0, 32)] + [(32 + 128 * i, 128) for i in range(8)] + [(1056, NSM)]

    def stageB3(self, es_outer):
        S = self.S
        pv, cf, cb = self.pv, self.cf, self.cb
        mT, h2T = self.mT, self.bufX
        with ExitStack() as es:
            sb = lambda n, s, d=F32: self.sb(es, n, s, d)
            woutb = sb("woutb", [128, 16, D], BF16)
            for n in range(4):
                S.dma(POOL, woutb[:, :, n * 512:(n + 1) * 512], self.wout[:, n * 512:(n + 1) * 512].rearrange("(k p) n -> p k n", p=128), writes=["woutb%d" % n])
            gp = sb("gp", [128, D])
            S.dma(SP, gp[:], self.gpost[:, 0, :], writes=["gp"])
            mo = sb("mo", [128, D])
            xt = sb("xt3", [128, D])
            x1 = sb("x1t", [128, D])
            xb = sb("xb3", [128, D], BF16)
            st = sb("st3", [128, 16])
            for (c0, nt) in self.tok_tiles():
                xrows = self.xseq[OWN0 + c0: OWN0 + c0 + nt, :] if c0 < 1056 else self.xsamp
                S.dma(SP, xt[0:nt, :], xrows, writes=["xt3"])
                self.I(POOL, "memset", st[:, 0:4], 0.0, writes=["st3"])
                self.ckpt("B3pre")
                for n in range(4):
                    ps, pk = self.psum()
                    for k in range(16):
                        self.I(PE, "matmul", ps[0:nt, :], mT[:, k, c0:c0 + nt], woutb[:, k, n * 512:(n + 1) * 512], start=(k == 0), stop=(k == 15),
                               reads=["mT", "woutb%d" % n], writes=[pk], track=(k == 15))
                    self.I(DVE, "tensor_copy", mo[0:nt, n * 512:(n + 1) * 512], ps[0:nt, :], reads=[pk], writes=["mo"])
                    self.I(ACT, "activation", xb[0:nt, n * 512:(n + 1) * 512], mo[0:nt, n * 512:(n + 1) * 512], AF.Square, accum_out=st[0:nt, n:n + 1], reads=["mo"], writes=["xb3", "st3"])
                self.ckpt("B3a")
                self.I(DVE, "tensor_reduce", st[0:nt, 4:5], st[0:nt, 0:4], AX.X, ALU.add, reads=["st3"], writes=["st3"])
                self.I(DVE, "tensor_scalar", st[0:nt, 5:6], st[0:nt, 4:5], 1.0 / D, 1e-6, ALU.mult, ALU.add, reads=["st3"], writes=["st3"])
                self.I(ACT, "activation", st[0:nt, 6:7], st[0:nt, 5:6], AF.Sqrt, reads=["st3"], writes=["st3"])
                self.I(DVE, "reciprocal", st[0:nt, 7:8], st[0:nt, 6:7], reads=["st3"], writes=["st3"])
                self.I(DVE, "scalar_tensor_tensor", mo[0:nt, :], mo[0:nt, :], st[0:nt, 7:8], gp[0:nt, :], ALU.mult, ALU.mult, reads=["mo", "st3", "gp"], writes=["mo"])
                self.I(POOL, "tensor_tensor", x1[0:nt, :], mo[0:nt, :], xt[0:nt, :], ALU.add, reads=["mo", "xt3"], writes=["x1t"])
                self.ckpt("B3b")
                S.dma(SP, self.x1s[c0:c0 + nt, :], x1[0:nt, :], reads=["x1t"], writes=["x1s"])
                self.ckpt("B3c")
                self.norm_T(x1, "x1t", nt, h2T[:, :, c0:c0 + nt], "h2T", xb, "xb3", st, "st3", 8, V_G3)
                self.ckpt("B3d")
            S.barrier()

    def norm_T(self, xt, kx, ntok, hT, hkey, xb, kb, st, ks, sc0, gbase):
        self.I(POOL, "memset", st[:, sc0:sc0 + 1], 0.0, writes=[ks])
        self.I(ACT, "activation", xb[0:ntok, :], xt[0:ntok, :], AF.Square, accum_out=st[0:ntok, sc0:sc0 + 1], reads=[kx], writes=[kb, ks])
        self.I(DVE, "tensor_scalar", st[0:ntok, sc0 + 1:sc0 + 2], st[0:ntok, sc0:sc0 + 1], 1.0 / D, 1e-6, ALU.mult, ALU.add, reads=[ks], writes=[ks])
        self.I(ACT, "activation", st[0:ntok, sc0 + 2:sc0 + 3], st[0:ntok, sc0 + 1:sc0 + 2], AF.Sqrt, reads=[ks], writes=[ks])
        self.I(DVE, "reciprocal", st[0:ntok, sc0 + 3:sc0 + 4], st[0:ntok, sc0 + 2:sc0 + 3], reads=[ks], writes=[ks])
        self.I(ACT, "activation", xb[0:ntok, :], xt[0:ntok, :], AF.Copy, scale=st[0:ntok, sc0 + 3:sc0 + 4], reads=[kx, ks, kb], writes=[kb])
        for half in range(2):
            ps, pk = self.psum()
            pT = ps[:].bitcast(BF16).rearrange("p (a b) -> p a b", b=128)
            for k8 in range(8):
                kc = half * 8 + k8
                self.I(PE, "transpose", pT[:, k8, 0:ntok], xb[0:ntok, kc * 128:(kc + 1) * 128], self.cb[0:ntok, 0:ntok],
                       reads=[kb, "cb"], writes=[pk], track=(k8 == 7))
            gcol = self.pv[:, gbase + half * 8: gbase + half * 8 + 8].unsqueeze(2).to_broadcast([128, 8, ntok])
            self.I(DVE, "tensor_tensor", hT[:, half * 8:half * 8 + 8, :], pT[:, :, 0:ntok], gcol, ALU.mult, reads=[pk, "pv"], writes=[hkey])

    def stageC(self, es_outer):
        S = self.S
        pv, cf, cb = self.pv, self.cf, self.cb
        h2T = self.bufX
        NA = 1024 + NSM
        with ExitStack() as es:
            sb = lambda n, s, d=F32: self.sb(es, n, s, d)
            actT = sb("actT", [128, NJ, NA], BF16)
            with ExitStack() as es1:
                sb1 = lambda n, s, d=F32: self.sb(es1, n, s, d)
                wf = [sb1("wf%d" % i, [128, 16, 256], BF16) for i in range(2)]
                gt = [sb1("gt%d" % i, [128, NOWN]) for i in range(2)]
                up = [sb1("up%d" % i, [128, NOWN]) for i in range(2)]
                cv = sb1("cv", [128, NA])
                ge = sb1("ge", [128, NA])
                gs6 = sb1("gs6", [128, 16, 6])
                chT = sb1("chT", [128, NJ, 32])
                pco = sb1("pco", [128, NJ, 2])
                sco = sb1("sco", [128, NJ, 16, 2])
                ct = sb1("ct", [32, 1408])
                stc = self.st_conv.rearrange("b r c -> (b r) c")
                for pc in range(4):
                    S.dma(SP, ct[:], stc[:, pc * 1408:(pc + 1) * 1408], writes=["ct"])
                    ps, pk = self.psum()
                    for jj in range(11):
                        self.I(PE, "transpose", ps[:, jj * 32:(jj + 1) * 32], ct[:, jj * 128:(jj + 1) * 128], cf[0:32, 0:32],
                               reads=["ct", "cf"], writes=[pk], track=(jj == 10))
                    self.I(DVE, "tensor_copy", chT[:, pc * 11:(pc + 1) * 11, :], ps[:, 0:352].rearrange("p (a b) -> p a b", b=32), reads=[pk], writes=["chT"])

                def issue(j):
                    slot = j % 2
                    S.dma(POOL, wf[slot][:], self.wF[:, j, :].rearrange("(k p) n -> p k n", p=128), writes=["wf%d" % slot])
                cwc = lambda j, t: pv[:, V_CW + 4 * j + t: V_CW + 4 * j + t + 1]
                issue(0)
                for j in range(NJ):
                    if j + 1 < NJ:
                        issue(j + 1)
                    w, wk = wf[j % 2], "wf%d" % (j % 2)
                    g_, kg = gt[j % 2], "gt%d" % (j % 2)
                    u_, ku = up[j % 2], "up%d" % (j % 2)
                    for (t0, tn) in self.tok_groups():
                        psg, pkg = self.psum()
                        psu, pku = self.psum()
                        for k in range(16):
                            self.I(PE, "matmul", psg[:, 0:tn], w[:, k, 0:128], h2T[:, k, t0:t0 + tn], start=(k == 0), stop=(k == 15),
                                   reads=[wk, "h2T"], writes=[pkg], track=(k == 15))
                        for k in range(16):
                            self.I(PE, "matmul", psu[:, 0:tn], w[:, k, 128:256], h2T[:, k, t0:t0 + tn], start=(k == 0), stop=(k == 15),
                                   reads=[wk, "h2T"], writes=[pku], track=(k == 15))
                        self.I(ACT, "copy", g_[:, t0:t0 + tn], psg[:, 0:tn], reads=[pkg], writes=[kg])
                        self.I(DVE, "tensor_copy", u_[:, t0:t0 + tn], psu[:, 0:tn], reads=[pku], writes=[ku])
                    self.I(POOL, "tensor_scalar", g_[:, 30:32], g_[:, 30:32], pv[:, V_FLAG:V_FLAG + 1], None, ALU.mult, reads=[kg, "pv"], writes=[kg])
                    self.I(ACT, "activation", cv[:, 0:1024], g_[:, 32:1056], AF.Identity, bias=cwc(j, 3), scale=cwc(j, 2), reads=[kg, "pv"], writes=["cv"])
                    self.I(DVE, "scalar_tensor_tensor", cv[:, 0:1024], g_[:, 31:1055], cwc(j, 1), cv[:, 0:1024], ALU.mult, ALU.add, reads=[kg, "cv", "pv"], writes=["cv"])
                    self.I(DVE, "scalar_tensor_tensor", cv[:, 0:1024], g_[:, 30:1054], cwc(j, 0), cv[:, 0:1024], ALU.mult, ALU.add, reads=[kg, "cv", "pv"], writes=["cv"])
                    gss = g_[:, 1056:NOWN].rearrange("p (b t) -> p b t", t=4)
                    self.I(POOL, "tensor_copy", gs6[:, :, 0:2], chT[:, j, :].rearrange("p (b r) -> p b r", r=2), reads=["chT"], writes=["gs6"])
                    self.I(POOL, "tensor_copy", gs6[:, :, 2:6], gss, reads=[kg], writes=["gs6"])
                    cvs = cv[:, 1024:NA].rearrange("p (b t) -> p b t", t=4)
                    self.I(ACT, "activation", cvs, gs6[:, :, 2:6], AF.Identity, bias=cwc(j, 3), scale=cwc(j, 2), reads=["gs6", "pv"], writes=["cv"])
                    self.I(DVE, "scalar_tensor_tensor", cvs, gs6[:, :, 1:5], cwc(j, 1), cvs, ALU.mult, ALU.add, reads=["gs6", "cv", "pv"], writes=["cv"])
                    self.I(DVE, "scalar_tensor_tensor", cvs, gs6[:, :, 0:4], cwc(j, 0), cvs, ALU.mult, ALU.add, reads=["gs6", "cv", "pv"], writes=["cv"])
                    self.I(POOL, "tensor_copy", pco[:, j, :], g_[:, 1054:1056], reads=[kg], writes=["pco"])
                    self.I(POOL, "tensor_copy", sco[:, j, :, :], gss[:, :, 2:4], reads=[kg], writes=["sco"])
                    self.I(ACT, "activation", ge[:, :], cv[:, :], AF.Gelu_apprx_tanh, reads=["cv"], writes=["ge"])
                    self.I(DVE, "tensor_tensor", actT[:, j, 0:1024], ge[:, 0:1024], u_[:, 32:1056], ALU.mult, reads=["ge", ku], writes=["actT"])
                    self.I(POOL, "tensor_tensor", actT[:, j, 1024:NA], ge[:, 1024:NA], u_[:, 1056:NOWN], ALU.mult, reads=["ge", ku], writes=["actT"])
                scr = self.stage_out("pco", pco[:], [128, NJ, 2], "pco")
                for r in range(2):
                    self.bg(self.o_pconv[r].rearrange("(j p) -> p j", p=128), scr[:, :, r], "pco")
                scrow = sb1("scrow", [32, 1408])
                for pc in range(4):
                    for j4 in range(0, 11, 4):
                        nj = min(4, 11 - j4)
                        ps, pk = self.psum()
                        for jj in range(nj):
                            j = pc * 11 + j4 + jj
                            self.I(PE, "transpose", ps[0:32, jj * 128:(jj + 1) * 128], sco[:, j, :, :].rearrange("p b r -> p (b r)"), cf[:, C_ID:C_ID + 128],
                                   reads=["sco", "cf"], writes=[pk], track=(jj == nj - 1))
                        self.I(DVE, "tensor_copy", scrow[:, j4 * 128:(j4 + nj) * 128], ps[0:32, 0:nj * 128], reads=[pk], writes=["scrow"])
                    S.dma(SP, self.o_sconv.rearrange("b r c -> (b r) c")[:, pc * 1408:(pc + 1) * 1408], scrow[:], reads=["scrow"], ring="spo")
                S.barrier()
            with ExitStack() as es2:
                sb2 = lambda n, s, d=F32: self.sb(es2, n, s, d)
                dummy = sb2("dmy2", [128, 2])
                self.I(POOL, "memset", dummy[:], 0.0, writes=["h2T", "dmy2"])
                bxf = self.bufX[:].rearrange("p a b -> p (a b)")
                wo = [bxf[:, i * 5632:(i + 1) * 5632].rearrange("p (k n) -> p k n", n=512) for i in range(2)]
                fo = sb2("fo", [128, 5, D])
                gp2 = sb2("gp2", [128, D])
                S.dma(SP, gp2[:], self.gpost[:, 1, :], writes=["gp2"])
                x1t = sb2("x1r", [128, D])
                st = sb2("stc", [128, 5, 8])
                xb = sb2("xbc", [128, 512], BF16)
                tiles = [(128 * i, 128) for i in range(8)] + [(1024, NSM)]
                sets = [tiles[0:5], tiles[5:9]]
                wn = 0
                for tset in sets:
                    self.I(POOL, "memset", st[:], 0.0, writes=["stc"])
                    for n in range(4):
                        banks = [self.psum() for _ in tset]
                        for kp in range(4):
                            slot = wn % 2
                            wn += 1
                            S.dma(POOL, wo[slot], self.wfo[kp * 1408:(kp + 1) * 1408, n * 512:(n + 1) * 512].rearrange("(k p) n -> p k n", p=128),
                                  writes=["wo%d" % slot])
                            for ti, (c0, nt) in enumerate(tset):
                                ps, pk = banks[ti]
                                for jj in range(11):
                                    j = kp * 11 + jj
                                    self.I(PE, "matmul", ps[0:nt, :], actT[:, j, c0:c0 + nt], wo[slot][:, jj, :], start=(j == 0), stop=(j == NJ - 1),
                                           reads=["actT", "wo%d" % slot], writes=[pk], track=(jj == 10))
                        for ti, (c0, nt) in enumerate(tset):
                            ps, pk = banks[ti]
                            self.I(DVE, "tensor_copy", fo[0:nt, ti, n * 512:(n + 1) * 512], ps[0:nt, :], reads=[pk], writes=["fo%d" % ti])
                            self.I(ACT, "activation", xb[0:nt, :], fo[0:nt, ti, n * 512:(n + 1) * 512], AF.Square, accum_out=st[0:nt, ti, n:n + 1], reads=["fo%d" % ti], writes=["xbc", "stc"])
                    for ti, (c0, nt) in enumerate(tset):
                        self.I(DVE, "tensor_reduce", st[0:nt, ti, 4:5], st[0:nt, ti, 0:4], AX.X, ALU.add, reads=["stc"], writes=["stc"])
                        self.I(DVE, "tensor_scalar", st[0:nt, ti, 5:6], st[0:nt, ti, 4:5], 1.0 / D, 1e-6, ALU.mult, ALU.add, reads=["stc"], writes=["stc"])
                        self.I(ACT, "activation", st[0:nt, ti, 6:7], st[0:nt, ti, 5:6], AF.Sqrt, reads=["stc"], writes=["stc"])
                        self.I(DVE, "reciprocal", st[0:nt, ti, 7:8], st[0:nt, ti, 6:7], reads=["stc"], writes=["stc"])
                        xc0 = 32 + c0
                        S.dma(SP, x1t[0:nt, :], self.x1s[xc0:xc0 + nt, :], reads=["x1s"], writes=["x1r"])
                        self.I(DVE, "scalar_tensor_tensor", fo[0:nt, ti, :], fo[0:nt, ti, :], st[0:nt, ti, 7:8], gp2[0:nt, :], ALU.mult, ALU.mult,
                               reads=["fo%d" % ti, "stc", "gp2"], writes=["fo%d" % ti])
                        self.I(POOL, "tensor_tensor", fo[0:nt, ti, :], fo[0:nt, ti, :], x1t[0:nt, :], ALU.add, reads=["fo%d" % ti, "x1r"], writes=["fo%d" % ti])
                        S.dma(SP, self.o_y[c0:c0 + nt, :], fo[0:nt, ti, :], reads=["fo%d" % ti], ring="spo")

    def make_hT(self, xrows, ntok, hT, hkey, slot):
        S = self.S
        xt, xb, st = self.xt[slot], self.xb[slot], self.xst[slot]
        kx, kb, ks = "xt%d" % slot, "xb%d" % slot, "xst%d" % slot
        S.dma(SP, xt[0:ntok, :], xrows, writes=[kx])
        self.I(POOL, "memset", st[:, 0:1], 0.0, writes=[ks])
        self.I(ACT, "activation", xb[0:ntok, :], xt[0:ntok, :], AF.Square, accum_out=st[0:ntok, 0:1], reads=[kx], writes=[kb, ks])
        self.I(DVE, "tensor_scalar", st[0:ntok, 1:2], st[0:ntok, 0:1], 1.0 / D, 1e-6, ALU.mult, ALU.add, reads=[ks], writes=[ks])
        self.I(ACT, "activation", st[0:ntok, 2:3], st[0:ntok, 1:2], AF.Sqrt, reads=[ks], writes=[ks])
        self.I(DVE, "reciprocal", st[0:ntok, 3:4], st[0:ntok, 2:3], reads=[ks], writes=[ks])
        self.I(ACT, "activation", xb[0:ntok, :], xt[0:ntok, :], AF.Copy, scale=st[0:ntok, 3:4], reads=[kx, ks, kb], writes=[kb])
        for half in range(2):
            ps, pk = self.psum()
            pT = ps[:].bitcast(BF16).rearrange("p (a b) -> p a b", b=128)
            for k8 in range(8):
                kc = half * 8 + k8
                self.I(PE, "transpose", pT[:, k8, 0:ntok], xb[0:ntok, kc * 128:(kc + 1) * 128],
                                                                self.cb[0:ntok, 0:ntok], reads=[kb, "cb"], writes=[pk], track=(k8 == 7))
            gcol = self.pv[:, V_G1 + half * 8: V_G1 + half * 8 + 8].unsqueeze(2).to_broadcast([128, 8, ntok])
            self.I(DVE, "tensor_tensor", hT[:, half * 8:half * 8 + 8, :], pT[:, :, 0:ntok], gcol, ALU.mult, reads=[pk, "pv"], writes=[hkey])

    def slab_stream(self, slabs):
        S = self.S
        n = len(slabs)

        def issue(i):
            ap, ncol = slabs[i]
            slot = self.wslot % 2
            self.wslot += 1
            key = "wb%d" % slot
            S.dma(POOL, self.wb[slot][:, :, 0:ncol], ap.rearrange("(k p) n -> p k n", p=128), writes=[key])
            return (self.wb[slot], key)
        cur = issue(0)
        for i in range(n):
            nxt = issue(i + 1) if i + 1 < n else None
            yield cur
            cur = nxt

    def proj(self, ps, pk, wt, wkey, c0, M, hT, hkey, N):
        S = self.S
        for k in range(16):
            self.I(PE, "matmul", ps[0:M, 0:N], wt[:, k, c0:c0 + M], hT[:, k, 0:N], start=(k == 0), stop=(k == 15), reads=[wkey, hkey], writes=[pk], track=(k == 15))

    def stageS(self):
        S = self.S
        pv, cf, cb = self.pv, self.cf, self.cb
        wA = self.wA_ap
        N = NSM
        with ExitStack() as es0:
            sb0 = lambda n, s, d=F32: self.sb(es0, n, s, d)
            sbon = sb0("sbon", [128, 8, N], BF16)
            sgst = sb0("sgst", [128, 8, N], BF16)
            self._stageS_proj(es0, sbon, sgst)
            self._stageS_scan(es0, sbon, sgst)

    def _stageS_proj(self, es0, sbon, sgst):
        S = self.S
        pv, cf, cb = self.pv, self.cf, self.cb
        wA = self.wA_ap
        N = NSM
        with ExitStack() as es:
            sb = lambda n, s, d=F32: self.sb(es, n, s, d)
            self.xt = [sb("xtS", [128, D])]
            self.xb = [sb("xbS", [128, D], BF16)]
            self.xst = [sb("xstS", [128, 4])]
            hT = sb("hTs", [128, 16, N], BF16)
            self.make_hT(self.xsamp, N, hT[:, :, :], "hTs", 0)
            S.dma(SP, self.scrH[:, :, 1056:NOWN], hT[:, :, :], reads=["hTs"], writes=["scrH_s"])
            shT = sb("shT", [128, 28, 16])
            sst = sb("sst", [16, DS])
            S.dma(SP, sst[:], self.st_shift, writes=["sst"])
            chunks = [(0, 64), (64, 64), (128, 128), (256, 32)] + [(288 + 128 * i, 128) for i in range(24)]
            ps, pk = self.psum()
            for ci, (o, M) in enumerate(chunks):
                self.I(PE, "transpose", ps[0:M, ci * 16:(ci + 1) * 16], sst[:, o:o + M], cf[0:16, 0:16], reads=["sst", "cf"], writes=[pk], track=(ci == 27))
            self.I(DVE, "tensor_copy", shT[:, :, :], ps[:, 0:448].rearrange("p (a b) -> p a b", b=16), reads=[pk], writes=["shT"])
            zls = sb("zls", [128, 28, 16])
            tw = sb("stw", [64, N], BF16)
            zam = sb("szam", [64, N], BF16)
            sga = sb("ssga", [128, N], BF16)
            sgb = sb("ssgb", [128, N], BF16)
            self.I(POOL, "memset", sgb[:], 0.0, writes=["ssgb"])
            tokS = sb("tokS", [N, 6, 1024])
            phys = {n: sb("s_" + n, [128, N]) for n in ("zc", "d", "zr", "zk", "zv", "es", "wd", "as", "kk", "kkn", "km", "dd", "av")}
            phys["hi"] = sb("s_hi", [128, N], BF16)
            phys["lo"] = sb("s_lo", [128, N], BF16)
            T = _TAlias(phys, {"kk2": "d", "rn": "zc", "t1": "dd", "ta": "es", "rk": "wd2"}, prefix="s_")
            phys["wd2"] = sb("s_wd2", [128, N])

            def mix_s(ps, pk, M, ci, out, kout, mu):
                zc, d = T["zc"], T["d"]
                v3 = lambda t: t[0:M, 0:N].rearrange("p (b t) -> p b t", t=4)
                self.I(ACT, "copy", zc[0:M, 0:N], ps[0:M, 0:N], reads=[pk], writes=["s_zc"])
                self.I(DVE, "tensor_tensor", v3(d)[:, :, 1:4], v3(zc)[:, :, 0:3], v3(zc)[:, :, 1:4], ALU.subtract, reads=["s_zc"], writes=["s_d"])
                self.I(DVE, "tensor_tensor", v3(d)[:, :, 0], shT[0:M, ci, :], v3(zc)[:, :, 0], ALU.subtract, reads=["s_zc", "shT"], writes=["s_d"])
                self.I(DVE, "scalar_tensor_tensor", out[0:M, 0:N], d[0:M, 0:N], mu, zc[0:M, 0:N], ALU.mult, ALU.add, reads=["s_d", "s_zc", "pv"], writes=[kout])
                self.I(POOL, "tensor_copy", zls[0:M, ci, :], v3(zc)[:, :, 3], reads=["s_zc"], writes=["zls"])

            slabs = [(wA[:, 0:288], 288)] + [(wA[:, 288 + q * 384: 288 + (q + 1) * 384], 384) for q in range(8)]
            stream = self.slab_stream(slabs)
            wt, wk = next(stream)
            for (ci, (cc0, M, dst, dk, func)) in enumerate(((0, 64, tw, "stw", AF.Tanh), (64, 64, zam, "szam", AF.Copy),
                                                             (128, 128, sga, "ssga", AF.Sigmoid), (256, 32, sgb, "ssgb", AF.Sigmoid))):
                ps, pk = self.psum()
                self.proj(ps, pk, wt, wk, cc0, M, hT, "hTs", N)
                mix_s(ps, pk, M, ci, T["zr"], "s_zr", pv[0:M, ci:ci + 1])
                self.I(ACT, "activation", dst[0:M, 0:N], T["zr"][0:M, 0:N], func, reads=["s_zr"], writes=[dk])
            bonesb = cb[:, C_BONES:C_BONES + 128]
            for q in range(8):
                wt, wk = next(stream)
                col = lambda base: pv[:, base + q: base + q + 1]
                qs = slice(q * 128, (q + 1) * 128)
                for t, nm in enumerate(("zr", "zk", "zv")):
                    ps, pk = self.psum()
                    self.proj(ps, pk, wt, wk, t * 128, 128, hT, "hTs", N)
                    mix_s(ps, pk, 128, 4 + 3 * q + t, T[nm], "s_" + nm, pv[:, V_MU + 3 * q + t: V_MU + 3 * q + t + 1])
                zr, zk, zv = T["zr"], T["zk"], T["zv"]
                ps, pk = self.psum()
                self.I(PE, "matmul", ps[:, 0:N], self.w2b[:, qs], tw[:, 0:N], start=True, stop=True, reads=["w2b", "stw"], writes=[pk])
                self.I(ACT, "activation", T["es"][:, 0:N], ps[:, 0:N], AF.Sigmoid, bias=col(V_W0), reads=[pk, "pv"], writes=["s_es"])
                self.I(ACT, "activation", T["wd"][:, 0:N], T["es"][:, 0:N], AF.Exp, scale=-DECAY_C, reads=["s_es"], writes=["s_wd"])
                ps, pk = self.psum()
                self.I(PE, "matmul", ps[:, 0:N], self.a2b[:, qs], zam[:, 0:N], start=True, stop=True, reads=["a2b", "szam"], writes=[pk])
                self.I(ACT, "activation", T["as"][:, 0:N], ps[:, 0:N], AF.Sigmoid, bias=col(V_A0), reads=[pk, "pv"], writes=["s_as"])
                self.I(POOL, "tensor_scalar", T["kk"][:, 0:N], zk[:, 0:N], col(V_KK), None, ALU.mult, reads=["s_zk", "pv"], writes=["s_kk"])
                self.I(POOL, "tensor_tensor", T["kk2"][:, 0:N], T["kk"][:, 0:N], T["kk"][:, 0:N], ALU.mult, reads=["s_kk"], writes=["s_d"])
                ps, pk = self.psum()
                self.bsum(ps, pk, T, "kk2", N)
                self.I(ACT, "activation", T["rn"][:, 0:N], ps[:, 0:N], AF.Sqrt, reads=[pk], writes=["s_zc"])
                self.I(DVE, "tensor_scalar_max", T["rn"][:, 0:N], T["rn"][:, 0:N], 1e-12, reads=["s_zc"], writes=["s_zc"])
                self.I(DVE, "reciprocal", T["rn"][:, 0:N], T["rn"][:, 0:N], reads=["s_zc"], writes=["s_zc"])
                self.I(DVE, "tensor_tensor", T["kkn"][:, 0:N], T["kk"][:, 0:N], T["rn"][:, 0:N], ALU.mult, reads=["s_kk", "s_zc"], writes=["s_kkn"])
                self.I(POOL, "tensor_scalar", T["t1"][:, 0:N], T["as"][:, 0:N], -1.0, col(V_KA), ALU.add, ALU.mult, reads=["s_as", "pv"], writes=["s_dd"])
                self.I(POOL, "tensor_tensor", T["t1"][:, 0:N], T["t1"][:, 0:N], zk[:, 0:N], ALU.mult, reads=["s_dd", "s_zk"], writes=["s_dd"])
                self.I(POOL, "tensor_tensor", T["km"][:, 0:N], T["t1"][:, 0:N], zk[:, 0:N], ALU.add, reads=["s_dd", "s_zk"], writes=["s_km"])
                self.I(POOL, "tensor_tensor", T["ta"][:, 0:N], T["kkn"][:, 0:N], T["as"][:, 0:N], ALU.mult, reads=["s_kkn", "s_as"], writes=["s_es"])
                self.I(POOL, "tensor_scalar", T["av"][:, 0:N], T["kkn"][:, 0:N], -1.0, None, ALU.mult, reads=["s_kkn"], writes=["s_av"])
                self.I(DVE, "scalar_tensor_tensor", T["rk"][:, 0:N], zr[:, 0:N], col(V_RK), T["km"][:, 0:N], ALU.mult, ALU.mult, reads=["s_zr", "s_km", "pv"], writes=["s_wd2"])
                ps, pk = self.psum()
                self.bsum(ps, pk, T, "rk", N)
                self.I(DVE, "tensor_tensor", sbon[:, q, :], ps[:, 0:N], zv[:, 0:N], ALU.mult, reads=[pk, "s_zv"], writes=["sbon"])
                self.I(POOL, "tensor_scalar", sbon[:, q, :], sbon[:, q, :], col(V_LB), None, ALU.add, reads=["sbon", "pv"], writes=["sbon"])
                ps, pk = self.psum()
                self.I(PE, "matmul", ps[:, 0:N], self.g2a[:, qs], sga[:, 0:N], start=True, stop=False, reads=["g2a", "ssga"], writes=[pk], track=False)
                self.I(PE, "matmul", ps[:, 0:N], self.g2b[:, qs], sgb[:, 0:N], start=False, stop=True, reads=["g2b", "ssgb"], writes=[pk])
                self.I(ACT, "copy", sgst[:, q, :], ps[:, 0:N], reads=[pk], writes=["sgst"])
                srcs = [("zr", "s_zr"), ("km", "s_km"), ("zv", "s_zv"), ("wd", "s_wd"), ("av", "s_av"), ("ta", "s_es")]
                psA, pkA = self.psum()
                psB, pkB = self.psum()
                for qi, (nm, kk_) in enumerate(srcs):
                    pp, ppk = (psA, pkA) if qi < 4 else (psB, pkB)
                    o = (qi % 4) * 128
                    self.I(PE, "transpose", pp[0:N, o:o + 128], T[nm][:, 0:N], cf[:, C_ID:C_ID + 128], reads=[kk_, "cf"], writes=[ppk], track=(qi in (3, 5)))
                self.I(DVE, "tensor_copy", tokS[:, 0:4, qs], psA[0:N, :].rearrange("p (a b) -> p a b", b=128), reads=[pkA], writes=["tokS"])
                self.I(ACT, "copy", tokS[:, 4:6, qs], psB[0:N, 0:256].rearrange("p (a b) -> p a b", b=128), reads=[pkB], writes=["tokS"])
            zrow = sb("zrow", [16, DS])
            for c0_ in range(0, 28, 4):
                ps, pk = self.psum()
                grp = list(enumerate(chunks))[c0_:c0_ + 4]
                for n_, (ci, (o, M)) in enumerate(grp):
                    self.I(PE, "transpose", ps[0:16, n_ * 128:n_ * 128 + M], zls[0:M, ci, :], cf[0:M, 0:M], reads=["zls", "cf"], writes=[pk], track=(n_ == len(grp) - 1))
                for n_, (ci, (o, M)) in enumerate(grp):
                    self.I(DVE, "tensor_copy", zrow[:, o:o + M], ps[0:16, n_ * 128:n_ * 128 + M], reads=[pk], writes=["zrow"])
            S.dma(SP, self.o_sshift, zrow[:], reads=["zrow"], ring="spo")
            S.dma(SP, self.scrS, tokS[:], reads=["tokS"], writes=["scrS"])
            S.barrier()

    def _stageS_scan(self, es0, sbon, sgst):
        S = self.S
        pv, cf, cb = self.pv, self.cf, self.cb
        N = NSM
        with ExitStack() as es:
            sb = lambda n, s, d=F32: self.sb(es, n, s, d)
            ytok = sb("ytok", [N, 1024])
            for bt in range(2):
                E = DVE
                kS, kq, kt, ky = "Sst%d" % bt, "qin%d" % bt, "stmp%d" % bt, "ysb%d" % bt
                Sst = sb(kS, [128, 64, 64])
                tmp = sb(kt, [128, 64, 64])
                qin = sb(kq, [128, 4, 6, 64])
                ysb = sb(ky, [128, 4, 64])
                sa = sb("sa%d" % bt, [128, 64])
                yq = sb("syq%d" % bt, [128, 4, 64])
                sst_ = sb("sstat%d" % bt, [128, 8, 4])
                S.dma(POOL, Sst[:].rearrange("p v k -> p (v k)"), self.st_wkv[bt * 128:(bt + 1) * 128, :], writes=[kS])
                for bl in range(8):
                    b = bt * 8 + bl
                    S.dma(SP, qin[16 * bl:16 * bl + 16, :, :, :], self.scrS[4 * b:4 * b + 4, :, :].rearrange("t q (h k) -> h t q k", k=64),
                          reads=["scrS"], writes=[kq + "_%d" % bl])
                bc_k = lambda t, qi: qin[:, t, qi, :].unsqueeze(1).to_broadcast([128, 64, 64])
                kqs = [kq + "_%d" % bl for bl in range(8)]
                for t in range(4):
                    self.I(E, "tensor_tensor", tmp[:], Sst[:], bc_k(t, 4), ALU.mult, reads=[kS] + kqs, writes=[kt])
                    self.I(DVE, "tensor_reduce", sa[:], tmp[:], AX.X, ALU.add, reads=[kt], writes=["sa%d" % bt])
                    self.I(E, "tensor_tensor", Sst[:], Sst[:], bc_k(t, 3), ALU.mult, reads=[kS] + kqs, writes=[kS])
                    self.I(E, "tensor_tensor", tmp[:], sa[:].unsqueeze(2).to_broadcast([128, 64, 64]), bc_k(t, 5), ALU.mult, reads=["sa%d" % bt] + kqs, writes=[kt])
                    self.I(E, "tensor_tensor", Sst[:], Sst[:], tmp[:], ALU.add, reads=[kS, kt], writes=[kS])
                    self.I(E, "tensor_tensor", tmp[:], qin[:, t, 2, :].unsqueeze(2).to_broadcast([128, 64, 64]), bc_k(t, 1), ALU.mult, reads=kqs, writes=[kt])
                    self.I(E, "tensor_tensor", Sst[:], Sst[:], tmp[:], ALU.add, reads=[kS, kt], writes=[kS])
                    self.I(E, "tensor_tensor", tmp[:], Sst[:], bc_k(t, 0), ALU.mult, reads=[kS] + kqs, writes=[kt])
                    self.I(DVE, "tensor_reduce", ysb[:, t, :], tmp[:], AX.X, ALU.add, reads=[kt], writes=[ky])
                S.dma(SP, self.o_swkv[bt * 128:(bt + 1) * 128, :], Sst[:].rearrange("p v k -> p (v k)"), reads=[kS], ring="spo")
                st_ = sst_
                ks_ = "sstat%d" % bt
                self.I(DVE, "tensor_reduce", st_[:, 0, :], ysb[:], AX.X, ALU.add, reads=[ky], writes=[ks_])
                self.I(E, "tensor_tensor", yq[:], ysb[:], ysb[:], ALU.mult, reads=[ky], writes=["syq%d" % bt])
                self.I(DVE, "tensor_reduce", st_[:, 1, :], yq[:], AX.X, ALU.add, reads=["syq%d" % bt], writes=[ks_])
                self.I(E, "tensor_scalar", st_[:, 2, :], st_[:, 0, :], 1.0 / 64, None, ALU.mult, reads=[ks_], writes=[ks_])
                self.I(E, "tensor_tensor", st_[:, 3, :], st_[:, 2, :], st_[:, 2, :], ALU.mult, reads=[ks_], writes=[ks_])
                self.I(E, "tensor_scalar", st_[:, 4, :], st_[:, 1, :], 1.0 / 64, 64e-5, ALU.mult, ALU.add, reads=[ks_], writes=[ks_])
                self.I(E, "tensor_tensor", st_[:, 4, :], st_[:, 4, :], st_[:, 3, :], ALU.subtract, reads=[ks_], writes=[ks_])
                self.I(ACT, "activation", st_[:, 5, :], st_[:, 4, :], AF.Sqrt, reads=[ks_], writes=[ks_])
                self.I(DVE, "reciprocal", st_[:, 6, :], st_[:, 5, :], reads=[ks_], writes=[ks_])
                self.I(E, "tensor_tensor", yq[:], ysb[:], st_[:, 2, :].unsqueeze(2).to_broadcast([128, 4, 64]), ALU.subtract, reads=[ky, ks_], writes=["syq%d" % bt])
                self.I(E, "tensor_tensor", yq[:], yq[:], st_[:, 6, :].unsqueeze(2).to_broadcast([128, 4, 64]), ALU.mult, reads=["syq%d" % bt, ks_], writes=["syq%d" % bt])
                for bl in range(8):
                    b = bt * 8 + bl
                    S.dma(SP, self.scrY[4 * b:4 * b + 4, :].rearrange("t (h v) -> h t v", v=64), yq[16 * bl:16 * bl + 16, :, :], reads=["syq%d" % bt], writes=["scrY%d" % b])
            S.dma(SP, ytok[:], self.scrY, reads=["scrY%d" % b for b in range(16)], writes=["ytok"])
            if "sdbg" in self.dbg:
                S.dma(SP, self.dout("d_ytok", [N, 1024]), ytok[:], reads=["ytok"], ring="spo")
                dbf = sb("dbf", [128, 2, 8, N])
                self.I(DVE, "tensor_copy", dbf[:, 0], sbon[:], reads=["sbon"], writes=["dbf"])
                self.I(DVE, "tensor_copy", dbf[:, 1], sgst[:], reads=["sgst"], writes=["dbf"])
                S.dma(SP, self.dout("d_sbg", [128, 2, 8, N]), dbf[:], reads=["dbf"], ring="spo")
            ytb = sb("ytb", [N, 1024], BF16)
            self.I(ACT, "copy", ytb[:], ytok[:], reads=["ytok"], writes=["ytb"])
            ps, pk = self.psum()
            pT = ps[:].bitcast(BF16).rearrange("p (a b) -> p a b", b=128)[:, :, 0:N]
            for q in range(8):
                self.I(PE, "transpose", pT[:, q, :], ytb[:, q * 128:(q + 1) * 128], cb[0:N, 0:N], reads=["ytb", "cb"], writes=[pk], track=(q == 7))
            yt = sb("syt", [128, 8, N])
            lw = pv[:, V_LW: V_LW + 8].unsqueeze(2).to_broadcast([128, 8, N])
            self.I(DVE, "tensor_tensor", yt[:], pT, lw, ALU.mult, reads=[pk, "pv"], writes=["syt"])
            self.I(POOL, "tensor_tensor", yt[:], yt[:], sbon[:], ALU.add, reads=["syt", "sbon"], writes=["syt"])
            self.I(DVE, "tensor_tensor", self.yaT[:, :, 1056:NOWN], yt[:], sgst[:], ALU.mult, reads=["syt", "sgst"], writes=["yaT"])
            S.barrier()

    def stageA(self, es0, xseq, wA):
        S = self.S
        pv, cf, cb = self.pv, self.cf, self.cb
        with ExitStack() as es:
            sb = lambda n, s, d=F32: self.sb(es, n, s, d)
            self.xt = [sb("xt%d" % i, [128, D]) for i in range(1)]
            self.xb = [sb("xb%d" % i, [128, D], BF16) for i in range(1)]
            self.xst = [sb("xst%d" % i, [128, 4]) for i in range(1)]
            hT = sb("hTb", [128, 16, BLK], BF16)
            zl = sb("zl", [128, 28])
            self.I(POOL, "memset", zl[:], 0.0, writes=["zl"])
            tw = sb("tw", [64, BLK], BF16)
            zam = sb("zam", [64, BLK], BF16)
            sga = sb("sga", [128, BLK], BF16)
            sgb = sb("sgb", [128, BLK], BF16)
            self.I(POOL, "memset", sgb[:], 0.0, writes=["sgb"])
            NCK = BLK // CH
            ARBD = sb("ARBD", [128, 4, NCK, 2, 128], BF16)
            BBD = sb("BBD", [128, 4, NCK, 128], BF16)
            KBD = sb("KBD", [128, 4, NCK, 128], BF16)
            VBD = sb("VBD", [128, 4, NCK, 128], BF16)
            for (t, k) in ((ARBD, "ARBD"), (BBD, "BBD"), (KBD, "KBD"), (VBD, "VBD")):
                self.I(POOL, "memset", t[:], 0.0, writes=[k + "%d" % q for q in range(4)])
            PCs = sb("PCs", [128, 4, NCK])
            bon = sb("bon", [128, 4, BLK], BF16)
            gst = sb("gst", [128, 4, BLK], BF16)
            Hf = [sb("Hf%d" % g, [128, 4, 64]) for g in range(2)]
            Hb = [sb("Hb%d" % g, [128, 4, 64], BF16) for g in range(2)]
            for g in range(2):
                self.I(POOL, "memset", Hf[g][:], 0.0, writes=["Hf%d" % g])
                self.I(POOL, "memset", Hb[g][:], 0.0, writes=["Hb%d" % g])
            names = ("zc", "d", "zr", "zk", "zv", "es", "cum", "dd", "pinv", "prr", "pa", "as", "kk", "kkn", "km")
            phys = {n: sb("t_" + n, [128, NH]) for n in names[:8]}
            spare = self.bufX[:, 8:16, :].rearrange("p a b -> p (a b)")
            sparef = spare.bitcast(F32)
            for i, n in enumerate(names[8:]):
                phys[n] = sparef[:, i * NH:(i + 1) * NH]
            phys["hi"] = spare[:, 7 * 2 * NH: 7 * 2 * NH + NH]
            phys["lo"] = spare[:, 7 * 2 * NH + NH: 7 * 2 * NH + 2 * NH]
            T = _TAlias(phys, {"kk2": "d", "rn": "zc", "t1": "dd", "ta": "es", "rk": "cum"})
            sc = {}
            for s in range(2):
                sc["NA1", s] = sb("NA1_%d" % s, [128, 4, 256], BF16)
                sc["NA2", s] = sb("NA2_%d" % s, [128, 4, 256], BF16)
                for i in range(2):
                    sc["Q", s, i] = sb("Q_%d%d" % (s, i), [128, 4, 128], BF16)
                    if s == 0:
                        sc["NL", s, i] = sb("NL_%d%d" % (s, i), [128, 4, 128], BF16)
                        sc["TU", s, i] = sb("TU_%d%d" % (s, i), [128, 4, 128], BF16)
                sc["B2", s] = sb("B2_%d" % s, [128, 4, 128], BF16)
                sc["K2", s] = sb("K2_%d" % s, [128, 4, 128], BF16)
                sc["V2", s] = sb("V2_%d" % s, [128, 4, 64], BF16)
                if s == 0:
                    sc["X2", s] = sb("X2_%d" % s, [128, 4, 64], BF16)
                    sc["U2", s] = sb("U2_%d" % s, [128, 4, 64], BF16)
                    sc["YBD", s] = sb("YBD_%d" % s, [128, 4, 128], BF16)
                    self.I(POOL, "memset", sc["YBD", s][:], 0.0, writes=["YBD_%d" % s])
                    sc["ys", s] = sb("ys_%d" % s, [128, 4, 64])
                    sc["yq", s] = sb("yq_%d" % s, [128, 4, 64])
                    sc["st", s] = sb("yst_%d" % s, [128, 8, 4])
                    sc["yt", s] = sb("yt_%d" % s, [128, 4, 64])
                else:
                    for nm in ("X2", "U2", "YBD", "ys", "yq", "st", "yt"):
                        sc[nm, s] = sc[nm, 0]
            tmpH = sb("tmpH", [128, 4, 64])

            nblk = NPR // BLK
            lr_slab = (wA[:, 0:288], 288)
            pair_slabs = [(wA[:, 288 + q * 384: 288 + (q + 1) * 384], 384) for q in range(8)]
            slabs = []
            for blk in range(nblk):
                slabs.append(lr_slab)
                slabs += pair_slabs
            stream = self.slab_stream(slabs)

            for blk in range(nblk):
                c0 = blk * BLK
                N = BLK
                for t4 in range(4):
                    self.make_hT(xseq[c0 + t4 * 128: c0 + (t4 + 1) * 128, :], 128, hT[:, :, t4 * 128:(t4 + 1) * 128], "hTb", 0)
                self.ckpt("hT")
                if blk == 1:
                    S.dma(SP, self.scrH[:, :, 0:32], hT[:, :, BLK - 32:BLK], reads=["hTb"], writes=["scrH_1"])
                elif blk >= 2:
                    S.dma(SP, self.scrH[:, :, 32 + (blk - 2) * BLK: 32 + (blk - 1) * BLK], hT[:, :, :], reads=["hTb"], writes=["scrH_%d" % blk])
                wt, wk = next(stream)
                for (ci, (cc0, M, dst, dk, func)) in enumerate(((0, 64, tw, "tw", AF.Tanh), (64, 64, zam, "zam", AF.Copy),
                                                                 (128, 128, sga, "sga", AF.Sigmoid), (256, 32, sgb, "sgb", AF.Sigmoid))):
                    ps, pk = self.psum()
                    self.proj(ps, pk, wt, wk, cc0, M, hT, "hTb", N)
                    for co in range(0, N, NH):
                        self.mix(ps[:, co:co + NH], pk, M, NH, ci, T, zl)
                        self.I(ACT, "activation", dst[0:M, co:co + NH], T["zr"][0:M, 0:NH], func, reads=["t_zr"], writes=[dk])
                self.ckpt("lowrank")
                for g in range(2):
                    for qg in range(4):
                        q = g * 4 + qg
                        wt, wk = next(stream)
                        self.prep_pair(wt, wk, hT, N, q, qg, T, zl, tw, zam, sga, sgb, ARBD, BBD, KBD, VBD, PCs, bon, gst, blk)
                        self.ckpt("prep0")
                    self.ckpt("prep")
                    self.scan_group(g, blk, NCK, sc, ARBD, BBD, KBD, VBD, PCs, bon, gst, Hf[g], Hb[g], tmpH)
            scr = self.stage_out("zl", zl[:], [128, 28], "zl")
            self.bg(self.o_pshift[0:64].rearrange("(p o) -> p o", o=1), scr[0:64, 0:1], "zl")
            self.bg(self.o_pshift[64:128].rearrange("(p o) -> p o", o=1), scr[0:64, 1:2], "zl")
            self.bg(self.o_pshift[128:256].rearrange("(p o) -> p o", o=1), scr[:, 2:3], "zl")
            self.bg(self.o_pshift[256:288].rearrange("(p o) -> p o", o=1), scr[0:32, 3:4], "zl")
            self.bg(self.o_pshift[288:DS].rearrange("(j p) -> p j", p=128), scr[:, 4:28], "zl")
            for g in range(2):
                hh = sb("hh%d" % g, [128, 4, 64], BF16)
                hl = sb("hl%d" % g, [128, 4, 64], BF16)
                self.I(POOL, "tensor_copy", hh[:], Hf[g][:], reads=["Hf%d" % g], writes=["hh%d" % g])
                self.I(POOL, "tensor_tensor", hl[:], Hf[g][:], hh[:], ALU.subtract, reads=["Hf%d" % g, "hh%d" % g], writes=["hl%d" % g])
                ps, pk = self.psum()
                for qg in range(4):
                    self.I(PE, "matmul", ps[0:64, qg * 128:(qg + 1) * 128], hh[:, qg, :], cb[:, 0:128], start=True, stop=False, reads=["hh%d" % g, "cb"], writes=[pk], track=False)
                    self.I(PE, "matmul", ps[0:64, qg * 128:(qg + 1) * 128], hl[:, qg, :], cb[:, 0:128], start=False, stop=True, reads=["hl%d" % g, "cb"], writes=[pk], track=(qg == 3))
                so = sb("so%d" % g, [64, 4, 2, 64])
                self.I(DVE, "tensor_copy", so[:], ps[0:64, :].rearrange("p (a b c) -> p a b c", a=4, b=2), reads=[pk], writes=["so%d" % g])
                S.dma(SP, self.o_pwkv[g * 8:(g + 1) * 8].rearrange("(q s) v k -> v q s k", s=2), so[:], reads=["so%d" % g], ring="spo")
            S.barrier()

    def mix(self, ps, pk, M, N, ci, T, zl):
        S = self.S
        zc, d, out = T["zc"], T["d"], T["zr"]
        self.mix_to(ps, pk, M, N, ci, zl, zc, "t_zc", d, "t_d", out, "t_zr", self.pv[0:M, ci:ci + 1] if ci < 4 else None)

    def mix_to(self, ps, pk, M, N, ci, zl, zc, kzc, d, kd, out, kout, mu):
        S = self.S
        self.I(ACT, "copy", zc[0:M, 0:N], ps[0:M, 0:N], reads=[pk], writes=[kzc])
        self.I(DVE, "tensor_tensor", d[0:M, 1:N], zc[0:M, 0:N - 1], zc[0:M, 1:N], ALU.subtract, reads=[kzc], writes=[kd])
        self.I(DVE, "tensor_tensor", d[0:M, 0:1], zl[0:M, ci:ci + 1], zc[0:M, 0:1], ALU.subtract, reads=[kzc, "zl"], writes=[kd])
        self.I(DVE, "scalar_tensor_tensor", out[0:M, 0:N], d[0:M, 0:N], mu, zc[0:M, 0:N], ALU.mult, ALU.add, reads=[kd, kzc, "pv"], writes=[kout])
        self.I(POOL, "tensor_copy", zl[0:M, ci:ci + 1], zc[0:M, N - 1:N], reads=[kzc], writes=["zl"])

    def bsum(self, ps, pk, T, name, N):
        S = self.S
        src, ksrc = T[name], T.key(name)
        bonesb = self.cb[:, C_BONES:C_BONES + 128]
        khi, klo = T.key("hi"), T.key("lo")
        self.I(ACT, "copy", T["hi"][:, 0:N], src[:, 0:N], reads=[ksrc], writes=[khi])
        self.I(POOL, "tensor_tensor", T["lo"][:, 0:N], src[:, 0:N], T["hi"][:, 0:N], ALU.subtract, reads=[ksrc, khi], writes=[klo])
        self.I(PE, "matmul", ps[:, 0:N], bonesb, T["hi"][:, 0:N], start=True, stop=False, reads=["cb", khi], writes=[pk], track=False)
        self.I(PE, "matmul", ps[:, 0:N], bonesb, T["lo"][:, 0:N], start=False, stop=True, reads=["cb", klo], writes=[pk])

    def prep_pair(self, wt, wk, hT, N, q, qg, T, zl, tw, zam, sga, sgb, ARBD, BBD, KBD, VBD, PCs, bon, gst, blk):
        S = self.S
        pv, cf = self.pv, self.cf
        col = lambda base: pv[:, base + q: base + q + 1]
        qs = slice(q * 128, (q + 1) * 128)
        need_y = blk >= 1
        W = N
        co = 0
        cw = slice(0, W)
        psw, pkw = self.psum()
        self.I(PE, "matmul", psw[:, 0:W], self.w2b[:, qs], tw[:, cw], start=True, stop=True, reads=["w2b", "tw"], writes=[pkw])
        psa, pka = self.psum()
        self.I(PE, "matmul", psa[:, 0:W], self.a2b[:, qs], zam[:, cw], start=True, stop=True, reads=["a2b", "zam"], writes=[pka])
        pj = []
        for t in range(3):
            if t == 0 and not need_y:
                pj.append(None)
                continue
            ps, pk = self.psum()
            self.proj(ps, pk, wt, wk, t * 128, 128, hT, "hTb", N)
            pj.append((ps, pk))
        self.I(ACT, "activation", T["es"][:, 0:W], psw[:, 0:W], AF.Sigmoid, bias=col(V_W0), reads=[pkw, "pv"], writes=["t_es"])
        self.I(ACT, "activation", T["as"][:, 0:W], psa[:, 0:W], AF.Sigmoid, bias=col(V_A0), reads=[pka, "pv"], writes=["t_as"])
        self.I(DVE, "tensor_tensor_scan", T["cum"][:, 0:W], cf[:, C_M01:C_M01 + W], T["es"][:, 0:W], 0.0, ALU.mult, ALU.add, reads=["t_es", "cf"], writes=["t_cum"])
        self.I(POOL, "tensor_tensor", T["dd"][:, 0:W], T["cum"][:, 0:W], T["es"][:, 0:W], ALU.subtract, reads=["t_cum", "t_es"], writes=["t_dd"])
        nck = W // CH
        ck0 = 0

        def mixq(t, nm):
            ps, pk = pj[t]
            ci = 4 + 3 * q + t
            self.mix_to(ps[:, cw], pk, 128, W, ci, zl, T["zc"], "t_zc", T["d"], "t_d", T[nm], "t_" + nm, pv[:, V_MU + 3 * q + t: V_MU + 3 * q + t + 1])
        if need_y:
            mixq(0, "zr")
        self.I(ACT, "activation", T["pinv"][:, 0:W], T["cum"][:, 0:W], AF.Exp, scale=DECAY_C, reads=["t_cum"], writes=["t_pinv"])
        self.I(ACT, "activation", T["prr"][:, 0:W], T["cum"][:, 0:W], AF.Exp, scale=-DECAY_C, reads=["t_cum"], writes=["t_prr"])
        self.I(ACT, "activation", T["pa"][:, 0:W], T["dd"][:, 0:W], AF.Exp, scale=-DECAY_C, reads=["t_dd"], writes=["t_pa"])
        self.I(POOL, "tensor_copy", PCs[:, qg, ck0:ck0 + nck], T["prr"][:, 0:W].rearrange("p (c t) -> p c t", t=CH)[:, :, CH - 1], reads=["t_prr"], writes=["PCs%d" % qg])
        mixq(1, "zk")
        mixq(2, "zv")
        if True:
            zr, zk, zv = T["zr"], T["zk"], T["zv"]
            self.I(ACT, "activation", T["kk"][:, 0:W], zk[:, 0:W], AF.Copy, scale=col(V_KK), reads=["t_zk", "pv"], writes=["t_kk"])
            bonesb = self.cb[:, C_BONES:C_BONES + 128]
            self.I(ACT, "activation", T["hi"][:, 0:W], T["kk"][:, 0:W], AF.Square, reads=["t_kk"], writes=["t_hi"])
            ps, pk = self.psum()
            self.I(PE, "matmul", ps[:, 0:W], bonesb, T["hi"][:, 0:W], start=True, stop=True, reads=["cb", "t_hi"], writes=[pk])
            self.I(ACT, "activation", T["rn"][:, 0:W], ps[:, 0:W], AF.Sqrt, reads=[pk], writes=["t_zc"])
            self.I(DVE, "tensor_scalar_max", T["rn"][:, 0:W], T["rn"][:, 0:W], 1e-12, reads=["t_zc"], writes=["t_zc"])
            self.I(DVE, "reciprocal", T["rn"][:, 0:W], T["rn"][:, 0:W], reads=["t_zc"], writes=["t_zc"])
            self.I(DVE, "tensor_tensor", T["kkn"][:, 0:W], T["kk"][:, 0:W], T["rn"][:, 0:W], ALU.mult, reads=["t_kk", "t_zc"], writes=["t_kkn"])
            self.I(POOL, "tensor_scalar", T["t1"][:, 0:W], T["as"][:, 0:W], -1.0, col(V_KA), ALU.add, ALU.mult, reads=["t_as", "pv"], writes=["t_dd"])
            self.I(POOL, "tensor_tensor", T["t1"][:, 0:W], T["t1"][:, 0:W], zk[:, 0:W], ALU.mult, reads=["t_dd", "t_zk"], writes=["t_dd"])
            self.I(POOL, "tensor_tensor", T["km"][:, 0:W], T["t1"][:, 0:W], zk[:, 0:W], ALU.add, reads=["t_dd", "t_zk"], writes=["t_km"])
            self.I(POOL, "tensor_tensor", T["ta"][:, 0:W], T["kkn"][:, 0:W], T["as"][:, 0:W], ALU.mult, reads=["t_kkn", "t_as"], writes=["t_es"])
            if need_y:
                self.I(DVE, "scalar_tensor_tensor", T["lo"][:, 0:W], zr[:, 0:W], col(V_RK), T["km"][:, 0:W], ALU.mult, ALU.mult, reads=["t_zr", "t_km", "pv"], writes=["t_lo"])
                ps, pk = self.psum()
                self.I(PE, "matmul", ps[:, 0:W], bonesb, T["lo"][:, 0:W], start=True, stop=True, reads=["cb", "t_lo"], writes=[pk])
                self.I(DVE, "tensor_tensor", bon[:, qg, cw], ps[:, 0:W], zv[:, 0:W], ALU.mult, reads=[pk, "t_zv"], writes=["bon%d" % qg])
                self.I(ACT, "activation", bon[:, qg, cw], bon[:, qg, cw], AF.Identity, bias=col(V_LB), reads=["bon%d" % qg, "pv"], writes=["bon%d" % qg])
                ps, pk = self.psum()
                self.I(PE, "matmul", ps[:, 0:W], self.g2a[:, qs], sga[:, cw], start=True, stop=False, reads=["g2a", "sga"], writes=[pk], track=False)
                self.I(PE, "matmul", ps[:, 0:W], self.g2b[:, qs], sgb[:, cw], start=False, stop=True, reads=["g2b", "sgb"], writes=[pk])
                self.I(ACT, "copy", gst[:, qg, cw], ps[:, 0:W], reads=[pk], writes=["gst%d" % qg])
            for h in range(2):
                pr = slice(64 * h, 64 * h + 64)
                cs = slice(64 * h, 64 * h + 64)
                v3 = lambda t: t[pr, 0:W].rearrange("p (c t) -> p c t", t=CH)
                e1 = DVE if h == 0 else POOL
                cks = slice(ck0, ck0 + nck)
                self.I(DVE, "scalar_tensor_tensor", ARBD[pr, qg, cks, 0, cs], v3(T["kkn"]), -1.0, v3(T["pa"]), ALU.mult, ALU.mult, reads=["t_kkn", "t_pa"], writes=["ARBD%d" % qg])
                if need_y:
                    self.I(e1, "tensor_tensor", ARBD[pr, qg, cks, 1, cs], v3(zr), v3(T["prr"]), ALU.mult, reads=["t_zr", "t_prr"], writes=["ARBD%d" % qg])
                self.I(e1, "tensor_tensor", KBD[pr, qg, cks, cs], v3(T["km"]), v3(T["pinv"]), ALU.mult, reads=["t_km", "t_pinv"], writes=["KBD%d" % qg])
                self.I(e1, "tensor_tensor", BBD[pr, qg, cks, cs], v3(T["ta"]), v3(T["pinv"]), ALU.mult, reads=["t_es", "t_pinv"], writes=["BBD%d" % qg])
                self.I(ACT, "copy", VBD[pr, qg, cks, cs], v3(zv), reads=["t_zv"], writes=["VBD%d" % qg])

    def scan_pre(self, g, c, gc, sc, ARBD, BBD, KBD, VBD):
        cf, cb = self.cf, self.cb
        s = gc % 2
        kin = ["ARBD%d" % q for q in range(4)] + ["BBD%d" % q for q in range(4)] + ["KBD%d" % q for q in range(4)] + ["VBD%d" % q for q in range(4)]
        NA1, NA2 = sc["NA1", s], sc["NA2", s]
        kNA1, kNA2 = "NA1_%d" % s, "NA2_%d" % s
        identb = cb[:, 0:128]
        iselb = cb[:, C_ISEL:C_ISEL + 64]
        mask2 = cf[:, C_MUS:C_MUS + 256].unsqueeze(1).to_broadcast([128, 2, 256])
        for (dst, kd, L) in ((NA1, kNA1, BBD), (NA2, kNA2, KBD)):
            for hf in range(2):
                ps, pk = self.psum()
                for j in range(2):
                    q = 2 * hf + j
                    self.I(PE, "matmul", ps[:, j * 256:(j + 1) * 256], L[:, q, c, :], ARBD[:, q, c, :, :].rearrange("p a b -> p (a b)"),
                           start=True, stop=True, reads=kin, writes=[pk], track=(j == 1))
                self.I(DVE, "tensor_tensor", dst[:, 2 * hf:2 * hf + 2, :], ps[:].rearrange("p (a b) -> p a b", b=256), mask2, ALU.mult, reads=[pk, "cf"], writes=[kd])
                yield
        NL0, kNL0 = sc["NL", 0, 0], "NL_00"
        ps, pk = self.psum()
        for q in range(4):
            self.I(PE, "matmul", ps[:, q * 128:(q + 1) * 128], ARBD[:, q, c, 0, :], BBD[:, q, c, :], start=True, stop=True, reads=kin, writes=[pk], track=(q == 3))
        mL = cf[:, C_MLS:C_MLS + 128].unsqueeze(1).to_broadcast([128, 4, 128])
        self.I(DVE, "tensor_tensor", NL0[:], ps[:].rearrange("p (a b) -> p a b", b=128), mL, ALU.mult, reads=[pk, "cf"], writes=[kNL0])
        yield
        B2, K2, V2 = sc["B2", s], sc["K2", s], sc["V2", s]
        for (dst, kd, L) in ((B2, "B2_%d" % s, BBD), (K2, "K2_%d" % s, KBD)):
            ps, pk = self.psum()
            for q in range(4):
                self.I(PE, "matmul", ps[:, q * 128:(q + 1) * 128], L[:, q, c, :], identb, start=True, stop=True, reads=kin + ["cb"], writes=[pk], track=(q == 3))
            self.I(ACT, "copy", dst[:], ps[:].rearrange("p (a b) -> p a b", b=128), reads=[pk], writes=[kd])
            yield
        ps, pk = self.psum()
        for q in range(4):
            self.I(PE, "matmul", ps[:, q * 64:(q + 1) * 64], VBD[:, q, c, :], iselb, start=True, stop=True, reads=kin + ["cb"], writes=[pk], track=(q == 3))
        self.I(ACT, "copy", V2[:], ps[:, 0:256].rearrange("p (a b) -> p a b", b=64), reads=[pk], writes=["V2_%d" % s])
        yield
        TUc, kTU = NA1[:, :, 0:128], [kNA1]
        NLc, kNL = NL0, kNL0
        Qc, kQ = sc["Q", s, 0], "Q_%d0" % s
        idb4 = identb.unsqueeze(1).to_broadcast([128, 4, 128])
        self.I(POOL, "tensor_tensor", Qc[:], NA1[:, :, 0:128], idb4, ALU.add, reads=[kNA1, "cb"], writes=[kQ])
        for lvl in range(1, 6):
            i = lvl % 2
            NLn, kNLn = sc["NL", 0, i], "NL_0%d" % i
            TUn = sc["TU", 0, i]
            kTUn = ["TU_0%da" % i, "TU_0%db" % i]
            Qn, kQn = sc["Q", s, i], "Q_%d%d" % (s, i)
            psN, pkN = self.psum()
            for q in range(4):
                self.I(PE, "matmul", psN[:, q * 128:(q + 1) * 128], TUc[:, q, :], NLc[:, q, :], start=True, stop=True, reads=kTU + [kNL], writes=[pkN], track=(q == 3))
            if lvl < 5:
                psTa, pkTa = self.psum()
                psTb, pkTb = self.psum()
                for q in range(4):
                    pT_, pkT_ = (psTa, pkTa) if q < 2 else (psTb, pkTb)
                    self.I(PE, "matmul", pT_[:, (q % 2) * 128:(q % 2 + 1) * 128], NLc[:, q, :], TUc[:, q, :], start=True, stop=True, reads=kTU + [kNL], writes=[pkT_], track=(q % 2 == 1))
            self.I(ACT, "copy", NLn[:], psN[:].rearrange("p (a b) -> p a b", b=128), reads=[pkN], writes=[kNLn])
            if lvl < 5:
                self.I(ACT, "copy", TUn[:, 0:2, :], psTa[:, 0:256].rearrange("p (a b) -> p a b", b=128), reads=[pkTa], writes=[kTUn[0]])
                self.I(DVE, "tensor_copy", TUn[:, 2:4, :], psTb[:, 0:256].rearrange("p (a b) -> p a b", b=128), reads=[pkTb], writes=[kTUn[1]])
            yield
            psQ, pkQ = self.psum()
            for q in range(4):
                self.I(PE, "matmul", psQ[:, q * 128:(q + 1) * 128], NLn[:, q, :], Qc[:, q, :], start=True, stop=True, reads=[kNLn, kQ], writes=[pkQ], track=(q == 3))
            self.I(DVE, "tensor_tensor", Qn[:], psQ[:].rearrange("p (a b) -> p a b", b=128), Qc[:], ALU.add, reads=[pkQ, kQ], writes=[kQn])
            yield
            TUc, kTU, NLc, kNL, Qc, kQ = TUn, kTUn, NLn, kNLn, Qn, kQn

    def scan_chain(self, g, c, gc, sc, ARBD, PCs, bon, gst, Hf, Hb, tmpH):
        s = gc % 2
        kin = ["ARBD%d" % q for q in range(4)]
        NA1, NA2 = sc["NA1", s], sc["NA2", s]
        kNA1, kNA2 = "NA1_%d" % s, "NA2_%d" % s
        B2, K2, V2 = sc["B2", s], sc["K2", s], sc["V2", s]
        kB2, kK2, kV2 = "B2_%d" % s, "K2_%d" % s, "V2_%d" % s
        Qc, kQ = sc["Q", s, 1], "Q_%d1" % s
        kH = "Hf%d" % g
        kHb = "Hb%d" % g
        X2, U2 = sc["X2", 0], sc["U2", 0]
        ps, pk = self.psum()
        for q in range(4):
            self.I(PE, "matmul", ps[:, q * 64:(q + 1) * 64], ARBD[:, q, c, 0, :], Hb[:, q, :], start=True, stop=False, reads=kin + [kHb], writes=[pk], track=False)
            self.I(PE, "matmul", ps[:, q * 64:(q + 1) * 64], NA2[:, q, 0:128], V2[:, q, :], start=False, stop=True, reads=[kNA2, kV2], writes=[pk], track=(q == 3))
        self.I(ACT, "copy", X2[:], ps[:, 0:256].rearrange("p (a b) -> p a b", b=64), reads=[pk], writes=["X2_0"])
        yield
        ps, pk = self.psum()
        for q in range(4):
            self.I(PE, "matmul", ps[:, q * 64:(q + 1) * 64], Qc[:, q, :], X2[:, q, :], start=True, stop=True, reads=[kQ, "X2_0"], writes=[pk], track=(q == 3))
        self.I(ACT, "copy", U2[:], ps[:, 0:256].rearrange("p (a b) -> p a b", b=64), reads=[pk], writes=["U2_0"])
        yield
        need_y = gc >= OWN0 // CH
        if need_y:
            psY, pkY = self.psum()
            for q in range(4):
                self.I(PE, "matmul", psY[:, q * 64:(q + 1) * 64], ARBD[:, q, c, 1, :], Hb[:, q, :], start=True, stop=False, reads=kin + [kHb], writes=[pkY], track=False)
                self.I(PE, "matmul", psY[:, q * 64:(q + 1) * 64], NA1[:, q, 128:256], U2[:, q, :], start=False, stop=False, reads=[kNA1, "U2_0"], writes=[pkY], track=False)
                self.I(PE, "matmul", psY[:, q * 64:(q + 1) * 64], NA2[:, q, 128:256], V2[:, q, :], start=False, stop=True, reads=[kNA2, kV2], writes=[pkY], track=(q == 3))
        ps, pk = self.psum()
        for q in range(4):
            self.I(PE, "matmul", ps[:, q * 64:(q + 1) * 64], B2[:, q, :], U2[:, q, :], start=True, stop=False, reads=[kB2, "U2_0"], writes=[pk], track=False)
            self.I(PE, "matmul", ps[:, q * 64:(q + 1) * 64], K2[:, q, :], V2[:, q, :], start=False, stop=True, reads=[kK2, kV2], writes=[pk], track=(q == 3))
        self.I(DVE, "tensor_tensor", tmpH[:], Hf[:], ps[:, 0:256].rearrange("p (a b) -> p a b", b=64), ALU.add, reads=[pk, kH], writes=["tmpH"])
        pcb = PCs[:, :, c:c + 1].to_broadcast([128, 4, 64])
        self.I(DVE, "tensor_tensor", Hf[:], tmpH[:], pcb, ALU.mult, reads=["tmpH"] + ["PCs%d" % q for q in range(4)], writes=[kH])
        self.I(ACT, "copy", Hb[:], Hf[:], reads=[kH], writes=[kHb])
        yield
        if need_y:
            yield from self.y_post(g, c, gc, s, sc, psY, pkY, bon, gst)

    def scan_group(self, g, blk, NCK, sc, ARBD, BBD, KBD, VBD, PCs, bon, gst, Hf, Hb, tmpH):
        def drain(gen):
            for _ in gen:
                pass

        def interleave(ga, gb):
            a_live = b_live = True
            while a_live or b_live:
                if a_live:
                    try:
                        next(ga)
                    except StopIteration:
                        a_live = False
                if b_live:
                    try:
                        next(gb)
                    except StopIteration:
                        b_live = False
        gc0 = blk * NCK
        drain(self.scan_pre(g, 0, gc0, sc, ARBD, BBD, KBD, VBD))
        for c in range(NCK):
            ch = self.scan_chain(g, c, gc0 + c, sc, ARBD, PCs, bon, gst, Hf, Hb, tmpH)
            if c + 1 < NCK:
                interleave(ch, self.scan_pre(g, c + 1, gc0 + c + 1, sc, ARBD, BBD, KBD, VBD))
            else:
                drain(ch)

    def y_post(self, g, c, gc, s, sc, psY, pkY, bon, gst):
        S = self.S
        pv, cb = self.pv, self.cb
        ys, yq, st, yt, YBD = sc["ys", s], sc["yq", s], sc["st", s], sc["yt", s], sc["YBD", s]
        kys, kyq, kst, kyt, kY = "ys_0", "yq_0", "yst_0", "yt_0", "YBD_0"
        self.I(ACT, "copy", ys[:], psY[:, 0:256].rearrange("p (a b) -> p a b", b=64), reads=[pkY], writes=[kys])
        self.I(DVE, "tensor_reduce", st[:, 0, :], ys[:], AX.X, ALU.add, reads=[kys], writes=[kst])
        self.I(POOL, "tensor_tensor", yq[:], ys[:], ys[:], ALU.mult, reads=[kys], writes=[kyq])
        self.I(DVE, "tensor_reduce", st[:, 1, :], yq[:], AX.X, ALU.add, reads=[kyq], writes=[kst])
        yield
        self.I(DVE, "tensor_scalar", st[:, 2, :], st[:, 0, :], 1.0 / 64, None, ALU.mult, reads=[kst], writes=[kst])
        self.I(DVE, "tensor_tensor", st[:, 3, :], st[:, 2, :], st[:, 2, :], ALU.mult, reads=[kst], writes=[kst])
        self.I(DVE, "scalar_tensor_tensor", st[:, 4, :], st[:, 1, :], 1.0 / 64, st[:, 3, :], ALU.mult, ALU.subtract, reads=[kst], writes=[kst])
        self.I(DVE, "tensor_scalar", st[:, 4, :], st[:, 4, :], 64e-5, None, ALU.add, reads=[kst], writes=[kst])
        self.I(ACT, "activation", st[:, 5, :], st[:, 4, :], AF.Sqrt, reads=[kst], writes=[kst])
        self.I(DVE, "reciprocal", st[:, 6, :], st[:, 5, :], reads=[kst], writes=[kst])
        yield
        self.I(DVE, "tensor_tensor", yq[:], ys[:], st[:, 2, :].unsqueeze(2).to_broadcast([128, 4, 64]), ALU.subtract, reads=[kys, kst], writes=[kyq])
        for h in range(2):
            pr = slice(64 * h, 64 * h + 64)
            self.I(DVE if h == 0 else POOL, "tensor_tensor", YBD[pr, :, pr], yq[pr, :, :], st[pr, 6, :].unsqueeze(2).to_broadcast([64, 4, 64]), ALU.mult, reads=[kyq, kst], writes=[kY])
        yield
        ps, pk = self.psum()
        for q in range(4):
            self.I(PE, "matmul", ps[:, q * 64:(q + 1) * 64], YBD[:, q, :], cb[:, C_ISEL:C_ISEL + 64], start=True, stop=True, reads=[kY, "cb"], writes=[pk], track=(q == 3))
        t0 = 0
        col0 = gc * CH - OWN0
        if col0 < 0:
            t0 = -col0
            col0 = 0
        nt = CH - t0
        cl = c * CH + t0
        lw = pv[:, V_LW + 4 * g: V_LW + 4 * g + 4].unsqueeze(2).to_broadcast([128, 4, nt])
        psv = ps[:, 0:256].rearrange("p (a b) -> p a b", b=64)[:, :, t0:CH]
        self.I(DVE, "tensor_tensor", yt[:, :, 0:nt], psv, lw, ALU.mult, reads=[pk, "pv"], writes=[kyt])
        self.I(POOL, "tensor_tensor", yt[:, :, 0:nt], yt[:, :, 0:nt], bon[:, :, cl:cl + nt], ALU.add, reads=[kyt] + ["bon%d" % q for q in range(4)], writes=[kyt])
        self.I(DVE, "tensor_tensor", self.yaT[:, 4 * g:4 * g + 4, col0:col0 + nt], yt[:, :, 0:nt], gst[:, :, cl:cl + nt], ALU.mult, reads=[kyt] + ["gst%d" % q for q in range(4)], writes=["yaT"])


def core_inputs(inp, c, pv_base):
    b, p = c // 2, c % 2
    x = inp["x_prompt"][b]
    if p == 1:
        xseq = np.ascontiguousarray(x)
    else:
        xseq = np.concatenate([np.zeros((1024, D), np.float32), x[:1024]], axis=0)
    pa = permA()
    return {
        "xseq": xseq,
        "xsamp": np.ascontiguousarray(inp["x_sample"][16 * c:16 * c + 16].reshape(NSM, D)),
        "pvec": core_pvec(pv_base, p),
        "st_pool": np.ascontiguousarray(inp["state_pool"][0, 16 * c:16 * c + 16]),
        "st_conv": np.ascontiguousarray(inp["state_conv"][0, 16 * c:16 * c + 16]),
        "st_shift": np.ascontiguousarray(inp["state_shift"][0, 16 * c:16 * c + 16, 0][:, pa]),
        "st_wkv": np.ascontiguousarray(inp["state_wkv"][0, 16 * c:16 * c + 16].reshape(256, 4096)),
    }


def shared_inputs(inp):
    w_in = inp["w_in"][0]
    wG = np.empty((D, 16, 256), np.float32)
    wG[:, :, 0:128] = w_in[:, 4384:6432].reshape(D, 16, 128)
    wG[:, :, 128:256] = w_in[:, 6432:8480].reshape(D, 16, 128)
    wfi = inp["w_ffn_in"][0]
    wF = np.empty((D, NJ, 256), np.float32)
    wF[:, :, 0:128] = wfi[:, 0:DFF].reshape(D, NJ, 128)
    wF[:, :, 128:256] = wfi[:, DFF:2 * DFF].reshape(D, NJ, 128)
    gpost = np.empty((128, 2, D), np.float32)
    gpost[:, 0, :] = inp["norm_post_mix"][0][None, :]
    gpost[:, 1, :] = inp["norm_post_ffn"][0][None, :]
    return {
        "wA": np.ascontiguousarray(w_in[:, permA()]),
        "consts": host_consts(),
        "w2": np.ascontiguousarray(inp["w2"][0]),
        "a2": np.ascontiguousarray(inp["a2"][0]),
        "g2": np.ascontiguousarray(inp["g2"][0]),
        "wP": np.ascontiguousarray(w_in[:, 3360:4384]),
        "wG": wG,
        "wa": np.ascontiguousarray(inp["w_branch_a"][0]),
        "wbr": np.ascontiguousarray(inp["w_branch_b"][0]),
        "poolw": np.ascontiguousarray(inp["pool_w"][0]),
        "wout": np.ascontiguousarray(inp["w_out"][0]),
        "gpost": gpost,
        "wF": wF,
        "wfo": np.ascontiguousarray(inp["w_ffn_out"][0]),
    }


_NC_CACHE = {}


def get_nc(dbg=(), stages="SABC"):
    key = (tuple(sorted(dbg)), stages)
    if key not in _NC_CACHE:
        B = Builder(dbg=set(dbg), stages=stages)
        nc = B.build()
        _NC_CACHE[key] = (nc, B)
    return _NC_CACHE[key]


def run_cores(inp, cores, dbg=(), stages="SABC", trace=False):
    nc, B = get_nc(dbg, stages)
    sh = shared_inputs(inp)
    pv_base = host_pvec(inp)
    names = set(B.ins.keys())
    maps = []
    for c in cores:
        m = dict(sh)
        m.update(core_inputs(inp, c, pv_base))
        maps.append({k: v for k, v in m.items() if k in names})
    res = run_bass_kernel_spmd(nc, maps, core_ids=list(range(len(cores))), trace=trace)
    return res


def kernel(**inp):
    inp = {k: np.asarray(v) for k, v in inp.items()}
    res = run_cores(inp, list(range(8)))
    R = res.results
    pa = permA()
    y_prompt = np.empty((4, 2048, D), np.float32)
    y_sample = np.empty((128, 4, D), np.float32)
    p_shift = np.empty((1, 4, 1, DS), np.float32)
    p_wkv = np.empty((1, 4, 16, 64, 64), np.float32)
    p_pool = np.empty((1, 4, 15, 1024), np.float32)
    p_conv = np.empty((1, 4, 2, DFF), np.float32)
    s_shift = np.zeros((1, 128, 1, DS), np.float32)
    s_wkv = np.zeros((1, 128, 16, 64, 64), np.float32)
    s_pool = np.empty((1, 128, 15, 1024), np.float32)
    s_conv = np.empty((1, 128, 2, DFF), np.float32)
    for c in range(8):
        b, p = c // 2, c % 2
        r = R[c]
        y_prompt[b, p * 1024:(p + 1) * 1024] = r["o_y"][0:1024]
        y_sample[16 * c:16 * c + 16] = r["o_y"][1024:1024 + NSM].reshape(16, 4, D)
        if p == 1:
            p_shift[0, b, 0, pa] = r["o_pshift"]
            p_wkv[0, b] = r["o_pwkv"]
            p_pool[0, b] = r["o_ppool"]
            p_conv[0, b] = r["o_pconv"]
        s_pool[0, 16 * c:16 * c + 16] = r["o_spool"]
        s_conv[0, 16 * c:16 * c + 16] = r["o_sconv"]
        if "o_sshift" in r:
            s_shift[0, 16 * c:16 * c + 16, 0][:, pa] = r["o_sshift"]
            s_wkv[0, 16 * c:16 * c + 16] = r["o_swkv"].reshape(16, 16, 64, 64)
    return (y_prompt, y_sample, p_shift, p_wkv, p_pool, p_conv, s_shift, s_wkv, s_pool, s_conv)
```

```python
import os
import numpy as np
import concourse.bass as bass
import concourse.mybir as mybir
from contextlib import ExitStack
from concourse.bass_utils import run_bass_kernel_spmd

F32 = mybir.dt.float32
BF16 = mybir.dt.bfloat16
AF = mybir.ActivationFunctionType
ALU = mybir.AluOpType
AX = mybir.AxisListType

PE, ACT, DVE, POOL, SP = "pe", "act", "dve", "pool", "sp"
NDMASEM = 16
VCLOCK = os.environ.get('VCLOCK', '1') == '1'
EMBED_WAIT = os.environ.get('EMBW', '1') == '1'
SAME_ENGINE_NOWAIT = os.environ.get('SENW', '0') == '1'


class Sched:
    def __init__(self, nc, es):
        self.nc = nc
        self.q = {e: [] for e in (PE, ACT, DVE, POOL, SP)}
        self.cnt = {e: 0 for e in (PE, ACT, DVE, POOL)}
        self.sem = {e: es.enter_context(nc.semaphore("s_" + e)) for e in (PE, ACT, DVE, POOL)}
        self.dsem = {}
        self.dcnt = {}
        for e in (SP, "spo", "bg", POOL):
            self.dsem[e] = [es.enter_context(nc.semaphore("d_%s%d" % (e, i))) for i in range(NDMASEM)]
            self.dcnt[e] = 0
        self.pending = {e: [] for e in (PE, ACT, DVE, POOL, SP)}
        self.tok_ring = {}
        self.clock = {}
        self.tclk = {}
        self.lastw = {}
        self.readers = {}
        self.all_tokens = {}

    def _deps(self, eng, reads, writes):
        toks = []
        for k in reads:
            w = self.lastw.get(k)
            if w is not None:
                toks.append(w)
        for k in writes:
            w = self.lastw.get(k)
            if w is not None:
                toks.append(w)
            toks.extend(self.readers.get(k, ()))
        clk = self.clock.setdefault(eng, {})
        need = {}
        for tok in toks:
            s, v, src = tok[0], tok[1], tok[2]
            if src == eng and eng == PE:
                continue
            if clk.get(s.name, 0) >= v:
                continue
            if need.get(s.name, (None, 0))[1] < v:
                need[s.name] = (s, v, self.tclk.get((s.name, v), {}))
        for (s, v) in self.pending[eng]:
            if clk.get(s.name, 0) >= v:
                continue
            if need.get(s.name, (None, 0))[1] < v:
                need[s.name] = (s, v, self.tclk.get((s.name, v), {}))
        self.pending[eng] = []
        if VCLOCK and len(need) > 1:
            drop = set()
            for name, (s, v, c) in need.items():
                for name2, (s2, v2, c2) in need.items():
                    if name2 != name and name2 not in drop and c2.get(name, 0) >= v:
                        drop.add(name)
                        break
            for name in drop:
                del need[name]
        out = []
        for name, (s, v, c) in need.items():
            if clk.get(name, 0) < v:
                clk[name] = v
            for n2, v2 in c.items():
                if clk.get(n2, 0) < v2:
                    clk[n2] = v2
            out.append((s, v))
        return out

    def _stamp(self, eng, tok):
        c = dict(self.clock.get(eng, {}))
        c[tok[0].name] = tok[1]
        self.tclk[(tok[0].name, tok[1])] = c

    def barrier(self):
        if self.frozen:
            return
        allw = []
        for e in (PE, ACT, DVE, POOL):
            if self.cnt[e] > 0:
                allw.append((self.sem[e], self.cnt[e], e))
        for name, (s, v) in self.all_tokens.items():
            if self.tok_ring.get(name) != "bg":
                allw.append((s, v, "dma"))
        for eng in (PE, ACT, DVE, POOL, SP):
            for (s, v, src) in allw:
                if src == eng:
                    continue
                self.pending[eng].append((s, v))

    def _commit(self, tok, reads, writes):
        for k in writes:
            self.lastw[k] = tok
            self.readers[k] = []
        for k in reads:
            lst = self.readers.setdefault(k, [])
            lst[:] = [t for t in lst if t[0].name != tok[0].name]
            lst.append(tok)

    frozen = False

    def op(self, eng, fn, reads=(), writes=(), track=True):
        if self.frozen:
            return None
        waits = self._deps(eng, reads, writes)
        tok = None
        if not track:
            assert eng == PE
            self.pend_r = getattr(self, "pend_r", set()) | set(reads)
        if track:
            self.cnt[eng] += 1
            tok = (self.sem[eng], self.cnt[eng], eng)
            self._stamp(eng, tok)
            if eng == PE and getattr(self, "pend_r", None):
                reads = list(set(reads) | self.pend_r)
                self.pend_r = set()
            self._commit(tok, reads, writes)
        self.q[eng].append((waits, fn, tok))
        return tok

    def dma(self, qeng, out, in_, reads=(), writes=(), ring=None, **kw):
        if self.frozen:
            return None
        waits = self._deps(qeng, reads, writes)
        rk = ring or qeng
        n = self.dcnt[rk]
        self.dcnt[rk] += 1
        s = self.dsem[rk][n % NDMASEM]
        v = 16 * (n // NDMASEM + 1)
        tok = (s, v, "dma")
        self._stamp(qeng, tok)
        self._commit(tok, reads, writes)
        self.q[qeng].append((waits, lambda e: e.dma_start(out=out, in_=in_, **kw), tok))
        self.all_tokens[s.name] = (s, v)
        self.tok_ring[s.name] = rk
        return tok

    def emit(self):
        nc = self.nc
        fin = dict(self.all_tokens)
        for e in (PE, ACT, DVE, POOL):
            if self.cnt[e] > 0:
                fin[self.sem[e].name] = (self.sem[e], self.cnt[e])
        q = self.q
        with nc.Block() as block:
            def run(eng_name):
                def body(e):
                    for (waits, fn, tok) in q[eng_name]:
                        emb = None
                        if EMBED_WAIT and waits:
                            emb = waits[-1]
                            waits = waits[:-1]
                        for (s, v) in waits:
                            e.wait_ge(s, v)
                        ins = fn(e)
                        if emb is not None:
                            ins._wait_ge(emb[0], emb[1])
                        if tok is not None:
                            ins.then_inc(tok[0], 16 if tok[2] == "dma" else 1)
                    if eng_name == SP:
                        for name, (s, v) in fin.items():
                            e.wait_ge(s, v)
                return body
            block.tensor(run(PE))
            block.scalar(run(ACT))
            block.vector(run(DVE))
            block.gpsimd(run(POOL))
            block.sync(run(SP))


D = 2048
DS = 3360
NPR = 2048
NSM = 64
OWN0 = 992
NOWN = 1120
BLK = 512
NH = 512
CH = 64
C_ID, C_BONES, C_ISEL, C_MUS, C_MUI, C_MLS, C_M01 = 0, 128, 256, 320, 448, 576, 704
NCONST = 704 + 512
V_MULR = 0
V_MU = 4
V_W0, V_A0, V_KK, V_KA, V_RK, V_LW, V_LB = 28, 36, 44, 52, 60, 68, 76
V_G1 = 84
V_PS = 100
V_G3 = 108
V_FLAG = 124
V_INVC = 125
V_CW = 189
NV = 189 + 176
DFF = 5632
NJ = 44
DECAY_C = 0.6065306597126334


def host_consts():
    c = np.zeros((128, NCONST), np.float32)
    p = np.arange(128)[:, None]
    j = np.arange(128)[None, :]
    c[:, C_ID:C_ID + 128] = (p == j)
    c[:, C_BONES:C_BONES + 128] = (p // 64 == j // 64)
    c[:, C_ISEL:C_ISEL + 64] = (p % 64 == np.arange(64)[None, :])
    c[:, C_MUS:C_MUS + 128] = (p % 64 < j % 64)
    c[:, C_MUI:C_MUI + 128] = (p % 64 <= j % 64)
    c[:, C_MLS:C_MLS + 128] = (p % 64 > j % 64)
    c[:, C_M01:C_M01 + 512] = (np.arange(512)[None, :] % 64 != 0)
    return c


def permA():
    idx = list(range(3072, 3360))
    for q in range(8):
        idx += list(range(q * 128, q * 128 + 128))
        idx += list(range(1024 + q * 128, 1024 + q * 128 + 128))
        idx += list(range(2048 + q * 128, 2048 + q * 128 + 128))
    return np.array(idx)


def host_pvec(inp):
    v = np.zeros((128, NV), np.float32)
    mu = inp["mu_shift"][0]
    v[0:64, 0] = mu[3072:3136]
    v[0:64, 1] = mu[3136:3200]
    v[0:128, 2] = mu[3200:3328]
    v[0:32, 3] = mu[3328:3360]
    for q in range(8):
        for t in range(3):
            v[:, V_MU + 3 * q + t] = mu[t * 1024 + q * 128: t * 1024 + q * 128 + 128]
    for (col, name) in ((V_W0, "w0"), (V_A0, "a0"), (V_KK, "k_k"), (V_KA, "k_a"), (V_RK, "r_k"), (V_LW, "lnx_w"), (V_LB, "lnx_b")):
        a = inp[name][0].reshape(-1)
        for q in range(8):
            v[:, col + q] = a[q * 128:(q + 1) * 128]
    g = inp["norm_pre_mix"][0]
    g3 = inp["norm_pre_ffn"][0]
    for k in range(16):
        v[:, V_G1 + k] = g[k * 128:(k + 1) * 128]
        v[:, V_G3 + k] = g3[k * 128:(k + 1) * 128]
    psc = inp["pool_scale"][0]
    for k in range(8):
        v[:, V_PS + k] = psc[k * 128:(k + 1) * 128]
    cw, cbias = inp["conv_w"][0], inp["conv_b"][0]
    for j in range(NJ):
        for t in range(3):
            v[:, V_CW + 4 * j + t] = cw[t, j * 128:(j + 1) * 128]
        v[:, V_CW + 4 * j + 3] = cbias[j * 128:(j + 1) * 128]
    return v


def core_pvec(base, p):
    v = base.copy()
    v[:, V_FLAG] = float(p)
    for gi, win in enumerate((2, 4, 8, 16)):
        for j in range(16):
            pos = p * 1024 + j
            v[:, V_INVC + gi * 16 + j] = 1.0 / min(win, pos + 1)
    return v


class _TAlias:
    def __init__(self, phys, alias, prefix="t_"):
        self.phys = phys
        self.alias = alias
        self.prefix = prefix

    def _n(self, n):
        return self.alias.get(n, n)

    def __getitem__(self, n):
        return self.phys[self._n(n)]

    def key(self, n):
        return self.prefix + self._n(n)


class StopBuild(Exception):
    pass


class Builder:
    stop_at = None

    def ckpt(self, name):
        if self.stop_at == name and not self.S.frozen:
            print("frozen at", name)
            self.S.frozen = True
            self.dbg = set()

    def __init__(self, dbg=None, stages="A"):
        self.dbg = dbg or set()
        self.stages = stages
        self.nc = bass.Bass("TRN2", target_bir_lowering=False)
        self.ins = {}
        self.outs = {}
        self.psn = 0

    def din(self, name, shape, dt=F32):
        t = self.nc.dram_tensor(name, list(shape), dt, kind="ExternalInput").ap()
        self.ins[name] = t
        return t

    def dout(self, name, shape, dt=F32):
        t = self.nc.dram_tensor(name, list(shape), dt, kind="ExternalOutput").ap()
        self.outs[name] = t
        return t

    def I(self, eng, meth, *args, reads=(), writes=(), track=True, **kw):
        return self.S.op(eng, lambda e: getattr(e, meth)(*args, **kw), reads=reads, writes=writes, track=track)

    def stage_out(self, name, tile_ap, shape, key):
        scr = self.nc.dram_tensor("scr_" + name, list(shape), F32).ap()
        self.S.dma(SP, scr, tile_ap, reads=[key], writes=["scr_" + name], ring="spo")
        return scr

    def bg(self, dst, src, name):
        self.S.dma(SP, dst, src, reads=["scr_" + name], ring="bg", allow_slow_non_contiguous=True)

    def sb(self, es, name, shape, dt=F32):
        return es.enter_context(self.nc.sbuf_tensor(name, list(shape), dt))

    def psum(self):
        i = self.psn % 8
        self.psn += 1
        return self.PS[i], "ps%d" % i

    def build(self):
        nc = self.nc
        xseq = self.xseq = self.din("xseq", [NPR, D])
        xsamp = self.xsamp = self.din("xsamp", [NSM, D])
        wA = self.din("wA", [D, DS])
        consts = self.din("consts", [128, NCONST])
        pvec = self.din("pvec", [128, NV])
        w2 = self.din("w2", [64, 1024])
        a2 = self.din("a2", [64, 1024])
        g2 = self.din("g2", [160, 1024])
        self.wP = self.din("wP", [D, 1024])
        self.wG = self.din("wG", [D, 16, 256])
        self.wa = self.din("wa", [1024, D])
        self.wbr = self.din("wbr", [1024, D])
        self.poolw = self.din("poolw", [4, 256, 256])
        self.wout = self.din("wout", [D, D])
        self.gpost = self.din("gpost", [128, 2, D])
        self.wF = self.din("wF", [D, NJ, 256])
        self.wfo = self.din("wfo", [DFF, D])
        self.st_pool = self.din("st_pool", [16, 15, 1024])
        self.st_conv = self.din("st_conv", [16, 2, DFF])
        self.o_pshift = self.dout("o_pshift", [DS])
        self.o_pwkv = self.dout("o_pwkv", [16, 64, 64])
        self.o_ppool = self.dout("o_ppool", [15, 1024])
        self.o_pconv = self.dout("o_pconv", [2, DFF])
        self.o_spool = self.dout("o_spool", [16, 15, 1024])
        self.o_sconv = self.dout("o_sconv", [16, 2, DFF])
        self.o_y = self.dout("o_y", [1024 + NSM, D])
        self.st_shift = self.din("st_shift", [16, DS])
        self.st_wkv = self.din("st_wkv", [256, 4096])
        self.o_sshift = self.dout("o_sshift", [16, DS])
        self.o_swkv = self.dout("o_swkv", [256, 4096])
        self.x1s = nc.dram_tensor("x1s", [NOWN, D], F32).ap()
        self.scrS = nc.dram_tensor("scrS", [NSM, 6, 1024], F32).ap()
        self.scrY = nc.dram_tensor("scrY", [NSM, 1024], F32).ap()
        self.scrH = nc.dram_tensor("scrH", [128, 16, NOWN], BF16).ap()
        if "ya" in self.dbg:
            self.o_ya = self.dout("d_ya", [128, 8, NOWN])
        with ExitStack() as es:
            S = self.S = Sched(nc, es)
            self.PS = [es.enter_context(nc.psum_tensor("ps%d" % i, [128, 512], F32)) for i in range(8)]
            cf = self.cf = self.sb(es, "cf", [128, NCONST])
            cb = self.cb = self.sb(es, "cb", [128, 320], BF16)
            pv = self.pv = self.sb(es, "pv", [128, NV])
            S.dma(SP, cf[:], consts, writes=["cf"])
            S.dma(SP, pv[:], pvec, writes=["pv"])
            S.dma(POOL, cb[:], consts[:, 0:320], writes=["cb"])
            self.wslot = 0
            bufX = self.bufX = self.sb(es, "bufX", [128, 16, NOWN], BF16)
            self.yaT = bufX[:, 0:8, :]
            self.ybT = bufX[:, 8:16, :]
            with ExitStack() as esw:
                self.wb = [self.sb(esw, "wb%d" % i, [128, 16, 384], BF16) for i in range(2)]
                self.wA_ap = wA
                with ExitStack() as esA:
                    w2b = self.w2b = self.sb(esA, "w2b", [64, 1024], BF16)
                    a2b = self.a2b = self.sb(esA, "a2b", [64, 1024], BF16)
                    g2a = self.g2a = self.sb(esA, "g2a", [128, 1024], BF16)
                    g2b = self.g2b = self.sb(esA, "g2b", [128, 1024], BF16)
                    S.dma(POOL, w2b[:], w2, writes=["w2b"])
                    S.dma(POOL, a2b[:], a2, writes=["a2b"])
                    S.dma(POOL, g2a[:], g2[0:128, :], writes=["g2a"])
                    self.I(POOL, "memset", g2b[:], 0.0, writes=["g2b"])
                    S.dma(POOL, g2b[0:32, :], g2[128:160, :], writes=["g2b"])
                    if "S" in self.stages:
                        self.stageS()
                    self.stageA(esA, xseq, wA)
                if "ya" in self.dbg:
                    with ExitStack() as esd:
                        yaf = self.sb(esd, "yaf", [128, 8, NOWN], F32)
                        self.I(DVE, "tensor_copy", yaf[:], self.yaT, reads=["yaT"], writes=["yaf"])
                        S.dma(SP, self.o_ya, yaf[:], reads=["yaf"], ring="spo")
                        S.barrier()
                if "B" in self.stages:
                    with ExitStack() as esm:
                        self.mT = self.sb(esm, "mT", [128, 16, NOWN], BF16)
                        with ExitStack() as esb:
                            self.hTo = self.sb(esb, "hTo", [128, 16, NOWN], BF16)
                            self.stageB12(esb)
                        self.stageB3(esm)
            if "C" in self.stages:
                self.stageC(es)
            S.emit()
        return nc

    def tok_groups(self):
        return [(0, 512), (512, 512), (1024, NOWN - 1024)]

    def stageB12(self, es_outer):
        S = self.S
        pv, cf, cb = self.pv, self.cf, self.cb
        hT = self.hTo
        with ExitStack() as es:
            sb = lambda n, s, d=F32: self.sb(es, n, s, d)
            self.xt = [sb("xtB", [128, D])]
            self.xb = [sb("xbB", [128, D], BF16)]
            self.xst = [sb("xstB", [128, 4])]
            for (c0, c1, k) in ((0, 32, "scrH_1"), (32, 544, "scrH_2"), (544, 1056, "scrH_3"), (1056, NOWN, "scrH_s")):
                S.dma(SP, hT[:, :, c0:c1], self.scrH[:, :, c0:c1], reads=[k], writes=["hTo"])
            self.ckpt("B0")
            ybT = self.ybT
            with ExitStack() as esp:
                sbp = lambda n, s, d=F32: self.sb(esp, n, s, d)
                pwb = sbp("pwb", [128, 4, 2, 256], BF16)
                S.dma(POOL, pwb[:], self.poolw.rearrange("g (k p) n -> p g k n", p=128), writes=["pwb"])
                phT = sbp("phT", [128, 8, 16, 15])
                for half in range(2):
                    sp_t = sbp("sp_t%d" % half, [120, 1024])
                    S.dma(SP, sp_t[:], self.st_pool[8 * half: 8 * half + 8].rearrange("b j c -> (b j) c"), writes=["sp_t%d" % half])
                    for c4 in range(2):
                        ps, pk = self.psum()
                        for ch in range(4):
                            self.I(PE, "transpose", ps[:, ch * 120:(ch + 1) * 120], sp_t[:, (c4 * 4 + ch) * 128:(c4 * 4 + ch + 1) * 128], cf[0:120, 0:120],
                                   reads=["sp_t%d" % half, "cf"], writes=[pk], track=(ch == 3))
                        self.I(DVE, "tensor_copy", phT[:, c4 * 4:c4 * 4 + 4, 8 * half:8 * half + 8, :],
                               ps[:, 0:480].rearrange("p (a b j) -> p a b j", a=4, b=8), reads=[pk], writes=["phT"])
                self.ckpt("B1a")
                S.dma(SP, self.o_spool[:, 0:11, :], self.st_pool[:, 4:15, :], ring="spo")
                zp = sbp("zp", [128, NOWN])
                pa_ = sbp("ppA", [128, NOWN])
                pb_ = sbp("ppB", [128, NOWN])
                dT = sbp("dT", [128, 8, NOWN], BF16)
                self.I(POOL, "memset", dT[:], 0.0, writes=["dT"])
                ppo = sbp("ppo", [128, 8, 15])
                spo = sbp("spo", [128, 8, 16, 4])
                bs = sbp("bs", [128, 16, 19])
                bs2 = sbp("bs2", [128, 16, 19])
                slabs = [(self.wP[:, ch * 128:(ch + 1) * 128], 128) for ch in range(8)]
                stream = self.slab_stream(slabs)
                NP_ = 1056
                for ch in range(8):
                    wt, wk = next(stream)
                    gi = ch // 2
                    win = 2 << gi
                    for (t0, tn) in self.tok_groups():
                        ps, pk = self.psum()
                        for k in range(16):
                            self.I(PE, "matmul", ps[:, 0:tn], wt[:, k, 0:128], hT[:, k, t0:t0 + tn], start=(k == 0), stop=(k == 15),
                                   reads=[wk, "hTo"], writes=[pk], track=(k == 15))
                        self.I(ACT, "copy", zp[:, t0:t0 + tn], ps[:, 0:tn], reads=[pk], writes=["zp"])
                    src, ksrc = zp, "zp"
                    bufs = [(pa_, "ppA"), (pb_, "ppB")]
                    for j in range(gi + 1):
                        sh = 1 << j
                        dst, kdst = bufs[j % 2]
                        self.I(DVE if j % 2 == 0 else POOL, "tensor_tensor", dst[:, sh:NP_], src[:, sh:NP_], src[:, 0:NP_ - sh], ALU.add,
                               reads=[ksrc], writes=[kdst])
                        src, ksrc = dst, kdst
                    lo = win - 1
                    self.I(DVE, "scalar_tensor_tensor", dT[:, ch, lo:NP_], src[:, lo:NP_], 1.0 / win, zp[:, lo:NP_], ALU.mult, ALU.subtract,
                           reads=[ksrc, "zp"], writes=["dT"])
                    other, kother = bufs[(gi + 1) % 2]
                    self.I(POOL, "tensor_tensor", other[:, 32:48], src[:, 32:48], pv[:, V_INVC + gi * 16: V_INVC + gi * 16 + 16], ALU.mult,
                           reads=[ksrc, "pv"], writes=[kother])
                    self.I(POOL, "tensor_tensor", dT[:, ch, 32:48], other[:, 32:48], zp[:, 32:48], ALU.subtract, reads=[kother, "zp"], writes=["dT"])
                    self.I(POOL, "tensor_copy", ppo[:, ch, :], zp[:, NP_ - 15:NP_], reads=["zp"], writes=["ppo"])
                    zps = zp[:, NP_:NOWN].rearrange("p (b t) -> p b t", t=4)
                    self.I(POOL, "tensor_copy", bs[:, :, 0:15], phT[:, ch, :, :], reads=["phT"], writes=["bs"])
                    self.I(POOL, "tensor_copy", bs[:, :, 15:19], zps, reads=["zp"], writes=["bs"])
                    self.I(POOL, "tensor_copy", spo[:, ch, :, :], zps, reads=["zp"], writes=["spo"])
                    ssrc, kss = bs, "bs"
                    sbufs = [(bs2, "bs2"), (bs, "bs")]
                    for j in range(gi + 1):
                        sh = 1 << j
                        dst, kdst = sbufs[j % 2]
                        self.I(DVE, "tensor_tensor", dst[:, :, sh:19], ssrc[:, :, sh:19], ssrc[:, :, 0:19 - sh], ALU.add, reads=[kss], writes=[kdst])
                        ssrc, kss = dst, kdst
                    self.I(DVE, "scalar_tensor_tensor", dT[:, ch, NP_:NOWN].rearrange("p (b t) -> p b t", t=4), ssrc[:, :, 15:19], 1.0 / win, zps,
                           ALU.mult, ALU.subtract, reads=[kss, "zp"], writes=["dT"])
                self.ckpt("B1b")
                scr = self.stage_out("ppo", ppo[:], [128, 8, 15], "ppo")
                for ch in range(8):
                    self.bg(self.o_ppool[:, ch * 128:(ch + 1) * 128].rearrange("j p -> p j"), scr[:, ch, :], "ppo")
                sprow = sbp("sprow", [NSM, 1024])
                for c4 in range(2):
                    ps, pk = self.psum()
                    for ch in range(4):
                        self.I(PE, "transpose", ps[0:NSM, ch * 128:(ch + 1) * 128], spo[:, c4 * 4 + ch, :, :].rearrange("p b t -> p (b t)"), cf[:, C_ID:C_ID + 128],
                               reads=["spo", "cf"], writes=[pk], track=(ch == 3))
                    self.I(DVE, "tensor_copy", sprow[:, c4 * 512:(c4 + 1) * 512], ps[0:NSM, :], reads=[pk], writes=["sprow"])
                for b in range(16):
                    S.dma(SP, self.o_spool[b, 11:15, :], sprow[4 * b:4 * b + 4, :], reads=["sprow"], ring="spo")
                self.ckpt("B1c")
                for oc in range(8):
                    gi, o2 = oc // 2, oc % 2
                    for (t0, tn) in self.tok_groups():
                        ps, pk = self.psum()
                        for kk in range(2):
                            self.I(PE, "matmul", ps[:, 0:tn], pwb[:, gi, kk, o2 * 128:(o2 + 1) * 128], dT[:, 2 * gi + kk, t0:t0 + tn],
                                   start=(kk == 0), stop=(kk == 1), reads=["pwb", "dT"], writes=[pk], track=(kk == 1))
                        self.I(ACT, "activation", ybT[:, oc, t0:t0 + tn], ps[:, 0:tn], AF.Copy, scale=pv[:, V_PS + oc: V_PS + oc + 1],
                               reads=[pk, "pv"], writes=["ybT"])
                S.barrier()
            self.ckpt("B1d")
            mT = self.mT
            tmp = [sb("b2t%d" % i, [128, 512]) for i in range(4)]
            njs = 16

            def issue(j):
                slot = self.wslot % 2
                self.wslot += 1
                key = "wb%d" % slot
                w = self.wb[slot]
                wflat = w[:].rearrange("p a b -> p (a b)")
                S.dma(POOL, wflat[:, 0:4096].rearrange("p (k n) -> p k n", n=256), self.wG[:, j, :].rearrange("(k p) n -> p k n", p=128), writes=[key])
                S.dma(POOL, wflat[:, 4096:5120].rearrange("p (k n) -> p k n", n=128), self.wa[:, j * 128:(j + 1) * 128].rearrange("(k p) n -> p k n", p=128), writes=[key])
                S.dma(POOL, wflat[:, 5120:6144].rearrange("p (k n) -> p k n", n=128), self.wbr[:, j * 128:(j + 1) * 128].rearrange("(k p) n -> p k n", p=128), writes=[key])
                return (wflat, key)
            cur = issue(0)
            for j in range(njs):
                nxt = issue(j + 1) if j + 1 < njs else None
                wflat, wk = cur
                wg = wflat[:, 0:4096].rearrange("p (k n) -> p k n", n=256)
                wa_ = wflat[:, 4096:5120].rearrange("p (k n) -> p k n", n=128)
                wb_ = wflat[:, 5120:6144].rearrange("p (k n) -> p k n", n=128)
                for (t0, tn) in self.tok_groups():
                    psa, pka = self.psum()
                    psb, pkb = self.psum()
                    ppa, pkpa = self.psum()
                    ppb, pkpb = self.psum()
                    for k in range(16):
                        self.I(PE, "matmul", psa[:, 0:tn], wg[:, k, 0:128], hT[:, k, t0:t0 + tn], start=(k == 0), stop=(k == 15),
                               reads=[wk, "hTo"], writes=[pka], track=(k == 15))
                    for k in range(16):
                        self.I(PE, "matmul", psb[:, 0:tn], wg[:, k, 128:256], hT[:, k, t0:t0 + tn], start=(k == 0), stop=(k == 15),
                               reads=[wk, "hTo"], writes=[pkb], track=(k == 15))
                    for k in range(8):
                        self.I(PE, "matmul", ppa[:, 0:tn], wa_[:, k, :], self.yaT[:, k, t0:t0 + tn], start=(k == 0), stop=(k == 7),
                               reads=[wk, "yaT"], writes=[pkpa], track=(k == 7))
                    for k in range(8):
                        self.I(PE, "matmul", ppb[:, 0:tn], wb_[:, k, :], ybT[:, k, t0:t0 + tn], start=(k == 0), stop=(k == 7),
                               reads=[wk, "ybT"], writes=[pkpb], track=(k == 7))
                    self.I(ACT, "activation", tmp[0][:, 0:tn], psa[:, 0:tn], AF.Sigmoid, reads=[pka], writes=["b2t0"])
                    self.I(ACT, "activation", tmp[1][:, 0:tn], psb[:, 0:tn], AF.Sigmoid, reads=[pkb], writes=["b2t1"])
                    self.I(DVE, "tensor_tensor", tmp[2][:, 0:tn], tmp[0][:, 0:tn], ppa[:, 0:tn], ALU.mult, reads=["b2t0", pkpa], writes=["b2t2"])
                    self.I(DVE, "tensor_tensor", tmp[3][:, 0:tn], tmp[1][:, 0:tn], ppb[:, 0:tn], ALU.mult, reads=["b2t1", pkpb], writes=["b2t3"])
                    self.I(POOL, "tensor_tensor", mT[:, j, t0:t0 + tn], tmp[2][:, 0:tn], tmp[3][:, 0:tn], ALU.add, reads=["b2t2", "b2t3"], writes=["mT"])
                cur = nxt
            S.barrier()

    def tok_tiles(self):
        self.ckpt("B2")
        return [(0, 32)] + [(32 + 128 * i, 128) for i in range(8)] + [(1056, NSM)]

    def stageB3(self, es_outer):
        S = self.S
        pv, cf, cb = self.pv, self.cf, self.cb
        mT, h2T = self.mT, self.bufX
        with ExitStack() as es:
            sb = lambda n, s, d=F32: self.sb(es, n, s, d)
            woutb = sb("woutb", [128, 16, D], BF16)
            for n in range(4):
                S.dma(POOL, woutb[:, :, n * 512:(n + 1) * 512], self.wout[:, n * 512:(n + 1) * 512].rearrange("(k p) n -> p k n", p=128), writes=["woutb%d" % n])
            gp = sb("gp", [128, D])
            S.dma(SP, gp[:], self.gpost[:, 0, :], writes=["gp"])
            mo = sb("mo", [128, D])
            xt = sb("xt3", [128, D])
            x1 = sb("x1t", [128, D])
            xb = sb("xb3", [128, D], BF16)
            st = sb("st3", [128, 16])
            for (c0, nt) in self.tok_tiles():
                xrows = self.xseq[OWN0 + c0: OWN0 + c0 + nt, :] if c0 < 1056 else self.xsamp
                S.dma(SP, xt[0:nt, :], xrows, writes=["xt3"])
                self.I(POOL, "memset", st[:, 0:4], 0.0, writes=["st3"])
                self.ckpt("B3pre")
                for n in range(4):
                    ps, pk = self.psum()
                    for k in range(16):
                        self.I(PE, "matmul", ps[0:nt, :], mT[:, k, c0:c0 + nt], woutb[:, k, n * 512:(n + 1) * 512], start=(k == 0), stop=(k == 15),
                               reads=["mT", "woutb%d" % n], writes=[pk], track=(k == 15))
                    self.I(DVE, "tensor_copy", mo[0:nt, n * 512:(n + 1) * 512], ps[0:nt, :], reads=[pk], writes=["mo"])
                    self.I(ACT, "activation", xb[0:nt, n * 512:(n + 1) * 512], mo[0:nt, n * 512:(n + 1) * 512], AF.Square, accum_out=st[0:nt, n:n + 1], reads=["mo"], writes=["xb3", "st3"])
                self.ckpt("B3a")
                self.I(DVE, "tensor_reduce", st[0:nt, 4:5], st[0:nt, 0:4], AX.X, ALU.add, reads=["st3"], writes=["st3"])
                self.I(DVE, "tensor_scalar", st[0:nt, 5:6], st[0:nt, 4:5], 1.0 / D, 1e-6, ALU.mult, ALU.add, reads=["st3"], writes=["st3"])
                self.I(ACT, "activation", st[0:nt, 6:7], st[0:nt, 5:6], AF.Sqrt, reads=["st3"], writes=["st3"])
                self.I(DVE, "reciprocal", st[0:nt, 7:8], st[0:nt, 6:7], reads=["st3"], writes=["st3"])
                self.I(DVE, "scalar_tensor_tensor", x1[0:nt, :], mo[0:nt, :], st[0:nt, 7:8], gp[0:nt, :], ALU.mult, ALU.mult, reads=["mo", "st3", "gp"], writes=["x1t"])
                self.I(POOL, "tensor_tensor", x1[0:nt, :], x1[0:nt, :], xt[0:nt, :], ALU.add, reads=["x1t", "xt3"], writes=["x1t"])
                self.ckpt("B3b")
                S.dma(SP, self.x1s[c0:c0 + nt, :], x1[0:nt, :], reads=["x1t"], writes=["x1s"])
                self.ckpt("B3c")
                self.norm_T(x1, "x1t", nt, h2T[:, :, c0:c0 + nt], "h2T", xb, "xb3", st, "st3", 8, V_G3)
                self.ckpt("B3d")
            S.barrier()

    def norm_T(self, xt, kx, ntok, hT, hkey, xb, kb, st, ks, sc0, gbase):
        self.I(POOL, "memset", st[:, sc0:sc0 + 1], 0.0, writes=[ks])
        self.I(ACT, "activation", xb[0:ntok, :], xt[0:ntok, :], AF.Square, accum_out=st[0:ntok, sc0:sc0 + 1], reads=[kx], writes=[kb, ks])
        self.I(DVE, "tensor_scalar", st[0:ntok, sc0 + 1:sc0 + 2], st[0:ntok, sc0:sc0 + 1], 1.0 / D, 1e-6, ALU.mult, ALU.add, reads=[ks], writes=[ks])
        self.I(ACT, "activation", st[0:ntok, sc0 + 2:sc0 + 3], st[0:ntok, sc0 + 1:sc0 + 2], AF.Sqrt, reads=[ks], writes=[ks])
        self.I(DVE, "reciprocal", st[0:ntok, sc0 + 3:sc0 + 4], st[0:ntok, sc0 + 2:sc0 + 3], reads=[ks], writes=[ks])
        self.I(ACT, "activation", xb[0:ntok, :], xt[0:ntok, :], AF.Copy, scale=st[0:ntok, sc0 + 3:sc0 + 4], reads=[kx, ks, kb], writes=[kb])
        for half in range(2):
            ps, pk = self.psum()
            pT = ps[:].bitcast(BF16).rearrange("p (a b) -> p a b", b=128)
            for k8 in range(8):
                kc = half * 8 + k8
                self.I(PE, "transpose", pT[:, k8, 0:ntok], xb[0:ntok, kc * 128:(kc + 1) * 128], self.cb[0:ntok, 0:ntok],
                       reads=[kb, "cb"], writes=[pk], track=(k8 == 7))
            gcol = self.pv[:, gbase + half * 8: gbase + half * 8 + 8].unsqueeze(2).to_broadcast([128, 8, ntok])
            self.I(DVE, "tensor_tensor", hT[:, half * 8:half * 8 + 8, :], pT[:, :, 0:ntok], gcol, ALU.mult, reads=[pk, "pv"], writes=[hkey])

    def stageC(self, es_outer):
        S = self.S
        pv, cf, cb = self.pv, self.cf, self.cb
        h2T = self.bufX
        NA = 1024 + NSM
        with ExitStack() as es:
            sb = lambda n, s, d=F32: self.sb(es, n, s, d)
            actT = sb("actT", [128, NJ, NA], BF16)
            with ExitStack() as es1:
                sb1 = lambda n, s, d=F32: self.sb(es1, n, s, d)
                wf = [sb1("wf%d" % i, [128, 16, 256], BF16) for i in range(2)]
                gt = [sb1("gt%d" % i, [128, NOWN]) for i in range(2)]
                up = [sb1("up%d" % i, [128, NOWN]) for i in range(2)]
                cv = sb1("cv", [128, NA])
                ge = sb1("ge", [128, NA])
                gs6 = sb1("gs6", [128, 16, 6])
                chT = sb1("chT", [128, NJ, 32])
                pco = sb1("pco", [128, NJ, 2])
                sco = sb1("sco", [128, NJ, 16, 2])
                ct = sb1("ct", [32, 1408])
                stc = self.st_conv.rearrange("b r c -> (b r) c")
                for pc in range(4):
                    S.dma(SP, ct[:], stc[:, pc * 1408:(pc + 1) * 1408], writes=["ct"])
                    ps, pk = self.psum()
                    for jj in range(11):
                        self.I(PE, "transpose", ps[:, jj * 32:(jj + 1) * 32], ct[:, jj * 128:(jj + 1) * 128], cf[0:32, 0:32],
                               reads=["ct", "cf"], writes=[pk], track=(jj == 10))
                    self.I(DVE, "tensor_copy", chT[:, pc * 11:(pc + 1) * 11, :], ps[:, 0:352].rearrange("p (a b) -> p a b", b=32), reads=[pk], writes=["chT"])

                def issue(j):
                    slot = j % 2
                    S.dma(POOL, wf[slot][:], self.wF[:, j, :].rearrange("(k p) n -> p k n", p=128), writes=["wf%d" % slot])
                cwc = lambda j, t: pv[:, V_CW + 4 * j + t: V_CW + 4 * j + t + 1]
                issue(0)
                for j in range(NJ):
                    if j + 1 < NJ:
                        issue(j + 1)
                    w, wk = wf[j % 2], "wf%d" % (j % 2)
                    g_, kg = gt[j % 2], "gt%d" % (j % 2)
                    u_, ku = up[j % 2], "up%d" % (j % 2)
                    for (t0, tn) in self.tok_groups():
                        psg, pkg = self.psum()
                        psu, pku = self.psum()
                        for k in range(16):
                            self.I(PE, "matmul", psg[:, 0:tn], w[:, k, 0:128], h2T[:, k, t0:t0 + tn], start=(k == 0), stop=(k == 15),
                                   reads=[wk, "h2T"], writes=[pkg], track=(k == 15))
                        for k in range(16):
                            self.I(PE, "matmul", psu[:, 0:tn], w[:, k, 128:256], h2T[:, k, t0:t0 + tn], start=(k == 0), stop=(k == 15),
                                   reads=[wk, "h2T"], writes=[pku], track=(k == 15))
                        self.I(ACT, "copy", g_[:, t0:t0 + tn], psg[:, 0:tn], reads=[pkg], writes=[kg])
                        self.I(DVE, "tensor_copy", u_[:, t0:t0 + tn], psu[:, 0:tn], reads=[pku], writes=[ku])
                    self.I(POOL, "tensor_scalar", g_[:, 30:32], g_[:, 30:32], pv[:, V_FLAG:V_FLAG + 1], None, ALU.mult, reads=[kg, "pv"], writes=[kg])
                    self.I(ACT, "activation", cv[:, 0:1024], g_[:, 32:1056], AF.Identity, bias=cwc(j, 3), scale=cwc(j, 2), reads=[kg, "pv"], writes=["cv"])
                    self.I(DVE, "scalar_tensor_tensor", cv[:, 0:1024], g_[:, 31:1055], cwc(j, 1), cv[:, 0:1024], ALU.mult, ALU.add, reads=[kg, "cv", "pv"], writes=["cv"])
                    self.I(DVE, "scalar_tensor_tensor", cv[:, 0:1024], g_[:, 30:1054], cwc(j, 0), cv[:, 0:1024], ALU.mult, ALU.add, reads=[kg, "cv", "pv"], writes=["cv"])
                    gss = g_[:, 1056:NOWN].rearrange("p (b t) -> p b t", t=4)
                    self.I(POOL, "tensor_copy", gs6[:, :, 0:2], chT[:, j, :].rearrange("p (b r) -> p b r", r=2), reads=["chT"], writes=["gs6"])
                    self.I(POOL, "tensor_copy", gs6[:, :, 2:6], gss, reads=[kg], writes=["gs6"])
                    cvs = cv[:, 1024:NA].rearrange("p (b t) -> p b t", t=4)
                    self.I(ACT, "activation", cvs, gs6[:, :, 2:6], AF.Identity, bias=cwc(j, 3), scale=cwc(j, 2), reads=["gs6", "pv"], writes=["cv"])
                    self.I(DVE, "scalar_tensor_tensor", cvs, gs6[:, :, 1:5], cwc(j, 1), cvs, ALU.mult, ALU.add, reads=["gs6", "cv", "pv"], writes=["cv"])
                    self.I(DVE, "scalar_tensor_tensor", cvs, gs6[:, :, 0:4], cwc(j, 0), cvs, ALU.mult, ALU.add, reads=["gs6", "cv", "pv"], writes=["cv"])
                    self.I(POOL, "tensor_copy", pco[:, j, :], g_[:, 1054:1056], reads=[kg], writes=["pco"])
                    self.I(POOL, "tensor_copy", sco[:, j, :, :], gss[:, :, 2:4], reads=[kg], writes=["sco"])
                    self.I(ACT, "activation", ge[:, :], cv[:, :], AF.Gelu_apprx_tanh, reads=["cv"], writes=["ge"])
                    self.I(DVE, "tensor_tensor", actT[:, j, 0:1024], ge[:, 0:1024], u_[:, 32:1056], ALU.mult, reads=["ge", ku], writes=["actT"])
                    self.I(POOL, "tensor_tensor", actT[:, j, 1024:NA], ge[:, 1024:NA], u_[:, 1056:NOWN], ALU.mult, reads=["ge", ku], writes=["actT"])
                scr = self.stage_out("pco", pco[:], [128, NJ, 2], "pco")
                for r in range(2):
                    self.bg(self.o_pconv[r].rearrange("(j p) -> p j", p=128), scr[:, :, r], "pco")
                scrow = sb1("scrow", [32, 1408])
                for pc in range(4):
                    for j4 in range(0, 11, 4):
                        nj = min(4, 11 - j4)
                        ps, pk = self.psum()
                        for jj in range(nj):
                            j = pc * 11 + j4 + jj
                            self.I(PE, "transpose", ps[0:32, jj * 128:(jj + 1) * 128], sco[:, j, :, :].rearrange("p b r -> p (b r)"), cf[:, C_ID:C_ID + 128],
                                   reads=["sco", "cf"], writes=[pk], track=(jj == nj - 1))
                        self.I(DVE, "tensor_copy", scrow[:, j4 * 128:(j4 + nj) * 128], ps[0:32, 0:nj * 128], reads=[pk], writes=["scrow"])
                    S.dma(SP, self.o_sconv.rearrange("b r c -> (b r) c")[:, pc * 1408:(pc + 1) * 1408], scrow[:], reads=["scrow"], ring="spo")
                S.barrier()
            with ExitStack() as es2:
                sb2 = lambda n, s, d=F32: self.sb(es2, n, s, d)
                dummy = sb2("dmy2", [128, 2])
                self.I(POOL, "memset", dummy[:], 0.0, writes=["h2T", "dmy2"])
                bxf = self.bufX[:].rearrange("p a b -> p (a b)")
                wo = [bxf[:, i * 5632:(i + 1) * 5632].rearrange("p (k n) -> p k n", n=512) for i in range(2)]
                fo = sb2("fo", [128, 5, D])
                gp2 = sb2("gp2", [128, D])
                S.dma(SP, gp2[:], self.gpost[:, 1, :], writes=["gp2"])
                x1t = sb2("x1r", [128, D])
                st = sb2("stc", [128, 5, 8])
                xb = sb2("xbc", [128, 512], BF16)
                tiles = [(128 * i, 128) for i in range(8)] + [(1024, NSM)]
                sets = [tiles[0:5], tiles[5:9]]
                wn = 0
                for tset in sets:
                    self.I(POOL, "memset", st[:], 0.0, writes=["stc"])
                    for n in range(4):
                        banks = [self.psum() for _ in tset]
                        for kp in range(4):
                            slot = wn % 2
                            wn += 1
                            S.dma(POOL, wo[slot], self.wfo[kp * 1408:(kp + 1) * 1408, n * 512:(n + 1) * 512].rearrange("(k p) n -> p k n", p=128),
                                  writes=["wo%d" % slot])
                            for ti, (c0, nt) in enumerate(tset):
                                ps, pk = banks[ti]
                                for jj in range(11):
                                    j = kp * 11 + jj
                                    self.I(PE, "matmul", ps[0:nt, :], actT[:, j, c0:c0 + nt], wo[slot][:, jj, :], start=(j == 0), stop=(j == NJ - 1),
                                           reads=["actT", "wo%d" % slot], writes=[pk], track=(jj == 10))
                        for ti, (c0, nt) in enumerate(tset):
                            ps, pk = banks[ti]
                            self.I(DVE, "tensor_copy", fo[0:nt, ti, n * 512:(n + 1) * 512], ps[0:nt, :], reads=[pk], writes=["fo%d" % ti])
                            self.I(ACT, "activation", xb[0:nt, :], fo[0:nt, ti, n * 512:(n + 1) * 512], AF.Square, accum_out=st[0:nt, ti, n:n + 1], reads=["fo%d" % ti], writes=["xbc", "stc"])
                    for ti, (c0, nt) in enumerate(tset):
                        self.I(DVE, "tensor_reduce", st[0:nt, ti, 4:5], st[0:nt, ti, 0:4], AX.X, ALU.add, reads=["stc"], writes=["stc"])
                        self.I(DVE, "tensor_scalar", st[0:nt, ti, 5:6], st[0:nt, ti, 4:5], 1.0 / D, 1e-6, ALU.mult, ALU.add, reads=["stc"], writes=["stc"])
                        self.I(ACT, "activation", st[0:nt, ti, 6:7], st[0:nt, ti, 5:6], AF.Sqrt, reads=["stc"], writes=["stc"])
                        self.I(DVE, "reciprocal", st[0:nt, ti, 7:8], st[0:nt, ti, 6:7], reads=["stc"], writes=["stc"])
                        xc0 = 32 + c0
                        S.dma(SP, x1t[0:nt, :], self.x1s[xc0:xc0 + nt, :], reads=["x1s"], writes=["x1r"])
                        self.I(DVE, "scalar_tensor_tensor", fo[0:nt, ti, :], fo[0:nt, ti, :], st[0:nt, ti, 7:8], gp2[0:nt, :], ALU.mult, ALU.mult,
                               reads=["fo%d" % ti, "stc", "gp2"], writes=["fo%d" % ti])
                        self.I(POOL, "tensor_tensor", fo[0:nt, ti, :], fo[0:nt, ti, :], x1t[0:nt, :], ALU.add, reads=["fo%d" % ti, "x1r"], writes=["fo%d" % ti])
                        S.dma(SP, self.o_y[c0:c0 + nt, :], fo[0:nt, ti, :], reads=["fo%d" % ti], ring="spo")

    def make_hT(self, xrows, ntok, hT, hkey, slot):
        S = self.S
        xt, xb, st = self.xt[slot], self.xb[slot], self.xst[slot]
        kx, kb, ks = "xt%d" % slot, "xb%d" % slot, "xst%d" % slot
        S.dma(SP, xt[0:ntok, :], xrows, writes=[kx])
        self.I(POOL, "memset", st[:, 0:1], 0.0, writes=[ks])
        self.I(ACT, "activation", xb[0:ntok, :], xt[0:ntok, :], AF.Square, accum_out=st[0:ntok, 0:1], reads=[kx], writes=[kb, ks])
        self.I(DVE, "tensor_scalar", st[0:ntok, 1:2], st[0:ntok, 0:1], 1.0 / D, 1e-6, ALU.mult, ALU.add, reads=[ks], writes=[ks])
        self.I(ACT, "activation", st[0:ntok, 2:3], st[0:ntok, 1:2], AF.Sqrt, reads=[ks], writes=[ks])
        self.I(DVE, "reciprocal", st[0:ntok, 3:4], st[0:ntok, 2:3], reads=[ks], writes=[ks])
        self.I(ACT, "activation", xb[0:ntok, :], xt[0:ntok, :], AF.Copy, scale=st[0:ntok, 3:4], reads=[kx, ks, kb], writes=[kb])
        for half in range(2):
            ps, pk = self.psum()
            pT = ps[:].bitcast(BF16).rearrange("p (a b) -> p a b", b=128)
            for k8 in range(8):
                kc = half * 8 + k8
                self.I(PE, "transpose", pT[:, k8, 0:ntok], xb[0:ntok, kc * 128:(kc + 1) * 128],
                                                                self.cb[0:ntok, 0:ntok], reads=[kb, "cb"], writes=[pk], track=(k8 == 7))
            gcol = self.pv[:, V_G1 + half * 8: V_G1 + half * 8 + 8].unsqueeze(2).to_broadcast([128, 8, ntok])
            self.I(DVE, "tensor_tensor", hT[:, half * 8:half * 8 + 8, :], pT[:, :, 0:ntok], gcol, ALU.mult, reads=[pk, "pv"], writes=[hkey])

    def slab_stream(self, slabs):
        S = self.S
        n = len(slabs)

        def issue(i):
            ap, ncol = slabs[i]
            slot = self.wslot % 2
            self.wslot += 1
            key = "wb%d" % slot
            S.dma(POOL, self.wb[slot][:, :, 0:ncol], ap.rearrange("(k p) n -> p k n", p=128), writes=[key])
            return (self.wb[slot], key)
        cur = issue(0)
        for i in range(n):
            nxt = issue(i + 1) if i + 1 < n else None
            yield cur
            cur = nxt

    def proj(self, ps, pk, wt, wkey, c0, M, hT, hkey, N):
        S = self.S
        for k in range(16):
            self.I(PE, "matmul", ps[0:M, 0:N], wt[:, k, c0:c0 + M], hT[:, k, 0:N], start=(k == 0), stop=(k == 15), reads=[wkey, hkey], writes=[pk], track=(k == 15))

    def stageS(self):
        S = self.S
        pv, cf, cb = self.pv, self.cf, self.cb
        wA = self.wA_ap
        N = NSM
        with ExitStack() as es0:
            sb0 = lambda n, s, d=F32: self.sb(es0, n, s, d)
            sbon = sb0("sbon", [128, 8, N], BF16)
            sgst = sb0("sgst", [128, 8, N], BF16)
            self._stageS_proj(es0, sbon, sgst)
            self._stageS_scan(es0, sbon, sgst)

    def _stageS_proj(self, es0, sbon, sgst):
        S = self.S
        pv, cf, cb = self.pv, self.cf, self.cb
        wA = self.wA_ap
        N = NSM
        with ExitStack() as es:
            sb = lambda n, s, d=F32: self.sb(es, n, s, d)
            self.xt = [sb("xtS", [128, D])]
            self.xb = [sb("xbS", [128, D], BF16)]
            self.xst = [sb("xstS", [128, 4])]
            hT = sb("hTs", [128, 16, N], BF16)
            self.make_hT(self.xsamp, N, hT[:, :, :], "hTs", 0)
            S.dma(SP, self.scrH[:, :, 1056:NOWN], hT[:, :, :], reads=["hTs"], writes=["scrH_s"])
            shT = sb("shT", [128, 28, 16])
            sst = sb("sst", [16, DS])
            S.dma(SP, sst[:], self.st_shift, writes=["sst"])
            chunks = [(0, 64), (64, 64), (128, 128), (256, 32)] + [(288 + 128 * i, 128) for i in range(24)]
            ps, pk = self.psum()
            for ci, (o, M) in enumerate(chunks):
                self.I(PE, "transpose", ps[0:M, ci * 16:(ci + 1) * 16], sst[:, o:o + M], cf[0:16, 0:16], reads=["sst", "cf"], writes=[pk], track=(ci == 27))
            self.I(DVE, "tensor_copy", shT[:, :, :], ps[:, 0:448].rearrange("p (a b) -> p a b", b=16), reads=[pk], writes=["shT"])
            zls = sb("zls", [128, 28, 16])
            tw = sb("stw", [64, N], BF16)
            zam = sb("szam", [64, N], BF16)
            sga = sb("ssga", [128, N], BF16)
            sgb = sb("ssgb", [128, N], BF16)
            self.I(POOL, "memset", sgb[:], 0.0, writes=["ssgb"])
            tokS = sb("tokS", [N, 6, 1024])
            phys = {n: sb("s_" + n, [128, N]) for n in ("zc", "d", "zr", "zk", "zv", "es", "wd", "as", "kk", "kkn", "km", "dd", "av")}
            phys["hi"] = sb("s_hi", [128, N], BF16)
            phys["lo"] = sb("s_lo", [128, N], BF16)
            T = _TAlias(phys, {"kk2": "d", "rn": "zc", "t1": "dd", "ta": "es", "rk": "wd2"}, prefix="s_")
            phys["wd2"] = sb("s_wd2", [128, N])

            def mix_s(ps, pk, M, ci, out, kout, mu):
                zc, d = T["zc"], T["d"]
                v3 = lambda t: t[0:M, 0:N].rearrange("p (b t) -> p b t", t=4)
                self.I(ACT, "copy", zc[0:M, 0:N], ps[0:M, 0:N], reads=[pk], writes=["s_zc"])
                self.I(DVE, "tensor_tensor", v3(d)[:, :, 1:4], v3(zc)[:, :, 0:3], v3(zc)[:, :, 1:4], ALU.subtract, reads=["s_zc"], writes=["s_d"])
                self.I(DVE, "tensor_tensor", v3(d)[:, :, 0], shT[0:M, ci, :], v3(zc)[:, :, 0], ALU.subtract, reads=["s_zc", "shT"], writes=["s_d"])
                self.I(DVE, "scalar_tensor_tensor", out[0:M, 0:N], d[0:M, 0:N], mu, zc[0:M, 0:N], ALU.mult, ALU.add, reads=["s_d", "s_zc", "pv"], writes=[kout])
                self.I(POOL, "tensor_copy", zls[0:M, ci, :], v3(zc)[:, :, 3], reads=["s_zc"], writes=["zls"])

            slabs = [(wA[:, 0:288], 288)] + [(wA[:, 288 + q * 384: 288 + (q + 1) * 384], 384) for q in range(8)]
            stream = self.slab_stream(slabs)
            wt, wk = next(stream)
            for (ci, (cc0, M, dst, dk, func)) in enumerate(((0, 64, tw, "stw", AF.Tanh), (64, 64, zam, "szam", AF.Copy),
                                                             (128, 128, sga, "ssga", AF.Sigmoid), (256, 32, sgb, "ssgb", AF.Sigmoid))):
                ps, pk = self.psum()
                self.proj(ps, pk, wt, wk, cc0, M, hT, "hTs", N)
                mix_s(ps, pk, M, ci, T["zr"], "s_zr", pv[0:M, ci:ci + 1])
                self.I(ACT, "activation", dst[0:M, 0:N], T["zr"][0:M, 0:N], func, reads=["s_zr"], writes=[dk])
            bonesb = cb[:, C_BONES:C_BONES + 128]
            for q in range(8):
                wt, wk = next(stream)
                col = lambda base: pv[:, base + q: base + q + 1]
                qs = slice(q * 128, (q + 1) * 128)
                for t, nm in enumerate(("zr", "zk", "zv")):
                    ps, pk = self.psum()
                    self.proj(ps, pk, wt, wk, t * 128, 128, hT, "hTs", N)
                    mix_s(ps, pk, 128, 4 + 3 * q + t, T[nm], "s_" + nm, pv[:, V_MU + 3 * q + t: V_MU + 3 * q + t + 1])
                zr, zk, zv = T["zr"], T["zk"], T["zv"]
                ps, pk = self.psum()
                self.I(PE, "matmul", ps[:, 0:N], self.w2b[:, qs], tw[:, 0:N], start=True, stop=True, reads=["w2b", "stw"], writes=[pk])
                self.I(ACT, "activation", T["es"][:, 0:N], ps[:, 0:N], AF.Sigmoid, bias=col(V_W0), reads=[pk, "pv"], writes=["s_es"])
                self.I(ACT, "activation", T["wd"][:, 0:N], T["es"][:, 0:N], AF.Exp, scale=-DECAY_C, reads=["s_es"], writes=["s_wd"])
                ps, pk = self.psum()
                self.I(PE, "matmul", ps[:, 0:N], self.a2b[:, qs], zam[:, 0:N], start=True, stop=True, reads=["a2b", "szam"], writes=[pk])
                self.I(ACT, "activation", T["as"][:, 0:N], ps[:, 0:N], AF.Sigmoid, bias=col(V_A0), reads=[pk, "pv"], writes=["s_as"])
                self.I(POOL, "tensor_scalar", T["kk"][:, 0:N], zk[:, 0:N], col(V_KK), None, ALU.mult, reads=["s_zk", "pv"], writes=["s_kk"])
                self.I(POOL, "tensor_tensor", T["kk2"][:, 0:N], T["kk"][:, 0:N], T["kk"][:, 0:N], ALU.mult, reads=["s_kk"], writes=["s_d"])
                ps, pk = self.psum()
                self.bsum(ps, pk, T, "kk2", N)
                self.I(ACT, "activation", T["rn"][:, 0:N], ps[:, 0:N], AF.Sqrt, reads=[pk], writes=["s_zc"])
                self.I(DVE, "tensor_scalar_max", T["rn"][:, 0:N], T["rn"][:, 0:N], 1e-12, reads=["s_zc"], writes=["s_zc"])
                self.I(DVE, "reciprocal", T["rn"][:, 0:N], T["rn"][:, 0:N], reads=["s_zc"], writes=["s_zc"])
                self.I(DVE, "tensor_tensor", T["kkn"][:, 0:N], T["kk"][:, 0:N], T["rn"][:, 0:N], ALU.mult, reads=["s_kk", "s_zc"], writes=["s_kkn"])
                self.I(POOL, "tensor_scalar", T["t1"][:, 0:N], T["as"][:, 0:N], -1.0, col(V_KA), ALU.add, ALU.mult, reads=["s_as", "pv"], writes=["s_dd"])
                self.I(POOL, "tensor_tensor", T["t1"][:, 0:N], T["t1"][:, 0:N], zk[:, 0:N], ALU.mult, reads=["s_dd", "s_zk"], writes=["s_dd"])
                self.I(POOL, "tensor_tensor", T["km"][:, 0:N], T["t1"][:, 0:N], zk[:, 0:N], ALU.add, reads=["s_dd", "s_zk"], writes=["s_km"])
                self.I(POOL, "tensor_tensor", T["ta"][:, 0:N], T["kkn"][:, 0:N], T["as"][:, 0:N], ALU.mult, reads=["s_kkn", "s_as"], writes=["s_es"])
                self.I(POOL, "tensor_scalar", T["av"][:, 0:N], T["kkn"][:, 0:N], -1.0, None, ALU.mult, reads=["s_kkn"], writes=["s_av"])
                self.I(DVE, "scalar_tensor_tensor", T["rk"][:, 0:N], zr[:, 0:N], col(V_RK), T["km"][:, 0:N], ALU.mult, ALU.mult, reads=["s_zr", "s_km", "pv"], writes=["s_wd2"])
                ps, pk = self.psum()
                self.bsum(ps, pk, T, "rk", N)
                self.I(DVE, "tensor_tensor", sbon[:, q, :], ps[:, 0:N], zv[:, 0:N], ALU.mult, reads=[pk, "s_zv"], writes=["sbon"])
                self.I(POOL, "tensor_scalar", sbon[:, q, :], sbon[:, q, :], col(V_LB), None, ALU.add, reads=["sbon", "pv"], writes=["sbon"])
                ps, pk = self.psum()
                self.I(PE, "matmul", ps[:, 0:N], self.g2a[:, qs], sga[:, 0:N], start=True, stop=False, reads=["g2a", "ssga"], writes=[pk], track=False)
                self.I(PE, "matmul", ps[:, 0:N], self.g2b[:, qs], sgb[:, 0:N], start=False, stop=True, reads=["g2b", "ssgb"], writes=[pk])
                self.I(ACT, "copy", sgst[:, q, :], ps[:, 0:N], reads=[pk], writes=["sgst"])
                srcs = [("zr", "s_zr"), ("km", "s_km"), ("zv", "s_zv"), ("wd", "s_wd"), ("av", "s_av"), ("ta", "s_es")]
                psA, pkA = self.psum()
                psB, pkB = self.psum()
                for qi, (nm, kk_) in enumerate(srcs):
                    pp, ppk = (psA, pkA) if qi < 4 else (psB, pkB)
                    o = (qi % 4) * 128
                    self.I(PE, "transpose", pp[0:N, o:o + 128], T[nm][:, 0:N], cf[:, C_ID:C_ID + 128], reads=[kk_, "cf"], writes=[ppk], track=(qi in (3, 5)))
                self.I(DVE, "tensor_copy", tokS[:, 0:4, qs], psA[0:N, :].rearrange("p (a b) -> p a b", b=128), reads=[pkA], writes=["tokS"])
                self.I(ACT, "copy", tokS[:, 4:6, qs], psB[0:N, 0:256].rearrange("p (a b) -> p a b", b=128), reads=[pkB], writes=["tokS"])
            zrow = sb("zrow", [16, DS])
            for c0_ in range(0, 28, 4):
                ps, pk = self.psum()
                grp = list(enumerate(chunks))[c0_:c0_ + 4]
                for n_, (ci, (o, M)) in enumerate(grp):
                    self.I(PE, "transpose", ps[0:16, n_ * 128:n_ * 128 + M], zls[0:M, ci, :], cf[0:M, 0:M], reads=["zls", "cf"], writes=[pk], track=(n_ == len(grp) - 1))
                for n_, (ci, (o, M)) in enumerate(grp):
                    self.I(DVE, "tensor_copy", zrow[:, o:o + M], ps[0:16, n_ * 128:n_ * 128 + M], reads=[pk], writes=["zrow"])
            S.dma(SP, self.o_sshift, zrow[:], reads=["zrow"], ring="spo")
            S.dma(SP, self.scrS, tokS[:], reads=["tokS"], writes=["scrS"])
            S.barrier()

    def _stageS_scan(self, es0, sbon, sgst):
        S = self.S
        pv, cf, cb = self.pv, self.cf, self.cb
        N = NSM
        with ExitStack() as es:
            sb = lambda n, s, d=F32: self.sb(es, n, s, d)
            ytok = sb("ytok", [N, 1024])
            for bt in range(2):
                E = DVE
                kS, kq, kt, ky = "Sst%d" % bt, "qin%d" % bt, "stmp%d" % bt, "ysb%d" % bt
                Sst = sb(kS, [128, 64, 64])
                tmp = sb(kt, [128, 64, 64])
                qin = sb(kq, [128, 4, 6, 64])
                ysb = sb(ky, [128, 4, 64])
                sa = sb("sa%d" % bt, [128, 64])
                yq = sb("syq%d" % bt, [128, 4, 64])
                sst_ = sb("sstat%d" % bt, [128, 8, 4])
                S.dma(POOL, Sst[:].rearrange("p v k -> p (v k)"), self.st_wkv[bt * 128:(bt + 1) * 128, :], writes=[kS])
                for bl in range(8):
                    b = bt * 8 + bl
                    S.dma(SP, qin[16 * bl:16 * bl + 16, :, :, :], self.scrS[4 * b:4 * b + 4, :, :].rearrange("t q (h k) -> h t q k", k=64),
                          reads=["scrS"], writes=[kq + "_%d" % bl])
                bc_k = lambda t, qi: qin[:, t, qi, :].unsqueeze(1).to_broadcast([128, 64, 64])
                kqs = [kq + "_%d" % bl for bl in range(8)]
                for t in range(4):
                    self.I(E, "tensor_tensor", tmp[:], Sst[:], bc_k(t, 4), ALU.mult, reads=[kS] + kqs, writes=[kt])
                    self.I(DVE, "tensor_reduce", sa[:], tmp[:], AX.X, ALU.add, reads=[kt], writes=["sa%d" % bt])
                    self.I(E, "tensor_tensor", Sst[:], Sst[:], bc_k(t, 3), ALU.mult, reads=[kS] + kqs, writes=[kS])
                    self.I(E, "tensor_tensor", tmp[:], sa[:].unsqueeze(2).to_broadcast([128, 64, 64]), bc_k(t, 5), ALU.mult, reads=["sa%d" % bt] + kqs, writes=[kt])
                    self.I(E, "tensor_tensor", Sst[:], Sst[:], tmp[:], ALU.add, reads=[kS, kt], writes=[kS])
                    self.I(E, "tensor_tensor", tmp[:], qin[:, t, 2, :].unsqueeze(2).to_broadcast([128, 64, 64]), bc_k(t, 1), ALU.mult, reads=kqs, writes=[kt])
                    self.I(E, "tensor_tensor", Sst[:], Sst[:], tmp[:], ALU.add, reads=[kS, kt], writes=[kS])
                    self.I(E, "tensor_tensor", tmp[:], Sst[:], bc_k(t, 0), ALU.mult, reads=[kS] + kqs, writes=[kt])
                    self.I(DVE, "tensor_reduce", ysb[:, t, :], tmp[:], AX.X, ALU.add, reads=[kt], writes=[ky])
                S.dma(SP, self.o_swkv[bt * 128:(bt + 1) * 128, :], Sst[:].rearrange("p v k -> p (v k)"), reads=[kS], ring="spo")
                st_ = sst_
                ks_ = "sstat%d" % bt
                self.I(DVE, "tensor_reduce", st_[:, 0, :], ysb[:], AX.X, ALU.add, reads=[ky], writes=[ks_])
                self.I(E, "tensor_tensor", yq[:], ysb[:], ysb[:], ALU.mult, reads=[ky], writes=["syq%d" % bt])
                self.I(DVE, "tensor_reduce", st_[:, 1, :], yq[:], AX.X, ALU.add, reads=["syq%d" % bt], writes=[ks_])
                self.I(E, "tensor_scalar", st_[:, 2, :], st_[:, 0, :], 1.0 / 64, None, ALU.mult, reads=[ks_], writes=[ks_])
                self.I(E, "tensor_tensor", st_[:, 3, :], st_[:, 2, :], st_[:, 2, :], ALU.mult, reads=[ks_], writes=[ks_])
                self.I(E, "tensor_scalar", st_[:, 4, :], st_[:, 1, :], 1.0 / 64, 64e-5, ALU.mult, ALU.add, reads=[ks_], writes=[ks_])
                self.I(E, "tensor_tensor", st_[:, 4, :], st_[:, 4, :], st_[:, 3, :], ALU.subtract, reads=[ks_], writes=[ks_])
                self.I(ACT, "activation", st_[:, 5, :], st_[:, 4, :], AF.Sqrt, reads=[ks_], writes=[ks_])
                self.I(DVE, "reciprocal", st_[:, 6, :], st_[:, 5, :], reads=[ks_], writes=[ks_])
                self.I(E, "tensor_tensor", yq[:], ysb[:], st_[:, 2, :].unsqueeze(2).to_broadcast([128, 4, 64]), ALU.subtract, reads=[ky, ks_], writes=["syq%d" % bt])
                self.I(E, "tensor_tensor", yq[:], yq[:], st_[:, 6, :].unsqueeze(2).to_broadcast([128, 4, 64]), ALU.mult, reads=["syq%d" % bt, ks_], writes=["syq%d" % bt])
                for bl in range(8):
                    b = bt * 8 + bl
                    S.dma(SP, self.scrY[4 * b:4 * b + 4, :].rearrange("t (h v) -> h t v", v=64), yq[16 * bl:16 * bl + 16, :, :], reads=["syq%d" % bt], writes=["scrY%d" % b])
            S.dma(SP, ytok[:], self.scrY, reads=["scrY%d" % b for b in range(16)], writes=["ytok"])
            if "sdbg" in self.dbg:
                S.dma(SP, self.dout("d_ytok", [N, 1024]), ytok[:], reads=["ytok"], ring="spo")
                dbf = sb("dbf", [128, 2, 8, N])
                self.I(DVE, "tensor_copy", dbf[:, 0], sbon[:], reads=["sbon"], writes=["dbf"])
                self.I(DVE, "tensor_copy", dbf[:, 1], sgst[:], reads=["sgst"], writes=["dbf"])
                S.dma(SP, self.dout("d_sbg", [128, 2, 8, N]), dbf[:], reads=["dbf"], ring="spo")
            ytb = sb("ytb", [N, 1024], BF16)
            self.I(ACT, "copy", ytb[:], ytok[:], reads=["ytok"], writes=["ytb"])
            ps, pk = self.psum()
            pT = ps[:].bitcast(BF16).rearrange("p (a b) -> p a b", b=128)[:, :, 0:N]
            for q in range(8):
                self.I(PE, "transpose", pT[:, q, :], ytb[:, q * 128:(q + 1) * 128], cb[0:N, 0:N], reads=["ytb", "cb"], writes=[pk], track=(q == 7))
            yt = sb("syt", [128, 8, N])
            lw = pv[:, V_LW: V_LW + 8].unsqueeze(2).to_broadcast([128, 8, N])
            self.I(DVE, "tensor_tensor", yt[:], pT, lw, ALU.mult, reads=[pk, "pv"], writes=["syt"])
            self.I(POOL, "tensor_tensor", yt[:], yt[:], sbon[:], ALU.add, reads=["syt", "sbon"], writes=["syt"])
            self.I(DVE, "tensor_tensor", self.yaT[:, :, 1056:NOWN], yt[:], sgst[:], ALU.mult, reads=["syt", "sgst"], writes=["yaT"])
            S.barrier()

    def stageA(self, es0, xseq, wA):
        S = self.S
        pv, cf, cb = self.pv, self.cf, self.cb
        with ExitStack() as es:
            sb = lambda n, s, d=F32: self.sb(es, n, s, d)
            self.xt = [sb("xt%d" % i, [128, D]) for i in range(1)]
            self.xb = [sb("xb%d" % i, [128, D], BF16) for i in range(1)]
            self.xst = [sb("xst%d" % i, [128, 4]) for i in range(1)]
            hT = sb("hTb", [128, 16, BLK], BF16)
            zl = sb("zl", [128, 28])
            self.I(POOL, "memset", zl[:], 0.0, writes=["zl"])
            tw = sb("tw", [64, BLK], BF16)
            zam = sb("zam", [64, BLK], BF16)
            sga = sb("sga", [128, BLK], BF16)
            sgb = sb("sgb", [128, BLK], BF16)
            self.I(POOL, "memset", sgb[:], 0.0, writes=["sgb"])
            NCK = BLK // CH
            ARBD = sb("ARBD", [128, 4, NCK, 2, 128], BF16)
            BBD = sb("BBD", [128, 4, NCK, 128], BF16)
            KBD = sb("KBD", [128, 4, NCK, 128], BF16)
            VBD = sb("VBD", [128, 4, NCK, 128], BF16)
            for (t, k) in ((ARBD, "ARBD"), (BBD, "BBD"), (KBD, "KBD"), (VBD, "VBD")):
                self.I(POOL, "memset", t[:], 0.0, writes=[k + "%d" % q for q in range(4)])
            PCs = sb("PCs", [128, 4, NCK])
            bon = sb("bon", [128, 4, BLK], BF16)
            gst = sb("gst", [128, 4, BLK], BF16)
            Hf = [sb("Hf%d" % g, [128, 4, 64]) for g in range(2)]
            Hb = [sb("Hb%d" % g, [128, 4, 64], BF16) for g in range(2)]
            for g in range(2):
                self.I(POOL, "memset", Hf[g][:], 0.0, writes=["Hf%d" % g])
                self.I(POOL, "memset", Hb[g][:], 0.0, writes=["Hb%d" % g])
            names = ("zc", "d", "zr", "zk", "zv", "es", "cum", "dd", "pinv", "prr", "pa", "as", "kk", "kkn", "km")
            phys = {n: sb("t_" + n, [128, NH]) for n in names[:8]}
            spare = self.bufX[:, 8:16, :].rearrange("p a b -> p (a b)")
            sparef = spare.bitcast(F32)
            for i, n in enumerate(names[8:]):
                phys[n] = sparef[:, i * NH:(i + 1) * NH]
            phys["hi"] = spare[:, 7 * 2 * NH: 7 * 2 * NH + NH]
            phys["lo"] = spare[:, 7 * 2 * NH + NH: 7 * 2 * NH + 2 * NH]
            T = _TAlias(phys, {"kk2": "d", "rn": "zc", "t1": "dd", "ta": "es", "rk": "cum"})
            sc = {}
            for s in range(2):
                sc["NA1", s] = sb("NA1_%d" % s, [128, 4, 256], BF16)
                sc["NA2", s] = sb("NA2_%d" % s, [128, 4, 256], BF16)
                for i in range(2):
                    sc["Q", s, i] = sb("Q_%d%d" % (s, i), [128, 4, 128], BF16)
                    if s == 0:
                        sc["NL", s, i] = sb("NL_%d%d" % (s, i), [128, 4, 128], BF16)
                        sc["TU", s, i] = sb("TU_%d%d" % (s, i), [128, 4, 128], BF16)
                sc["B2", s] = sb("B2_%d" % s, [128, 4, 128], BF16)
                sc["K2", s] = sb("K2_%d" % s, [128, 4, 128], BF16)
                sc["V2", s] = sb("V2_%d" % s, [128, 4, 64], BF16)
                if s == 0:
                    sc["X2", s] = sb("X2_%d" % s, [128, 4, 64], BF16)
                    sc["U2", s] = sb("U2_%d" % s, [128, 4, 64], BF16)
                    sc["YBD", s] = sb("YBD_%d" % s, [128, 4, 128], BF16)
                    self.I(POOL, "memset", sc["YBD", s][:], 0.0, writes=["YBD_%d" % s])
                    sc["ys", s] = sb("ys_%d" % s, [128, 4, 64])
                    sc["yq", s] = sb("yq_%d" % s, [128, 4, 64])
                    sc["st", s] = sb("yst_%d" % s, [128, 8, 4])
                    sc["yt", s] = sb("yt_%d" % s, [128, 4, 64])
                else:
                    for nm in ("X2", "U2", "YBD", "ys", "yq", "st", "yt"):
                        sc[nm, s] = sc[nm, 0]
            tmpH = sb("tmpH", [128, 4, 64])

            nblk = NPR // BLK
            lr_slab = (wA[:, 0:288], 288)
            pair_slabs = [(wA[:, 288 + q * 384: 288 + (q + 1) * 384], 384) for q in range(8)]
            slabs = []
            for blk in range(nblk):
                slabs.append(lr_slab)
                slabs += pair_slabs
            stream = self.slab_stream(slabs)

            for blk in range(nblk):
                c0 = blk * BLK
                N = BLK
                for t4 in range(4):
                    self.make_hT(xseq[c0 + t4 * 128: c0 + (t4 + 1) * 128, :], 128, hT[:, :, t4 * 128:(t4 + 1) * 128], "hTb", 0)
                self.ckpt("hT")
                if blk == 1:
                    S.dma(SP, self.scrH[:, :, 0:32], hT[:, :, BLK - 32:BLK], reads=["hTb"], writes=["scrH_1"])
                elif blk >= 2:
                    S.dma(SP, self.scrH[:, :, 32 + (blk - 2) * BLK: 32 + (blk - 1) * BLK], hT[:, :, :], reads=["hTb"], writes=["scrH_%d" % blk])
                wt, wk = next(stream)
                for (ci, (cc0, M, dst, dk, func)) in enumerate(((0, 64, tw, "tw", AF.Tanh), (64, 64, zam, "zam", AF.Copy),
                                                                 (128, 128, sga, "sga", AF.Sigmoid), (256, 32, sgb, "sgb", AF.Sigmoid))):
                    ps, pk = self.psum()
                    self.proj(ps, pk, wt, wk, cc0, M, hT, "hTb", N)
                    for co in range(0, N, NH):
                        self.mix(ps[:, co:co + NH], pk, M, NH, ci, T, zl)
                        self.I(ACT, "activation", dst[0:M, co:co + NH], T["zr"][0:M, 0:NH], func, reads=["t_zr"], writes=[dk])
                self.ckpt("lowrank")
                for g in range(2):
                    for qg in range(4):
                        q = g * 4 + qg
                        wt, wk = next(stream)
                        self.prep_pair(wt, wk, hT, N, q, qg, T, zl, tw, zam, sga, sgb, ARBD, BBD, KBD, VBD, PCs, bon, gst, blk)
                        self.ckpt("prep0")
                    self.ckpt("prep")
                    self.scan_group(g, blk, NCK, sc, ARBD, BBD, KBD, VBD, PCs, bon, gst, Hf[g], Hb[g], tmpH)
            scr = self.stage_out("zl", zl[:], [128, 28], "zl")
            self.bg(self.o_pshift[0:64].rearrange("(p o) -> p o", o=1), scr[0:64, 0:1], "zl")
            self.bg(self.o_pshift[64:128].rearrange("(p o) -> p o", o=1), scr[0:64, 1:2], "zl")
            self.bg(self.o_pshift[128:256].rearrange("(p o) -> p o", o=1), scr[:, 2:3], "zl")
            self.bg(self.o_pshift[256:288].rearrange("(p o) -> p o", o=1), scr[0:32, 3:4], "zl")
            self.bg(self.o_pshift[288:DS].rearrange("(j p) -> p j", p=128), scr[:, 4:28], "zl")
            for g in range(2):
                hh = sb("hh%d" % g, [128, 4, 64], BF16)
                hl = sb("hl%d" % g, [128, 4, 64], BF16)
                self.I(POOL, "tensor_copy", hh[:], Hf[g][:], reads=["Hf%d" % g], writes=["hh%d" % g])
                self.I(POOL, "tensor_tensor", hl[:], Hf[g][:], hh[:], ALU.subtract, reads=["Hf%d" % g, "hh%d" % g], writes=["hl%d" % g])
                ps, pk = self.psum()
                for qg in range(4):
                    self.I(PE, "matmul", ps[0:64, qg * 128:(qg + 1) * 128], hh[:, qg, :], cb[:, 0:128], start=True, stop=False, reads=["hh%d" % g, "cb"], writes=[pk], track=False)
                    self.I(PE, "matmul", ps[0:64, qg * 128:(qg + 1) * 128], hl[:, qg, :], cb[:, 0:128], start=False, stop=True, reads=["hl%d" % g, "cb"], writes=[pk], track=(qg == 3))
                so = sb("so%d" % g, [64, 4, 2, 64])
                self.I(DVE, "tensor_copy", so[:], ps[0:64, :].rearrange("p (a b c) -> p a b c", a=4, b=2), reads=[pk], writes=["so%d" % g])
                S.dma(SP, self.o_pwkv[g * 8:(g + 1) * 8].rearrange("(q s) v k -> v q s k", s=2), so[:], reads=["so%d" % g], ring="spo")
            S.barrier()

    def mix(self, ps, pk, M, N, ci, T, zl):
        S = self.S
        zc, d, out = T["zc"], T["d"], T["zr"]
        self.mix_to(ps, pk, M, N, ci, zl, zc, "t_zc", d, "t_d", out, "t_zr", self.pv[0:M, ci:ci + 1] if ci < 4 else None)

    def mix_to(self, ps, pk, M, N, ci, zl, zc, kzc, d, kd, out, kout, mu):
        S = self.S
        self.I(ACT, "copy", zc[0:M, 0:N], ps[0:M, 0:N], reads=[pk], writes=[kzc])
        self.I(DVE, "tensor_tensor", d[0:M, 1:N], zc[0:M, 0:N - 1], zc[0:M, 1:N], ALU.subtract, reads=[kzc], writes=[kd])
        self.I(DVE, "tensor_tensor", d[0:M, 0:1], zl[0:M, ci:ci + 1], zc[0:M, 0:1], ALU.subtract, reads=[kzc, "zl"], writes=[kd])
        self.I(DVE, "scalar_tensor_tensor", out[0:M, 0:N], d[0:M, 0:N], mu, zc[0:M, 0:N], ALU.mult, ALU.add, reads=[kd, kzc, "pv"], writes=[kout])
        self.I(POOL, "tensor_copy", zl[0:M, ci:ci + 1], zc[0:M, N - 1:N], reads=[kzc], writes=["zl"])

    def bsum(self, ps, pk, T, name, N):
        S = self.S
        src, ksrc = T[name], T.key(name)
        bonesb = self.cb[:, C_BONES:C_BONES + 128]
        khi, klo = T.key("hi"), T.key("lo")
        self.I(ACT, "copy", T["hi"][:, 0:N], src[:, 0:N], reads=[ksrc], writes=[khi])
        self.I(POOL, "tensor_tensor", T["lo"][:, 0:N], src[:, 0:N], T["hi"][:, 0:N], ALU.subtract, reads=[ksrc, khi], writes=[klo])
        self.I(PE, "matmul", ps[:, 0:N], bonesb, T["hi"][:, 0:N], start=True, stop=False, reads=["cb", khi], writes=[pk], track=False)
        self.I(PE, "matmul", ps[:, 0:N], bonesb, T["lo"][:, 0:N], start=False, stop=True, reads=["cb", klo], writes=[pk])

    def prep_pair(self, wt, wk, hT, N, q, qg, T, zl, tw, zam, sga, sgb, ARBD, BBD, KBD, VBD, PCs, bon, gst, blk):
        S = self.S
        pv, cf = self.pv, self.cf
        col = lambda base: pv[:, base + q: base + q + 1]
        qs = slice(q * 128, (q + 1) * 128)
        need_y = blk >= 1
        W = N
        co = 0
        cw = slice(0, W)
        psw, pkw = self.psum()
        self.I(PE, "matmul", psw[:, 0:W], self.w2b[:, qs], tw[:, cw], start=True, stop=True, reads=["w2b", "tw"], writes=[pkw])
        psa, pka = self.psum()
        self.I(PE, "matmul", psa[:, 0:W], self.a2b[:, qs], zam[:, cw], start=True, stop=True, reads=["a2b", "zam"], writes=[pka])
        pj = []
        for t in range(3):
            if t == 0 and not need_y:
                pj.append(None)
                continue
            ps, pk = self.psum()
            self.proj(ps, pk, wt, wk, t * 128, 128, hT, "hTb", N)
            pj.append((ps, pk))
        self.I(ACT, "activation", T["es"][:, 0:W], psw[:, 0:W], AF.Sigmoid, bias=col(V_W0), reads=[pkw, "pv"], writes=["t_es"])
        self.I(ACT, "activation", T["as"][:, 0:W], psa[:, 0:W], AF.Sigmoid, bias=col(V_A0), reads=[pka, "pv"], writes=["t_as"])
        self.I(DVE, "tensor_tensor_scan", T["cum"][:, 0:W], cf[:, C_M01:C_M01 + W], T["es"][:, 0:W], 0.0, ALU.mult, ALU.add, reads=["t_es", "cf"], writes=["t_cum"])
        self.I(POOL, "tensor_tensor", T["dd"][:, 0:W], T["cum"][:, 0:W], T["es"][:, 0:W], ALU.subtract, reads=["t_cum", "t_es"], writes=["t_dd"])
        nck = W // CH
        ck0 = 0

        def mixq(t, nm):
            ps, pk = pj[t]
            ci = 4 + 3 * q + t
            self.mix_to(ps[:, cw], pk, 128, W, ci, zl, T["zc"], "t_zc", T["d"], "t_d", T[nm], "t_" + nm, pv[:, V_MU + 3 * q + t: V_MU + 3 * q + t + 1])
        if need_y:
            mixq(0, "zr")
        self.I(ACT, "activation", T["pinv"][:, 0:W], T["cum"][:, 0:W], AF.Exp, scale=DECAY_C, reads=["t_cum"], writes=["t_pinv"])
        self.I(ACT, "activation", T["prr"][:, 0:W], T["cum"][:, 0:W], AF.Exp, scale=-DECAY_C, reads=["t_cum"], writes=["t_prr"])
        self.I(ACT, "activation", T["pa"][:, 0:W], T["dd"][:, 0:W], AF.Exp, scale=-DECAY_C, reads=["t_dd"], writes=["t_pa"])
        self.I(POOL, "tensor_copy", PCs[:, qg, ck0:ck0 + nck], T["prr"][:, 0:W].rearrange("p (c t) -> p c t", t=CH)[:, :, CH - 1], reads=["t_prr"], writes=["PCs%d" % qg])
        mixq(1, "zk")
        mixq(2, "zv")
        if True:
            zr, zk, zv = T["zr"], T["zk"], T["zv"]
            self.I(ACT, "activation", T["kk"][:, 0:W], zk[:, 0:W], AF.Copy, scale=col(V_KK), reads=["t_zk", "pv"], writes=["t_kk"])
            bonesb = self.cb[:, C_BONES:C_BONES + 128]
            self.I(ACT, "activation", T["hi"][:, 0:W], T["kk"][:, 0:W], AF.Square, reads=["t_kk"], writes=["t_hi"])
            ps, pk = self.psum()
            self.I(PE, "matmul", ps[:, 0:W], bonesb, T["hi"][:, 0:W], start=True, stop=True, reads=["cb", "t_hi"], writes=[pk])
            self.I(ACT, "activation", T["rn"][:, 0:W], ps[:, 0:W], AF.Sqrt, reads=[pk], writes=["t_zc"])
            self.I(DVE, "tensor_scalar_max", T["rn"][:, 0:W], T["rn"][:, 0:W], 1e-12, reads=["t_zc"], writes=["t_zc"])
            self.I(DVE, "reciprocal", T["rn"][:, 0:W], T["rn"][:, 0:W], reads=["t_zc"], writes=["t_zc"])
            self.I(DVE, "tensor_tensor", T["kkn"][:, 0:W], T["kk"][:, 0:W], T["rn"][:, 0:W], ALU.mult, reads=["t_kk", "t_zc"], writes=["t_kkn"])
            self.I(POOL, "tensor_scalar", T["t1"][:, 0:W], T["as"][:, 0:W], -1.0, col(V_KA), ALU.add, ALU.mult, reads=["t_as", "pv"], writes=["t_dd"])
            self.I(POOL, "tensor_tensor", T["t1"][:, 0:W], T["t1"][:, 0:W], zk[:, 0:W], ALU.mult, reads=["t_dd", "t_zk"], writes=["t_dd"])
            self.I(POOL, "tensor_tensor", T["km"][:, 0:W], T["t1"][:, 0:W], zk[:, 0:W], ALU.add, reads=["t_dd", "t_zk"], writes=["t_km"])
            self.I(POOL, "tensor_tensor", T["ta"][:, 0:W], T["kkn"][:, 0:W], T["as"][:, 0:W], ALU.mult, reads=["t_kkn", "t_as"], writes=["t_es"])
            if need_y:
                self.I(DVE, "scalar_tensor_tensor", T["lo"][:, 0:W], zr[:, 0:W], col(V_RK), T["km"][:, 0:W], ALU.mult, ALU.mult, reads=["t_zr", "t_km", "pv"], writes=["t_lo"])
                ps, pk = self.psum()
                self.I(PE, "matmul", ps[:, 0:W], bonesb, T["lo"][:, 0:W], start=True, stop=True, reads=["cb", "t_lo"], writes=[pk])
                self.I(DVE, "tensor_tensor", bon[:, qg, cw], ps[:, 0:W], zv[:, 0:W], ALU.mult, reads=[pk, "t_zv"], writes=["bon%d" % qg])
                self.I(ACT, "activation", bon[:, qg, cw], bon[:, qg, cw], AF.Identity, bias=col(V_LB), reads=["bon%d" % qg, "pv"], writes=["bon%d" % qg])
                ps, pk = self.psum()
                self.I(PE, "matmul", ps[:, 0:W], self.g2a[:, qs], sga[:, cw], start=True, stop=False, reads=["g2a", "sga"], writes=[pk], track=False)
                self.I(PE, "matmul", ps[:, 0:W], self.g2b[:, qs], sgb[:, cw], start=False, stop=True, reads=["g2b", "sgb"], writes=[pk])
                self.I(ACT, "copy", gst[:, qg, cw], ps[:, 0:W], reads=[pk], writes=["gst%d" % qg])
            for h in range(2):
                pr = slice(64 * h, 64 * h + 64)
                cs = slice(64 * h, 64 * h + 64)
                v3 = lambda t: t[pr, 0:W].rearrange("p (c t) -> p c t", t=CH)
                e1 = DVE if h == 0 else POOL
                cks = slice(ck0, ck0 + nck)
                self.I(DVE, "scalar_tensor_tensor", ARBD[pr, qg, cks, 0, cs], v3(T["kkn"]), -1.0, v3(T["pa"]), ALU.mult, ALU.mult, reads=["t_kkn", "t_pa"], writes=["ARBD%d" % qg])
                if need_y:
                    self.I(e1, "tensor_tensor", ARBD[pr, qg, cks, 1, cs], v3(zr), v3(T["prr"]), ALU.mult, reads=["t_zr", "t_prr"], writes=["ARBD%d" % qg])
                self.I(e1, "tensor_tensor", KBD[pr, qg, cks, cs], v3(T["km"]), v3(T["pinv"]), ALU.mult, reads=["t_km", "t_pinv"], writes=["KBD%d" % qg])
                self.I(e1, "tensor_tensor", BBD[pr, qg, cks, cs], v3(T["ta"]), v3(T["pinv"]), ALU.mult, reads=["t_es", "t_pinv"], writes=["BBD%d" % qg])
                self.I(ACT, "copy", VBD[pr, qg, cks, cs], v3(zv), reads=["t_zv"], writes=["VBD%d" % qg])

    def scan_pre(self, g, c, gc, sc, ARBD, BBD, KBD, VBD):
        cf, cb = self.cf, self.cb
        s = gc % 2
        kin = ["ARBD%d" % q for q in range(4)] + ["BBD%d" % q for q in range(4)] + ["KBD%d" % q for q in range(4)] + ["VBD%d" % q for q in range(4)]
        NA1, NA2 = sc["NA1", s], sc["NA2", s]
        kNA1, kNA2 = "NA1_%d" % s, "NA2_%d" % s
        identb = cb[:, 0:128]
        iselb = cb[:, C_ISEL:C_ISEL + 64]
        mask2 = cf[:, C_MUS:C_MUS + 256].unsqueeze(1).to_broadcast([128, 2, 256])
        for (dst, kd, L) in ((NA1, kNA1, BBD), (NA2, kNA2, KBD)):
            for hf in range(2):
                ps, pk = self.psum()
                for j in range(2):
                    q = 2 * hf + j
                    self.I(PE, "matmul", ps[:, j * 256:(j + 1) * 256], L[:, q, c, :], ARBD[:, q, c, :, :].rearrange("p a b -> p (a b)"),
                           start=True, stop=True, reads=kin, writes=[pk], track=(j == 1))
                self.I(DVE, "tensor_tensor", dst[:, 2 * hf:2 * hf + 2, :], ps[:].rearrange("p (a b) -> p a b", b=256), mask2, ALU.mult, reads=[pk, "cf"], writes=[kd])
                yield
        NL0, kNL0 = sc["NL", 0, 0], "NL_00"
        ps, pk = self.psum()
        for q in range(4):
            self.I(PE, "matmul", ps[:, q * 128:(q + 1) * 128], ARBD[:, q, c, 0, :], BBD[:, q, c, :], start=True, stop=True, reads=kin, writes=[pk], track=(q == 3))
        mL = cf[:, C_MLS:C_MLS + 128].unsqueeze(1).to_broadcast([128, 4, 128])
        self.I(DVE, "tensor_tensor", NL0[:], ps[:].rearrange("p (a b) -> p a b", b=128), mL, ALU.mult, reads=[pk, "cf"], writes=[kNL0])
        yield
        B2, K2, V2 = sc["B2", s], sc["K2", s], sc["V2", s]
        for (dst, kd, L) in ((B2, "B2_%d" % s, BBD), (K2, "K2_%d" % s, KBD)):
            ps, pk = self.psum()
            for q in range(4):
                self.I(PE, "matmul", ps[:, q * 128:(q + 1) * 128], L[:, q, c, :], identb, start=True, stop=True, reads=kin + ["cb"], writes=[pk], track=(q == 3))
            self.I(ACT, "copy", dst[:], ps[:].rearrange("p (a b) -> p a b", b=128), reads=[pk], writes=[kd])
            yield
        ps, pk = self.psum()
        for q in range(4):
            self.I(PE, "matmul", ps[:, q * 64:(q + 1) * 64], VBD[:, q, c, :], iselb, start=True, stop=True, reads=kin + ["cb"], writes=[pk], track=(q == 3))
        self.I(ACT, "copy", V2[:], ps[:, 0:256].rearrange("p (a b) -> p a b", b=64), reads=[pk], writes=["V2_%d" % s])
        yield
        TUc, kTU = NA1[:, :, 0:128], [kNA1]
        NLc, kNL = NL0, kNL0
        Qc, kQ = sc["Q", s, 0], "Q_%d0" % s
        idb4 = identb.unsqueeze(1).to_broadcast([128, 4, 128])
        self.I(POOL, "tensor_tensor", Qc[:], NA1[:, :, 0:128], idb4, ALU.add, reads=[kNA1, "cb"], writes=[kQ])
        for lvl in range(1, 6):
            i = lvl % 2
            NLn, kNLn = sc["NL", 0, i], "NL_0%d" % i
            TUn = sc["TU", 0, i]
            kTUn = ["TU_0%da" % i, "TU_0%db" % i]
            Qn, kQn = sc["Q", s, i], "Q_%d%d" % (s, i)
            psN, pkN = self.psum()
            for q in range(4):
                self.I(PE, "matmul", psN[:, q * 128:(q + 1) * 128], TUc[:, q, :], NLc[:, q, :], start=True, stop=True, reads=kTU + [kNL], writes=[pkN], track=(q == 3))
            if lvl < 5:
                psTa, pkTa = self.psum()
                psTb, pkTb = self.psum()
                for q in range(4):
                    pT_, pkT_ = (psTa, pkTa) if q < 2 else (psTb, pkTb)
                    self.I(PE, "matmul", pT_[:, (q % 2) * 128:(q % 2 + 1) * 128], NLc[:, q, :], TUc[:, q, :], start=True, stop=True, reads=kTU + [kNL], writes=[pkT_], track=(q % 2 == 1))
            self.I(ACT, "copy", NLn[:], psN[:].rearrange("p (a b) -> p a b", b=128), reads=[pkN], writes=[kNLn])
            if lvl < 5:
                self.I(ACT, "copy", TUn[:, 0:2, :], psTa[:, 0:256].rearrange("p (a b) -> p a b", b=128), reads=[pkTa], writes=[kTUn[0]])
                self.I(DVE, "tensor_copy", TUn[:, 2:4, :], psTb[:, 0:256].rearrange("p (a b) -> p a b", b=128), reads=[pkTb], writes=[kTUn[1]])
            yield
            psQ, pkQ = self.psum()
            for q in range(4):
                self.I(PE, "matmul", psQ[:, q * 128:(q + 1) * 128], NLn[:, q, :], Qc[:, q, :], start=True, stop=True, reads=[kNLn, kQ], writes=[pkQ], track=(q == 3))
            self.I(DVE, "tensor_tensor", Qn[:], psQ[:].rearrange("p (a b) -> p a b", b=128), Qc[:], ALU.add, reads=[pkQ, kQ], writes=[kQn])
            yield
            TUc, kTU, NLc, kNL, Qc, kQ = TUn, kTUn, NLn, kNLn, Qn, kQn

    def scan_chain(self, g, c, gc, sc, ARBD, PCs, bon, gst, Hf, Hb, tmpH):
        s = gc % 2
        kin = ["ARBD%d" % q for q in range(4)]
        NA1, NA2 = sc["NA1", s], sc["NA2", s]
        kNA1, kNA2 = "NA1_%d" % s, "NA2_%d" % s
        B2, K2, V2 = sc["B2", s], sc["K2", s], sc["V2", s]
        kB2, kK2, kV2 = "B2_%d" % s, "K2_%d" % s, "V2_%d" % s
        Qc, kQ = sc["Q", s, 1], "Q_%d1" % s
        kH = "Hf%d" % g
        kHb = "Hb%d" % g
        X2, U2 = sc["X2", 0], sc["U2", 0]
        ps, pk = self.psum()
        for q in range(4):
            self.I(PE, "matmul", ps[:, q * 64:(q + 1) * 64], ARBD[:, q, c, 0, :], Hb[:, q, :], start=True, stop=False, reads=kin + [kHb], writes=[pk], track=False)
            self.I(PE, "matmul", ps[:, q * 64:(q + 1) * 64], NA2[:, q, 0:128], V2[:, q, :], start=False, stop=True, reads=[kNA2, kV2], writes=[pk], track=(q == 3))
        self.I(ACT, "copy", X2[:], ps[:, 0:256].rearrange("p (a b) -> p a b", b=64), reads=[pk], writes=["X2_0"])
        yield
        ps, pk = self.psum()
        for q in range(4):
            self.I(PE, "matmul", ps[:, q * 64:(q + 1) * 64], Qc[:, q, :], X2[:, q, :], start=True, stop=True, reads=[kQ, "X2_0"], writes=[pk], track=(q == 3))
        self.I(ACT, "copy", U2[:], ps[:, 0:256].rearrange("p (a b) -> p a b", b=64), reads=[pk], writes=["U2_0"])
        yield
        need_y = gc >= OWN0 // CH
        if need_y:
            psY, pkY = self.psum()
            for q in range(4):
                self.I(PE, "matmul", psY[:, q * 64:(q + 1) * 64], ARBD[:, q, c, 1, :], Hb[:, q, :], start=True, stop=False, reads=kin + [kHb], writes=[pkY], track=False)
                self.I(PE, "matmul", psY[:, q * 64:(q + 1) * 64], NA1[:, q, 128:256], U2[:, q, :], start=False, stop=False, reads=[kNA1, "U2_0"], writes=[pkY], track=False)
                self.I(PE, "matmul", psY[:, q * 64:(q + 1) * 64], NA2[:, q, 128:256], V2[:, q, :], start=False, stop=True, reads=[kNA2, kV2], writes=[pkY], track=(q == 3))
        ps, pk = self.psum()
        for q in range(4):
            self.I(PE, "matmul", ps[:, q * 64:(q + 1) * 64], B2[:, q, :], U2[:, q, :], start=True, stop=False, reads=[kB2, "U2_0"], writes=[pk], track=False)
            self.I(PE, "matmul", ps[:, q * 64:(q + 1) * 64], K2[:, q, :], V2[:, q, :], start=False, stop=True, reads=[kK2, kV2], writes=[pk], track=(q == 3))
        self.I(DVE, "tensor_tensor", tmpH[:], Hf[:], ps[:, 0:256].rearrange("p (a b) -> p a b", b=64), ALU.add, reads=[pk, kH], writes=["tmpH"])
        pcb = PCs[:, :, c:c + 1].to_broadcast([128, 4, 64])
        self.I(DVE, "tensor_tensor", Hb[:], tmpH[:], pcb, ALU.mult, reads=["tmpH"] + ["PCs%d" % q for q in range(4)], writes=[kHb])
        self.I(POOL, "tensor_tensor", Hf[:], tmpH[:], pcb, ALU.mult, reads=["tmpH"] + ["PCs%d" % q for q in range(4)], writes=[kH])
        yield
        if need_y:
            yield from self.y_post(g, c, gc, s, sc, psY, pkY, bon, gst)

    def scan_group(self, g, blk, NCK, sc, ARBD, BBD, KBD, VBD, PCs, bon, gst, Hf, Hb, tmpH):
        def drain(gen):
            for _ in gen:
                pass

        def interleave(ga, gb):
            a_live = b_live = True
            while a_live or b_live:
                if a_live:
                    try:
                        next(ga)
                    except StopIteration:
                        a_live = False
                for _ in range(2):
                    if b_live:
                        try:
                            next(gb)
                        except StopIteration:
                            b_live = False
        gc0 = blk * NCK
        drain(self.scan_pre(g, 0, gc0, sc, ARBD, BBD, KBD, VBD))
        for c in range(NCK):
            ch = self.scan_chain(g, c, gc0 + c, sc, ARBD, PCs, bon, gst, Hf, Hb, tmpH)
            if c + 1 < NCK:
                interleave(ch, self.scan_pre(g, c + 1, gc0 + c + 1, sc, ARBD, BBD, KBD, VBD))
            else:
                drain(ch)

    def y_post(self, g, c, gc, s, sc, psY, pkY, bon, gst):
        S = self.S
        pv, cb = self.pv, self.cb
        ys, yq, st, yt, YBD = sc["ys", s], sc["yq", s], sc["st", s], sc["yt", s], sc["YBD", s]
        kys, kyq, kst, kyt, kY = "ys_0", "yq_0", "yst_0", "yt_0", "YBD_0"
        self.I(ACT, "copy", ys[:], psY[:, 0:256].rearrange("p (a b) -> p a b", b=64), reads=[pkY], writes=[kys])
        self.I(DVE, "tensor_reduce", st[:, 0, :], ys[:], AX.X, ALU.add, reads=[kys], writes=[kst])
        self.I(POOL, "tensor_tensor", yq[:], ys[:], ys[:], ALU.mult, reads=[kys], writes=[kyq])
        self.I(DVE, "tensor_reduce", st[:, 1, :], yq[:], AX.X, ALU.add, reads=[kyq], writes=[kst])
        yield
        self.I(DVE, "tensor_scalar", st[:, 2, :], st[:, 0, :], 1.0 / 64, None, ALU.mult, reads=[kst], writes=[kst])
        self.I(DVE, "tensor_tensor", st[:, 3, :], st[:, 2, :], st[:, 2, :], ALU.mult, reads=[kst], writes=[kst])
        self.I(DVE, "scalar_tensor_tensor", st[:, 4, :], st[:, 1, :], 1.0 / 64, st[:, 3, :], ALU.mult, ALU.subtract, reads=[kst], writes=[kst])
        self.I(DVE, "tensor_scalar", st[:, 4, :], st[:, 4, :], 64e-5, None, ALU.add, reads=[kst], writes=[kst])
        self.I(ACT, "activation", st[:, 5, :], st[:, 4, :], AF.Sqrt, reads=[kst], writes=[kst])
        self.I(DVE, "reciprocal", st[:, 6, :], st[:, 5, :], reads=[kst], writes=[kst])
        yield
        self.I(DVE, "tensor_tensor", yq[:], ys[:], st[:, 2, :].unsqueeze(2).to_broadcast([128, 4, 64]), ALU.subtract, reads=[kys, kst], writes=[kyq])
        for h in range(2):
            pr = slice(64 * h, 64 * h + 64)
            self.I(DVE if h == 0 else POOL, "tensor_tensor", YBD[pr, :, pr], yq[pr, :, :], st[pr, 6, :].unsqueeze(2).to_broadcast([64, 4, 64]), ALU.mult, reads=[kyq, kst], writes=[kY])
        yield
        ps, pk = self.psum()
        for q in range(4):
            self.I(PE, "matmul", ps[:, q * 64:(q + 1) * 64], YBD[:, q, :], cb[:, C_ISEL:C_ISEL + 64], start=True, stop=True, reads=[kY, "cb"], writes=[pk], track=(q == 3))
        t0 = 0
        col0 = gc * CH - OWN0
        if col0 < 0:
            t0 = -col0
            col0 = 0
        nt = CH - t0
        cl = c * CH + t0
        lw = pv[:, V_LW + 4 * g: V_LW + 4 * g + 4].unsqueeze(2).to_broadcast([128, 4, nt])
        psv = ps[:, 0:256].rearrange("p (a b) -> p a b", b=64)[:, :, t0:CH]
        self.I(DVE, "tensor_tensor", yt[:, :, 0:nt], psv, lw, ALU.mult, reads=[pk, "pv"], writes=[kyt])
        self.I(POOL, "tensor_tensor", yt[:, :, 0:nt], yt[:, :, 0:nt], bon[:, :, cl:cl + nt], ALU.add, reads=[kyt] + ["bon%d" % q for q in range(4)], writes=[kyt])
        self.I(DVE, "tensor_tensor", self.yaT[:, 4 * g:4 * g + 4, col0:col0 + nt], yt[:, :, 0:nt], gst[:, :, cl:cl + nt], ALU.mult, reads=[kyt] + ["gst%d" % q for q in range(4)], writes=["yaT"])


def core_inputs(inp, c, pv_base):
    b, p = c // 2, c % 2
    x = inp["x_prompt"][b]
    if p == 1:
        xseq = np.ascontiguousarray(x)
    else:
        xseq = np.concatenate([np.zeros((1024, D), np.float32), x[:1024]], axis=0)
    pa = permA()
    return {
        "xseq": xseq,
        "xsamp": np.ascontiguousarray(inp["x_sample"][16 * c:16 * c + 16].reshape(NSM, D)),
        "pvec": core_pvec(pv_base, p),
        "st_pool": np.ascontiguousarray(inp["state_pool"][0, 16 * c:16 * c + 16]),
        "st_conv": np.ascontiguousarray(inp["state_conv"][0, 16 * c:16 * c + 16]),
        "st_shift": np.ascontiguousarray(inp["state_shift"][0, 16 * c:16 * c + 16, 0][:, pa]),
        "st_wkv": np.ascontiguousarray(inp["state_wkv"][0, 16 * c:16 * c + 16].reshape(256, 4096)),
    }


def shared_inputs(inp):
    w_in = inp["w_in"][0]
    wG = np.empty((D, 16, 256), np.float32)
    wG[:, :, 0:128] = w_in[:, 4384:6432].reshape(D, 16, 128)
    wG[:, :, 128:256] = w_in[:, 6432:8480].reshape(D, 16, 128)
    wfi = inp["w_ffn_in"][0]
    wF = np.empty((D, NJ, 256), np.float32)
    wF[:, :, 0:128] = wfi[:, 0:DFF].reshape(D, NJ, 128)
    wF[:, :, 128:256] = wfi[:, DFF:2 * DFF].reshape(D, NJ, 128)
    gpost = np.empty((128, 2, D), np.float32)
    gpost[:, 0, :] = inp["norm_post_mix"][0][None, :]
    gpost[:, 1, :] = inp["norm_post_ffn"][0][None, :]
    return {
        "wA": np.ascontiguousarray(w_in[:, permA()]),
        "consts": host_consts(),
        "w2": np.ascontiguousarray(inp["w2"][0]),
        "a2": np.ascontiguousarray(inp["a2"][0]),
        "g2": np.ascontiguousarray(inp["g2"][0]),
        "wP": np.ascontiguousarray(w_in[:, 3360:4384]),
        "wG": wG,
        "wa": np.ascontiguousarray(inp["w_branch_a"][0]),
        "wbr": np.ascontiguousarray(inp["w_branch_b"][0]),
        "poolw": np.ascontiguousarray(inp["pool_w"][0]),
        "wout": np.ascontiguousarray(inp["w_out"][0]),
        "gpost": gpost,
        "wF": wF,
        "wfo": np.ascontiguousarray(inp["w_ffn_out"][0]),
    }


_NC_CACHE = {}


def get_nc(dbg=(), stages="SABC"):
    key = (tuple(sorted(dbg)), stages)
    if key not in _NC_CACHE:
        B = Builder(dbg=set(dbg), stages=stages)
        nc = B.build()
        _NC_CACHE[key] = (nc, B)
    return _NC_CACHE[key]


def run_cores(inp, cores, dbg=(), stages="SABC", trace=False):
    nc, B = get_nc(dbg, stages)
    sh = shared_inputs(inp)
    pv_base = host_pvec(inp)
    names = set(B.ins.keys())
    maps = []
    for c in cores:
        m = dict(sh)
        m.update(core_inputs(inp, c, pv_base))
        maps.append({k: v for k, v in m.items() if k in names})
    res = run_bass_kernel_spmd(nc, maps, core_ids=list(range(len(cores))), trace=trace)
    return res


def kernel(**inp):
    inp = {k: np.asarray(v) for k, v in inp.items()}
    res = run_cores(inp, list(range(8)))
    R = res.results
    pa = permA()
    y_prompt = np.empty((4, 2048, D), np.float32)
    y_sample = np.empty((128, 4, D), np.float32)
    p_shift = np.empty((1, 4, 1, DS), np.float32)
    p_wkv = np.empty((1, 4, 16, 64, 64), np.float32)
    p_pool = np.empty((1, 4, 15, 1024), np.float32)
    p_conv = np.empty((1, 4, 2, DFF), np.float32)
    s_shift = np.zeros((1, 128, 1, DS), np.float32)
    s_wkv = np.zeros((1, 128, 16, 64, 64), np.float32)
    s_pool = np.empty((1, 128, 15, 1024), np.float32)
    s_conv = np.empty((1, 128, 2, DFF), np.float32)
    for c in range(8):
        b, p = c // 2, c % 2
        r = R[c]
        y_prompt[b, p * 1024:(p + 1) * 1024] = r["o_y"][0:1024]
        y_sample[16 * c:16 * c + 16] = r["o_y"][1024:1024 + NSM].reshape(16, 4, D)
        if p == 1:
            p_shift[0, b, 0, pa] = r["o_pshift"]
            p_wkv[0, b] = r["o_pwkv"]
            p_pool[0, b] = r["o_ppool"]
            p_conv[0, b] = r["o_pconv"]
        s_pool[0, 16 * c:16 * c + 16] = r["o_spool"]
        s_conv[0, 16 * c:16 * c + 16] = r["o_sconv"]
        if "o_sshift" in r:
            s_shift[0, 16 * c:16 * c + 16, 0][:, pa] = r["o_sshift"]
            s_wkv[0, 16 * c:16 * c + 16] = r["o_swkv"].reshape(16, 16, 64, 64)
    return (y_prompt, y_sample, p_shift, p_wkv, p_pool, p_conv, s_shift, s_wkv, s_pool, s_conv)
```
